# Optimizing a Trainium2 kernel written in Bass

```python
import jax
import jax.numpy as jnp
from jax import lax
import numpy as np

D_MODEL = 2048
BATCH = 8
SEQ = 2048
DEPTH = 2

GRID_W = 64
CTX_LEN = 256
N_BRANCH = 4
BRANCH_W = D_MODEL // 4
FN_GROUPS = 4
FN_W = BRANCH_W
NA_HEADS = 8
NA_DH = BRANCH_W // NA_HEADS
NA_KH = 8
NA_KW = 16
NA_SCALE = NA_DH ** -0.5
MLA_HEADS = 4
MLA_NOPE = 128
MLA_ROPE = 64
MLA_V = BRANCH_W // MLA_HEADS
MLA_QLORA = D_MODEL // 4
MLA_KVLORA = D_MODEL // 8
MLA_SCALE = (MLA_NOPE + MLA_ROPE) ** -0.5
RW_DH = 64
RW_HEADS = BRANCH_W // RW_DH
RW_W = RW_HEADS * RW_DH
RW_DECAY_LORA = 32
RW_AAA_LORA = 32
RW_GATE_LORA = 96
RW_GN_EPS = 64e-5
N_EXPERTS = 16
EXPERT_FF = D_MODEL // 2
CAPACITY_FACTOR = 2
ROPE_THETA = 10000.0
NORM_EPS = 1e-6
Q_BLOCK = 128
NEG_INF = -1e30
F32 = jnp.float32

RW_KS_SPEC = (('k', RW_W), ('v', RW_W), ('wf', RW_DECAY_LORA), ('wb', RW_DECAY_LORA),
              ('af', RW_AAA_LORA), ('ab', RW_AAA_LORA))
RW_QS_SPEC = (('r', RW_W), ('g', RW_GATE_LORA))
RW_KS = 2 * RW_W + 2 * RW_DECAY_LORA + 2 * RW_AAA_LORA
RW_QS = RW_W + RW_GATE_LORA
KEY_SPEC = (('na_k', NA_HEADS * NA_DH), ('na_v', NA_HEADS * NA_DH), ('mla_ckv', MLA_KVLORA),
            ('mla_kpe', MLA_ROPE), ('rw_ks', RW_KS))
QUERY_SPEC = (('na_q', NA_HEADS * NA_DH), ('mla_cq', MLA_QLORA), ('rw_qs', RW_QS), ('fn', FN_W),
              ('gate', N_BRANCH * D_MODEL))
KEY_COLS = 2 * NA_HEADS * NA_DH + MLA_KVLORA + MLA_ROPE + RW_KS
Z_COLS = KEY_COLS + NA_HEADS * NA_DH + MLA_QLORA + RW_QS + FN_W + N_BRANCH * D_MODEL

kernel_name = 'hybrid_gated_diffusion_block'


def _split(z, spec):
    out, off = {}, 0
    for name, width in spec:
        out[name] = z[..., off:off + width]
        off += width
    return out


def _rmsnorm(x, g):
    xf = x.astype(F32)
    y = xf * lax.rsqrt(jnp.mean(xf * xf, axis=-1, keepdims=True) + NORM_EPS)
    return (y * g.astype(F32)).astype(x.dtype)


def _modulate(x, g, shift, scale):
    return _rmsnorm(x, g) * (1 + scale) + shift


def _heads(t, n_heads, gain=None):
    t = t.reshape(t.shape[0], t.shape[1], n_heads, -1)
    return t if gain is None else _rmsnorm(t, gain)


def _rope_1d(x, pos):
    half = x.shape[-1] // 2
    freqs = ROPE_THETA ** (-jnp.arange(half, dtype=F32) / half)
    ang = pos.astype(F32)[:, None] * freqs[None, :]
    cos, sin = jnp.cos(ang)[None, :, None, :], jnp.sin(ang)[None, :, None, :]
    xf = x.astype(F32)
    x1, x2 = xf[..., :half], xf[..., half:]
    return jnp.concatenate([x1 * cos - x2 * sin, x1 * sin + x2 * cos], -1).astype(x.dtype)


def _rope_2d(x, prow, pcol):
    half = x.shape[-1] // 2
    return jnp.concatenate([_rope_1d(x[..., :half], prow), _rope_1d(x[..., half:], pcol)], -1)


def _rope_pe(t, prow, pcol):
    return jnp.concatenate([t[..., :MLA_NOPE], _rope_2d(t[..., MLA_NOPE:], prow, pcol)], -1)


def _softmax_attn(q, k, v, scale):
    s = jnp.einsum('bqhd,bkhd->bhqk', q, k).astype(F32) * scale
    p = jax.nn.softmax(s, axis=-1).astype(v.dtype)
    return jnp.einsum('bhqk,bkhd->bqhd', p, v)


def _blocked_attn(q, k, v, scale):
    B, L, H, d = q.shape
    qb = jnp.moveaxis(q.reshape(B, L // Q_BLOCK, Q_BLOCK, H, d), 1, 0)
    o = lax.map(lambda qi: _softmax_attn(qi, k, v, scale), qb)
    return jnp.moveaxis(o, 0, 1).reshape(B, L, H, v.shape[-1])


def _neigh_attn_latent(q, k, v, kc, vc, rpb):
    B, L, H, dh = q.shape
    rows = L // GRID_W
    kh = min(NA_KH, rows)
    r = jnp.arange(rows)
    col = jnp.arange(GRID_W)
    rs = jnp.clip(r - kh // 2, 0, rows - kh)
    band = rs[:, None] + jnp.arange(kh)[None, :]
    cs = jnp.clip(col - NA_KW // 2, 0, GRID_W - NA_KW)
    valid = (col[None, :] >= cs[:, None]) & (col[None, :] < cs[:, None] + NA_KW)
    dr_idx = band - r[:, None] + NA_KH - 1
    dc_idx = jnp.clip(col[None, :] - col[:, None] + NA_KW - 1, 0, 2 * NA_KW - 2)
    bias = rpb.astype(F32)[:, dr_idx[:, None, :, None], dc_idx[None, :, None, :]]
    bias = jnp.where(valid[None, None, :, None, :], bias, NEG_INF)
    qg = q.reshape(B, rows, GRID_W, H, dh)
    kb = k.reshape(B, rows, GRID_W, H, dh)[:, band]
    vb = v.reshape(B, rows, GRID_W, H, dh)[:, band]
    s_loc = jnp.einsum('brqhd,brikhd->bhrqik', qg, kb).astype(F32) * NA_SCALE + bias[None]
    s_ctx = jnp.einsum('brqhd,bmhd->bhrqm', qg, kc).astype(F32) * NA_SCALE
    n_loc = kh * GRID_W
    s = jnp.concatenate([s_loc.reshape(B, H, rows, GRID_W, n_loc), s_ctx], axis=-1)
    p = jax.nn.softmax(s, axis=-1).astype(v.dtype)
    p_loc = p[..., :n_loc].reshape(B, H, rows, GRID_W, kh, GRID_W)
    o = (jnp.einsum('bhrqik,brikhd->brqhd', p_loc, vb)
         + jnp.einsum('bhrqm,bmhd->brqhd', p[..., n_loc:], vc))
    return o.reshape(B, L, H, dh)


def _mla_q(cq, p, prow=None, pcol=None):
    B, N, _ = cq.shape
    q = (_rmsnorm(cq, p['mla_cq_norm']) @ p['mla_w_uq']).reshape(B, N, MLA_HEADS, MLA_NOPE + MLA_ROPE)
    q = _rmsnorm(q, p['mla_q_norm'])
    return q if prow is None else _rope_pe(q, prow, pcol)


def _mla_kv(ckv, kpe, p, prow=None, pcol=None):
    B, N, _ = ckv.shape
    kv = (_rmsnorm(ckv, p['mla_ckv_norm']) @ p['mla_w_ukv']).reshape(B, N, MLA_HEADS, MLA_NOPE + MLA_V)
    k_pe = jnp.broadcast_to(kpe[:, :, None, :], (B, N, MLA_HEADS, MLA_ROPE))
    k = _rmsnorm(jnp.concatenate([kv[..., :MLA_NOPE], k_pe], axis=-1), p['mla_k_norm'])
    if prow is not None:
        k = _rope_pe(k, prow, pcol)
    return k, kv[..., MLA_NOPE:]


def _fourier(zf):
    B, N, _ = zf.shape
    g = zf.astype(F32).reshape(B, N, FN_GROUPS, FN_W // FN_GROUPS).transpose(0, 2, 1, 3)
    y = jnp.real(jnp.fft.fft2(g, norm='ortho'))
    return y.transpose(0, 2, 1, 3).reshape(B, N, FN_W).astype(zf.dtype)


def _shift_mix(z, mu):
    zp = jnp.pad(z, ((0, 0), (1, 1), (0, 0)))
    sh = 0.5 * (zp[:, :-2] + zp[:, 2:])
    return z + (sh - z) * mu


def _rwkv_prepare(zks, p):
    B, N, _ = zks.shape
    s = _split(zks.astype(F32), RW_KS_SPEC)
    hd = lambda t: t.reshape(B, N, RW_HEADS, RW_DH)
    k = s['k']
    v = hd(s['v'])
    kk = hd(k * p['rw_k_k'])
    kk = kk / jnp.maximum(jnp.sqrt(jnp.sum(kk * kk, axis=-1, keepdims=True)), 1e-12)
    dirs = []
    for d, (wn, an) in enumerate((('wf', 'af'), ('wb', 'ab'))):
        w = p['rw_w0'][d] + jnp.tanh(s[wn]) @ p['rw_w2'][d]
        decay = jnp.exp(-jnp.exp(-jax.nn.softplus(-w) - 0.5))
        a = jax.nn.sigmoid(p['rw_a0'][d] + s[an] @ p['rw_a2'][d])
        kd = k * (1.0 + (a - 1.0) * p['rw_k_a'])
        dirs.append((hd(decay), kk * hd(a), hd(kd)))
    return v, kk, dirs


def _wkv7(S0, decay, kk, b, k, v, r=None, reverse=False):
    seq = [jnp.moveaxis(t, 1, 0) for t in (decay, kk, b, k, v)]
    if r is not None:
        seq.append(jnp.moveaxis(r, 1, 0))

    def step(S, xs):
        w_t, kk_t, b_t, k_t, v_t = xs[:5]
        sa = jnp.einsum('bhvk,bhk->bhv', S, kk_t)
        S = S * w_t[:, :, None, :] - sa[..., None] * b_t[:, :, None, :] + v_t[..., None] * k_t[:, :, None, :]
        y = jnp.einsum('bhvk,bhk->bhv', S, xs[5]) if r is not None else None
        return S, y

    S, ys = lax.scan(step, S0, tuple(seq), reverse=reverse)
    return S, (None if r is None else jnp.moveaxis(ys, 0, 1))


def _rwkv_out(y, r, k_sum, v, g_lora, p):
    B, N = y.shape[:2]
    mu = jnp.mean(y, axis=-1, keepdims=True)
    var = jnp.mean(jnp.square(y - mu), axis=-1, keepdims=True)
    yn = ((y - mu) * lax.rsqrt(var + RW_GN_EPS)).reshape(B, N, RW_W) * p['rw_ln_w'] + p['rw_ln_b']
    bonus = jnp.sum(r * k_sum * p['rw_r_k'], axis=-1, keepdims=True) * v
    gate = jax.nn.sigmoid(g_lora) @ p['rw_g2']
    return (yn + bonus.reshape(B, N, RW_W)) * gate


def _rwkv_branch(zks, zqs, zks_c, zqs_c, p):
    B = zks.shape[0]
    v, kk, dirs = _rwkv_prepare(_shift_mix(zks, p['rw_mu_ks']), p)
    vc, kkc, dirs_c = _rwkv_prepare(_shift_mix(zks_c, p['rw_mu_ks']), p)
    qs = _split(_shift_mix(zqs, p['rw_mu_qs']).astype(F32), RW_QS_SPEC)
    r = qs['r'].reshape(B, -1, RW_HEADS, RW_DH)
    if zqs_c is not None:
        qsc = _split(_shift_mix(zqs_c, p['rw_mu_qs']).astype(F32), RW_QS_SPEC)
        rc = qsc['r'].reshape(B, -1, RW_HEADS, RW_DH)
    else:
        qsc, rc = None, None
    S0 = jnp.zeros((B, RW_HEADS, RW_DH, RW_DH), F32)
    ys, ycs = [], []
    for d, reverse in enumerate((False, True)):
        decay_c, b_c, k_c = dirs_c[d]
        S_c, y_c = _wkv7(S0, decay_c, kkc, b_c, k_c, vc, rc, reverse)
        decay_l, b_l, k_l = dirs[d]
        _, y_l = _wkv7(S_c, decay_l, kk, b_l, k_l, v, r, reverse)
        ys.append(y_l)
        ycs.append(y_c)
    out = _rwkv_out(ys[0] + ys[1], r, dirs[0][2] + dirs[1][2], v, qs['g'], p).astype(zks.dtype)
    if rc is None:
        return out, None
    out_c = _rwkv_out(ycs[0] + ycs[1], rc, dirs_c[0][2] + dirs_c[1][2], vc, qsc['g'], p).astype(zks.dtype)
    return out, out_c


def _merge(branches, gate_cols, w_br, w_out):
    B, N, _ = gate_cols.shape
    gates = jax.nn.sigmoid(gate_cols.astype(F32)).astype(gate_cols.dtype).reshape(B, N, N_BRANCH, D_MODEL)
    acc = gates[:, :, 0] * (branches[0] @ w_br[0])
    for i in range(1, N_BRANCH):
        acc = acc + gates[:, :, i] * (branches[i] @ w_br[i])
    return acc @ w_out


def _mixer(h, hc, p, ctx_out):
    B, L, _ = h.shape
    n_ctx = hc.shape[1]
    pos = jnp.arange(L)
    prow, pcol = pos // GRID_W, pos % GRID_W
    z = h @ p['w_in']
    zk = _split(z[..., :KEY_COLS], KEY_SPEC)
    zq = _split(z[..., KEY_COLS:], QUERY_SPEC)
    zc = hc @ (p['w_in'] if ctx_out else p['w_in'][:, :KEY_COLS])
    zck = _split(zc[..., :KEY_COLS], KEY_SPEC)
    zcq = _split(zc[..., KEY_COLS:], QUERY_SPEC) if ctx_out else None

    na_k = _heads(zk['na_k'], NA_HEADS, p['na_k_norm'])
    na_v = _heads(zk['na_v'], NA_HEADS)
    na_kc = _heads(zck['na_k'], NA_HEADS, p['na_k_norm'])
    na_vc = _heads(zck['na_v'], NA_HEADS)
    na_q = _heads(zq['na_q'], NA_HEADS, p['na_q_norm'])
    y_na = _neigh_attn_latent(na_q, na_k, na_v, na_kc, na_vc, p['na_rpb']).reshape(B, L, BRANCH_W)

    m_k, m_v = _mla_kv(zk['mla_ckv'], zk['mla_kpe'], p, prow, pcol)
    m_kc, m_vc = _mla_kv(zck['mla_ckv'], zck['mla_kpe'], p)
    m_q = _mla_q(zq['mla_cq'], p, prow, pcol)
    y_mla = _blocked_attn(m_q, jnp.concatenate([m_k, m_kc], axis=1), jnp.concatenate([m_v, m_vc], axis=1),
                          MLA_SCALE).reshape(B, L, BRANCH_W)

    y_rw, yc_rw = _rwkv_branch(zk['rw_ks'], zq['rw_qs'], zck['rw_ks'],
                               zcq['rw_qs'] if ctx_out else None, p)

    y_fn = _fourier(zq['fn'])

    y = _merge((y_fn, y_na, y_mla, y_rw), zq['gate'], p['w_br'], p['w_out'])
    if not ctx_out:
        return y, None
    yc_na = _softmax_attn(_heads(zcq['na_q'], NA_HEADS, p['na_q_norm']), na_kc, na_vc,
                          NA_SCALE).reshape(B, n_ctx, BRANCH_W)
    yc_mla = _softmax_attn(_mla_q(zcq['mla_cq'], p), m_kc, m_vc, MLA_SCALE).reshape(B, n_ctx, BRANCH_W)
    yc_fn = _fourier(zcq['fn'])
    yc = _merge((yc_fn, yc_na, yc_mla, yc_rw), zcq['gate'], p['w_br'], p['w_out'])
    return y, yc


def _expert_choice_moe(h, w_router, w1, w3, w2):
    B, N, D = h.shape
    cap = CAPACITY_FACTOR * N // N_EXPERTS
    aff = jax.nn.softmax((h @ w_router).astype(F32), axis=-1)
    gate, idx = lax.top_k(jnp.swapaxes(aff, 1, 2), cap)
    xe = jax.vmap(lambda hb, ib: hb[ib])(h, idx)
    a = jnp.einsum('becd,edf->becf', xe, w1)
    u = jnp.einsum('becd,edf->becf', xe, w3)
    ye = jnp.einsum('becf,efd->becd', jax.nn.silu(a) * u, w2) * gate[..., None].astype(h.dtype)
    return jax.vmap(lambda yb, ib: jnp.zeros((N, D), yb.dtype).at[ib.reshape(-1)].add(yb.reshape(-1, D)))(ye, idx)


def setup_inputs(seed: int = 0) -> dict:
    key = jax.random.key(seed)
    keys = iter(jax.random.split(key, 48))

    def nrm(shape, scale):
        return jax.random.normal(next(keys), shape, F32) * scale

    def gain(shape):
        return 1.0 + nrm(shape, 0.02)

    def unif(shape, lo, hi):
        return jax.random.uniform(next(keys), shape, F32, lo, hi)

    return {
        'x': nrm((BATCH, SEQ, D_MODEL), 1.0),
        'c': nrm((BATCH, D_MODEL), 1.0),
        'ctx': nrm((BATCH, CTX_LEN, D_MODEL), 1.0),
        'c_ctx': nrm((D_MODEL,), 1.0),
        'ada_w': nrm((DEPTH, D_MODEL, 6 * D_MODEL), 0.5 * D_MODEL ** -0.5),
        'ada_b': nrm((DEPTH, 6 * D_MODEL), 0.01),
        'norm1_g': gain((DEPTH, D_MODEL)),
        'norm2_g': gain((DEPTH, D_MODEL)),
        'w_in': nrm((DEPTH, D_MODEL, Z_COLS), D_MODEL ** -0.5),
        'na_q_norm': gain((DEPTH, NA_DH)),
        'na_k_norm': gain((DEPTH, NA_DH)),
        'na_rpb': nrm((DEPTH, NA_HEADS, 2 * NA_KH - 1, 2 * NA_KW - 1), 0.1),
        'mla_cq_norm': gain((DEPTH, MLA_QLORA)),
        'mla_ckv_norm': gain((DEPTH, MLA_KVLORA)),
        'mla_w_uq': nrm((DEPTH, MLA_QLORA, MLA_HEADS * (MLA_NOPE + MLA_ROPE)), MLA_QLORA ** -0.5),
        'mla_w_ukv': nrm((DEPTH, MLA_KVLORA, MLA_HEADS * (MLA_NOPE + MLA_V)), MLA_KVLORA ** -0.5),
        'mla_q_norm': gain((DEPTH, MLA_NOPE + MLA_ROPE)),
        'mla_k_norm': gain((DEPTH, MLA_NOPE + MLA_ROPE)),
        'rw_mu_ks': unif((DEPTH, RW_KS), 0.0, 1.0),
        'rw_mu_qs': unif((DEPTH, RW_QS), 0.0, 1.0),
        'rw_w0': unif((DEPTH, 2, RW_W), -4.0, 1.0),
        'rw_w2': nrm((DEPTH, 2, RW_DECAY_LORA, RW_W), 0.1),
        'rw_a0': nrm((DEPTH, 2, RW_W), 0.1),
        'rw_a2': nrm((DEPTH, 2, RW_AAA_LORA, RW_W), 0.5 * RW_AAA_LORA ** -0.5),
        'rw_g2': nrm((DEPTH, RW_GATE_LORA, RW_W), RW_GATE_LORA ** -0.5),
        'rw_k_k': 0.85 + nrm((DEPTH, RW_W), 0.05),
        'rw_k_a': 1.0 + nrm((DEPTH, RW_W), 0.05),
        'rw_r_k': nrm((DEPTH, RW_HEADS, RW_DH), 0.1),
        'rw_ln_w': gain((DEPTH, RW_W)),
        'rw_ln_b': nrm((DEPTH, RW_W), 0.01),
        'w_br': nrm((DEPTH, N_BRANCH, BRANCH_W, D_MODEL), BRANCH_W ** -0.5),
        'w_out': nrm((DEPTH, D_MODEL, D_MODEL), D_MODEL ** -0.5),
        'moe_router': nrm((DEPTH, D_MODEL, N_EXPERTS), D_MODEL ** -0.5),
        'moe_w1': nrm((DEPTH, N_EXPERTS, D_MODEL, EXPERT_FF), D_MODEL ** -0.5),
        'moe_w3': nrm((DEPTH, N_EXPERTS, D_MODEL, EXPERT_FF), D_MODEL ** -0.5),
        'moe_w2': nrm((DEPTH, N_EXPERTS, EXPERT_FF, D_MODEL), EXPERT_FF ** -0.5),
    }


def reference(x, c, ctx, c_ctx, ada_w, ada_b, norm1_g, norm2_g, w_in, na_q_norm, na_k_norm, na_rpb,
              mla_cq_norm, mla_ckv_norm, mla_w_uq, mla_w_ukv, mla_q_norm, mla_k_norm,
              rw_mu_ks, rw_mu_qs, rw_w0, rw_w2, rw_a0, rw_a2, rw_g2, rw_k_k, rw_k_a, rw_r_k,
              rw_ln_w, rw_ln_b, w_br, w_out, moe_router, moe_w1, moe_w3, moe_w2):
    silu_c = jax.nn.silu(c)
    silu_cc = jax.nn.silu(c_ctx)[None, :]
    xc = ctx
    for layer in range(DEPTH):
        last = layer == DEPTH - 1
        p = {
            'w_in': w_in[layer], 'na_q_norm': na_q_norm[layer], 'na_k_norm': na_k_norm[layer],
            'na_rpb': na_rpb[layer], 'mla_cq_norm': mla_cq_norm[layer], 'mla_ckv_norm': mla_ckv_norm[layer],
            'mla_w_uq': mla_w_uq[layer], 'mla_w_ukv': mla_w_ukv[layer], 'mla_q_norm': mla_q_norm[layer],
            'mla_k_norm': mla_k_norm[layer], 'rw_mu_ks': rw_mu_ks[layer], 'rw_mu_qs': rw_mu_qs[layer],
            'rw_w0': rw_w0[layer], 'rw_w2': rw_w2[layer], 'rw_a0': rw_a0[layer], 'rw_a2': rw_a2[layer],
            'rw_g2': rw_g2[layer], 'rw_k_k': rw_k_k[layer], 'rw_k_a': rw_k_a[layer], 'rw_r_k': rw_r_k[layer],
            'rw_ln_w': rw_ln_w[layer], 'rw_ln_b': rw_ln_b[layer], 'w_br': w_br[layer], 'w_out': w_out[layer],
        }
        mod = (silu_c @ ada_w[layer] + ada_b[layer])[:, None, :]
        mod_c = (silu_cc @ ada_w[layer] + ada_b[layer])[:, None, :]
        sh1, sc1, g1, sh2, sc2, g2 = jnp.split(mod, 6, axis=-1)
        csh1, csc1, cg1, csh2, csc2, cg2 = jnp.split(mod_c, 6, axis=-1)
        h = _modulate(x, norm1_g[layer], sh1, sc1)
        hc = _modulate(xc, norm1_g[layer], csh1, csc1)
        y, yc = _mixer(h, hc, p, not last)
        x = x + g1 * y
        x = x + g2 * _expert_choice_moe(_modulate(x, norm2_g[layer], sh2, sc2), moe_router[layer],
                                        moe_w1[layer], moe_w3[layer], moe_w2[layer])
        if not last:
            xc = xc + cg1 * yc
            xc = xc + cg2 * _expert_choice_moe(_modulate(xc, norm2_g[layer], csh2, csc2), moe_router[layer],
                                               moe_w1[layer], moe_w3[layer], moe_w2[layer])
    return x
```

```python
import numpy as np
import ml_dtypes
from contextlib import ExitStack
import concourse.bass as bass
import concourse.mybir as mybir
from concourse.bass_utils import run_bass_kernel_spmd

F32 = mybir.dt.float32
BF16 = mybir.dt.bfloat16
I32 = mybir.dt.int32
AF = mybir.ActivationFunctionType
ALU = mybir.AluOpType
AX = mybir.AxisListType


class R:
    __slots__ = ("w", "r", "name", "excl")

    def __init__(self, name="", excl=False):
        self.w = []
        self.r = []
        self.name = name
        self.excl = excl


def PR():
    return R("psum", True)


class K:
    ENG = ("tensor", "vector", "scalar", "gpsimd", "sync")
    NDMA = 24
    MAXSEM = 30000

    def __init__(self, nc, es):
        self.nc = nc
        self.es = es
        self.prog = {e: [] for e in self.ENG}
        self.sem = {}
        self.epoch = {e: 0 for e in self.ENG}
        for e in self.ENG:
            self.sem[(e, 0)] = es.enter_context(nc.semaphore("s_%s_0" % e))
        self.cnt = {e: 0 for e in self.ENG}
        self.waited = {e: {} for e in self.ENG}
        self.dsem = [es.enter_context(nc.semaphore("d%d" % i)) for i in range(self.NDMA)]
        self.dval = [0] * self.NDMA
        self.dnext = 0
        self.ninstr = 0

    def sb(self, name, shape, dt=F32, es=None):
        self.uid = getattr(self, "uid", 0) + 1
        name = "%s_u%d" % (name, self.uid)
        return (es or self.es).enter_context(self.nc.sbuf_tensor(name, list(shape), dt))

    def ps(self, name, shape, dt=F32, es=None):
        n = int(np.prod(shape[1:])) * (2 if dt == BF16 else 4)
        assert n % 2048 == 0, (name, shape, n)
        self.uid = getattr(self, "uid", 0) + 1
        name = "%s_u%d" % (name, self.uid)
        return (es or self.es).enter_context(self.nc.psum_tensor(name, list(shape), dt))

    def _key(self, dep):
        return dep[0] if dep[0] in self.ENG else ("d", dep[0])

    def _collect(self, reads, writes):
        deps = {}
        for r in reads:
            for d in r.w:
                k = d[0]
                if deps.get(k, 0) < d[1]:
                    deps[k] = d[1]
        for w in writes:
            for d in w.w + w.r:
                k = d[0]
                if deps.get(k, 0) < d[1]:
                    deps[k] = d[1]
        return deps

    def _emit_waits(self, eng, deps):
        wd = self.waited[eng]
        for k, v in deps.items():
            if wd.get(k, 0) >= v:
                continue
            wd[k] = v
            sem = self.sem[k] if isinstance(k, tuple) else self.dsem[k]
            getattr(self.nc, eng).wait_ge(sem, v)

    def _mark(self, dep, reads, writes):
        for r in reads:
            r.r.append(dep)
        for w in writes:
            w.w = [dep]
            w.r = []

    def op(self, eng, fn, reads=(), writes=()):
        if any(r.excl for r in reads):
            writes = list(writes) + [r for r in reads if r.excl]
            reads = [r for r in reads if not r.excl]
        deps = self._collect(reads, writes)
        self._emit_waits(eng, deps)
        if self.cnt[eng] >= self.MAXSEM:
            self.epoch[eng] += 1
            self.cnt[eng] = 0
            self.sem[(eng, self.epoch[eng])] = self.es.enter_context(
                self.nc.semaphore("s_%s_%d" % (eng, self.epoch[eng])))
        self.cnt[eng] += 1
        key = (eng, self.epoch[eng])
        dep = (key, self.cnt[eng])
        fn(getattr(self.nc, eng)).then_inc(self.sem[key], 1)
        self._mark(dep, reads, writes)
        self.ninstr += 1

    def dma(self, out, in_, reads=(), writes=(), q="sync", **kw):
        q = "sync"
        deps = self._collect(reads, writes)
        i = self.dnext
        self.dnext = (self.dnext + 1) % self.NDMA
        if self.dval[i] > 0:
            deps[i] = max(deps.get(i, 0), self.dval[i])
        self._emit_waits(q, deps)
        self.dval[i] += 16
        dep = (i, self.dval[i])
        getattr(self.nc, q).dma_start(out=out, in_=in_, **kw).then_inc(self.dsem[i], 16)
        self._mark(dep, reads, writes)
        self.ninstr += 1

    def finish(self, final_regions):
        deps = {}
        for e in self.ENG:
            if self.cnt[e]:
                deps[(e, self.epoch[e])] = self.cnt[e]
        for i in range(self.NDMA):
            if self.dval[i]:
                deps[i] = self.dval[i]
        self._emit_waits("sync", deps)

    def barrier(self):
        deps = {}
        for e in self.ENG:
            if self.cnt[e]:
                deps[(e, self.epoch[e])] = self.cnt[e]
        for i in range(self.NDMA):
            if self.dval[i]:
                deps[i] = self.dval[i]
        for e in self.ENG:
            self._emit_waits(e, dict(deps))


D = 2048
L = 2048
NCTX = 256
T = L + NCTX
NT = T // 128
DEPTH = 2
ZC = 12832
C_NAK, C_NAV, C_CKV, C_KPE, C_RWKS = 0, 512, 1024, 1280, 1344
C_NAQ, C_CQ, C_RWQS, C_FN, C_GATE = 2496, 3008, 3520, 4128, 4640
EPS = 1e-6


def bf16_np(a):
    return np.asarray(a, dtype=np.float32).astype(ml_dtypes.bfloat16)


class Prog:
    def __init__(self, depth=DEPTH, ext=None, stages=None):
        self.depth = depth
        self.ext = ext or {}
        self.stages = stages
        self.nc = bass.Bass("TRN2", target_bir_lowering=False)
        self.es = ExitStack()
        self.k = K(self.nc, self.es)
        self.dr = {}
        self.kinds = {}
        self.modT = {}
        self.final_out = True
        import os
        self.dbg = {kk[4:]: int(v) for kk, v in os.environ.items() if kk.startswith("DBG_")}

    def dram(self, name, shape, dt=F32, kind="Internal"):
        kind = self.ext.get(name, kind)
        t = self.nc.dram_tensor(name, list(shape), dt, kind=kind)
        self.dr[name] = (t.ap(), R(name))
        self.kinds[name] = kind
        return self.dr[name]

    def want(self, s):
        return self.stages is None or s in self.stages

    def consts(self):
        k = self.k
        self.identf = k.sb("identf", [128, 128], F32)
        self.ident = k.sb("ident", [128, 128], BF16)
        self.r_ident = R("ident")
        k.op("gpsimd", lambda e: e.memset(self.identf[:], 0.0), writes=[self.r_ident])
        k.op("gpsimd", lambda e: e.affine_select(out=self.identf[:], in_=self.identf[:], pattern=[[-1, 128]],
                                                  compare_op=ALU.not_equal, fill=1.0, base=0, channel_multiplier=1),
             reads=[self.r_ident], writes=[self.r_ident])
        k.op("vector", lambda e: e.tensor_copy(out=self.ident[:], in_=self.identf[:]),
             reads=[self.r_ident], writes=[self.r_ident])
        ap, r = self.dram("cvT", [128, 16, 2], F32, "ExternalInput")
        self.sT = k.sb("sT", [128, 16, 2], F32)
        self.r_sT = R("sT")
        k.dma(self.sT[:], ap, reads=[r], writes=[self.r_sT])
        k.op("scalar", lambda e: e.activation(out=self.sT[:], in_=self.sT[:], func=AF.Silu),
             reads=[self.r_sT], writes=[self.r_sT])

    def stage_mod(self, l):
        k = self.k
        adaw, r_adaw = self.dr["ada_w"]
        adab, r_adab = self.dr["ada_b2"]
        modD, r_modD = self.dram("modD%d" % l, [2, 6 * D], F32)
        modT = k.sb("modT%d" % l, [128, 6, 16, 2], F32)
        r_modT = R()
        self.modT[l] = (modT, r_modT)
        with ExitStack() as es:
            wt = [k.sb("mod_w%d" % i, [128, 2048], F32, es) for i in range(3)]
            r_wt = [R() for _ in range(3)]
            ps = k.ps("mod_ps", [2, 2048], F32, es)
            r_ps = PR()
            pT = k.ps("mod_pT", [128, 4, 128], F32, es)
            r_pT = PR()
            row = [k.sb("mod_row%d" % i, [128, 2048], F32, es) for i in range(2)]
            r_row = [R() for _ in range(2)]
            for i in range(2):
                k.op("gpsimd", lambda e, i=i: e.memset(row[i][:], 0.0), writes=[r_row[i]])
            bias = k.sb("mod_bias", [2, 6 * D], F32, es)
            r_bias = R()
            k.dma(bias[:], adab[l], reads=[r_adab], writes=[r_bias])
            i2 = self.identf[0:2, 0:2]
            it = 0
            for nb in range(6):
                for kc in range(16):
                    w = wt[it % 3]
                    rw = r_wt[it % 3]
                    it += 1
                    k.dma(w[:], adaw[l, kc * 128:(kc + 1) * 128, nb * 2048:(nb + 1) * 2048], reads=[r_adaw], writes=[rw])

                    def mm(e, w=w, kc=kc):
                        last = None
                        for j in range(4):
                            last = e.matmul(ps[:, j * 512:(j + 1) * 512], lhsT=self.sT[:, kc, :], rhs=w[:, j * 512:(j + 1) * 512],
                                            start=(kc == 0), stop=(kc == 15))
                        return last
                    k.op("tensor", mm, reads=[rw, self.r_sT], writes=[r_ps])
                rt = row[nb % 2]
                rr = r_row[nb % 2]
                k.op("vector", lambda e, rt=rt, nb=nb: e.tensor_tensor(out=rt[0:2, :], in0=ps[:], in1=bias[:, nb * 2048:(nb + 1) * 2048], op=ALU.add),
                     reads=[r_ps, r_bias], writes=[rr])
                k.dma(modD[:, nb * 2048:(nb + 1) * 2048], rt[0:2, :], reads=[rr], writes=[r_modD], q="gpsimd")

                for g in range(4):
                    def tr(e, rt=rt, g=g):
                        last = None
                        for c in range(4):
                            cc = g * 4 + c
                            last = e.transpose(out=pT[:, c, :], in_=rt[:, cc * 128:(cc + 1) * 128], identity=self.identf[:])
                        return last
                    k.op("tensor", tr, reads=[rr, self.r_ident], writes=[r_pT])
                    k.op("vector", lambda e, nb=nb, g=g: e.tensor_copy(out=modT[:, nb, g * 4:(g + 1) * 4, :], in_=pT[:, :, 0:2]),
                         reads=[r_pT], writes=[r_modT])
            k.barrier()

    def stage_normz(self, l, xname):
        k = self.k
        xin, r_xin = self.dr[xname]
        win, r_win = self.dr["w_in"]
        g1T_d, r_g1T = self.dr["norm1_gT"]
        z, r_z = self.dram("z%d" % l, [T, ZC], F32)
        self.r_ztile = [R() for _ in range(NT)]
        modT, r_modT = self.modT[l]
        with ExitStack() as es:
            hT = k.sb("hT", [128, 16, T], BF16, es)
            r_hT = [[R() for _ in range(16)] for _ in range(NT)]
            aT = k.sb("nz_aT", [128, 16, 2], F32, es)
            r_aT = R()
            gT = k.sb("nz_gT", [128, 16], F32, es)
            k.dma(gT[:], g1T_d[l], reads=[r_g1T], writes=[r_aT])
            k.op("vector", lambda e: e.tensor_scalar(out=aT[:], in0=modT[:, 1, :, :], scalar1=1.0, scalar2=None, op0=ALU.add),
                 reads=[r_modT], writes=[r_aT])
            k.op("vector", lambda e: e.tensor_tensor(out=aT[:], in0=aT[:], in1=gT[:].unsqueeze(2).to_broadcast([128, 16, 2]), op=ALU.mult),
                 reads=[r_aT], writes=[r_aT])
            self.norm_to_hT(es, xin, r_xin, hT, r_hT, aT, r_aT, modT[:, 0, :, :], r_modT, "nz")
            wf = [k.sb("nz_wf%d" % i, [128, 4, 512], F32, es) for i in range(3)]
            r_wf = [R() for _ in range(3)]
            wb = [k.sb("nz_wb%d" % i, [128, 16, 512], BF16, es) for i in range(2)]
            r_wb = [[R() for _ in range(4)] for _ in range(2)]
            pz = [k.ps("nz_pz%d" % i, [128, 512], F32, es) for i in range(4)]
            r_pz = [PR() for _ in range(4)]
            zo = [k.sb("nz_zo%d" % i, [128, 512], F32, es) for i in range(4)]
            r_zo = [R() for _ in range(4)]
            ncb = (ZC + 511) // 512
            cnt = 0
            ld = 0
            for cb in range(self.dbg.get("NZ_NCB", ncb)):
                c0 = cb * 512
                cw = min(512, ZC - c0)
                b, rb = wb[cb % 2], r_wb[cb % 2]
                for g in range(4):
                    f, rf = wf[ld % 3], r_wf[ld % 3]
                    k.dma(f[:, :, 0:cw], win[l, g * 512:(g + 1) * 512, c0:c0 + cw].rearrange("(kc p) n -> p kc n", p=128),
                          reads=[r_win], writes=[rf])
                    eng = ("vector", "gpsimd", "scalar")[ld % 3]
                    ld += 1
                    if eng == "scalar":
                        k.op("scalar", lambda e, f=f, b=b, cw=cw, g=g: e.activation(out=b[:, g * 4:(g + 1) * 4, 0:cw], in_=f[:, :, 0:cw], func=AF.Copy),
                             reads=[rf], writes=[rb[g]])
                    else:
                        k.op(eng, lambda e, f=f, b=b, cw=cw, g=g: e.tensor_copy(out=b[:, g * 4:(g + 1) * 4, 0:cw], in_=f[:, :, 0:cw]),
                             reads=[rf], writes=[rb[g]])
                for t in range(self.dbg.get("NZ_NT", NT)):
                    p, rp = pz[cnt % 4], r_pz[cnt % 4]
                    o, ro = zo[cnt % 4], r_zo[cnt % 4]
                    cnt += 1

                    def mm(e, p=p, t=t, b=b, cw=cw):
                        last = None
                        for kc in range(16):
                            last = e.matmul(p[:, 0:cw], lhsT=hT[:, kc, t * 128:(t + 1) * 128], rhs=b[:, kc, 0:cw],
                                            start=(kc == 0), stop=(kc == 15))
                        return last
                    k.op("tensor", mm, reads=rb + r_hT[t], writes=[rp])
                    if t % 2 == 0:
                        k.op("vector", lambda e, o=o, p=p, cw=cw: e.tensor_copy(out=o[:, 0:cw], in_=p[:, 0:cw]), reads=[rp], writes=[ro])
                    else:
                        k.op("scalar", lambda e, o=o, p=p, cw=cw: e.activation(out=o[:, 0:cw], in_=p[:, 0:cw], func=AF.Copy), reads=[rp], writes=[ro])
                    k.dma(z[t * 128:(t + 1) * 128, c0:c0 + cw], o[:, 0:cw], reads=[ro], writes=[self.r_ztile[t]], q="gpsimd")
            k.barrier()

    def norm_to_hT(self, es, xin, r_xin, hT, r_hT, aT, r_aT, shT, r_shT, pfx, h_tok=None):
        k = self.k
        xt = [k.sb(pfx + "_xt%d" % i, [128, D], F32, es) for i in range(2)]
        r_xt = [R() for _ in range(2)]
        xn = [k.sb(pfx + "_xn%d" % i, [128, D], BF16, es) for i in range(2)]
        r_xn = [R() for _ in range(2)]
        junk = k.sb(pfx + "_junk", [128, D], BF16, es)
        r_junk = R()
        st = k.sb(pfx + "_st", [128, NT, 2], F32, es)
        r_stl = [R() for _ in range(NT)]
        pt = [k.ps(pfx + "_pt%d" % i, [128, 8, 128], BF16, es) for i in range(2)]
        r_pt = [PR() for _ in range(2)]
        for t in range(self.dbg.get("NZ_NT", NT)):
            row = 1 if t < 2 else 0
            r_st = r_stl[t]
            x_, rx = xt[t % 2], r_xt[t % 2]
            n_, rn = xn[t % 2], r_xn[t % 2]
            STEP = self.dbg.get("STEP", 99)
            if STEP < 1:
                continue
            k.dma(x_[:], xin[t * 128:(t + 1) * 128, :], reads=[r_xin], writes=[rx])
            if STEP < 2:
                continue
            k.op("scalar", lambda e, x_=x_, t=t: e.activation(out=junk[:], in_=x_[:], func=AF.Square, accum_out=st[:, t, 0:1]),
                 reads=[rx], writes=[r_junk, r_st])
            if STEP < 3:
                continue
            k.op("scalar", lambda e, t=t: e.activation(out=st[:, t, 1:2], in_=st[:, t, 0:1], func=AF.Sqrt, bias=self.eps_t[:, 0:1], scale=1.0 / D),
                 reads=[r_st], writes=[r_st])
            if STEP < 4:
                continue
            k.op("vector", lambda e, t=t: e.reciprocal(out=st[:, t, 1:2], in_=st[:, t, 1:2]), reads=[r_st], writes=[r_st])
            if STEP < 5:
                continue
            k.op("vector", lambda e, x_=x_, n_=n_, t=t: e.tensor_scalar(out=n_[:], in0=x_[:], scalar1=st[:, t, 1:2], scalar2=None, op0=ALU.mult),
                 reads=[rx, r_st], writes=[rn])
            if STEP < 6:
                continue
            for half in range(2):
                p, rp = pt[half], r_pt[half]

                def tr(e, p=p, n_=n_, half=half):
                    last = None
                    for j in range(8):
                        dc = half * 8 + j
                        last = e.transpose(out=p[:, j, :], in_=n_[:, dc * 128:(dc + 1) * 128], identity=self.ident[:])
                    return last
                k.op("tensor", tr, reads=[rn, self.r_ident], writes=[rp])
                if STEP < 7:
                    continue
                for j in range(8):
                    dc = half * 8 + j
                    if STEP == 7 and j % 2 == 1:
                        continue
                    if STEP == 8 and j % 2 == 0:
                        continue
                    eng = "vector" if half == 0 else "scalar"
                    if eng == "vector":
                        k.op("vector", lambda e, p=p, j=j, dc=dc, t=t, row=row: e.tensor_scalar(
                            out=hT[:, dc, t * 128:(t + 1) * 128], in0=p[:, j, :], scalar1=aT[:, dc, row:row + 1],
                            scalar2=shT[:, dc, row:row + 1], op0=ALU.mult, op1=ALU.add),
                            reads=[rp, r_aT, r_shT], writes=[r_hT[t][dc]])
                    else:
                        k.op("scalar", lambda e, p=p, j=j, dc=dc, t=t, row=row: e.activation(
                            out=hT[:, dc, t * 128:(t + 1) * 128], in_=p[:, j, :], func=AF.Identity,
                            scale=aT[:, dc, row:row + 1], bias=shT[:, dc, row:row + 1]),
                            reads=[rp, r_aT, r_shT], writes=[r_hT[t][dc]])

    def head_rms(self, eng_sq, x_, w, nh, dh, st, junk, out, gain_bc, reads, r_st, r_junk, writes, tmp, r_tmp):
        k = self.k
        x3 = x_.rearrange("p (h d) -> p h d", h=nh)
        k.op("vector", lambda e: e.tensor_tensor(out=junk, in0=x_, in1=x_, op=ALU.mult), reads=reads, writes=[r_junk])
        k.op("vector", lambda e: e.tensor_reduce(out=st, in_=junk.rearrange("p (h d) -> p h d", h=nh), axis=AX.X, op=ALU.add),
             reads=[r_junk], writes=[r_st])
        k.op("scalar", lambda e: e.activation(out=st, in_=st, func=AF.Sqrt, bias=self.eps_t[:, 0:1], scale=1.0 / dh),
             reads=[r_st], writes=[r_st])
        k.op("vector", lambda e: e.reciprocal(out=st, in_=st), reads=[r_st], writes=[r_st])
        k.op("vector", lambda e: e.tensor_tensor(out=tmp.rearrange("p (h d) -> p h d", h=nh), in0=x3,
                                                in1=st.unsqueeze(2).to_broadcast([128, nh, dh]), op=ALU.mult),
             reads=reads + [r_st], writes=[r_tmp])
        k.op("gpsimd", lambda e: e.tensor_tensor(out=out, in0=tmp, in1=gain_bc, op=ALU.mult), reads=[r_tmp] + [self.r_gain], writes=writes)

    def stage_na(self, l):
        k = self.k
        z, _ = self.dr["z%d" % l]
        r_zt = self.r_ztile
        yd, r_yd = self.dram("y_na%d" % l, [T, 512], F32)
        gq_d, r_gq = self.dr["na_qg"]
        gk_d, r_gk = self.dr["na_kg"]
        bias_d, r_bias_d = self.dr["na_bias"]
        with ExitStack() as es:
            QT = k.sb("na_QT", [128, 4, T], BF16, es)
            KT = k.sb("na_KT", [128, 4, T], BF16, es)
            r_QT = [R() for _ in range(NT)]
            r_KT = [R() for _ in range(NT)]
            Va = k.sb("na_V", [128, NT, 8, 65], BF16, es)
            r_Va = [R() for _ in range(NT)]
            yna = k.sb("na_y", [128, NT, 512], F32, es)
            r_yna = [R() for _ in range(NT)]
            gq = k.sb("na_gq", [128, 512], F32, es)
            gk = k.sb("na_gk", [128, 512], F32, es)
            self.r_gain = R()
            k.dma(gq[:], gq_d[l].partition_broadcast(128), reads=[r_gq], writes=[self.r_gain])
            k.dma(gk[:], gk_d[l].partition_broadcast(128), reads=[r_gk], writes=[self.r_gain])
            k.op("vector", lambda e: e.tensor_scalar(out=gq[:], in0=gq[:], scalar1=0.125, scalar2=None, op0=ALU.mult),
                 reads=[self.r_gain], writes=[self.r_gain])
            r_ones = R()
            k.op("gpsimd", lambda e: e.memset(Va[:, :, :, 64:65], 1.0), writes=r_Va)
            xin = [k.sb("na_x%d" % i, [128, 512], F32, es) for i in range(6)]
            r_xin = [R() for _ in range(6)]
            junk = k.sb("na_junk", [128, 512], F32, es)
            r_junk = R()
            tmp = k.sb("na_tmp", [128, 512], F32, es)
            r_tmp = R()
            st = k.sb("na_st", [128, NT, 2, 8], F32, es)
            xb = [k.sb("na_xb%d" % i, [128, 512], BF16, es) for i in range(2)]
            r_xb = [R() for _ in range(2)]
            ptr = [k.ps("na_ptr%d" % i, [128, 8, 128], BF16, es) for i in range(1)]
            r_ptr = [PR() for _ in range(1)]
            ld = 0
            nb = 0
            for t in range(NT):
                rows = slice(t * 128, (t + 1) * 128)
                for which, c0 in ((0, C_NAQ), (1, C_NAK)):
                    x_, rx = xin[ld % 6], r_xin[ld % 6]
                    ld += 1
                    k.dma(x_[:], z[rows, c0:c0 + 512], reads=[r_zt[t]], writes=[rx])
                    b_, rb = xb[nb % 2], r_xb[nb % 2]
                    nb += 1
                    r_st = R()
                    self.head_rms(None, x_[:], 512, 8, 64, st[:, t, which, :], junk[:], b_[:], (gq if which == 0 else gk)[:],
                                  [rx], r_st, r_junk, [rb], tmp[:], r_tmp)
                    p, rp = ptr[0], r_ptr[0]

                    def tr(e, p=p, b_=b_):
                        last = None
                        for j in range(4):
                            last = e.transpose(out=p[:, j, :], in_=b_[:, j * 128:(j + 1) * 128], identity=self.ident[:])
                        return last
                    k.op("tensor", tr, reads=[rb, self.r_ident], writes=[rp])
                    dst = QT if which == 0 else KT
                    rd = (r_QT if which == 0 else r_KT)[t]
                    k.op("scalar", lambda e, dst=dst, p=p, rows=rows: e.activation(out=dst[:, :, rows], in_=p[:, 0:4, :], func=AF.Copy),
                         reads=[rp], writes=[rd])
                x_, rx = xin[ld % 6], r_xin[ld % 6]
                ld += 1
                k.dma(x_[:], z[rows, C_NAV:C_NAV + 512], reads=[r_zt[t]], writes=[rx])
                k.op("vector", lambda e, x_=x_, t=t: e.tensor_copy(out=Va[:, t, :, 0:64], in_=x_[:].rearrange("p (h d) -> p h d", h=8)),
                     reads=[rx], writes=[r_Va[t]])
            bias = [k.sb("na_bias%d" % i, [128, 5, 5, 128], F32, es) for i in range(2)]
            r_bias = [R() for _ in range(2)]
            stp = [k.ps("na_st%d" % i, [128, 8, 128], F32, es) for i in range(2)]
            r_stp = [PR() for _ in range(2)]
            po = [k.ps("na_po%d" % i, [128, 512], F32, es) for i in range(2)]
            r_po = [PR() for _ in range(2)]
            sbt = [k.sb("na_sb%d" % i, [128, 5, 128], F32, es) for i in range(2)]
            r_sbt = [R() for _ in range(2)]
            PT = [k.sb("na_PT%d" % i, [128, 7, 128], BF16, es) for i in range(2)]
            r_PT = [R() for _ in range(2)]
            rc = k.sb("na_rc", [128, 2], F32, es)
            r_rc = [R(), R()]
            it = 0
            for h in range(8):
                bs, rbs = bias[h % 2], r_bias[h % 2]
                for c in range(5):
                    k.dma(bs[:, c, :, :], bias_d[l, h, c].rearrange("t k q -> k t q"), reads=[r_bias_d], writes=[rbs])
                hp, hs = h // 2, (h % 2) * 64
                for t in range(NT):
                    if t < 2:
                        kts = [0, 1]
                        nbias = 0
                        cls = 0
                    else:
                        j = t - 2
                        lo = min(max(j - 2, 0), 11)
                        kts = [lo + 2 + i for i in range(5)] + [0, 1]
                        nbias = 5
                        cls = {0: 0, 1: 1, 14: 3, 15: 4}.get(j, 2)
                    nk = len(kts)
                    sp, rsp = stp[it % 2], r_stp[it % 2]
                    pp, rpp = PT[it % 2], r_PT[it % 2]
                    sb_, rsb = sbt[it % 2], r_sbt[it % 2]
                    o_, ro = po[it % 2], r_po[it % 2]
                    rcc, rrc = rc[:, it % 2:it % 2 + 1], r_rc[it % 2]
                    it += 1

                    def qk(e, sp=sp, kts=kts, t=t, hp=hp, hs=hs):
                        last = None
                        for i, kt in enumerate(kts):
                            last = e.matmul(sp[:, i, :], lhsT=KT[hs:hs + 64, hp, kt * 128:(kt + 1) * 128],
                                            rhs=QT[hs:hs + 64, hp, t * 128:(t + 1) * 128], start=True, stop=True)
                        return last
                    k.op("tensor", qk, reads=[r_KT[kt] for kt in kts] + [r_QT[t]], writes=[rsp])
                    if nbias:
                        k.op("vector", lambda e, sb_=sb_, sp=sp, bs=bs, cls=cls: e.tensor_tensor(out=sb_[:], in0=sp[:, 0:5, :], in1=bs[:, cls, :, :], op=ALU.add),
                             reads=[rsp, rbs], writes=[rsb])
                        k.op("scalar", lambda e, pp=pp, sb_=sb_: e.activation(out=pp[:, 0:5, :], in_=sb_[:], func=AF.Exp), reads=[rsb], writes=[rpp])
                        k.op("scalar", lambda e, pp=pp, sp=sp: e.activation(out=pp[:, 5:7, :], in_=sp[:, 5:7, :], func=AF.Exp), reads=[rsp, rpp], writes=[rpp])
                    else:
                        k.op("scalar", lambda e, pp=pp, sp=sp: e.activation(out=pp[:, 0:2, :], in_=sp[:, 0:2, :], func=AF.Exp), reads=[rsp], writes=[rpp])

                    def pv(e, o_=o_, pp=pp, kts=kts, h=h, nk=nk):
                        last = None
                        for i, kt in enumerate(kts):
                            last = e.matmul(o_[:, 0:65], lhsT=pp[:, i, :], rhs=Va[:, kt, h, :], start=(i == 0), stop=(i == nk - 1))
                        return last
                    k.op("tensor", pv, reads=[rpp] + [r_Va[kt] for kt in kts], writes=[ro])
                    k.op("vector", lambda e, rcc=rcc, o_=o_: e.reciprocal(out=rcc, in_=o_[:, 64:65]), reads=[ro], writes=[rrc])
                    k.op("vector", lambda e, rcc=rcc, o_=o_, t=t, h=h: e.tensor_scalar(out=yna[:, t, h * 64:(h + 1) * 64], in0=o_[:, 0:64],
                                                                                   scalar1=rcc, scalar2=None, op0=ALU.mult),
                         reads=[ro, rrc], writes=[r_yna[t]])
            for t in range(NT):
                k.dma(yd[t * 128:(t + 1) * 128, :], yna[:, t, :], reads=[r_yna[t]], writes=[r_yd], q="gpsimd")
            k.barrier()

    def stage_mla(self, l):
        k = self.k
        z, _ = self.dr["z%d" % l]
        r_zt = self.r_ztile
        yd, r_yd = self.dram("y_mla%d" % l, [T, 512], F32)
        MSC = 192.0 ** -0.5
        with ExitStack() as es:
            QTn = k.sb("ml_QTn", [128, 4, T], BF16, es)
            KTn = k.sb("ml_KTn", [128, 4, T], BF16, es)
            QTp = k.sb("ml_QTp", [128, 2, T], BF16, es)
            KTp = k.sb("ml_KTp", [128, 2, T], BF16, es)
            r_Q = [R() for _ in range(NT)]
            r_K = [R() for _ in range(NT)]
            Va = k.sb("ml_V", [128, NT, 4, 129], BF16, es)
            r_Va = [R() for _ in range(NT)]
            ym = k.sb("ml_y", [128, NT, 512], F32, es)
            r_ym = [R() for _ in range(NT)]
            k.op("gpsimd", lambda e: e.memset(Va[:, :, :, 128:129], 1.0), writes=r_Va)
            with ExitStack() as es2:
                self.r_gain = R()
                g_cq = k.sb("ml_gcq", [128, 512], F32, es2)
                g_ckv = k.sb("ml_gckv", [128, 256], F32, es2)
                g_q = k.sb("ml_gq", [128, 768], F32, es2)
                g_kn = k.sb("ml_gkn", [128, 512], F32, es2)
                g_kp = k.sb("ml_gkp", [128, 64], F32, es2)
                for tl, nm in ((g_cq, "mla_cqg"), (g_ckv, "mla_ckvg"), (g_q, "mla_qg"), (g_kn, "mla_kng"), (g_kp, "mla_kpg")):
                    ap, rr = self.dr[nm]
                    k.dma(tl[:], ap[l].partition_broadcast(128), reads=[rr], writes=[self.r_gain])
                k.op("vector", lambda e: e.tensor_scalar(out=g_q[:], in0=g_q[:], scalar1=MSC, scalar2=None, op0=ALU.mult),
                     reads=[self.r_gain], writes=[self.r_gain])
                wuq = k.sb("ml_wuq", [128, 4, 768], BF16, es2)
                wukv = k.sb("ml_wukv", [128, 2, 1024], BF16, es2)
                r_w = R()
                wst = k.sb("ml_wst", [128, 4, 768], F32, es2)
                r_wst = R()
                ap, rr = self.dr["mla_w_uq"]
                k.dma(wst[:], ap[l].rearrange("(kc p) n -> p kc n", p=128), reads=[rr], writes=[r_wst])
                k.op("vector", lambda e: e.tensor_copy(out=wuq[:], in_=wst[:]), reads=[r_wst], writes=[r_w])
                ap, rr = self.dr["mla_w_ukv"]
                wst2 = wst[:].rearrange("p a b -> p (a b)")[:, 0:2048].rearrange("p (a b) -> p a b", a=2)
                k.dma(wst2, ap[l].rearrange("(kc p) n -> p kc n", p=128), reads=[rr], writes=[r_wst])
                k.op("vector", lambda e: e.tensor_copy(out=wukv[:], in_=wst2), reads=[r_wst], writes=[r_w])
                cos_d, r_cos = self.dr["rope_cos"]
                sin_d, r_sin = self.dr["rope_sin"]
                xin = [k.sb("ml_x%d" % i, [128, 832], F32, es2) for i in range(2)]
                r_xin = [R() for _ in range(2)]
                cs = [k.sb("ml_cs%d" % i, [128, 2, 64], F32, es2) for i in range(2)]
                r_cs = [R() for _ in range(2)]
                junk = k.sb("ml_junk", [128, 1024], F32, es2)
                r_junk = R()
                tmp = k.sb("ml_tmp", [128, 1024], F32, es2)
                r_tmp = R()
                st = k.sb("ml_st", [128, NT, 16], F32, es2)
                cb = k.sb("ml_cb", [128, 768], BF16, es2)
                r_cb = [R(), R()]
                cT = k.sb("ml_cT", [128, 6, 128], BF16, es2)
                r_cT = [R(), R()]
                qf = k.sb("ml_qf", [128, 768], F32, es2)
                r_qf = R()
                qn = k.sb("ml_qn", [128, 768], F32, es2)
                r_qn = R()
                kvf = k.sb("ml_kvf", [128, 1024], F32, es2)
                r_kvf = R()
                qb = k.sb("ml_qb", [128, 4, 128], BF16, es2)
                qpb = k.sb("ml_qpb", [128, 4, 64], BF16, es2)
                kb = k.sb("ml_kb", [128, 4, 128], BF16, es2)
                kpb = k.sb("ml_kpb", [128, 4, 64], BF16, es2)
                r_qb, r_qpb, r_kb, r_kpb = R(), R(), R(), R()
                rp1 = k.sb("ml_rp1", [128, 4, 64], F32, es2)
                rp2 = k.sb("ml_rp2", [128, 4, 64], F32, es2)
                r_rp1, r_rp2 = R(), R()
                kpg = k.sb("ml_kpgt", [128, 64], F32, es2)
                kpr = k.sb("ml_kpr", [128, 64], F32, es2)
                r_kpg, r_kpr = R(), R()
                ptr = k.ps("ml_ptr", [128, 8, 128], BF16, es2)
                r_ptr = PR()
                pq = k.ps("ml_pq", [128, 2, 512], F32, es2)
                r_pq = PR()
                pkv = k.ps("ml_pkv", [128, 2, 512], F32, es2)
                r_pkv = PR()

                def rope(x3, nh, cs_, out3, reads, writes):
                    cosb = cs_[:, 0, :].unsqueeze(1).to_broadcast([128, nh, 64])
                    k.op("vector", lambda e: e.tensor_tensor(out=rp1[:, 0:nh, :], in0=x3, in1=cosb, op=ALU.mult), reads=reads, writes=[r_rp1])
                    x5 = x3.rearrange("p h (g a d) -> p h g a d", g=2, a=2)
                    o5 = rp2[:, 0:nh, :].rearrange("p h (g a d) -> p h g a d", g=2, a=2)
                    s4 = cs_[:, 1, :].rearrange("p (g a d) -> p g a d", g=2, a=2)
                    for h in range(nh):
                        for a in range(2):
                            k.op("gpsimd", lambda e, h=h, a=a: e.tensor_tensor(out=o5[:, h, :, a, :], in0=x5[:, h, :, 1 - a, :], in1=s4[:, :, a, :], op=ALU.mult),
                                 reads=reads, writes=[r_rp2])
                    k.op("vector", lambda e: e.tensor_tensor(out=out3, in0=rp1[:, 0:nh, :], in1=rp2[:, 0:nh, :], op=ALU.add),
                         reads=[r_rp1, r_rp2], writes=writes)

                for t in range(NT):
                    rows = slice(t * 128, (t + 1) * 128)
                    x_, rx = xin[t % 2], r_xin[t % 2]
                    cs_, rcs = cs[t % 2], r_cs[t % 2]
                    k.dma(x_[:, 0:512], z[rows, C_CQ:C_CQ + 512], reads=[r_zt[t]], writes=[rx])
                    k.dma(x_[:, 512:832], z[rows, C_CKV:C_CKV + 320], reads=[r_zt[t]], writes=[rx])
                    k.dma(cs_[:, 0, :], cos_d[rows, :], reads=[r_cos], writes=[rcs])
                    k.dma(cs_[:, 1, :], sin_d[rows, :], reads=[r_sin], writes=[rcs])
                    self.head_rms(None, x_[:, 0:512], 512, 1, 512, st[:, t, 0:1], junk[:, 0:512], cb[:, 0:512], g_cq[:],
                                  [rx], R(), r_junk, [r_cb[0]], tmp[:, 0:512], r_tmp)
                    self.head_rms(None, x_[:, 512:768], 256, 1, 256, st[:, t, 1:2], junk[:, 0:256], cb[:, 512:768], g_ckv[:],
                                  [rx], R(), r_junk, [r_cb[1]], tmp[:, 0:256], r_tmp)

                    def tr1(e):
                        last = None
                        for j in range(6):
                            last = e.transpose(out=ptr[:, j, :], in_=cb[:, j * 128:(j + 1) * 128], identity=self.ident[:])
                        return last
                    k.op("tensor", tr1, reads=r_cb + [self.r_ident], writes=[r_ptr])
                    k.op("scalar", lambda e: e.activation(out=cT[:], in_=ptr[:, 0:6, :], func=AF.Copy), reads=[r_ptr], writes=r_cT)

                    def mmq(e):
                        last = None
                        for hf in range(2):
                            for kc in range(4):
                                last = e.matmul(pq[:, hf, 0:384], lhsT=cT[:, kc, :], rhs=wuq[:, kc, hf * 384:(hf + 1) * 384],
                                                start=(kc == 0), stop=(kc == 3))
                        return last
                    k.op("tensor", mmq, reads=r_cT + [r_w], writes=[r_pq])

                    def mmkv(e):
                        last = None
                        for hf in range(2):
                            for kc in range(2):
                                last = e.matmul(pkv[:, hf, :], lhsT=cT[:, 4 + kc, :], rhs=wukv[:, kc, hf * 512:(hf + 1) * 512],
                                                start=(kc == 0), stop=(kc == 1))
                        return last
                    k.op("tensor", mmkv, reads=r_cT + [r_w], writes=[r_pkv])
                    k.op("scalar", lambda e: e.activation(out=qf[:].rearrange("p (a b) -> p a b", a=2), in_=pq[:, :, 0:384], func=AF.Copy),
                         reads=[r_pq], writes=[r_qf])
                    k.op("vector", lambda e: e.tensor_copy(out=kvf[:].rearrange("p (a b) -> p a b", a=2), in_=pkv[:]), reads=[r_pkv], writes=[r_kvf])
                    self.head_rms(None, qf[:], 768, 4, 192, st[:, t, 2:6], junk[:, 0:768], qn[:], g_q[:], [r_qf], R(), r_junk, [r_qn],
                                  tmp[:, 0:768], r_tmp)
                    qn3 = qn[:].rearrange("p (h d) -> p h d", h=4)
                    k.op("vector", lambda e: e.tensor_copy(out=qb[:], in_=qn3[:, :, 0:128]), reads=[r_qn], writes=[r_qb])
                    rope(qn3[:, :, 128:192], 4, cs_, qpb[:], [r_qn, rcs], [r_qpb])
                    kv4 = kvf[:].rearrange("p (h s d) -> p h s d", h=4, s=2)
                    k.op("vector", lambda e, t=t: e.tensor_copy(out=Va[:, t, :, 0:128], in_=kv4[:, :, 1, :]), reads=[r_kvf], writes=[r_Va[t]])
                    kpe = x_[:, 768:832]
                    r_sk = R()
                    k.op("vector", lambda e: e.tensor_tensor(out=junk[:, 0:512].rearrange("p (h d) -> p h d", h=4), in0=kv4[:, :, 0, :], in1=kv4[:, :, 0, :], op=ALU.mult),
                         reads=[r_kvf], writes=[r_junk])
                    k.op("vector", lambda e, t=t: e.tensor_reduce(out=st[:, t, 6:10], in_=junk[:, 0:512].rearrange("p (h d) -> p h d", h=4), axis=AX.X, op=ALU.add),
                         reads=[r_junk], writes=[r_sk])
                    k.op("vector", lambda e: e.tensor_tensor(out=junk[:, 512:576], in0=kpe, in1=kpe, op=ALU.mult), reads=[rx], writes=[r_junk])
                    k.op("vector", lambda e, t=t: e.tensor_reduce(out=st[:, t, 10:11], in_=junk[:, 512:576], axis=AX.X, op=ALU.add), reads=[r_junk], writes=[r_sk])
                    k.op("vector", lambda e, t=t: e.tensor_scalar(out=st[:, t, 6:10], in0=st[:, t, 6:10], scalar1=st[:, t, 10:11], scalar2=None, op0=ALU.add),
                         reads=[r_sk], writes=[r_sk])
                    k.op("scalar", lambda e, t=t: e.activation(out=st[:, t, 6:10], in_=st[:, t, 6:10], func=AF.Sqrt, bias=self.eps_t[:, 0:1], scale=1.0 / 192),
                         reads=[r_sk], writes=[r_sk])
                    k.op("vector", lambda e, t=t: e.reciprocal(out=st[:, t, 6:10], in_=st[:, t, 6:10]), reads=[r_sk], writes=[r_sk])
                    k.op("vector", lambda e, t=t: e.tensor_tensor(out=tmp[:, 0:512].rearrange("p (h d) -> p h d", h=4), in0=kv4[:, :, 0, :],
                                                                 in1=st[:, t, 6:10].unsqueeze(2).to_broadcast([128, 4, 128]), op=ALU.mult),
                         reads=[r_kvf, r_sk], writes=[r_tmp])
                    k.op("gpsimd", lambda e: e.tensor_tensor(out=kb[:].rearrange("p h d -> p (h d)"), in0=tmp[:, 0:512], in1=g_kn[:], op=ALU.mult),
                         reads=[r_tmp, self.r_gain], writes=[r_kb])
                    k.op("vector", lambda e: e.tensor_tensor(out=kpg[:], in0=kpe, in1=g_kp[:], op=ALU.mult), reads=[rx, self.r_gain], writes=[r_kpg])
                    rope(kpg[:].unsqueeze(1), 1, cs_, kpr[:].unsqueeze(1), [r_kpg, rcs], [r_kpr])
                    k.op("vector", lambda e, t=t: e.tensor_tensor(out=kpb[:], in0=kpr[:].unsqueeze(1).to_broadcast([128, 4, 64]),
                                                                 in1=st[:, t, 6:10].unsqueeze(2).to_broadcast([128, 4, 64]), op=ALU.mult),
                         reads=[r_kpr, r_sk], writes=[r_kpb])
                    for (nb_, pb_, rnb, rpb_, dn, dp, rd) in ((qb, qpb, r_qb, r_qpb, QTn, QTp, r_Q[t]), (kb, kpb, r_kb, r_kpb, KTn, KTp, r_K[t])):
                        def tr2(e, nb_=nb_, pb_=pb_):
                            last = None
                            for h in range(4):
                                last = e.transpose(out=ptr[:, h, :], in_=nb_[:, h, :], identity=self.ident[:])
                            for i in range(2):
                                last = e.transpose(out=ptr[:, 4 + i, :], in_=pb_[:, 2 * i:2 * i + 2, :].rearrange("p a b -> p (a b)"), identity=self.ident[:])
                            return last
                        k.op("tensor", tr2, reads=[rnb, rpb_, self.r_ident], writes=[r_ptr])
                        k.op("scalar", lambda e, dn=dn, rows=rows: e.activation(out=dn[:, :, rows], in_=ptr[:, 0:4, :], func=AF.Copy), reads=[r_ptr], writes=[rd])
                        k.op("vector", lambda e, dp=dp, rows=rows: e.tensor_copy(out=dp[:, :, rows], in_=ptr[:, 4:6, :]), reads=[r_ptr], writes=[rd])
                k.barrier()
            with ExitStack() as es3:
                stp = [k.ps("ml_st%d" % i, [128, 512], F32, es3) for i in range(2)]
                r_stp = [PR() for _ in range(2)]
                acc = [k.ps("ml_acc%d" % i, [128, 512], F32, es3) for i in range(4)]
                r_acc = [PR() for _ in range(4)]
                PT = [k.sb("ml_PT%d" % i, [128, 512], BF16, es3) for i in range(3)]
                r_PT = [R() for _ in range(3)]
                rc = k.sb("ml_rc", [128, 4], F32, es3)
                r_rc = [R() for _ in range(4)]
                groups = [[0, 1]] + [[2 + 4 * g + i for i in range(4)] for g in range(4)]
                it = 0
                for grp in groups:
                    kts = [0, 1] if grp[0] == 0 else list(range(NT))
                    q0 = grp[0] * 128
                    nq = len(grp) * 128
                    for h in range(4):
                        hp, hs = h // 2, (h % 2) * 64
                        for ki, kt in enumerate(kts):
                            sp, rsp = stp[it % 2], r_stp[it % 2]
                            pp, rpp = PT[it % 3], r_PT[it % 3]
                            it += 1

                            def qk(e, sp=sp, kt=kt, h=h, hp=hp, hs=hs, q0=q0, nq=nq):
                                e.matmul(sp[:, 0:nq], lhsT=KTn[:, h, kt * 128:(kt + 1) * 128], rhs=QTn[:, h, q0:q0 + nq], start=True, stop=False)
                                return e.matmul(sp[:, 0:nq], lhsT=KTp[hs:hs + 64, hp, kt * 128:(kt + 1) * 128], rhs=QTp[hs:hs + 64, hp, q0:q0 + nq],
                                                start=False, stop=True)
                            k.op("tensor", qk, reads=[r_K[kt]] + [r_Q[t] for t in grp], writes=[rsp])
                            k.op("scalar", lambda e, pp=pp, sp=sp, nq=nq: e.activation(out=pp[:, 0:nq], in_=sp[:, 0:nq], func=AF.Exp), reads=[rsp], writes=[rpp])
                            for i, t in enumerate(grp):
                                k.op("tensor", lambda e, i=i, pp=pp, kt=kt, h=h, ki=ki, kts=kts: e.matmul(
                                    acc[i][:, 0:129], lhsT=pp[:, i * 128:(i + 1) * 128], rhs=Va[:, kt, h, :], start=(ki == 0), stop=(ki == len(kts) - 1)),
                                    reads=[rpp, r_Va[kt]], writes=[r_acc[i]])
                        for i, t in enumerate(grp):
                            k.op("vector", lambda e, i=i: e.reciprocal(out=rc[:, i:i + 1], in_=acc[i][:, 128:129]), reads=[r_acc[i]], writes=[r_rc[i]])
                            k.op("vector", lambda e, i=i, t=t, h=h: e.tensor_scalar(out=ym[:, t, h * 128:(h + 1) * 128], in0=acc[i][:, 0:128],
                                                                                  scalar1=rc[:, i:i + 1], scalar2=None, op0=ALU.mult),
                                 reads=[r_acc[i], r_rc[i]], writes=[r_ym[t]])
                for t in range(NT):
                    k.dma(yd[t * 128:(t + 1) * 128, :], ym[:, t, :], reads=[r_ym[t]], writes=[r_yd])
                k.barrier()

    def stage_fourier(self, l):
        k = self.k
        z, _ = self.dr["z%d" % l]
        r_zt = self.r_ztile
        yd, r_yd = self.dram("y_fn%d" % l, [T, 512], F32)
        csd, r_csd = self.dr["dft_c"]
        dL, r_dL = self.dr["dft_L"]
        dC, r_dC = self.dr["dft_C"]
        with ExitStack() as es:
            G = k.sb("fn_G", [128, NT, 2, 512], BF16, es)
            r_G = [R() for _ in range(NT)]
            CS = k.sb("fn_CS", [128, 256], BF16, es)
            r_CS = R()
            k.dma(CS[:], csd, reads=[r_csd], writes=[r_CS])
            xin = [k.sb("fn_x%d" % i, [128, 512], F32, es) for i in range(2)]
            r_xin = [R() for _ in range(2)]
            xb = [k.sb("fn_xb%d" % i, [128, 512], BF16, es) for i in range(2)]
            r_xb = [R() for _ in range(2)]
            xT = [k.sb("fn_xT%d" % i, [128, 4, 128], BF16, es) for i in range(2)]
            r_xT = [R() for _ in range(2)]
            ptr = k.ps("fn_ptr", [128, 8, 128], BF16, es)
            r_ptr = PR()
            pg = [k.ps("fn_pg%d" % i, [128, 2, 256], F32, es) for i in range(2)]
            r_pg = [PR() for _ in range(2)]
            py = [k.ps("fn_py%d" % i, [128, 512], F32, es) for i in range(2)]
            r_py = [PR() for _ in range(2)]
            for t in range(NT):
                x_, rx = xin[t % 2], r_xin[t % 2]
                b_, rb = xb[t % 2], r_xb[t % 2]
                T_, rT = xT[t % 2], r_xT[t % 2]
                k.dma(x_[:], z[t * 128:(t + 1) * 128, C_FN:C_FN + 512], reads=[r_zt[t]], writes=[rx])
                k.op("gpsimd", lambda e, x_=x_, b_=b_: e.tensor_copy(out=b_[:], in_=x_[:]), reads=[rx], writes=[rb])

                def tr(e, b_=b_):
                    last = None
                    for g in range(4):
                        last = e.transpose(out=ptr[:, g, :], in_=b_[:, g * 128:(g + 1) * 128], identity=self.ident[:])
                    return last
                k.op("tensor", tr, reads=[rb, self.r_ident], writes=[r_ptr])
                k.op("scalar", lambda e, T_=T_: e.activation(out=T_[:], in_=ptr[:, 0:4, :], func=AF.Copy), reads=[r_ptr], writes=[rT])
                for hf in range(2):
                    p, rp = pg[hf], r_pg[hf]

                    def mm(e, p=p, T_=T_, hf=hf):
                        last = None
                        for gg in range(2):
                            last = e.matmul(p[:, gg, :], lhsT=T_[:, hf * 2 + gg, :], rhs=CS[:], start=True, stop=True)
                        return last
                    k.op("tensor", mm, reads=[rT, r_CS], writes=[rp])
                    G4 = G[:, t, :, :].rearrange("p c (g d) -> p c g d", g=4)
                    if hf == 0:
                        k.op("vector", lambda e, p=p, G4=G4, hf=hf: e.tensor_copy(out=G4[:, :, 0:2, :].rearrange("p c g d -> p g c d"),
                                                                                in_=p[:].rearrange("p g (c d) -> p g c d", c=2)),
                             reads=[rp], writes=[r_G[t]])
                    else:
                        k.op("scalar", lambda e, p=p, G4=G4, hf=hf: e.activation(out=G4[:, :, 2:4, :].rearrange("p c g d -> p g c d"),
                                                                                in_=p[:].rearrange("p g (c d) -> p g c d", c=2), func=AF.Copy),
                             reads=[rp, r_G[t]], writes=[r_G[t]])
            Dm = [k.sb("fn_D%d" % i, [128, 2, 16, 128], BF16, es) for i in range(2)]
            r_Dm = [R() for _ in range(2)]
            yo = [k.sb("fn_yo%d" % i, [128, 512], F32, es) for i in range(2)]
            r_yo = [R() for _ in range(2)]
            for to in range(NT):
                ctx = to < 2
                nin = 2 if ctx else 16
                t0 = 0 if ctx else 2
                src = dC if ctx else dL
                rsrc = r_dC if ctx else r_dL
                col0 = (to - t0) * 128
                D_, rD = Dm[to % 2], r_Dm[to % 2]
                for c in range(2):
                    k.dma(D_[:, c, 0:nin, :], src[c, :, col0:col0 + 128].rearrange("(nt p) q -> p nt q", p=128), reads=[rsrc], writes=[rD])
                p, rp = py[to % 2], r_py[to % 2]

                def mm2(e, p=p, D_=D_, nin=nin, t0=t0):
                    last = None
                    n = 0
                    for nt in range(nin):
                        for c in range(2):
                            last = e.matmul(p[:], lhsT=D_[:, c, nt, :], rhs=G[:, t0 + nt, c, :], start=(n == 0), stop=(n == 2 * nin - 1))
                            n += 1
                    return last
                k.op("tensor", mm2, reads=[rD] + [r_G[t0 + nt] for nt in range(nin)], writes=[rp])
                o_, ro = yo[to % 2], r_yo[to % 2]
                k.op("vector", lambda e, o_=o_, p=p: e.tensor_copy(out=o_[:], in_=p[:]), reads=[rp], writes=[ro])
                k.dma(yd[to * 128:(to + 1) * 128, :], o_[:], reads=[ro], writes=[r_yd])
            k.barrier()

    RWC = 3072

    def stage_rwprep(self, l):
        k = self.k
        z, _ = self.dr["z%d" % l]
        r_zt = self.r_ztile
        rwd, r_rwd = self.dram("rwd%d" % l, [2, T, self.RWC], F32)
        rwo, r_rwo = self.dram("rwo%d" % l, [T, 608], F32)
        self.r_rwd_t = [R() for _ in range(NT)]
        with ExitStack() as es:
            def bc(nm, n):
                tl = k.sb("rp_" + nm, [128, n], F32, es)
                ap, rr = self.dr[nm]
                k.dma(tl[:], ap[l].partition_broadcast(128), reads=[rr], writes=[r_c])
                return tl
            r_c = R()
            mu = bc("rw_mu", 1760)
            kkg = bc("rw_kk", 512)
            kag = bc("rw_ka", 512)
            w0 = bc("rw_w0f", 1024)
            a0 = bc("rw_a0f", 1024)
            w2 = k.sb("rp_w2", [32, 2, 2, 512], F32, es)
            ap, rr = self.dr["rw_wa2"]
            k.dma(w2[:], ap[l], reads=[rr], writes=[r_c])
            cur = [k.sb("rp_cur%d" % i, [128, 1760], F32, es) for i in range(2)]
            prv = [k.sb("rp_prv%d" % i, [128, 1760], F32, es) for i in range(2)]
            nxt = [k.sb("rp_nxt%d" % i, [128, 1760], F32, es) for i in range(2)]
            r_cur = [R() for _ in range(2)]
            r_prv = [R() for _ in range(2)]
            r_nxt = [R() for _ in range(2)]
            mx = k.sb("rp_mx", [128, 1760], F32, es)
            r_mx = R()
            d_ = k.sb("rp_d", [128, 1760], F32, es)
            r_d = R()
            st = k.sb("rp_st", [128, NT, 8], F32, es)
            lo_in = k.sb("rp_loin", [128, 128], F32, es)
            r_loin = R()
            lT = k.sb("rp_lT", [32, 4, 128], F32, es)
            r_lT = R()
            ob = [k.sb("rp_ob%d" % i, [128, 2, self.RWC], F32, es) for i in range(1)]
            r_ob = [[R() for _ in range(8)] for _ in range(1)]
            oo = k.sb("rp_oo", [128, 608], F32, es)
            r_oo = R()
            tw = k.sb("rp_tw", [128, 512], F32, es)
            r_tw = R()
            ta = k.sb("rp_ta", [128, 2, 512], F32, es)
            r_ta = [R(), R()]
            ptr = k.ps("rp_ptr", [128, 512], F32, es)
            r_ptr = PR()
            pw = [k.ps("rp_pw%d" % i, [128, 512], F32, es) for i in range(4)]
            r_pw = [PR() for _ in range(4)]
            for t in range(NT):
                rows = slice(t * 128, (t + 1) * 128)
                c_, rc_ = cur[t % 2], r_cur[t % 2]
                p_, rp_ = prv[t % 2], r_prv[t % 2]
                n_, rn_ = nxt[t % 2], r_nxt[t % 2]
                first = t in (0, 2)
                last_ = t in (1, NT - 1)
                for dst, rdst, off, lo_, hi_ in ((c_, rc_, 0, 0, 128), (p_, rp_, -1, 1 if first else 0, 128), (n_, rn_, 1, 0, 127 if last_ else 128)):
                    if hi_ - lo_ < 128:
                        k.op("gpsimd", lambda e, dst=dst: e.memset(dst[:], 0.0), writes=[rdst])
                    r0 = t * 128 + off
                    zt = [r_zt[t]] + ([r_zt[t - 1]] if off < 0 and t > 0 else []) + ([r_zt[t + 1]] if off > 0 and t < NT - 1 else [])
                    k.dma(dst[lo_:hi_, 0:1152], z[r0 + lo_:r0 + hi_, C_RWKS:C_RWKS + 1152], reads=zt, writes=[rdst])
                    k.dma(dst[lo_:hi_, 1152:1760], z[r0 + lo_:r0 + hi_, C_RWQS:C_RWQS + 608], reads=zt, writes=[rdst])
                k.op("vector", lambda e, p_=p_, n_=n_: e.tensor_tensor(out=d_[:], in0=p_[:], in1=n_[:], op=ALU.add), reads=[rp_, rn_], writes=[r_d])
                k.op("vector", lambda e, c_=c_: e.scalar_tensor_tensor(out=d_[:], in0=d_[:], scalar=0.5, in1=c_[:], op0=ALU.mult, op1=ALU.subtract),
                     reads=[r_d, rc_], writes=[r_d])
                k.op("gpsimd", lambda e: e.tensor_tensor(out=d_[:], in0=d_[:], in1=mu[:], op=ALU.mult), reads=[r_d, r_c], writes=[r_d])
                k.op("vector", lambda e, c_=c_: e.tensor_tensor(out=mx[:], in0=d_[:], in1=c_[:], op=ALU.add), reads=[r_d, rc_], writes=[r_mx])
                o_, ro = ob[0], r_ob[0]
                kx = mx[:, 0:512]
                vx = mx[:, 512:1024]
                rx_ = mx[:, 1152:1664]
                r_s = R()
                kk_o = o_[:, 0, 0:512]
                k.op("vector", lambda e: e.tensor_tensor(out=kk_o, in0=kx, in1=kkg[:], op=ALU.mult), reads=[r_mx, r_c], writes=[ro[0]])
                k.op("vector", lambda e: e.tensor_tensor(out=d_[:, 0:512], in0=kk_o, in1=kk_o, op=ALU.mult), reads=[ro[0]], writes=[r_d])
                k.op("vector", lambda e, t=t: e.tensor_reduce(out=st[:, t, :], in_=d_[:, 0:512].rearrange("p (h d) -> p h d", h=8), axis=AX.X, op=ALU.add),
                     reads=[r_d], writes=[r_s])
                k.op("scalar", lambda e, t=t: e.activation(out=st[:, t, :], in_=st[:, t, :], func=AF.Sqrt), reads=[r_s], writes=[r_s])
                k.op("vector", lambda e, t=t: e.tensor_scalar(out=st[:, t, :], in0=st[:, t, :], scalar1=1e-12, scalar2=None, op0=ALU.max), reads=[r_s], writes=[r_s])
                k.op("vector", lambda e, t=t: e.reciprocal(out=st[:, t, :], in_=st[:, t, :]), reads=[r_s], writes=[r_s])
                k.op("vector", lambda e, t=t: e.tensor_tensor(out=kk_o.rearrange("p (h d) -> p h d", h=8), in0=kk_o.rearrange("p (h d) -> p h d", h=8),
                                                             in1=st[:, t, :].unsqueeze(2).to_broadcast([128, 8, 64]), op=ALU.mult),
                     reads=[ro[0], r_s], writes=[ro[0]])
                k.op("gpsimd", lambda e: e.tensor_copy(out=o_[:, 1, 0:512], in_=kk_o), reads=[ro[0]], writes=[ro[1]])
                k.op("gpsimd", lambda e: e.tensor_copy(out=o_[:, :, 2048:2560], in_=rx_.unsqueeze(1).to_broadcast([128, 2, 512])), reads=[r_mx], writes=[ro[2]])
                k.op("gpsimd", lambda e: e.tensor_copy(out=o_[:, :, 2560:3072], in_=vx.unsqueeze(1).to_broadcast([128, 2, 512])), reads=[r_mx], writes=[ro[3]])
                k.op("scalar", lambda e: e.activation(out=lo_in[:, 0:64], in_=mx[:, 1024:1088], func=AF.Tanh), reads=[r_mx], writes=[r_loin])
                k.op("vector", lambda e: e.tensor_copy(out=lo_in[:, 64:128], in_=mx[:, 1088:1152]), reads=[r_mx, r_loin], writes=[r_loin])

                def trl(e):
                    last = None
                    for j in range(4):
                        last = e.transpose(out=ptr[0:32, j * 128:(j + 1) * 128], in_=lo_in[:, j * 32:(j + 1) * 32], identity=self.identf[:])
                    return last
                k.op("tensor", trl, reads=[r_loin, self.r_ident], writes=[r_ptr])
                k.op("vector", lambda e: e.tensor_copy(out=lT[:], in_=ptr[0:32, :].rearrange("p (a b) -> p a b", a=4)), reads=[r_ptr], writes=[r_lT])
                for d in range(2):
                    p, rp = pw[d], r_pw[d]
                    k.op("tensor", lambda e, p=p, d=d: e.matmul(p[:], lhsT=lT[:, d, :], rhs=w2[:, 0, d, :], start=True, stop=True), reads=[r_lT, r_c], writes=[rp])
                    k.op("vector", lambda e, p=p, d=d: e.tensor_tensor(out=tw[:], in0=p[:], in1=w0[:, d * 512:(d + 1) * 512], op=ALU.add), reads=[rp, r_c], writes=[r_tw])
                    k.op("scalar", lambda e: e.activation(out=tw[:], in_=tw[:], func=AF.Sigmoid), reads=[r_tw], writes=[r_tw])
                    k.op("scalar", lambda e, d=d: e.activation(out=o_[:, d, 512:1024], in_=tw[:], func=AF.Exp, scale=-float(np.exp(-0.5))), reads=[r_tw], writes=[ro[4 + d]])
                    p, rp = pw[2 + d], r_pw[2 + d]
                    k.op("tensor", lambda e, p=p, d=d: e.matmul(p[:], lhsT=lT[:, 2 + d, :], rhs=w2[:, 1, d, :], start=True, stop=True), reads=[r_lT, r_c], writes=[rp])
                    k.op("vector", lambda e, p=p, d=d: e.tensor_tensor(out=ta[:, d, :], in0=p[:], in1=a0[:, d * 512:(d + 1) * 512], op=ALU.add), reads=[rp, r_c], writes=[r_ta[d]])
                    k.op("scalar", lambda e, d=d: e.activation(out=ta[:, d, :], in_=ta[:, d, :], func=AF.Sigmoid), reads=[r_ta[d]], writes=[r_ta[d]])
                    k.op("gpsimd", lambda e, d=d: e.tensor_tensor(out=o_[:, d, 1024:1536], in0=kk_o, in1=ta[:, d, :], op=ALU.mult), reads=[ro[0], r_ta[d]], writes=[ro[6 + d]])
                    k.op("vector", lambda e, d=d: e.scalar_tensor_tensor(out=ta[:, d, :], in0=ta[:, d, :], scalar=-1.0, in1=kag[:], op0=ALU.add, op1=ALU.mult),
                         reads=[r_ta[d], ro[6 + d], r_c], writes=[r_ta[d]])
                    k.op("vector", lambda e, d=d: e.scalar_tensor_tensor(out=o_[:, d, 1536:2048], in0=ta[:, d, :], scalar=1.0, in1=kx, op0=ALU.add, op1=ALU.mult),
                         reads=[r_ta[d], r_mx], writes=[ro[6 + d]])
                k.op("vector", lambda e: e.tensor_tensor(out=oo[:, 0:512], in0=o_[:, 0, 1536:2048], in1=o_[:, 1, 1536:2048], op=ALU.add), reads=[ro[6], ro[7]], writes=[r_oo])
                k.op("gpsimd", lambda e: e.tensor_copy(out=oo[:, 512:608], in_=mx[:, 1664:1760]), reads=[r_mx, r_oo], writes=[r_oo])
                for d in range(2):
                    k.dma(rwd[d, rows, :], o_[:, d, :], reads=ro, writes=[self.r_rwd_t[t]])
                k.dma(rwo[rows, :], oo[:], reads=[r_oo], writes=[r_rwo])
            k.barrier()

    def stage_rwscan(self, l):
        k = self.k
        rwd, _ = self.dr["rwd%d" % l]
        r_rwd_t = self.r_rwd_t
        yfb, r_yfb = self.dram("rwy%d" % l, [2, T, 512], F32)
        self.r_rwy_t = [R() for _ in range(NT)]
        e2d, r_e2d = self.dr["rw_e2"]
        pmd, r_pmd = self.dr["rw_pm"]
        j64d, r_j64d = self.dr["rw_j64"]
        with ExitStack() as es:
            E2 = k.sb("rs_E2", [128, 64, 128], F32, es)
            Pm = k.sb("rs_Pm", [128, 128], F32, es)
            J64 = k.sb("rs_J64", [64, 64], F32, es)
            r_cst = R()
            for c in range(4):
                k.dma(E2[:, c * 16:(c + 1) * 16, :], e2d[:, c * 16:(c + 1) * 16, :], reads=[r_e2d], writes=[r_cst])
            k.dma(Pm[:], pmd, reads=[r_pmd], writes=[r_cst])
            k.dma(J64[:], j64d, reads=[r_j64d], writes=[r_cst])
            S = k.sb("rs_S", [128, 512], F32, es)
            r_S = R()
            k.op("vector", lambda e: e.memset(S[:], 0.0), writes=[r_S])
            X = [k.sb("rs_X%d" % i, [128, self.RWC], F32, es) for i in range(2)]
            r_X = [R() for _ in range(2)]
            Vr = k.sb("rs_Vr", [128, 8, 2, 64], F32, es)
            r_Vr = R()
            vT = [k.sb("rs_vT%d" % i, [128, 8, 64], F32, es) for i in range(2)]
            r_vT = [R() for _ in range(2)]
            ys = [k.sb("rs_ys%d" % i, [128, 8, 64], F32, es) for i in range(2)]
            r_ys = [R() for _ in range(2)]
            yo = k.sb("rs_yo", [64, 8, 2, 64], F32, es)
            r_yo = R()
            yb2 = k.sb("rs_yb2", [64, 512], F32, es)
            r_yb2 = R()
            yb3 = k.sb("rs_yb3", [64, 512], F32, es)
            r_yb3 = R()
            t1 = k.sb("rs_t1", [128, 512], F32, es)
            t2 = k.sb("rs_t2", [128, 512], F32, es)
            t3 = k.sb("rs_t3", [128, 512], F32, es)
            r_t1, r_t2, r_t3 = R(), R(), R()
            sa = k.sb("rs_sa", [128, 8], F32, es)
            r_sa = R()
            bcp = [k.ps("rs_bc%d" % i, [128, 512], F32, es) for i in range(5)]
            r_bcp = [PR() for _ in range(5)]
            pmisc = k.ps("rs_pm", [128, 8, 128], F32, es)
            r_pmisc = PR()
            pj = k.ps("rs_pj", [128, 512], F32, es)
            r_pj = PR()
            S3 = S[:].rearrange("p (h d) -> p h d", h=8)
            blk = 0
            nblk_dbg = self.dbg.get("RW_NBLK", 10 ** 9)
            for seg0, seglen in ((0, NCTX), (NCTX, L)):
                nb = seglen // 64
                for j in range(nb):
                    if blk >= nblk_dbg:
                        break
                    X_, rX = X[blk % 2], r_X[blk % 2]
                    vT_, rvT = vT[blk % 2], r_vT[blk % 2]
                    ys_, rys = ys[blk % 2], r_ys[blk % 2]
                    blk += 1
                    f0 = seg0 + 64 * j
                    b0 = seg0 + seglen - 64 * (j + 1)
                    k.dma(X_[0:64, :], rwd[0, f0:f0 + 64, :], reads=[r_rwd_t[f0 // 128]], writes=[rX])
                    k.dma(X_[64:128, :], rwd[1, b0:b0 + 64, :], reads=[r_rwd_t[b0 // 128]], writes=[rX])
                    k.op("gpsimd", lambda e, X_=X_: e.tensor_copy(out=Vr[:], in_=X_[:, 2560:3072].rearrange("p (h d) -> p h d", h=8).unsqueeze(2).to_broadcast([128, 8, 2, 64])),
                         reads=[rX], writes=[r_Vr])

                    def vtr(e):
                        last = None
                        for h in range(8):
                            last = e.matmul(pmisc[:, h, :], lhsT=Vr[:, h, :, :].rearrange("p a b -> p (a b)"), rhs=Pm[:], start=True, stop=True)
                        return last
                    k.op("tensor", vtr, reads=[r_Vr, r_cst], writes=[r_pmisc])
                    k.op("vector", lambda e, vT_=vT_: e.tensor_copy(out=vT_[0:64, :, :], in_=pmisc[0:64, :, 0:64]), reads=[r_pmisc], writes=[rvT])
                    k.op("vector", lambda e, vT_=vT_: e.tensor_copy(out=vT_[64:128, :, :], in_=pmisc[64:128, :, 64:128]), reads=[r_pmisc, rvT], writes=[rvT])
                    for i in range(64):
                        for q in range(5):
                            k.op("tensor", lambda e, q=q, i=i, X_=X_: e.matmul(bcp[q][:], lhsT=E2[:, i, :], rhs=X_[:, q * 512:(q + 1) * 512], start=True, stop=True),
                                 reads=[rX, r_cst], writes=[r_bcp[q]])
                        bkk, bdec, bb, bkd, br = [bcp[q][:].rearrange("p (h d) -> p h d", h=8) for q in range(5)]
                        k.op("vector", lambda e, bkk=bkk: e.tensor_tensor(out=t1[:].rearrange("p (h d) -> p h d", h=8), in0=S3, in1=bkk, op=ALU.mult),
                             reads=[r_S, r_bcp[0]], writes=[r_t1])
                        k.op("vector", lambda e: e.tensor_reduce(out=sa[:], in_=t1[:].rearrange("p (h d) -> p h d", h=8), axis=AX.X, op=ALU.add),
                             reads=[r_t1], writes=[r_sa])
                        k.op("vector", lambda e, bdec=bdec: e.tensor_tensor(out=S3, in0=S3, in1=bdec, op=ALU.mult), reads=[r_S, r_bcp[1]], writes=[r_S])
                        k.op("vector", lambda e, bb=bb: e.tensor_tensor(out=t2[:].rearrange("p (h d) -> p h d", h=8), in0=bb,
                                                                       in1=sa[:].unsqueeze(2).to_broadcast([128, 8, 64]), op=ALU.mult),
                             reads=[r_bcp[2], r_sa], writes=[r_t2])
                        k.op("vector", lambda e: e.tensor_tensor(out=S[:], in0=S[:], in1=t2[:], op=ALU.subtract), reads=[r_S, r_t2], writes=[r_S])
                        k.op("vector", lambda e, bkd=bkd, vT_=vT_, i=i: e.tensor_tensor(out=t3[:].rearrange("p (h d) -> p h d", h=8), in0=bkd,
                                                                                       in1=vT_[:, :, i:i + 1].to_broadcast([128, 8, 64]), op=ALU.mult),
                             reads=[r_bcp[3], rvT], writes=[r_t3])
                        k.op("vector", lambda e: e.tensor_tensor(out=S[:], in0=S[:], in1=t3[:], op=ALU.add), reads=[r_S, r_t3], writes=[r_S])
                        k.op("vector", lambda e, br=br: e.tensor_tensor(out=t1[:].rearrange("p (h d) -> p h d", h=8), in0=S3, in1=br, op=ALU.mult),
                             reads=[r_S, r_bcp[4]], writes=[r_t1])
                        k.op("vector", lambda e, ys_=ys_, i=i: e.tensor_reduce(out=ys_[:, :, i:i + 1].rearrange("p h o -> p (h o)"), in_=t1[:].rearrange("p (h d) -> p h d", h=8), axis=AX.X, op=ALU.add),
                             reads=[r_t1], writes=[rys])
                    def ytr(e, ys_=ys_):
                        last = None
                        for h in range(8):
                            last = e.matmul(pmisc[0:64, h, :], lhsT=ys_[:, h, :], rhs=self.identf[:], start=True, stop=True)
                        return last
                    k.op("tensor", ytr, reads=[rys, self.r_ident], writes=[r_pmisc])
                    k.op("vector", lambda e: e.tensor_copy(out=yo[:].rearrange("p h a d -> p h (a d)"), in_=pmisc[0:64, :, :]), reads=[r_pmisc], writes=[r_yo])
                    k.op("gpsimd", lambda e: e.tensor_copy(out=yb2[:].rearrange("p (h d) -> p h d", h=8), in_=yo[:, :, 1, :]), reads=[r_yo], writes=[r_yb2])
                    k.op("tensor", lambda e: e.matmul(pj[0:64, :], lhsT=J64[:], rhs=yb2[:], start=True, stop=True), reads=[r_yb2, r_cst], writes=[r_pj])
                    k.op("scalar", lambda e: e.activation(out=yb3[:], in_=pj[0:64, :], func=AF.Copy), reads=[r_pj], writes=[r_yb3])
                    k.dma(yfb[0, f0:f0 + 64, :].rearrange("p (h d) -> p h d", h=8), yo[:, :, 0, :], reads=[r_yo], writes=[self.r_rwy_t[f0 // 128]])
                    k.dma(yfb[1, b0:b0 + 64, :], yb3[:], reads=[r_yb3], writes=[self.r_rwy_t[b0 // 128]])
            k.barrier()

    def stage_rwout(self, l):
        k = self.k
        rwd, _ = self.dr["rwd%d" % l]
        rwo, r_rwo = self.dr["rwo%d" % l]
        yfb, _ = self.dr["rwy%d" % l]
        yd, r_yd = self.dram("y_rw%d" % l, [T, 512], F32)
        with ExitStack() as es:
            r_c = R()

            def bc(nm, n):
                tl = k.sb("ro_" + nm, [128, n], F32, es)
                ap, rr = self.dr[nm]
                k.dma(tl[:], ap[l].partition_broadcast(128), reads=[rr], writes=[r_c])
                return tl
            lnw = bc("rw_lnw", 512)
            lnb = bc("rw_lnb", 512)
            rkg = bc("rw_rk", 512)
            g2 = k.sb("ro_g2", [96, 512], F32, es)
            ap, rr = self.dr["rw_g2"]
            k.dma(g2[:], ap[l], reads=[rr], writes=[r_c])
            gne = k.sb("ro_gne", [128, 1], F32, es)
            k.op("vector", lambda e: e.memset(gne[:], 64e-5), writes=[r_c])
            yf = [k.sb("ro_yf%d" % i, [128, 512], F32, es) for i in range(2)]
            yb = [k.sb("ro_yb%d" % i, [128, 512], F32, es) for i in range(2)]
            rv = [k.sb("ro_rv%d" % i, [128, 1024], F32, es) for i in range(2)]
            kg = [k.sb("ro_kg%d" % i, [128, 608], F32, es) for i in range(2)]
            r_in = [[R() for _ in range(4)] for _ in range(2)]
            y = k.sb("ro_y", [128, 512], F32, es)
            r_y = R()
            sq = k.sb("ro_sq", [128, 512], F32, es)
            r_sq = R()
            st = k.sb("ro_st", [128, NT, 3, 8], F32, es)
            sg = k.sb("ro_sg", [128, 128], F32, es)
            r_sg = R()
            gT = k.sb("ro_gT", [96, 128], F32, es)
            r_gT = R()
            out = [k.sb("ro_out%d" % i, [128, 512], F32, es) for i in range(2)]
            r_out = [R() for _ in range(2)]
            ptr = k.ps("ro_ptr", [128, 512], F32, es)
            r_ptr = PR()
            pg = k.ps("ro_pg", [128, 512], F32, es)
            r_pg = PR()
            k.op("vector", lambda e: e.memset(sg[:], 0.0), writes=[r_sg])
            for t in range(NT):
                rows = slice(t * 128, (t + 1) * 128)
                i2 = t % 2
                ri = r_in[i2]
                k.dma(yf[i2][:], yfb[0, rows, :], reads=[self.r_rwy_t[t]], writes=[ri[0]])
                k.dma(yb[i2][:], yfb[1, rows, :], reads=[self.r_rwy_t[t]], writes=[ri[1]])
                k.dma(rv[i2][:], rwd[0, rows, 2048:3072], reads=[self.r_rwd_t[t]], writes=[ri[2]])
                k.dma(kg[i2][:], rwo[rows, :], reads=[r_rwo], writes=[ri[3]])
                r_s = R()
                y3 = y[:].rearrange("p (h d) -> p h d", h=8)
                k.op("vector", lambda e, i2=i2: e.tensor_tensor(out=y[:], in0=yf[i2][:], in1=yb[i2][:], op=ALU.add), reads=[ri[0], ri[1]], writes=[r_y])
                k.op("vector", lambda e, t=t: e.tensor_reduce(out=st[:, t, 0, :], in_=y3, axis=AX.X, op=ALU.add), reads=[r_y], writes=[r_s])
                k.op("vector", lambda e, t=t: e.tensor_scalar(out=st[:, t, 0, :], in0=st[:, t, 0, :], scalar1=1.0 / 64, scalar2=None, op0=ALU.mult), reads=[r_s], writes=[r_s])
                k.op("vector", lambda e, t=t: e.tensor_tensor(out=y3, in0=y3, in1=st[:, t, 0, :].unsqueeze(2).to_broadcast([128, 8, 64]), op=ALU.subtract),
                     reads=[r_y, r_s], writes=[r_y])
                k.op("gpsimd", lambda e: e.tensor_tensor(out=sq[:], in0=y[:], in1=y[:], op=ALU.mult), reads=[r_y], writes=[r_sq])
                k.op("vector", lambda e, t=t: e.tensor_reduce(out=st[:, t, 1, :], in_=sq[:].rearrange("p (h d) -> p h d", h=8), axis=AX.X, op=ALU.add), reads=[r_sq], writes=[r_s])
                k.op("scalar", lambda e, t=t: e.activation(out=st[:, t, 1, :], in_=st[:, t, 1, :], func=AF.Sqrt, bias=gne[:, 0:1], scale=1.0 / 64), reads=[r_s, r_c], writes=[r_s])
                k.op("vector", lambda e, t=t: e.reciprocal(out=st[:, t, 1, :], in_=st[:, t, 1, :]), reads=[r_s], writes=[r_s])
                k.op("vector", lambda e, t=t: e.tensor_tensor(out=y3, in0=y3, in1=st[:, t, 1, :].unsqueeze(2).to_broadcast([128, 8, 64]), op=ALU.mult),
                     reads=[r_y, r_s], writes=[r_y])
                k.op("gpsimd", lambda e: e.tensor_tensor(out=y[:], in0=y[:], in1=lnw[:], op=ALU.mult), reads=[r_y, r_c], writes=[r_y])
                k.op("gpsimd", lambda e: e.tensor_tensor(out=y[:], in0=y[:], in1=lnb[:], op=ALU.add), reads=[r_y, r_c], writes=[r_y])
                k.op("vector", lambda e, i2=i2: e.tensor_tensor(out=sq[:], in0=rv[i2][:, 0:512], in1=kg[i2][:, 0:512], op=ALU.mult), reads=[ri[2], ri[3], r_sq], writes=[r_sq])
                k.op("vector", lambda e: e.tensor_tensor(out=sq[:], in0=sq[:], in1=rkg[:], op=ALU.mult), reads=[r_sq, r_c], writes=[r_sq])
                k.op("vector", lambda e, t=t: e.tensor_reduce(out=st[:, t, 2, :], in_=sq[:].rearrange("p (h d) -> p h d", h=8), axis=AX.X, op=ALU.add), reads=[r_sq], writes=[r_s])
                k.op("vector", lambda e, t=t, i2=i2: e.tensor_tensor(out=sq[:].rearrange("p (h d) -> p h d", h=8), in0=rv[i2][:, 512:1024].rearrange("p (h d) -> p h d", h=8),
                                                                    in1=st[:, t, 2, :].unsqueeze(2).to_broadcast([128, 8, 64]), op=ALU.mult),
                     reads=[ri[2], r_s, r_sq], writes=[r_sq])
                k.op("vector", lambda e: e.tensor_tensor(out=y[:], in0=y[:], in1=sq[:], op=ALU.add), reads=[r_y, r_sq], writes=[r_y])
                k.op("scalar", lambda e, i2=i2: e.activation(out=sg[:, 0:96], in_=kg[i2][:, 512:608], func=AF.Sigmoid), reads=[ri[3]], writes=[r_sg])
                k.op("tensor", lambda e: e.transpose(out=ptr[:, 0:128], in_=sg[:], identity=self.identf[:]), reads=[r_sg, self.r_ident], writes=[r_ptr])
                k.op("vector", lambda e: e.tensor_copy(out=gT[:], in_=ptr[0:96, 0:128]), reads=[r_ptr], writes=[r_gT])
                k.op("tensor", lambda e: e.matmul(pg[:], lhsT=gT[:], rhs=g2[:], start=True, stop=True), reads=[r_gT, r_c], writes=[r_pg])
                o_, ro = out[i2], r_out[i2]
                k.op("vector", lambda e, o_=o_: e.tensor_tensor(out=o_[:], in0=y[:], in1=pg[:], op=ALU.mult), reads=[r_y, r_pg], writes=[ro])
                k.dma(yd[rows, :], o_[:], reads=[ro], writes=[r_yd])
            k.barrier()

    def stage_merge(self, l, xname):
        k = self.k
        z, _ = self.dr["z%d" % l]
        r_zt = self.r_ztile
        xin, r_xin = self.dr[xname]
        modD, r_modD = self.dr["modD%d" % l]
        wbr, r_wbr = self.dr["w_br"]
        wout, r_wout = self.dr["w_out"]
        accD, r_accD = self.dram("accD%d" % l, [T, D], BF16)
        r_acc_t = [R() for _ in range(NT)]
        x1, r_x1 = self.dram("x1_%d" % l, [T, D], F32)
        self.r_x1_t = [R() for _ in range(NT)]
        ybr = [self.dr[n % l] for n in ("y_fn%d", "y_na%d", "y_mla%d", "y_rw%d")]
        with ExitStack() as es:
            W = k.sb("mg_W", [128, 16, 2048], BF16, es)
            r_W = [R() for _ in range(16)]
            wst = [k.sb("mg_wst%d" % i, [128, 2048], F32, es) for i in range(2)]
            r_wst = [R() for _ in range(2)]
            for i in range(16):
                b, kc = i // 4, i % 4
                k.dma(wst[i % 2][:], wbr[l, b, kc * 128:(kc + 1) * 128, :], reads=[r_wbr], writes=[r_wst[i % 2]])
                eng = ("vector", "gpsimd")[i % 2]
                k.op(eng, lambda e, i=i: e.tensor_copy(out=W[:, i, :], in_=wst[i % 2][:]), reads=[r_wst[i % 2]], writes=[r_W[i]])
            with ExitStack() as es2:
                yt = k.sb("mg_yt", [128, 4, 512], F32, es2)
                r_yt = R()
                ytb = k.sb("mg_ytb", [128, 2048], BF16, es2)
                r_ytb = R()
                yT = k.sb("mg_yT", [128, 16, 128], BF16, es2)
                r_yT = R()
                gt = [k.sb("mg_gt%d" % i, [128, 2048], F32, es2) for i in range(2)]
                r_gt = [R() for _ in range(2)]
                acc = k.sb("mg_acc", [128, 2048], F32, es2)
                r_acc = R()
                accb = k.sb("mg_accb", [128, 2048], BF16, es2)
                r_accb = R()
                tmp = k.sb("mg_tmp", [128, 512], F32, es2)
                r_tmp = R()
                ptr = [k.ps("mg_ptr%d" % i, [128, 8, 128], BF16, es2) for i in range(2)]
                r_ptr = [PR() for _ in range(2)]
                pm = [k.ps("mg_pm%d" % i, [128, 512], F32, es2) for i in range(5)]
                r_pm = [PR() for _ in range(5)]
                ip = 0
                ig = 0
                for t in range(NT):
                    rows = slice(t * 128, (t + 1) * 128)
                    for b in range(4):
                        k.dma(yt[:, b, :], ybr[b][0][rows, :], reads=[ybr[b][1]], writes=[r_yt])
                    k.op("gpsimd", lambda e: e.tensor_copy(out=ytb[:], in_=yt[:].rearrange("p a b -> p (a b)")), reads=[r_yt], writes=[r_ytb])
                    for hf in range(2):
                        def tr(e, hf=hf):
                            last = None
                            for j in range(8):
                                c = hf * 8 + j
                                last = e.transpose(out=ptr[hf][:, j, :], in_=ytb[:, c * 128:(c + 1) * 128], identity=self.ident[:])
                            return last
                        k.op("tensor", tr, reads=[r_ytb, self.r_ident], writes=[r_ptr[hf]])
                        if hf == 0:
                            k.op("vector", lambda e: e.tensor_copy(out=yT[:, 0:8, :], in_=ptr[0][:]), reads=[r_ptr[0]], writes=[r_yT])
                        else:
                            k.op("scalar", lambda e: e.activation(out=yT[:, 8:16, :], in_=ptr[1][:], func=AF.Copy), reads=[r_ptr[1], r_yT], writes=[r_yT])
                    for b in range(4):
                        g_, rg = gt[ig % 2], r_gt[ig % 2]
                        ig += 1
                        k.dma(g_[:], z[rows, C_GATE + b * 2048:C_GATE + (b + 1) * 2048], reads=[r_zt[t]], writes=[rg])
                        k.op("scalar", lambda e, g_=g_: e.activation(out=g_[:], in_=g_[:], func=AF.Sigmoid), reads=[rg], writes=[rg])
                        for cbk in range(4):
                            p, rp = pm[ip % 5], r_pm[ip % 5]
                            ip += 1
                            cs_ = slice(cbk * 512, (cbk + 1) * 512)

                            def mm(e, p=p, b=b, cs_=cs_):
                                last = None
                                for kc in range(4):
                                    last = e.matmul(p[:], lhsT=yT[:, b * 4 + kc, :], rhs=W[:, b * 4 + kc, cs_], start=(kc == 0), stop=(kc == 3))
                                return last
                            k.op("tensor", mm, reads=[r_yT] + r_W[b * 4:b * 4 + 4], writes=[rp])
                            if b == 0:
                                k.op("vector", lambda e, p=p, g_=g_, cs_=cs_: e.tensor_tensor(out=acc[:, cs_], in0=p[:], in1=g_[:, cs_], op=ALU.mult),
                                     reads=[rp, rg], writes=[r_acc])
                            else:
                                k.op("vector", lambda e, p=p, g_=g_, cs_=cs_: e.tensor_tensor(out=tmp[:], in0=p[:], in1=g_[:, cs_], op=ALU.mult),
                                     reads=[rp, rg], writes=[r_tmp])
                                k.op("gpsimd", lambda e, cs_=cs_: e.tensor_tensor(out=acc[:, cs_], in0=acc[:, cs_], in1=tmp[:], op=ALU.add),
                                     reads=[r_tmp, r_acc], writes=[r_acc])
                    k.op("vector", lambda e: e.tensor_copy(out=accb[:], in_=acc[:]), reads=[r_acc], writes=[r_accb])
                    k.dma(accD[rows, :], accb[:], reads=[r_accb], writes=[r_acc_t[t]])
                k.barrier()
            for i in range(16):
                k.dma(wst[i % 2][:], wout[l, i * 128:(i + 1) * 128, :], reads=[r_wout], writes=[r_wst[i % 2]])
                eng = ("vector", "gpsimd")[i % 2]
                k.op(eng, lambda e, i=i: e.tensor_copy(out=W[:, i, :], in_=wst[i % 2][:]), reads=[r_wst[i % 2]], writes=[r_W[i]])
            with ExitStack() as es3:
                g1 = k.sb("mg_g1", [128, 2, 2048], F32, es3)
                r_g1 = R()
                for r_ in range(2):
                    k.dma(g1[:, r_, :], modD[r_:r_ + 1, 2 * D:3 * D].partition_broadcast(128), reads=[r_modD], writes=[r_g1])
                ab = [k.sb("mg_ab%d" % i, [128, 2048], BF16, es3) for i in range(2)]
                r_ab = [R() for _ in range(2)]
                aT = k.sb("mg_aT", [128, 16, 128], BF16, es3)
                r_aT = R()
                xt = [k.sb("mg_xt%d" % i, [128, 2048], F32, es3) for i in range(2)]
                r_xt = [R() for _ in range(2)]
                xo = [k.sb("mg_xo%d" % i, [128, 2048], F32, es3) for i in range(2)]
                r_xo = [R() for _ in range(2)]
                tmp = k.sb("mg_tmp2", [128, 512], F32, es3)
                r_tmp = R()
                ptr = [k.ps("mg_ptrb%d" % i, [128, 8, 128], BF16, es3) for i in range(2)]
                r_ptr = [PR() for _ in range(2)]
                pm = [k.ps("mg_pmb%d" % i, [128, 512], F32, es3) for i in range(4)]
                r_pm = [PR() for _ in range(4)]
                ip = 0
                for t in range(NT):
                    rows = slice(t * 128, (t + 1) * 128)
                    row = 1 if t < 2 else 0
                    a_, ra = ab[t % 2], r_ab[t % 2]
                    x_, rx = xt[t % 2], r_xt[t % 2]
                    o_, ro = xo[t % 2], r_xo[t % 2]
                    k.dma(a_[:], accD[rows, :], reads=[r_acc_t[t]], writes=[ra])
                    k.dma(x_[:], xin[rows, :], reads=[r_xin], writes=[rx])
                    for hf in range(2):
                        def tr(e, hf=hf, a_=a_):
                            last = None
                            for j in range(8):
                                c = hf * 8 + j
                                last = e.transpose(out=ptr[hf][:, j, :], in_=a_[:, c * 128:(c + 1) * 128], identity=self.ident[:])
                            return last
                        k.op("tensor", tr, reads=[ra, self.r_ident], writes=[r_ptr[hf]])
                        if hf == 0:
                            k.op("vector", lambda e: e.tensor_copy(out=aT[:, 0:8, :], in_=ptr[0][:]), reads=[r_ptr[0]], writes=[r_aT])
                        else:
                            k.op("scalar", lambda e: e.activation(out=aT[:, 8:16, :], in_=ptr[1][:], func=AF.Copy), reads=[r_ptr[1], r_aT], writes=[r_aT])
                    for cbk in range(4):
                        p, rp = pm[ip % 4], r_pm[ip % 4]
                        ip += 1
                        cs_ = slice(cbk * 512, (cbk + 1) * 512)

                        def mm(e, p=p, cs_=cs_):
                            last = None
                            for kc in range(16):
                                last = e.matmul(p[:], lhsT=aT[:, kc, :], rhs=W[:, kc, cs_], start=(kc == 0), stop=(kc == 15))
                            return last
                        k.op("tensor", mm, reads=[r_aT] + r_W, writes=[rp])
                        k.op("vector", lambda e, p=p, cs_=cs_, row=row: e.tensor_tensor(out=tmp[:], in0=p[:], in1=g1[:, row, cs_], op=ALU.mult),
                             reads=[rp, r_g1], writes=[r_tmp])
                        k.op("gpsimd", lambda e, o_=o_, x_=x_, cs_=cs_: e.tensor_tensor(out=o_[:, cs_], in0=tmp[:], in1=x_[:, cs_], op=ALU.add),
                             reads=[r_tmp, rx], writes=[ro])
                    k.dma(x1[rows, :], o_[:], reads=[ro], writes=[self.r_x1_t[t]])
                k.barrier()

    def stage_moe(self, l, final):
        k = self.k
        x1, _ = self.dr["x1_%d" % l]
        r_x1_t = self.r_x1_t
        modD, r_modD = self.dr["modD%d" % l]
        XE, r_XE = self.dram("moe_xe%d" % l, [16, 128, 16 * 288], BF16)
        WS, r_WS = self.dram("moe_ws%d" % l, [16, 128, 2 * 2048], BF16)
        WSC, r_WSC = self.dram("moe_wsc%d" % l, [16, 32, 256], BF16)
        YE, r_YE = self.dram("moe_ye%d" % l, [16, 288, 2048], BF16)
        r_XEe = [R() for _ in range(16)]
        r_WSe = [R() for _ in range(16)]
        r_YEe = [R() for _ in range(16)]
        if final:
            xo_d, r_xo_d = self.dram("out", [L, D], F32, "ExternalOutput")
        else:
            xo_d, r_xo_d = self.dram("x2_%d" % l, [T, D], F32)
        self.r_x2_t = [R() for _ in range(NT)]
        w1d, r_w1d = self.dr["moe_w1"]
        w3d, r_w3d = self.dr["moe_w3"]
        w2d, r_w2d = self.dr["moe_w2"]
        with ExitStack() as es:
            h2 = k.sb("mo_h2", [128, NT, 2048], BF16, es)
            r_h2 = [R() for _ in range(NT)]
            aff = k.sb("mo_aff", [128, NT, 32], F32, es)
            r_aff = [R() for _ in range(NT)]
            affT = k.sb("mo_affT", [32, T], F32, es)
            r_affT = R()
            k.op("gpsimd", lambda e: e.memset(aff[:], 0.0), writes=r_aff)
            r_c = R()
            with ExitStack() as es2:
                abc = k.sb("mo_abc", [128, 2, 2048], F32, es2)
                shbc = k.sb("mo_shbc", [128, 2, 2048], F32, es2)
                gbc = k.sb("mo_gbc", [128, 2048], F32, es2)
                ap, rr = self.dr["norm2_gb"]
                k.dma(gbc[:], ap[l].partition_broadcast(128), reads=[rr], writes=[r_c])
                for r_ in range(2):
                    k.dma(abc[:, r_, :], modD[r_:r_ + 1, 4 * D:5 * D].partition_broadcast(128), reads=[r_modD], writes=[r_c])
                    k.dma(shbc[:, r_, :], modD[r_:r_ + 1, 3 * D:4 * D].partition_broadcast(128), reads=[r_modD], writes=[r_c])
                for r_ in range(2):
                    k.op("vector", lambda e, r_=r_: e.scalar_tensor_tensor(out=abc[:, r_, :], in0=abc[:, r_, :], scalar=1.0, in1=gbc[:], op0=ALU.add, op1=ALU.mult),
                         reads=[r_c], writes=[r_c])
                rtr = k.sb("mo_rtr", [128, 16, 32], F32, es2)
                k.op("gpsimd", lambda e: e.memset(rtr[:], 0.0), writes=[r_c])
                ap, rr = self.dr["moe_router"]
                k.dma(rtr[:, :, 0:16], ap[l].rearrange("(kc p) n -> p kc n", p=128), reads=[rr], writes=[r_c])
                xt = [k.sb("mo_xt%d" % i, [128, 2048], F32, es2) for i in range(2)]
                r_xt = [R() for _ in range(2)]
                hf_ = k.sb("mo_hf", [128, 2048], F32, es2)
                r_hf = R()
                hT = k.sb("mo_hT", [128, 16, 128], F32, es2)
                r_hT = R()
                junk = k.sb("mo_junk", [128, 2048], BF16, es2)
                r_junk = R()
                st = k.sb("mo_st", [128, NT, 4], F32, es2)
                lg = k.sb("mo_lg", [128, 16], F32, es2)
                r_lg = R()
                ptr = [k.ps("mo_ptr%d" % i, [128, 4, 128], F32, es2) for i in range(4)]
                r_ptr = [PR() for _ in range(4)]
                pl = k.ps("mo_pl", [128, 512], F32, es2)
                r_pl = PR()
                pT = k.ps("mo_pT", [128, 512], F32, es2)
                r_pT = PR()
                for t in range(NT):
                    rows = slice(t * 128, (t + 1) * 128)
                    row = 1 if t < 2 else 0
                    x_, rx = xt[t % 2], r_xt[t % 2]
                    r_s = R()
                    k.dma(x_[:], x1[rows, :], reads=[r_x1_t[t]], writes=[rx])
                    k.op("scalar", lambda e, x_=x_, t=t: e.activation(out=junk[:], in_=x_[:], func=AF.Square, accum_out=st[:, t, 0:1]), reads=[rx], writes=[r_junk, r_s])
                    k.op("scalar", lambda e, t=t: e.activation(out=st[:, t, 1:2], in_=st[:, t, 0:1], func=AF.Sqrt, bias=self.eps_t[:, 0:1], scale=1.0 / D), reads=[r_s], writes=[r_s])
                    k.op("vector", lambda e, t=t: e.reciprocal(out=st[:, t, 1:2], in_=st[:, t, 1:2]), reads=[r_s], writes=[r_s])
                    k.op("vector", lambda e, x_=x_, t=t, row=row: e.scalar_tensor_tensor(out=hf_[:], in0=x_[:], scalar=st[:, t, 1:2], in1=abc[:, row, :], op0=ALU.mult, op1=ALU.mult),
                         reads=[rx, r_s, r_c], writes=[r_hf])
                    k.op("gpsimd", lambda e, row=row: e.tensor_tensor(out=hf_[:], in0=hf_[:], in1=shbc[:, row, :], op=ALU.add), reads=[r_hf, r_c], writes=[r_hf])
                    k.op("scalar", lambda e, t=t: e.activation(out=h2[:, t, :], in_=hf_[:], func=AF.Copy), reads=[r_hf], writes=[r_h2[t]])
                    for g in range(4):
                        def tr(e, g=g):
                            last = None
                            for j in range(4):
                                c = g * 4 + j
                                last = e.transpose(out=ptr[g][:, j, :], in_=hf_[:, c * 128:(c + 1) * 128], identity=self.identf[:])
                            return last
                        k.op("tensor", tr, reads=[r_hf, self.r_ident], writes=[r_ptr[g]])
                        if g % 2 == 0:
                            k.op("vector", lambda e, g=g: e.tensor_copy(out=hT[:, g * 4:(g + 1) * 4, :], in_=ptr[g][:]), reads=[r_ptr[g]], writes=[r_hT])
                        else:
                            k.op("scalar", lambda e, g=g: e.activation(out=hT[:, g * 4:(g + 1) * 4, :], in_=ptr[g][:], func=AF.Copy), reads=[r_ptr[g], r_hT], writes=[r_hT])

                    def mmr(e):
                        last = None
                        for kc in range(16):
                            last = e.matmul(pl[:, 0:32], lhsT=hT[:, kc, :], rhs=rtr[:, kc, :], start=(kc == 0), stop=(kc == 15))
                        return last
                    k.op("tensor", mmr, reads=[r_hT, r_c], writes=[r_pl])
                    k.op("vector", lambda e, t=t: e.tensor_reduce(out=st[:, t, 2:3], in_=pl[:, 0:16], axis=AX.X, op=ALU.max), reads=[r_pl], writes=[r_s])
                    k.op("vector", lambda e, t=t: e.tensor_scalar(out=lg[:], in0=pl[:, 0:16], scalar1=st[:, t, 2:3], scalar2=None, op0=ALU.subtract), reads=[r_pl, r_s], writes=[r_lg])
                    k.op("scalar", lambda e, t=t: e.activation(out=lg[:], in_=lg[:], func=AF.Exp, accum_out=st[:, t, 3:4]), reads=[r_lg], writes=[r_lg, r_s])
                    k.op("vector", lambda e, t=t: e.reciprocal(out=st[:, t, 3:4], in_=st[:, t, 3:4]), reads=[r_s], writes=[r_s])
                    k.op("vector", lambda e, t=t: e.tensor_scalar(out=aff[:, t, 0:16], in0=lg[:], scalar1=st[:, t, 3:4], scalar2=None, op0=ALU.mult), reads=[r_lg, r_s], writes=[r_aff[t]])
                    k.op("tensor", lambda e, t=t: e.transpose(out=pT[0:32, 0:128], in_=aff[:, t, :], identity=self.identf[:]), reads=[r_aff[t], self.r_ident], writes=[r_pT])
                    k.op("vector", lambda e, rows=rows: e.tensor_copy(out=affT[:, rows], in_=pT[0:32, 0:128]), reads=[r_pT], writes=[r_affT])
                k.barrier()
            with ExitStack() as es3:
                sel = k.sb("mo_sel", [32, 16, 128], F32, es3)
                ap, rr = self.dr["moe_sel"]
                k.dma(sel[:], ap, reads=[rr], writes=[r_c])
                jidx = k.sb("mo_jidx", [128, 2], F32, es3)
                ap, rr = self.dr["moe_jidx"]
                k.dma(jidx[:], ap, reads=[rr], writes=[r_c])
                ones = k.sb("mo_ones", [128, 128], BF16, es3)
                k.op("vector", lambda e: e.memset(ones[:], 1.0), writes=[r_c])
                A = k.sb("mo_A", [128, T], F32, es3)
                r_A = R()
                cmp = [k.sb("mo_cmp%d" % i, [128, 2048], BF16, es3) for i in range(2)]
                r_cmp = [R() for _ in range(2)]
                Ws = k.sb("mo_Ws", [128, 2, 2048], BF16, es3)
                r_Ws = R()
                Ps = k.sb("mo_Ps", [128, 2, 2048], BF16, es3)
                r_Ps = R()
                Wc = k.sb("mo_Wc", [128, 256], BF16, es3)
                r_Wc = R()
                Pc = k.sb("mo_Pc", [128, 256], BF16, es3)
                r_Pc = R()
                PsT = k.sb("mo_PsT", [128, NT, 256], BF16, es3)
                r_PsT = R()
                xe = [k.sb("mo_xe%d" % i, [128, 16, 288], BF16, es3) for i in range(2)]
                r_xe = [R() for _ in range(2)]
                prk = k.ps("mo_prk", [128, 2048], F32, es3)
                r_prk = PR()
                pA = k.ps("mo_pA", [128, 512], F32, es3)
                r_pA = PR()
                ptb = k.ps("mo_ptb", [128, 8, 128], BF16, es3)
                r_ptb = PR()
                pg = [k.ps("mo_pg%d" % i, [128, 512], F32, es3) for i in range(2)]
                r_pg = [PR() for _ in range(2)]
                ic = 0
                ig = 0
                for e_ in range(16):
                    for cb in range(5):
                        c0 = cb * 512
                        cw = min(512, T - c0)
                        k.op("tensor", lambda e, c0=c0, cw=cw, e_=e_: e.matmul(pA[:, 0:cw], lhsT=sel[:, e_, :], rhs=affT[:, c0:c0 + cw], start=True, stop=True),
                             reads=[r_affT, r_c], writes=[r_pA])
                        k.op("scalar", lambda e, c0=c0, cw=cw: e.activation(out=A[:, c0:c0 + cw], in_=pA[:, 0:cw], func=AF.Copy), reads=[r_pA], writes=[r_A])
                    for mt in range(16):
                        c_, rc_ = cmp[ic % 2], r_cmp[ic % 2]
                        ic += 1
                        k.op("vector", lambda e, c_=c_, mt=mt, e_=e_: e.tensor_scalar(out=c_[:], in0=A[:, NCTX:T], scalar1=aff[:, 2 + mt, e_:e_ + 1], scalar2=None, op0=ALU.is_lt),
                             reads=[r_A, r_aff[2 + mt]], writes=[rc_])

                        def mmc(e, c_=c_, mt=mt):
                            last = None
                            for cb in range(4):
                                last = e.matmul(prk[:, cb * 512:(cb + 1) * 512], lhsT=ones[:], rhs=c_[:, cb * 512:(cb + 1) * 512], start=(mt == 0), stop=(mt == 15))
                            return last
                        k.op("tensor", mmc, reads=[rc_, r_c], writes=[r_prk])
                    for jt in range(2):
                        k.op("vector", lambda e, jt=jt: e.scalar_tensor_tensor(out=Ws[:, jt, :], in0=prk[:], scalar=jidx[:, jt:jt + 1], in1=A[:, NCTX:T], op0=ALU.is_equal, op1=ALU.mult),
                             reads=[r_prk, r_A, r_c], writes=[r_Ws])
                        k.op("vector", lambda e, jt=jt: e.tensor_scalar(out=Ps[:, jt, :], in0=prk[:], scalar1=jidx[:, jt:jt + 1], scalar2=None, op0=ALU.is_equal),
                             reads=[r_prk, r_c], writes=[r_Ps])
                    k.dma(WS[e_].rearrange("p (a b) -> p a b", a=2), Ws[:], reads=[r_Ws], writes=[r_WSe[e_]])
                    for mt in range(2):
                        c_, rc_ = cmp[ic % 2], r_cmp[ic % 2]
                        ic += 1
                        k.op("vector", lambda e, c_=c_, mt=mt, e_=e_: e.tensor_scalar(out=c_[:, 0:256], in0=A[:, 0:NCTX], scalar1=aff[:, mt, e_:e_ + 1], scalar2=None, op0=ALU.is_lt),
                             reads=[r_A, r_aff[mt]], writes=[rc_])
                        k.op("tensor", lambda e, c_=c_, mt=mt: e.matmul(prk[:, 0:256], lhsT=ones[:], rhs=c_[:, 0:256], start=(mt == 0), stop=(mt == 1)),
                             reads=[rc_, r_c], writes=[r_prk])
                    k.op("vector", lambda e: e.scalar_tensor_tensor(out=Wc[:], in0=prk[:, 0:256], scalar=jidx[:, 0:1], in1=A[:, 0:NCTX], op0=ALU.is_equal, op1=ALU.mult),
                         reads=[r_prk, r_A, r_c], writes=[r_Wc])
                    k.op("vector", lambda e: e.tensor_scalar(out=Pc[:], in0=prk[:, 0:256], scalar1=jidx[:, 0:1], scalar2=None, op0=ALU.is_equal),
                         reads=[r_prk, r_c], writes=[r_Pc])
                    k.dma(WSC[e_], Wc[0:32, :], reads=[r_Wc], writes=[r_WSe[e_]])
                    for nt in range(16):
                        if nt % 4 == 0:
                            def trp(e, nt=nt):
                                last = None
                                for q in range(4):
                                    for jt in range(2):
                                        last = e.transpose(out=ptb[:, q * 2 + jt, :], in_=Ps[:, jt, (nt + q) * 128:(nt + q + 1) * 128], identity=self.ident[:])
                                return last
                            k.op("tensor", trp, reads=[r_Ps, self.r_ident], writes=[r_ptb])
                            k.op("scalar", lambda e, nt=nt: e.activation(out=PsT[:, 2 + nt:2 + nt + 4, :], in_=ptb[:].rearrange("p (q j) n -> p q (j n)", q=4), func=AF.Copy),
                                 reads=[r_ptb], writes=[r_PsT])

                    def trc(e):
                        last = None
                        for nt in range(2):
                            last = e.transpose(out=ptb[:, nt, 0:32], in_=Pc[0:32, nt * 128:(nt + 1) * 128], identity=self.ident[0:32, 0:32])
                        return last
                    k.op("tensor", trc, reads=[r_Pc, self.r_ident], writes=[r_ptb])
                    k.op("scalar", lambda e: e.activation(out=PsT[:, 0:2, 0:32], in_=ptb[:, 0:2, 0:32], func=AF.Copy), reads=[r_ptb], writes=[r_PsT])
                    x_, rx = xe[e_ % 2], r_xe[e_ % 2]
                    for dc in range(16):
                        p, rp = pg[ig % 2], r_pg[ig % 2]
                        ig += 1

                        def mmg(e, p=p, dc=dc):
                            last = None
                            for nt in range(16):
                                last = e.matmul(p[:, 0:256], lhsT=h2[:, 2 + nt, dc * 128:(dc + 1) * 128], rhs=PsT[:, 2 + nt, :], start=(nt == 0), stop=(nt == 15))
                            for nt in range(2):
                                last = e.matmul(p[:, 256:288], lhsT=h2[:, nt, dc * 128:(dc + 1) * 128], rhs=PsT[:, nt, 0:32], start=(nt == 0), stop=(nt == 1))
                            return last
                        k.op("tensor", mmg, reads=[r_PsT] + r_h2, writes=[rp])
                        if dc % 2 == 0:
                            k.op("vector", lambda e, p=p, x_=x_, dc=dc: e.tensor_copy(out=x_[:, dc, :], in_=p[:, 0:288]), reads=[rp], writes=[rx])
                        else:
                            k.op("scalar", lambda e, p=p, x_=x_, dc=dc: e.activation(out=x_[:, dc, :], in_=p[:, 0:288], func=AF.Copy), reads=[rp, rx], writes=[rx])
                    k.dma(XE[e_], x_[:].rearrange("p a b -> p (a b)"), reads=[rx], writes=[r_XEe[e_]])
                k.barrier()
        with ExitStack() as es:
            w1b = k.sb("mo_w1b", [128, 16, 1024], BF16, es)
            w3b = k.sb("mo_w3b", [128, 16, 1024], BF16, es)
            w2b = k.sb("mo_w2b", [128, 8, 2048], BF16, es)
            r_w1 = [R() for _ in range(8)]
            r_w3 = [R() for _ in range(8)]
            r_w2 = [R() for _ in range(8)]
            wst = [k.sb("mo_wst%d" % i, [128, 2048], F32, es) for i in range(3)]
            r_wst = [R() for _ in range(3)]
            xe = [k.sb("mo_xeb%d" % i, [128, 16, 288], BF16, es) for i in range(2)]
            r_xe = [R() for _ in range(2)]
            gT = k.sb("mo_gT", [128, 8, 288], BF16, es)
            r_gT = [R() for _ in range(8)]
            sa = [k.sb("mo_sa%d" % i, [128, 288], F32, es) for i in range(2)]
            r_sa = [R() for _ in range(2)]
            yeb = [k.sb("mo_yeb%d" % i, [128, 3, 2048], BF16, es) for i in range(2)]
            r_yeb = [R() for _ in range(2)]
            pa = [k.ps("mo_pa%d" % i, [128, 512], F32, es) for i in range(2)]
            r_pa = [PR() for _ in range(2)]
            pu = [k.ps("mo_pu%d" % i, [128, 512], F32, es) for i in range(2)]
            r_pu = [PR() for _ in range(2)]
            py = [k.ps("mo_py%d" % i, [128, 512], F32, es) for i in range(3)]
            r_py = [PR() for _ in range(3)]
            iw = 0
            iy = 0
            engs = ("vector", "gpsimd")
            for e_ in range(16):
                x_, rx = xe[e_ % 2], r_xe[e_ % 2]
                k.dma(x_[:].rearrange("p a b -> p (a b)"), XE[e_], reads=[r_XEe[e_]], writes=[rx])
                for (wd, rwd_, wb_, rw_) in ((w1d, r_w1d, w1b, r_w1), (w3d, r_w3d, w3b, r_w3)):
                    for pr in range(8):
                        s_, rs = wst[iw % 3], r_wst[iw % 3]
                        k.dma(s_[:].rearrange("p (a n) -> p a n", a=2), wd[l, e_, pr * 256:(pr + 1) * 256, :].rearrange("(a p) n -> p a n", p=128), reads=[rwd_], writes=[rs])
                        k.op(engs[iw % 2], lambda e, s_=s_, wb_=wb_, pr=pr: e.tensor_copy(out=wb_[:, 2 * pr:2 * pr + 2, :], in_=s_[:].rearrange("p (a n) -> p a n", a=2)),
                             reads=[rs], writes=[rw_[pr]])
                        iw += 1
                for fc in range(8):
                    s_, rs = wst[iw % 3], r_wst[iw % 3]
                    k.dma(s_[:], w2d[l, e_, fc * 128:(fc + 1) * 128, :], reads=[r_w2d], writes=[rs])
                    k.op(engs[iw % 2], lambda e, s_=s_, fc=fc: e.tensor_copy(out=w2b[:, fc, :], in_=s_[:]), reads=[rs], writes=[r_w2[fc]])
                    iw += 1
                for fc in range(8):
                    pa_, rpa = pa[fc % 2], r_pa[fc % 2]
                    pu_, rpu = pu[fc % 2], r_pu[fc % 2]
                    sa_, rsa = sa[fc % 2], r_sa[fc % 2]
                    for (wb_, rw_, p_, rp_) in ((w1b, r_w1, pa_, rpa), (w3b, r_w3, pu_, rpu)):
                        def mmf(e, wb_=wb_, p_=p_, fc=fc, x_=x_):
                            last = None
                            for kc in range(16):
                                last = e.matmul(p_[:, 0:288], lhsT=wb_[:, kc, fc * 128:(fc + 1) * 128], rhs=x_[:, kc, :], start=(kc == 0), stop=(kc == 15))
                            return last
                        k.op("tensor", mmf, reads=rw_ + [rx], writes=[rp_])
                    k.op("scalar", lambda e, sa_=sa_, pa_=pa_: e.activation(out=sa_[:], in_=pa_[:, 0:288], func=AF.Silu), reads=[rpa], writes=[rsa])
                    k.op("vector", lambda e, sa_=sa_, pu_=pu_, fc=fc: e.tensor_tensor(out=gT[:, fc, :], in0=sa_[:], in1=pu_[:, 0:288], op=ALU.mult), reads=[rsa, rpu], writes=[r_gT[fc]])
                y_, ry = yeb[e_ % 2], r_yeb[e_ % 2]
                for jt, (j0, M) in enumerate(((0, 128), (128, 128), (256, 32))):
                    for cbk in range(4):
                        p_, rp_ = py[iy % 3], r_py[iy % 3]
                        iy += 1
                        cs_ = slice(cbk * 512, (cbk + 1) * 512)

                        def mmy(e, p_=p_, j0=j0, M=M, cs_=cs_):
                            last = None
                            for fc in range(8):
                                last = e.matmul(p_[0:M, :], lhsT=gT[:, fc, j0:j0 + M], rhs=w2b[:, fc, cs_], start=(fc == 0), stop=(fc == 7))
                            return last
                        k.op("tensor", mmy, reads=r_gT + r_w2, writes=[rp_])
                        if iy % 2 == 0:
                            k.op("vector", lambda e, p_=p_, y_=y_, jt=jt, M=M, cs_=cs_: e.tensor_copy(out=y_[0:M, jt, cs_], in_=p_[0:M, :]), reads=[rp_], writes=[ry])
                        else:
                            k.op("scalar", lambda e, p_=p_, y_=y_, jt=jt, M=M, cs_=cs_: e.activation(out=y_[0:M, jt, cs_], in_=p_[0:M, :], func=AF.Copy), reads=[rp_], writes=[ry])
                k.dma(YE[e_, 0:256, :].rearrange("(a p) n -> p a n", p=128), y_[:, 0:2, :], reads=[ry], writes=[r_YEe[e_]])
                k.dma(YE[e_, 256:288, :], y_[0:32, 2, :], reads=[ry], writes=[r_YEe[e_]])
            k.barrier()
        with ExitStack() as es:
            macc = k.sb("mo_macc", [128, 8, 2048], F32, es)
            r_macc = [R() for _ in range(8)]
            g2 = k.sb("mo_g2", [128, 2, 2048], F32, es)
            r_g2 = R()
            for r_ in range(2):
                k.dma(g2[:, r_, :], modD[r_:r_ + 1, 5 * D:6 * D].partition_broadcast(128), reads=[r_modD], writes=[r_g2])
            ye = [k.sb("mo_ye%d" % i, [128, 2, 2048], BF16, es) for i in range(2)]
            r_ye = [R() for _ in range(2)]
            ws = [k.sb("mo_ws%d" % i, [128, 2, 1024], BF16, es) for i in range(2)]
            r_ws = [R() for _ in range(2)]
            xt = [k.sb("mo_x2t%d" % i, [128, 2048], F32, es) for i in range(2)]
            r_xt = [R() for _ in range(2)]
            ps = [k.ps("mo_ps%d" % i, [128, 512], F32, es) for i in range(4)]
            r_ps = [PR() for _ in range(4)]
            ip = 0
            il = 0
            for part in range(3):
                ntl = 2 if part == 0 else 8
                t0 = 0 if part == 0 else 2 + (part - 1) * 8
                for e_ in range(16):
                    y_, ry = ye[il % 2], r_ye[il % 2]
                    w_, rw = ws[il % 2], r_ws[il % 2]
                    il += 1
                    if part == 0:
                        k.dma(y_[0:32, 0, :], YE[e_, 256:288, :], reads=[r_YEe[e_]], writes=[ry])
                        k.dma(w_[0:32, 0, 0:256], WSC[e_], reads=[r_WSe[e_]], writes=[rw])
                    else:
                        k.dma(y_[:], YE[e_, 0:256, :].rearrange("(a p) n -> p a n", p=128), reads=[r_YEe[e_]], writes=[ry])
                        c0 = (part - 1) * 1024
                        k.dma(w_[:], WS[e_].rearrange("p (a b) -> p a b", a=2)[:, :, c0:c0 + 1024], reads=[r_WSe[e_]], writes=[rw])
                    for tl in range(ntl):
                        for cbk in range(4):
                            p_, rp_ = ps[ip % 4], r_ps[ip % 4]
                            ip += 1
                            cs_ = slice(cbk * 512, (cbk + 1) * 512)
                            if part == 0:
                                k.op("tensor", lambda e, p_=p_, w_=w_, y_=y_, tl=tl, cs_=cs_: e.matmul(p_[:], lhsT=w_[0:32, 0, tl * 128:(tl + 1) * 128], rhs=y_[0:32, 0, cs_], start=True, stop=True),
                                     reads=[rw, ry], writes=[rp_])
                            else:
                                def mms(e, p_=p_, w_=w_, y_=y_, tl=tl, cs_=cs_):
                                    e.matmul(p_[:], lhsT=w_[:, 0, tl * 128:(tl + 1) * 128], rhs=y_[:, 0, cs_], start=True, stop=False)
                                    return e.matmul(p_[:], lhsT=w_[:, 1, tl * 128:(tl + 1) * 128], rhs=y_[:, 1, cs_], start=False, stop=True)
                                k.op("tensor", mms, reads=[rw, ry], writes=[rp_])
                            if e_ == 0:
                                k.op("vector", lambda e, p_=p_, tl=tl, cs_=cs_: e.tensor_copy(out=macc[:, tl, cs_], in_=p_[:]), reads=[rp_], writes=[r_macc[tl]])
                            else:
                                k.op("vector", lambda e, p_=p_, tl=tl, cs_=cs_: e.tensor_tensor(out=macc[:, tl, cs_], in0=macc[:, tl, cs_], in1=p_[:], op=ALU.add),
                                     reads=[rp_, r_macc[tl]], writes=[r_macc[tl]])
                for tl in range(ntl):
                    t = t0 + tl
                    rows = slice(t * 128, (t + 1) * 128)
                    row = 1 if t < 2 else 0
                    x_, rx = xt[t % 2], r_xt[t % 2]
                    k.dma(x_[:], x1[rows, :], reads=[r_x1_t[t]], writes=[rx])
                    k.op("gpsimd", lambda e, tl=tl, row=row: e.tensor_tensor(out=macc[:, tl, :], in0=macc[:, tl, :], in1=g2[:, row, :], op=ALU.mult),
                         reads=[r_macc[tl], r_g2], writes=[r_macc[tl]])
                    k.op("vector", lambda e, tl=tl, x_=x_: e.tensor_tensor(out=x_[:], in0=x_[:], in1=macc[:, tl, :], op=ALU.add), reads=[rx, r_macc[tl]], writes=[rx])
                    if final:
                        if t >= 2:
                            k.dma(xo_d[(t - 2) * 128:(t - 1) * 128, :], x_[:], reads=[rx], writes=[r_xo_d])
                    else:
                        k.dma(xo_d[rows, :], x_[:], reads=[rx], writes=[self.r_x2_t[t]])
            k.barrier()

    def declare_inputs(self):
        dp = self.depth
        self.dram("xcat", [T, D], F32, "ExternalInput")
        self.dram("ada_w", [dp, D, 6 * D], F32, "ExternalInput")
        self.dram("ada_b2", [dp, 2, 6 * D], F32, "ExternalInput")
        self.dram("norm1_gT", [dp, 128, 16], F32, "ExternalInput")
        self.dram("w_in", [dp, D, ZC], F32, "ExternalInput")
        self.dram("na_qg", [dp, 1, 512], F32, "ExternalInput")
        self.dram("na_kg", [dp, 1, 512], F32, "ExternalInput")
        self.dram("na_bias", [dp, 8, 5, 5, 128, 128], F32, "ExternalInput")
        self.dram("mla_cqg", [dp, 1, 512], F32, "ExternalInput")
        self.dram("mla_ckvg", [dp, 1, 256], F32, "ExternalInput")
        self.dram("mla_qg", [dp, 1, 768], F32, "ExternalInput")
        self.dram("mla_kng", [dp, 1, 512], F32, "ExternalInput")
        self.dram("mla_kpg", [dp, 1, 64], F32, "ExternalInput")
        self.dram("mla_w_uq", [dp, 512, 768], F32, "ExternalInput")
        self.dram("mla_w_ukv", [dp, 256, 1024], F32, "ExternalInput")
        self.dram("rope_cos", [T, 64], F32, "ExternalInput")
        self.dram("dft_c", [128, 256], BF16, "ExternalInput")
        self.dram("norm2_gb", [dp, 1, D], F32, "ExternalInput")
        self.dram("moe_router", [dp, D, 16], F32, "ExternalInput")
        self.dram("moe_w1", [dp, 16, D, 1024], F32, "ExternalInput")
        self.dram("moe_w3", [dp, 16, D, 1024], F32, "ExternalInput")
        self.dram("moe_w2", [dp, 16, 1024, D], F32, "ExternalInput")
        self.dram("moe_sel", [32, 16, 128], F32, "ExternalInput")
        self.dram("moe_jidx", [128, 2], F32, "ExternalInput")
        self.dram("w_br", [dp, 4, 512, D], F32, "ExternalInput")
        self.dram("w_out", [dp, D, D], F32, "ExternalInput")
        self.dram("rw_mu", [dp, 1, 1760], F32, "ExternalInput")
        self.dram("rw_kk", [dp, 1, 512], F32, "ExternalInput")
        self.dram("rw_ka", [dp, 1, 512], F32, "ExternalInput")
        self.dram("rw_w0f", [dp, 1, 1024], F32, "ExternalInput")
        self.dram("rw_a0f", [dp, 1, 1024], F32, "ExternalInput")
        self.dram("rw_wa2", [dp, 32, 2, 2, 512], F32, "ExternalInput")
        self.dram("rw_lnw", [dp, 1, 512], F32, "ExternalInput")
        self.dram("rw_lnb", [dp, 1, 512], F32, "ExternalInput")
        self.dram("rw_rk", [dp, 1, 512], F32, "ExternalInput")
        self.dram("rw_g2", [dp, 96, 512], F32, "ExternalInput")
        self.dram("rw_e2", [128, 64, 128], F32, "ExternalInput")
        self.dram("rw_pm", [128, 128], F32, "ExternalInput")
        self.dram("rw_j64", [64, 64], F32, "ExternalInput")
        self.dram("dft_L", [2, L, L], BF16, "ExternalInput")
        self.dram("dft_C", [2, NCTX, NCTX], BF16, "ExternalInput")
        self.dram("rope_sin", [T, 64], F32, "ExternalInput")

    def build(self):
        k = self.k
        self.declare_inputs()
        self.eps_t = k.sb("eps_t", [128, 1], F32)
        r = R()
        k.op("vector", lambda e: e.memset(self.eps_t[:], EPS), writes=[r])
        k.barrier()
        self.consts()
        xname = "xcat"
        for l in range(self.depth):
            if self.want("mod"):
                self.stage_mod(l)
            else:
                self.dram("modD%d" % l, [2, 6 * D], F32)
            if self.want("normz"):
                self.stage_normz(l, xname)
            else:
                self.dram("z%d" % l, [T, ZC], F32)
                self.r_ztile = [self.dr["z%d" % l][1]] * NT
            if self.want("na"):
                self.stage_na(l)
            if self.want("mla"):
                self.stage_mla(l)
            if self.want("fourier"):
                self.stage_fourier(l)
            if self.want("rwprep"):
                self.stage_rwprep(l)
            if self.want("rwscan"):
                self.stage_rwscan(l)
            if self.want("rwout"):
                self.stage_rwout(l)
            else:
                for n in ("y_fn%d", "y_na%d", "y_mla%d", "y_rw%d"):
                    if (n % l) not in self.dr:
                        self.dram(n % l, [T, 512], F32)
            if self.want("merge"):
                self.stage_merge(l, xname)
            else:
                self.dram("x1_%d" % l, [T, D], F32)
                self.r_x1_t = [self.dr["x1_%d" % l][1]] * NT
            if self.want("moe"):
                self.stage_moe(l, final=(l == self.depth - 1) and self.final_out)
                xname = "x2_%d" % l
        k.finish([])
        return self.nc


def _na_bias_index():
    rows, kh, W, KW = 32, 8, 64, 16
    dr = np.zeros((5, 5, 128, 128), np.int64)
    dc = np.zeros((5, 5, 128, 128), np.int64)
    va = np.zeros((5, 5, 128, 128), bool)
    kk = np.arange(128)
    qq = np.arange(128)
    for cls, j in enumerate((0, 1, 5, 14, 15)):
        lo = min(max(j - 2, 0), 11)
        for i in range(5):
            kt = lo + i
            krow = (2 * kt + kk // 64)[:, None]
            kc = (kk % 64)[:, None]
            r = (2 * j + qq // 64)[None, :]
            qc = (qq % 64)[None, :]
            rs = np.clip(r - kh // 2, 0, rows - kh)
            cs = np.clip(qc - KW // 2, 0, W - KW)
            v = (krow >= rs) & (krow < rs + kh) & (kc >= cs) & (kc < cs + KW)
            dr[cls, i] = np.clip(krow - r + 8 - 1, 0, 14)
            dc[cls, i] = np.clip(kc - qc + KW - 1, 0, 2 * KW - 2)
            va[cls, i] = v
    return dr, dc, va


def _rope_tables():
    pos = np.arange(L)
    prow, pcol = pos // 64, pos % 64
    freqs = 10000.0 ** (-np.arange(16, dtype=np.float32) / 16)
    ar = prow[:, None].astype(np.float32) * freqs[None, :]
    ac = pcol[:, None].astype(np.float32) * freqs[None, :]
    cos = np.concatenate([np.cos(ar), np.cos(ar), np.cos(ac), np.cos(ac)], 1)
    sin = np.concatenate([-np.sin(ar), np.sin(ar), -np.sin(ac), np.sin(ac)], 1)
    cosT = np.concatenate([np.ones((NCTX, 64), np.float32), cos.astype(np.float32)], 0)
    sinT = np.concatenate([np.zeros((NCTX, 64), np.float32), sin.astype(np.float32)], 0)
    return np.ascontiguousarray(cosT), np.ascontiguousarray(sinT)


_DFT = None


def _dft_consts():
    global _DFT
    if _DFT is None:
        c = np.arange(128)
        ang = 2 * np.pi * ((c[:, None] * c[None, :]) % 128) / 128.0
        dft_c = bf16_np(np.concatenate([np.cos(ang), np.sin(ang)], 1))
        out = []
        for n in (L, NCTX):
            i = np.arange(n, dtype=np.int64)
            ang = 2 * np.pi * ((i[:, None] * i[None, :]) % n) / float(n)
            sc = 1.0 / np.sqrt(n * 128.0)
            out.append(bf16_np(np.stack([np.cos(ang) * sc, -np.sin(ang) * sc], 0)))
        _DFT = (dft_c, out[0], out[1])
    return _DFT


def _rw_consts():
    e2 = np.zeros((128, 64, 128), np.float32)
    for i in range(64):
        e2[i, i, 0:64] = 1.0
        e2[64 + 63 - i, i, 64:128] = 1.0
    pm = np.zeros((128, 128), np.float32)
    for i in range(64):
        pm[i, i] = 1.0
        pm[64 + i, 64 + 63 - i] = 1.0
    j64 = np.ascontiguousarray(np.eye(64, dtype=np.float32)[::-1])
    return e2, pm, j64


def host_core(inp, b):
    f32 = np.float32
    m = {}
    m["xcat"] = np.concatenate([inp["ctx"][b], inp["x"][b]], 0).astype(f32)
    cv = np.stack([inp["c"][b], inp["c_ctx"]], 0)
    m["cvT"] = np.ascontiguousarray(cv.reshape(2, 16, 128).transpose(2, 1, 0))
    return m


def host_inputs(inp, b, depth=DEPTH):
    m = host_shared(inp, depth)
    m.update(host_core(inp, b))
    return m


def host_shared(inp, depth=DEPTH):
    f32 = np.float32
    m = {}
    m["ada_w"] = inp["ada_w"][:depth]
    m["ada_b2"] = np.ascontiguousarray(np.broadcast_to(inp["ada_b"][:depth, None, :], (depth, 2, 6 * D)))
    m["norm1_gT"] = np.ascontiguousarray(inp["norm1_g"][:depth].reshape(depth, 16, 128).transpose(0, 2, 1))
    m["w_in"] = inp["w_in"][:depth]
    m["na_qg"] = np.ascontiguousarray(np.tile(inp["na_q_norm"][:depth, None, :], (1, 1, 8)))
    m["na_kg"] = np.ascontiguousarray(np.tile(inp["na_k_norm"][:depth, None, :], (1, 1, 8)))
    dr, dc, va = _na_bias_index()
    rpb = inp["na_rpb"][:depth]
    gathered = rpb[:, :, dr, dc]
    m["na_bias"] = np.where(va[None, None], gathered, f32(-1e30)).astype(f32)
    m["mla_cqg"] = np.ascontiguousarray(inp["mla_cq_norm"][:depth, None, :])
    m["mla_ckvg"] = np.ascontiguousarray(inp["mla_ckv_norm"][:depth, None, :])
    m["mla_qg"] = np.ascontiguousarray(np.tile(inp["mla_q_norm"][:depth, None, :], (1, 1, 4)))
    m["mla_kng"] = np.ascontiguousarray(np.tile(inp["mla_k_norm"][:depth, None, :128], (1, 1, 4)))
    m["mla_kpg"] = np.ascontiguousarray(inp["mla_k_norm"][:depth, None, 128:192])
    m["mla_w_uq"] = inp["mla_w_uq"][:depth]
    m["mla_w_ukv"] = inp["mla_w_ukv"][:depth]
    m["dft_c"], m["dft_L"], m["dft_C"] = _dft_consts()
    m["rw_mu"] = np.ascontiguousarray(np.concatenate([inp["rw_mu_ks"][:depth], inp["rw_mu_qs"][:depth]], 1)[:, None, :])
    m["rw_kk"] = np.ascontiguousarray(inp["rw_k_k"][:depth, None, :])
    m["rw_ka"] = np.ascontiguousarray(inp["rw_k_a"][:depth, None, :])
    m["rw_w0f"] = np.ascontiguousarray(inp["rw_w0"][:depth].reshape(depth, 1, 1024))
    m["rw_a0f"] = np.ascontiguousarray(inp["rw_a0"][:depth].reshape(depth, 1, 1024))
    wa2 = np.stack([inp["rw_w2"][:depth], inp["rw_a2"][:depth]], 1)
    m["rw_wa2"] = np.ascontiguousarray(wa2.transpose(0, 3, 1, 2, 4))
    m["rw_lnw"] = np.ascontiguousarray(inp["rw_ln_w"][:depth, None, :])
    m["rw_lnb"] = np.ascontiguousarray(inp["rw_ln_b"][:depth, None, :])
    m["rw_rk"] = np.ascontiguousarray(inp["rw_r_k"][:depth].reshape(depth, 1, 512))
    m["rw_g2"] = inp["rw_g2"][:depth]
    m["rw_e2"], m["rw_pm"], m["rw_j64"] = _rw_consts()
    m["w_br"] = inp["w_br"][:depth]
    m["norm2_gb"] = np.ascontiguousarray(inp["norm2_g"][:depth, None, :])
    m["moe_router"] = inp["moe_router"][:depth]
    m["moe_w1"] = inp["moe_w1"][:depth]
    m["moe_w3"] = inp["moe_w3"][:depth]
    m["moe_w2"] = inp["moe_w2"][:depth]
    sel = np.zeros((32, 16, 128), np.float32)
    for e in range(16):
        sel[e, e, :] = 1.0
    m["moe_sel"] = sel
    m["moe_jidx"] = np.ascontiguousarray(np.stack([np.arange(128), np.arange(128) + 128], 1).astype(np.float32))
    m["w_out"] = inp["w_out"][:depth]
    cosT, sinT = _rope_tables()
    m["rope_cos"], m["rope_sin"] = cosT, sinT
    return m


_PROG = None


def kernel(**inputs):
    global _PROG
    inp = {k_: np.asarray(v) for k_, v in inputs.items()}
    if _PROG is None:
        P = Prog(depth=DEPTH)
        P.build()
        _PROG = P
    P = _PROG
    shared = host_shared(inp, DEPTH)
    in_maps = []
    for b in range(8):
        m = dict(shared)
        m.update(host_core(inp, b))
        in_maps.append({n: m[n] for n, kd in P.kinds.items() if kd == "ExternalInput"})
    res = run_bass_kernel_spmd(P.nc, in_maps, core_ids=list(range(8)))
    return np.stack([np.asarray(r["out"], dtype=np.float32) for r in res.results], 0)
```

```python
import numpy as np
import ml_dtypes
from contextlib import ExitStack
import concourse.bass as bass
import concourse.mybir as mybir
from concourse.bass_utils import run_bass_kernel_spmd

F32 = mybir.dt.float32
BF16 = mybir.dt.bfloat16
I32 = mybir.dt.int32
AF = mybir.ActivationFunctionType
ALU = mybir.AluOpType
AX = mybir.AxisListType


class R:
    __slots__ = ("w", "r", "name", "excl")

    def __init__(self, name="", excl=False):
        self.w = []
        self.r = []
        self.name = name
        self.excl = excl


def PR():
    return R("psum", True)


class K:
    ENG = ("tensor", "vector", "scalar", "gpsimd", "sync")
    NDMA = 24
    MAXSEM = 30000

    def __init__(self, nc, es):
        self.nc = nc
        self.es = es
        self.prog = {e: [] for e in self.ENG}
        self.sem = {}
        self.epoch = {e: 0 for e in self.ENG}
        for e in self.ENG:
            self.sem[(e, 0)] = es.enter_context(nc.semaphore("s_%s_0" % e))
        self.cnt = {e: 0 for e in self.ENG}
        self.waited = {e: {} for e in self.ENG}
        self.dsem = [es.enter_context(nc.semaphore("d%d" % i)) for i in range(self.NDMA)]
        self.dval = [0] * self.NDMA
        self.dnext = 0
        self.ninstr = 0

    def sb(self, name, shape, dt=F32, es=None):
        self.uid = getattr(self, "uid", 0) + 1
        name = "%s_u%d" % (name, self.uid)
        return (es or self.es).enter_context(self.nc.sbuf_tensor(name, list(shape), dt))

    def ps(self, name, shape, dt=F32, es=None):
        n = int(np.prod(shape[1:])) * (2 if dt == BF16 else 4)
        assert n % 2048 == 0, (name, shape, n)
        self.uid = getattr(self, "uid", 0) + 1
        name = "%s_u%d" % (name, self.uid)
        return (es or self.es).enter_context(self.nc.psum_tensor(name, list(shape), dt))

    def _key(self, dep):
        return dep[0] if dep[0] in self.ENG else ("d", dep[0])

    def _collect(self, reads, writes):
        deps = {}
        for r in reads:
            for d in r.w:
                k = d[0]
                if deps.get(k, 0) < d[1]:
                    deps[k] = d[1]
        for w in writes:
            for d in w.w + w.r:
                k = d[0]
                if deps.get(k, 0) < d[1]:
                    deps[k] = d[1]
        return deps

    def _emit_waits(self, eng, deps):
        wd = self.waited[eng]
        for k, v in deps.items():
            if wd.get(k, 0) >= v:
                continue
            wd[k] = v
            sem = self.sem[k] if isinstance(k, tuple) else self.dsem[k]
            getattr(self.nc, eng).wait_ge(sem, v)

    def _mark(self, dep, reads, writes):
        for r in reads:
            r.r.append(dep)
        for w in writes:
            w.w = [dep]
            w.r = []

    def op(self, eng, fn, reads=(), writes=()):
        if any(r.excl for r in reads):
            writes = list(writes) + [r for r in reads if r.excl]
            reads = [r for r in reads if not r.excl]
        deps = self._collect(reads, writes)
        self._emit_waits(eng, deps)
        if self.cnt[eng] >= self.MAXSEM:
            self.epoch[eng] += 1
            self.cnt[eng] = 0
            self.sem[(eng, self.epoch[eng])] = self.es.enter_context(
                self.nc.semaphore("s_%s_%d" % (eng, self.epoch[eng])))
        self.cnt[eng] += 1
        key = (eng, self.epoch[eng])
        dep = (key, self.cnt[eng])
        fn(getattr(self.nc, eng)).then_inc(self.sem[key], 1)
        self._mark(dep, reads, writes)
        self.ninstr += 1

    STORE_Q = "scalar"

    def dma(self, out, in_, reads=(), writes=(), q="sync", **kw):
        q = self.STORE_Q if str(out.space) == "DRAM" else "sync"
        deps = self._collect(reads, writes)
        i = self.dnext
        self.dnext = (self.dnext + 1) % self.NDMA
        if self.dval[i] > 0:
            deps[i] = max(deps.get(i, 0), self.dval[i])
        self._emit_waits(q, deps)
        self.dval[i] += 16
        dep = (i, self.dval[i])
        getattr(self.nc, q).dma_start(out=out, in_=in_, **kw).then_inc(self.dsem[i], 16)
        self._mark(dep, reads, writes)
        self.ninstr += 1

    def finish(self, final_regions):
        deps = {}
        for e in self.ENG:
            if self.cnt[e]:
                deps[(e, self.epoch[e])] = self.cnt[e]
        for i in range(self.NDMA):
            if self.dval[i]:
                deps[i] = self.dval[i]
        self._emit_waits("sync", deps)

    def barrier(self):
        deps = {}
        for e in self.ENG:
            if self.cnt[e]:
                deps[(e, self.epoch[e])] = self.cnt[e]
        for i in range(self.NDMA):
            if self.dval[i]:
                deps[i] = self.dval[i]
        for e in self.ENG:
            self._emit_waits(e, dict(deps))


D = 2048
L = 2048
NCTX = 256
T = L + NCTX
NT = T // 128
DEPTH = 2
ZC = 12832
C_NAK, C_NAV, C_CKV, C_KPE, C_RWKS = 0, 512, 1024, 1280, 1344
C_NAQ, C_CQ, C_RWQS, C_FN, C_GATE = 2496, 3008, 3520, 4128, 4640
EPS = 1e-6


def bf16_np(a):
    return np.asarray(a, dtype=np.float32).astype(ml_dtypes.bfloat16)


class Prog:
    def __init__(self, depth=DEPTH, ext=None, stages=None):
        self.depth = depth
        self.ext = ext or {}
        self.stages = stages
        self.nc = bass.Bass("TRN2", target_bir_lowering=False)
        self.es = ExitStack()
        self.k = K(self.nc, self.es)
        self.dr = {}
        self.kinds = {}
        self.modT = {}
        self.final_out = True
        import os
        self.dbg = {kk[4:]: int(v) for kk, v in os.environ.items() if kk.startswith("DBG_")}

    def dram(self, name, shape, dt=F32, kind="Internal"):
        kind = self.ext.get(name, kind)
        t = self.nc.dram_tensor(name, list(shape), dt, kind=kind)
        self.dr[name] = (t.ap(), R(name))
        self.kinds[name] = kind
        return self.dr[name]

    def want(self, s):
        return self.stages is None or s in self.stages

    def consts(self):
        k = self.k
        self.identf = k.sb("identf", [128, 128], F32)
        self.ident = k.sb("ident", [128, 128], BF16)
        self.r_ident = R("ident")
        k.op("gpsimd", lambda e: e.memset(self.identf[:], 0.0), writes=[self.r_ident])
        k.op("gpsimd", lambda e: e.affine_select(out=self.identf[:], in_=self.identf[:], pattern=[[-1, 128]],
                                                  compare_op=ALU.not_equal, fill=1.0, base=0, channel_multiplier=1),
             reads=[self.r_ident], writes=[self.r_ident])
        k.op("vector", lambda e: e.tensor_copy(out=self.ident[:], in_=self.identf[:]),
             reads=[self.r_ident], writes=[self.r_ident])
        ap, r = self.dram("cvT", [128, 16, 2], F32, "ExternalInput")
        self.sT = k.sb("sT", [128, 16, 2], F32)
        self.r_sT = R("sT")
        k.dma(self.sT[:], ap, reads=[r], writes=[self.r_sT])
        k.op("scalar", lambda e: e.activation(out=self.sT[:], in_=self.sT[:], func=AF.Silu),
             reads=[self.r_sT], writes=[self.r_sT])

    def stage_mod(self, l):
        k = self.k
        adaw, r_adaw = self.dr["ada_w"]
        adab, r_adab = self.dr["ada_b2"]
        modD, r_modD = self.dram("modD%d" % l, [2, 6 * D], F32)
        modT = k.sb("modT%d" % l, [128, 6, 16, 2], F32)
        r_modT = R()
        self.modT[l] = (modT, r_modT)
        with ExitStack() as es:
            wt = [k.sb("mod_w%d" % i, [128, 2048], F32, es) for i in range(3)]
            r_wt = [R() for _ in range(3)]
            ps = k.ps("mod_ps", [2, 2048], F32, es)
            r_ps = PR()
            pT = k.ps("mod_pT", [128, 4, 128], F32, es)
            r_pT = PR()
            row = [k.sb("mod_row%d" % i, [128, 2048], F32, es) for i in range(2)]
            r_row = [R() for _ in range(2)]
            for i in range(2):
                k.op("gpsimd", lambda e, i=i: e.memset(row[i][:], 0.0), writes=[r_row[i]])
            bias = k.sb("mod_bias", [2, 6 * D], F32, es)
            r_bias = R()
            k.dma(bias[:], adab[l], reads=[r_adab], writes=[r_bias])
            i2 = self.identf[0:2, 0:2]
            it = 0
            for nb in range(6):
                for kc in range(16):
                    w = wt[it % 3]
                    rw = r_wt[it % 3]
                    it += 1
                    k.dma(w[:], adaw[l, kc * 128:(kc + 1) * 128, nb * 2048:(nb + 1) * 2048], reads=[r_adaw], writes=[rw])

                    def mm(e, w=w, kc=kc):
                        last = None
                        for j in range(4):
                            last = e.matmul(ps[:, j * 512:(j + 1) * 512], lhsT=self.sT[:, kc, :], rhs=w[:, j * 512:(j + 1) * 512],
                                            start=(kc == 0), stop=(kc == 15))
                        return last
                    k.op("tensor", mm, reads=[rw, self.r_sT], writes=[r_ps])
                rt = row[nb % 2]
                rr = r_row[nb % 2]
                k.op("vector", lambda e, rt=rt, nb=nb: e.tensor_tensor(out=rt[0:2, :], in0=ps[:], in1=bias[:, nb * 2048:(nb + 1) * 2048], op=ALU.add),
                     reads=[r_ps, r_bias], writes=[rr])
                k.dma(modD[:, nb * 2048:(nb + 1) * 2048], rt[0:2, :], reads=[rr], writes=[r_modD], q="gpsimd")

                for g in range(4):
                    def tr(e, rt=rt, g=g):
                        last = None
                        for c in range(4):
                            cc = g * 4 + c
                            last = e.transpose(out=pT[:, c, :], in_=rt[:, cc * 128:(cc + 1) * 128], identity=self.identf[:])
                        return last
                    k.op("tensor", tr, reads=[rr, self.r_ident], writes=[r_pT])
                    k.op("vector", lambda e, nb=nb, g=g: e.tensor_copy(out=modT[:, nb, g * 4:(g + 1) * 4, :], in_=pT[:, :, 0:2]),
                         reads=[r_pT], writes=[r_modT])
            k.barrier()

    def stage_normz(self, l, xname):
        k = self.k
        xin, r_xin = self.dr[xname]
        win, r_win = self.dr["w_in"]
        g1T_d, r_g1T = self.dr["norm1_gT"]
        z, r_z = self.dram("z%d" % l, [T, ZC], F32)
        self.r_ztile = [R() for _ in range(NT)]
        modT, r_modT = self.modT[l]
        with ExitStack() as es:
            hT = k.sb("hT", [128, 16, T], BF16, es)
            r_hT = [[R() for _ in range(16)] for _ in range(NT)]
            aT = k.sb("nz_aT", [128, 16, 2], F32, es)
            r_aT = R()
            gT = k.sb("nz_gT", [128, 16], F32, es)
            k.dma(gT[:], g1T_d[l], reads=[r_g1T], writes=[r_aT])
            k.op("vector", lambda e: e.tensor_scalar(out=aT[:], in0=modT[:, 1, :, :], scalar1=1.0, scalar2=None, op0=ALU.add),
                 reads=[r_modT], writes=[r_aT])
            k.op("vector", lambda e: e.tensor_tensor(out=aT[:], in0=aT[:], in1=gT[:].unsqueeze(2).to_broadcast([128, 16, 2]), op=ALU.mult),
                 reads=[r_aT], writes=[r_aT])
            self.norm_to_hT(es, xin, r_xin, hT, r_hT, aT, r_aT, modT[:, 0, :, :], r_modT, "nz")
            wf = [k.sb("nz_wf%d" % i, [128, 4, 512], F32, es) for i in range(3)]
            r_wf = [R() for _ in range(3)]
            wb = [k.sb("nz_wb%d" % i, [128, 16, 512], BF16, es) for i in range(2)]
            r_wb = [[R() for _ in range(4)] for _ in range(2)]
            pz = [k.ps("nz_pz%d" % i, [128, 512], F32, es) for i in range(4)]
            r_pz = [PR() for _ in range(4)]
            zo = [k.sb("nz_zo%d" % i, [128, 512], F32, es) for i in range(4)]
            r_zo = [R() for _ in range(4)]
            ncb = (ZC + 511) // 512
            cnt = 0
            ld = 0
            for cb in range(self.dbg.get("NZ_NCB", ncb)):
                c0 = cb * 512
                cw = min(512, ZC - c0)
                b, rb = wb[cb % 2], r_wb[cb % 2]
                for g in range(4):
                    f, rf = wf[ld % 3], r_wf[ld % 3]
                    k.dma(f[:, :, 0:cw], win[l, g * 512:(g + 1) * 512, c0:c0 + cw].rearrange("(kc p) n -> p kc n", p=128),
                          reads=[r_win], writes=[rf])
                    eng = ("vector", "scalar")[ld % 2]
                    ld += 1
                    if eng == "scalar":
                        k.op("scalar", lambda e, f=f, b=b, cw=cw, g=g: e.activation(out=b[:, g * 4:(g + 1) * 4, 0:cw], in_=f[:, :, 0:cw], func=AF.Copy),
                             reads=[rf], writes=[rb[g]])
                    else:
                        k.op(eng, lambda e, f=f, b=b, cw=cw, g=g: e.tensor_copy(out=b[:, g * 4:(g + 1) * 4, 0:cw], in_=f[:, :, 0:cw]),
                             reads=[rf], writes=[rb[g]])
                for t in range(self.dbg.get("NZ_NT", NT)):
                    p, rp = pz[cnt % 4], r_pz[cnt % 4]
                    o, ro = zo[cnt % 4], r_zo[cnt % 4]
                    cnt += 1

                    def mm(e, p=p, t=t, b=b, cw=cw):
                        last = None
                        for kc in range(16):
                            last = e.matmul(p[:, 0:cw], lhsT=hT[:, kc, t * 128:(t + 1) * 128], rhs=b[:, kc, 0:cw],
                                            start=(kc == 0), stop=(kc == 15))
                        return last
                    k.op("tensor", mm, reads=rb + r_hT[t], writes=[rp])
                    if t % 2 == 0:
                        k.op("vector", lambda e, o=o, p=p, cw=cw: e.tensor_copy(out=o[:, 0:cw], in_=p[:, 0:cw]), reads=[rp], writes=[ro])
                    else:
                        k.op("scalar", lambda e, o=o, p=p, cw=cw: e.activation(out=o[:, 0:cw], in_=p[:, 0:cw], func=AF.Copy), reads=[rp], writes=[ro])
                    k.dma(z[t * 128:(t + 1) * 128, c0:c0 + cw], o[:, 0:cw], reads=[ro], writes=[self.r_ztile[t]], q="gpsimd")
            k.barrier()

    def norm_to_hT(self, es, xin, r_xin, hT, r_hT, aT, r_aT, shT, r_shT, pfx, h_tok=None):
        k = self.k
        xt = [k.sb(pfx + "_xt%d" % i, [128, D], F32, es) for i in range(2)]
        r_xt = [R() for _ in range(2)]
        xn = [k.sb(pfx + "_xn%d" % i, [128, D], BF16, es) for i in range(2)]
        r_xn = [R() for _ in range(2)]
        junk = k.sb(pfx + "_junk", [128, D], BF16, es)
        r_junk = R()
        st = k.sb(pfx + "_st", [128, NT, 2], F32, es)
        r_stl = [R() for _ in range(NT)]
        pt = [k.ps(pfx + "_pt%d" % i, [128, 8, 128], BF16, es) for i in range(2)]
        r_pt = [PR() for _ in range(2)]
        for t in range(self.dbg.get("NZ_NT", NT)):
            row = 1 if t < 2 else 0
            r_st = r_stl[t]
            x_, rx = xt[t % 2], r_xt[t % 2]
            n_, rn = xn[t % 2], r_xn[t % 2]
            STEP = self.dbg.get("STEP", 99)
            if STEP < 1:
                continue
            k.dma(x_[:], xin[t * 128:(t + 1) * 128, :], reads=[r_xin], writes=[rx])
            if STEP < 2:
                continue
            k.op("scalar", lambda e, x_=x_, t=t: e.activation(out=junk[:], in_=x_[:], func=AF.Square, accum_out=st[:, t, 0:1]),
                 reads=[rx], writes=[r_junk, r_st])
            if STEP < 3:
                continue
            k.op("scalar", lambda e, t=t: e.activation(out=st[:, t, 1:2], in_=st[:, t, 0:1], func=AF.Sqrt, bias=self.eps_t[:, 0:1], scale=1.0 / D),
                 reads=[r_st], writes=[r_st])
            if STEP < 4:
                continue
            k.op("vector", lambda e, t=t: e.reciprocal(out=st[:, t, 1:2], in_=st[:, t, 1:2]), reads=[r_st], writes=[r_st])
            if STEP < 5:
                continue
            k.op("vector", lambda e, x_=x_, n_=n_, t=t: e.tensor_scalar(out=n_[:], in0=x_[:], scalar1=st[:, t, 1:2], scalar2=None, op0=ALU.mult),
                 reads=[rx, r_st], writes=[rn])
            if STEP < 6:
                continue
            for half in range(2):
                p, rp = pt[half], r_pt[half]

                def tr(e, p=p, n_=n_, half=half):
                    last = None
                    for j in range(8):
                        dc = half * 8 + j
                        last = e.transpose(out=p[:, j, :], in_=n_[:, dc * 128:(dc + 1) * 128], identity=self.ident[:])
                    return last
                k.op("tensor", tr, reads=[rn, self.r_ident], writes=[rp])
                if STEP < 7:
                    continue
                for j in range(8):
                    dc = half * 8 + j
                    if STEP == 7 and j % 2 == 1:
                        continue
                    if STEP == 8 and j % 2 == 0:
                        continue
                    eng = "vector" if half == 0 else "scalar"
                    if eng == "vector":
                        k.op("vector", lambda e, p=p, j=j, dc=dc, t=t, row=row: e.tensor_scalar(
                            out=hT[:, dc, t * 128:(t + 1) * 128], in0=p[:, j, :], scalar1=aT[:, dc, row:row + 1],
                            scalar2=shT[:, dc, row:row + 1], op0=ALU.mult, op1=ALU.add),
                            reads=[rp, r_aT, r_shT], writes=[r_hT[t][dc]])
                    else:
                        k.op("scalar", lambda e, p=p, j=j, dc=dc, t=t, row=row: e.activation(
                            out=hT[:, dc, t * 128:(t + 1) * 128], in_=p[:, j, :], func=AF.Identity,
                            scale=aT[:, dc, row:row + 1], bias=shT[:, dc, row:row + 1]),
                            reads=[rp, r_aT, r_shT], writes=[r_hT[t][dc]])

    def head_rms(self, eng_sq, x_, w, nh, dh, st, junk, out, gain_bc, reads, r_st, r_junk, writes, tmp, r_tmp):
        k = self.k
        x3 = x_.rearrange("p (h d) -> p h d", h=nh)
        k.op("vector", lambda e: e.tensor_tensor(out=junk, in0=x_, in1=x_, op=ALU.mult), reads=reads, writes=[r_junk])
        k.op("vector", lambda e: e.tensor_reduce(out=st, in_=junk.rearrange("p (h d) -> p h d", h=nh), axis=AX.X, op=ALU.add),
             reads=[r_junk], writes=[r_st])
        k.op("scalar", lambda e: e.activation(out=st, in_=st, func=AF.Sqrt, bias=self.eps_t[:, 0:1], scale=1.0 / dh),
             reads=[r_st], writes=[r_st])
        k.op("vector", lambda e: e.reciprocal(out=st, in_=st), reads=[r_st], writes=[r_st])
        k.op("vector", lambda e: e.tensor_tensor(out=tmp.rearrange("p (h d) -> p h d", h=nh), in0=x3,
                                                in1=st.unsqueeze(2).to_broadcast([128, nh, dh]), op=ALU.mult),
             reads=reads + [r_st], writes=[r_tmp])
        k.op("gpsimd", lambda e: e.tensor_tensor(out=out, in0=tmp, in1=gain_bc, op=ALU.mult), reads=[r_tmp] + [self.r_gain], writes=writes)

    def stage_na(self, l):
        k = self.k
        z, _ = self.dr["z%d" % l]
        r_zt = self.r_ztile
        yd, r_yd = self.dram("y_na%d" % l, [T, 512], F32)
        gq_d, r_gq = self.dr["na_qg"]
        gk_d, r_gk = self.dr["na_kg"]
        bias_d, r_bias_d = self.dr["na_bias"]
        with ExitStack() as es:
            QT = k.sb("na_QT", [128, 4, T], BF16, es)
            KT = k.sb("na_KT", [128, 4, T], BF16, es)
            r_QT = [R() for _ in range(NT)]
            r_KT = [R() for _ in range(NT)]
            Va = k.sb("na_V", [128, NT, 8, 65], BF16, es)
            r_Va = [R() for _ in range(NT)]
            yna = k.sb("na_y", [128, NT, 512], F32, es)
            r_yna = [R() for _ in range(NT)]
            gq = k.sb("na_gq", [128, 512], F32, es)
            gk = k.sb("na_gk", [128, 512], F32, es)
            self.r_gain = R()
            k.dma(gq[:], gq_d[l].partition_broadcast(128), reads=[r_gq], writes=[self.r_gain])
            k.dma(gk[:], gk_d[l].partition_broadcast(128), reads=[r_gk], writes=[self.r_gain])
            k.op("vector", lambda e: e.tensor_scalar(out=gq[:], in0=gq[:], scalar1=0.125, scalar2=None, op0=ALU.mult),
                 reads=[self.r_gain], writes=[self.r_gain])
            r_ones = R()
            k.op("gpsimd", lambda e: e.memset(Va[:, :, :, 64:65], 1.0), writes=r_Va)
            xin = [k.sb("na_x%d" % i, [128, 512], F32, es) for i in range(6)]
            r_xin = [R() for _ in range(6)]
            junk = k.sb("na_junk", [128, 512], F32, es)
            r_junk = R()
            tmp = k.sb("na_tmp", [128, 512], F32, es)
            r_tmp = R()
            st = k.sb("na_st", [128, NT, 2, 8], F32, es)
            xb = [k.sb("na_xb%d" % i, [128, 512], BF16, es) for i in range(2)]
            r_xb = [R() for _ in range(2)]
            ptr = [k.ps("na_ptr%d" % i, [128, 8, 128], BF16, es) for i in range(1)]
            r_ptr = [PR() for _ in range(1)]
            ld = 0
            nb = 0
            for t in range(NT):
                rows = slice(t * 128, (t + 1) * 128)
                for which, c0 in ((0, C_NAQ), (1, C_NAK)):
                    x_, rx = xin[ld % 6], r_xin[ld % 6]
                    ld += 1
                    k.dma(x_[:], z[rows, c0:c0 + 512], reads=[r_zt[t]], writes=[rx])
                    b_, rb = xb[nb % 2], r_xb[nb % 2]
                    nb += 1
                    r_st = R()
                    self.head_rms(None, x_[:], 512, 8, 64, st[:, t, which, :], junk[:], b_[:], (gq if which == 0 else gk)[:],
                                  [rx], r_st, r_junk, [rb], tmp[:], r_tmp)
                    p, rp = ptr[0], r_ptr[0]

                    def tr(e, p=p, b_=b_):
                        last = None
                        for j in range(4):
                            last = e.transpose(out=p[:, j, :], in_=b_[:, j * 128:(j + 1) * 128], identity=self.ident[:])
                        return last
                    k.op("tensor", tr, reads=[rb, self.r_ident], writes=[rp])
                    dst = QT if which == 0 else KT
                    rd = (r_QT if which == 0 else r_KT)[t]
                    k.op("scalar", lambda e, dst=dst, p=p, rows=rows: e.activation(out=dst[:, :, rows], in_=p[:, 0:4, :], func=AF.Copy),
                         reads=[rp], writes=[rd])
                x_, rx = xin[ld % 6], r_xin[ld % 6]
                ld += 1
                k.dma(x_[:], z[rows, C_NAV:C_NAV + 512], reads=[r_zt[t]], writes=[rx])
                k.op("vector", lambda e, x_=x_, t=t: e.tensor_copy(out=Va[:, t, :, 0:64], in_=x_[:].rearrange("p (h d) -> p h d", h=8)),
                     reads=[rx], writes=[r_Va[t]])
            bias = [k.sb("na_bias%d" % i, [128, 5, 5, 128], F32, es) for i in range(2)]
            r_bias = [R() for _ in range(2)]
            stp = [k.ps("na_st%d" % i, [128, 8, 128], F32, es) for i in range(2)]
            r_stp = [PR() for _ in range(2)]
            po = [k.ps("na_po%d" % i, [128, 512], F32, es) for i in range(2)]
            r_po = [PR() for _ in range(2)]
            sbt = [k.sb("na_sb%d" % i, [128, 5, 128], F32, es) for i in range(2)]
            r_sbt = [R() for _ in range(2)]
            PT = [k.sb("na_PT%d" % i, [128, 7, 128], BF16, es) for i in range(2)]
            r_PT = [R() for _ in range(2)]
            rc = k.sb("na_rc", [128, 2], F32, es)
            r_rc = [R(), R()]
            it = 0
            for h in range(8):
                bs, rbs = bias[h % 2], r_bias[h % 2]
                for c in range(5):
                    k.dma(bs[:, c, :, :], bias_d[l, h, c].rearrange("t k q -> k t q"), reads=[r_bias_d], writes=[rbs])
                hp, hs = h // 2, (h % 2) * 64
                for t in range(NT):
                    if t < 2:
                        kts = [0, 1]
                        nbias = 0
                        cls = 0
                    else:
                        j = t - 2
                        lo = min(max(j - 2, 0), 11)
                        kts = [lo + 2 + i for i in range(5)] + [0, 1]
                        nbias = 5
                        cls = {0: 0, 1: 1, 14: 3, 15: 4}.get(j, 2)
                    nk = len(kts)
                    sp, rsp = stp[it % 2], r_stp[it % 2]
                    pp, rpp = PT[it % 2], r_PT[it % 2]
                    sb_, rsb = sbt[it % 2], r_sbt[it % 2]
                    o_, ro = po[it % 2], r_po[it % 2]
                    rcc, rrc = rc[:, it % 2:it % 2 + 1], r_rc[it % 2]
                    it += 1

                    def qk(e, sp=sp, kts=kts, t=t, hp=hp, hs=hs):
                        last = None
                        for i, kt in enumerate(kts):
                            last = e.matmul(sp[:, i, :], lhsT=KT[hs:hs + 64, hp, kt * 128:(kt + 1) * 128],
                                            rhs=QT[hs:hs + 64, hp, t * 128:(t + 1) * 128], start=True, stop=True)
                        return last
                    k.op("tensor", qk, reads=[r_KT[kt] for kt in kts] + [r_QT[t]], writes=[rsp])
                    if nbias:
                        k.op("vector", lambda e, sb_=sb_, sp=sp, bs=bs, cls=cls: e.tensor_tensor(out=sb_[:], in0=sp[:, 0:5, :], in1=bs[:, cls, :, :], op=ALU.add),
                             reads=[rsp, rbs], writes=[rsb])
                        k.op("scalar", lambda e, pp=pp, sb_=sb_: e.activation(out=pp[:, 0:5, :], in_=sb_[:], func=AF.Exp), reads=[rsb], writes=[rpp])
                        k.op("scalar", lambda e, pp=pp, sp=sp: e.activation(out=pp[:, 5:7, :], in_=sp[:, 5:7, :], func=AF.Exp), reads=[rsp, rpp], writes=[rpp])
                    else:
                        k.op("scalar", lambda e, pp=pp, sp=sp: e.activation(out=pp[:, 0:2, :], in_=sp[:, 0:2, :], func=AF.Exp), reads=[rsp], writes=[rpp])

                    def pv(e, o_=o_, pp=pp, kts=kts, h=h, nk=nk):
                        last = None
                        for i, kt in enumerate(kts):
                            last = e.matmul(o_[:, 0:65], lhsT=pp[:, i, :], rhs=Va[:, kt, h, :], start=(i == 0), stop=(i == nk - 1))
                        return last
                    k.op("tensor", pv, reads=[rpp] + [r_Va[kt] for kt in kts], writes=[ro])
                    k.op("vector", lambda e, rcc=rcc, o_=o_: e.reciprocal(out=rcc, in_=o_[:, 64:65]), reads=[ro], writes=[rrc])
                    k.op("vector", lambda e, rcc=rcc, o_=o_, t=t, h=h: e.tensor_scalar(out=yna[:, t, h * 64:(h + 1) * 64], in0=o_[:, 0:64],
                                                                                   scalar1=rcc, scalar2=None, op0=ALU.mult),
                         reads=[ro, rrc], writes=[r_yna[t]])
            for t in range(NT):
                k.dma(yd[t * 128:(t + 1) * 128, :], yna[:, t, :], reads=[r_yna[t]], writes=[r_yd], q="gpsimd")
            k.barrier()

    def stage_mla(self, l):
        k = self.k
        z, _ = self.dr["z%d" % l]
        r_zt = self.r_ztile
        yd, r_yd = self.dram("y_mla%d" % l, [T, 512], F32)
        MSC = 192.0 ** -0.5
        with ExitStack() as es:
            QTn = k.sb("ml_QTn", [128, 4, T], BF16, es)
            KTn = k.sb("ml_KTn", [128, 4, T], BF16, es)
            QTp = k.sb("ml_QTp", [128, 2, T], BF16, es)
            KTp = k.sb("ml_KTp", [128, 2, T], BF16, es)
            r_Q = [R() for _ in range(NT)]
            r_K = [R() for _ in range(NT)]
            Va = k.sb("ml_V", [128, NT, 4, 129], BF16, es)
            r_Va = [R() for _ in range(NT)]
            ym = k.sb("ml_y", [128, NT, 512], F32, es)
            r_ym = [R() for _ in range(NT)]
            k.op("gpsimd", lambda e: e.memset(Va[:, :, :, 128:129], 1.0), writes=r_Va)
            with ExitStack() as es2:
                self.r_gain = R()
                g_cq = k.sb("ml_gcq", [128, 512], F32, es2)
                g_ckv = k.sb("ml_gckv", [128, 256], F32, es2)
                g_q = k.sb("ml_gq", [128, 768], F32, es2)
                g_kn = k.sb("ml_gkn", [128, 512], F32, es2)
                g_kp = k.sb("ml_gkp", [128, 64], F32, es2)
                for tl, nm in ((g_cq, "mla_cqg"), (g_ckv, "mla_ckvg"), (g_q, "mla_qg"), (g_kn, "mla_kng"), (g_kp, "mla_kpg")):
                    ap, rr = self.dr[nm]
                    k.dma(tl[:], ap[l].partition_broadcast(128), reads=[rr], writes=[self.r_gain])
                k.op("vector", lambda e: e.tensor_scalar(out=g_q[:], in0=g_q[:], scalar1=MSC, scalar2=None, op0=ALU.mult),
                     reads=[self.r_gain], writes=[self.r_gain])
                wuq = k.sb("ml_wuq", [128, 4, 768], BF16, es2)
                wukv = k.sb("ml_wukv", [128, 2, 1024], BF16, es2)
                r_w = R()
                wst = k.sb("ml_wst", [128, 4, 768], F32, es2)
                r_wst = R()
                ap, rr = self.dr["mla_w_uq"]
                k.dma(wst[:], ap[l].rearrange("(kc p) n -> p kc n", p=128), reads=[rr], writes=[r_wst])
                k.op("vector", lambda e: e.tensor_copy(out=wuq[:], in_=wst[:]), reads=[r_wst], writes=[r_w])
                ap, rr = self.dr["mla_w_ukv"]
                wst2 = wst[:].rearrange("p a b -> p (a b)")[:, 0:2048].rearrange("p (a b) -> p a b", a=2)
                k.dma(wst2, ap[l].rearrange("(kc p) n -> p kc n", p=128), reads=[rr], writes=[r_wst])
                k.op("vector", lambda e: e.tensor_copy(out=wukv[:], in_=wst2), reads=[r_wst], writes=[r_w])
                cos_d, r_cos = self.dr["rope_cos"]
                sin_d, r_sin = self.dr["rope_sin"]
                xin = [k.sb("ml_x%d" % i, [128, 832], F32, es2) for i in range(2)]
                r_xin = [R() for _ in range(2)]
                cs = [k.sb("ml_cs%d" % i, [128, 2, 64], F32, es2) for i in range(2)]
                r_cs = [R() for _ in range(2)]
                junk = k.sb("ml_junk", [128, 1024], F32, es2)
                r_junk = R()
                tmp = k.sb("ml_tmp", [128, 1024], F32, es2)
                r_tmp = R()
                st = k.sb("ml_st", [128, NT, 16], F32, es2)
                cb = k.sb("ml_cb", [128, 768], BF16, es2)
                r_cb = [R(), R()]
                cT = k.sb("ml_cT", [128, 6, 128], BF16, es2)
                r_cT = [R(), R()]
                qf = k.sb("ml_qf", [128, 768], F32, es2)
                r_qf = R()
                qn = k.sb("ml_qn", [128, 768], F32, es2)
                r_qn = R()
                kvf = k.sb("ml_kvf", [128, 1024], F32, es2)
                r_kvf = R()
                qb = k.sb("ml_qb", [128, 4, 128], BF16, es2)
                qpb = k.sb("ml_qpb", [128, 4, 64], BF16, es2)
                kb = k.sb("ml_kb", [128, 4, 128], BF16, es2)
                kpb = k.sb("ml_kpb", [128, 4, 64], BF16, es2)
                r_qb, r_qpb, r_kb, r_kpb = R(), R(), R(), R()
                rp1 = k.sb("ml_rp1", [128, 4, 64], F32, es2)
                rp2 = k.sb("ml_rp2", [128, 4, 64], F32, es2)
                r_rp1, r_rp2 = R(), R()
                kpg = k.sb("ml_kpgt", [128, 64], F32, es2)
                kpr = k.sb("ml_kpr", [128, 64], F32, es2)
                r_kpg, r_kpr = R(), R()
                ptr = k.ps("ml_ptr", [128, 8, 128], BF16, es2)
                r_ptr = PR()
                pq = k.ps("ml_pq", [128, 2, 512], F32, es2)
                r_pq = PR()
                pkv = k.ps("ml_pkv", [128, 2, 512], F32, es2)
                r_pkv = PR()

                def rope(x3, nh, cs_, out3, reads, writes):
                    cosb = cs_[:, 0, :].unsqueeze(1).to_broadcast([128, nh, 64])
                    k.op("vector", lambda e: e.tensor_tensor(out=rp1[:, 0:nh, :], in0=x3, in1=cosb, op=ALU.mult), reads=reads, writes=[r_rp1])
                    x5 = x3.rearrange("p h (g a d) -> p h g a d", g=2, a=2)
                    o5 = rp2[:, 0:nh, :].rearrange("p h (g a d) -> p h g a d", g=2, a=2)
                    s4 = cs_[:, 1, :].rearrange("p (g a d) -> p g a d", g=2, a=2)
                    for h in range(nh):
                        for a in range(2):
                            k.op("gpsimd", lambda e, h=h, a=a: e.tensor_tensor(out=o5[:, h, :, a, :], in0=x5[:, h, :, 1 - a, :], in1=s4[:, :, a, :], op=ALU.mult),
                                 reads=reads, writes=[r_rp2])
                    k.op("vector", lambda e: e.tensor_tensor(out=out3, in0=rp1[:, 0:nh, :], in1=rp2[:, 0:nh, :], op=ALU.add),
                         reads=[r_rp1, r_rp2], writes=writes)

                for t in range(NT):
                    rows = slice(t * 128, (t + 1) * 128)
                    x_, rx = xin[t % 2], r_xin[t % 2]
                    cs_, rcs = cs[t % 2], r_cs[t % 2]
                    k.dma(x_[:, 0:512], z[rows, C_CQ:C_CQ + 512], reads=[r_zt[t]], writes=[rx])
                    k.dma(x_[:, 512:832], z[rows, C_CKV:C_CKV + 320], reads=[r_zt[t]], writes=[rx])
                    k.dma(cs_[:, 0, :], cos_d[rows, :], reads=[r_cos], writes=[rcs])
                    k.dma(cs_[:, 1, :], sin_d[rows, :], reads=[r_sin], writes=[rcs])
                    self.head_rms(None, x_[:, 0:512], 512, 1, 512, st[:, t, 0:1], junk[:, 0:512], cb[:, 0:512], g_cq[:],
                                  [rx], R(), r_junk, [r_cb[0]], tmp[:, 0:512], r_tmp)
                    self.head_rms(None, x_[:, 512:768], 256, 1, 256, st[:, t, 1:2], junk[:, 0:256], cb[:, 512:768], g_ckv[:],
                                  [rx], R(), r_junk, [r_cb[1]], tmp[:, 0:256], r_tmp)

                    def tr1(e):
                        last = None
                        for j in range(6):
                            last = e.transpose(out=ptr[:, j, :], in_=cb[:, j * 128:(j + 1) * 128], identity=self.ident[:])
                        return last
                    k.op("tensor", tr1, reads=r_cb + [self.r_ident], writes=[r_ptr])
                    k.op("scalar", lambda e: e.activation(out=cT[:], in_=ptr[:, 0:6, :], func=AF.Copy), reads=[r_ptr], writes=r_cT)

                    def mmq(e):
                        last = None
                        for hf in range(2):
                            for kc in range(4):
                                last = e.matmul(pq[:, hf, 0:384], lhsT=cT[:, kc, :], rhs=wuq[:, kc, hf * 384:(hf + 1) * 384],
                                                start=(kc == 0), stop=(kc == 3))
                        return last
                    k.op("tensor", mmq, reads=r_cT + [r_w], writes=[r_pq])

                    def mmkv(e):
                        last = None
                        for hf in range(2):
                            for kc in range(2):
                                last = e.matmul(pkv[:, hf, :], lhsT=cT[:, 4 + kc, :], rhs=wukv[:, kc, hf * 512:(hf + 1) * 512],
                                                start=(kc == 0), stop=(kc == 1))
                        return last
                    k.op("tensor", mmkv, reads=r_cT + [r_w], writes=[r_pkv])
                    k.op("scalar", lambda e: e.activation(out=qf[:].rearrange("p (a b) -> p a b", a=2), in_=pq[:, :, 0:384], func=AF.Copy),
                         reads=[r_pq], writes=[r_qf])
                    k.op("vector", lambda e: e.tensor_copy(out=kvf[:].rearrange("p (a b) -> p a b", a=2), in_=pkv[:]), reads=[r_pkv], writes=[r_kvf])
                    self.head_rms(None, qf[:], 768, 4, 192, st[:, t, 2:6], junk[:, 0:768], qn[:], g_q[:], [r_qf], R(), r_junk, [r_qn],
                                  tmp[:, 0:768], r_tmp)
                    qn3 = qn[:].rearrange("p (h d) -> p h d", h=4)
                    k.op("vector", lambda e: e.tensor_copy(out=qb[:], in_=qn3[:, :, 0:128]), reads=[r_qn], writes=[r_qb])
                    rope(qn3[:, :, 128:192], 4, cs_, qpb[:], [r_qn, rcs], [r_qpb])
                    kv4 = kvf[:].rearrange("p (h s d) -> p h s d", h=4, s=2)
                    k.op("vector", lambda e, t=t: e.tensor_copy(out=Va[:, t, :, 0:128], in_=kv4[:, :, 1, :]), reads=[r_kvf], writes=[r_Va[t]])
                    kpe = x_[:, 768:832]
                    r_sk = R()
                    k.op("vector", lambda e: e.tensor_tensor(out=junk[:, 0:512].rearrange("p (h d) -> p h d", h=4), in0=kv4[:, :, 0, :], in1=kv4[:, :, 0, :], op=ALU.mult),
                         reads=[r_kvf], writes=[r_junk])
                    k.op("vector", lambda e, t=t: e.tensor_reduce(out=st[:, t, 6:10], in_=junk[:, 0:512].rearrange("p (h d) -> p h d", h=4), axis=AX.X, op=ALU.add),
                         reads=[r_junk], writes=[r_sk])
                    k.op("vector", lambda e: e.tensor_tensor(out=junk[:, 512:576], in0=kpe, in1=kpe, op=ALU.mult), reads=[rx], writes=[r_junk])
                    k.op("vector", lambda e, t=t: e.tensor_reduce(out=st[:, t, 10:11], in_=junk[:, 512:576], axis=AX.X, op=ALU.add), reads=[r_junk], writes=[r_sk])
                    k.op("vector", lambda e, t=t: e.tensor_scalar(out=st[:, t, 6:10], in0=st[:, t, 6:10], scalar1=st[:, t, 10:11], scalar2=None, op0=ALU.add),
                         reads=[r_sk], writes=[r_sk])
                    k.op("scalar", lambda e, t=t: e.activation(out=st[:, t, 6:10], in_=st[:, t, 6:10], func=AF.Sqrt, bias=self.eps_t[:, 0:1], scale=1.0 / 192),
                         reads=[r_sk], writes=[r_sk])
                    k.op("vector", lambda e, t=t: e.reciprocal(out=st[:, t, 6:10], in_=st[:, t, 6:10]), reads=[r_sk], writes=[r_sk])
                    k.op("vector", lambda e, t=t: e.tensor_tensor(out=tmp[:, 0:512].rearrange("p (h d) -> p h d", h=4), in0=kv4[:, :, 0, :],
                                                                 in1=st[:, t, 6:10].unsqueeze(2).to_broadcast([128, 4, 128]), op=ALU.mult),
                         reads=[r_kvf, r_sk], writes=[r_tmp])
                    k.op("gpsimd", lambda e: e.tensor_tensor(out=kb[:].rearrange("p h d -> p (h d)"), in0=tmp[:, 0:512], in1=g_kn[:], op=ALU.mult),
                         reads=[r_tmp, self.r_gain], writes=[r_kb])
                    k.op("vector", lambda e: e.tensor_tensor(out=kpg[:], in0=kpe, in1=g_kp[:], op=ALU.mult), reads=[rx, self.r_gain], writes=[r_kpg])
                    rope(kpg[:].unsqueeze(1), 1, cs_, kpr[:].unsqueeze(1), [r_kpg, rcs], [r_kpr])
                    k.op("vector", lambda e, t=t: e.tensor_tensor(out=kpb[:], in0=kpr[:].unsqueeze(1).to_broadcast([128, 4, 64]),
                                                                 in1=st[:, t, 6:10].unsqueeze(2).to_broadcast([128, 4, 64]), op=ALU.mult),
                         reads=[r_kpr, r_sk], writes=[r_kpb])
                    for (nb_, pb_, rnb, rpb_, dn, dp, rd) in ((qb, qpb, r_qb, r_qpb, QTn, QTp, r_Q[t]), (kb, kpb, r_kb, r_kpb, KTn, KTp, r_K[t])):
                        def tr2(e, nb_=nb_, pb_=pb_):
                            last = None
                            for h in range(4):
                                last = e.transpose(out=ptr[:, h, :], in_=nb_[:, h, :], identity=self.ident[:])
                            for i in range(2):
                                last = e.transpose(out=ptr[:, 4 + i, :], in_=pb_[:, 2 * i:2 * i + 2, :].rearrange("p a b -> p (a b)"), identity=self.ident[:])
                            return last
                        k.op("tensor", tr2, reads=[rnb, rpb_, self.r_ident], writes=[r_ptr])
                        k.op("scalar", lambda e, dn=dn, rows=rows: e.activation(out=dn[:, :, rows], in_=ptr[:, 0:4, :], func=AF.Copy), reads=[r_ptr], writes=[rd])
                        k.op("vector", lambda e, dp=dp, rows=rows: e.tensor_copy(out=dp[:, :, rows], in_=ptr[:, 4:6, :]), reads=[r_ptr], writes=[rd])
                k.barrier()
            with ExitStack() as es3:
                stp = [k.ps("ml_st%d" % i, [128, 512], F32, es3) for i in range(2)]
                r_stp = [PR() for _ in range(2)]
                acc = [k.ps("ml_acc%d" % i, [128, 512], F32, es3) for i in range(4)]
                r_acc = [PR() for _ in range(4)]
                PT = [k.sb("ml_PT%d" % i, [128, 512], BF16, es3) for i in range(3)]
                r_PT = [R() for _ in range(3)]
                rc = k.sb("ml_rc", [128, 4], F32, es3)
                r_rc = [R() for _ in range(4)]
                groups = [[0, 1]] + [[2 + 4 * g + i for i in range(4)] for g in range(4)]
                it = 0
                for grp in groups:
                    kts = [0, 1] if grp[0] == 0 else list(range(NT))
                    q0 = grp[0] * 128
                    nq = len(grp) * 128
                    for h in range(4):
                        hp, hs = h // 2, (h % 2) * 64
                        for ki, kt in enumerate(kts):
                            sp, rsp = stp[it % 2], r_stp[it % 2]
                            pp, rpp = PT[it % 3], r_PT[it % 3]
                            it += 1

                            def qk(e, sp=sp, kt=kt, h=h, hp=hp, hs=hs, q0=q0, nq=nq):
                                e.matmul(sp[:, 0:nq], lhsT=KTn[:, h, kt * 128:(kt + 1) * 128], rhs=QTn[:, h, q0:q0 + nq], start=True, stop=False)
                                return e.matmul(sp[:, 0:nq], lhsT=KTp[hs:hs + 64, hp, kt * 128:(kt + 1) * 128], rhs=QTp[hs:hs + 64, hp, q0:q0 + nq],
                                                start=False, stop=True)
                            k.op("tensor", qk, reads=[r_K[kt]] + [r_Q[t] for t in grp], writes=[rsp])
                            k.op("scalar", lambda e, pp=pp, sp=sp, nq=nq: e.activation(out=pp[:, 0:nq], in_=sp[:, 0:nq], func=AF.Exp), reads=[rsp], writes=[rpp])
                            for i, t in enumerate(grp):
                                k.op("tensor", lambda e, i=i, pp=pp, kt=kt, h=h, ki=ki, kts=kts: e.matmul(
                                    acc[i][:, 0:129], lhsT=pp[:, i * 128:(i + 1) * 128], rhs=Va[:, kt, h, :], start=(ki == 0), stop=(ki == len(kts) - 1)),
                                    reads=[rpp, r_Va[kt]], writes=[r_acc[i]])
                        for i, t in enumerate(grp):
                            k.op("vector", lambda e, i=i: e.reciprocal(out=rc[:, i:i + 1], in_=acc[i][:, 128:129]), reads=[r_acc[i]], writes=[r_rc[i]])
                            k.op("vector", lambda e, i=i, t=t, h=h: e.tensor_scalar(out=ym[:, t, h * 128:(h + 1) * 128], in0=acc[i][:, 0:128],
                                                                                  scalar1=rc[:, i:i + 1], scalar2=None, op0=ALU.mult),
                                 reads=[r_acc[i], r_rc[i]], writes=[r_ym[t]])
                for t in range(NT):
                    k.dma(yd[t * 128:(t + 1) * 128, :], ym[:, t, :], reads=[r_ym[t]], writes=[r_yd])
                k.barrier()

    def stage_fourier(self, l):
        k = self.k
        z, _ = self.dr["z%d" % l]
        r_zt = self.r_ztile
        yd, r_yd = self.dram("y_fn%d" % l, [T, 512], F32)
        csd, r_csd = self.dr["dft_c"]
        dL, r_dL = self.dr["dft_L"]
        dC, r_dC = self.dr["dft_C"]
        with ExitStack() as es:
            G = k.sb("fn_G", [128, NT, 2, 512], BF16, es)
            r_G = [R() for _ in range(NT)]
            CS = k.sb("fn_CS", [128, 256], BF16, es)
            r_CS = R()
            k.dma(CS[:], csd, reads=[r_csd], writes=[r_CS])
            xin = [k.sb("fn_x%d" % i, [128, 512], F32, es) for i in range(2)]
            r_xin = [R() for _ in range(2)]
            xb = [k.sb("fn_xb%d" % i, [128, 512], BF16, es) for i in range(2)]
            r_xb = [R() for _ in range(2)]
            xT = [k.sb("fn_xT%d" % i, [128, 4, 128], BF16, es) for i in range(2)]
            r_xT = [R() for _ in range(2)]
            ptr = k.ps("fn_ptr", [128, 8, 128], BF16, es)
            r_ptr = PR()
            pg = [k.ps("fn_pg%d" % i, [128, 2, 256], F32, es) for i in range(2)]
            r_pg = [PR() for _ in range(2)]
            py = [k.ps("fn_py%d" % i, [128, 512], F32, es) for i in range(2)]
            r_py = [PR() for _ in range(2)]
            for t in range(NT):
                x_, rx = xin[t % 2], r_xin[t % 2]
                b_, rb = xb[t % 2], r_xb[t % 2]
                T_, rT = xT[t % 2], r_xT[t % 2]
                k.dma(x_[:], z[t * 128:(t + 1) * 128, C_FN:C_FN + 512], reads=[r_zt[t]], writes=[rx])
                k.op("gpsimd", lambda e, x_=x_, b_=b_: e.tensor_copy(out=b_[:], in_=x_[:]), reads=[rx], writes=[rb])

                def tr(e, b_=b_):
                    last = None
                    for g in range(4):
                        last = e.transpose(out=ptr[:, g, :], in_=b_[:, g * 128:(g + 1) * 128], identity=self.ident[:])
                    return last
                k.op("tensor", tr, reads=[rb, self.r_ident], writes=[r_ptr])
                k.op("scalar", lambda e, T_=T_: e.activation(out=T_[:], in_=ptr[:, 0:4, :], func=AF.Copy), reads=[r_ptr], writes=[rT])
                for hf in range(2):
                    p, rp = pg[hf], r_pg[hf]

                    def mm(e, p=p, T_=T_, hf=hf):
                        last = None
                        for gg in range(2):
                            last = e.matmul(p[:, gg, :], lhsT=T_[:, hf * 2 + gg, :], rhs=CS[:], start=True, stop=True)
                        return last
                    k.op("tensor", mm, reads=[rT, r_CS], writes=[rp])
                    G4 = G[:, t, :, :].rearrange("p c (g d) -> p c g d", g=4)
                    if hf == 0:
                        k.op("vector", lambda e, p=p, G4=G4, hf=hf: e.tensor_copy(out=G4[:, :, 0:2, :].rearrange("p c g d -> p g c d"),
                                                                                in_=p[:].rearrange("p g (c d) -> p g c d", c=2)),
                             reads=[rp], writes=[r_G[t]])
                    else:
                        k.op("scalar", lambda e, p=p, G4=G4, hf=hf: e.activation(out=G4[:, :, 2:4, :].rearrange("p c g d -> p g c d"),
                                                                                in_=p[:].rearrange("p g (c d) -> p g c d", c=2), func=AF.Copy),
                             reads=[rp, r_G[t]], writes=[r_G[t]])
            Dm = [k.sb("fn_D%d" % i, [128, 2, 16, 128], BF16, es) for i in range(2)]
            r_Dm = [R() for _ in range(2)]
            yo = [k.sb("fn_yo%d" % i, [128, 512], F32, es) for i in range(2)]
            r_yo = [R() for _ in range(2)]
            for to in range(NT):
                ctx = to < 2
                nin = 2 if ctx else 16
                t0 = 0 if ctx else 2
                src = dC if ctx else dL
                rsrc = r_dC if ctx else r_dL
                col0 = (to - t0) * 128
                D_, rD = Dm[to % 2], r_Dm[to % 2]
                for c in range(2):
                    k.dma(D_[:, c, 0:nin, :], src[c, :, col0:col0 + 128].rearrange("(nt p) q -> p nt q", p=128), reads=[rsrc], writes=[rD])
                p, rp = py[to % 2], r_py[to % 2]

                def mm2(e, p=p, D_=D_, nin=nin, t0=t0):
                    last = None
                    n = 0
                    for nt in range(nin):
                        for c in range(2):
                            last = e.matmul(p[:], lhsT=D_[:, c, nt, :], rhs=G[:, t0 + nt, c, :], start=(n == 0), stop=(n == 2 * nin - 1))
                            n += 1
                    return last
                k.op("tensor", mm2, reads=[rD] + [r_G[t0 + nt] for nt in range(nin)], writes=[rp])
                o_, ro = yo[to % 2], r_yo[to % 2]
                k.op("vector", lambda e, o_=o_, p=p: e.tensor_copy(out=o_[:], in_=p[:]), reads=[rp], writes=[ro])
                k.dma(yd[to * 128:(to + 1) * 128, :], o_[:], reads=[ro], writes=[r_yd])
            k.barrier()

    RWC = 3072

    def stage_rwprep(self, l):
        k = self.k
        z, _ = self.dr["z%d" % l]
        r_zt = self.r_ztile
        rwd, r_rwd = self.dram("rwd%d" % l, [2, T, self.RWC], F32)
        rwo, r_rwo = self.dram("rwo%d" % l, [T, 608], F32)
        self.r_rwd_t = [R() for _ in range(NT)]
        with ExitStack() as es:
            def bc(nm, n):
                tl = k.sb("rp_" + nm, [128, n], F32, es)
                ap, rr = self.dr[nm]
                k.dma(tl[:], ap[l].partition_broadcast(128), reads=[rr], writes=[r_c])
                return tl
            r_c = R()
            mu = bc("rw_mu", 1760)
            kkg = bc("rw_kk", 512)
            kag = bc("rw_ka", 512)
            w0 = bc("rw_w0f", 1024)
            a0 = bc("rw_a0f", 1024)
            w2 = k.sb("rp_w2", [32, 2, 2, 512], F32, es)
            ap, rr = self.dr["rw_wa2"]
            k.dma(w2[:], ap[l], reads=[rr], writes=[r_c])
            cur = [k.sb("rp_cur%d" % i, [128, 1760], F32, es) for i in range(2)]
            prv = [k.sb("rp_prv%d" % i, [128, 1760], F32, es) for i in range(2)]
            nxt = [k.sb("rp_nxt%d" % i, [128, 1760], F32, es) for i in range(2)]
            r_cur = [R() for _ in range(2)]
            r_prv = [R() for _ in range(2)]
            r_nxt = [R() for _ in range(2)]
            mx = k.sb("rp_mx", [128, 1760], F32, es)
            r_mx = R()
            d_ = k.sb("rp_d", [128, 1760], F32, es)
            r_d = R()
            st = k.sb("rp_st", [128, NT, 8], F32, es)
            lo_in = k.sb("rp_loin", [128, 128], F32, es)
            r_loin = R()
            lT = k.sb("rp_lT", [32, 4, 128], F32, es)
            r_lT = R()
            ob = [k.sb("rp_ob%d" % i, [128, 2, self.RWC], F32, es) for i in range(1)]
            r_ob = [[R() for _ in range(8)] for _ in range(1)]
            oo = k.sb("rp_oo", [128, 608], F32, es)
            r_oo = R()
            tw = k.sb("rp_tw", [128, 512], F32, es)
            r_tw = R()
            ta = k.sb("rp_ta", [128, 2, 512], F32, es)
            r_ta = [R(), R()]
            ptr = k.ps("rp_ptr", [128, 512], F32, es)
            r_ptr = PR()
            pw = [k.ps("rp_pw%d" % i, [128, 512], F32, es) for i in range(4)]
            r_pw = [PR() for _ in range(4)]
            for t in range(NT):
                rows = slice(t * 128, (t + 1) * 128)
                c_, rc_ = cur[t % 2], r_cur[t % 2]
                p_, rp_ = prv[t % 2], r_prv[t % 2]
                n_, rn_ = nxt[t % 2], r_nxt[t % 2]
                first = t in (0, 2)
                last_ = t in (1, NT - 1)
                for dst, rdst, off, lo_, hi_ in ((c_, rc_, 0, 0, 128), (p_, rp_, -1, 1 if first else 0, 128), (n_, rn_, 1, 0, 127 if last_ else 128)):
                    if hi_ - lo_ < 128:
                        k.op("gpsimd", lambda e, dst=dst: e.memset(dst[:], 0.0), writes=[rdst])
                    r0 = t * 128 + off
                    zt = [r_zt[t]] + ([r_zt[t - 1]] if off < 0 and t > 0 else []) + ([r_zt[t + 1]] if off > 0 and t < NT - 1 else [])
                    k.dma(dst[lo_:hi_, 0:1152], z[r0 + lo_:r0 + hi_, C_RWKS:C_RWKS + 1152], reads=zt, writes=[rdst])
                    k.dma(dst[lo_:hi_, 1152:1760], z[r0 + lo_:r0 + hi_, C_RWQS:C_RWQS + 608], reads=zt, writes=[rdst])
                k.op("vector", lambda e, p_=p_, n_=n_: e.tensor_tensor(out=d_[:], in0=p_[:], in1=n_[:], op=ALU.add), reads=[rp_, rn_], writes=[r_d])
                k.op("vector", lambda e, c_=c_: e.scalar_tensor_tensor(out=d_[:], in0=d_[:], scalar=0.5, in1=c_[:], op0=ALU.mult, op1=ALU.subtract),
                     reads=[r_d, rc_], writes=[r_d])
                k.op("vector", lambda e: e.tensor_tensor(out=d_[:], in0=d_[:], in1=mu[:], op=ALU.mult), reads=[r_d, r_c], writes=[r_d])
                k.op("vector", lambda e, c_=c_: e.tensor_tensor(out=mx[:], in0=d_[:], in1=c_[:], op=ALU.add), reads=[r_d, rc_], writes=[r_mx])
                o_, ro = ob[0], r_ob[0]
                kx = mx[:, 0:512]
                vx = mx[:, 512:1024]
                rx_ = mx[:, 1152:1664]
                r_s = R()
                kk_o = o_[:, 0, 0:512]
                k.op("vector", lambda e: e.tensor_tensor(out=kk_o, in0=kx, in1=kkg[:], op=ALU.mult), reads=[r_mx, r_c], writes=[ro[0]])
                k.op("vector", lambda e: e.tensor_tensor(out=d_[:, 0:512], in0=kk_o, in1=kk_o, op=ALU.mult), reads=[ro[0]], writes=[r_d])
                k.op("vector", lambda e, t=t: e.tensor_reduce(out=st[:, t, :], in_=d_[:, 0:512].rearrange("p (h d) -> p h d", h=8), axis=AX.X, op=ALU.add),
                     reads=[r_d], writes=[r_s])
                k.op("scalar", lambda e, t=t: e.activation(out=st[:, t, :], in_=st[:, t, :], func=AF.Sqrt), reads=[r_s], writes=[r_s])
                k.op("vector", lambda e, t=t: e.tensor_scalar(out=st[:, t, :], in0=st[:, t, :], scalar1=1e-12, scalar2=None, op0=ALU.max), reads=[r_s], writes=[r_s])
                k.op("vector", lambda e, t=t: e.reciprocal(out=st[:, t, :], in_=st[:, t, :]), reads=[r_s], writes=[r_s])
                k.op("vector", lambda e, t=t: e.tensor_tensor(out=kk_o.rearrange("p (h d) -> p h d", h=8), in0=kk_o.rearrange("p (h d) -> p h d", h=8),
                                                             in1=st[:, t, :].unsqueeze(2).to_broadcast([128, 8, 64]), op=ALU.mult),
                     reads=[ro[0], r_s], writes=[ro[0]])
                k.op("gpsimd", lambda e: e.tensor_copy(out=o_[:, 1, 0:512], in_=kk_o), reads=[ro[0]], writes=[ro[1]])
                k.op("gpsimd", lambda e: e.tensor_copy(out=o_[:, :, 2048:2560], in_=rx_.unsqueeze(1).to_broadcast([128, 2, 512])), reads=[r_mx], writes=[ro[2]])
                k.op("gpsimd", lambda e: e.tensor_copy(out=o_[:, :, 2560:3072], in_=vx.unsqueeze(1).to_broadcast([128, 2, 512])), reads=[r_mx], writes=[ro[3]])
                k.op("scalar", lambda e: e.activation(out=lo_in[:, 0:64], in_=mx[:, 1024:1088], func=AF.Tanh), reads=[r_mx], writes=[r_loin])
                k.op("vector", lambda e: e.tensor_copy(out=lo_in[:, 64:128], in_=mx[:, 1088:1152]), reads=[r_mx, r_loin], writes=[r_loin])

                def trl(e):
                    last = None
                    for j in range(4):
                        last = e.transpose(out=ptr[0:32, j * 128:(j + 1) * 128], in_=lo_in[:, j * 32:(j + 1) * 32], identity=self.identf[:])
                    return last
                k.op("tensor", trl, reads=[r_loin, self.r_ident], writes=[r_ptr])
                k.op("vector", lambda e: e.tensor_copy(out=lT[:], in_=ptr[0:32, :].rearrange("p (a b) -> p a b", a=4)), reads=[r_ptr], writes=[r_lT])
                for d in range(2):
                    p, rp = pw[d], r_pw[d]
                    k.op("tensor", lambda e, p=p, d=d: e.matmul(p[:], lhsT=lT[:, d, :], rhs=w2[:, 0, d, :], start=True, stop=True), reads=[r_lT, r_c], writes=[rp])
                    k.op("vector", lambda e, p=p, d=d: e.tensor_tensor(out=tw[:], in0=p[:], in1=w0[:, d * 512:(d + 1) * 512], op=ALU.add), reads=[rp, r_c], writes=[r_tw])
                    k.op("scalar", lambda e: e.activation(out=tw[:], in_=tw[:], func=AF.Sigmoid), reads=[r_tw], writes=[r_tw])
                    k.op("vector", lambda e, d=d: e.tensor_scalar(out=o_[:, d, 512:1024], in0=tw[:], scalar1=-float(np.exp(-0.5)), scalar2=None, op0=ALU.mult), reads=[r_tw], writes=[ro[4 + d]])
                    p, rp = pw[2 + d], r_pw[2 + d]
                    k.op("tensor", lambda e, p=p, d=d: e.matmul(p[:], lhsT=lT[:, 2 + d, :], rhs=w2[:, 1, d, :], start=True, stop=True), reads=[r_lT, r_c], writes=[rp])
                    k.op("vector", lambda e, p=p, d=d: e.tensor_tensor(out=ta[:, d, :], in0=p[:], in1=a0[:, d * 512:(d + 1) * 512], op=ALU.add), reads=[rp, r_c], writes=[r_ta[d]])
                    k.op("scalar", lambda e, d=d: e.activation(out=ta[:, d, :], in_=ta[:, d, :], func=AF.Sigmoid), reads=[r_ta[d]], writes=[r_ta[d]])
                    k.op("gpsimd", lambda e, d=d: e.tensor_tensor(out=o_[:, d, 1024:1536], in0=kk_o, in1=ta[:, d, :], op=ALU.mult), reads=[ro[0], r_ta[d]], writes=[ro[6 + d]])
                    k.op("vector", lambda e, d=d: e.scalar_tensor_tensor(out=ta[:, d, :], in0=ta[:, d, :], scalar=-1.0, in1=kag[:], op0=ALU.add, op1=ALU.mult),
                         reads=[r_ta[d], ro[6 + d], r_c], writes=[r_ta[d]])
                    k.op("vector", lambda e, d=d: e.scalar_tensor_tensor(out=o_[:, d, 1536:2048], in0=ta[:, d, :], scalar=1.0, in1=kx, op0=ALU.add, op1=ALU.mult),
                         reads=[r_ta[d], r_mx], writes=[ro[6 + d]])
                k.op("vector", lambda e: e.tensor_tensor(out=oo[:, 0:512], in0=o_[:, 0, 1536:2048], in1=o_[:, 1, 1536:2048], op=ALU.add), reads=[ro[6], ro[7]], writes=[r_oo])
                k.op("gpsimd", lambda e: e.tensor_copy(out=oo[:, 512:608], in_=mx[:, 1664:1760]), reads=[r_mx, r_oo], writes=[r_oo])
                for d in range(2):
                    k.dma(rwd[d, rows, :], o_[:, d, :], reads=ro, writes=[self.r_rwd_t[t]])
                k.dma(rwo[rows, :], oo[:], reads=[r_oo], writes=[r_rwo])
            k.barrier()

    def stage_rwscan_seq(self, l):
        k = self.k
        rwd, _ = self.dr["rwd%d" % l]
        r_rwd_t = self.r_rwd_t
        yfb, r_yfb = self.dram("rwy%d" % l, [2, T, 512], F32)
        self.r_rwy_t = [R() for _ in range(NT)]
        e2d, r_e2d = self.dr["rw_e2"]
        pmd, r_pmd = self.dr["rw_pm"]
        j64d, r_j64d = self.dr["rw_j64"]
        with ExitStack() as es:
            E2 = k.sb("rs_E2", [128, 64, 128], F32, es)
            Pm = k.sb("rs_Pm", [128, 128], F32, es)
            J64 = k.sb("rs_J64", [64, 64], F32, es)
            r_cst = R()
            for c in range(4):
                k.dma(E2[:, c * 16:(c + 1) * 16, :], e2d[:, c * 16:(c + 1) * 16, :], reads=[r_e2d], writes=[r_cst])
            k.dma(Pm[:], pmd, reads=[r_pmd], writes=[r_cst])
            k.dma(J64[:], j64d, reads=[r_j64d], writes=[r_cst])
            S = k.sb("rs_S", [128, 512], F32, es)
            r_S = R()
            k.op("vector", lambda e: e.memset(S[:], 0.0), writes=[r_S])
            X = [k.sb("rs_X%d" % i, [128, self.RWC], F32, es) for i in range(2)]
            r_X = [R() for _ in range(2)]
            Vr = k.sb("rs_Vr", [128, 8, 2, 64], F32, es)
            r_Vr = R()
            vT = [k.sb("rs_vT%d" % i, [128, 8, 64], F32, es) for i in range(2)]
            r_vT = [R() for _ in range(2)]
            ys = [k.sb("rs_ys%d" % i, [128, 8, 64], F32, es) for i in range(2)]
            r_ys = [R() for _ in range(2)]
            yo = k.sb("rs_yo", [64, 8, 2, 64], F32, es)
            r_yo = R()
            yb2 = k.sb("rs_yb2", [64, 512], F32, es)
            r_yb2 = R()
            yb3 = k.sb("rs_yb3", [64, 512], F32, es)
            r_yb3 = R()
            t1 = k.sb("rs_t1", [128, 512], F32, es)
            t2 = k.sb("rs_t2", [128, 512], F32, es)
            t3 = k.sb("rs_t3", [128, 512], F32, es)
            r_t1, r_t2, r_t3 = R(), R(), R()
            sa = k.sb("rs_sa", [128, 8], F32, es)
            r_sa = R()
            bcp = [k.ps("rs_bc%d" % i, [128, 512], F32, es) for i in range(5)]
            r_bcp = [PR() for _ in range(5)]
            pmisc = k.ps("rs_pm", [128, 8, 128], F32, es)
            r_pmisc = PR()
            pj = k.ps("rs_pj", [128, 512], F32, es)
            r_pj = PR()
            S3 = S[:].rearrange("p (h d) -> p h d", h=8)
            blk = 0
            nblk_dbg = self.dbg.get("RW_NBLK", 10 ** 9)
            for seg0, seglen in ((0, NCTX), (NCTX, L)):
                nb = seglen // 64
                for j in range(nb):
                    if blk >= nblk_dbg:
                        break
                    X_, rX = X[blk % 2], r_X[blk % 2]
                    vT_, rvT = vT[blk % 2], r_vT[blk % 2]
                    ys_, rys = ys[blk % 2], r_ys[blk % 2]
                    blk += 1
                    f0 = seg0 + 64 * j
                    b0 = seg0 + seglen - 64 * (j + 1)
                    k.dma(X_[0:64, :], rwd[0, f0:f0 + 64, :], reads=[r_rwd_t[f0 // 128]], writes=[rX])
                    k.dma(X_[64:128, :], rwd[1, b0:b0 + 64, :], reads=[r_rwd_t[b0 // 128]], writes=[rX])
                    k.op("gpsimd", lambda e, X_=X_: e.tensor_copy(out=Vr[:], in_=X_[:, 2560:3072].rearrange("p (h d) -> p h d", h=8).unsqueeze(2).to_broadcast([128, 8, 2, 64])),
                         reads=[rX], writes=[r_Vr])

                    def vtr(e):
                        last = None
                        for h in range(8):
                            last = e.matmul(pmisc[:, h, :], lhsT=Vr[:, h, :, :].rearrange("p a b -> p (a b)"), rhs=Pm[:], start=True, stop=True)
                        return last
                    k.op("tensor", vtr, reads=[r_Vr, r_cst], writes=[r_pmisc])
                    k.op("vector", lambda e, vT_=vT_: e.tensor_copy(out=vT_[0:64, :, :], in_=pmisc[0:64, :, 0:64]), reads=[r_pmisc], writes=[rvT])
                    k.op("vector", lambda e, vT_=vT_: e.tensor_copy(out=vT_[64:128, :, :], in_=pmisc[64:128, :, 64:128]), reads=[r_pmisc, rvT], writes=[rvT])
                    for i in range(64):
                        for q in range(5):
                            k.op("tensor", lambda e, q=q, i=i, X_=X_: e.matmul(bcp[q][:], lhsT=E2[:, i, :], rhs=X_[:, q * 512:(q + 1) * 512], start=True, stop=True),
                                 reads=[rX, r_cst], writes=[r_bcp[q]])
                        bkk, bdec, bb, bkd, br = [bcp[q][:].rearrange("p (h d) -> p h d", h=8) for q in range(5)]
                        k.op("vector", lambda e, bkk=bkk: e.tensor_tensor(out=t1[:].rearrange("p (h d) -> p h d", h=8), in0=S3, in1=bkk, op=ALU.mult),
                             reads=[r_S, r_bcp[0]], writes=[r_t1])
                        k.op("vector", lambda e: e.tensor_reduce(out=sa[:], in_=t1[:].rearrange("p (h d) -> p h d", h=8), axis=AX.X, op=ALU.add),
                             reads=[r_t1], writes=[r_sa])
                        k.op("vector", lambda e, bdec=bdec: e.tensor_tensor(out=S3, in0=S3, in1=bdec, op=ALU.mult), reads=[r_S, r_bcp[1]], writes=[r_S])
                        k.op("vector", lambda e, bb=bb: e.tensor_tensor(out=t2[:].rearrange("p (h d) -> p h d", h=8), in0=bb,
                                                                       in1=sa[:].unsqueeze(2).to_broadcast([128, 8, 64]), op=ALU.mult),
                             reads=[r_bcp[2], r_sa], writes=[r_t2])
                        k.op("vector", lambda e: e.tensor_tensor(out=S[:], in0=S[:], in1=t2[:], op=ALU.subtract), reads=[r_S, r_t2], writes=[r_S])
                        k.op("vector", lambda e, bkd=bkd, vT_=vT_, i=i: e.tensor_tensor(out=t3[:].rearrange("p (h d) -> p h d", h=8), in0=bkd,
                                                                                       in1=vT_[:, :, i:i + 1].to_broadcast([128, 8, 64]), op=ALU.mult),
                             reads=[r_bcp[3], rvT], writes=[r_t3])
                        k.op("vector", lambda e: e.tensor_tensor(out=S[:], in0=S[:], in1=t3[:], op=ALU.add), reads=[r_S, r_t3], writes=[r_S])
                        k.op("vector", lambda e, br=br: e.tensor_tensor(out=t1[:].rearrange("p (h d) -> p h d", h=8), in0=S3, in1=br, op=ALU.mult),
                             reads=[r_S, r_bcp[4]], writes=[r_t1])
                        k.op("vector", lambda e, ys_=ys_, i=i: e.tensor_reduce(out=ys_[:, :, i:i + 1].rearrange("p h o -> p (h o)"), in_=t1[:].rearrange("p (h d) -> p h d", h=8), axis=AX.X, op=ALU.add),
                             reads=[r_t1], writes=[rys])
                    def ytr(e, ys_=ys_):
                        last = None
                        for h in range(8):
                            last = e.matmul(pmisc[0:64, h, :], lhsT=ys_[:, h, :], rhs=self.identf[:], start=True, stop=True)
                        return last
                    k.op("tensor", ytr, reads=[rys, self.r_ident], writes=[r_pmisc])
                    k.op("vector", lambda e: e.tensor_copy(out=yo[:].rearrange("p h a d -> p h (a d)"), in_=pmisc[0:64, :, :]), reads=[r_pmisc], writes=[r_yo])
                    k.op("gpsimd", lambda e: e.tensor_copy(out=yb2[:].rearrange("p (h d) -> p h d", h=8), in_=yo[:, :, 1, :]), reads=[r_yo], writes=[r_yb2])
                    k.op("tensor", lambda e: e.matmul(pj[0:64, :], lhsT=J64[:], rhs=yb2[:], start=True, stop=True), reads=[r_yb2, r_cst], writes=[r_pj])
                    k.op("scalar", lambda e: e.activation(out=yb3[:], in_=pj[0:64, :], func=AF.Copy), reads=[r_pj], writes=[r_yb3])
                    k.dma(yfb[0, f0:f0 + 64, :].rearrange("p (h d) -> p h d", h=8), yo[:, :, 0, :], reads=[r_yo], writes=[self.r_rwy_t[f0 // 128]])
                    k.dma(yfb[1, b0:b0 + 64, :], yb3[:], reads=[r_yb3], writes=[self.r_rwy_t[b0 // 128]])
            k.barrier()

    def stage_rwscan(self, l):
        k = self.k
        rwd, _ = self.dr["rwd%d" % l]
        r_rwd_t = self.r_rwd_t
        yfb, r_yfb = self.dram("rwy%d" % l, [2, T, 512], F32)
        self.r_rwy_t = [R() for _ in range(NT)]
        mkd, r_mkd = self.dr["rw_masks"]
        dmd, r_dmd = self.dr["rw_dm"]
        with ExitStack() as es:
            MK = k.sb("rc_MK", [128, 6, 128], F32, es)
            DM = k.sb("rc_DM", [128, 2, 1024], F32, es)
            ones = k.sb("rc_ones", [128, 64], F32, es)
            r_cst = R()
            k.dma(MK[:], mkd, reads=[r_mkd], writes=[r_cst])
            k.dma(DM[:], dmd, reads=[r_dmd], writes=[r_cst])
            k.op("vector", lambda e: e.memset(ones[:], 1.0), writes=[r_cst])
            MI, BO, MS, nMS, nMST, nMI = [MK[:, i, :] for i in range(6)]
            S = k.sb("rc_S", [128, 512], F32, es)
            r_S = R()
            k.op("vector", lambda e: e.memset(S[:], 0.0), writes=[r_S])
            X = [k.sb("rc_X%d" % i, [128, self.RWC], F32, es) for i in range(2)]
            r_X = [R() for _ in range(2)]
            fac = [k.sb("rc_fac%d" % i, [128, 512], F32, es) for i in range(4)]
            r_fac = [R() for _ in range(4)]
            tl = k.sb("rc_tl", [128, 2, 512], F32, es)
            r_tl = [R(), R()]
            pl = [k.sb("rc_pl%d" % i, [128, 512], F32, es) for i in range(6)]
            r_pl = [R() for _ in range(6)]
            bd = [k.sb("rc_bd%d" % i, [128, 8, 128], F32, es) for i in range(7)]
            r_bd = [R() for _ in range(7)]
            xT = [k.sb("rc_xT%d" % i, [128, 8, 128], F32, es) for i in range(4)]
            r_xT = [[R(), R()] for _ in range(4)]
            AT = k.sb("rc_AT", [128, 8, 128], F32, es)
            ApT = k.sb("rc_ApT", [128, 8, 128], F32, es)
            nBpT = k.sb("rc_nBpT", [128, 8, 128], F32, es)
            r_AT, r_ApT, r_nBpT = [R(), R()], [R(), R()], [R(), R()]
            NTb = [k.sb("rc_NT%d" % i, [128, 8, 128], F32, es) for i in range(2)]
            Mb = [k.sb("rc_M%d" % i, [128, 8, 128], F32, es) for i in range(2)]
            r_NT = [[R(), R()] for _ in range(2)]
            r_M = [[R(), R()] for _ in range(2)]
            G = k.sb("rc_G", [128, 512], F32, es)
            r_G = R()
            Pend = k.sb("rc_Pend", [128, 512], F32, es)
            r_Pend = R()
            Yo = [k.sb("rc_Yo%d" % i, [128, 512], F32, es) for i in range(2)]
            r_Yo = [R() for _ in range(2)]
            pb = [k.ps("rc_pb%d" % i, [128, 512], F32, es) for i in range(8)]
            r_pb = [PR() for _ in range(8)]
            ipb = [0]

            def bank():
                i = ipb[0] % 8
                ipb[0] += 1
                return pb[i], r_pb[i]
            eng2 = ("vector", "gpsimd")
            blk = 0
            nblk_dbg = self.dbg.get("RW_NBLK", 10 ** 9)
            for seg0, seglen in ((0, NCTX), (NCTX, L)):
                for j in range(seglen // 64):
                    if blk >= nblk_dbg:
                        break
                    X_, rX = X[blk % 2], r_X[blk % 2]
                    Yo_, rYo = Yo[blk % 2], r_Yo[blk % 2]
                    blk += 1
                    f0 = seg0 + 64 * j
                    b0 = seg0 + seglen - 64 * (j + 1)
                    k.dma(X_[0:64, :], rwd[0, f0:f0 + 64, :], reads=[r_rwd_t[f0 // 128]], writes=[rX])
                    k.dma(X_[64:128, :], rwd[1, b0:b0 + 64, :], reads=[r_rwd_t[b0 // 128]], writes=[rX])
                    kk_, lw_, b_, kd_, r_ = [X_[:, q * 512:(q + 1) * 512] for q in range(5)]

                    def Vh(h, X_=X_):
                        return X_[:, 2560 + h * 64:2560 + (h + 1) * 64]
                    pLi, rLi = bank()
                    pLt, rLt = bank()
                    k.op("tensor", lambda e: e.matmul(pLi[:], lhsT=MI, rhs=lw_, start=True, stop=True), reads=[rX, r_cst], writes=[rLi])
                    k.op("tensor", lambda e: e.matmul(pLt[:], lhsT=BO, rhs=lw_, start=True, stop=True), reads=[rX, r_cst], writes=[rLt])
                    k.op("scalar", lambda e: e.activation(out=fac[0][:], in_=pLi[:], func=AF.Exp), reads=[rLi], writes=[r_fac[0]])
                    k.op("scalar", lambda e: e.activation(out=fac[1][:], in_=pLi[:], func=AF.Exp, scale=-1.0), reads=[rLi], writes=[r_fac[1]])
                    k.op("vector", lambda e: e.tensor_tensor(out=tl[:, 0, :], in0=pLi[:], in1=lw_, op=ALU.subtract), reads=[rLi, rX], writes=[r_tl[0]])
                    k.op("scalar", lambda e: e.activation(out=fac[2][:], in_=tl[:, 0, :], func=AF.Exp), reads=[r_tl[0]], writes=[r_fac[2]])
                    k.op("vector", lambda e: e.tensor_copy(out=tl[:, 1, :], in_=pLt[:]), reads=[rLt], writes=[r_tl[1]])
                    k.op("vector", lambda e: e.tensor_tensor(out=tl[:, 1, :], in0=tl[:, 1, :], in1=pLi[:], op=ALU.subtract), reads=[rLi, r_tl[1]], writes=[r_tl[1]])
                    k.op("scalar", lambda e: e.activation(out=fac[3][:], in_=tl[:, 1, :], func=AF.Exp), reads=[r_tl[1]], writes=[r_fac[3]])
                    for i, (src, fi) in enumerate(((kk_, 2), (r_, 0), (kd_, 1), (b_, 1), (kd_, 3), (b_, 3))):
                        k.op("vector" if i < 4 else "gpsimd", lambda e, i=i, src=src, fi=fi: e.tensor_tensor(out=pl[i][:], in0=src, in1=fac[fi][:], op=ALU.mult),
                             reads=[rX, r_fac[fi]], writes=[r_pl[i]])
                    for i in range(7):
                        src = pl[i][:] if i < 6 else lw_
                        dm = DM[:, 1 if i == 5 else 0, :]
                        rs = [r_pl[i]] if i < 6 else [rX]
                        k.op("vector" if i < 4 else "gpsimd", lambda e, i=i, src=src, dm=dm: e.tensor_tensor(
                            out=bd[i][:].rearrange("p h (a d) -> p h a d", a=2), in0=src.rearrange("p (h d) -> p h d", h=8).unsqueeze(2).to_broadcast([128, 8, 2, 64]),
                            in1=dm.rearrange("p (h a d) -> p h a d", h=8, a=2), op=ALU.mult), reads=rs + [r_cst], writes=[r_bd[i]])
                    for qi in range(4):
                        for hf in range(2):
                            p_, rp_ = bank()

                            def tr(e, p_=p_, qi=qi, hf=hf):
                                last = None
                                for hh in range(4):
                                    last = e.transpose(out=p_[:, hh * 128:(hh + 1) * 128], in_=bd[qi][:, hf * 4 + hh, :], identity=self.identf[:])
                                return last
                            k.op("tensor", tr, reads=[r_bd[qi], self.r_ident], writes=[rp_])
                            if (qi + hf) % 2 == 0:
                                k.op("vector", lambda e, p_=p_, qi=qi, hf=hf: e.tensor_copy(out=xT[qi][:, hf * 4:(hf + 1) * 4, :].rearrange("p a b -> p (a b)"), in_=p_[:]),
                                     reads=[rp_], writes=[r_xT[qi][hf]])
                            else:
                                k.op("scalar", lambda e, p_=p_, qi=qi, hf=hf: e.activation(out=xT[qi][:, hf * 4:(hf + 1) * 4, :].rearrange("p a b -> p (a b)"), in_=p_[:], func=AF.Copy),
                                     reads=[rp_], writes=[r_xT[qi][hf]])
                    qT, rT, kT, bT = xT
                    specs = ((bT, 3, qT, 0, nMS, NTb[0], r_NT[0]), (qT, 0, bT, 3, nMST, Mb[0], r_M[0]), (kT, 2, qT, 0, MS, AT, r_AT),
                             (kT, 2, rT, 1, MI, ApT, r_ApT), (bT, 3, rT, 1, nMI, nBpT, r_nBpT))
                    ie = 0
                    for (lt, li, rt, ri, mk, dst, rdst) in specs:
                        for hf in range(2):
                            p_, rp_ = bank()

                            def mm(e, p_=p_, lt=lt, rt=rt, hf=hf):
                                last = None
                                for hh in range(4):
                                    h = hf * 4 + hh
                                    last = e.matmul(p_[:, hh * 128:(hh + 1) * 128], lhsT=lt[:, h, :], rhs=rt[:, h, :], start=True, stop=True)
                                return last
                            k.op("tensor", mm, reads=[r_xT[li][hf], r_xT[ri][hf]], writes=[rp_])
                            k.op("vector", lambda e, p_=p_, dst=dst, hf=hf, mk=mk: e.tensor_tensor(
                                out=dst[:, hf * 4:(hf + 1) * 4, :], in0=p_[:].rearrange("p (a b) -> p a b", a=4), in1=mk.unsqueeze(1).to_broadcast([128, 4, 128]), op=ALU.mult),
                                reads=[rp_, r_cst], writes=[rdst[hf]])
                            ie += 1
                    pG, rG = bank()

                    def mmG(e):
                        last = None
                        for h in range(8):
                            e.matmul(pG[:, h * 64:(h + 1) * 64], lhsT=qT[:, h, :], rhs=S[:, h * 64:(h + 1) * 64], start=True, stop=False)
                            last = e.matmul(pG[:, h * 64:(h + 1) * 64], lhsT=AT[:, h, :], rhs=Vh(h), start=False, stop=True)
                        return last
                    k.op("tensor", mmG, reads=r_xT[0] + r_AT + [r_S, rX], writes=[rG])
                    k.op("vector", lambda e: e.tensor_copy(out=G[:], in_=pG[:]), reads=[rG], writes=[r_G])
                    cur = 0
                    for lev in range(6):
                        NTc, Mc = NTb[cur], Mb[cur]
                        pU, rU = bank()

                        def mmU(e, pU=pU, NTc=NTc):
                            last = None
                            for h in range(8):
                                last = e.matmul(pU[:, h * 64:(h + 1) * 64], lhsT=NTc[:, h, :], rhs=G[:, h * 64:(h + 1) * 64], start=True, stop=True)
                            return last
                        k.op("tensor", mmU, reads=r_NT[cur] + [r_G], writes=[rU])
                        if lev < 5:
                            nxt = 1 - cur
                            for (lt, rt, dst, rdst, rl, rr_) in ((Mc, NTc, NTb[nxt], r_NT[nxt], r_M[cur], r_NT[cur]), (NTc, Mc, Mb[nxt], r_M[nxt], r_NT[cur], r_M[cur])):
                                for hf in range(2):
                                    p_, rp_ = bank()

                                    def mm2(e, p_=p_, lt=lt, rt=rt, hf=hf):
                                        last = None
                                        for hh in range(4):
                                            h = hf * 4 + hh
                                            last = e.matmul(p_[:, hh * 128:(hh + 1) * 128], lhsT=lt[:, h, :], rhs=rt[:, h, :], start=True, stop=True)
                                        return last
                                    k.op("tensor", mm2, reads=[rl[hf], rr_[hf]], writes=[rp_])
                                    k.op("scalar", lambda e, p_=p_, dst=dst, hf=hf: e.activation(out=dst[:, hf * 4:(hf + 1) * 4, :].rearrange("p a b -> p (a b)"), in_=p_[:], func=AF.Copy),
                                         reads=[rp_], writes=[rdst[hf]])
                            cur = nxt
                        k.op("vector", lambda e, pU=pU: e.tensor_tensor(out=G[:], in0=G[:], in1=pU[:], op=ALU.add), reads=[rU, r_G], writes=[r_G])
                    pY, rY = bank()

                    def mmY(e):
                        last = None
                        for h in range(8):
                            hs = slice(h * 64, (h + 1) * 64)
                            e.matmul(pY[:, hs], lhsT=rT[:, h, :], rhs=S[:, hs], start=True, stop=False)
                            e.matmul(pY[:, hs], lhsT=ApT[:, h, :], rhs=Vh(h), start=False, stop=False)
                            last = e.matmul(pY[:, hs], lhsT=nBpT[:, h, :], rhs=G[:, hs], start=False, stop=True)
                        return last
                    k.op("tensor", mmY, reads=r_xT[1] + r_ApT + r_nBpT + [r_S, rX, r_G], writes=[rY])
                    k.op("scalar", lambda e, Yo_=Yo_: e.activation(out=Yo_[:], in_=pY[:], func=AF.Copy), reads=[rY], writes=[rYo])
                    k.dma(yfb[0, f0:f0 + 64, :], Yo_[0:64, :], reads=[rYo], writes=[self.r_rwy_t[f0 // 128]])
                    k.dma(yfb[1, b0:b0 + 64, :], Yo_[64:128, :], reads=[rYo], writes=[self.r_rwy_t[b0 // 128]])
                    pL, rL = bank()

                    def mmL(e):
                        last = None
                        for h in range(8):
                            last = e.matmul(pL[:, h * 64:(h + 1) * 64], lhsT=bd[6][:, h, :], rhs=ones[:], start=True, stop=True)
                        return last
                    k.op("tensor", mmL, reads=[r_bd[6], r_cst], writes=[rL])
                    k.op("scalar", lambda e: e.activation(out=Pend[:], in_=pL[:], func=AF.Exp), reads=[rL], writes=[r_Pend])
                    pS, rS = bank()

                    def mmS(e):
                        last = None
                        for h in range(8):
                            hs = slice(h * 64, (h + 1) * 64)
                            e.matmul(pS[:, hs], lhsT=bd[4][:, h, :], rhs=Vh(h), start=True, stop=False)
                            last = e.matmul(pS[:, hs], lhsT=bd[5][:, h, :], rhs=G[:, hs], start=False, stop=True)
                        return last
                    k.op("tensor", mmS, reads=[r_bd[4], r_bd[5], rX, r_G], writes=[rS])
                    k.op("vector", lambda e: e.tensor_tensor(out=S[:], in0=S[:], in1=Pend[:], op=ALU.mult), reads=[r_S, r_Pend], writes=[r_S])
                    k.op("vector", lambda e: e.tensor_tensor(out=S[:], in0=S[:], in1=pS[:], op=ALU.add), reads=[r_S, rS], writes=[r_S])
            k.barrier()

    def stage_rwout(self, l):
        k = self.k
        rwd, _ = self.dr["rwd%d" % l]
        rwo, r_rwo = self.dr["rwo%d" % l]
        yfb, _ = self.dr["rwy%d" % l]
        yd, r_yd = self.dram("y_rw%d" % l, [T, 512], F32)
        with ExitStack() as es:
            r_c = R()

            def bc(nm, n):
                tl = k.sb("ro_" + nm, [128, n], F32, es)
                ap, rr = self.dr[nm]
                k.dma(tl[:], ap[l].partition_broadcast(128), reads=[rr], writes=[r_c])
                return tl
            lnw = bc("rw_lnw", 512)
            lnb = bc("rw_lnb", 512)
            rkg = bc("rw_rk", 512)
            g2 = k.sb("ro_g2", [96, 512], F32, es)
            ap, rr = self.dr["rw_g2"]
            k.dma(g2[:], ap[l], reads=[rr], writes=[r_c])
            gne = k.sb("ro_gne", [128, 1], F32, es)
            k.op("vector", lambda e: e.memset(gne[:], 64e-5), writes=[r_c])
            yf = [k.sb("ro_yf%d" % i, [128, 512], F32, es) for i in range(2)]
            yb = [k.sb("ro_yb%d" % i, [128, 512], F32, es) for i in range(2)]
            rv = [k.sb("ro_rv%d" % i, [128, 1024], F32, es) for i in range(2)]
            kg = [k.sb("ro_kg%d" % i, [128, 608], F32, es) for i in range(2)]
            r_in = [[R() for _ in range(4)] for _ in range(2)]
            y = k.sb("ro_y", [128, 512], F32, es)
            r_y = R()
            sq = k.sb("ro_sq", [128, 512], F32, es)
            r_sq = R()
            st = k.sb("ro_st", [128, NT, 3, 8], F32, es)
            sg = k.sb("ro_sg", [128, 128], F32, es)
            r_sg = R()
            gT = k.sb("ro_gT", [96, 128], F32, es)
            r_gT = R()
            out = [k.sb("ro_out%d" % i, [128, 512], F32, es) for i in range(2)]
            r_out = [R() for _ in range(2)]
            ptr = k.ps("ro_ptr", [128, 512], F32, es)
            r_ptr = PR()
            pg = k.ps("ro_pg", [128, 512], F32, es)
            r_pg = PR()
            k.op("vector", lambda e: e.memset(sg[:], 0.0), writes=[r_sg])
            for t in range(NT):
                rows = slice(t * 128, (t + 1) * 128)
                i2 = t % 2
                ri = r_in[i2]
                k.dma(yf[i2][:], yfb[0, rows, :], reads=[self.r_rwy_t[t]], writes=[ri[0]])
                k.dma(yb[i2][:], yfb[1, rows, :], reads=[self.r_rwy_t[t]], writes=[ri[1]])
                k.dma(rv[i2][:], rwd[0, rows, 2048:3072], reads=[self.r_rwd_t[t]], writes=[ri[2]])
                k.dma(kg[i2][:], rwo[rows, :], reads=[r_rwo], writes=[ri[3]])
                r_s = R()
                y3 = y[:].rearrange("p (h d) -> p h d", h=8)
                k.op("vector", lambda e, i2=i2: e.tensor_tensor(out=y[:], in0=yf[i2][:], in1=yb[i2][:], op=ALU.add), reads=[ri[0], ri[1]], writes=[r_y])
                k.op("vector", lambda e, t=t: e.tensor_reduce(out=st[:, t, 0, :], in_=y3, axis=AX.X, op=ALU.add), reads=[r_y], writes=[r_s])
                k.op("vector", lambda e, t=t: e.tensor_scalar(out=st[:, t, 0, :], in0=st[:, t, 0, :], scalar1=1.0 / 64, scalar2=None, op0=ALU.mult), reads=[r_s], writes=[r_s])
                k.op("vector", lambda e, t=t: e.tensor_tensor(out=y3, in0=y3, in1=st[:, t, 0, :].unsqueeze(2).to_broadcast([128, 8, 64]), op=ALU.subtract),
                     reads=[r_y, r_s], writes=[r_y])
                k.op("gpsimd", lambda e: e.tensor_tensor(out=sq[:], in0=y[:], in1=y[:], op=ALU.mult), reads=[r_y], writes=[r_sq])
                k.op("vector", lambda e, t=t: e.tensor_reduce(out=st[:, t, 1, :], in_=sq[:].rearrange("p (h d) -> p h d", h=8), axis=AX.X, op=ALU.add), reads=[r_sq], writes=[r_s])
                k.op("scalar", lambda e, t=t: e.activation(out=st[:, t, 1, :], in_=st[:, t, 1, :], func=AF.Sqrt, bias=gne[:, 0:1], scale=1.0 / 64), reads=[r_s, r_c], writes=[r_s])
                k.op("vector", lambda e, t=t: e.reciprocal(out=st[:, t, 1, :], in_=st[:, t, 1, :]), reads=[r_s], writes=[r_s])
                k.op("vector", lambda e, t=t: e.tensor_tensor(out=y3, in0=y3, in1=st[:, t, 1, :].unsqueeze(2).to_broadcast([128, 8, 64]), op=ALU.mult),
                     reads=[r_y, r_s], writes=[r_y])
                k.op("gpsimd", lambda e: e.tensor_tensor(out=y[:], in0=y[:], in1=lnw[:], op=ALU.mult), reads=[r_y, r_c], writes=[r_y])
                k.op("gpsimd", lambda e: e.tensor_tensor(out=y[:], in0=y[:], in1=lnb[:], op=ALU.add), reads=[r_y, r_c], writes=[r_y])
                k.op("vector", lambda e, i2=i2: e.tensor_tensor(out=sq[:], in0=rv[i2][:, 0:512], in1=kg[i2][:, 0:512], op=ALU.mult), reads=[ri[2], ri[3], r_sq], writes=[r_sq])
                k.op("vector", lambda e: e.tensor_tensor(out=sq[:], in0=sq[:], in1=rkg[:], op=ALU.mult), reads=[r_sq, r_c], writes=[r_sq])
                k.op("vector", lambda e, t=t: e.tensor_reduce(out=st[:, t, 2, :], in_=sq[:].rearrange("p (h d) -> p h d", h=8), axis=AX.X, op=ALU.add), reads=[r_sq], writes=[r_s])
                k.op("vector", lambda e, t=t, i2=i2: e.tensor_tensor(out=sq[:].rearrange("p (h d) -> p h d", h=8), in0=rv[i2][:, 512:1024].rearrange("p (h d) -> p h d", h=8),
                                                                    in1=st[:, t, 2, :].unsqueeze(2).to_broadcast([128, 8, 64]), op=ALU.mult),
                     reads=[ri[2], r_s, r_sq], writes=[r_sq])
                k.op("vector", lambda e: e.tensor_tensor(out=y[:], in0=y[:], in1=sq[:], op=ALU.add), reads=[r_y, r_sq], writes=[r_y])
                k.op("scalar", lambda e, i2=i2: e.activation(out=sg[:, 0:96], in_=kg[i2][:, 512:608], func=AF.Sigmoid), reads=[ri[3]], writes=[r_sg])
                k.op("tensor", lambda e: e.transpose(out=ptr[:, 0:128], in_=sg[:], identity=self.identf[:]), reads=[r_sg, self.r_ident], writes=[r_ptr])
                k.op("vector", lambda e: e.tensor_copy(out=gT[:], in_=ptr[0:96, 0:128]), reads=[r_ptr], writes=[r_gT])
                k.op("tensor", lambda e: e.matmul(pg[:], lhsT=gT[:], rhs=g2[:], start=True, stop=True), reads=[r_gT, r_c], writes=[r_pg])
                o_, ro = out[i2], r_out[i2]
                k.op("vector", lambda e, o_=o_: e.tensor_tensor(out=o_[:], in0=y[:], in1=pg[:], op=ALU.mult), reads=[r_y, r_pg], writes=[ro])
                k.dma(yd[rows, :], o_[:], reads=[ro], writes=[r_yd])
            k.barrier()

    def stage_merge(self, l, xname):
        k = self.k
        z, _ = self.dr["z%d" % l]
        r_zt = self.r_ztile
        xin, r_xin = self.dr[xname]
        modD, r_modD = self.dr["modD%d" % l]
        wbr, r_wbr = self.dr["w_br"]
        wout, r_wout = self.dr["w_out"]
        accD, r_accD = self.dram("accD%d" % l, [T, D], BF16)
        r_acc_t = [R() for _ in range(NT)]
        x1, r_x1 = self.dram("x1_%d" % l, [T, D], F32)
        self.r_x1_t = [R() for _ in range(NT)]
        ybr = [self.dr[n % l] for n in ("y_fn%d", "y_na%d", "y_mla%d", "y_rw%d")]
        with ExitStack() as es:
            W = k.sb("mg_W", [128, 16, 2048], BF16, es)
            r_W = [R() for _ in range(16)]
            wst = [k.sb("mg_wst%d" % i, [128, 2048], F32, es) for i in range(2)]
            r_wst = [R() for _ in range(2)]
            for i in range(16):
                b, kc = i // 4, i % 4
                k.dma(wst[i % 2][:], wbr[l, b, kc * 128:(kc + 1) * 128, :], reads=[r_wbr], writes=[r_wst[i % 2]])
                if i % 2 == 0:
                    k.op("vector", lambda e, i=i: e.tensor_copy(out=W[:, i, :], in_=wst[i % 2][:]), reads=[r_wst[i % 2]], writes=[r_W[i]])
                else:
                    k.op("scalar", lambda e, i=i: e.activation(out=W[:, i, :], in_=wst[i % 2][:], func=AF.Copy), reads=[r_wst[i % 2]], writes=[r_W[i]])
            with ExitStack() as es2:
                yt = k.sb("mg_yt", [128, 4, 512], F32, es2)
                r_yt = R()
                ytb = k.sb("mg_ytb", [128, 2048], BF16, es2)
                r_ytb = R()
                yT = k.sb("mg_yT", [128, 16, 128], BF16, es2)
                r_yT = R()
                gt = [k.sb("mg_gt%d" % i, [128, 2048], F32, es2) for i in range(2)]
                r_gt = [R() for _ in range(2)]
                acc = k.sb("mg_acc", [128, 2048], F32, es2)
                r_acc = R()
                accb = k.sb("mg_accb", [128, 2048], BF16, es2)
                r_accb = R()
                tmp = k.sb("mg_tmp", [128, 512], F32, es2)
                r_tmp = R()
                ptr = [k.ps("mg_ptr%d" % i, [128, 8, 128], BF16, es2) for i in range(2)]
                r_ptr = [PR() for _ in range(2)]
                pm = [k.ps("mg_pm%d" % i, [128, 512], F32, es2) for i in range(5)]
                r_pm = [PR() for _ in range(5)]
                ip = 0
                ig = 0
                for t in range(NT):
                    rows = slice(t * 128, (t + 1) * 128)
                    for b in range(4):
                        k.dma(yt[:, b, :], ybr[b][0][rows, :], reads=[ybr[b][1]], writes=[r_yt])
                    k.op("scalar", lambda e: e.activation(out=ytb[:], in_=yt[:].rearrange("p a b -> p (a b)"), func=AF.Copy), reads=[r_yt], writes=[r_ytb])
                    for hf in range(2):
                        def tr(e, hf=hf):
                            last = None
                            for j in range(8):
                                c = hf * 8 + j
                                last = e.transpose(out=ptr[hf][:, j, :], in_=ytb[:, c * 128:(c + 1) * 128], identity=self.ident[:])
                            return last
                        k.op("tensor", tr, reads=[r_ytb, self.r_ident], writes=[r_ptr[hf]])
                        if hf == 0:
                            k.op("vector", lambda e: e.tensor_copy(out=yT[:, 0:8, :], in_=ptr[0][:]), reads=[r_ptr[0]], writes=[r_yT])
                        else:
                            k.op("scalar", lambda e: e.activation(out=yT[:, 8:16, :], in_=ptr[1][:], func=AF.Copy), reads=[r_ptr[1], r_yT], writes=[r_yT])
                    for b in range(4):
                        g_, rg = gt[ig % 2], r_gt[ig % 2]
                        ig += 1
                        k.dma(g_[:], z[rows, C_GATE + b * 2048:C_GATE + (b + 1) * 2048], reads=[r_zt[t]], writes=[rg])
                        k.op("scalar", lambda e, g_=g_: e.activation(out=g_[:], in_=g_[:], func=AF.Sigmoid), reads=[rg], writes=[rg])
                        for cbk in range(4):
                            p, rp = pm[ip % 5], r_pm[ip % 5]
                            ip += 1
                            cs_ = slice(cbk * 512, (cbk + 1) * 512)

                            def mm(e, p=p, b=b, cs_=cs_):
                                last = None
                                for kc in range(4):
                                    last = e.matmul(p[:], lhsT=yT[:, b * 4 + kc, :], rhs=W[:, b * 4 + kc, cs_], start=(kc == 0), stop=(kc == 3))
                                return last
                            k.op("tensor", mm, reads=[r_yT] + r_W[b * 4:b * 4 + 4], writes=[rp])
                            if b == 0:
                                k.op("vector", lambda e, p=p, g_=g_, cs_=cs_: e.tensor_tensor(out=acc[:, cs_], in0=p[:], in1=g_[:, cs_], op=ALU.mult),
                                     reads=[rp, rg], writes=[r_acc])
                            else:
                                k.op("vector", lambda e, p=p, g_=g_, cs_=cs_: e.tensor_tensor(out=tmp[:], in0=p[:], in1=g_[:, cs_], op=ALU.mult),
                                     reads=[rp, rg], writes=[r_tmp])
                                k.op("vector", lambda e, cs_=cs_: e.tensor_tensor(out=acc[:, cs_], in0=acc[:, cs_], in1=tmp[:], op=ALU.add),
                                     reads=[r_tmp, r_acc], writes=[r_acc])
                    k.op("vector", lambda e: e.tensor_copy(out=accb[:], in_=acc[:]), reads=[r_acc], writes=[r_accb])
                    k.dma(accD[rows, :], accb[:], reads=[r_accb], writes=[r_acc_t[t]])
                k.barrier()
            for i in range(16):
                k.dma(wst[i % 2][:], wout[l, i * 128:(i + 1) * 128, :], reads=[r_wout], writes=[r_wst[i % 2]])
                if i % 2 == 0:
                    k.op("vector", lambda e, i=i: e.tensor_copy(out=W[:, i, :], in_=wst[i % 2][:]), reads=[r_wst[i % 2]], writes=[r_W[i]])
                else:
                    k.op("scalar", lambda e, i=i: e.activation(out=W[:, i, :], in_=wst[i % 2][:], func=AF.Copy), reads=[r_wst[i % 2]], writes=[r_W[i]])
            with ExitStack() as es3:
                g1 = k.sb("mg_g1", [128, 2, 2048], F32, es3)
                r_g1 = R()
                for r_ in range(2):
                    k.dma(g1[:, r_, :], modD[r_:r_ + 1, 2 * D:3 * D].partition_broadcast(128), reads=[r_modD], writes=[r_g1])
                ab = [k.sb("mg_ab%d" % i, [128, 2048], BF16, es3) for i in range(2)]
                r_ab = [R() for _ in range(2)]
                aT = k.sb("mg_aT", [128, 16, 128], BF16, es3)
                r_aT = R()
                xt = [k.sb("mg_xt%d" % i, [128, 2048], F32, es3) for i in range(2)]
                r_xt = [R() for _ in range(2)]
                xo = [k.sb("mg_xo%d" % i, [128, 2048], F32, es3) for i in range(2)]
                r_xo = [R() for _ in range(2)]
                tmp = k.sb("mg_tmp2", [128, 512], F32, es3)
                r_tmp = R()
                ptr = [k.ps("mg_ptrb%d" % i, [128, 8, 128], BF16, es3) for i in range(2)]
                r_ptr = [PR() for _ in range(2)]
                pm = [k.ps("mg_pmb%d" % i, [128, 512], F32, es3) for i in range(4)]
                r_pm = [PR() for _ in range(4)]
                ip = 0
                for t in range(NT):
                    rows = slice(t * 128, (t + 1) * 128)
                    row = 1 if t < 2 else 0
                    a_, ra = ab[t % 2], r_ab[t % 2]
                    x_, rx = xt[t % 2], r_xt[t % 2]
                    o_, ro = xo[t % 2], r_xo[t % 2]
                    k.dma(a_[:], accD[rows, :], reads=[r_acc_t[t]], writes=[ra])
                    k.dma(x_[:], xin[rows, :], reads=[r_xin], writes=[rx])
                    for hf in range(2):
                        def tr(e, hf=hf, a_=a_):
                            last = None
                            for j in range(8):
                                c = hf * 8 + j
                                last = e.transpose(out=ptr[hf][:, j, :], in_=a_[:, c * 128:(c + 1) * 128], identity=self.ident[:])
                            return last
                        k.op("tensor", tr, reads=[ra, self.r_ident], writes=[r_ptr[hf]])
                        if hf == 0:
                            k.op("vector", lambda e: e.tensor_copy(out=aT[:, 0:8, :], in_=ptr[0][:]), reads=[r_ptr[0]], writes=[r_aT])
                        else:
                            k.op("scalar", lambda e: e.activation(out=aT[:, 8:16, :], in_=ptr[1][:], func=AF.Copy), reads=[r_ptr[1], r_aT], writes=[r_aT])
                    for cbk in range(4):
                        p, rp = pm[ip % 4], r_pm[ip % 4]
                        ip += 1
                        cs_ = slice(cbk * 512, (cbk + 1) * 512)

                        def mm(e, p=p, cs_=cs_):
                            last = None
                            for kc in range(16):
                                last = e.matmul(p[:], lhsT=aT[:, kc, :], rhs=W[:, kc, cs_], start=(kc == 0), stop=(kc == 15))
                            return last
                        k.op("tensor", mm, reads=[r_aT] + r_W, writes=[rp])
                        k.op("vector", lambda e, p=p, cs_=cs_, row=row: e.tensor_tensor(out=tmp[:], in0=p[:], in1=g1[:, row, cs_], op=ALU.mult),
                             reads=[rp, r_g1], writes=[r_tmp])
                        k.op("vector", lambda e, o_=o_, x_=x_, cs_=cs_: e.tensor_tensor(out=o_[:, cs_], in0=tmp[:], in1=x_[:, cs_], op=ALU.add),
                             reads=[r_tmp, rx], writes=[ro])
                    k.dma(x1[rows, :], o_[:], reads=[ro], writes=[self.r_x1_t[t]])
                k.barrier()

    def stage_moe(self, l, final):
        k = self.k
        x1, _ = self.dr["x1_%d" % l]
        r_x1_t = self.r_x1_t
        modD, r_modD = self.dr["modD%d" % l]
        XE, r_XE = self.dram("moe_xe%d" % l, [16, 128, 16 * 288], BF16)
        WS, r_WS = self.dram("moe_ws%d" % l, [16, 128, 2 * 2048], BF16)
        WSC, r_WSC = self.dram("moe_wsc%d" % l, [16, 32, 256], BF16)
        YE, r_YE = self.dram("moe_ye%d" % l, [16, 288, 2048], BF16)
        r_XEe = [R() for _ in range(16)]
        r_WSe = [R() for _ in range(16)]
        r_YEe = [R() for _ in range(16)]
        if final:
            xo_d, r_xo_d = self.dram("out", [L, D], F32, "ExternalOutput")
        else:
            xo_d, r_xo_d = self.dram("x2_%d" % l, [T, D], F32)
        self.r_x2_t = [R() for _ in range(NT)]
        w1d, r_w1d = self.dr["moe_w1"]
        w3d, r_w3d = self.dr["moe_w3"]
        w2d, r_w2d = self.dr["moe_w2"]
        with ExitStack() as es:
            h2 = k.sb("mo_h2", [128, NT, 2048], BF16, es)
            r_h2 = [R() for _ in range(NT)]
            aff = k.sb("mo_aff", [128, NT, 32], F32, es)
            r_aff = [R() for _ in range(NT)]
            affT = k.sb("mo_affT", [32, T], F32, es)
            r_affT = R()
            k.op("gpsimd", lambda e: e.memset(aff[:], 0.0), writes=r_aff)
            r_c = R()
            with ExitStack() as es2:
                abc = k.sb("mo_abc", [128, 2, 2048], F32, es2)
                shbc = k.sb("mo_shbc", [128, 2, 2048], F32, es2)
                gbc = k.sb("mo_gbc", [128, 2048], F32, es2)
                ap, rr = self.dr["norm2_gb"]
                k.dma(gbc[:], ap[l].partition_broadcast(128), reads=[rr], writes=[r_c])
                for r_ in range(2):
                    k.dma(abc[:, r_, :], modD[r_:r_ + 1, 4 * D:5 * D].partition_broadcast(128), reads=[r_modD], writes=[r_c])
                    k.dma(shbc[:, r_, :], modD[r_:r_ + 1, 3 * D:4 * D].partition_broadcast(128), reads=[r_modD], writes=[r_c])
                for r_ in range(2):
                    k.op("vector", lambda e, r_=r_: e.scalar_tensor_tensor(out=abc[:, r_, :], in0=abc[:, r_, :], scalar=1.0, in1=gbc[:], op0=ALU.add, op1=ALU.mult),
                         reads=[r_c], writes=[r_c])
                rtr = k.sb("mo_rtr", [128, 16, 32], F32, es2)
                k.op("gpsimd", lambda e: e.memset(rtr[:], 0.0), writes=[r_c])
                ap, rr = self.dr["moe_router"]
                k.dma(rtr[:, :, 0:16], ap[l].rearrange("(kc p) n -> p kc n", p=128), reads=[rr], writes=[r_c])
                xt = [k.sb("mo_xt%d" % i, [128, 2048], F32, es2) for i in range(2)]
                r_xt = [R() for _ in range(2)]
                hf_ = k.sb("mo_hf", [128, 2048], F32, es2)
                r_hf = R()
                hT = k.sb("mo_hT", [128, 16, 128], F32, es2)
                r_hT = R()
                junk = k.sb("mo_junk", [128, 2048], BF16, es2)
                r_junk = R()
                st = k.sb("mo_st", [128, NT, 4], F32, es2)
                lg = k.sb("mo_lg", [128, 16], F32, es2)
                r_lg = R()
                ptr = [k.ps("mo_ptr%d" % i, [128, 4, 128], F32, es2) for i in range(4)]
                r_ptr = [PR() for _ in range(4)]
                pl = k.ps("mo_pl", [128, 512], F32, es2)
                r_pl = PR()
                pT = k.ps("mo_pT", [128, 512], F32, es2)
                r_pT = PR()
                for t in range(NT):
                    rows = slice(t * 128, (t + 1) * 128)
                    row = 1 if t < 2 else 0
                    x_, rx = xt[t % 2], r_xt[t % 2]
                    r_s = R()
                    k.dma(x_[:], x1[rows, :], reads=[r_x1_t[t]], writes=[rx])
                    k.op("scalar", lambda e, x_=x_, t=t: e.activation(out=junk[:], in_=x_[:], func=AF.Square, accum_out=st[:, t, 0:1]), reads=[rx], writes=[r_junk, r_s])
                    k.op("scalar", lambda e, t=t: e.activation(out=st[:, t, 1:2], in_=st[:, t, 0:1], func=AF.Sqrt, bias=self.eps_t[:, 0:1], scale=1.0 / D), reads=[r_s], writes=[r_s])
                    k.op("vector", lambda e, t=t: e.reciprocal(out=st[:, t, 1:2], in_=st[:, t, 1:2]), reads=[r_s], writes=[r_s])
                    k.op("vector", lambda e, x_=x_, t=t, row=row: e.scalar_tensor_tensor(out=hf_[:], in0=x_[:], scalar=st[:, t, 1:2], in1=abc[:, row, :], op0=ALU.mult, op1=ALU.mult),
                         reads=[rx, r_s, r_c], writes=[r_hf])
                    k.op("vector", lambda e, row=row: e.tensor_tensor(out=hf_[:], in0=hf_[:], in1=shbc[:, row, :], op=ALU.add), reads=[r_hf, r_c], writes=[r_hf])
                    k.op("scalar", lambda e, t=t: e.activation(out=h2[:, t, :], in_=hf_[:], func=AF.Copy), reads=[r_hf], writes=[r_h2[t]])
                    for g in range(4):
                        def tr(e, g=g):
                            last = None
                            for j in range(4):
                                c = g * 4 + j
                                last = e.transpose(out=ptr[g][:, j, :], in_=hf_[:, c * 128:(c + 1) * 128], identity=self.identf[:])
                            return last
                        k.op("tensor", tr, reads=[r_hf, self.r_ident], writes=[r_ptr[g]])
                        if g % 2 == 0:
                            k.op("vector", lambda e, g=g: e.tensor_copy(out=hT[:, g * 4:(g + 1) * 4, :], in_=ptr[g][:]), reads=[r_ptr[g]], writes=[r_hT])
                        else:
                            k.op("scalar", lambda e, g=g: e.activation(out=hT[:, g * 4:(g + 1) * 4, :], in_=ptr[g][:], func=AF.Copy), reads=[r_ptr[g], r_hT], writes=[r_hT])

                    def mmr(e):
                        last = None
                        for kc in range(16):
                            last = e.matmul(pl[:, 0:32], lhsT=hT[:, kc, :], rhs=rtr[:, kc, :], start=(kc == 0), stop=(kc == 15))
                        return last
                    k.op("tensor", mmr, reads=[r_hT, r_c], writes=[r_pl])
                    k.op("vector", lambda e, t=t: e.tensor_reduce(out=st[:, t, 2:3], in_=pl[:, 0:16], axis=AX.X, op=ALU.max), reads=[r_pl], writes=[r_s])
                    k.op("vector", lambda e, t=t: e.tensor_scalar(out=lg[:], in0=pl[:, 0:16], scalar1=st[:, t, 2:3], scalar2=None, op0=ALU.subtract), reads=[r_pl, r_s], writes=[r_lg])
                    k.op("scalar", lambda e, t=t: e.activation(out=lg[:], in_=lg[:], func=AF.Exp, accum_out=st[:, t, 3:4]), reads=[r_lg], writes=[r_lg, r_s])
                    k.op("vector", lambda e, t=t: e.reciprocal(out=st[:, t, 3:4], in_=st[:, t, 3:4]), reads=[r_s], writes=[r_s])
                    k.op("vector", lambda e, t=t: e.tensor_scalar(out=aff[:, t, 0:16], in0=lg[:], scalar1=st[:, t, 3:4], scalar2=None, op0=ALU.mult), reads=[r_lg, r_s], writes=[r_aff[t]])
                    k.op("tensor", lambda e, t=t: e.transpose(out=pT[0:32, 0:128], in_=aff[:, t, :], identity=self.identf[:]), reads=[r_aff[t], self.r_ident], writes=[r_pT])
                    k.op("vector", lambda e, rows=rows: e.tensor_copy(out=affT[:, rows], in_=pT[0:32, 0:128]), reads=[r_pT], writes=[r_affT])
                k.barrier()
            with ExitStack() as es3:
                sel = k.sb("mo_sel", [32, 16, 128], F32, es3)
                ap, rr = self.dr["moe_sel"]
                k.dma(sel[:], ap, reads=[rr], writes=[r_c])
                jidx = k.sb("mo_jidx", [128, 2], F32, es3)
                ap, rr = self.dr["moe_jidx"]
                k.dma(jidx[:], ap, reads=[rr], writes=[r_c])
                ones = k.sb("mo_ones", [128, 128], BF16, es3)
                k.op("vector", lambda e: e.memset(ones[:], 1.0), writes=[r_c])
                A = k.sb("mo_A", [128, T], F32, es3)
                r_A = R()
                cmp = [k.sb("mo_cmp%d" % i, [128, 2048], BF16, es3) for i in range(2)]
                r_cmp = [R() for _ in range(2)]
                Ws = k.sb("mo_Ws", [128, 2, 2048], BF16, es3)
                r_Ws = R()
                Ps = k.sb("mo_Ps", [128, 2, 2048], BF16, es3)
                r_Ps = R()
                Wc = k.sb("mo_Wc", [128, 256], BF16, es3)
                r_Wc = R()
                Pc = k.sb("mo_Pc", [128, 256], BF16, es3)
                r_Pc = R()
                PsT = k.sb("mo_PsT", [128, NT, 256], BF16, es3)
                r_PsT = R()
                xe = [k.sb("mo_xe%d" % i, [128, 16, 288], BF16, es3) for i in range(2)]
                r_xe = [R() for _ in range(2)]
                prk = k.ps("mo_prk", [128, 2048], F32, es3)
                r_prk = PR()
                pA = k.ps("mo_pA", [128, 512], F32, es3)
                r_pA = PR()
                ptb = k.ps("mo_ptb", [128, 8, 128], BF16, es3)
                r_ptb = PR()
                pg = [k.ps("mo_pg%d" % i, [128, 512], F32, es3) for i in range(2)]
                r_pg = [PR() for _ in range(2)]
                ic = 0
                ig = 0
                for e_ in range(16):
                    for cb in range(5):
                        c0 = cb * 512
                        cw = min(512, T - c0)
                        k.op("tensor", lambda e, c0=c0, cw=cw, e_=e_: e.matmul(pA[:, 0:cw], lhsT=sel[:, e_, :], rhs=affT[:, c0:c0 + cw], start=True, stop=True),
                             reads=[r_affT, r_c], writes=[r_pA])
                        k.op("scalar", lambda e, c0=c0, cw=cw: e.activation(out=A[:, c0:c0 + cw], in_=pA[:, 0:cw], func=AF.Copy), reads=[r_pA], writes=[r_A])
                    for mt in range(16):
                        c_, rc_ = cmp[ic % 2], r_cmp[ic % 2]
                        ic += 1
                        k.op("vector", lambda e, c_=c_, mt=mt, e_=e_: e.tensor_scalar(out=c_[:], in0=A[:, NCTX:T], scalar1=aff[:, 2 + mt, e_:e_ + 1], scalar2=None, op0=ALU.is_lt),
                             reads=[r_A, r_aff[2 + mt]], writes=[rc_])

                        def mmc(e, c_=c_, mt=mt):
                            last = None
                            for cb in range(4):
                                last = e.matmul(prk[:, cb * 512:(cb + 1) * 512], lhsT=ones[:], rhs=c_[:, cb * 512:(cb + 1) * 512], start=(mt == 0), stop=(mt == 15))
                            return last
                        k.op("tensor", mmc, reads=[rc_, r_c], writes=[r_prk])
                    for jt in range(2):
                        k.op("vector", lambda e, jt=jt: e.scalar_tensor_tensor(out=Ws[:, jt, :], in0=prk[:], scalar=jidx[:, jt:jt + 1], in1=A[:, NCTX:T], op0=ALU.is_equal, op1=ALU.mult),
                             reads=[r_prk, r_A, r_c], writes=[r_Ws])
                        k.op("vector", lambda e, jt=jt: e.tensor_scalar(out=Ps[:, jt, :], in0=prk[:], scalar1=jidx[:, jt:jt + 1], scalar2=None, op0=ALU.is_equal),
                             reads=[r_prk, r_c], writes=[r_Ps])
                    k.dma(WS[e_].rearrange("p (a b) -> p a b", a=2), Ws[:], reads=[r_Ws], writes=[r_WSe[e_]])
                    for mt in range(2):
                        c_, rc_ = cmp[ic % 2], r_cmp[ic % 2]
                        ic += 1
                        k.op("vector", lambda e, c_=c_, mt=mt, e_=e_: e.tensor_scalar(out=c_[:, 0:256], in0=A[:, 0:NCTX], scalar1=aff[:, mt, e_:e_ + 1], scalar2=None, op0=ALU.is_lt),
                             reads=[r_A, r_aff[mt]], writes=[rc_])
                        k.op("tensor", lambda e, c_=c_, mt=mt: e.matmul(prk[:, 0:256], lhsT=ones[:], rhs=c_[:, 0:256], start=(mt == 0), stop=(mt == 1)),
                             reads=[rc_, r_c], writes=[r_prk])
                    k.op("vector", lambda e: e.scalar_tensor_tensor(out=Wc[:], in0=prk[:, 0:256], scalar=jidx[:, 0:1], in1=A[:, 0:NCTX], op0=ALU.is_equal, op1=ALU.mult),
                         reads=[r_prk, r_A, r_c], writes=[r_Wc])
                    k.op("vector", lambda e: e.tensor_scalar(out=Pc[:], in0=prk[:, 0:256], scalar1=jidx[:, 0:1], scalar2=None, op0=ALU.is_equal),
                         reads=[r_prk, r_c], writes=[r_Pc])
                    k.dma(WSC[e_], Wc[0:32, :], reads=[r_Wc], writes=[r_WSe[e_]])
                    for nt in range(16):
                        if nt % 4 == 0:
                            def trp(e, nt=nt):
                                last = None
                                for q in range(4):
                                    for jt in range(2):
                                        last = e.transpose(out=ptb[:, q * 2 + jt, :], in_=Ps[:, jt, (nt + q) * 128:(nt + q + 1) * 128], identity=self.ident[:])
                                return last
                            k.op("tensor", trp, reads=[r_Ps, self.r_ident], writes=[r_ptb])
                            k.op("scalar", lambda e, nt=nt: e.activation(out=PsT[:, 2 + nt:2 + nt + 4, :], in_=ptb[:].rearrange("p (q j) n -> p q (j n)", q=4), func=AF.Copy),
                                 reads=[r_ptb], writes=[r_PsT])

                    def trc(e):
                        last = None
                        for nt in range(2):
                            last = e.transpose(out=ptb[:, nt, 0:32], in_=Pc[0:32, nt * 128:(nt + 1) * 128], identity=self.ident[0:32, 0:32])
                        return last
                    k.op("tensor", trc, reads=[r_Pc, self.r_ident], writes=[r_ptb])
                    k.op("scalar", lambda e: e.activation(out=PsT[:, 0:2, 0:32], in_=ptb[:, 0:2, 0:32], func=AF.Copy), reads=[r_ptb], writes=[r_PsT])
                    x_, rx = xe[e_ % 2], r_xe[e_ % 2]
                    for dc in range(16):
                        p, rp = pg[ig % 2], r_pg[ig % 2]
                        ig += 1

                        def mmg(e, p=p, dc=dc):
                            last = None
                            for nt in range(16):
                                last = e.matmul(p[:, 0:256], lhsT=h2[:, 2 + nt, dc * 128:(dc + 1) * 128], rhs=PsT[:, 2 + nt, :], start=(nt == 0), stop=(nt == 15))
                            for nt in range(2):
                                last = e.matmul(p[:, 256:288], lhsT=h2[:, nt, dc * 128:(dc + 1) * 128], rhs=PsT[:, nt, 0:32], start=(nt == 0), stop=(nt == 1))
                            return last
                        k.op("tensor", mmg, reads=[r_PsT] + r_h2, writes=[rp])
                        k.op("scalar", lambda e, p=p, x_=x_, dc=dc: e.activation(out=x_[:, dc, :], in_=p[:, 0:288], func=AF.Copy), reads=[rp, rx], writes=[rx])
                    k.dma(XE[e_], x_[:].rearrange("p a b -> p (a b)"), reads=[rx], writes=[r_XEe[e_]])
                k.barrier()
        if self.dbg.get("MOE_PH", 9) < 2:
            return
        with ExitStack() as es:
            w1b = k.sb("mo_w1b", [128, 16, 1024], BF16, es)
            w3b = k.sb("mo_w3b", [128, 16, 1024], BF16, es)
            w2b = k.sb("mo_w2b", [128, 8, 2048], BF16, es)
            r_w1 = [R() for _ in range(8)]
            r_w3 = [R() for _ in range(8)]
            r_w2 = [R() for _ in range(8)]
            wst = [k.sb("mo_wst%d" % i, [128, 4096], F32, es) for i in range(3)]
            r_wst = [R() for _ in range(3)]
            xe = [k.sb("mo_xeb%d" % i, [128, 16, 288], BF16, es) for i in range(2)]
            r_xe = [R() for _ in range(2)]
            gT = k.sb("mo_gT", [128, 8, 288], BF16, es)
            r_gT = [R() for _ in range(8)]
            sa8 = k.sb("mo_sa8", [128, 8, 288], F32, es)
            r_sa8 = [R() for _ in range(8)]
            yeb = [k.sb("mo_yeb%d" % i, [128, 3, 2048], BF16, es) for i in range(1)]
            r_yeb = [R() for _ in range(1)]
            pa = [k.ps("mo_pa%d" % i, [128, 512], F32, es) for i in range(2)]
            r_pa = [PR() for _ in range(2)]
            pu = [k.ps("mo_pu%d" % i, [128, 512], F32, es) for i in range(2)]
            r_pu = [PR() for _ in range(2)]
            py = [k.ps("mo_py%d" % i, [128, 512], F32, es) for i in range(3)]
            r_py = [PR() for _ in range(3)]
            iw = 0
            iy = 0
            engs = ("vector", "gpsimd")
            for e_ in range(16):
                x_, rx = xe[e_ % 2], r_xe[e_ % 2]
                k.dma(x_[:].rearrange("p a b -> p (a b)"), XE[e_], reads=[r_XEe[e_]], writes=[rx])
                for (wd, rwd_, wb_, rw_) in ((w1d, r_w1d, w1b, r_w1), (w3d, r_w3d, w3b, r_w3)):
                    for pr in range(4):
                        s_, rs = wst[iw % 3], r_wst[iw % 3]
                        k.dma(s_[:].rearrange("p (a n) -> p a n", a=4), wd[l, e_, pr * 512:(pr + 1) * 512, :].rearrange("(a p) n -> p a n", p=128), reads=[rwd_], writes=[rs])
                        for hh in range(2):
                            sv = s_[:, hh * 2048:(hh + 1) * 2048].rearrange("p (a n) -> p a n", a=2)
                            dv = wb_[:, 4 * pr + 2 * hh:4 * pr + 2 * hh + 2, :]
                            if hh == 0:
                                k.op("vector", lambda e, sv=sv, dv=dv: e.tensor_copy(out=dv, in_=sv), reads=[rs], writes=[rw_[2 * pr + hh]])
                            else:
                                k.op("scalar", lambda e, sv=sv, dv=dv: e.activation(out=dv, in_=sv, func=AF.Copy), reads=[rs], writes=[rw_[2 * pr + hh]])
                        iw += 1
                for fp in range(4):
                    s_, rs = wst[iw % 3], r_wst[iw % 3]
                    k.dma(s_[:].rearrange("p (a n) -> p a n", a=2), w2d[l, e_, fp * 256:(fp + 1) * 256, :].rearrange("(a p) n -> p a n", p=128), reads=[r_w2d], writes=[rs])
                    for hh in range(2):
                        fc = 2 * fp + hh
                        sv = s_[:, hh * 2048:(hh + 1) * 2048]
                        if hh == 0:
                            k.op("vector", lambda e, sv=sv, fc=fc: e.tensor_copy(out=w2b[:, fc, :], in_=sv), reads=[rs], writes=[r_w2[fc]])
                        else:
                            k.op("scalar", lambda e, sv=sv, fc=fc: e.activation(out=w2b[:, fc, :], in_=sv, func=AF.Copy), reads=[rs], writes=[r_w2[fc]])
                    iw += 1
                for fc in range(8):
                    pa_, rpa = pa[fc % 2], r_pa[fc % 2]

                    def mma(e, p_=pa_, fc=fc, x_=x_):
                        last = None
                        for kc in range(16):
                            last = e.matmul(p_[:, 0:288], lhsT=w1b[:, kc, fc * 128:(fc + 1) * 128], rhs=x_[:, kc, :], start=(kc == 0), stop=(kc == 15))
                        return last
                    k.op("tensor", mma, reads=r_w1 + [rx], writes=[rpa])
                    k.op("scalar", lambda e, pa_=pa_, fc=fc: e.activation(out=sa8[:, fc, :], in_=pa_[:, 0:288], func=AF.Silu), reads=[rpa], writes=[r_sa8[fc]])
                for fc in range(8):
                    pu_, rpu = pu[fc % 2], r_pu[fc % 2]

                    def mmu(e, p_=pu_, fc=fc, x_=x_):
                        last = None
                        for kc in range(16):
                            last = e.matmul(p_[:, 0:288], lhsT=w3b[:, kc, fc * 128:(fc + 1) * 128], rhs=x_[:, kc, :], start=(kc == 0), stop=(kc == 15))
                        return last
                    k.op("tensor", mmu, reads=r_w3 + [rx], writes=[rpu])
                    k.op("vector", lambda e, pu_=pu_, fc=fc: e.tensor_tensor(out=gT[:, fc, :], in0=sa8[:, fc, :], in1=pu_[:, 0:288], op=ALU.mult), reads=[r_sa8[fc], rpu], writes=[r_gT[fc]])
                y_, ry = yeb[0], r_yeb[0]
                for jt, (j0, M) in enumerate(((0, 128), (128, 128), (256, 32))):
                    for cbk in range(4):
                        p_, rp_ = py[iy % 3], r_py[iy % 3]
                        iy += 1
                        cs_ = slice(cbk * 512, (cbk + 1) * 512)

                        def mmy(e, p_=p_, j0=j0, M=M, cs_=cs_):
                            last = None
                            for fc in range(8):
                                last = e.matmul(p_[0:M, :], lhsT=gT[:, fc, j0:j0 + M], rhs=w2b[:, fc, cs_], start=(fc == 0), stop=(fc == 7))
                            return last
                        k.op("tensor", mmy, reads=r_gT + r_w2, writes=[rp_])
                        if iy % 2 == 0:
                            k.op("vector", lambda e, p_=p_, y_=y_, jt=jt, M=M, cs_=cs_: e.tensor_copy(out=y_[0:M, jt, cs_], in_=p_[0:M, :]), reads=[rp_], writes=[ry])
                        else:
                            k.op("scalar", lambda e, p_=p_, y_=y_, jt=jt, M=M, cs_=cs_: e.activation(out=y_[0:M, jt, cs_], in_=p_[0:M, :], func=AF.Copy), reads=[rp_], writes=[ry])
                k.dma(YE[e_, 0:256, :].rearrange("(a p) n -> p a n", p=128), y_[:, 0:2, :], reads=[ry], writes=[r_YEe[e_]])
                k.dma(YE[e_, 256:288, :], y_[0:32, 2, :], reads=[ry], writes=[r_YEe[e_]])
            k.barrier()
        if self.dbg.get("MOE_PH", 9) < 3:
            return
        with ExitStack() as es:
            macc = k.sb("mo_macc", [128, 8, 2048], F32, es)
            r_macc = [R() for _ in range(8)]
            g2 = k.sb("mo_g2", [128, 2, 2048], F32, es)
            r_g2 = R()
            for r_ in range(2):
                k.dma(g2[:, r_, :], modD[r_:r_ + 1, 5 * D:6 * D].partition_broadcast(128), reads=[r_modD], writes=[r_g2])
            ye = [k.sb("mo_ye%d" % i, [128, 2, 2048], BF16, es) for i in range(2)]
            r_ye = [R() for _ in range(2)]
            ws = [k.sb("mo_ws%d" % i, [128, 2, 1024], BF16, es) for i in range(2)]
            r_ws = [R() for _ in range(2)]
            xt = [k.sb("mo_x2t%d" % i, [128, 2048], F32, es) for i in range(2)]
            r_xt = [R() for _ in range(2)]
            ps = [k.ps("mo_ps%d" % i, [128, 512], F32, es) for i in range(4)]
            r_ps = [PR() for _ in range(4)]
            ip = 0
            il = 0
            for part in range(3):
                ntl = 2 if part == 0 else 8
                t0 = 0 if part == 0 else 2 + (part - 1) * 8
                for e_ in range(16):
                    y_, ry = ye[il % 2], r_ye[il % 2]
                    w_, rw = ws[il % 2], r_ws[il % 2]
                    il += 1
                    if part == 0:
                        k.dma(y_[0:32, 0, :], YE[e_, 256:288, :], reads=[r_YEe[e_]], writes=[ry])
                        k.dma(w_[0:32, 0, 0:256], WSC[e_], reads=[r_WSe[e_]], writes=[rw])
                    else:
                        k.dma(y_[:], YE[e_, 0:256, :].rearrange("(a p) n -> p a n", p=128), reads=[r_YEe[e_]], writes=[ry])
                        c0 = (part - 1) * 1024
                        k.dma(w_[:], WS[e_].rearrange("p (a b) -> p a b", a=2)[:, :, c0:c0 + 1024], reads=[r_WSe[e_]], writes=[rw])
                    for tl in range(ntl):
                        for cbk in range(4):
                            p_, rp_ = ps[ip % 4], r_ps[ip % 4]
                            ip += 1
                            cs_ = slice(cbk * 512, (cbk + 1) * 512)
                            if part == 0:
                                k.op("tensor", lambda e, p_=p_, w_=w_, y_=y_, tl=tl, cs_=cs_: e.matmul(p_[:], lhsT=w_[0:32, 0, tl * 128:(tl + 1) * 128], rhs=y_[0:32, 0, cs_], start=True, stop=True),
                                     reads=[rw, ry], writes=[rp_])
                            else:
                                def mms(e, p_=p_, w_=w_, y_=y_, tl=tl, cs_=cs_):
                                    e.matmul(p_[:], lhsT=w_[:, 0, tl * 128:(tl + 1) * 128], rhs=y_[:, 0, cs_], start=True, stop=False)
                                    return e.matmul(p_[:], lhsT=w_[:, 1, tl * 128:(tl + 1) * 128], rhs=y_[:, 1, cs_], start=False, stop=True)
                                k.op("tensor", mms, reads=[rw, ry], writes=[rp_])
                            if e_ == 0:
                                k.op("vector", lambda e, p_=p_, tl=tl, cs_=cs_: e.tensor_copy(out=macc[:, tl, cs_], in_=p_[:]), reads=[rp_], writes=[r_macc[tl]])
                            else:
                                k.op("vector", lambda e, p_=p_, tl=tl, cs_=cs_: e.tensor_tensor(out=macc[:, tl, cs_], in0=macc[:, tl, cs_], in1=p_[:], op=ALU.add),
                                     reads=[rp_, r_macc[tl]], writes=[r_macc[tl]])
                for tl in range(ntl):
                    t = t0 + tl
                    rows = slice(t * 128, (t + 1) * 128)
                    row = 1 if t < 2 else 0
                    x_, rx = xt[t % 2], r_xt[t % 2]
                    k.dma(x_[:], x1[rows, :], reads=[r_x1_t[t]], writes=[rx])
                    k.op("vector", lambda e, tl=tl, row=row: e.tensor_tensor(out=macc[:, tl, :], in0=macc[:, tl, :], in1=g2[:, row, :], op=ALU.mult),
                         reads=[r_macc[tl], r_g2], writes=[r_macc[tl]])
                    k.op("vector", lambda e, tl=tl, x_=x_: e.tensor_tensor(out=x_[:], in0=x_[:], in1=macc[:, tl, :], op=ALU.add), reads=[rx, r_macc[tl]], writes=[rx])
                    if final:
                        if t >= 2:
                            k.dma(xo_d[(t - 2) * 128:(t - 1) * 128, :], x_[:], reads=[rx], writes=[r_xo_d])
                    else:
                        k.dma(xo_d[rows, :], x_[:], reads=[rx], writes=[self.r_x2_t[t]])
            k.barrier()

    def declare_inputs(self):
        dp = self.depth
        self.dram("xcat", [T, D], F32, "ExternalInput")
        self.dram("ada_w", [dp, D, 6 * D], F32, "ExternalInput")
        self.dram("ada_b2", [dp, 2, 6 * D], F32, "ExternalInput")
        self.dram("norm1_gT", [dp, 128, 16], F32, "ExternalInput")
        self.dram("w_in", [dp, D, ZC], F32, "ExternalInput")
        self.dram("na_qg", [dp, 1, 512], F32, "ExternalInput")
        self.dram("na_kg", [dp, 1, 512], F32, "ExternalInput")
        self.dram("na_bias", [dp, 8, 5, 5, 128, 128], F32, "ExternalInput")
        self.dram("mla_cqg", [dp, 1, 512], F32, "ExternalInput")
        self.dram("mla_ckvg", [dp, 1, 256], F32, "ExternalInput")
        self.dram("mla_qg", [dp, 1, 768], F32, "ExternalInput")
        self.dram("mla_kng", [dp, 1, 512], F32, "ExternalInput")
        self.dram("mla_kpg", [dp, 1, 64], F32, "ExternalInput")
        self.dram("mla_w_uq", [dp, 512, 768], F32, "ExternalInput")
        self.dram("mla_w_ukv", [dp, 256, 1024], F32, "ExternalInput")
        self.dram("rope_cos", [T, 64], F32, "ExternalInput")
        self.dram("dft_c", [128, 256], BF16, "ExternalInput")
        self.dram("norm2_gb", [dp, 1, D], F32, "ExternalInput")
        self.dram("moe_router", [dp, D, 16], F32, "ExternalInput")
        self.dram("moe_w1", [dp, 16, D, 1024], F32, "ExternalInput")
        self.dram("moe_w3", [dp, 16, D, 1024], F32, "ExternalInput")
        self.dram("moe_w2", [dp, 16, 1024, D], F32, "ExternalInput")
        self.dram("moe_sel", [32, 16, 128], F32, "ExternalInput")
        self.dram("moe_jidx", [128, 2], F32, "ExternalInput")
        self.dram("w_br", [dp, 4, 512, D], F32, "ExternalInput")
        self.dram("w_out", [dp, D, D], F32, "ExternalInput")
        self.dram("rw_mu", [dp, 1, 1760], F32, "ExternalInput")
        self.dram("rw_kk", [dp, 1, 512], F32, "ExternalInput")
        self.dram("rw_ka", [dp, 1, 512], F32, "ExternalInput")
        self.dram("rw_w0f", [dp, 1, 1024], F32, "ExternalInput")
        self.dram("rw_a0f", [dp, 1, 1024], F32, "ExternalInput")
        self.dram("rw_wa2", [dp, 32, 2, 2, 512], F32, "ExternalInput")
        self.dram("rw_lnw", [dp, 1, 512], F32, "ExternalInput")
        self.dram("rw_lnb", [dp, 1, 512], F32, "ExternalInput")
        self.dram("rw_rk", [dp, 1, 512], F32, "ExternalInput")
        self.dram("rw_g2", [dp, 96, 512], F32, "ExternalInput")
        self.dram("rw_e2", [128, 64, 128], F32, "ExternalInput")
        self.dram("rw_pm", [128, 128], F32, "ExternalInput")
        self.dram("rw_j64", [64, 64], F32, "ExternalInput")
        self.dram("rw_masks", [128, 6, 128], F32, "ExternalInput")
        self.dram("rw_dm", [128, 2, 1024], F32, "ExternalInput")
        self.dram("dft_L", [2, L, L], BF16, "ExternalInput")
        self.dram("dft_C", [2, NCTX, NCTX], BF16, "ExternalInput")
        self.dram("rope_sin", [T, 64], F32, "ExternalInput")

    def build(self):
        k = self.k
        self.declare_inputs()
        self.eps_t = k.sb("eps_t", [128, 1], F32)
        r = R()
        k.op("vector", lambda e: e.memset(self.eps_t[:], EPS), writes=[r])
        k.barrier()
        self.consts()
        xname = "xcat"
        for l in range(self.depth):
            if self.want("mod"):
                self.stage_mod(l)
            else:
                self.dram("modD%d" % l, [2, 6 * D], F32)
            if self.want("normz"):
                self.stage_normz(l, xname)
            else:
                self.dram("z%d" % l, [T, ZC], F32)
                self.r_ztile = [self.dr["z%d" % l][1]] * NT
            if self.want("na"):
                self.stage_na(l)
            if self.want("mla"):
                self.stage_mla(l)
            if self.want("fourier"):
                self.stage_fourier(l)
            if self.want("rwprep"):
                self.stage_rwprep(l)
            if self.want("rwscan"):
                self.stage_rwscan(l)
            if self.want("rwout"):
                self.stage_rwout(l)
            else:
                for n in ("y_fn%d", "y_na%d", "y_mla%d", "y_rw%d"):
                    if (n % l) not in self.dr:
                        self.dram(n % l, [T, 512], F32)
            if self.want("merge"):
                self.stage_merge(l, xname)
            else:
                self.dram("x1_%d" % l, [T, D], F32)
                self.r_x1_t = [self.dr["x1_%d" % l][1]] * NT
            if self.want("moe"):
                self.stage_moe(l, final=(l == self.depth - 1) and self.final_out)
                xname = "x2_%d" % l
        k.finish([])
        return self.nc


def _na_bias_index():
    rows, kh, W, KW = 32, 8, 64, 16
    dr = np.zeros((5, 5, 128, 128), np.int64)
    dc = np.zeros((5, 5, 128, 128), np.int64)
    va = np.zeros((5, 5, 128, 128), bool)
    kk = np.arange(128)
    qq = np.arange(128)
    for cls, j in enumerate((0, 1, 5, 14, 15)):
        lo = min(max(j - 2, 0), 11)
        for i in range(5):
            kt = lo + i
            krow = (2 * kt + kk // 64)[:, None]
            kc = (kk % 64)[:, None]
            r = (2 * j + qq // 64)[None, :]
            qc = (qq % 64)[None, :]
            rs = np.clip(r - kh // 2, 0, rows - kh)
            cs = np.clip(qc - KW // 2, 0, W - KW)
            v = (krow >= rs) & (krow < rs + kh) & (kc >= cs) & (kc < cs + KW)
            dr[cls, i] = np.clip(krow - r + 8 - 1, 0, 14)
            dc[cls, i] = np.clip(kc - qc + KW - 1, 0, 2 * KW - 2)
            va[cls, i] = v
    return dr, dc, va


def _rope_tables():
    pos = np.arange(L)
    prow, pcol = pos // 64, pos % 64
    freqs = 10000.0 ** (-np.arange(16, dtype=np.float32) / 16)
    ar = prow[:, None].astype(np.float32) * freqs[None, :]
    ac = pcol[:, None].astype(np.float32) * freqs[None, :]
    cos = np.concatenate([np.cos(ar), np.cos(ar), np.cos(ac), np.cos(ac)], 1)
    sin = np.concatenate([-np.sin(ar), np.sin(ar), -np.sin(ac), np.sin(ac)], 1)
    cosT = np.concatenate([np.ones((NCTX, 64), np.float32), cos.astype(np.float32)], 0)
    sinT = np.concatenate([np.zeros((NCTX, 64), np.float32), sin.astype(np.float32)], 0)
    return np.ascontiguousarray(cosT), np.ascontiguousarray(sinT)


_DFT = None


def _dft_consts():
    global _DFT
    if _DFT is None:
        c = np.arange(128)
        ang = 2 * np.pi * ((c[:, None] * c[None, :]) % 128) / 128.0
        dft_c = bf16_np(np.concatenate([np.cos(ang), np.sin(ang)], 1))
        out = []
        for n in (L, NCTX):
            i = np.arange(n, dtype=np.int64)
            ang = 2 * np.pi * ((i[:, None] * i[None, :]) % n) / float(n)
            sc = 1.0 / np.sqrt(n * 128.0)
            out.append(bf16_np(np.stack([np.cos(ang) * sc, -np.sin(ang) * sc], 0)))
        _DFT = (dft_c, out[0], out[1])
    return _DFT


def _rw_chunk_consts():
    MS = np.zeros((128, 128), np.float32)
    MI = np.zeros((128, 128), np.float32)
    BO = np.zeros((128, 128), np.float32)
    for d in range(2):
        for a in range(64):
            for c in range(64):
                before = (a < c) if d == 0 else (a > c)
                MS[d * 64 + a, d * 64 + c] = 1.0 if before else 0.0
                MI[d * 64 + a, d * 64 + c] = 1.0 if (before or a == c) else 0.0
                BO[d * 64 + a, d * 64 + c] = 1.0
    masks = np.stack([MI, BO, MS, -MS, -MS.T, -MI], 1).astype(np.float32)
    dm = np.zeros((128, 8, 2, 64), np.float32)
    dm[0:64, :, 0, :] = 1.0
    dm[64:128, :, 1, :] = 1.0
    dm = dm.reshape(128, 1024)
    return np.ascontiguousarray(masks), np.ascontiguousarray(np.stack([dm, -dm], 1))


def _rw_consts():
    e2 = np.zeros((128, 64, 128), np.float32)
    for i in range(64):
        e2[i, i, 0:64] = 1.0
        e2[64 + 63 - i, i, 64:128] = 1.0
    pm = np.zeros((128, 128), np.float32)
    for i in range(64):
        pm[i, i] = 1.0
        pm[64 + i, 64 + 63 - i] = 1.0
    j64 = np.ascontiguousarray(np.eye(64, dtype=np.float32)[::-1])
    return e2, pm, j64


def host_core(inp, b):
    f32 = np.float32
    m = {}
    m["xcat"] = np.concatenate([inp["ctx"][b], inp["x"][b]], 0).astype(f32)
    cv = np.stack([inp["c"][b], inp["c_ctx"]], 0)
    m["cvT"] = np.ascontiguousarray(cv.reshape(2, 16, 128).transpose(2, 1, 0))
    return m


def host_inputs(inp, b, depth=DEPTH):
    m = host_shared(inp, depth)
    m.update(host_core(inp, b))
    return m


def host_shared(inp, depth=DEPTH):
    f32 = np.float32
    m = {}
    m["ada_w"] = inp["ada_w"][:depth]
    m["ada_b2"] = np.ascontiguousarray(np.broadcast_to(inp["ada_b"][:depth, None, :], (depth, 2, 6 * D)))
    m["norm1_gT"] = np.ascontiguousarray(inp["norm1_g"][:depth].reshape(depth, 16, 128).transpose(0, 2, 1))
    m["w_in"] = inp["w_in"][:depth]
    m["na_qg"] = np.ascontiguousarray(np.tile(inp["na_q_norm"][:depth, None, :], (1, 1, 8)))
    m["na_kg"] = np.ascontiguousarray(np.tile(inp["na_k_norm"][:depth, None, :], (1, 1, 8)))
    dr, dc, va = _na_bias_index()
    rpb = inp["na_rpb"][:depth]
    gathered = rpb[:, :, dr, dc]
    m["na_bias"] = np.where(va[None, None], gathered, f32(-1e30)).astype(f32)
    m["mla_cqg"] = np.ascontiguousarray(inp["mla_cq_norm"][:depth, None, :])
    m["mla_ckvg"] = np.ascontiguousarray(inp["mla_ckv_norm"][:depth, None, :])
    m["mla_qg"] = np.ascontiguousarray(np.tile(inp["mla_q_norm"][:depth, None, :], (1, 1, 4)))
    m["mla_kng"] = np.ascontiguousarray(np.tile(inp["mla_k_norm"][:depth, None, :128], (1, 1, 4)))
    m["mla_kpg"] = np.ascontiguousarray(inp["mla_k_norm"][:depth, None, 128:192])
    m["mla_w_uq"] = inp["mla_w_uq"][:depth]
    m["mla_w_ukv"] = inp["mla_w_ukv"][:depth]
    m["dft_c"], m["dft_L"], m["dft_C"] = _dft_consts()
    m["rw_mu"] = np.ascontiguousarray(np.concatenate([inp["rw_mu_ks"][:depth], inp["rw_mu_qs"][:depth]], 1)[:, None, :])
    m["rw_kk"] = np.ascontiguousarray(inp["rw_k_k"][:depth, None, :])
    m["rw_ka"] = np.ascontiguousarray(inp["rw_k_a"][:depth, None, :])
    m["rw_w0f"] = np.ascontiguousarray(inp["rw_w0"][:depth].reshape(depth, 1, 1024))
    m["rw_a0f"] = np.ascontiguousarray(inp["rw_a0"][:depth].reshape(depth, 1, 1024))
    wa2 = np.stack([inp["rw_w2"][:depth], inp["rw_a2"][:depth]], 1)
    m["rw_wa2"] = np.ascontiguousarray(wa2.transpose(0, 3, 1, 2, 4))
    m["rw_lnw"] = np.ascontiguousarray(inp["rw_ln_w"][:depth, None, :])
    m["rw_lnb"] = np.ascontiguousarray(inp["rw_ln_b"][:depth, None, :])
    m["rw_rk"] = np.ascontiguousarray(inp["rw_r_k"][:depth].reshape(depth, 1, 512))
    m["rw_g2"] = inp["rw_g2"][:depth]
    m["rw_e2"], m["rw_pm"], m["rw_j64"] = _rw_consts()
    m["rw_masks"], m["rw_dm"] = _rw_chunk_consts()
    m["w_br"] = inp["w_br"][:depth]
    m["norm2_gb"] = np.ascontiguousarray(inp["norm2_g"][:depth, None, :])
    m["moe_router"] = inp["moe_router"][:depth]
    m["moe_w1"] = inp["moe_w1"][:depth]
    m["moe_w3"] = inp["moe_w3"][:depth]
    m["moe_w2"] = inp["moe_w2"][:depth]
    sel = np.zeros((32, 16, 128), np.float32)
    for e in range(16):
        sel[e, e, :] = 1.0
    m["moe_sel"] = sel
    m["moe_jidx"] = np.ascontiguousarray(np.stack([np.arange(128), np.arange(128) + 128], 1).astype(np.float32))
    m["w_out"] = inp["w_out"][:depth]
    cosT, sinT = _rope_tables()
    m["rope_cos"], m["rope_sin"] = cosT, sinT
    return m


_PROG = None


def kernel(**inputs):
    global _PROG
    inp = {k_: np.asarray(v) for k_, v in inputs.items()}
    if _PROG is None:
        P = Prog(depth=DEPTH)
        P.build()
        _PROG = P
    P = _PROG
    shared = host_shared(inp, DEPTH)
    in_maps = []
    for b in range(8):
        m = dict(shared)
        m.update(host_core(inp, b))
        in_maps.append({n: m[n] for n, kd in P.kinds.items() if kd == "ExternalInput"})
    res = run_bass_kernel_spmd(P.nc, in_maps, core_ids=list(range(8)))
    return np.stack([np.asarray(r["out"], dtype=np.float32) for r in res.results], 0)
```

```python
import numpy as np
import ml_dtypes
from contextlib import ExitStack
import concourse.bass as bass
import concourse.mybir as mybir
from concourse.bass_utils import run_bass_kernel_spmd

F32 = mybir.dt.float32
BF16 = mybir.dt.bfloat16
I32 = mybir.dt.int32
F32R = mybir.dt.float32r


def fr(ap):
    return ap.bitcast(F32R)
AF = mybir.ActivationFunctionType
ALU = mybir.AluOpType
AX = mybir.AxisListType


class R:
    __slots__ = ("w", "r", "name", "excl")

    def __init__(self, name="", excl=False):
        self.w = []
        self.r = []
        self.name = name
        self.excl = excl


def PR():
    return R("psum", True)


class K:
    ENG = ("tensor", "vector", "scalar", "gpsimd", "sync")
    NDMA = 24
    MAXSEM = 30000

    def __init__(self, nc, es):
        self.nc = nc
        self.es = es
        self.prog = {e: [] for e in self.ENG}
        self.sem = {}
        self.epoch = {e: 0 for e in self.ENG}
        for e in self.ENG:
            self.sem[(e, 0)] = es.enter_context(nc.semaphore("s_%s_0" % e))
        self.cnt = {e: 0 for e in self.ENG}
        self.waited = {e: {} for e in self.ENG}
        self.dsem = [es.enter_context(nc.semaphore("d%d" % i)) for i in range(self.NDMA)]
        self.dval = [0] * self.NDMA
        self.dnext = 0
        self.ninstr = 0

    def sb(self, name, shape, dt=F32, es=None):
        self.uid = getattr(self, "uid", 0) + 1
        name = "%s_u%d" % (name, self.uid)
        return (es or self.es).enter_context(self.nc.sbuf_tensor(name, list(shape), dt))

    def ps(self, name, shape, dt=F32, es=None):
        n = int(np.prod(shape[1:])) * (2 if dt == BF16 else 4)
        assert n % 2048 == 0, (name, shape, n)
        self.uid = getattr(self, "uid", 0) + 1
        name = "%s_u%d" % (name, self.uid)
        return (es or self.es).enter_context(self.nc.psum_tensor(name, list(shape), dt))

    def _key(self, dep):
        return dep[0] if dep[0] in self.ENG else ("d", dep[0])

    def _collect(self, reads, writes):
        deps = {}
        for r in reads:
            for d in r.w:
                k = d[0]
                if deps.get(k, 0) < d[1]:
                    deps[k] = d[1]
        for w in writes:
            for d in w.w + w.r:
                k = d[0]
                if deps.get(k, 0) < d[1]:
                    deps[k] = d[1]
        return deps

    def _emit_waits(self, eng, deps):
        wd = self.waited[eng]
        for k, v in deps.items():
            if wd.get(k, 0) >= v:
                continue
            wd[k] = v
            sem = self.sem[k] if isinstance(k, tuple) else self.dsem[k]
            getattr(self.nc, eng).wait_ge(sem, v)

    def _mark(self, dep, reads, writes):
        for r in reads:
            r.r.append(dep)
        for w in writes:
            w.w = [dep]
            w.r = []

    def op(self, eng, fn, reads=(), writes=()):
        if any(r.excl for r in reads):
            writes = list(writes) + [r for r in reads if r.excl]
            reads = [r for r in reads if not r.excl]
        deps = self._collect(reads, writes)
        self._emit_waits(eng, deps)
        if self.cnt[eng] >= self.MAXSEM:
            self.epoch[eng] += 1
            self.cnt[eng] = 0
            self.sem[(eng, self.epoch[eng])] = self.es.enter_context(
                self.nc.semaphore("s_%s_%d" % (eng, self.epoch[eng])))
        self.cnt[eng] += 1
        key = (eng, self.epoch[eng])
        dep = (key, self.cnt[eng])
        fn(getattr(self.nc, eng)).then_inc(self.sem[key], 1)
        self._mark(dep, reads, writes)
        self.ninstr += 1

    STORE_Q = "scalar"

    def dma(self, out, in_, reads=(), writes=(), q="sync", **kw):
        q = self.STORE_Q if str(out.space) == "DRAM" else "sync"
        deps = self._collect(reads, writes)
        i = self.dnext
        self.dnext = (self.dnext + 1) % self.NDMA
        if self.dval[i] > 0:
            deps[i] = max(deps.get(i, 0), self.dval[i])
        self._emit_waits(q, deps)
        self.dval[i] += 16
        dep = (i, self.dval[i])
        getattr(self.nc, q).dma_start(out=out, in_=in_, **kw).then_inc(self.dsem[i], 16)
        self._mark(dep, reads, writes)
        self.ninstr += 1

    def finish(self, final_regions):
        deps = {}
        for e in self.ENG:
            if self.cnt[e]:
                deps[(e, self.epoch[e])] = self.cnt[e]
        for i in range(self.NDMA):
            if self.dval[i]:
                deps[i] = self.dval[i]
        self._emit_waits("sync", deps)

    def barrier(self):
        deps = {}
        for e in self.ENG:
            if self.cnt[e]:
                deps[(e, self.epoch[e])] = self.cnt[e]
        for i in range(self.NDMA):
            if self.dval[i]:
                deps[i] = self.dval[i]
        for e in self.ENG:
            self._emit_waits(e, dict(deps))


D = 2048
L = 2048
NCTX = 256
T = L + NCTX
NT = T // 128
DEPTH = 2
ZC = 12832
C_NAK, C_NAV, C_CKV, C_KPE, C_RWKS = 0, 512, 1024, 1280, 1344
C_NAQ, C_CQ, C_RWQS, C_FN, C_GATE = 2496, 3008, 3520, 4128, 4640
EPS = 1e-6


def bf16_np(a):
    return np.asarray(a, dtype=np.float32).astype(ml_dtypes.bfloat16)


class Prog:
    def __init__(self, depth=DEPTH, ext=None, stages=None):
        self.depth = depth
        self.ext = ext or {}
        self.stages = stages
        self.nc = bass.Bass("TRN2", target_bir_lowering=False)
        self.es = ExitStack()
        self.k = K(self.nc, self.es)
        self.dr = {}
        self.kinds = {}
        self.modT = {}
        self.final_out = True
        import os
        self.dbg = {kk[4:]: int(v) for kk, v in os.environ.items() if kk.startswith("DBG_")}

    def dram(self, name, shape, dt=F32, kind="Internal"):
        kind = self.ext.get(name, kind)
        t = self.nc.dram_tensor(name, list(shape), dt, kind=kind)
        self.dr[name] = (t.ap(), R(name))
        self.kinds[name] = kind
        return self.dr[name]

    def want(self, s):
        return self.stages is None or s in self.stages

    def consts(self):
        k = self.k
        self.identf = k.sb("identf", [128, 128], F32)
        self.ident = k.sb("ident", [128, 128], BF16)
        self.r_ident = R("ident")
        k.op("gpsimd", lambda e: e.memset(self.identf[:], 0.0), writes=[self.r_ident])
        k.op("gpsimd", lambda e: e.affine_select(out=self.identf[:], in_=self.identf[:], pattern=[[-1, 128]],
                                                  compare_op=ALU.not_equal, fill=1.0, base=0, channel_multiplier=1),
             reads=[self.r_ident], writes=[self.r_ident])
        k.op("vector", lambda e: e.tensor_copy(out=self.ident[:], in_=self.identf[:]),
             reads=[self.r_ident], writes=[self.r_ident])
        ap, r = self.dram("cvT", [128, 16, 2], F32, "ExternalInput")
        self.sT = k.sb("sT", [128, 16, 2], F32)
        self.r_sT = R("sT")
        k.dma(self.sT[:], ap, reads=[r], writes=[self.r_sT])
        k.op("scalar", lambda e: e.activation(out=self.sT[:], in_=self.sT[:], func=AF.Silu),
             reads=[self.r_sT], writes=[self.r_sT])

    def stage_mod(self, l):
        k = self.k
        adaw, r_adaw = self.dr["ada_w"]
        adab, r_adab = self.dr["ada_b2"]
        modD, r_modD = self.dram("modD%d" % l, [2, 6 * D], F32)
        modT = k.sb("modT%d" % l, [128, 6, 16, 2], F32)
        r_modT = R()
        self.modT[l] = (modT, r_modT)
        with ExitStack() as es:
            wt = [k.sb("mod_w%d" % i, [128, 2048], F32, es) for i in range(3)]
            r_wt = [R() for _ in range(3)]
            ps = k.ps("mod_ps", [2, 2048], F32, es)
            r_ps = PR()
            pT = k.ps("mod_pT", [128, 4, 128], F32, es)
            r_pT = PR()
            row = [k.sb("mod_row%d" % i, [128, 2048], F32, es) for i in range(2)]
            r_row = [R() for _ in range(2)]
            for i in range(2):
                k.op("gpsimd", lambda e, i=i: e.memset(row[i][:], 0.0), writes=[r_row[i]])
            bias = k.sb("mod_bias", [2, 6 * D], F32, es)
            r_bias = R()
            k.dma(bias[:], adab[l], reads=[r_adab], writes=[r_bias])
            i2 = self.identf[0:2, 0:2]
            it = 0
            for nb in range(6):
                for kc in range(16):
                    w = wt[it % 3]
                    rw = r_wt[it % 3]
                    it += 1
                    k.dma(w[:], adaw[l, kc * 128:(kc + 1) * 128, nb * 2048:(nb + 1) * 2048], reads=[r_adaw], writes=[rw])

                    def mm(e, w=w, kc=kc):
                        last = None
                        for j in range(4):
                            last = e.matmul(ps[:, j * 512:(j + 1) * 512], lhsT=self.sT[:, kc, :], rhs=w[:, j * 512:(j + 1) * 512],
                                            start=(kc == 0), stop=(kc == 15))
                        return last
                    k.op("tensor", mm, reads=[rw, self.r_sT], writes=[r_ps])
                rt = row[nb % 2]
                rr = r_row[nb % 2]
                k.op("vector", lambda e, rt=rt, nb=nb: e.tensor_tensor(out=rt[0:2, :], in0=ps[:], in1=bias[:, nb * 2048:(nb + 1) * 2048], op=ALU.add),
                     reads=[r_ps, r_bias], writes=[rr])
                k.dma(modD[:, nb * 2048:(nb + 1) * 2048], rt[0:2, :], reads=[rr], writes=[r_modD], q="gpsimd")

                for g in range(4):
                    def tr(e, rt=rt, g=g):
                        last = None
                        for c in range(4):
                            cc = g * 4 + c
                            last = e.transpose(out=pT[:, c, :], in_=rt[:, cc * 128:(cc + 1) * 128], identity=self.identf[:])
                        return last
                    k.op("tensor", tr, reads=[rr, self.r_ident], writes=[r_pT])
                    k.op("vector", lambda e, nb=nb, g=g: e.tensor_copy(out=modT[:, nb, g * 4:(g + 1) * 4, :], in_=pT[:, :, 0:2]),
                         reads=[r_pT], writes=[r_modT])
            k.barrier()

    def stage_normz(self, l, xname):
        k = self.k
        xin, r_xin = self.dr[xname]
        win, r_win = self.dr["w_in"]
        g1T_d, r_g1T = self.dr["norm1_gT"]
        z, r_z = self.dram("z%d" % l, [T, ZC], F32)
        self.r_ztile = [R() for _ in range(NT)]
        modT, r_modT = self.modT[l]
        with ExitStack() as es:
            hT = k.sb("hT", [128, 16, T], BF16, es)
            r_hT = [[R() for _ in range(16)] for _ in range(NT)]
            aT = k.sb("nz_aT", [128, 16, 2], F32, es)
            r_aT = R()
            gT = k.sb("nz_gT", [128, 16], F32, es)
            k.dma(gT[:], g1T_d[l], reads=[r_g1T], writes=[r_aT])
            k.op("vector", lambda e: e.tensor_scalar(out=aT[:], in0=modT[:, 1, :, :], scalar1=1.0, scalar2=None, op0=ALU.add),
                 reads=[r_modT], writes=[r_aT])
            k.op("vector", lambda e: e.tensor_tensor(out=aT[:], in0=aT[:], in1=gT[:].unsqueeze(2).to_broadcast([128, 16, 2]), op=ALU.mult),
                 reads=[r_aT], writes=[r_aT])
            self.norm_to_hT(es, xin, r_xin, hT, r_hT, aT, r_aT, modT[:, 0, :, :], r_modT, "nz")
            wf = [k.sb("nz_wf%d" % i, [128, 4, 512], F32, es) for i in range(3)]
            r_wf = [R() for _ in range(3)]
            wb = [k.sb("nz_wb%d" % i, [128, 16, 512], BF16, es) for i in range(2)]
            r_wb = [[R() for _ in range(4)] for _ in range(2)]
            pz = [k.ps("nz_pz%d" % i, [128, 512], F32, es) for i in range(4)]
            r_pz = [PR() for _ in range(4)]
            zo = [k.sb("nz_zo%d" % i, [128, 512], F32, es) for i in range(4)]
            r_zo = [R() for _ in range(4)]
            ncb = (ZC + 511) // 512
            cnt = 0
            ld = 0
            for cb in range(self.dbg.get("NZ_NCB", ncb)):
                c0 = cb * 512
                cw = min(512, ZC - c0)
                b, rb = wb[cb % 2], r_wb[cb % 2]
                for g in range(4):
                    f, rf = wf[ld % 3], r_wf[ld % 3]
                    k.dma(f[:, :, 0:cw], win[l, g * 512:(g + 1) * 512, c0:c0 + cw].rearrange("(kc p) n -> p kc n", p=128),
                          reads=[r_win], writes=[rf])
                    eng = ("vector", "scalar")[ld % 2]
                    ld += 1
                    if eng == "scalar":
                        k.op("scalar", lambda e, f=f, b=b, cw=cw, g=g: e.activation(out=b[:, g * 4:(g + 1) * 4, 0:cw], in_=f[:, :, 0:cw], func=AF.Copy),
                             reads=[rf], writes=[rb[g]])
                    else:
                        k.op(eng, lambda e, f=f, b=b, cw=cw, g=g: e.tensor_copy(out=b[:, g * 4:(g + 1) * 4, 0:cw], in_=f[:, :, 0:cw]),
                             reads=[rf], writes=[rb[g]])
                for t in range(self.dbg.get("NZ_NT", NT)):
                    p, rp = pz[cnt % 4], r_pz[cnt % 4]
                    o, ro = zo[cnt % 4], r_zo[cnt % 4]
                    cnt += 1

                    def mm(e, p=p, t=t, b=b, cw=cw):
                        last = None
                        for kc in range(16):
                            last = e.matmul(p[:, 0:cw], lhsT=hT[:, kc, t * 128:(t + 1) * 128], rhs=b[:, kc, 0:cw],
                                            start=(kc == 0), stop=(kc == 15))
                        return last
                    k.op("tensor", mm, reads=rb + r_hT[t], writes=[rp])
                    if t % 2 == 0:
                        k.op("vector", lambda e, o=o, p=p, cw=cw: e.tensor_copy(out=o[:, 0:cw], in_=p[:, 0:cw]), reads=[rp], writes=[ro])
                    else:
                        k.op("scalar", lambda e, o=o, p=p, cw=cw: e.activation(out=o[:, 0:cw], in_=p[:, 0:cw], func=AF.Copy), reads=[rp], writes=[ro])
                    k.dma(z[t * 128:(t + 1) * 128, c0:c0 + cw], o[:, 0:cw], reads=[ro], writes=[self.r_ztile[t]], q="gpsimd")
            k.barrier()

    def norm_to_hT(self, es, xin, r_xin, hT, r_hT, aT, r_aT, shT, r_shT, pfx, h_tok=None):
        k = self.k
        xt = [k.sb(pfx + "_xt%d" % i, [128, D], F32, es) for i in range(2)]
        r_xt = [R() for _ in range(2)]
        xn = [k.sb(pfx + "_xn%d" % i, [128, D], BF16, es) for i in range(2)]
        r_xn = [R() for _ in range(2)]
        junk = k.sb(pfx + "_junk", [128, D], BF16, es)
        r_junk = R()
        st = k.sb(pfx + "_st", [128, NT, 2], F32, es)
        r_stl = [R() for _ in range(NT)]
        pt = [k.ps(pfx + "_pt%d" % i, [128, 8, 128], BF16, es) for i in range(2)]
        r_pt = [PR() for _ in range(2)]
        for t in range(self.dbg.get("NZ_NT", NT)):
            row = 1 if t < 2 else 0
            r_st = r_stl[t]
            x_, rx = xt[t % 2], r_xt[t % 2]
            n_, rn = xn[t % 2], r_xn[t % 2]
            STEP = self.dbg.get("STEP", 99)
            if STEP < 1:
                continue
            k.dma(x_[:], xin[t * 128:(t + 1) * 128, :], reads=[r_xin], writes=[rx])
            if STEP < 2:
                continue
            k.op("scalar", lambda e, x_=x_, t=t: e.activation(out=junk[:], in_=x_[:], func=AF.Square, accum_out=st[:, t, 0:1]),
                 reads=[rx], writes=[r_junk, r_st])
            if STEP < 3:
                continue
            k.op("scalar", lambda e, t=t: e.activation(out=st[:, t, 1:2], in_=st[:, t, 0:1], func=AF.Sqrt, bias=self.eps_t[:, 0:1], scale=1.0 / D),
                 reads=[r_st], writes=[r_st])
            if STEP < 4:
                continue
            k.op("vector", lambda e, t=t: e.reciprocal(out=st[:, t, 1:2], in_=st[:, t, 1:2]), reads=[r_st], writes=[r_st])
            if STEP < 5:
                continue
            k.op("vector", lambda e, x_=x_, n_=n_, t=t: e.tensor_scalar(out=n_[:], in0=x_[:], scalar1=st[:, t, 1:2], scalar2=None, op0=ALU.mult),
                 reads=[rx, r_st], writes=[rn])
            if STEP < 6:
                continue
            for half in range(2):
                p, rp = pt[half], r_pt[half]

                def tr(e, p=p, n_=n_, half=half):
                    last = None
                    for j in range(8):
                        dc = half * 8 + j
                        last = e.transpose(out=p[:, j, :], in_=n_[:, dc * 128:(dc + 1) * 128], identity=self.ident[:])
                    return last
                k.op("tensor", tr, reads=[rn, self.r_ident], writes=[rp])
                if STEP < 7:
                    continue
                for j in range(8):
                    dc = half * 8 + j
                    if STEP == 7 and j % 2 == 1:
                        continue
                    if STEP == 8 and j % 2 == 0:
                        continue
                    eng = "vector" if half == 0 else "scalar"
                    if eng == "vector":
                        k.op("vector", lambda e, p=p, j=j, dc=dc, t=t, row=row: e.tensor_scalar(
                            out=hT[:, dc, t * 128:(t + 1) * 128], in0=p[:, j, :], scalar1=aT[:, dc, row:row + 1],
                            scalar2=shT[:, dc, row:row + 1], op0=ALU.mult, op1=ALU.add),
                            reads=[rp, r_aT, r_shT], writes=[r_hT[t][dc]])
                    else:
                        k.op("scalar", lambda e, p=p, j=j, dc=dc, t=t, row=row: e.activation(
                            out=hT[:, dc, t * 128:(t + 1) * 128], in_=p[:, j, :], func=AF.Identity,
                            scale=aT[:, dc, row:row + 1], bias=shT[:, dc, row:row + 1]),
                            reads=[rp, r_aT, r_shT], writes=[r_hT[t][dc]])

    def head_rms(self, eng_sq, x_, w, nh, dh, st, junk, out, gain_bc, reads, r_st, r_junk, writes, tmp, r_tmp):
        k = self.k
        x3 = x_.rearrange("p (h d) -> p h d", h=nh)
        k.op("vector", lambda e: e.tensor_tensor(out=junk, in0=x_, in1=x_, op=ALU.mult), reads=reads, writes=[r_junk])
        k.op("vector", lambda e: e.tensor_reduce(out=st, in_=junk.rearrange("p (h d) -> p h d", h=nh), axis=AX.X, op=ALU.add),
             reads=[r_junk], writes=[r_st])
        k.op("scalar", lambda e: e.activation(out=st, in_=st, func=AF.Sqrt, bias=self.eps_t[:, 0:1], scale=1.0 / dh),
             reads=[r_st], writes=[r_st])
        k.op("vector", lambda e: e.reciprocal(out=st, in_=st), reads=[r_st], writes=[r_st])
        k.op("vector", lambda e: e.tensor_tensor(out=tmp.rearrange("p (h d) -> p h d", h=nh), in0=x3,
                                                in1=st.unsqueeze(2).to_broadcast([128, nh, dh]), op=ALU.mult),
             reads=reads + [r_st], writes=[r_tmp])
        k.op("gpsimd", lambda e: e.tensor_tensor(out=out, in0=tmp, in1=gain_bc, op=ALU.mult), reads=[r_tmp] + [self.r_gain], writes=writes)

    def stage_na(self, l):
        k = self.k
        z, _ = self.dr["z%d" % l]
        r_zt = self.r_ztile
        yd, r_yd = self.dram("y_na%d" % l, [T, 512], F32)
        gq_d, r_gq = self.dr["na_qg"]
        gk_d, r_gk = self.dr["na_kg"]
        bias_d, r_bias_d = self.dr["na_bias"]
        with ExitStack() as es:
            QT = k.sb("na_QT", [128, 4, T], BF16, es)
            KT = k.sb("na_KT", [128, 4, T], BF16, es)
            r_QT = [R() for _ in range(NT)]
            r_KT = [R() for _ in range(NT)]
            Va = k.sb("na_V", [128, NT, 8, 65], BF16, es)
            r_Va = [R() for _ in range(NT)]
            yna = k.sb("na_y", [128, NT, 512], F32, es)
            r_yna = [R() for _ in range(NT)]
            gq = k.sb("na_gq", [128, 512], F32, es)
            gk = k.sb("na_gk", [128, 512], F32, es)
            self.r_gain = R()
            k.dma(gq[:], gq_d[l].partition_broadcast(128), reads=[r_gq], writes=[self.r_gain])
            k.dma(gk[:], gk_d[l].partition_broadcast(128), reads=[r_gk], writes=[self.r_gain])
            k.op("vector", lambda e: e.tensor_scalar(out=gq[:], in0=gq[:], scalar1=0.125, scalar2=None, op0=ALU.mult),
                 reads=[self.r_gain], writes=[self.r_gain])
            r_ones = R()
            k.op("gpsimd", lambda e: e.memset(Va[:, :, :, 64:65], 1.0), writes=r_Va)
            xin = [k.sb("na_x%d" % i, [128, 512], F32, es) for i in range(6)]
            r_xin = [R() for _ in range(6)]
            junk = k.sb("na_junk", [128, 512], F32, es)
            r_junk = R()
            tmp = k.sb("na_tmp", [128, 512], F32, es)
            r_tmp = R()
            st = k.sb("na_st", [128, NT, 2, 8], F32, es)
            xb = [k.sb("na_xb%d" % i, [128, 512], BF16, es) for i in range(2)]
            r_xb = [R() for _ in range(2)]
            ptr = [k.ps("na_ptr%d" % i, [128, 8, 128], BF16, es) for i in range(1)]
            r_ptr = [PR() for _ in range(1)]
            ld = 0
            nb = 0
            for t in range(NT):
                rows = slice(t * 128, (t + 1) * 128)
                for which, c0 in ((0, C_NAQ), (1, C_NAK)):
                    x_, rx = xin[ld % 6], r_xin[ld % 6]
                    ld += 1
                    k.dma(x_[:], z[rows, c0:c0 + 512], reads=[r_zt[t]], writes=[rx])
                    b_, rb = xb[nb % 2], r_xb[nb % 2]
                    nb += 1
                    r_st = R()
                    self.head_rms(None, x_[:], 512, 8, 64, st[:, t, which, :], junk[:], b_[:], (gq if which == 0 else gk)[:],
                                  [rx], r_st, r_junk, [rb], tmp[:], r_tmp)
                    p, rp = ptr[0], r_ptr[0]

                    def tr(e, p=p, b_=b_):
                        last = None
                        for j in range(4):
                            last = e.transpose(out=p[:, j, :], in_=b_[:, j * 128:(j + 1) * 128], identity=self.ident[:])
                        return last
                    k.op("tensor", tr, reads=[rb, self.r_ident], writes=[rp])
                    dst = QT if which == 0 else KT
                    rd = (r_QT if which == 0 else r_KT)[t]
                    k.op("scalar", lambda e, dst=dst, p=p, rows=rows: e.activation(out=dst[:, :, rows], in_=p[:, 0:4, :], func=AF.Copy),
                         reads=[rp], writes=[rd])
                x_, rx = xin[ld % 6], r_xin[ld % 6]
                ld += 1
                k.dma(x_[:], z[rows, C_NAV:C_NAV + 512], reads=[r_zt[t]], writes=[rx])
                k.op("vector", lambda e, x_=x_, t=t: e.tensor_copy(out=Va[:, t, :, 0:64], in_=x_[:].rearrange("p (h d) -> p h d", h=8)),
                     reads=[rx], writes=[r_Va[t]])
            bias = [k.sb("na_bias%d" % i, [128, 5, 5, 128], F32, es) for i in range(2)]
            r_bias = [R() for _ in range(2)]
            stp = [k.ps("na_st%d" % i, [128, 8, 128], F32, es) for i in range(2)]
            r_stp = [PR() for _ in range(2)]
            po = [k.ps("na_po%d" % i, [128, 512], F32, es) for i in range(2)]
            r_po = [PR() for _ in range(2)]
            sbt = [k.sb("na_sb%d" % i, [128, 5, 128], F32, es) for i in range(2)]
            r_sbt = [R() for _ in range(2)]
            PT = [k.sb("na_PT%d" % i, [128, 7, 128], BF16, es) for i in range(2)]
            r_PT = [R() for _ in range(2)]
            rc = k.sb("na_rc", [128, 2], F32, es)
            r_rc = [R(), R()]
            it = 0
            for h in range(8):
                bs, rbs = bias[h % 2], r_bias[h % 2]
                for c in range(5):
                    k.dma(bs[:, c, :, :], bias_d[l, h, c].rearrange("t k q -> k t q"), reads=[r_bias_d], writes=[rbs])
                hp, hs = h // 2, (h % 2) * 64
                for t in range(NT):
                    if t < 2:
                        kts = [0, 1]
                        nbias = 0
                        cls = 0
                    else:
                        j = t - 2
                        lo = min(max(j - 2, 0), 11)
                        kts = [lo + 2 + i for i in range(5)] + [0, 1]
                        nbias = 5
                        cls = {0: 0, 1: 1, 14: 3, 15: 4}.get(j, 2)
                    nk = len(kts)
                    sp, rsp = stp[it % 2], r_stp[it % 2]
                    pp, rpp = PT[it % 2], r_PT[it % 2]
                    sb_, rsb = sbt[it % 2], r_sbt[it % 2]
                    o_, ro = po[it % 2], r_po[it % 2]
                    rcc, rrc = rc[:, it % 2:it % 2 + 1], r_rc[it % 2]
                    it += 1

                    def qk(e, sp=sp, kts=kts, t=t, hp=hp, hs=hs):
                        last = None
                        for i, kt in enumerate(kts):
                            last = e.matmul(sp[:, i, :], lhsT=KT[hs:hs + 64, hp, kt * 128:(kt + 1) * 128],
                                            rhs=QT[hs:hs + 64, hp, t * 128:(t + 1) * 128], start=True, stop=True)
                        return last
                    k.op("tensor", qk, reads=[r_KT[kt] for kt in kts] + [r_QT[t]], writes=[rsp])
                    if nbias:
                        k.op("vector", lambda e, sb_=sb_, sp=sp, bs=bs, cls=cls: e.tensor_tensor(out=sb_[:], in0=sp[:, 0:5, :], in1=bs[:, cls, :, :], op=ALU.add),
                             reads=[rsp, rbs], writes=[rsb])
                        k.op("scalar", lambda e, pp=pp, sb_=sb_: e.activation(out=pp[:, 0:5, :], in_=sb_[:], func=AF.Exp), reads=[rsb], writes=[rpp])
                        k.op("scalar", lambda e, pp=pp, sp=sp: e.activation(out=pp[:, 5:7, :], in_=sp[:, 5:7, :], func=AF.Exp), reads=[rsp, rpp], writes=[rpp])
                    else:
                        k.op("scalar", lambda e, pp=pp, sp=sp: e.activation(out=pp[:, 0:2, :], in_=sp[:, 0:2, :], func=AF.Exp), reads=[rsp], writes=[rpp])

                    def pv(e, o_=o_, pp=pp, kts=kts, h=h, nk=nk):
                        last = None
                        for i, kt in enumerate(kts):
                            last = e.matmul(o_[:, 0:65], lhsT=pp[:, i, :], rhs=Va[:, kt, h, :], start=(i == 0), stop=(i == nk - 1))
                        return last
                    k.op("tensor", pv, reads=[rpp] + [r_Va[kt] for kt in kts], writes=[ro])
                    k.op("vector", lambda e, rcc=rcc, o_=o_: e.reciprocal(out=rcc, in_=o_[:, 64:65]), reads=[ro], writes=[rrc])
                    k.op("vector", lambda e, rcc=rcc, o_=o_, t=t, h=h: e.tensor_scalar(out=yna[:, t, h * 64:(h + 1) * 64], in0=o_[:, 0:64],
                                                                                   scalar1=rcc, scalar2=None, op0=ALU.mult),
                         reads=[ro, rrc], writes=[r_yna[t]])
            for t in range(NT):
                k.dma(yd[t * 128:(t + 1) * 128, :], yna[:, t, :], reads=[r_yna[t]], writes=[r_yd], q="gpsimd")
            k.barrier()

    def stage_mla(self, l):
        k = self.k
        z, _ = self.dr["z%d" % l]
        r_zt = self.r_ztile
        yd, r_yd = self.dram("y_mla%d" % l, [T, 512], F32)
        MSC = 192.0 ** -0.5
        with ExitStack() as es:
            QTn = k.sb("ml_QTn", [128, 4, T], BF16, es)
            KTn = k.sb("ml_KTn", [128, 4, T], BF16, es)
            QTp = k.sb("ml_QTp", [128, 2, T], BF16, es)
            KTp = k.sb("ml_KTp", [128, 2, T], BF16, es)
            r_Q = [R() for _ in range(NT)]
            r_K = [R() for _ in range(NT)]
            Va = k.sb("ml_V", [128, NT, 4, 129], BF16, es)
            r_Va = [R() for _ in range(NT)]
            ym = k.sb("ml_y", [128, NT, 512], F32, es)
            r_ym = [R() for _ in range(NT)]
            k.op("gpsimd", lambda e: e.memset(Va[:, :, :, 128:129], 1.0), writes=r_Va)
            with ExitStack() as es2:
                self.r_gain = R()
                g_cq = k.sb("ml_gcq", [128, 512], F32, es2)
                g_ckv = k.sb("ml_gckv", [128, 256], F32, es2)
                g_q = k.sb("ml_gq", [128, 768], F32, es2)
                g_kn = k.sb("ml_gkn", [128, 512], F32, es2)
                g_kp = k.sb("ml_gkp", [128, 64], F32, es2)
                for tl, nm in ((g_cq, "mla_cqg"), (g_ckv, "mla_ckvg"), (g_q, "mla_qg"), (g_kn, "mla_kng"), (g_kp, "mla_kpg")):
                    ap, rr = self.dr[nm]
                    k.dma(tl[:], ap[l].partition_broadcast(128), reads=[rr], writes=[self.r_gain])
                k.op("vector", lambda e: e.tensor_scalar(out=g_q[:], in0=g_q[:], scalar1=MSC, scalar2=None, op0=ALU.mult),
                     reads=[self.r_gain], writes=[self.r_gain])
                wuq = k.sb("ml_wuq", [128, 4, 768], BF16, es2)
                wukv = k.sb("ml_wukv", [128, 2, 1024], BF16, es2)
                r_w = R()
                wst = k.sb("ml_wst", [128, 4, 768], F32, es2)
                r_wst = R()
                ap, rr = self.dr["mla_w_uq"]
                k.dma(wst[:], ap[l].rearrange("(kc p) n -> p kc n", p=128), reads=[rr], writes=[r_wst])
                k.op("vector", lambda e: e.tensor_copy(out=wuq[:], in_=wst[:]), reads=[r_wst], writes=[r_w])
                ap, rr = self.dr["mla_w_ukv"]
                wst2 = wst[:].rearrange("p a b -> p (a b)")[:, 0:2048].rearrange("p (a b) -> p a b", a=2)
                k.dma(wst2, ap[l].rearrange("(kc p) n -> p kc n", p=128), reads=[rr], writes=[r_wst])
                k.op("vector", lambda e: e.tensor_copy(out=wukv[:], in_=wst2), reads=[r_wst], writes=[r_w])
                cos_d, r_cos = self.dr["rope_cos"]
                sin_d, r_sin = self.dr["rope_sin"]
                xin = [k.sb("ml_x%d" % i, [128, 832], F32, es2) for i in range(2)]
                r_xin = [R() for _ in range(2)]
                cs = [k.sb("ml_cs%d" % i, [128, 2, 64], F32, es2) for i in range(2)]
                r_cs = [R() for _ in range(2)]
                junk = k.sb("ml_junk", [128, 1024], F32, es2)
                r_junk = R()
                tmp = k.sb("ml_tmp", [128, 1024], F32, es2)
                r_tmp = R()
                st = k.sb("ml_st", [128, NT, 16], F32, es2)
                cb = k.sb("ml_cb", [128, 768], BF16, es2)
                r_cb = [R(), R()]
                cT = k.sb("ml_cT", [128, 6, 128], BF16, es2)
                r_cT = [R(), R()]
                qf = k.sb("ml_qf", [128, 768], F32, es2)
                r_qf = R()
                qn = k.sb("ml_qn", [128, 768], F32, es2)
                r_qn = R()
                kvf = k.sb("ml_kvf", [128, 1024], F32, es2)
                r_kvf = R()
                qb = k.sb("ml_qb", [128, 4, 128], BF16, es2)
                qpb = k.sb("ml_qpb", [128, 4, 64], BF16, es2)
                kb = k.sb("ml_kb", [128, 4, 128], BF16, es2)
                kpb = k.sb("ml_kpb", [128, 4, 64], BF16, es2)
                r_qb, r_qpb, r_kb, r_kpb = R(), R(), R(), R()
                rp1 = k.sb("ml_rp1", [128, 4, 64], F32, es2)
                rp2 = k.sb("ml_rp2", [128, 4, 64], F32, es2)
                r_rp1, r_rp2 = R(), R()
                kpg = k.sb("ml_kpgt", [128, 64], F32, es2)
                kpr = k.sb("ml_kpr", [128, 64], F32, es2)
                r_kpg, r_kpr = R(), R()
                ptr = k.ps("ml_ptr", [128, 8, 128], BF16, es2)
                r_ptr = PR()
                pq = k.ps("ml_pq", [128, 2, 512], F32, es2)
                r_pq = PR()
                pkv = k.ps("ml_pkv", [128, 2, 512], F32, es2)
                r_pkv = PR()

                def rope(x3, nh, cs_, out3, reads, writes):
                    cosb = cs_[:, 0, :].unsqueeze(1).to_broadcast([128, nh, 64])
                    k.op("vector", lambda e: e.tensor_tensor(out=rp1[:, 0:nh, :], in0=x3, in1=cosb, op=ALU.mult), reads=reads, writes=[r_rp1])
                    x5 = x3.rearrange("p h (g a d) -> p h g a d", g=2, a=2)
                    o5 = rp2[:, 0:nh, :].rearrange("p h (g a d) -> p h g a d", g=2, a=2)
                    s4 = cs_[:, 1, :].rearrange("p (g a d) -> p g a d", g=2, a=2)
                    for h in range(nh):
                        for a in range(2):
                            k.op("gpsimd", lambda e, h=h, a=a: e.tensor_tensor(out=o5[:, h, :, a, :], in0=x5[:, h, :, 1 - a, :], in1=s4[:, :, a, :], op=ALU.mult),
                                 reads=reads, writes=[r_rp2])
                    k.op("vector", lambda e: e.tensor_tensor(out=out3, in0=rp1[:, 0:nh, :], in1=rp2[:, 0:nh, :], op=ALU.add),
                         reads=[r_rp1, r_rp2], writes=writes)

                for t in range(NT):
                    rows = slice(t * 128, (t + 1) * 128)
                    x_, rx = xin[t % 2], r_xin[t % 2]
                    cs_, rcs = cs[t % 2], r_cs[t % 2]
                    k.dma(x_[:, 0:512], z[rows, C_CQ:C_CQ + 512], reads=[r_zt[t]], writes=[rx])
                    k.dma(x_[:, 512:832], z[rows, C_CKV:C_CKV + 320], reads=[r_zt[t]], writes=[rx])
                    k.dma(cs_[:, 0, :], cos_d[rows, :], reads=[r_cos], writes=[rcs])
                    k.dma(cs_[:, 1, :], sin_d[rows, :], reads=[r_sin], writes=[rcs])
                    self.head_rms(None, x_[:, 0:512], 512, 1, 512, st[:, t, 0:1], junk[:, 0:512], cb[:, 0:512], g_cq[:],
                                  [rx], R(), r_junk, [r_cb[0]], tmp[:, 0:512], r_tmp)
                    self.head_rms(None, x_[:, 512:768], 256, 1, 256, st[:, t, 1:2], junk[:, 0:256], cb[:, 512:768], g_ckv[:],
                                  [rx], R(), r_junk, [r_cb[1]], tmp[:, 0:256], r_tmp)

                    def tr1(e):
                        last = None
                        for j in range(6):
                            last = e.transpose(out=ptr[:, j, :], in_=cb[:, j * 128:(j + 1) * 128], identity=self.ident[:])
                        return last
                    k.op("tensor", tr1, reads=r_cb + [self.r_ident], writes=[r_ptr])
                    k.op("scalar", lambda e: e.activation(out=cT[:], in_=ptr[:, 0:6, :], func=AF.Copy), reads=[r_ptr], writes=r_cT)

                    def mmq(e):
                        last = None
                        for hf in range(2):
                            for kc in range(4):
                                last = e.matmul(pq[:, hf, 0:384], lhsT=cT[:, kc, :], rhs=wuq[:, kc, hf * 384:(hf + 1) * 384],
                                                start=(kc == 0), stop=(kc == 3))
                        return last
                    k.op("tensor", mmq, reads=r_cT + [r_w], writes=[r_pq])

                    def mmkv(e):
                        last = None
                        for hf in range(2):
                            for kc in range(2):
                                last = e.matmul(pkv[:, hf, :], lhsT=cT[:, 4 + kc, :], rhs=wukv[:, kc, hf * 512:(hf + 1) * 512],
                                                start=(kc == 0), stop=(kc == 1))
                        return last
                    k.op("tensor", mmkv, reads=r_cT + [r_w], writes=[r_pkv])
                    k.op("scalar", lambda e: e.activation(out=qf[:].rearrange("p (a b) -> p a b", a=2), in_=pq[:, :, 0:384], func=AF.Copy),
                         reads=[r_pq], writes=[r_qf])
                    k.op("vector", lambda e: e.tensor_copy(out=kvf[:].rearrange("p (a b) -> p a b", a=2), in_=pkv[:]), reads=[r_pkv], writes=[r_kvf])
                    self.head_rms(None, qf[:], 768, 4, 192, st[:, t, 2:6], junk[:, 0:768], qn[:], g_q[:], [r_qf], R(), r_junk, [r_qn],
                                  tmp[:, 0:768], r_tmp)
                    qn3 = qn[:].rearrange("p (h d) -> p h d", h=4)
                    k.op("vector", lambda e: e.tensor_copy(out=qb[:], in_=qn3[:, :, 0:128]), reads=[r_qn], writes=[r_qb])
                    rope(qn3[:, :, 128:192], 4, cs_, qpb[:], [r_qn, rcs], [r_qpb])
                    kv4 = kvf[:].rearrange("p (h s d) -> p h s d", h=4, s=2)
                    k.op("vector", lambda e, t=t: e.tensor_copy(out=Va[:, t, :, 0:128], in_=kv4[:, :, 1, :]), reads=[r_kvf], writes=[r_Va[t]])
                    kpe = x_[:, 768:832]
                    r_sk = R()
                    k.op("vector", lambda e: e.tensor_tensor(out=junk[:, 0:512].rearrange("p (h d) -> p h d", h=4), in0=kv4[:, :, 0, :], in1=kv4[:, :, 0, :], op=ALU.mult),
                         reads=[r_kvf], writes=[r_junk])
                    k.op("vector", lambda e, t=t: e.tensor_reduce(out=st[:, t, 6:10], in_=junk[:, 0:512].rearrange("p (h d) -> p h d", h=4), axis=AX.X, op=ALU.add),
                         reads=[r_junk], writes=[r_sk])
                    k.op("vector", lambda e: e.tensor_tensor(out=junk[:, 512:576], in0=kpe, in1=kpe, op=ALU.mult), reads=[rx], writes=[r_junk])
                    k.op("vector", lambda e, t=t: e.tensor_reduce(out=st[:, t, 10:11], in_=junk[:, 512:576], axis=AX.X, op=ALU.add), reads=[r_junk], writes=[r_sk])
                    k.op("vector", lambda e, t=t: e.tensor_scalar(out=st[:, t, 6:10], in0=st[:, t, 6:10], scalar1=st[:, t, 10:11], scalar2=None, op0=ALU.add),
                         reads=[r_sk], writes=[r_sk])
                    k.op("scalar", lambda e, t=t: e.activation(out=st[:, t, 6:10], in_=st[:, t, 6:10], func=AF.Sqrt, bias=self.eps_t[:, 0:1], scale=1.0 / 192),
                         reads=[r_sk], writes=[r_sk])
                    k.op("vector", lambda e, t=t: e.reciprocal(out=st[:, t, 6:10], in_=st[:, t, 6:10]), reads=[r_sk], writes=[r_sk])
                    k.op("vector", lambda e, t=t: e.tensor_tensor(out=tmp[:, 0:512].rearrange("p (h d) -> p h d", h=4), in0=kv4[:, :, 0, :],
                                                                 in1=st[:, t, 6:10].unsqueeze(2).to_broadcast([128, 4, 128]), op=ALU.mult),
                         reads=[r_kvf, r_sk], writes=[r_tmp])
                    k.op("gpsimd", lambda e: e.tensor_tensor(out=kb[:].rearrange("p h d -> p (h d)"), in0=tmp[:, 0:512], in1=g_kn[:], op=ALU.mult),
                         reads=[r_tmp, self.r_gain], writes=[r_kb])
                    k.op("vector", lambda e: e.tensor_tensor(out=kpg[:], in0=kpe, in1=g_kp[:], op=ALU.mult), reads=[rx, self.r_gain], writes=[r_kpg])
                    rope(kpg[:].unsqueeze(1), 1, cs_, kpr[:].unsqueeze(1), [r_kpg, rcs], [r_kpr])
                    k.op("vector", lambda e, t=t: e.tensor_tensor(out=kpb[:], in0=kpr[:].unsqueeze(1).to_broadcast([128, 4, 64]),
                                                                 in1=st[:, t, 6:10].unsqueeze(2).to_broadcast([128, 4, 64]), op=ALU.mult),
                         reads=[r_kpr, r_sk], writes=[r_kpb])
                    for (nb_, pb_, rnb, rpb_, dn, dp, rd) in ((qb, qpb, r_qb, r_qpb, QTn, QTp, r_Q[t]), (kb, kpb, r_kb, r_kpb, KTn, KTp, r_K[t])):
                        def tr2(e, nb_=nb_, pb_=pb_):
                            last = None
                            for h in range(4):
                                last = e.transpose(out=ptr[:, h, :], in_=nb_[:, h, :], identity=self.ident[:])
                            for i in range(2):
                                last = e.transpose(out=ptr[:, 4 + i, :], in_=pb_[:, 2 * i:2 * i + 2, :].rearrange("p a b -> p (a b)"), identity=self.ident[:])
                            return last
                        k.op("tensor", tr2, reads=[rnb, rpb_, self.r_ident], writes=[r_ptr])
                        k.op("scalar", lambda e, dn=dn, rows=rows: e.activation(out=dn[:, :, rows], in_=ptr[:, 0:4, :], func=AF.Copy), reads=[r_ptr], writes=[rd])
                        k.op("vector", lambda e, dp=dp, rows=rows: e.tensor_copy(out=dp[:, :, rows], in_=ptr[:, 4:6, :]), reads=[r_ptr], writes=[rd])
                k.barrier()
            with ExitStack() as es3:
                stp = [k.ps("ml_st%d" % i, [128, 512], F32, es3) for i in range(2)]
                r_stp = [PR() for _ in range(2)]
                acc = [k.ps("ml_acc%d" % i, [128, 512], F32, es3) for i in range(4)]
                r_acc = [PR() for _ in range(4)]
                PT = [k.sb("ml_PT%d" % i, [128, 512], BF16, es3) for i in range(3)]
                r_PT = [R() for _ in range(3)]
                rc = k.sb("ml_rc", [128, 4], F32, es3)
                r_rc = [R() for _ in range(4)]
                groups = [[0, 1]] + [[2 + 4 * g + i for i in range(4)] for g in range(4)]
                it = 0
                for grp in groups:
                    kts = [0, 1] if grp[0] == 0 else list(range(NT))
                    q0 = grp[0] * 128
                    nq = len(grp) * 128
                    for h in range(4):
                        hp, hs = h // 2, (h % 2) * 64
                        for ki, kt in enumerate(kts):
                            sp, rsp = stp[it % 2], r_stp[it % 2]
                            pp, rpp = PT[it % 3], r_PT[it % 3]
                            it += 1

                            def qk(e, sp=sp, kt=kt, h=h, hp=hp, hs=hs, q0=q0, nq=nq):
                                e.matmul(sp[:, 0:nq], lhsT=KTn[:, h, kt * 128:(kt + 1) * 128], rhs=QTn[:, h, q0:q0 + nq], start=True, stop=False)
                                return e.matmul(sp[:, 0:nq], lhsT=KTp[hs:hs + 64, hp, kt * 128:(kt + 1) * 128], rhs=QTp[hs:hs + 64, hp, q0:q0 + nq],
                                                start=False, stop=True)
                            k.op("tensor", qk, reads=[r_K[kt]] + [r_Q[t] for t in grp], writes=[rsp])
                            k.op("scalar", lambda e, pp=pp, sp=sp, nq=nq: e.activation(out=pp[:, 0:nq], in_=sp[:, 0:nq], func=AF.Exp), reads=[rsp], writes=[rpp])
                            for i, t in enumerate(grp):
                                k.op("tensor", lambda e, i=i, pp=pp, kt=kt, h=h, ki=ki, kts=kts: e.matmul(
                                    acc[i][:, 0:129], lhsT=pp[:, i * 128:(i + 1) * 128], rhs=Va[:, kt, h, :], start=(ki == 0), stop=(ki == len(kts) - 1)),
                                    reads=[rpp, r_Va[kt]], writes=[r_acc[i]])
                        for i, t in enumerate(grp):
                            k.op("vector", lambda e, i=i: e.reciprocal(out=rc[:, i:i + 1], in_=acc[i][:, 128:129]), reads=[r_acc[i]], writes=[r_rc[i]])
                            k.op("vector", lambda e, i=i, t=t, h=h: e.tensor_scalar(out=ym[:, t, h * 128:(h + 1) * 128], in0=acc[i][:, 0:128],
                                                                                  scalar1=rc[:, i:i + 1], scalar2=None, op0=ALU.mult),
                                 reads=[r_acc[i], r_rc[i]], writes=[r_ym[t]])
                for t in range(NT):
                    k.dma(yd[t * 128:(t + 1) * 128, :], ym[:, t, :], reads=[r_ym[t]], writes=[r_yd])
                k.barrier()

    def stage_fourier(self, l):
        k = self.k
        z, _ = self.dr["z%d" % l]
        r_zt = self.r_ztile
        yd, r_yd = self.dram("y_fn%d" % l, [T, 512], F32)
        csd, r_csd = self.dr["dft_c"]
        dL, r_dL = self.dr["dft_L"]
        dC, r_dC = self.dr["dft_C"]
        with ExitStack() as es:
            G = k.sb("fn_G", [128, NT, 2, 512], BF16, es)
            r_G = [R() for _ in range(NT)]
            CS = k.sb("fn_CS", [128, 256], BF16, es)
            r_CS = R()
            k.dma(CS[:], csd, reads=[r_csd], writes=[r_CS])
            xin = [k.sb("fn_x%d" % i, [128, 512], F32, es) for i in range(2)]
            r_xin = [R() for _ in range(2)]
            xb = [k.sb("fn_xb%d" % i, [128, 512], BF16, es) for i in range(2)]
            r_xb = [R() for _ in range(2)]
            xT = [k.sb("fn_xT%d" % i, [128, 4, 128], BF16, es) for i in range(2)]
            r_xT = [R() for _ in range(2)]
            ptr = k.ps("fn_ptr", [128, 8, 128], BF16, es)
            r_ptr = PR()
            pg = [k.ps("fn_pg%d" % i, [128, 2, 256], F32, es) for i in range(2)]
            r_pg = [PR() for _ in range(2)]
            py = [k.ps("fn_py%d" % i, [128, 512], F32, es) for i in range(2)]
            r_py = [PR() for _ in range(2)]
            for t in range(NT):
                x_, rx = xin[t % 2], r_xin[t % 2]
                b_, rb = xb[t % 2], r_xb[t % 2]
                T_, rT = xT[t % 2], r_xT[t % 2]
                k.dma(x_[:], z[t * 128:(t + 1) * 128, C_FN:C_FN + 512], reads=[r_zt[t]], writes=[rx])
                k.op("gpsimd", lambda e, x_=x_, b_=b_: e.tensor_copy(out=b_[:], in_=x_[:]), reads=[rx], writes=[rb])

                def tr(e, b_=b_):
                    last = None
                    for g in range(4):
                        last = e.transpose(out=ptr[:, g, :], in_=b_[:, g * 128:(g + 1) * 128], identity=self.ident[:])
                    return last
                k.op("tensor", tr, reads=[rb, self.r_ident], writes=[r_ptr])
                k.op("scalar", lambda e, T_=T_: e.activation(out=T_[:], in_=ptr[:, 0:4, :], func=AF.Copy), reads=[r_ptr], writes=[rT])
                for hf in range(2):
                    p, rp = pg[hf], r_pg[hf]

                    def mm(e, p=p, T_=T_, hf=hf):
                        last = None
                        for gg in range(2):
                            last = e.matmul(p[:, gg, :], lhsT=T_[:, hf * 2 + gg, :], rhs=CS[:], start=True, stop=True)
                        return last
                    k.op("tensor", mm, reads=[rT, r_CS], writes=[rp])
                    G4 = G[:, t, :, :].rearrange("p c (g d) -> p c g d", g=4)
                    if hf == 0:
                        k.op("vector", lambda e, p=p, G4=G4, hf=hf: e.tensor_copy(out=G4[:, :, 0:2, :].rearrange("p c g d -> p g c d"),
                                                                                in_=p[:].rearrange("p g (c d) -> p g c d", c=2)),
                             reads=[rp], writes=[r_G[t]])
                    else:
                        k.op("scalar", lambda e, p=p, G4=G4, hf=hf: e.activation(out=G4[:, :, 2:4, :].rearrange("p c g d -> p g c d"),
                                                                                in_=p[:].rearrange("p g (c d) -> p g c d", c=2), func=AF.Copy),
                             reads=[rp, r_G[t]], writes=[r_G[t]])
            Dm = [k.sb("fn_D%d" % i, [128, 2, 16, 128], BF16, es) for i in range(2)]
            r_Dm = [R() for _ in range(2)]
            yo = [k.sb("fn_yo%d" % i, [128, 512], F32, es) for i in range(2)]
            r_yo = [R() for _ in range(2)]
            for to in range(NT):
                ctx = to < 2
                nin = 2 if ctx else 16
                t0 = 0 if ctx else 2
                src = dC if ctx else dL
                rsrc = r_dC if ctx else r_dL
                col0 = (to - t0) * 128
                D_, rD = Dm[to % 2], r_Dm[to % 2]
                for c in range(2):
                    k.dma(D_[:, c, 0:nin, :], src[c, :, col0:col0 + 128].rearrange("(nt p) q -> p nt q", p=128), reads=[rsrc], writes=[rD])
                p, rp = py[to % 2], r_py[to % 2]

                def mm2(e, p=p, D_=D_, nin=nin, t0=t0):
                    last = None
                    n = 0
                    for nt in range(nin):
                        for c in range(2):
                            last = e.matmul(p[:], lhsT=D_[:, c, nt, :], rhs=G[:, t0 + nt, c, :], start=(n == 0), stop=(n == 2 * nin - 1))
                            n += 1
                    return last
                k.op("tensor", mm2, reads=[rD] + [r_G[t0 + nt] for nt in range(nin)], writes=[rp])
                o_, ro = yo[to % 2], r_yo[to % 2]
                k.op("vector", lambda e, o_=o_, p=p: e.tensor_copy(out=o_[:], in_=p[:]), reads=[rp], writes=[ro])
                k.dma(yd[to * 128:(to + 1) * 128, :], o_[:], reads=[ro], writes=[r_yd])
            k.barrier()

    RWC = 3072

    def stage_rwprep(self, l):
        k = self.k
        z, _ = self.dr["z%d" % l]
        r_zt = self.r_ztile
        rwd, r_rwd = self.dram("rwd%d" % l, [2, T, self.RWC], F32)
        rwo, r_rwo = self.dram("rwo%d" % l, [T, 608], F32)
        self.r_rwd_t = [R() for _ in range(NT)]
        with ExitStack() as es:
            def bc(nm, n):
                tl = k.sb("rp_" + nm, [128, n], F32, es)
                ap, rr = self.dr[nm]
                k.dma(tl[:], ap[l].partition_broadcast(128), reads=[rr], writes=[r_c])
                return tl
            r_c = R()
            mu = bc("rw_mu", 1760)
            kkg = bc("rw_kk", 512)
            kag = bc("rw_ka", 512)
            w0 = bc("rw_w0f", 1024)
            a0 = bc("rw_a0f", 1024)
            w2 = k.sb("rp_w2", [32, 2, 2, 512], F32, es)
            ap, rr = self.dr["rw_wa2"]
            k.dma(w2[:], ap[l], reads=[rr], writes=[r_c])
            cur = [k.sb("rp_cur%d" % i, [128, 1760], F32, es) for i in range(2)]
            prv = [k.sb("rp_prv%d" % i, [128, 1760], F32, es) for i in range(2)]
            nxt = [k.sb("rp_nxt%d" % i, [128, 1760], F32, es) for i in range(2)]
            r_cur = [R() for _ in range(2)]
            r_prv = [R() for _ in range(2)]
            r_nxt = [R() for _ in range(2)]
            mx = k.sb("rp_mx", [128, 1760], F32, es)
            r_mx = R()
            d_ = k.sb("rp_d", [128, 1760], F32, es)
            r_d = R()
            st = k.sb("rp_st", [128, NT, 8], F32, es)
            lo_in = k.sb("rp_loin", [128, 128], F32, es)
            r_loin = R()
            lT = k.sb("rp_lT", [32, 4, 128], F32, es)
            r_lT = R()
            ob = [k.sb("rp_ob%d" % i, [128, 2, self.RWC], F32, es) for i in range(1)]
            r_ob = [[R() for _ in range(8)] for _ in range(1)]
            oo = k.sb("rp_oo", [128, 608], F32, es)
            r_oo = R()
            tw = k.sb("rp_tw", [128, 512], F32, es)
            r_tw = R()
            ta = k.sb("rp_ta", [128, 2, 512], F32, es)
            r_ta = [R(), R()]
            ptr = k.ps("rp_ptr", [128, 512], F32, es)
            r_ptr = PR()
            pw = [k.ps("rp_pw%d" % i, [128, 512], F32, es) for i in range(4)]
            r_pw = [PR() for _ in range(4)]
            for t in range(NT):
                rows = slice(t * 128, (t + 1) * 128)
                c_, rc_ = cur[t % 2], r_cur[t % 2]
                p_, rp_ = prv[t % 2], r_prv[t % 2]
                n_, rn_ = nxt[t % 2], r_nxt[t % 2]
                first = t in (0, 2)
                last_ = t in (1, NT - 1)
                for dst, rdst, off, lo_, hi_ in ((c_, rc_, 0, 0, 128), (p_, rp_, -1, 1 if first else 0, 128), (n_, rn_, 1, 0, 127 if last_ else 128)):
                    if hi_ - lo_ < 128:
                        k.op("gpsimd", lambda e, dst=dst: e.memset(dst[:], 0.0), writes=[rdst])
                    r0 = t * 128 + off
                    zt = [r_zt[t]] + ([r_zt[t - 1]] if off < 0 and t > 0 else []) + ([r_zt[t + 1]] if off > 0 and t < NT - 1 else [])
                    k.dma(dst[lo_:hi_, 0:1152], z[r0 + lo_:r0 + hi_, C_RWKS:C_RWKS + 1152], reads=zt, writes=[rdst])
                    k.dma(dst[lo_:hi_, 1152:1760], z[r0 + lo_:r0 + hi_, C_RWQS:C_RWQS + 608], reads=zt, writes=[rdst])
                k.op("vector", lambda e, p_=p_, n_=n_: e.tensor_tensor(out=d_[:], in0=p_[:], in1=n_[:], op=ALU.add), reads=[rp_, rn_], writes=[r_d])
                k.op("vector", lambda e, c_=c_: e.scalar_tensor_tensor(out=d_[:], in0=d_[:], scalar=0.5, in1=c_[:], op0=ALU.mult, op1=ALU.subtract),
                     reads=[r_d, rc_], writes=[r_d])
                k.op("vector", lambda e: e.tensor_tensor(out=d_[:], in0=d_[:], in1=mu[:], op=ALU.mult), reads=[r_d, r_c], writes=[r_d])
                k.op("vector", lambda e, c_=c_: e.tensor_tensor(out=mx[:], in0=d_[:], in1=c_[:], op=ALU.add), reads=[r_d, rc_], writes=[r_mx])
                o_, ro = ob[0], r_ob[0]
                kx = mx[:, 0:512]
                vx = mx[:, 512:1024]
                rx_ = mx[:, 1152:1664]
                r_s = R()
                kk_o = o_[:, 0, 0:512]
                k.op("vector", lambda e: e.tensor_tensor(out=kk_o, in0=kx, in1=kkg[:], op=ALU.mult), reads=[r_mx, r_c], writes=[ro[0]])
                k.op("vector", lambda e: e.tensor_tensor(out=d_[:, 0:512], in0=kk_o, in1=kk_o, op=ALU.mult), reads=[ro[0]], writes=[r_d])
                k.op("vector", lambda e, t=t: e.tensor_reduce(out=st[:, t, :], in_=d_[:, 0:512].rearrange("p (h d) -> p h d", h=8), axis=AX.X, op=ALU.add),
                     reads=[r_d], writes=[r_s])
                k.op("scalar", lambda e, t=t: e.activation(out=st[:, t, :], in_=st[:, t, :], func=AF.Sqrt), reads=[r_s], writes=[r_s])
                k.op("vector", lambda e, t=t: e.tensor_scalar(out=st[:, t, :], in0=st[:, t, :], scalar1=1e-12, scalar2=None, op0=ALU.max), reads=[r_s], writes=[r_s])
                k.op("vector", lambda e, t=t: e.reciprocal(out=st[:, t, :], in_=st[:, t, :]), reads=[r_s], writes=[r_s])
                k.op("vector", lambda e, t=t: e.tensor_tensor(out=kk_o.rearrange("p (h d) -> p h d", h=8), in0=kk_o.rearrange("p (h d) -> p h d", h=8),
                                                             in1=st[:, t, :].unsqueeze(2).to_broadcast([128, 8, 64]), op=ALU.mult),
                     reads=[ro[0], r_s], writes=[ro[0]])
                k.op("gpsimd", lambda e: e.tensor_copy(out=o_[:, 1, 0:512], in_=kk_o), reads=[ro[0]], writes=[ro[1]])
                k.op("gpsimd", lambda e: e.tensor_copy(out=o_[:, :, 2048:2560], in_=rx_.unsqueeze(1).to_broadcast([128, 2, 512])), reads=[r_mx], writes=[ro[2]])
                k.op("gpsimd", lambda e: e.tensor_copy(out=o_[:, :, 2560:3072], in_=vx.unsqueeze(1).to_broadcast([128, 2, 512])), reads=[r_mx], writes=[ro[3]])
                k.op("scalar", lambda e: e.activation(out=lo_in[:, 0:64], in_=mx[:, 1024:1088], func=AF.Tanh), reads=[r_mx], writes=[r_loin])
                k.op("vector", lambda e: e.tensor_copy(out=lo_in[:, 64:128], in_=mx[:, 1088:1152]), reads=[r_mx, r_loin], writes=[r_loin])

                def trl(e):
                    last = None
                    for j in range(4):
                        last = e.transpose(out=ptr[0:32, j * 128:(j + 1) * 128], in_=lo_in[:, j * 32:(j + 1) * 32], identity=self.identf[:])
                    return last
                k.op("tensor", trl, reads=[r_loin, self.r_ident], writes=[r_ptr])
                k.op("vector", lambda e: e.tensor_copy(out=lT[:], in_=ptr[0:32, :].rearrange("p (a b) -> p a b", a=4)), reads=[r_ptr], writes=[r_lT])
                for d in range(2):
                    p, rp = pw[d], r_pw[d]
                    k.op("tensor", lambda e, p=p, d=d: e.matmul(p[:], lhsT=lT[:, d, :], rhs=w2[:, 0, d, :], start=True, stop=True), reads=[r_lT, r_c], writes=[rp])
                    k.op("vector", lambda e, p=p, d=d: e.tensor_tensor(out=tw[:], in0=p[:], in1=w0[:, d * 512:(d + 1) * 512], op=ALU.add), reads=[rp, r_c], writes=[r_tw])
                    k.op("scalar", lambda e: e.activation(out=tw[:], in_=tw[:], func=AF.Sigmoid), reads=[r_tw], writes=[r_tw])
                    k.op("vector", lambda e, d=d: e.tensor_scalar(out=o_[:, d, 512:1024], in0=tw[:], scalar1=-float(np.exp(-0.5)), scalar2=None, op0=ALU.mult), reads=[r_tw], writes=[ro[4 + d]])
                    p, rp = pw[2 + d], r_pw[2 + d]
                    k.op("tensor", lambda e, p=p, d=d: e.matmul(p[:], lhsT=lT[:, 2 + d, :], rhs=w2[:, 1, d, :], start=True, stop=True), reads=[r_lT, r_c], writes=[rp])
                    k.op("vector", lambda e, p=p, d=d: e.tensor_tensor(out=ta[:, d, :], in0=p[:], in1=a0[:, d * 512:(d + 1) * 512], op=ALU.add), reads=[rp, r_c], writes=[r_ta[d]])
                    k.op("scalar", lambda e, d=d: e.activation(out=ta[:, d, :], in_=ta[:, d, :], func=AF.Sigmoid), reads=[r_ta[d]], writes=[r_ta[d]])
                    k.op("gpsimd", lambda e, d=d: e.tensor_tensor(out=o_[:, d, 1024:1536], in0=kk_o, in1=ta[:, d, :], op=ALU.mult), reads=[ro[0], r_ta[d]], writes=[ro[6 + d]])
                    k.op("vector", lambda e, d=d: e.scalar_tensor_tensor(out=ta[:, d, :], in0=ta[:, d, :], scalar=-1.0, in1=kag[:], op0=ALU.add, op1=ALU.mult),
                         reads=[r_ta[d], ro[6 + d], r_c], writes=[r_ta[d]])
                    k.op("vector", lambda e, d=d: e.scalar_tensor_tensor(out=o_[:, d, 1536:2048], in0=ta[:, d, :], scalar=1.0, in1=kx, op0=ALU.add, op1=ALU.mult),
                         reads=[r_ta[d], r_mx], writes=[ro[6 + d]])
                k.op("vector", lambda e: e.tensor_tensor(out=oo[:, 0:512], in0=o_[:, 0, 1536:2048], in1=o_[:, 1, 1536:2048], op=ALU.add), reads=[ro[6], ro[7]], writes=[r_oo])
                k.op("gpsimd", lambda e: e.tensor_copy(out=oo[:, 512:608], in_=mx[:, 1664:1760]), reads=[r_mx, r_oo], writes=[r_oo])
                for d in range(2):
                    k.dma(rwd[d, rows, :], o_[:, d, :], reads=ro, writes=[self.r_rwd_t[t]])
                k.dma(rwo[rows, :], oo[:], reads=[r_oo], writes=[r_rwo])
            k.barrier()

    def stage_rwscan_seq(self, l):
        k = self.k
        rwd, _ = self.dr["rwd%d" % l]
        r_rwd_t = self.r_rwd_t
        yfb, r_yfb = self.dram("rwy%d" % l, [2, T, 512], F32)
        self.r_rwy_t = [R() for _ in range(NT)]
        e2d, r_e2d = self.dr["rw_e2"]
        pmd, r_pmd = self.dr["rw_pm"]
        j64d, r_j64d = self.dr["rw_j64"]
        with ExitStack() as es:
            E2 = k.sb("rs_E2", [128, 64, 128], F32, es)
            Pm = k.sb("rs_Pm", [128, 128], F32, es)
            J64 = k.sb("rs_J64", [64, 64], F32, es)
            r_cst = R()
            for c in range(4):
                k.dma(E2[:, c * 16:(c + 1) * 16, :], e2d[:, c * 16:(c + 1) * 16, :], reads=[r_e2d], writes=[r_cst])
            k.dma(Pm[:], pmd, reads=[r_pmd], writes=[r_cst])
            k.dma(J64[:], j64d, reads=[r_j64d], writes=[r_cst])
            S = k.sb("rs_S", [128, 512], F32, es)
            r_S = R()
            k.op("vector", lambda e: e.memset(S[:], 0.0), writes=[r_S])
            X = [k.sb("rs_X%d" % i, [128, self.RWC], F32, es) for i in range(2)]
            r_X = [R() for _ in range(2)]
            Vr = k.sb("rs_Vr", [128, 8, 2, 64], F32, es)
            r_Vr = R()
            vT = [k.sb("rs_vT%d" % i, [128, 8, 64], F32, es) for i in range(2)]
            r_vT = [R() for _ in range(2)]
            ys = [k.sb("rs_ys%d" % i, [128, 8, 64], F32, es) for i in range(2)]
            r_ys = [R() for _ in range(2)]
            yo = k.sb("rs_yo", [64, 8, 2, 64], F32, es)
            r_yo = R()
            yb2 = k.sb("rs_yb2", [64, 512], F32, es)
            r_yb2 = R()
            yb3 = k.sb("rs_yb3", [64, 512], F32, es)
            r_yb3 = R()
            t1 = k.sb("rs_t1", [128, 512], F32, es)
            t2 = k.sb("rs_t2", [128, 512], F32, es)
            t3 = k.sb("rs_t3", [128, 512], F32, es)
            r_t1, r_t2, r_t3 = R(), R(), R()
            sa = k.sb("rs_sa", [128, 8], F32, es)
            r_sa = R()
            bcp = [k.ps("rs_bc%d" % i, [128, 512], F32, es) for i in range(5)]
            r_bcp = [PR() for _ in range(5)]
            pmisc = k.ps("rs_pm", [128, 8, 128], F32, es)
            r_pmisc = PR()
            pj = k.ps("rs_pj", [128, 512], F32, es)
            r_pj = PR()
            S3 = S[:].rearrange("p (h d) -> p h d", h=8)
            blk = 0
            nblk_dbg = self.dbg.get("RW_NBLK", 10 ** 9)
            for seg0, seglen in ((0, NCTX), (NCTX, L)):
                nb = seglen // 64
                for j in range(nb):
                    if blk >= nblk_dbg:
                        break
                    X_, rX = X[blk % 2], r_X[blk % 2]
                    vT_, rvT = vT[blk % 2], r_vT[blk % 2]
                    ys_, rys = ys[blk % 2], r_ys[blk % 2]
                    blk += 1
                    f0 = seg0 + 64 * j
                    b0 = seg0 + seglen - 64 * (j + 1)
                    k.dma(X_[0:64, :], rwd[0, f0:f0 + 64, :], reads=[r_rwd_t[f0 // 128]], writes=[rX])
                    k.dma(X_[64:128, :], rwd[1, b0:b0 + 64, :], reads=[r_rwd_t[b0 // 128]], writes=[rX])
                    k.op("gpsimd", lambda e, X_=X_: e.tensor_copy(out=Vr[:], in_=X_[:, 2560:3072].rearrange("p (h d) -> p h d", h=8).unsqueeze(2).to_broadcast([128, 8, 2, 64])),
                         reads=[rX], writes=[r_Vr])

                    def vtr(e):
                        last = None
                        for h in range(8):
                            last = e.matmul(pmisc[:, h, :], lhsT=Vr[:, h, :, :].rearrange("p a b -> p (a b)"), rhs=Pm[:], start=True, stop=True)
                        return last
                    k.op("tensor", vtr, reads=[r_Vr, r_cst], writes=[r_pmisc])
                    k.op("vector", lambda e, vT_=vT_: e.tensor_copy(out=vT_[0:64, :, :], in_=pmisc[0:64, :, 0:64]), reads=[r_pmisc], writes=[rvT])
                    k.op("vector", lambda e, vT_=vT_: e.tensor_copy(out=vT_[64:128, :, :], in_=pmisc[64:128, :, 64:128]), reads=[r_pmisc, rvT], writes=[rvT])
                    for i in range(64):
                        for q in range(5):
                            k.op("tensor", lambda e, q=q, i=i, X_=X_: e.matmul(bcp[q][:], lhsT=E2[:, i, :], rhs=X_[:, q * 512:(q + 1) * 512], start=True, stop=True),
                                 reads=[rX, r_cst], writes=[r_bcp[q]])
                        bkk, bdec, bb, bkd, br = [bcp[q][:].rearrange("p (h d) -> p h d", h=8) for q in range(5)]
                        k.op("vector", lambda e, bkk=bkk: e.tensor_tensor(out=t1[:].rearrange("p (h d) -> p h d", h=8), in0=S3, in1=bkk, op=ALU.mult),
                             reads=[r_S, r_bcp[0]], writes=[r_t1])
                        k.op("vector", lambda e: e.tensor_reduce(out=sa[:], in_=t1[:].rearrange("p (h d) -> p h d", h=8), axis=AX.X, op=ALU.add),
                             reads=[r_t1], writes=[r_sa])
                        k.op("vector", lambda e, bdec=bdec: e.tensor_tensor(out=S3, in0=S3, in1=bdec, op=ALU.mult), reads=[r_S, r_bcp[1]], writes=[r_S])
                        k.op("vector", lambda e, bb=bb: e.tensor_tensor(out=t2[:].rearrange("p (h d) -> p h d", h=8), in0=bb,
                                                                       in1=sa[:].unsqueeze(2).to_broadcast([128, 8, 64]), op=ALU.mult),
                             reads=[r_bcp[2], r_sa], writes=[r_t2])
                        k.op("vector", lambda e: e.tensor_tensor(out=S[:], in0=S[:], in1=t2[:], op=ALU.subtract), reads=[r_S, r_t2], writes=[r_S])
                        k.op("vector", lambda e, bkd=bkd, vT_=vT_, i=i: e.tensor_tensor(out=t3[:].rearrange("p (h d) -> p h d", h=8), in0=bkd,
                                                                                       in1=vT_[:, :, i:i + 1].to_broadcast([128, 8, 64]), op=ALU.mult),
                             reads=[r_bcp[3], rvT], writes=[r_t3])
                        k.op("vector", lambda e: e.tensor_tensor(out=S[:], in0=S[:], in1=t3[:], op=ALU.add), reads=[r_S, r_t3], writes=[r_S])
                        k.op("vector", lambda e, br=br: e.tensor_tensor(out=t1[:].rearrange("p (h d) -> p h d", h=8), in0=S3, in1=br, op=ALU.mult),
                             reads=[r_S, r_bcp[4]], writes=[r_t1])
                        k.op("vector", lambda e, ys_=ys_, i=i: e.tensor_reduce(out=ys_[:, :, i:i + 1].rearrange("p h o -> p (h o)"), in_=t1[:].rearrange("p (h d) -> p h d", h=8), axis=AX.X, op=ALU.add),
                             reads=[r_t1], writes=[rys])
                    def ytr(e, ys_=ys_):
                        last = None
                        for h in range(8):
                            last = e.matmul(pmisc[0:64, h, :], lhsT=ys_[:, h, :], rhs=self.identf[:], start=True, stop=True)
                        return last
                    k.op("tensor", ytr, reads=[rys, self.r_ident], writes=[r_pmisc])
                    k.op("vector", lambda e: e.tensor_copy(out=yo[:].rearrange("p h a d -> p h (a d)"), in_=pmisc[0:64, :, :]), reads=[r_pmisc], writes=[r_yo])
                    k.op("gpsimd", lambda e: e.tensor_copy(out=yb2[:].rearrange("p (h d) -> p h d", h=8), in_=yo[:, :, 1, :]), reads=[r_yo], writes=[r_yb2])
                    k.op("tensor", lambda e: e.matmul(pj[0:64, :], lhsT=J64[:], rhs=yb2[:], start=True, stop=True), reads=[r_yb2, r_cst], writes=[r_pj])
                    k.op("scalar", lambda e: e.activation(out=yb3[:], in_=pj[0:64, :], func=AF.Copy), reads=[r_pj], writes=[r_yb3])
                    k.dma(yfb[0, f0:f0 + 64, :].rearrange("p (h d) -> p h d", h=8), yo[:, :, 0, :], reads=[r_yo], writes=[self.r_rwy_t[f0 // 128]])
                    k.dma(yfb[1, b0:b0 + 64, :], yb3[:], reads=[r_yb3], writes=[self.r_rwy_t[b0 // 128]])
            k.barrier()

    def stage_rwscan(self, l):
        k = self.k
        rwd, _ = self.dr["rwd%d" % l]
        r_rwd_t = self.r_rwd_t
        yfb, r_yfb = self.dram("rwy%d" % l, [2, T, 512], F32)
        self.r_rwy_t = [R() for _ in range(NT)]
        mkd, r_mkd = self.dr["rw_masks"]
        dmd, r_dmd = self.dr["rw_dm"]
        with ExitStack() as es:
            MK = k.sb("rc_MK", [128, 6, 128], F32, es)
            DM = k.sb("rc_DM", [128, 2, 1024], F32, es)
            ones = k.sb("rc_ones", [128, 64], F32, es)
            r_cst = R()
            k.dma(MK[:], mkd, reads=[r_mkd], writes=[r_cst])
            k.dma(DM[:], dmd, reads=[r_dmd], writes=[r_cst])
            k.op("vector", lambda e: e.memset(ones[:], 1.0), writes=[r_cst])
            MI, BO, MS, nMS, nMST, nMI = [MK[:, i, :] for i in range(6)]
            S = k.sb("rc_S", [128, 512], F32R, es)
            r_S = R()
            X = [k.sb("rc_X%d" % i, [128, self.RWC], F32, es) for i in range(2)]
            r_X = [R() for _ in range(2)]
            fac = [k.sb("rc_fac%d" % i, [128, 512], F32, es) for i in range(4)]
            r_fac = [R() for _ in range(4)]
            tl = k.sb("rc_tl", [128, 2, 512], F32, es)
            r_tl = [R(), R()]
            k.op("vector", lambda e: e.memset(tl[:, 0, :], 0.0), writes=[r_tl[0]])
            k.op("vector", lambda e: e.tensor_copy(out=S[:], in_=tl[:, 0, :]), reads=[r_tl[0]], writes=[r_S])
            pl = [k.sb("rc_pl%d" % i, [128, 512], F32, es) for i in range(6)]
            r_pl = [R() for _ in range(6)]
            bd03 = [k.sb("rc_bd%d" % i, [128, 8, 128], F32R, es) for i in range(4)]
            r_bd03 = [R() for _ in range(4)]
            PB = []
            for par in range(2):
                d = {}
                d["bd4"] = k.sb("rc_bd4_%d" % par, [128, 8, 128], F32R, es)
                d["bd5"] = k.sb("rc_bd5_%d" % par, [128, 8, 128], F32R, es)
                d["bd6"] = k.sb("rc_bd6_%d" % par, [128, 8, 128], F32, es)
                d["r_bd"] = [R(), R(), R()]
                d["xT"] = [k.sb("rc_xT%d_%d" % (i, par), [128, 8, 128], F32R, es) for i in range(4)]
                d["r_xT"] = [[R(), R()] for _ in range(4)]
                for nm in ("AT", "ApT", "nBpT", "NT0", "M0"):
                    d[nm] = k.sb("rc_%s_%d" % (nm, par), [128, 8, 128], F32R, es)
                    d["r_" + nm] = [R(), R()]
                d["Vc"] = k.sb("rc_Vc%d" % par, [128, 512], F32R, es)
                d["r_Vc"] = R()
                PB.append(d)
            NTb = [k.sb("rc_NT%d" % i, [128, 8, 128], F32R, es) for i in range(2)]
            Mb = [k.sb("rc_M%d" % i, [128, 8, 128], F32R, es) for i in range(2)]
            r_NT = [[R(), R()] for _ in range(2)]
            r_M = [[R(), R()] for _ in range(2)]
            G = k.sb("rc_G", [128, 512], F32R, es)
            r_G = R()
            Pend = k.sb("rc_Pend", [128, 512], F32, es)
            r_Pend = R()
            Yo = [k.sb("rc_Yo%d" % i, [128, 512], F32, es) for i in range(2)]
            r_Yo = [R() for _ in range(2)]
            pb = [k.ps("rc_pb%d" % i, [128, 512], F32, es) for i in range(8)]
            r_pb = [PR() for _ in range(8)]
            ipb = [0]

            def bank():
                i = ipb[0] % 8
                ipb[0] += 1
                return pb[i], r_pb[i]

            chunks = []
            for seg0, seglen in ((0, NCTX), (NCTX, L)):
                for j in range(seglen // 64):
                    chunks.append((seg0 + 64 * j, seg0 + seglen - 64 * (j + 1)))
            chunks = chunks[:self.dbg.get("RW_NBLK", 10 ** 9)]

            def prologue_steps(ci):
                f0, b0 = chunks[ci]
                par = ci % 2
                d = PB[par]
                X_, rX = X[par], r_X[par]
                kk_, lw_, b_, kd_, r_ = [X_[:, q * 512:(q + 1) * 512] for q in range(5)]
                steps = []

                def s_load():
                    k.dma(X_[0:64, :], rwd[0, f0:f0 + 64, :], reads=[r_rwd_t[f0 // 128]], writes=[rX])
                    k.dma(X_[64:128, :], rwd[1, b0:b0 + 64, :], reads=[r_rwd_t[b0 // 128]], writes=[rX])
                    k.op("gpsimd", lambda e: e.tensor_copy(out=d["Vc"][:], in_=X_[:, 2560:3072]), reads=[rX], writes=[d["r_Vc"]])
                steps.append(s_load)

                def s_cum():
                    pLi, rLi = bank()
                    pLt, rLt = bank()
                    k.op("tensor", lambda e: e.matmul(pLi[:], lhsT=MI, rhs=lw_, start=True, stop=True), reads=[rX, r_cst], writes=[rLi])
                    k.op("tensor", lambda e: e.matmul(pLt[:], lhsT=BO, rhs=lw_, start=True, stop=True), reads=[rX, r_cst], writes=[rLt])
                    k.op("scalar", lambda e: e.activation(out=fac[0][:], in_=pLi[:], func=AF.Exp), reads=[rLi], writes=[r_fac[0]])
                    k.op("scalar", lambda e: e.activation(out=fac[1][:], in_=pLi[:], func=AF.Exp, scale=-1.0), reads=[rLi], writes=[r_fac[1]])
                    k.op("vector", lambda e: e.tensor_tensor(out=tl[:, 0, :], in0=pLi[:], in1=lw_, op=ALU.subtract), reads=[rLi, rX], writes=[r_tl[0]])
                    k.op("scalar", lambda e: e.activation(out=fac[2][:], in_=tl[:, 0, :], func=AF.Exp), reads=[r_tl[0]], writes=[r_fac[2]])
                    k.op("vector", lambda e: e.tensor_copy(out=tl[:, 1, :], in_=pLt[:]), reads=[rLt], writes=[r_tl[1]])
                    k.op("vector", lambda e: e.tensor_tensor(out=tl[:, 1, :], in0=tl[:, 1, :], in1=pLi[:], op=ALU.subtract), reads=[rLi, r_tl[1]], writes=[r_tl[1]])
                    k.op("scalar", lambda e: e.activation(out=fac[3][:], in_=tl[:, 1, :], func=AF.Exp), reads=[r_tl[1]], writes=[r_fac[3]])
                steps.append(s_cum)

                def s_prod():
                    for i, (src, fi) in enumerate(((kk_, 2), (r_, 0), (kd_, 1), (b_, 1), (kd_, 3), (b_, 3))):
                        k.op("vector" if i < 4 else "gpsimd", lambda e, i=i, src=src, fi=fi: e.tensor_tensor(out=pl[i][:], in0=src, in1=fac[fi][:], op=ALU.mult),
                             reads=[rX, r_fac[fi]], writes=[r_pl[i]])
                    for i in range(7):
                        src = pl[i][:] if i < 6 else lw_
                        dm = DM[:, 1 if i == 5 else 0, :]
                        rs = [r_pl[i]] if i < 6 else [rX]
                        if i < 4:
                            dst, rd = bd03[i], r_bd03[i]
                        else:
                            dst, rd = d["bd%d" % i], d["r_bd"][i - 4]
                        k.op("vector" if i < 4 else "gpsimd", lambda e, src=src, dm=dm, dst=dst: e.tensor_tensor(
                            out=dst[:].rearrange("p h (a d) -> p h a d", a=2), in0=src.rearrange("p (h d) -> p h d", h=8).unsqueeze(2).to_broadcast([128, 8, 2, 64]),
                            in1=dm.rearrange("p (h a d) -> p h a d", h=8, a=2), op=ALU.mult), reads=rs + [r_cst], writes=[rd])
                steps.append(s_prod)
                for qi in range(4):
                    def s_tr(qi=qi):
                        for hf in range(2):
                            p_, rp_ = bank()

                            def tr(e, p_=p_, hf=hf):
                                last = None
                                for hh in range(4):
                                    last = e.transpose(out=p_[:, hh * 128:(hh + 1) * 128], in_=bd03[qi][:, hf * 4 + hh, :].bitcast(F32), identity=self.identf[:])
                                return last
                            k.op("tensor", tr, reads=[r_bd03[qi], self.r_ident], writes=[rp_])
                            k.op("scalar", lambda e, p_=p_, hf=hf: e.activation(out=d["xT"][qi][:, hf * 4:(hf + 1) * 4, :].rearrange("p a b -> p (a b)"), in_=p_[:], func=AF.Copy),
                                 reads=[rp_], writes=[d["r_xT"][qi][hf]])
                    steps.append(s_tr)
                qT, rT, kT, bT = d["xT"]
                specs = ((bT, 3, qT, 0, nMS, "NT0"), (qT, 0, bT, 3, nMST, "M0"), (kT, 2, qT, 0, MS, "AT"), (kT, 2, rT, 1, MI, "ApT"), (bT, 3, rT, 1, nMI, "nBpT"))
                for (lt, li, rt, ri, mk, nm) in specs:
                    def s_in(lt=lt, li=li, rt=rt, ri=ri, mk=mk, nm=nm):
                        for hf in range(2):
                            p_, rp_ = bank()

                            def mm(e, p_=p_, hf=hf):
                                last = None
                                for hh in range(4):
                                    h = hf * 4 + hh
                                    last = e.matmul(p_[:, hh * 128:(hh + 1) * 128], lhsT=lt[:, h, :], rhs=rt[:, h, :], start=True, stop=True)
                                return last
                            k.op("tensor", mm, reads=[d["r_xT"][li][hf], d["r_xT"][ri][hf]], writes=[rp_])
                            k.op("gpsimd" if False else "vector", lambda e, p_=p_, hf=hf: e.tensor_tensor(
                                out=d[nm][:, hf * 4:(hf + 1) * 4, :], in0=p_[:].rearrange("p (a b) -> p a b", a=4), in1=mk.unsqueeze(1).to_broadcast([128, 4, 128]), op=ALU.mult),
                                reads=[rp_, r_cst], writes=[d["r_" + nm][hf]])
                    steps.append(s_in)
                return steps

            def solve(ci, fillers):
                f0, b0 = chunks[ci]
                par = ci % 2
                d = PB[par]
                Yo_, rYo = Yo[par], r_Yo[par]
                qT, rT, kT, bT = d["xT"]
                Vc_, rVc = d["Vc"], d["r_Vc"]

                def Vh(h):
                    return Vc_[:, h * 64:(h + 1) * 64]

                def fill(n):
                    for _ in range(n):
                        if fillers:
                            fillers.pop(0)()
                pG, rG = bank()

                def mmG(e):
                    last = None
                    for h in range(8):
                        e.matmul(pG[:, h * 64:(h + 1) * 64], lhsT=qT[:, h, :], rhs=S[:, h * 64:(h + 1) * 64], start=True, stop=False)
                        last = e.matmul(pG[:, h * 64:(h + 1) * 64], lhsT=d["AT"][:, h, :], rhs=Vh(h), start=False, stop=True)
                    return last
                k.op("tensor", mmG, reads=d["r_xT"][0] + d["r_AT"] + [r_S, rVc], writes=[rG])
                k.op("vector", lambda e: e.tensor_copy(out=G[:], in_=pG[:]), reads=[rG], writes=[r_G])
                fill(2)
                curT, curM, rcT, rcM = d["NT0"], d["M0"], d["r_NT0"], d["r_M0"]
                for lev in range(6):
                    pU, rU = bank()

                    def mmU(e, pU=pU, NTc=curT):
                        last = None
                        for h in range(8):
                            last = e.matmul(pU[:, h * 64:(h + 1) * 64], lhsT=NTc[:, h, :], rhs=G[:, h * 64:(h + 1) * 64], start=True, stop=True)
                        return last
                    k.op("tensor", mmU, reads=rcT + [r_G], writes=[rU])
                    if lev < 5:
                        nx = lev % 2
                        for (lt, rt, dst, rdst, rl, rr_) in ((curM, curT, NTb[nx], r_NT[nx], rcM, rcT), (curT, curM, Mb[nx], r_M[nx], rcT, rcM)):
                            for hf in range(2):
                                p_, rp_ = bank()

                                def mm2(e, p_=p_, lt=lt, rt=rt, hf=hf):
                                    last = None
                                    for hh in range(4):
                                        h = hf * 4 + hh
                                        last = e.matmul(p_[:, hh * 128:(hh + 1) * 128], lhsT=lt[:, h, :], rhs=rt[:, h, :], start=True, stop=True)
                                    return last
                                k.op("tensor", mm2, reads=[rl[hf], rr_[hf]], writes=[rp_])
                                k.op("scalar", lambda e, p_=p_, dst=dst, hf=hf: e.activation(out=dst[:, hf * 4:(hf + 1) * 4, :].rearrange("p a b -> p (a b)"), in_=p_[:], func=AF.Copy),
                                     reads=[rp_], writes=[rdst[hf]])
                        curT, curM, rcT, rcM = NTb[nx], Mb[nx], r_NT[nx], r_M[nx]
                    k.op("vector", lambda e, pU=pU: e.tensor_tensor(out=G[:], in0=G[:].bitcast(F32), in1=pU[:], op=ALU.add), reads=[rU, r_G], writes=[r_G])
                    fill(2)
                pY, rY = bank()

                def mmY(e):
                    last = None
                    for h in range(8):
                        hs = slice(h * 64, (h + 1) * 64)
                        e.matmul(pY[:, hs], lhsT=rT[:, h, :], rhs=S[:, hs], start=True, stop=False)
                        e.matmul(pY[:, hs], lhsT=d["ApT"][:, h, :], rhs=Vh(h), start=False, stop=False)
                        last = e.matmul(pY[:, hs], lhsT=d["nBpT"][:, h, :], rhs=G[:, hs], start=False, stop=True)
                    return last
                k.op("tensor", mmY, reads=d["r_xT"][1] + d["r_ApT"] + d["r_nBpT"] + [r_S, rVc, r_G], writes=[rY])
                k.op("scalar", lambda e: e.activation(out=Yo_[:], in_=pY[:], func=AF.Copy), reads=[rY], writes=[rYo])
                k.dma(yfb[0, f0:f0 + 64, :], Yo_[0:64, :], reads=[rYo], writes=[self.r_rwy_t[f0 // 128]])
                k.dma(yfb[1, b0:b0 + 64, :], Yo_[64:128, :], reads=[rYo], writes=[self.r_rwy_t[b0 // 128]])
                pL, rL = bank()

                def mmL(e):
                    last = None
                    for h in range(8):
                        last = e.matmul(pL[:, h * 64:(h + 1) * 64], lhsT=d["bd6"][:, h, :], rhs=ones[:], start=True, stop=True)
                    return last
                k.op("tensor", mmL, reads=[d["r_bd"][2], r_cst], writes=[rL])
                k.op("scalar", lambda e: e.activation(out=Pend[:], in_=pL[:], func=AF.Exp), reads=[rL], writes=[r_Pend])
                pS, rS = bank()

                def mmS(e):
                    last = None
                    for h in range(8):
                        hs = slice(h * 64, (h + 1) * 64)
                        e.matmul(pS[:, hs], lhsT=d["bd4"][:, h, :], rhs=Vh(h), start=True, stop=False)
                        last = e.matmul(pS[:, hs], lhsT=d["bd5"][:, h, :], rhs=G[:, hs], start=False, stop=True)
                    return last
                k.op("tensor", mmS, reads=[d["r_bd"][0], d["r_bd"][1], rVc, r_G], writes=[rS])
                k.op("vector", lambda e: e.tensor_tensor(out=S[:], in0=S[:].bitcast(F32), in1=Pend[:], op=ALU.mult), reads=[r_S, r_Pend], writes=[r_S])
                k.op("vector", lambda e: e.tensor_tensor(out=S[:], in0=S[:].bitcast(F32), in1=pS[:], op=ALU.add), reads=[r_S, rS], writes=[r_S])
                while fillers:
                    fillers.pop(0)()

            if chunks:
                for st_ in prologue_steps(0):
                    st_()
                for ci in range(len(chunks)):
                    fillers = prologue_steps(ci + 1) if ci + 1 < len(chunks) else []
                    solve(ci, fillers)
            k.barrier()

    def stage_rwout(self, l):
        k = self.k
        rwd, _ = self.dr["rwd%d" % l]
        rwo, r_rwo = self.dr["rwo%d" % l]
        yfb, _ = self.dr["rwy%d" % l]
        yd, r_yd = self.dram("y_rw%d" % l, [T, 512], F32)
        with ExitStack() as es:
            r_c = R()

            def bc(nm, n):
                tl = k.sb("ro_" + nm, [128, n], F32, es)
                ap, rr = self.dr[nm]
                k.dma(tl[:], ap[l].partition_broadcast(128), reads=[rr], writes=[r_c])
                return tl
            lnw = bc("rw_lnw", 512)
            lnb = bc("rw_lnb", 512)
            rkg = bc("rw_rk", 512)
            g2 = k.sb("ro_g2", [96, 512], F32, es)
            ap, rr = self.dr["rw_g2"]
            k.dma(g2[:], ap[l], reads=[rr], writes=[r_c])
            gne = k.sb("ro_gne", [128, 1], F32, es)
            k.op("vector", lambda e: e.memset(gne[:], 64e-5), writes=[r_c])
            yf = [k.sb("ro_yf%d" % i, [128, 512], F32, es) for i in range(2)]
            yb = [k.sb("ro_yb%d" % i, [128, 512], F32, es) for i in range(2)]
            rv = [k.sb("ro_rv%d" % i, [128, 1024], F32, es) for i in range(2)]
            kg = [k.sb("ro_kg%d" % i, [128, 608], F32, es) for i in range(2)]
            r_in = [[R() for _ in range(4)] for _ in range(2)]
            y = k.sb("ro_y", [128, 512], F32, es)
            r_y = R()
            sq = k.sb("ro_sq", [128, 512], F32, es)
            r_sq = R()
            st = k.sb("ro_st", [128, NT, 3, 8], F32, es)
            sg = k.sb("ro_sg", [128, 128], F32, es)
            r_sg = R()
            gT = k.sb("ro_gT", [96, 128], F32, es)
            r_gT = R()
            out = [k.sb("ro_out%d" % i, [128, 512], F32, es) for i in range(2)]
            r_out = [R() for _ in range(2)]
            ptr = k.ps("ro_ptr", [128, 512], F32, es)
            r_ptr = PR()
            pg = k.ps("ro_pg", [128, 512], F32, es)
            r_pg = PR()
            k.op("vector", lambda e: e.memset(sg[:], 0.0), writes=[r_sg])
            for t in range(NT):
                rows = slice(t * 128, (t + 1) * 128)
                i2 = t % 2
                ri = r_in[i2]
                k.dma(yf[i2][:], yfb[0, rows, :], reads=[self.r_rwy_t[t]], writes=[ri[0]])
                k.dma(yb[i2][:], yfb[1, rows, :], reads=[self.r_rwy_t[t]], writes=[ri[1]])
                k.dma(rv[i2][:], rwd[0, rows, 2048:3072], reads=[self.r_rwd_t[t]], writes=[ri[2]])
                k.dma(kg[i2][:], rwo[rows, :], reads=[r_rwo], writes=[ri[3]])
                r_s = R()
                y3 = y[:].rearrange("p (h d) -> p h d", h=8)
                k.op("vector", lambda e, i2=i2: e.tensor_tensor(out=y[:], in0=yf[i2][:], in1=yb[i2][:], op=ALU.add), reads=[ri[0], ri[1]], writes=[r_y])
                k.op("vector", lambda e, t=t: e.tensor_reduce(out=st[:, t, 0, :], in_=y3, axis=AX.X, op=ALU.add), reads=[r_y], writes=[r_s])
                k.op("vector", lambda e, t=t: e.tensor_scalar(out=st[:, t, 0, :], in0=st[:, t, 0, :], scalar1=1.0 / 64, scalar2=None, op0=ALU.mult), reads=[r_s], writes=[r_s])
                k.op("vector", lambda e, t=t: e.tensor_tensor(out=y3, in0=y3, in1=st[:, t, 0, :].unsqueeze(2).to_broadcast([128, 8, 64]), op=ALU.subtract),
                     reads=[r_y, r_s], writes=[r_y])
                k.op("gpsimd", lambda e: e.tensor_tensor(out=sq[:], in0=y[:], in1=y[:], op=ALU.mult), reads=[r_y], writes=[r_sq])
                k.op("vector", lambda e, t=t: e.tensor_reduce(out=st[:, t, 1, :], in_=sq[:].rearrange("p (h d) -> p h d", h=8), axis=AX.X, op=ALU.add), reads=[r_sq], writes=[r_s])
                k.op("scalar", lambda e, t=t: e.activation(out=st[:, t, 1, :], in_=st[:, t, 1, :], func=AF.Sqrt, bias=gne[:, 0:1], scale=1.0 / 64), reads=[r_s, r_c], writes=[r_s])
                k.op("vector", lambda e, t=t: e.reciprocal(out=st[:, t, 1, :], in_=st[:, t, 1, :]), reads=[r_s], writes=[r_s])
                k.op("vector", lambda e, t=t: e.tensor_tensor(out=y3, in0=y3, in1=st[:, t, 1, :].unsqueeze(2).to_broadcast([128, 8, 64]), op=ALU.mult),
                     reads=[r_y, r_s], writes=[r_y])
                k.op("gpsimd", lambda e: e.tensor_tensor(out=y[:], in0=y[:], in1=lnw[:], op=ALU.mult), reads=[r_y, r_c], writes=[r_y])
                k.op("gpsimd", lambda e: e.tensor_tensor(out=y[:], in0=y[:], in1=lnb[:], op=ALU.add), reads=[r_y, r_c], writes=[r_y])
                k.op("vector", lambda e, i2=i2: e.tensor_tensor(out=sq[:], in0=rv[i2][:, 0:512], in1=kg[i2][:, 0:512], op=ALU.mult), reads=[ri[2], ri[3], r_sq], writes=[r_sq])
                k.op("vector", lambda e: e.tensor_tensor(out=sq[:], in0=sq[:], in1=rkg[:], op=ALU.mult), reads=[r_sq, r_c], writes=[r_sq])
                k.op("vector", lambda e, t=t: e.tensor_reduce(out=st[:, t, 2, :], in_=sq[:].rearrange("p (h d) -> p h d", h=8), axis=AX.X, op=ALU.add), reads=[r_sq], writes=[r_s])
                k.op("vector", lambda e, t=t, i2=i2: e.tensor_tensor(out=sq[:].rearrange("p (h d) -> p h d", h=8), in0=rv[i2][:, 512:1024].rearrange("p (h d) -> p h d", h=8),
                                                                    in1=st[:, t, 2, :].unsqueeze(2).to_broadcast([128, 8, 64]), op=ALU.mult),
                     reads=[ri[2], r_s, r_sq], writes=[r_sq])
                k.op("vector", lambda e: e.tensor_tensor(out=y[:], in0=y[:], in1=sq[:], op=ALU.add), reads=[r_y, r_sq], writes=[r_y])
                k.op("scalar", lambda e, i2=i2: e.activation(out=sg[:, 0:96], in_=kg[i2][:, 512:608], func=AF.Sigmoid), reads=[ri[3]], writes=[r_sg])
                k.op("tensor", lambda e: e.transpose(out=ptr[:, 0:128], in_=sg[:], identity=self.identf[:]), reads=[r_sg, self.r_ident], writes=[r_ptr])
                k.op("vector", lambda e: e.tensor_copy(out=gT[:], in_=ptr[0:96, 0:128]), reads=[r_ptr], writes=[r_gT])
                k.op("tensor", lambda e: e.matmul(pg[:], lhsT=gT[:], rhs=g2[:], start=True, stop=True), reads=[r_gT, r_c], writes=[r_pg])
                o_, ro = out[i2], r_out[i2]
                k.op("vector", lambda e, o_=o_: e.tensor_tensor(out=o_[:], in0=y[:], in1=pg[:], op=ALU.mult), reads=[r_y, r_pg], writes=[ro])
                k.dma(yd[rows, :], o_[:], reads=[ro], writes=[r_yd])
            k.barrier()

    def stage_merge(self, l, xname):
        k = self.k
        z, _ = self.dr["z%d" % l]
        r_zt = self.r_ztile
        xin, r_xin = self.dr[xname]
        modD, r_modD = self.dr["modD%d" % l]
        wbr, r_wbr = self.dr["w_br"]
        wout, r_wout = self.dr["w_out"]
        accD, r_accD = self.dram("accD%d" % l, [T, D], BF16)
        r_acc_t = [R() for _ in range(NT)]
        x1, r_x1 = self.dram("x1_%d" % l, [T, D], F32)
        self.r_x1_t = [R() for _ in range(NT)]
        ybr = [self.dr[n % l] for n in ("y_fn%d", "y_na%d", "y_mla%d", "y_rw%d")]
        with ExitStack() as es:
            W = k.sb("mg_W", [128, 16, 2048], BF16, es)
            r_W = [R() for _ in range(16)]
            wst = [k.sb("mg_wst%d" % i, [128, 2048], F32, es) for i in range(2)]
            r_wst = [R() for _ in range(2)]
            for i in range(16):
                b, kc = i // 4, i % 4
                k.dma(wst[i % 2][:], wbr[l, b, kc * 128:(kc + 1) * 128, :], reads=[r_wbr], writes=[r_wst[i % 2]])
                if i % 2 == 0:
                    k.op("vector", lambda e, i=i: e.tensor_copy(out=W[:, i, :], in_=wst[i % 2][:]), reads=[r_wst[i % 2]], writes=[r_W[i]])
                else:
                    k.op("scalar", lambda e, i=i: e.activation(out=W[:, i, :], in_=wst[i % 2][:], func=AF.Copy), reads=[r_wst[i % 2]], writes=[r_W[i]])
            with ExitStack() as es2:
                yt = k.sb("mg_yt", [128, 4, 512], F32, es2)
                r_yt = R()
                ytb = k.sb("mg_ytb", [128, 2048], BF16, es2)
                r_ytb = R()
                yT = k.sb("mg_yT", [128, 16, 128], BF16, es2)
                r_yT = R()
                gt = [k.sb("mg_gt%d" % i, [128, 2048], F32, es2) for i in range(2)]
                r_gt = [R() for _ in range(2)]
                acc = k.sb("mg_acc", [128, 2048], F32, es2)
                r_acc = R()
                accb = k.sb("mg_accb", [128, 2048], BF16, es2)
                r_accb = R()
                tmp = k.sb("mg_tmp", [128, 512], F32, es2)
                r_tmp = R()
                ptr = [k.ps("mg_ptr%d" % i, [128, 8, 128], BF16, es2) for i in range(2)]
                r_ptr = [PR() for _ in range(2)]
                pm = [k.ps("mg_pm%d" % i, [128, 512], F32, es2) for i in range(5)]
                r_pm = [PR() for _ in range(5)]
                ip = 0
                ig = 0
                for t in range(NT):
                    rows = slice(t * 128, (t + 1) * 128)
                    for b in range(4):
                        k.dma(yt[:, b, :], ybr[b][0][rows, :], reads=[ybr[b][1]], writes=[r_yt])
                    k.op("scalar", lambda e: e.activation(out=ytb[:], in_=yt[:].rearrange("p a b -> p (a b)"), func=AF.Copy), reads=[r_yt], writes=[r_ytb])
                    for hf in range(2):
                        def tr(e, hf=hf):
                            last = None
                            for j in range(8):
                                c = hf * 8 + j
                                last = e.transpose(out=ptr[hf][:, j, :], in_=ytb[:, c * 128:(c + 1) * 128], identity=self.ident[:])
                            return last
                        k.op("tensor", tr, reads=[r_ytb, self.r_ident], writes=[r_ptr[hf]])
                        if hf == 0:
                            k.op("vector", lambda e: e.tensor_copy(out=yT[:, 0:8, :], in_=ptr[0][:]), reads=[r_ptr[0]], writes=[r_yT])
                        else:
                            k.op("scalar", lambda e: e.activation(out=yT[:, 8:16, :], in_=ptr[1][:], func=AF.Copy), reads=[r_ptr[1], r_yT], writes=[r_yT])
                    for b in range(4):
                        g_, rg = gt[ig % 2], r_gt[ig % 2]
                        ig += 1
                        k.dma(g_[:], z[rows, C_GATE + b * 2048:C_GATE + (b + 1) * 2048], reads=[r_zt[t]], writes=[rg])
                        k.op("scalar", lambda e, g_=g_: e.activation(out=g_[:], in_=g_[:], func=AF.Sigmoid), reads=[rg], writes=[rg])
                        for cbk in range(4):
                            p, rp = pm[ip % 5], r_pm[ip % 5]
                            ip += 1
                            cs_ = slice(cbk * 512, (cbk + 1) * 512)

                            def mm(e, p=p, b=b, cs_=cs_):
                                last = None
                                for kc in range(4):
                                    last = e.matmul(p[:], lhsT=yT[:, b * 4 + kc, :], rhs=W[:, b * 4 + kc, cs_], start=(kc == 0), stop=(kc == 3))
                                return last
                            k.op("tensor", mm, reads=[r_yT] + r_W[b * 4:b * 4 + 4], writes=[rp])
                            if b == 0:
                                k.op("vector", lambda e, p=p, g_=g_, cs_=cs_: e.tensor_tensor(out=acc[:, cs_], in0=p[:], in1=g_[:, cs_], op=ALU.mult),
                                     reads=[rp, rg], writes=[r_acc])
                            else:
                                k.op("vector", lambda e, p=p, g_=g_, cs_=cs_: e.tensor_tensor(out=tmp[:], in0=p[:], in1=g_[:, cs_], op=ALU.mult),
                                     reads=[rp, rg], writes=[r_tmp])
                                k.op("vector", lambda e, cs_=cs_: e.tensor_tensor(out=acc[:, cs_], in0=acc[:, cs_], in1=tmp[:], op=ALU.add),
                                     reads=[r_tmp, r_acc], writes=[r_acc])
                    k.op("vector", lambda e: e.tensor_copy(out=accb[:], in_=acc[:]), reads=[r_acc], writes=[r_accb])
                    k.dma(accD[rows, :], accb[:], reads=[r_accb], writes=[r_acc_t[t]])
                k.barrier()
            for i in range(16):
                k.dma(wst[i % 2][:], wout[l, i * 128:(i + 1) * 128, :], reads=[r_wout], writes=[r_wst[i % 2]])
                if i % 2 == 0:
                    k.op("vector", lambda e, i=i: e.tensor_copy(out=W[:, i, :], in_=wst[i % 2][:]), reads=[r_wst[i % 2]], writes=[r_W[i]])
                else:
                    k.op("scalar", lambda e, i=i: e.activation(out=W[:, i, :], in_=wst[i % 2][:], func=AF.Copy), reads=[r_wst[i % 2]], writes=[r_W[i]])
            with ExitStack() as es3:
                g1 = k.sb("mg_g1", [128, 2, 2048], F32, es3)
                r_g1 = R()
                for r_ in range(2):
                    k.dma(g1[:, r_, :], modD[r_:r_ + 1, 2 * D:3 * D].partition_broadcast(128), reads=[r_modD], writes=[r_g1])
                ab = [k.sb("mg_ab%d" % i, [128, 2048], BF16, es3) for i in range(2)]
                r_ab = [R() for _ in range(2)]
                aT = k.sb("mg_aT", [128, 16, 128], BF16, es3)
                r_aT = R()
                xt = [k.sb("mg_xt%d" % i, [128, 2048], F32, es3) for i in range(2)]
                r_xt = [R() for _ in range(2)]
                xo = [k.sb("mg_xo%d" % i, [128, 2048], F32, es3) for i in range(2)]
                r_xo = [R() for _ in range(2)]
                tmp = k.sb("mg_tmp2", [128, 512], F32, es3)
                r_tmp = R()
                ptr = [k.ps("mg_ptrb%d" % i, [128, 8, 128], BF16, es3) for i in range(2)]
                r_ptr = [PR() for _ in range(2)]
                pm = [k.ps("mg_pmb%d" % i, [128, 512], F32, es3) for i in range(4)]
                r_pm = [PR() for _ in range(4)]
                ip = 0
                for t in range(NT):
                    rows = slice(t * 128, (t + 1) * 128)
                    row = 1 if t < 2 else 0
                    a_, ra = ab[t % 2], r_ab[t % 2]
                    x_, rx = xt[t % 2], r_xt[t % 2]
                    o_, ro = xo[t % 2], r_xo[t % 2]
                    k.dma(a_[:], accD[rows, :], reads=[r_acc_t[t]], writes=[ra])
                    k.dma(x_[:], xin[rows, :], reads=[r_xin], writes=[rx])
                    for hf in range(2):
                        def tr(e, hf=hf, a_=a_):
                            last = None
                            for j in range(8):
                                c = hf * 8 + j
                                last = e.transpose(out=ptr[hf][:, j, :], in_=a_[:, c * 128:(c + 1) * 128], identity=self.ident[:])
                            return last
                        k.op("tensor", tr, reads=[ra, self.r_ident], writes=[r_ptr[hf]])
                        if hf == 0:
                            k.op("vector", lambda e: e.tensor_copy(out=aT[:, 0:8, :], in_=ptr[0][:]), reads=[r_ptr[0]], writes=[r_aT])
                        else:
                            k.op("scalar", lambda e: e.activation(out=aT[:, 8:16, :], in_=ptr[1][:], func=AF.Copy), reads=[r_ptr[1], r_aT], writes=[r_aT])
                    for cbk in range(4):
                        p, rp = pm[ip % 4], r_pm[ip % 4]
                        ip += 1
                        cs_ = slice(cbk * 512, (cbk + 1) * 512)

                        def mm(e, p=p, cs_=cs_):
                            last = None
                            for kc in range(16):
                                last = e.matmul(p[:], lhsT=aT[:, kc, :], rhs=W[:, kc, cs_], start=(kc == 0), stop=(kc == 15))
                            return last
                        k.op("tensor", mm, reads=[r_aT] + r_W, writes=[rp])
                        k.op("vector", lambda e, p=p, cs_=cs_, row=row: e.tensor_tensor(out=tmp[:], in0=p[:], in1=g1[:, row, cs_], op=ALU.mult),
                             reads=[rp, r_g1], writes=[r_tmp])
                        k.op("vector", lambda e, o_=o_, x_=x_, cs_=cs_: e.tensor_tensor(out=o_[:, cs_], in0=tmp[:], in1=x_[:, cs_], op=ALU.add),
                             reads=[r_tmp, rx], writes=[ro])
                    k.dma(x1[rows, :], o_[:], reads=[ro], writes=[self.r_x1_t[t]])
                k.barrier()

    def stage_moe(self, l, final):
        k = self.k
        x1, _ = self.dr["x1_%d" % l]
        r_x1_t = self.r_x1_t
        modD, r_modD = self.dr["modD%d" % l]
        XE, r_XE = self.dram("moe_xe%d" % l, [16, 128, 16 * 288], BF16)
        WS, r_WS = self.dram("moe_ws%d" % l, [16, 128, 2 * 2048], BF16)
        WSC, r_WSC = self.dram("moe_wsc%d" % l, [16, 32, 256], BF16)
        YE, r_YE = self.dram("moe_ye%d" % l, [16, 288, 2048], BF16)
        r_XEe = [R() for _ in range(16)]
        r_WSe = [R() for _ in range(16)]
        r_YEe = [R() for _ in range(16)]
        if final:
            xo_d, r_xo_d = self.dram("out", [L, D], F32, "ExternalOutput")
        else:
            xo_d, r_xo_d = self.dram("x2_%d" % l, [T, D], F32)
        self.r_x2_t = [R() for _ in range(NT)]
        w1d, r_w1d = self.dr["moe_w1"]
        w3d, r_w3d = self.dr["moe_w3"]
        w2d, r_w2d = self.dr["moe_w2"]
        with ExitStack() as es:
            h2 = k.sb("mo_h2", [128, NT, 2048], BF16, es)
            r_h2 = [R() for _ in range(NT)]
            aff = k.sb("mo_aff", [128, NT, 32], F32, es)
            r_aff = [R() for _ in range(NT)]
            affT = k.sb("mo_affT", [32, T], F32, es)
            r_affT = R()
            k.op("gpsimd", lambda e: e.memset(aff[:], 0.0), writes=r_aff)
            r_c = R()
            with ExitStack() as es2:
                abc = k.sb("mo_abc", [128, 2, 2048], F32, es2)
                shbc = k.sb("mo_shbc", [128, 2, 2048], F32, es2)
                gbc = k.sb("mo_gbc", [128, 2048], F32, es2)
                ap, rr = self.dr["norm2_gb"]
                k.dma(gbc[:], ap[l].partition_broadcast(128), reads=[rr], writes=[r_c])
                for r_ in range(2):
                    k.dma(abc[:, r_, :], modD[r_:r_ + 1, 4 * D:5 * D].partition_broadcast(128), reads=[r_modD], writes=[r_c])
                    k.dma(shbc[:, r_, :], modD[r_:r_ + 1, 3 * D:4 * D].partition_broadcast(128), reads=[r_modD], writes=[r_c])
                for r_ in range(2):
                    k.op("vector", lambda e, r_=r_: e.scalar_tensor_tensor(out=abc[:, r_, :], in0=abc[:, r_, :], scalar=1.0, in1=gbc[:], op0=ALU.add, op1=ALU.mult),
                         reads=[r_c], writes=[r_c])
                rtr = k.sb("mo_rtr", [128, 16, 32], F32, es2)
                k.op("gpsimd", lambda e: e.memset(rtr[:], 0.0), writes=[r_c])
                ap, rr = self.dr["moe_router"]
                k.dma(rtr[:, :, 0:16], ap[l].rearrange("(kc p) n -> p kc n", p=128), reads=[rr], writes=[r_c])
                xt = [k.sb("mo_xt%d" % i, [128, 2048], F32, es2) for i in range(2)]
                r_xt = [R() for _ in range(2)]
                hf_ = k.sb("mo_hf", [128, 2048], F32, es2)
                r_hf = R()
                hT = k.sb("mo_hT", [128, 16, 128], F32, es2)
                r_hT = R()
                junk = k.sb("mo_junk", [128, 2048], BF16, es2)
                r_junk = R()
                st = k.sb("mo_st", [128, NT, 4], F32, es2)
                lg = k.sb("mo_lg", [128, 16], F32, es2)
                r_lg = R()
                ptr = [k.ps("mo_ptr%d" % i, [128, 4, 128], F32, es2) for i in range(4)]
                r_ptr = [PR() for _ in range(4)]
                pl = k.ps("mo_pl", [128, 512], F32, es2)
                r_pl = PR()
                pT = k.ps("mo_pT", [128, 512], F32, es2)
                r_pT = PR()
                for t in range(NT):
                    rows = slice(t * 128, (t + 1) * 128)
                    row = 1 if t < 2 else 0
                    x_, rx = xt[t % 2], r_xt[t % 2]
                    r_s = R()
                    k.dma(x_[:], x1[rows, :], reads=[r_x1_t[t]], writes=[rx])
                    k.op("scalar", lambda e, x_=x_, t=t: e.activation(out=junk[:], in_=x_[:], func=AF.Square, accum_out=st[:, t, 0:1]), reads=[rx], writes=[r_junk, r_s])
                    k.op("scalar", lambda e, t=t: e.activation(out=st[:, t, 1:2], in_=st[:, t, 0:1], func=AF.Sqrt, bias=self.eps_t[:, 0:1], scale=1.0 / D), reads=[r_s], writes=[r_s])
                    k.op("vector", lambda e, t=t: e.reciprocal(out=st[:, t, 1:2], in_=st[:, t, 1:2]), reads=[r_s], writes=[r_s])
                    k.op("vector", lambda e, x_=x_, t=t, row=row: e.scalar_tensor_tensor(out=hf_[:], in0=x_[:], scalar=st[:, t, 1:2], in1=abc[:, row, :], op0=ALU.mult, op1=ALU.mult),
                         reads=[rx, r_s, r_c], writes=[r_hf])
                    k.op("vector", lambda e, row=row: e.tensor_tensor(out=hf_[:], in0=hf_[:], in1=shbc[:, row, :], op=ALU.add), reads=[r_hf, r_c], writes=[r_hf])
                    k.op("scalar", lambda e, t=t: e.activation(out=h2[:, t, :], in_=hf_[:], func=AF.Copy), reads=[r_hf], writes=[r_h2[t]])
                    for g in range(4):
                        def tr(e, g=g):
                            last = None
                            for j in range(4):
                                c = g * 4 + j
                                last = e.transpose(out=ptr[g][:, j, :], in_=hf_[:, c * 128:(c + 1) * 128], identity=self.identf[:])
                            return last
                        k.op("tensor", tr, reads=[r_hf, self.r_ident], writes=[r_ptr[g]])
                        if g % 2 == 0:
                            k.op("vector", lambda e, g=g: e.tensor_copy(out=hT[:, g * 4:(g + 1) * 4, :], in_=ptr[g][:]), reads=[r_ptr[g]], writes=[r_hT])
                        else:
                            k.op("scalar", lambda e, g=g: e.activation(out=hT[:, g * 4:(g + 1) * 4, :], in_=ptr[g][:], func=AF.Copy), reads=[r_ptr[g], r_hT], writes=[r_hT])

                    def mmr(e):
                        last = None
                        for kc in range(16):
                            last = e.matmul(pl[:, 0:32], lhsT=hT[:, kc, :], rhs=rtr[:, kc, :], start=(kc == 0), stop=(kc == 15))
                        return last
                    k.op("tensor", mmr, reads=[r_hT, r_c], writes=[r_pl])
                    k.op("vector", lambda e, t=t: e.tensor_reduce(out=st[:, t, 2:3], in_=pl[:, 0:16], axis=AX.X, op=ALU.max), reads=[r_pl], writes=[r_s])
                    k.op("vector", lambda e, t=t: e.tensor_scalar(out=lg[:], in0=pl[:, 0:16], scalar1=st[:, t, 2:3], scalar2=None, op0=ALU.subtract), reads=[r_pl, r_s], writes=[r_lg])
                    k.op("scalar", lambda e, t=t: e.activation(out=lg[:], in_=lg[:], func=AF.Exp, accum_out=st[:, t, 3:4]), reads=[r_lg], writes=[r_lg, r_s])
                    k.op("vector", lambda e, t=t: e.reciprocal(out=st[:, t, 3:4], in_=st[:, t, 3:4]), reads=[r_s], writes=[r_s])
                    k.op("vector", lambda e, t=t: e.tensor_scalar(out=aff[:, t, 0:16], in0=lg[:], scalar1=st[:, t, 3:4], scalar2=None, op0=ALU.mult), reads=[r_lg, r_s], writes=[r_aff[t]])
                    k.op("tensor", lambda e, t=t: e.transpose(out=pT[0:32, 0:128], in_=aff[:, t, :], identity=self.identf[:]), reads=[r_aff[t], self.r_ident], writes=[r_pT])
                    k.op("vector", lambda e, rows=rows: e.tensor_copy(out=affT[:, rows], in_=pT[0:32, 0:128]), reads=[r_pT], writes=[r_affT])
                k.barrier()
            with ExitStack() as es3:
                sel = k.sb("mo_sel", [32, 16, 128], F32, es3)
                ap, rr = self.dr["moe_sel"]
                k.dma(sel[:], ap, reads=[rr], writes=[r_c])
                jidx = k.sb("mo_jidx", [128, 2], F32, es3)
                ap, rr = self.dr["moe_jidx"]
                k.dma(jidx[:], ap, reads=[rr], writes=[r_c])
                ones = k.sb("mo_ones", [128, 128], BF16, es3)
                k.op("vector", lambda e: e.memset(ones[:], 1.0), writes=[r_c])
                A = k.sb("mo_A", [128, T], F32, es3)
                r_A = R()
                cmp = [k.sb("mo_cmp%d" % i, [128, 2048], BF16, es3) for i in range(2)]
                r_cmp = [R() for _ in range(2)]
                Ws = k.sb("mo_Ws", [128, 2, 2048], BF16, es3)
                r_Ws = R()
                Ps = k.sb("mo_Ps", [128, 2, 2048], BF16, es3)
                r_Ps = R()
                Wc = k.sb("mo_Wc", [128, 256], BF16, es3)
                r_Wc = R()
                Pc = k.sb("mo_Pc", [128, 256], BF16, es3)
                r_Pc = R()
                PsT = k.sb("mo_PsT", [128, NT, 256], BF16, es3)
                r_PsT = R()
                xe = [k.sb("mo_xe%d" % i, [128, 16, 288], BF16, es3) for i in range(2)]
                r_xe = [R() for _ in range(2)]
                prk = k.ps("mo_prk", [128, 2048], F32, es3)
                r_prk = PR()
                pA = k.ps("mo_pA", [128, 512], F32, es3)
                r_pA = PR()
                ptb = k.ps("mo_ptb", [128, 8, 128], BF16, es3)
                r_ptb = PR()
                pg = [k.ps("mo_pg%d" % i, [128, 512], F32, es3) for i in range(2)]
                r_pg = [PR() for _ in range(2)]
                ic = 0
                ig = 0
                for e_ in range(16):
                    for cb in range(5):
                        c0 = cb * 512
                        cw = min(512, T - c0)
                        k.op("tensor", lambda e, c0=c0, cw=cw, e_=e_: e.matmul(pA[:, 0:cw], lhsT=sel[:, e_, :], rhs=affT[:, c0:c0 + cw], start=True, stop=True),
                             reads=[r_affT, r_c], writes=[r_pA])
                        k.op("scalar", lambda e, c0=c0, cw=cw: e.activation(out=A[:, c0:c0 + cw], in_=pA[:, 0:cw], func=AF.Copy), reads=[r_pA], writes=[r_A])
                    for mt in range(16):
                        c_, rc_ = cmp[ic % 2], r_cmp[ic % 2]
                        ic += 1
                        k.op("vector", lambda e, c_=c_, mt=mt, e_=e_: e.tensor_scalar(out=c_[:], in0=A[:, NCTX:T], scalar1=aff[:, 2 + mt, e_:e_ + 1], scalar2=None, op0=ALU.is_lt),
                             reads=[r_A, r_aff[2 + mt]], writes=[rc_])

                        def mmc(e, c_=c_, mt=mt):
                            last = None
                            for cb in range(4):
                                last = e.matmul(prk[:, cb * 512:(cb + 1) * 512], lhsT=ones[:], rhs=c_[:, cb * 512:(cb + 1) * 512], start=(mt == 0), stop=(mt == 15))
                            return last
                        k.op("tensor", mmc, reads=[rc_, r_c], writes=[r_prk])
                    for jt in range(2):
                        k.op("vector", lambda e, jt=jt: e.scalar_tensor_tensor(out=Ws[:, jt, :], in0=prk[:], scalar=jidx[:, jt:jt + 1], in1=A[:, NCTX:T], op0=ALU.is_equal, op1=ALU.mult),
                             reads=[r_prk, r_A, r_c], writes=[r_Ws])
                        k.op("vector", lambda e, jt=jt: e.tensor_scalar(out=Ps[:, jt, :], in0=prk[:], scalar1=jidx[:, jt:jt + 1], scalar2=None, op0=ALU.is_equal),
                             reads=[r_prk, r_c], writes=[r_Ps])
                    k.dma(WS[e_].rearrange("p (a b) -> p a b", a=2), Ws[:], reads=[r_Ws], writes=[r_WSe[e_]])
                    for mt in range(2):
                        c_, rc_ = cmp[ic % 2], r_cmp[ic % 2]
                        ic += 1
                        k.op("vector", lambda e, c_=c_, mt=mt, e_=e_: e.tensor_scalar(out=c_[:, 0:256], in0=A[:, 0:NCTX], scalar1=aff[:, mt, e_:e_ + 1], scalar2=None, op0=ALU.is_lt),
                             reads=[r_A, r_aff[mt]], writes=[rc_])
                        k.op("tensor", lambda e, c_=c_, mt=mt: e.matmul(prk[:, 0:256], lhsT=ones[:], rhs=c_[:, 0:256], start=(mt == 0), stop=(mt == 1)),
                             reads=[rc_, r_c], writes=[r_prk])
                    k.op("vector", lambda e: e.scalar_tensor_tensor(out=Wc[:], in0=prk[:, 0:256], scalar=jidx[:, 0:1], in1=A[:, 0:NCTX], op0=ALU.is_equal, op1=ALU.mult),
                         reads=[r_prk, r_A, r_c], writes=[r_Wc])
                    k.op("vector", lambda e: e.tensor_scalar(out=Pc[:], in0=prk[:, 0:256], scalar1=jidx[:, 0:1], scalar2=None, op0=ALU.is_equal),
                         reads=[r_prk, r_c], writes=[r_Pc])
                    k.dma(WSC[e_], Wc[0:32, :], reads=[r_Wc], writes=[r_WSe[e_]])
                    for nt in range(16):
                        if nt % 4 == 0:
                            def trp(e, nt=nt):
                                last = None
                                for q in range(4):
                                    for jt in range(2):
                                        last = e.transpose(out=ptb[:, q * 2 + jt, :], in_=Ps[:, jt, (nt + q) * 128:(nt + q + 1) * 128], identity=self.ident[:])
                                return last
                            k.op("tensor", trp, reads=[r_Ps, self.r_ident], writes=[r_ptb])
                            k.op("scalar", lambda e, nt=nt: e.activation(out=PsT[:, 2 + nt:2 + nt + 4, :], in_=ptb[:].rearrange("p (q j) n -> p q (j n)", q=4), func=AF.Copy),
                                 reads=[r_ptb], writes=[r_PsT])

                    def trc(e):
                        last = None
                        for nt in range(2):
                            last = e.transpose(out=ptb[:, nt, 0:32], in_=Pc[0:32, nt * 128:(nt + 1) * 128], identity=self.ident[0:32, 0:32])
                        return last
                    k.op("tensor", trc, reads=[r_Pc, self.r_ident], writes=[r_ptb])
                    k.op("scalar", lambda e: e.activation(out=PsT[:, 0:2, 0:32], in_=ptb[:, 0:2, 0:32], func=AF.Copy), reads=[r_ptb], writes=[r_PsT])
                    x_, rx = xe[e_ % 2], r_xe[e_ % 2]
                    for dc in range(16):
                        p, rp = pg[ig % 2], r_pg[ig % 2]
                        ig += 1

                        def mmg(e, p=p, dc=dc):
                            last = None
                            for nt in range(16):
                                last = e.matmul(p[:, 0:256], lhsT=h2[:, 2 + nt, dc * 128:(dc + 1) * 128], rhs=PsT[:, 2 + nt, :], start=(nt == 0), stop=(nt == 15))
                            for nt in range(2):
                                last = e.matmul(p[:, 256:288], lhsT=h2[:, nt, dc * 128:(dc + 1) * 128], rhs=PsT[:, nt, 0:32], start=(nt == 0), stop=(nt == 1))
                            return last
                        k.op("tensor", mmg, reads=[r_PsT] + r_h2, writes=[rp])
                        k.op("scalar", lambda e, p=p, x_=x_, dc=dc: e.activation(out=x_[:, dc, :], in_=p[:, 0:288], func=AF.Copy), reads=[rp, rx], writes=[rx])
                    k.dma(XE[e_], x_[:].rearrange("p a b -> p (a b)"), reads=[rx], writes=[r_XEe[e_]])
                k.barrier()
        if self.dbg.get("MOE_PH", 9) < 2:
            return
        with ExitStack() as es:
            w1b = k.sb("mo_w1b", [128, 16, 1024], BF16, es)
            w3b = k.sb("mo_w3b", [128, 16, 1024], BF16, es)
            w2b = k.sb("mo_w2b", [128, 8, 2048], BF16, es)
            r_w1 = [R() for _ in range(8)]
            r_w3 = [R() for _ in range(8)]
            r_w2 = [R() for _ in range(8)]
            wst = [k.sb("mo_wst%d" % i, [128, 4096], F32, es) for i in range(3)]
            r_wst = [R() for _ in range(3)]
            xe = [k.sb("mo_xeb%d" % i, [128, 16, 288], BF16, es) for i in range(2)]
            r_xe = [R() for _ in range(2)]
            gT = k.sb("mo_gT", [128, 8, 288], BF16, es)
            r_gT = [R() for _ in range(8)]
            sa8 = k.sb("mo_sa8", [128, 8, 288], F32, es)
            r_sa8 = [R() for _ in range(8)]
            yeb = [k.sb("mo_yeb%d" % i, [128, 3, 2048], BF16, es) for i in range(1)]
            r_yeb = [R() for _ in range(1)]
            pa = [k.ps("mo_pa%d" % i, [128, 512], F32, es) for i in range(2)]
            r_pa = [PR() for _ in range(2)]
            pu = [k.ps("mo_pu%d" % i, [128, 512], F32, es) for i in range(2)]
            r_pu = [PR() for _ in range(2)]
            py = [k.ps("mo_py%d" % i, [128, 512], F32, es) for i in range(3)]
            r_py = [PR() for _ in range(3)]
            iw = 0
            iy = 0
            engs = ("vector", "gpsimd")
            for e_ in range(16):
                x_, rx = xe[e_ % 2], r_xe[e_ % 2]
                k.dma(x_[:].rearrange("p a b -> p (a b)"), XE[e_], reads=[r_XEe[e_]], writes=[rx])
                for (wd, rwd_, wb_, rw_) in ((w1d, r_w1d, w1b, r_w1), (w3d, r_w3d, w3b, r_w3)):
                    for pr in range(4):
                        s_, rs = wst[iw % 3], r_wst[iw % 3]
                        k.dma(s_[:].rearrange("p (a n) -> p a n", a=4), wd[l, e_, pr * 512:(pr + 1) * 512, :].rearrange("(a p) n -> p a n", p=128), reads=[rwd_], writes=[rs])
                        for hh in range(2):
                            sv = s_[:, hh * 2048:(hh + 1) * 2048].rearrange("p (a n) -> p a n", a=2)
                            dv = wb_[:, 4 * pr + 2 * hh:4 * pr + 2 * hh + 2, :]
                            if hh == 0:
                                k.op("vector", lambda e, sv=sv, dv=dv: e.tensor_copy(out=dv, in_=sv), reads=[rs], writes=[rw_[2 * pr + hh]])
                            else:
                                k.op("scalar", lambda e, sv=sv, dv=dv: e.activation(out=dv, in_=sv, func=AF.Copy), reads=[rs], writes=[rw_[2 * pr + hh]])
                        iw += 1
                for fp in range(4):
                    s_, rs = wst[iw % 3], r_wst[iw % 3]
                    k.dma(s_[:].rearrange("p (a n) -> p a n", a=2), w2d[l, e_, fp * 256:(fp + 1) * 256, :].rearrange("(a p) n -> p a n", p=128), reads=[r_w2d], writes=[rs])
                    for hh in range(2):
                        fc = 2 * fp + hh
                        sv = s_[:, hh * 2048:(hh + 1) * 2048]
                        if hh == 0:
                            k.op("vector", lambda e, sv=sv, fc=fc: e.tensor_copy(out=w2b[:, fc, :], in_=sv), reads=[rs], writes=[r_w2[fc]])
                        else:
                            k.op("scalar", lambda e, sv=sv, fc=fc: e.activation(out=w2b[:, fc, :], in_=sv, func=AF.Copy), reads=[rs], writes=[r_w2[fc]])
                    iw += 1
                for fc in range(8):
                    pa_, rpa = pa[fc % 2], r_pa[fc % 2]

                    def mma(e, p_=pa_, fc=fc, x_=x_):
                        last = None
                        for kc in range(16):
                            last = e.matmul(p_[:, 0:288], lhsT=w1b[:, kc, fc * 128:(fc + 1) * 128], rhs=x_[:, kc, :], start=(kc == 0), stop=(kc == 15))
                        return last
                    k.op("tensor", mma, reads=r_w1 + [rx], writes=[rpa])
                    k.op("scalar", lambda e, pa_=pa_, fc=fc: e.activation(out=sa8[:, fc, :], in_=pa_[:, 0:288], func=AF.Silu), reads=[rpa], writes=[r_sa8[fc]])
                for fc in range(8):
                    pu_, rpu = pu[fc % 2], r_pu[fc % 2]

                    def mmu(e, p_=pu_, fc=fc, x_=x_):
                        last = None
                        for kc in range(16):
                            last = e.matmul(p_[:, 0:288], lhsT=w3b[:, kc, fc * 128:(fc + 1) * 128], rhs=x_[:, kc, :], start=(kc == 0), stop=(kc == 15))
                        return last
                    k.op("tensor", mmu, reads=r_w3 + [rx], writes=[rpu])
                    k.op("vector", lambda e, pu_=pu_, fc=fc: e.tensor_tensor(out=gT[:, fc, :], in0=sa8[:, fc, :], in1=pu_[:, 0:288], op=ALU.mult), reads=[r_sa8[fc], rpu], writes=[r_gT[fc]])
                y_, ry = yeb[0], r_yeb[0]
                for jt, (j0, M) in enumerate(((0, 128), (128, 128), (256, 32))):
                    for cbk in range(4):
                        p_, rp_ = py[iy % 3], r_py[iy % 3]
                        iy += 1
                        cs_ = slice(cbk * 512, (cbk + 1) * 512)

                        def mmy(e, p_=p_, j0=j0, M=M, cs_=cs_):
                            last = None
                            for fc in range(8):
                                last = e.matmul(p_[0:M, :], lhsT=gT[:, fc, j0:j0 + M], rhs=w2b[:, fc, cs_], start=(fc == 0), stop=(fc == 7))
                            return last
                        k.op("tensor", mmy, reads=r_gT + r_w2, writes=[rp_])
                        if iy % 2 == 0:
                            k.op("vector", lambda e, p_=p_, y_=y_, jt=jt, M=M, cs_=cs_: e.tensor_copy(out=y_[0:M, jt, cs_], in_=p_[0:M, :]), reads=[rp_], writes=[ry])
                        else:
                            k.op("scalar", lambda e, p_=p_, y_=y_, jt=jt, M=M, cs_=cs_: e.activation(out=y_[0:M, jt, cs_], in_=p_[0:M, :], func=AF.Copy), reads=[rp_], writes=[ry])
                k.dma(YE[e_, 0:256, :].rearrange("(a p) n -> p a n", p=128), y_[:, 0:2, :], reads=[ry], writes=[r_YEe[e_]])
                k.dma(YE[e_, 256:288, :], y_[0:32, 2, :], reads=[ry], writes=[r_YEe[e_]])
            k.barrier()
        if self.dbg.get("MOE_PH", 9) < 3:
            return
        with ExitStack() as es:
            macc = k.sb("mo_macc", [128, 8, 2048], F32, es)
            r_macc = [R() for _ in range(8)]
            g2 = k.sb("mo_g2", [128, 2, 2048], F32, es)
            r_g2 = R()
            for r_ in range(2):
                k.dma(g2[:, r_, :], modD[r_:r_ + 1, 5 * D:6 * D].partition_broadcast(128), reads=[r_modD], writes=[r_g2])
            ye = [k.sb("mo_ye%d" % i, [128, 2, 2048], BF16, es) for i in range(2)]
            r_ye = [R() for _ in range(2)]
            ws = [k.sb("mo_ws%d" % i, [128, 2, 1024], BF16, es) for i in range(2)]
            r_ws = [R() for _ in range(2)]
            xt = [k.sb("mo_x2t%d" % i, [128, 2048], F32, es) for i in range(2)]
            r_xt = [R() for _ in range(2)]
            ps = [k.ps("mo_ps%d" % i, [128, 512], F32, es) for i in range(4)]
            r_ps = [PR() for _ in range(4)]
            ip = 0
            il = 0
            for part in range(3):
                ntl = 2 if part == 0 else 8
                t0 = 0 if part == 0 else 2 + (part - 1) * 8
                for e_ in range(16):
                    y_, ry = ye[il % 2], r_ye[il % 2]
                    w_, rw = ws[il % 2], r_ws[il % 2]
                    il += 1
                    if part == 0:
                        k.dma(y_[0:32, 0, :], YE[e_, 256:288, :], reads=[r_YEe[e_]], writes=[ry])
                        k.dma(w_[0:32, 0, 0:256], WSC[e_], reads=[r_WSe[e_]], writes=[rw])
                    else:
                        k.dma(y_[:], YE[e_, 0:256, :].rearrange("(a p) n -> p a n", p=128), reads=[r_YEe[e_]], writes=[ry])
                        c0 = (part - 1) * 1024
                        k.dma(w_[:], WS[e_].rearrange("p (a b) -> p a b", a=2)[:, :, c0:c0 + 1024], reads=[r_WSe[e_]], writes=[rw])
                    for tl in range(ntl):
                        for cbk in range(4):
                            p_, rp_ = ps[ip % 4], r_ps[ip % 4]
                            ip += 1
                            cs_ = slice(cbk * 512, (cbk + 1) * 512)
                            if part == 0:
                                k.op("tensor", lambda e, p_=p_, w_=w_, y_=y_, tl=tl, cs_=cs_: e.matmul(p_[:], lhsT=w_[0:32, 0, tl * 128:(tl + 1) * 128], rhs=y_[0:32, 0, cs_], start=True, stop=True),
                                     reads=[rw, ry], writes=[rp_])
                            else:
                                def mms(e, p_=p_, w_=w_, y_=y_, tl=tl, cs_=cs_):
                                    e.matmul(p_[:], lhsT=w_[:, 0, tl * 128:(tl + 1) * 128], rhs=y_[:, 0, cs_], start=True, stop=False)
                                    return e.matmul(p_[:], lhsT=w_[:, 1, tl * 128:(tl + 1) * 128], rhs=y_[:, 1, cs_], start=False, stop=True)
                                k.op("tensor", mms, reads=[rw, ry], writes=[rp_])
                            if e_ == 0:
                                k.op("vector", lambda e, p_=p_, tl=tl, cs_=cs_: e.tensor_copy(out=macc[:, tl, cs_], in_=p_[:]), reads=[rp_], writes=[r_macc[tl]])
                            else:
                                k.op("vector", lambda e, p_=p_, tl=tl, cs_=cs_: e.tensor_tensor(out=macc[:, tl, cs_], in0=macc[:, tl, cs_], in1=p_[:], op=ALU.add),
                                     reads=[rp_, r_macc[tl]], writes=[r_macc[tl]])
                for tl in range(ntl):
                    t = t0 + tl
                    rows = slice(t * 128, (t + 1) * 128)
                    row = 1 if t < 2 else 0
                    x_, rx = xt[t % 2], r_xt[t % 2]
                    k.dma(x_[:], x1[rows, :], reads=[r_x1_t[t]], writes=[rx])
                    k.op("vector", lambda e, tl=tl, row=row: e.tensor_tensor(out=macc[:, tl, :], in0=macc[:, tl, :], in1=g2[:, row, :], op=ALU.mult),
                         reads=[r_macc[tl], r_g2], writes=[r_macc[tl]])
                    k.op("vector", lambda e, tl=tl, x_=x_: e.tensor_tensor(out=x_[:], in0=x_[:], in1=macc[:, tl, :], op=ALU.add), reads=[rx, r_macc[tl]], writes=[rx])
                    if final:
                        if t >= 2:
                            k.dma(xo_d[(t - 2) * 128:(t - 1) * 128, :], x_[:], reads=[rx], writes=[r_xo_d])
                    else:
                        k.dma(xo_d[rows, :], x_[:], reads=[rx], writes=[self.r_x2_t[t]])
            k.barrier()

    def declare_inputs(self):
        dp = self.depth
        self.dram("xcat", [T, D], F32, "ExternalInput")
        self.dram("ada_w", [dp, D, 6 * D], F32, "ExternalInput")
        self.dram("ada_b2", [dp, 2, 6 * D], F32, "ExternalInput")
        self.dram("norm1_gT", [dp, 128, 16], F32, "ExternalInput")
        self.dram("w_in", [dp, D, ZC], F32, "ExternalInput")
        self.dram("na_qg", [dp, 1, 512], F32, "ExternalInput")
        self.dram("na_kg", [dp, 1, 512], F32, "ExternalInput")
        self.dram("na_bias", [dp, 8, 5, 5, 128, 128], F32, "ExternalInput")
        self.dram("mla_cqg", [dp, 1, 512], F32, "ExternalInput")
        self.dram("mla_ckvg", [dp, 1, 256], F32, "ExternalInput")
        self.dram("mla_qg", [dp, 1, 768], F32, "ExternalInput")
        self.dram("mla_kng", [dp, 1, 512], F32, "ExternalInput")
        self.dram("mla_kpg", [dp, 1, 64], F32, "ExternalInput")
        self.dram("mla_w_uq", [dp, 512, 768], F32, "ExternalInput")
        self.dram("mla_w_ukv", [dp, 256, 1024], F32, "ExternalInput")
        self.dram("rope_cos", [T, 64], F32, "ExternalInput")
        self.dram("dft_c", [128, 256], BF16, "ExternalInput")
        self.dram("norm2_gb", [dp, 1, D], F32, "ExternalInput")
        self.dram("moe_router", [dp, D, 16], F32, "ExternalInput")
        self.dram("moe_w1", [dp, 16, D, 1024], F32, "ExternalInput")
        self.dram("moe_w3", [dp, 16, D, 1024], F32, "ExternalInput")
        self.dram("moe_w2", [dp, 16, 1024, D], F32, "ExternalInput")
        self.dram("moe_sel", [32, 16, 128], F32, "ExternalInput")
        self.dram("moe_jidx", [128, 2], F32, "ExternalInput")
        self.dram("w_br", [dp, 4, 512, D], F32, "ExternalInput")
        self.dram("w_out", [dp, D, D], F32, "ExternalInput")
        self.dram("rw_mu", [dp, 1, 1760], F32, "ExternalInput")
        self.dram("rw_kk", [dp, 1, 512], F32, "ExternalInput")
        self.dram("rw_ka", [dp, 1, 512], F32, "ExternalInput")
        self.dram("rw_w0f", [dp, 1, 1024], F32, "ExternalInput")
        self.dram("rw_a0f", [dp, 1, 1024], F32, "ExternalInput")
        self.dram("rw_wa2", [dp, 32, 2, 2, 512], F32, "ExternalInput")
        self.dram("rw_lnw", [dp, 1, 512], F32, "ExternalInput")
        self.dram("rw_lnb", [dp, 1, 512], F32, "ExternalInput")
        self.dram("rw_rk", [dp, 1, 512], F32, "ExternalInput")
        self.dram("rw_g2", [dp, 96, 512], F32, "ExternalInput")
        self.dram("rw_e2", [128, 64, 128], F32, "ExternalInput")
        self.dram("rw_pm", [128, 128], F32, "ExternalInput")
        self.dram("rw_j64", [64, 64], F32, "ExternalInput")
        self.dram("rw_masks", [128, 6, 128], F32, "ExternalInput")
        self.dram("rw_dm", [128, 2, 1024], F32, "ExternalInput")
        self.dram("dft_L", [2, L, L], BF16, "ExternalInput")
        self.dram("dft_C", [2, NCTX, NCTX], BF16, "ExternalInput")
        self.dram("rope_sin", [T, 64], F32, "ExternalInput")

    def build(self):
        k = self.k
        self.declare_inputs()
        self.eps_t = k.sb("eps_t", [128, 1], F32)
        r = R()
        k.op("vector", lambda e: e.memset(self.eps_t[:], EPS), writes=[r])
        k.barrier()
        self.consts()
        xname = "xcat"
        for l in range(self.depth):
            if self.want("mod"):
                self.stage_mod(l)
            else:
                self.dram("modD%d" % l, [2, 6 * D], F32)
            if self.want("normz"):
                self.stage_normz(l, xname)
            else:
                self.dram("z%d" % l, [T, ZC], F32)
                self.r_ztile = [self.dr["z%d" % l][1]] * NT
            if self.want("na"):
                self.stage_na(l)
            if self.want("mla"):
                self.stage_mla(l)
            if self.want("fourier"):
                self.stage_fourier(l)
            if self.want("rwprep"):
                self.stage_rwprep(l)
            if self.want("rwscan"):
                self.stage_rwscan(l)
            if self.want("rwout"):
                self.stage_rwout(l)
            else:
                for n in ("y_fn%d", "y_na%d", "y_mla%d", "y_rw%d"):
                    if (n % l) not in self.dr:
                        self.dram(n % l, [T, 512], F32)
            if self.want("merge"):
                self.stage_merge(l, xname)
            else:
                self.dram("x1_%d" % l, [T, D], F32)
                self.r_x1_t = [self.dr["x1_%d" % l][1]] * NT
            if self.want("moe"):
                self.stage_moe(l, final=(l == self.depth - 1) and self.final_out)
                xname = "x2_%d" % l
        k.finish([])
        return self.nc


def _na_bias_index():
    rows, kh, W, KW = 32, 8, 64, 16
    dr = np.zeros((5, 5, 128, 128), np.int64)
    dc = np.zeros((5, 5, 128, 128), np.int64)
    va = np.zeros((5, 5, 128, 128), bool)
    kk = np.arange(128)
    qq = np.arange(128)
    for cls, j in enumerate((0, 1, 5, 14, 15)):
        lo = min(max(j - 2, 0), 11)
        for i in range(5):
            kt = lo + i
            krow = (2 * kt + kk // 64)[:, None]
            kc = (kk % 64)[:, None]
            r = (2 * j + qq // 64)[None, :]
            qc = (qq % 64)[None, :]
            rs = np.clip(r - kh // 2, 0, rows - kh)
            cs = np.clip(qc - KW // 2, 0, W - KW)
            v = (krow >= rs) & (krow < rs + kh) & (kc >= cs) & (kc < cs + KW)
            dr[cls, i] = np.clip(krow - r + 8 - 1, 0, 14)
            dc[cls, i] = np.clip(kc - qc + KW - 1, 0, 2 * KW - 2)
            va[cls, i] = v
    return dr, dc, va


def _rope_tables():
    pos = np.arange(L)
    prow, pcol = pos // 64, pos % 64
    freqs = 10000.0 ** (-np.arange(16, dtype=np.float32) / 16)
    ar = prow[:, None].astype(np.float32) * freqs[None, :]
    ac = pcol[:, None].astype(np.float32) * freqs[None, :]
    cos = np.concatenate([np.cos(ar), np.cos(ar), np.cos(ac), np.cos(ac)], 1)
    sin = np.concatenate([-np.sin(ar), np.sin(ar), -np.sin(ac), np.sin(ac)], 1)
    cosT = np.concatenate([np.ones((NCTX, 64), np.float32), cos.astype(np.float32)], 0)
    sinT = np.concatenate([np.zeros((NCTX, 64), np.float32), sin.astype(np.float32)], 0)
    return np.ascontiguousarray(cosT), np.ascontiguousarray(sinT)


_DFT = None


def _dft_consts():
    global _DFT
    if _DFT is None:
        c = np.arange(128)
        ang = 2 * np.pi * ((c[:, None] * c[None, :]) % 128) / 128.0
        dft_c = bf16_np(np.concatenate([np.cos(ang), np.sin(ang)], 1))
        out = []
        for n in (L, NCTX):
            i = np.arange(n, dtype=np.int64)
            ang = 2 * np.pi * ((i[:, None] * i[None, :]) % n) / float(n)
            sc = 1.0 / np.sqrt(n * 128.0)
            out.append(bf16_np(np.stack([np.cos(ang) * sc, -np.sin(ang) * sc], 0)))
        _DFT = (dft_c, out[0], out[1])
    return _DFT


def _rw_chunk_consts():
    MS = np.zeros((128, 128), np.float32)
    MI = np.zeros((128, 128), np.float32)
    BO = np.zeros((128, 128), np.float32)
    for d in range(2):
        for a in range(64):
            for c in range(64):
                before = (a < c) if d == 0 else (a > c)
                MS[d * 64 + a, d * 64 + c] = 1.0 if before else 0.0
                MI[d * 64 + a, d * 64 + c] = 1.0 if (before or a == c) else 0.0
                BO[d * 64 + a, d * 64 + c] = 1.0
    masks = np.stack([MI, BO, MS, -MS, -MS.T, -MI], 1).astype(np.float32)
    dm = np.zeros((128, 8, 2, 64), np.float32)
    dm[0:64, :, 0, :] = 1.0
    dm[64:128, :, 1, :] = 1.0
    dm = dm.reshape(128, 1024)
    return np.ascontiguousarray(masks), np.ascontiguousarray(np.stack([dm, -dm], 1))


def _rw_consts():
    e2 = np.zeros((128, 64, 128), np.float32)
    for i in range(64):
        e2[i, i, 0:64] = 1.0
        e2[64 + 63 - i, i, 64:128] = 1.0
    pm = np.zeros((128, 128), np.float32)
    for i in range(64):
        pm[i, i] = 1.0
        pm[64 + i, 64 + 63 - i] = 1.0
    j64 = np.ascontiguousarray(np.eye(64, dtype=np.float32)[::-1])
    return e2, pm, j64


def host_core(inp, b):
    f32 = np.float32
    m = {}
    m["xcat"] = np.concatenate([inp["ctx"][b], inp["x"][b]], 0).astype(f32)
    cv = np.stack([inp["c"][b], inp["c_ctx"]], 0)
    m["cvT"] = np.ascontiguousarray(cv.reshape(2, 16, 128).transpose(2, 1, 0))
    return m


def host_inputs(inp, b, depth=DEPTH):
    m = host_shared(inp, depth)
    m.update(host_core(inp, b))
    return m


def host_shared(inp, depth=DEPTH):
    f32 = np.float32
    m = {}
    m["ada_w"] = inp["ada_w"][:depth]
    m["ada_b2"] = np.ascontiguousarray(np.broadcast_to(inp["ada_b"][:depth, None, :], (depth, 2, 6 * D)))
    m["norm1_gT"] = np.ascontiguousarray(inp["norm1_g"][:depth].reshape(depth, 16, 128).transpose(0, 2, 1))
    m["w_in"] = inp["w_in"][:depth]
    m["na_qg"] = np.ascontiguousarray(np.tile(inp["na_q_norm"][:depth, None, :], (1, 1, 8)))
    m["na_kg"] = np.ascontiguousarray(np.tile(inp["na_k_norm"][:depth, None, :], (1, 1, 8)))
    dr, dc, va = _na_bias_index()
    rpb = inp["na_rpb"][:depth]
    gathered = rpb[:, :, dr, dc]
    m["na_bias"] = np.where(va[None, None], gathered, f32(-1e30)).astype(f32)
    m["mla_cqg"] = np.ascontiguousarray(inp["mla_cq_norm"][:depth, None, :])
    m["mla_ckvg"] = np.ascontiguousarray(inp["mla_ckv_norm"][:depth, None, :])
    m["mla_qg"] = np.ascontiguousarray(np.tile(inp["mla_q_norm"][:depth, None, :], (1, 1, 4)))
    m["mla_kng"] = np.ascontiguousarray(np.tile(inp["mla_k_norm"][:depth, None, :128], (1, 1, 4)))
    m["mla_kpg"] = np.ascontiguousarray(inp["mla_k_norm"][:depth, None, 128:192])
    m["mla_w_uq"] = inp["mla_w_uq"][:depth]
    m["mla_w_ukv"] = inp["mla_w_ukv"][:depth]
    m["dft_c"], m["dft_L"], m["dft_C"] = _dft_consts()
    m["rw_mu"] = np.ascontiguousarray(np.concatenate([inp["rw_mu_ks"][:depth], inp["rw_mu_qs"][:depth]], 1)[:, None, :])
    m["rw_kk"] = np.ascontiguousarray(inp["rw_k_k"][:depth, None, :])
    m["rw_ka"] = np.ascontiguousarray(inp["rw_k_a"][:depth, None, :])
    m["rw_w0f"] = np.ascontiguousarray(inp["rw_w0"][:depth].reshape(depth, 1, 1024))
    m["rw_a0f"] = np.ascontiguousarray(inp["rw_a0"][:depth].reshape(depth, 1, 1024))
    wa2 = np.stack([inp["rw_w2"][:depth], inp["rw_a2"][:depth]], 1)
    m["rw_wa2"] = np.ascontiguousarray(wa2.transpose(0, 3, 1, 2, 4))
    m["rw_lnw"] = np.ascontiguousarray(inp["rw_ln_w"][:depth, None, :])
    m["rw_lnb"] = np.ascontiguousarray(inp["rw_ln_b"][:depth, None, :])
    m["rw_rk"] = np.ascontiguousarray(inp["rw_r_k"][:depth].reshape(depth, 1, 512))
    m["rw_g2"] = inp["rw_g2"][:depth]
    m["rw_e2"], m["rw_pm"], m["rw_j64"] = _rw_consts()
    m["rw_masks"], m["rw_dm"] = _rw_chunk_consts()
    m["w_br"] = inp["w_br"][:depth]
    m["norm2_gb"] = np.ascontiguousarray(inp["norm2_g"][:depth, None, :])
    m["moe_router"] = inp["moe_router"][:depth]
    m["moe_w1"] = inp["moe_w1"][:depth]
    m["moe_w3"] = inp["moe_w3"][:depth]
    m["moe_w2"] = inp["moe_w2"][:depth]
    sel = np.zeros((32, 16, 128), np.float32)
    for e in range(16):
        sel[e, e, :] = 1.0
    m["moe_sel"] = sel
    m["moe_jidx"] = np.ascontiguousarray(np.stack([np.arange(128), np.arange(128) + 128], 1).astype(np.float32))
    m["w_out"] = inp["w_out"][:depth]
    cosT, sinT = _rope_tables()
    m["rope_cos"], m["rope_sin"] = cosT, sinT
    return m


_PROG = None


def kernel(**inputs):
    global _PROG
    inp = {k_: np.asarray(v) for k_, v in inputs.items()}
    if _PROG is None:
        P = Prog(depth=DEPTH)
        P.build()
        _PROG = P
    P = _PROG
    shared = host_shared(inp, DEPTH)
    in_maps = []
    for b in range(8):
        m = dict(shared)
        m.update(host_core(inp, b))
        in_maps.append({n: m[n] for n, kd in P.kinds.items() if kd == "ExternalInput"})
    res = run_bass_kernel_spmd(P.nc, in_maps, core_ids=list(range(8)))
    return np.stack([np.asarray(r["out"], dtype=np.float32) for r in res.results], 0)
```

```python
import numpy as np
import ml_dtypes
from contextlib import ExitStack
import concourse.bass as bass
import concourse.mybir as mybir
from concourse.bass_utils import run_bass_kernel_spmd

F32 = mybir.dt.float32
BF16 = mybir.dt.bfloat16
I32 = mybir.dt.int32
F32R = mybir.dt.float32r


def fr(ap):
    return ap.bitcast(F32R)
AF = mybir.ActivationFunctionType
ALU = mybir.AluOpType
AX = mybir.AxisListType


class R:
    __slots__ = ("w", "r", "name", "excl")

    def __init__(self, name="", excl=False):
        self.w = []
        self.r = []
        self.name = name
        self.excl = excl


def PR():
    return R("psum", True)


class K:
    ENG = ("tensor", "vector", "scalar", "gpsimd", "sync")
    NDMA = 24
    MAXSEM = 30000

    def __init__(self, nc, es):
        self.nc = nc
        self.es = es
        self.prog = {e: [] for e in self.ENG}
        self.sem = {}
        self.epoch = {e: 0 for e in self.ENG}
        for e in self.ENG:
            self.sem[(e, 0)] = es.enter_context(nc.semaphore("s_%s_0" % e))
        self.cnt = {e: 0 for e in self.ENG}
        self.waited = {e: {} for e in self.ENG}
        self.dsem = [es.enter_context(nc.semaphore("d%d" % i)) for i in range(self.NDMA)]
        self.dval = [0] * self.NDMA
        self.dnext = 0
        self.ninstr = 0

    def sb(self, name, shape, dt=F32, es=None):
        self.uid = getattr(self, "uid", 0) + 1
        name = "%s_u%d" % (name, self.uid)
        return (es or self.es).enter_context(self.nc.sbuf_tensor(name, list(shape), dt))

    def ps(self, name, shape, dt=F32, es=None):
        n = int(np.prod(shape[1:])) * (2 if dt == BF16 else 4)
        assert n % 2048 == 0, (name, shape, n)
        self.uid = getattr(self, "uid", 0) + 1
        name = "%s_u%d" % (name, self.uid)
        return (es or self.es).enter_context(self.nc.psum_tensor(name, list(shape), dt))

    def _key(self, dep):
        return dep[0] if dep[0] in self.ENG else ("d", dep[0])

    def _collect(self, reads, writes):
        deps = {}
        for r in reads:
            for d in r.w:
                k = d[0]
                if deps.get(k, 0) < d[1]:
                    deps[k] = d[1]
        for w in writes:
            for d in w.w + w.r:
                k = d[0]
                if deps.get(k, 0) < d[1]:
                    deps[k] = d[1]
        return deps

    def _emit_waits(self, eng, deps):
        wd = self.waited[eng]
        for k, v in deps.items():
            if wd.get(k, 0) >= v:
                continue
            wd[k] = v
            sem = self.sem[k] if isinstance(k, tuple) else self.dsem[k]
            getattr(self.nc, eng).wait_ge(sem, v)

    def _mark(self, dep, reads, writes):
        for r in reads:
            r.r.append(dep)
        for w in writes:
            w.w = [dep]
            w.r = []

    def op(self, eng, fn, reads=(), writes=()):
        if any(r.excl for r in reads):
            writes = list(writes) + [r for r in reads if r.excl]
            reads = [r for r in reads if not r.excl]
        deps = self._collect(reads, writes)
        self._emit_waits(eng, deps)
        if self.cnt[eng] >= self.MAXSEM:
            self.epoch[eng] += 1
            self.cnt[eng] = 0
            self.sem[(eng, self.epoch[eng])] = self.es.enter_context(
                self.nc.semaphore("s_%s_%d" % (eng, self.epoch[eng])))
        self.cnt[eng] += 1
        key = (eng, self.epoch[eng])
        dep = (key, self.cnt[eng])
        fn(getattr(self.nc, eng)).then_inc(self.sem[key], 1)
        self._mark(dep, reads, writes)
        self.ninstr += 1

    STORE_Q = "scalar"

    def dma(self, out, in_, reads=(), writes=(), q="sync", **kw):
        q = self.STORE_Q if str(out.space) == "DRAM" else "sync"
        deps = self._collect(reads, writes)
        i = self.dnext
        self.dnext = (self.dnext + 1) % self.NDMA
        if self.dval[i] > 0:
            deps[i] = max(deps.get(i, 0), self.dval[i])
        self._emit_waits(q, deps)
        self.dval[i] += 16
        dep = (i, self.dval[i])
        getattr(self.nc, q).dma_start(out=out, in_=in_, **kw).then_inc(self.dsem[i], 16)
        self._mark(dep, reads, writes)
        self.ninstr += 1

    def finish(self, final_regions):
        deps = {}
        for e in self.ENG:
            if self.cnt[e]:
                deps[(e, self.epoch[e])] = self.cnt[e]
        for i in range(self.NDMA):
            if self.dval[i]:
                deps[i] = self.dval[i]
        self._emit_waits("sync", deps)

    def barrier(self):
        deps = {}
        for e in self.ENG:
            if self.cnt[e]:
                deps[(e, self.epoch[e])] = self.cnt[e]
        for i in range(self.NDMA):
            if self.dval[i]:
                deps[i] = self.dval[i]
        for e in self.ENG:
            self._emit_waits(e, dict(deps))


D = 2048
L = 2048
NCTX = 256
T = L + NCTX
NT = T // 128
DEPTH = 2
ZC = 12832
C_NAK, C_NAV, C_CKV, C_KPE, C_RWKS = 0, 512, 1024, 1280, 1344
C_NAQ, C_CQ, C_RWQS, C_FN, C_GATE = 2496, 3008, 3520, 4128, 4640
EPS = 1e-6


def bf16_np(a):
    return np.asarray(a, dtype=np.float32).astype(ml_dtypes.bfloat16)


class Prog:
    def __init__(self, depth=DEPTH, ext=None, stages=None):
        self.depth = depth
        self.ext = ext or {}
        self.stages = stages
        self.nc = bass.Bass("TRN2", target_bir_lowering=False)
        self.es = ExitStack()
        self.k = K(self.nc, self.es)
        self.dr = {}
        self.kinds = {}
        self.modT = {}
        self.final_out = True
        import os
        self.dbg = {kk[4:]: int(v) for kk, v in os.environ.items() if kk.startswith("DBG_")}

    def dram(self, name, shape, dt=F32, kind="Internal"):
        kind = self.ext.get(name, kind)
        t = self.nc.dram_tensor(name, list(shape), dt, kind=kind)
        self.dr[name] = (t.ap(), R(name))
        self.kinds[name] = kind
        return self.dr[name]

    def want(self, s):
        return self.stages is None or s in self.stages

    def consts(self):
        k = self.k
        self.identf = k.sb("identf", [128, 128], F32)
        self.ident = k.sb("ident", [128, 128], BF16)
        self.r_ident = R("ident")
        k.op("gpsimd", lambda e: e.memset(self.identf[:], 0.0), writes=[self.r_ident])
        k.op("gpsimd", lambda e: e.affine_select(out=self.identf[:], in_=self.identf[:], pattern=[[-1, 128]],
                                                  compare_op=ALU.not_equal, fill=1.0, base=0, channel_multiplier=1),
             reads=[self.r_ident], writes=[self.r_ident])
        k.op("vector", lambda e: e.tensor_copy(out=self.ident[:], in_=self.identf[:]),
             reads=[self.r_ident], writes=[self.r_ident])
        ap, r = self.dram("cvT", [128, 16, 2], F32, "ExternalInput")
        self.sT = k.sb("sT", [128, 16, 2], F32)
        self.r_sT = R("sT")
        k.dma(self.sT[:], ap, reads=[r], writes=[self.r_sT])
        k.op("scalar", lambda e: e.activation(out=self.sT[:], in_=self.sT[:], func=AF.Silu),
             reads=[self.r_sT], writes=[self.r_sT])

    def stage_mod(self, l):
        k = self.k
        adaw, r_adaw = self.dr["ada_w"]
        adab, r_adab = self.dr["ada_b2"]
        modD, r_modD = self.dram("modD%d" % l, [2, 6 * D], F32)
        modT = k.sb("modT%d" % l, [128, 6, 16, 2], F32)
        r_modT = R()
        self.modT[l] = (modT, r_modT)
        with ExitStack() as es:
            wt = [k.sb("mod_w%d" % i, [128, 2048], F32, es) for i in range(3)]
            r_wt = [R() for _ in range(3)]
            ps = k.ps("mod_ps", [2, 2048], F32, es)
            r_ps = PR()
            pT = k.ps("mod_pT", [128, 4, 128], F32, es)
            r_pT = PR()
            row = [k.sb("mod_row%d" % i, [128, 2048], F32, es) for i in range(2)]
            r_row = [R() for _ in range(2)]
            for i in range(2):
                k.op("gpsimd", lambda e, i=i: e.memset(row[i][:], 0.0), writes=[r_row[i]])
            bias = k.sb("mod_bias", [2, 6 * D], F32, es)
            r_bias = R()
            k.dma(bias[:], adab[l], reads=[r_adab], writes=[r_bias])
            i2 = self.identf[0:2, 0:2]
            it = 0
            for nb in range(6):
                for kc in range(16):
                    w = wt[it % 3]
                    rw = r_wt[it % 3]
                    it += 1
                    k.dma(w[:], adaw[l, kc * 128:(kc + 1) * 128, nb * 2048:(nb + 1) * 2048], reads=[r_adaw], writes=[rw])

                    def mm(e, w=w, kc=kc):
                        last = None
                        for j in range(4):
                            last = e.matmul(ps[:, j * 512:(j + 1) * 512], lhsT=self.sT[:, kc, :], rhs=w[:, j * 512:(j + 1) * 512],
                                            start=(kc == 0), stop=(kc == 15))
                        return last
                    k.op("tensor", mm, reads=[rw, self.r_sT], writes=[r_ps])
                rt = row[nb % 2]
                rr = r_row[nb % 2]
                k.op("vector", lambda e, rt=rt, nb=nb: e.tensor_tensor(out=rt[0:2, :], in0=ps[:], in1=bias[:, nb * 2048:(nb + 1) * 2048], op=ALU.add),
                     reads=[r_ps, r_bias], writes=[rr])
                k.dma(modD[:, nb * 2048:(nb + 1) * 2048], rt[0:2, :], reads=[rr], writes=[r_modD], q="gpsimd")

                for g in range(4):
                    def tr(e, rt=rt, g=g):
                        last = None
                        for c in range(4):
                            cc = g * 4 + c
                            last = e.transpose(out=pT[:, c, :], in_=rt[:, cc * 128:(cc + 1) * 128], identity=self.identf[:])
                        return last
                    k.op("tensor", tr, reads=[rr, self.r_ident], writes=[r_pT])
                    k.op("vector", lambda e, nb=nb, g=g: e.tensor_copy(out=modT[:, nb, g * 4:(g + 1) * 4, :], in_=pT[:, :, 0:2]),
                         reads=[r_pT], writes=[r_modT])
            k.barrier()

    def stage_normz(self, l, xname):
        k = self.k
        xin, r_xin = self.dr[xname]
        win, r_win = self.dr["w_in"]
        g1T_d, r_g1T = self.dr["norm1_gT"]
        z, r_z = self.dram("z%d" % l, [T, ZC], F32)
        self.r_ztile = [R() for _ in range(NT)]
        modT, r_modT = self.modT[l]
        with ExitStack() as es:
            hT = k.sb("hT", [128, 16, T], BF16, es)
            r_hT = [[R() for _ in range(16)] for _ in range(NT)]
            aT = k.sb("nz_aT", [128, 16, 2], F32, es)
            r_aT = R()
            gT = k.sb("nz_gT", [128, 16], F32, es)
            k.dma(gT[:], g1T_d[l], reads=[r_g1T], writes=[r_aT])
            k.op("vector", lambda e: e.tensor_scalar(out=aT[:], in0=modT[:, 1, :, :], scalar1=1.0, scalar2=None, op0=ALU.add),
                 reads=[r_modT], writes=[r_aT])
            k.op("vector", lambda e: e.tensor_tensor(out=aT[:], in0=aT[:], in1=gT[:].unsqueeze(2).to_broadcast([128, 16, 2]), op=ALU.mult),
                 reads=[r_aT], writes=[r_aT])
            self.norm_to_hT(es, xin, r_xin, hT, r_hT, aT, r_aT, modT[:, 0, :, :], r_modT, "nz")
            wf = [k.sb("nz_wf%d" % i, [128, 4, 512], F32, es) for i in range(3)]
            r_wf = [R() for _ in range(3)]
            wb = [k.sb("nz_wb%d" % i, [128, 16, 512], BF16, es) for i in range(2)]
            r_wb = [[R() for _ in range(4)] for _ in range(2)]
            pz = [k.ps("nz_pz%d" % i, [128, 512], F32, es) for i in range(4)]
            r_pz = [PR() for _ in range(4)]
            zo = [k.sb("nz_zo%d" % i, [128, 512], F32, es) for i in range(4)]
            r_zo = [R() for _ in range(4)]
            ncb = (ZC + 511) // 512
            cnt = 0
            ld = 0
            for cb in range(self.dbg.get("NZ_NCB", ncb)):
                c0 = cb * 512
                cw = min(512, ZC - c0)
                b, rb = wb[cb % 2], r_wb[cb % 2]
                for g in range(4):
                    f, rf = wf[ld % 3], r_wf[ld % 3]
                    k.dma(f[:, :, 0:cw], win[l, g * 512:(g + 1) * 512, c0:c0 + cw].rearrange("(kc p) n -> p kc n", p=128),
                          reads=[r_win], writes=[rf])
                    eng = ("vector", "scalar")[ld % 2]
                    ld += 1
                    if eng == "scalar":
                        k.op("scalar", lambda e, f=f, b=b, cw=cw, g=g: e.activation(out=b[:, g * 4:(g + 1) * 4, 0:cw], in_=f[:, :, 0:cw], func=AF.Copy),
                             reads=[rf], writes=[rb[g]])
                    else:
                        k.op(eng, lambda e, f=f, b=b, cw=cw, g=g: e.tensor_copy(out=b[:, g * 4:(g + 1) * 4, 0:cw], in_=f[:, :, 0:cw]),
                             reads=[rf], writes=[rb[g]])
                for t in range(self.dbg.get("NZ_NT", NT)):
                    p, rp = pz[cnt % 4], r_pz[cnt % 4]
                    o, ro = zo[cnt % 4], r_zo[cnt % 4]
                    cnt += 1

                    def mm(e, p=p, t=t, b=b, cw=cw):
                        last = None
                        for kc in range(16):
                            last = e.matmul(p[:, 0:cw], lhsT=hT[:, kc, t * 128:(t + 1) * 128], rhs=b[:, kc, 0:cw],
                                            start=(kc == 0), stop=(kc == 15))
                        return last
                    k.op("tensor", mm, reads=rb + r_hT[t], writes=[rp])
                    if t % 2 == 0:
                        k.op("vector", lambda e, o=o, p=p, cw=cw: e.tensor_copy(out=o[:, 0:cw], in_=p[:, 0:cw]), reads=[rp], writes=[ro])
                    else:
                        k.op("scalar", lambda e, o=o, p=p, cw=cw: e.activation(out=o[:, 0:cw], in_=p[:, 0:cw], func=AF.Copy), reads=[rp], writes=[ro])
                    k.dma(z[t * 128:(t + 1) * 128, c0:c0 + cw], o[:, 0:cw], reads=[ro], writes=[self.r_ztile[t]], q="gpsimd")
            k.barrier()

    def norm_to_hT(self, es, xin, r_xin, hT, r_hT, aT, r_aT, shT, r_shT, pfx, h_tok=None):
        k = self.k
        xt = [k.sb(pfx + "_xt%d" % i, [128, D], F32, es) for i in range(2)]
        r_xt = [R() for _ in range(2)]
        xn = [k.sb(pfx + "_xn%d" % i, [128, D], BF16, es) for i in range(2)]
        r_xn = [R() for _ in range(2)]
        junk = k.sb(pfx + "_junk", [128, D], BF16, es)
        r_junk = R()
        st = k.sb(pfx + "_st", [128, NT, 2], F32, es)
        r_stl = [R() for _ in range(NT)]
        pt = [k.ps(pfx + "_pt%d" % i, [128, 8, 128], BF16, es) for i in range(2)]
        r_pt = [PR() for _ in range(2)]
        for t in range(self.dbg.get("NZ_NT", NT)):
            row = 1 if t < 2 else 0
            r_st = r_stl[t]
            x_, rx = xt[t % 2], r_xt[t % 2]
            n_, rn = xn[t % 2], r_xn[t % 2]
            STEP = self.dbg.get("STEP", 99)
            if STEP < 1:
                continue
            k.dma(x_[:], xin[t * 128:(t + 1) * 128, :], reads=[r_xin], writes=[rx])
            if STEP < 2:
                continue
            k.op("scalar", lambda e, x_=x_, t=t: e.activation(out=junk[:], in_=x_[:], func=AF.Square, accum_out=st[:, t, 0:1]),
                 reads=[rx], writes=[r_junk, r_st])
            if STEP < 3:
                continue
            k.op("scalar", lambda e, t=t: e.activation(out=st[:, t, 1:2], in_=st[:, t, 0:1], func=AF.Sqrt, bias=self.eps_t[:, 0:1], scale=1.0 / D),
                 reads=[r_st], writes=[r_st])
            if STEP < 4:
                continue
            k.op("vector", lambda e, t=t: e.reciprocal(out=st[:, t, 1:2], in_=st[:, t, 1:2]), reads=[r_st], writes=[r_st])
            if STEP < 5:
                continue
            k.op("vector", lambda e, x_=x_, n_=n_, t=t: e.tensor_scalar(out=n_[:], in0=x_[:], scalar1=st[:, t, 1:2], scalar2=None, op0=ALU.mult),
                 reads=[rx, r_st], writes=[rn])
            if STEP < 6:
                continue
            for half in range(2):
                p, rp = pt[half], r_pt[half]

                def tr(e, p=p, n_=n_, half=half):
                    last = None
                    for j in range(8):
                        dc = half * 8 + j
                        last = e.transpose(out=p[:, j, :], in_=n_[:, dc * 128:(dc + 1) * 128], identity=self.ident[:])
                    return last
                k.op("tensor", tr, reads=[rn, self.r_ident], writes=[rp])
                if STEP < 7:
                    continue
                for j in range(8):
                    dc = half * 8 + j
                    if STEP == 7 and j % 2 == 1:
                        continue
                    if STEP == 8 and j % 2 == 0:
                        continue
                    eng = "vector" if half == 0 else "scalar"
                    if eng == "vector":
                        k.op("vector", lambda e, p=p, j=j, dc=dc, t=t, row=row: e.tensor_scalar(
                            out=hT[:, dc, t * 128:(t + 1) * 128], in0=p[:, j, :], scalar1=aT[:, dc, row:row + 1],
                            scalar2=shT[:, dc, row:row + 1], op0=ALU.mult, op1=ALU.add),
                            reads=[rp, r_aT, r_shT], writes=[r_hT[t][dc]])
                    else:
                        k.op("scalar", lambda e, p=p, j=j, dc=dc, t=t, row=row: e.activation(
                            out=hT[:, dc, t * 128:(t + 1) * 128], in_=p[:, j, :], func=AF.Identity,
                            scale=aT[:, dc, row:row + 1], bias=shT[:, dc, row:row + 1]),
                            reads=[rp, r_aT, r_shT], writes=[r_hT[t][dc]])

    def head_rms(self, eng_sq, x_, w, nh, dh, st, junk, out, gain_bc, reads, r_st, r_junk, writes, tmp, r_tmp):
        k = self.k
        x3 = x_.rearrange("p (h d) -> p h d", h=nh)
        k.op("gpsimd", lambda e: e.tensor_tensor(out=tmp, in0=x_, in1=gain_bc, op=ALU.mult), reads=reads + [self.r_gain], writes=[r_tmp])
        k.op("vector", lambda e: e.tensor_tensor(out=junk, in0=x_, in1=x_, op=ALU.mult), reads=reads, writes=[r_junk])
        k.op("vector", lambda e: e.tensor_reduce(out=st, in_=junk.rearrange("p (h d) -> p h d", h=nh), axis=AX.X, op=ALU.add),
             reads=[r_junk], writes=[r_st])
        k.op("scalar", lambda e: e.activation(out=st, in_=st, func=AF.Sqrt, bias=self.eps_t[:, 0:1], scale=1.0 / dh),
             reads=[r_st], writes=[r_st])
        k.op("vector", lambda e: e.reciprocal(out=st, in_=st), reads=[r_st], writes=[r_st])
        k.op("vector", lambda e: e.tensor_tensor(out=out.rearrange("p (h d) -> p h d", h=nh), in0=tmp.rearrange("p (h d) -> p h d", h=nh),
                                                in1=st.unsqueeze(2).to_broadcast([128, nh, dh]), op=ALU.mult),
             reads=[r_tmp, r_st], writes=writes)

    def stage_na(self, l):
        k = self.k
        z, _ = self.dr["z%d" % l]
        r_zt = self.r_ztile
        yd, r_yd = self.dram("y_na%d" % l, [T, 512], F32)
        gq_d, r_gq = self.dr["na_qg"]
        gk_d, r_gk = self.dr["na_kg"]
        bias_d, r_bias_d = self.dr["na_bias"]
        with ExitStack() as es:
            QT = k.sb("na_QT", [128, 4, T], BF16, es)
            KT = k.sb("na_KT", [128, 4, T], BF16, es)
            r_QT = [R() for _ in range(NT)]
            r_KT = [R() for _ in range(NT)]
            Va = k.sb("na_V", [128, NT, 8, 65], BF16, es)
            r_Va = [R() for _ in range(NT)]
            yna = k.sb("na_y", [128, NT, 512], F32, es)
            r_yna = [R() for _ in range(NT)]
            gq = k.sb("na_gq", [128, 512], F32, es)
            gk = k.sb("na_gk", [128, 512], F32, es)
            self.r_gain = R()
            k.dma(gq[:], gq_d[l].partition_broadcast(128), reads=[r_gq], writes=[self.r_gain])
            k.dma(gk[:], gk_d[l].partition_broadcast(128), reads=[r_gk], writes=[self.r_gain])
            k.op("vector", lambda e: e.tensor_scalar(out=gq[:], in0=gq[:], scalar1=0.125, scalar2=None, op0=ALU.mult),
                 reads=[self.r_gain], writes=[self.r_gain])
            r_ones = R()
            k.op("gpsimd", lambda e: e.memset(Va[:, :, :, 64:65], 1.0), writes=r_Va)
            xin = [k.sb("na_x%d" % i, [128, 512], F32, es) for i in range(6)]
            r_xin = [R() for _ in range(6)]
            junk = k.sb("na_junk", [128, 512], F32, es)
            r_junk = R()
            tmp = k.sb("na_tmp", [128, 512], F32, es)
            r_tmp = R()
            st = k.sb("na_st", [128, NT, 2, 8], F32, es)
            xb = [k.sb("na_xb%d" % i, [128, 512], BF16, es) for i in range(2)]
            r_xb = [R() for _ in range(2)]
            ptr = [k.ps("na_ptr%d" % i, [128, 8, 128], BF16, es) for i in range(1)]
            r_ptr = [PR() for _ in range(1)]
            ld = 0
            nb = 0
            for t in range(NT):
                rows = slice(t * 128, (t + 1) * 128)
                for which, c0 in ((0, C_NAQ), (1, C_NAK)):
                    x_, rx = xin[ld % 6], r_xin[ld % 6]
                    ld += 1
                    k.dma(x_[:], z[rows, c0:c0 + 512], reads=[r_zt[t]], writes=[rx])
                    b_, rb = xb[nb % 2], r_xb[nb % 2]
                    nb += 1
                    r_st = R()
                    self.head_rms(None, x_[:], 512, 8, 64, st[:, t, which, :], junk[:], b_[:], (gq if which == 0 else gk)[:],
                                  [rx], r_st, r_junk, [rb], tmp[:], r_tmp)
                    p, rp = ptr[0], r_ptr[0]

                    def tr(e, p=p, b_=b_):
                        last = None
                        for j in range(4):
                            last = e.transpose(out=p[:, j, :], in_=b_[:, j * 128:(j + 1) * 128], identity=self.ident[:])
                        return last
                    k.op("tensor", tr, reads=[rb, self.r_ident], writes=[rp])
                    dst = QT if which == 0 else KT
                    rd = (r_QT if which == 0 else r_KT)[t]
                    k.op("scalar", lambda e, dst=dst, p=p, rows=rows: e.activation(out=dst[:, :, rows], in_=p[:, 0:4, :], func=AF.Copy),
                         reads=[rp], writes=[rd])
                x_, rx = xin[ld % 6], r_xin[ld % 6]
                ld += 1
                k.dma(x_[:], z[rows, C_NAV:C_NAV + 512], reads=[r_zt[t]], writes=[rx])
                k.op("vector", lambda e, x_=x_, t=t: e.tensor_copy(out=Va[:, t, :, 0:64], in_=x_[:].rearrange("p (h d) -> p h d", h=8)),
                     reads=[rx], writes=[r_Va[t]])
            bias = [k.sb("na_bias%d" % i, [128, 5, 5, 128], F32, es) for i in range(2)]
            r_bias = [R() for _ in range(2)]
            stp = [k.ps("na_st%d" % i, [128, 8, 128], F32, es) for i in range(2)]
            r_stp = [PR() for _ in range(2)]
            po = [k.ps("na_po%d" % i, [128, 512], F32, es) for i in range(2)]
            r_po = [PR() for _ in range(2)]
            sbt = [k.sb("na_sb%d" % i, [128, 5, 128], F32, es) for i in range(2)]
            r_sbt = [R() for _ in range(2)]
            PT = [k.sb("na_PT%d" % i, [128, 7, 128], BF16, es) for i in range(2)]
            r_PT = [R() for _ in range(2)]
            rc = k.sb("na_rc", [128, 2], F32, es)
            r_rc = [R(), R()]
            it = 0
            for h in range(8):
                bs, rbs = bias[h % 2], r_bias[h % 2]
                for c in range(5):
                    k.dma(bs[:, c, :, :], bias_d[l, h, c].rearrange("t k q -> k t q"), reads=[r_bias_d], writes=[rbs])
                hp, hs = h // 2, (h % 2) * 64
                for t in range(NT):
                    if t < 2:
                        kts = [0, 1]
                        nbias = 0
                        cls = 0
                    else:
                        j = t - 2
                        lo = min(max(j - 2, 0), 11)
                        kts = [lo + 2 + i for i in range(5)] + [0, 1]
                        nbias = 5
                        cls = {0: 0, 1: 1, 14: 3, 15: 4}.get(j, 2)
                    nk = len(kts)
                    sp, rsp = stp[it % 2], r_stp[it % 2]
                    pp, rpp = PT[it % 2], r_PT[it % 2]
                    sb_, rsb = sbt[it % 2], r_sbt[it % 2]
                    o_, ro = po[it % 2], r_po[it % 2]
                    rcc, rrc = rc[:, it % 2:it % 2 + 1], r_rc[it % 2]
                    it += 1

                    def qk(e, sp=sp, kts=kts, t=t, hp=hp, hs=hs):
                        last = None
                        for i, kt in enumerate(kts):
                            last = e.matmul(sp[:, i, :], lhsT=KT[hs:hs + 64, hp, kt * 128:(kt + 1) * 128],
                                            rhs=QT[hs:hs + 64, hp, t * 128:(t + 1) * 128], start=True, stop=True)
                        return last
                    k.op("tensor", qk, reads=[r_KT[kt] for kt in kts] + [r_QT[t]], writes=[rsp])
                    if nbias:
                        k.op("vector", lambda e, sb_=sb_, sp=sp, bs=bs, cls=cls: e.tensor_tensor(out=sb_[:], in0=sp[:, 0:5, :], in1=bs[:, cls, :, :], op=ALU.add),
                             reads=[rsp, rbs], writes=[rsb])
                        k.op("scalar", lambda e, pp=pp, sb_=sb_: e.activation(out=pp[:, 0:5, :], in_=sb_[:], func=AF.Exp), reads=[rsb], writes=[rpp])
                        k.op("scalar", lambda e, pp=pp, sp=sp: e.activation(out=pp[:, 5:7, :], in_=sp[:, 5:7, :], func=AF.Exp), reads=[rsp, rpp], writes=[rpp])
                    else:
                        k.op("scalar", lambda e, pp=pp, sp=sp: e.activation(out=pp[:, 0:2, :], in_=sp[:, 0:2, :], func=AF.Exp), reads=[rsp], writes=[rpp])

                    def pv(e, o_=o_, pp=pp, kts=kts, h=h, nk=nk):
                        last = None
                        for i, kt in enumerate(kts):
                            last = e.matmul(o_[:, 0:65], lhsT=pp[:, i, :], rhs=Va[:, kt, h, :], start=(i == 0), stop=(i == nk - 1))
                        return last
                    k.op("tensor", pv, reads=[rpp] + [r_Va[kt] for kt in kts], writes=[ro])
                    k.op("vector", lambda e, rcc=rcc, o_=o_: e.reciprocal(out=rcc, in_=o_[:, 64:65]), reads=[ro], writes=[rrc])
                    k.op("vector", lambda e, rcc=rcc, o_=o_, t=t, h=h: e.tensor_scalar(out=yna[:, t, h * 64:(h + 1) * 64], in0=o_[:, 0:64],
                                                                                   scalar1=rcc, scalar2=None, op0=ALU.mult),
                         reads=[ro, rrc], writes=[r_yna[t]])
            for t in range(NT):
                k.dma(yd[t * 128:(t + 1) * 128, :], yna[:, t, :], reads=[r_yna[t]], writes=[r_yd], q="gpsimd")
            k.barrier()

    def stage_mla(self, l):
        k = self.k
        z, _ = self.dr["z%d" % l]
        r_zt = self.r_ztile
        yd, r_yd = self.dram("y_mla%d" % l, [T, 512], F32)
        MSC = 192.0 ** -0.5
        with ExitStack() as es:
            QTn = k.sb("ml_QTn", [128, 4, T], BF16, es)
            KTn = k.sb("ml_KTn", [128, 4, T], BF16, es)
            QTp = k.sb("ml_QTp", [128, 2, T], BF16, es)
            KTp = k.sb("ml_KTp", [128, 2, T], BF16, es)
            r_Q = [R() for _ in range(NT)]
            r_K = [R() for _ in range(NT)]
            Va = k.sb("ml_V", [128, NT, 4, 129], BF16, es)
            r_Va = [R() for _ in range(NT)]
            ym = k.sb("ml_y", [128, NT, 512], F32, es)
            r_ym = [R() for _ in range(NT)]
            k.op("gpsimd", lambda e: e.memset(Va[:, :, :, 128:129], 1.0), writes=r_Va)
            with ExitStack() as es2:
                self.r_gain = R()
                g_cq = k.sb("ml_gcq", [128, 512], F32, es2)
                g_ckv = k.sb("ml_gckv", [128, 256], F32, es2)
                g_q = k.sb("ml_gq", [128, 768], F32, es2)
                g_kn = k.sb("ml_gkn", [128, 512], F32, es2)
                g_kp = k.sb("ml_gkp", [128, 64], F32, es2)
                for tl, nm in ((g_cq, "mla_cqg"), (g_ckv, "mla_ckvg"), (g_q, "mla_qg"), (g_kn, "mla_kng"), (g_kp, "mla_kpg")):
                    ap, rr = self.dr[nm]
                    k.dma(tl[:], ap[l].partition_broadcast(128), reads=[rr], writes=[self.r_gain])
                k.op("vector", lambda e: e.tensor_scalar(out=g_q[:], in0=g_q[:], scalar1=MSC, scalar2=None, op0=ALU.mult),
                     reads=[self.r_gain], writes=[self.r_gain])
                wuq = k.sb("ml_wuq", [128, 4, 768], BF16, es2)
                wukv = k.sb("ml_wukv", [128, 2, 1024], BF16, es2)
                r_w = R()
                wst = k.sb("ml_wst", [128, 4, 768], F32, es2)
                r_wst = R()
                ap, rr = self.dr["mla_w_uq"]
                k.dma(wst[:], ap[l].rearrange("(kc p) n -> p kc n", p=128), reads=[rr], writes=[r_wst])
                k.op("vector", lambda e: e.tensor_copy(out=wuq[:], in_=wst[:]), reads=[r_wst], writes=[r_w])
                ap, rr = self.dr["mla_w_ukv"]
                wst2 = wst[:].rearrange("p a b -> p (a b)")[:, 0:2048].rearrange("p (a b) -> p a b", a=2)
                k.dma(wst2, ap[l].rearrange("(kc p) n -> p kc n", p=128), reads=[rr], writes=[r_wst])
                k.op("vector", lambda e: e.tensor_copy(out=wukv[:], in_=wst2), reads=[r_wst], writes=[r_w])
                cos_d, r_cos = self.dr["rope_cos"]
                sin_d, r_sin = self.dr["rope_sin"]
                xin = [k.sb("ml_x%d" % i, [128, 832], F32, es2) for i in range(2)]
                r_xin = [R() for _ in range(2)]
                cs = [k.sb("ml_cs%d" % i, [128, 2, 64], F32, es2) for i in range(2)]
                r_cs = [R() for _ in range(2)]
                junk = k.sb("ml_junk", [128, 1024], F32, es2)
                r_junk = R()
                tmp = k.sb("ml_tmp", [128, 1024], F32, es2)
                r_tmp = R()
                st = k.sb("ml_st", [128, NT, 16], F32, es2)
                cb = k.sb("ml_cb", [128, 768], BF16, es2)
                r_cb = [R(), R()]
                cT = k.sb("ml_cT", [128, 6, 128], BF16, es2)
                r_cT = [R(), R()]
                qf = k.sb("ml_qf", [128, 768], F32, es2)
                r_qf = R()
                qn = k.sb("ml_qn", [128, 768], F32, es2)
                r_qn = R()
                kvf = k.sb("ml_kvf", [128, 1024], F32, es2)
                r_kvf = R()
                qb = k.sb("ml_qb", [128, 4, 128], BF16, es2)
                qpb = k.sb("ml_qpb", [128, 4, 64], BF16, es2)
                kb = k.sb("ml_kb", [128, 4, 128], BF16, es2)
                kpb = k.sb("ml_kpb", [128, 4, 64], BF16, es2)
                r_qb, r_qpb, r_kb, r_kpb = R(), R(), R(), R()
                rp1 = k.sb("ml_rp1", [128, 4, 64], F32, es2)
                rp2 = k.sb("ml_rp2", [128, 4, 64], F32, es2)
                r_rp1, r_rp2 = R(), R()
                kpg = k.sb("ml_kpgt", [128, 64], F32, es2)
                kpr = k.sb("ml_kpr", [128, 64], F32, es2)
                r_kpg, r_kpr = R(), R()
                ptr = k.ps("ml_ptr", [128, 8, 128], BF16, es2)
                r_ptr = PR()
                pq = k.ps("ml_pq", [128, 2, 512], F32, es2)
                r_pq = PR()
                pkv = k.ps("ml_pkv", [128, 2, 512], F32, es2)
                r_pkv = PR()

                def rope(x3, nh, cs_, out3, reads, writes):
                    cosb = cs_[:, 0, :].unsqueeze(1).to_broadcast([128, nh, 64])
                    k.op("vector", lambda e: e.tensor_tensor(out=rp1[:, 0:nh, :], in0=x3, in1=cosb, op=ALU.mult), reads=reads, writes=[r_rp1])
                    x5 = x3.rearrange("p h (g a d) -> p h g a d", g=2, a=2)
                    o5 = rp2[:, 0:nh, :].rearrange("p h (g a d) -> p h g a d", g=2, a=2)
                    s4 = cs_[:, 1, :].rearrange("p (g a d) -> p g a d", g=2, a=2)
                    for a in range(2):
                        k.op("gpsimd", lambda e, a=a: e.tensor_tensor(out=o5[:, :, :, a, :], in0=x5[:, :, :, 1 - a, :],
                                                                      in1=s4[:, :, a, :].unsqueeze(1).to_broadcast([128, nh, 2, 16]), op=ALU.mult),
                             reads=reads, writes=[r_rp2])
                    k.op("vector", lambda e: e.tensor_tensor(out=out3, in0=rp1[:, 0:nh, :], in1=rp2[:, 0:nh, :], op=ALU.add),
                         reads=[r_rp1, r_rp2], writes=writes)

                for t in range(NT):
                    rows = slice(t * 128, (t + 1) * 128)
                    x_, rx = xin[t % 2], r_xin[t % 2]
                    cs_, rcs = cs[t % 2], r_cs[t % 2]
                    k.dma(x_[:, 0:512], z[rows, C_CQ:C_CQ + 512], reads=[r_zt[t]], writes=[rx])
                    k.dma(x_[:, 512:832], z[rows, C_CKV:C_CKV + 320], reads=[r_zt[t]], writes=[rx])
                    k.dma(cs_[:, 0, :], cos_d[rows, :], reads=[r_cos], writes=[rcs])
                    k.dma(cs_[:, 1, :], sin_d[rows, :], reads=[r_sin], writes=[rcs])
                    self.head_rms(None, x_[:, 0:512], 512, 1, 512, st[:, t, 0:1], junk[:, 0:512], cb[:, 0:512], g_cq[:],
                                  [rx], R(), r_junk, [r_cb[0]], tmp[:, 0:512], r_tmp)
                    self.head_rms(None, x_[:, 512:768], 256, 1, 256, st[:, t, 1:2], junk[:, 0:256], cb[:, 512:768], g_ckv[:],
                                  [rx], R(), r_junk, [r_cb[1]], tmp[:, 0:256], r_tmp)

                    def tr1(e):
                        last = None
                        for j in range(6):
                            last = e.transpose(out=ptr[:, j, :], in_=cb[:, j * 128:(j + 1) * 128], identity=self.ident[:])
                        return last
                    k.op("tensor", tr1, reads=r_cb + [self.r_ident], writes=[r_ptr])
                    k.op("scalar", lambda e: e.activation(out=cT[:], in_=ptr[:, 0:6, :], func=AF.Copy), reads=[r_ptr], writes=r_cT)

                    def mmq(e):
                        last = None
                        for hf in range(2):
                            for kc in range(4):
                                last = e.matmul(pq[:, hf, 0:384], lhsT=cT[:, kc, :], rhs=wuq[:, kc, hf * 384:(hf + 1) * 384],
                                                start=(kc == 0), stop=(kc == 3))
                        return last
                    k.op("tensor", mmq, reads=r_cT + [r_w], writes=[r_pq])

                    def mmkv(e):
                        last = None
                        for hf in range(2):
                            for kc in range(2):
                                last = e.matmul(pkv[:, hf, :], lhsT=cT[:, 4 + kc, :], rhs=wukv[:, kc, hf * 512:(hf + 1) * 512],
                                                start=(kc == 0), stop=(kc == 1))
                        return last
                    k.op("tensor", mmkv, reads=r_cT + [r_w], writes=[r_pkv])
                    k.op("scalar", lambda e: e.activation(out=qf[:].rearrange("p (a b) -> p a b", a=2), in_=pq[:, :, 0:384], func=AF.Copy),
                         reads=[r_pq], writes=[r_qf])
                    k.op("vector", lambda e: e.tensor_copy(out=kvf[:].rearrange("p (a b) -> p a b", a=2), in_=pkv[:]), reads=[r_pkv], writes=[r_kvf])
                    self.head_rms(None, qf[:], 768, 4, 192, st[:, t, 2:6], junk[:, 0:768], qn[:], g_q[:], [r_qf], R(), r_junk, [r_qn],
                                  tmp[:, 0:768], r_tmp)
                    qn3 = qn[:].rearrange("p (h d) -> p h d", h=4)
                    k.op("vector", lambda e: e.tensor_copy(out=qb[:], in_=qn3[:, :, 0:128]), reads=[r_qn], writes=[r_qb])
                    rope(qn3[:, :, 128:192], 4, cs_, qpb[:], [r_qn, rcs], [r_qpb])
                    kv4 = kvf[:].rearrange("p (h s d) -> p h s d", h=4, s=2)
                    k.op("vector", lambda e, t=t: e.tensor_copy(out=Va[:, t, :, 0:128], in_=kv4[:, :, 1, :]), reads=[r_kvf], writes=[r_Va[t]])
                    kpe = x_[:, 768:832]
                    r_sk = R()
                    k.op("vector", lambda e: e.tensor_tensor(out=junk[:, 0:512].rearrange("p (h d) -> p h d", h=4), in0=kv4[:, :, 0, :], in1=kv4[:, :, 0, :], op=ALU.mult),
                         reads=[r_kvf], writes=[r_junk])
                    k.op("vector", lambda e, t=t: e.tensor_reduce(out=st[:, t, 6:10], in_=junk[:, 0:512].rearrange("p (h d) -> p h d", h=4), axis=AX.X, op=ALU.add),
                         reads=[r_junk], writes=[r_sk])
                    k.op("vector", lambda e: e.tensor_tensor(out=junk[:, 512:576], in0=kpe, in1=kpe, op=ALU.mult), reads=[rx], writes=[r_junk])
                    k.op("vector", lambda e, t=t: e.tensor_reduce(out=st[:, t, 10:11], in_=junk[:, 512:576], axis=AX.X, op=ALU.add), reads=[r_junk], writes=[r_sk])
                    k.op("vector", lambda e, t=t: e.tensor_scalar(out=st[:, t, 6:10], in0=st[:, t, 6:10], scalar1=st[:, t, 10:11], scalar2=None, op0=ALU.add),
                         reads=[r_sk], writes=[r_sk])
                    k.op("scalar", lambda e, t=t: e.activation(out=st[:, t, 6:10], in_=st[:, t, 6:10], func=AF.Sqrt, bias=self.eps_t[:, 0:1], scale=1.0 / 192),
                         reads=[r_sk], writes=[r_sk])
                    k.op("vector", lambda e, t=t: e.reciprocal(out=st[:, t, 6:10], in_=st[:, t, 6:10]), reads=[r_sk], writes=[r_sk])
                    k.op("vector", lambda e, t=t: e.tensor_tensor(out=tmp[:, 0:512].rearrange("p (h d) -> p h d", h=4), in0=kv4[:, :, 0, :],
                                                                 in1=st[:, t, 6:10].unsqueeze(2).to_broadcast([128, 4, 128]), op=ALU.mult),
                         reads=[r_kvf, r_sk], writes=[r_tmp])
                    k.op("gpsimd", lambda e: e.tensor_tensor(out=kb[:].rearrange("p h d -> p (h d)"), in0=tmp[:, 0:512], in1=g_kn[:], op=ALU.mult),
                         reads=[r_tmp, self.r_gain], writes=[r_kb])
                    k.op("vector", lambda e: e.tensor_tensor(out=kpg[:], in0=kpe, in1=g_kp[:], op=ALU.mult), reads=[rx, self.r_gain], writes=[r_kpg])
                    rope(kpg[:].unsqueeze(1), 1, cs_, kpr[:].unsqueeze(1), [r_kpg, rcs], [r_kpr])
                    k.op("vector", lambda e, t=t: e.tensor_tensor(out=kpb[:], in0=kpr[:].unsqueeze(1).to_broadcast([128, 4, 64]),
                                                                 in1=st[:, t, 6:10].unsqueeze(2).to_broadcast([128, 4, 64]), op=ALU.mult),
                         reads=[r_kpr, r_sk], writes=[r_kpb])
                    for (nb_, pb_, rnb, rpb_, dn, dp, rd) in ((qb, qpb, r_qb, r_qpb, QTn, QTp, r_Q[t]), (kb, kpb, r_kb, r_kpb, KTn, KTp, r_K[t])):
                        def tr2(e, nb_=nb_, pb_=pb_):
                            last = None
                            for h in range(4):
                                last = e.transpose(out=ptr[:, h, :], in_=nb_[:, h, :], identity=self.ident[:])
                            for i in range(2):
                                last = e.transpose(out=ptr[:, 4 + i, :], in_=pb_[:, 2 * i:2 * i + 2, :].rearrange("p a b -> p (a b)"), identity=self.ident[:])
                            return last
                        k.op("tensor", tr2, reads=[rnb, rpb_, self.r_ident], writes=[r_ptr])
                        k.op("scalar", lambda e, dn=dn, rows=rows: e.activation(out=dn[:, :, rows], in_=ptr[:, 0:4, :], func=AF.Copy), reads=[r_ptr], writes=[rd])
                        k.op("vector", lambda e, dp=dp, rows=rows: e.tensor_copy(out=dp[:, :, rows], in_=ptr[:, 4:6, :]), reads=[r_ptr], writes=[rd])
                k.barrier()
            with ExitStack() as es3:
                stp = [k.ps("ml_st%d" % i, [128, 512], F32, es3) for i in range(2)]
                r_stp = [PR() for _ in range(2)]
                acc = [k.ps("ml_acc%d" % i, [128, 512], F32, es3) for i in range(4)]
                r_acc = [PR() for _ in range(4)]
                PT = [k.sb("ml_PT%d" % i, [128, 512], BF16, es3) for i in range(3)]
                r_PT = [R() for _ in range(3)]
                rc = k.sb("ml_rc", [128, 4], F32, es3)
                r_rc = [R() for _ in range(4)]
                groups = [[0, 1]] + [[2 + 4 * g + i for i in range(4)] for g in range(4)]
                it = 0
                for grp in groups:
                    kts = [0, 1] if grp[0] == 0 else list(range(NT))
                    q0 = grp[0] * 128
                    nq = len(grp) * 128
                    for h in range(4):
                        hp, hs = h // 2, (h % 2) * 64
                        for ki, kt in enumerate(kts):
                            sp, rsp = stp[it % 2], r_stp[it % 2]
                            pp, rpp = PT[it % 3], r_PT[it % 3]
                            it += 1

                            def qk(e, sp=sp, kt=kt, h=h, hp=hp, hs=hs, q0=q0, nq=nq):
                                e.matmul(sp[:, 0:nq], lhsT=KTn[:, h, kt * 128:(kt + 1) * 128], rhs=QTn[:, h, q0:q0 + nq], start=True, stop=False)
                                return e.matmul(sp[:, 0:nq], lhsT=KTp[hs:hs + 64, hp, kt * 128:(kt + 1) * 128], rhs=QTp[hs:hs + 64, hp, q0:q0 + nq],
                                                start=False, stop=True)
                            k.op("tensor", qk, reads=[r_K[kt]] + [r_Q[t] for t in grp], writes=[rsp])
                            k.op("scalar", lambda e, pp=pp, sp=sp, nq=nq: e.activation(out=pp[:, 0:nq], in_=sp[:, 0:nq], func=AF.Exp), reads=[rsp], writes=[rpp])
                            for i, t in enumerate(grp):
                                k.op("tensor", lambda e, i=i, pp=pp, kt=kt, h=h, ki=ki, kts=kts: e.matmul(
                                    acc[i][:, 0:129], lhsT=pp[:, i * 128:(i + 1) * 128], rhs=Va[:, kt, h, :], start=(ki == 0), stop=(ki == len(kts) - 1)),
                                    reads=[rpp, r_Va[kt]], writes=[r_acc[i]])
                        for i, t in enumerate(grp):
                            k.op("vector", lambda e, i=i: e.reciprocal(out=rc[:, i:i + 1], in_=acc[i][:, 128:129]), reads=[r_acc[i]], writes=[r_rc[i]])
                            k.op("vector", lambda e, i=i, t=t, h=h: e.tensor_scalar(out=ym[:, t, h * 128:(h + 1) * 128], in0=acc[i][:, 0:128],
                                                                                  scalar1=rc[:, i:i + 1], scalar2=None, op0=ALU.mult),
                                 reads=[r_acc[i], r_rc[i]], writes=[r_ym[t]])
                for t in range(NT):
                    k.dma(yd[t * 128:(t + 1) * 128, :], ym[:, t, :], reads=[r_ym[t]], writes=[r_yd])
                k.barrier()

    def stage_fourier(self, l):
        k = self.k
        z, _ = self.dr["z%d" % l]
        r_zt = self.r_ztile
        yd, r_yd = self.dram("y_fn%d" % l, [T, 512], F32)
        csd, r_csd = self.dr["dft_c"]
        dL, r_dL = self.dr["dft_L"]
        dC, r_dC = self.dr["dft_C"]
        with ExitStack() as es:
            G = k.sb("fn_G", [128, NT, 2, 512], BF16, es)
            r_G = [R() for _ in range(NT)]
            CS = k.sb("fn_CS", [128, 256], BF16, es)
            r_CS = R()
            k.dma(CS[:], csd, reads=[r_csd], writes=[r_CS])
            xin = [k.sb("fn_x%d" % i, [128, 512], F32, es) for i in range(2)]
            r_xin = [R() for _ in range(2)]
            xb = [k.sb("fn_xb%d" % i, [128, 512], BF16, es) for i in range(2)]
            r_xb = [R() for _ in range(2)]
            xT = [k.sb("fn_xT%d" % i, [128, 4, 128], BF16, es) for i in range(2)]
            r_xT = [R() for _ in range(2)]
            ptr = k.ps("fn_ptr", [128, 8, 128], BF16, es)
            r_ptr = PR()
            pg = [k.ps("fn_pg%d" % i, [128, 2, 256], F32, es) for i in range(2)]
            r_pg = [PR() for _ in range(2)]
            py = [k.ps("fn_py%d" % i, [128, 512], F32, es) for i in range(2)]
            r_py = [PR() for _ in range(2)]
            for t in range(NT):
                x_, rx = xin[t % 2], r_xin[t % 2]
                b_, rb = xb[t % 2], r_xb[t % 2]
                T_, rT = xT[t % 2], r_xT[t % 2]
                k.dma(x_[:], z[t * 128:(t + 1) * 128, C_FN:C_FN + 512], reads=[r_zt[t]], writes=[rx])
                k.op("gpsimd", lambda e, x_=x_, b_=b_: e.tensor_copy(out=b_[:], in_=x_[:]), reads=[rx], writes=[rb])

                def tr(e, b_=b_):
                    last = None
                    for g in range(4):
                        last = e.transpose(out=ptr[:, g, :], in_=b_[:, g * 128:(g + 1) * 128], identity=self.ident[:])
                    return last
                k.op("tensor", tr, reads=[rb, self.r_ident], writes=[r_ptr])
                k.op("scalar", lambda e, T_=T_: e.activation(out=T_[:], in_=ptr[:, 0:4, :], func=AF.Copy), reads=[r_ptr], writes=[rT])
                for hf in range(2):
                    p, rp = pg[hf], r_pg[hf]

                    def mm(e, p=p, T_=T_, hf=hf):
                        last = None
                        for gg in range(2):
                            last = e.matmul(p[:, gg, :], lhsT=T_[:, hf * 2 + gg, :], rhs=CS[:], start=True, stop=True)
                        return last
                    k.op("tensor", mm, reads=[rT, r_CS], writes=[rp])
                    G4 = G[:, t, :, :].rearrange("p c (g d) -> p c g d", g=4)
                    if hf == 0:
                        k.op("vector", lambda e, p=p, G4=G4, hf=hf: e.tensor_copy(out=G4[:, :, 0:2, :].rearrange("p c g d -> p g c d"),
                                                                                in_=p[:].rearrange("p g (c d) -> p g c d", c=2)),
                             reads=[rp], writes=[r_G[t]])
                    else:
                        k.op("scalar", lambda e, p=p, G4=G4, hf=hf: e.activation(out=G4[:, :, 2:4, :].rearrange("p c g d -> p g c d"),
                                                                                in_=p[:].rearrange("p g (c d) -> p g c d", c=2), func=AF.Copy),
                             reads=[rp, r_G[t]], writes=[r_G[t]])
            Dm = [k.sb("fn_D%d" % i, [128, 2, 16, 128], BF16, es) for i in range(2)]
            r_Dm = [R() for _ in range(2)]
            yo = [k.sb("fn_yo%d" % i, [128, 512], F32, es) for i in range(2)]
            r_yo = [R() for _ in range(2)]
            for to in range(NT):
                ctx = to < 2
                nin = 2 if ctx else 16
                t0 = 0 if ctx else 2
                src = dC if ctx else dL
                rsrc = r_dC if ctx else r_dL
                col0 = (to - t0) * 128
                D_, rD = Dm[to % 2], r_Dm[to % 2]
                for c in range(2):
                    k.dma(D_[:, c, 0:nin, :], src[c, :, col0:col0 + 128].rearrange("(nt p) q -> p nt q", p=128), reads=[rsrc], writes=[rD])
                p, rp = py[to % 2], r_py[to % 2]

                def mm2(e, p=p, D_=D_, nin=nin, t0=t0):
                    last = None
                    n = 0
                    for nt in range(nin):
                        for c in range(2):
                            last = e.matmul(p[:], lhsT=D_[:, c, nt, :], rhs=G[:, t0 + nt, c, :], start=(n == 0), stop=(n == 2 * nin - 1))
                            n += 1
                    return last
                k.op("tensor", mm2, reads=[rD] + [r_G[t0 + nt] for nt in range(nin)], writes=[rp])
                o_, ro = yo[to % 2], r_yo[to % 2]
                k.op("vector", lambda e, o_=o_, p=p: e.tensor_copy(out=o_[:], in_=p[:]), reads=[rp], writes=[ro])
                k.dma(yd[to * 128:(to + 1) * 128, :], o_[:], reads=[ro], writes=[r_yd])
            k.barrier()

    RWC = 3072

    def stage_rwprep(self, l):
        k = self.k
        z, _ = self.dr["z%d" % l]
        r_zt = self.r_ztile
        rwd, r_rwd = self.dram("rwd%d" % l, [2, T, self.RWC], F32)
        rwo, r_rwo = self.dram("rwo%d" % l, [T, 608], F32)
        self.r_rwd_t = [R() for _ in range(NT)]
        with ExitStack() as es:
            def bc(nm, n):
                tl = k.sb("rp_" + nm, [128, n], F32, es)
                ap, rr = self.dr[nm]
                k.dma(tl[:], ap[l].partition_broadcast(128), reads=[rr], writes=[r_c])
                return tl
            r_c = R()
            mu = bc("rw_mu", 1760)
            kkg = bc("rw_kk", 512)
            kag = bc("rw_ka", 512)
            w0 = bc("rw_w0f", 1024)
            a0 = bc("rw_a0f", 1024)
            w2 = k.sb("rp_w2", [32, 2, 2, 512], F32, es)
            ap, rr = self.dr["rw_wa2"]
            k.dma(w2[:], ap[l], reads=[rr], writes=[r_c])
            cur = [k.sb("rp_cur%d" % i, [128, 1760], F32, es) for i in range(2)]
            prv = [k.sb("rp_prv%d" % i, [128, 1760], F32, es) for i in range(2)]
            nxt = [k.sb("rp_nxt%d" % i, [128, 1760], F32, es) for i in range(2)]
            r_cur = [R() for _ in range(2)]
            r_prv = [R() for _ in range(2)]
            r_nxt = [R() for _ in range(2)]
            mx = k.sb("rp_mx", [128, 1760], F32, es)
            r_mx = R()
            d_ = k.sb("rp_d", [128, 1760], F32, es)
            r_d = R()
            st = k.sb("rp_st", [128, NT, 8], F32, es)
            lo_in = k.sb("rp_loin", [128, 128], F32, es)
            r_loin = R()
            lT = k.sb("rp_lT", [32, 4, 128], F32, es)
            r_lT = R()
            ob = [k.sb("rp_ob%d" % i, [128, 2, self.RWC], F32, es) for i in range(1)]
            r_ob = [[R() for _ in range(8)] for _ in range(1)]
            oo = k.sb("rp_oo", [128, 608], F32, es)
            r_oo = R()
            tw = k.sb("rp_tw", [128, 512], F32, es)
            r_tw = R()
            ta = k.sb("rp_ta", [128, 2, 512], F32, es)
            r_ta = [R(), R()]
            ptr = k.ps("rp_ptr", [128, 512], F32, es)
            r_ptr = PR()
            pw = [k.ps("rp_pw%d" % i, [128, 512], F32, es) for i in range(4)]
            r_pw = [PR() for _ in range(4)]
            for t in range(NT):
                rows = slice(t * 128, (t + 1) * 128)
                c_, rc_ = cur[t % 2], r_cur[t % 2]
                p_, rp_ = prv[t % 2], r_prv[t % 2]
                n_, rn_ = nxt[t % 2], r_nxt[t % 2]
                first = t in (0, 2)
                last_ = t in (1, NT - 1)
                for dst, rdst, off, lo_, hi_ in ((c_, rc_, 0, 0, 128), (p_, rp_, -1, 1 if first else 0, 128), (n_, rn_, 1, 0, 127 if last_ else 128)):
                    if hi_ - lo_ < 128:
                        k.op("gpsimd", lambda e, dst=dst: e.memset(dst[:], 0.0), writes=[rdst])
                    r0 = t * 128 + off
                    zt = [r_zt[t]] + ([r_zt[t - 1]] if off < 0 and t > 0 else []) + ([r_zt[t + 1]] if off > 0 and t < NT - 1 else [])
                    k.dma(dst[lo_:hi_, 0:1152], z[r0 + lo_:r0 + hi_, C_RWKS:C_RWKS + 1152], reads=zt, writes=[rdst])
                    k.dma(dst[lo_:hi_, 1152:1760], z[r0 + lo_:r0 + hi_, C_RWQS:C_RWQS + 608], reads=zt, writes=[rdst])
                k.op("vector", lambda e, p_=p_, n_=n_: e.tensor_tensor(out=d_[:], in0=p_[:], in1=n_[:], op=ALU.add), reads=[rp_, rn_], writes=[r_d])
                k.op("vector", lambda e, c_=c_: e.scalar_tensor_tensor(out=d_[:], in0=d_[:], scalar=0.5, in1=c_[:], op0=ALU.mult, op1=ALU.subtract),
                     reads=[r_d, rc_], writes=[r_d])
                k.op("vector", lambda e: e.tensor_tensor(out=d_[:], in0=d_[:], in1=mu[:], op=ALU.mult), reads=[r_d, r_c], writes=[r_d])
                k.op("vector", lambda e, c_=c_: e.tensor_tensor(out=mx[:], in0=d_[:], in1=c_[:], op=ALU.add), reads=[r_d, rc_], writes=[r_mx])
                o_, ro = ob[0], r_ob[0]
                kx = mx[:, 0:512]
                vx = mx[:, 512:1024]
                rx_ = mx[:, 1152:1664]
                r_s = R()
                kk_o = o_[:, 0, 0:512]
                k.op("vector", lambda e: e.tensor_tensor(out=kk_o, in0=kx, in1=kkg[:], op=ALU.mult), reads=[r_mx, r_c], writes=[ro[0]])
                k.op("vector", lambda e: e.tensor_tensor(out=d_[:, 0:512], in0=kk_o, in1=kk_o, op=ALU.mult), reads=[ro[0]], writes=[r_d])
                k.op("vector", lambda e, t=t: e.tensor_reduce(out=st[:, t, :], in_=d_[:, 0:512].rearrange("p (h d) -> p h d", h=8), axis=AX.X, op=ALU.add),
                     reads=[r_d], writes=[r_s])
                k.op("scalar", lambda e, t=t: e.activation(out=st[:, t, :], in_=st[:, t, :], func=AF.Sqrt), reads=[r_s], writes=[r_s])
                k.op("vector", lambda e, t=t: e.tensor_scalar(out=st[:, t, :], in0=st[:, t, :], scalar1=1e-12, scalar2=None, op0=ALU.max), reads=[r_s], writes=[r_s])
                k.op("vector", lambda e, t=t: e.reciprocal(out=st[:, t, :], in_=st[:, t, :]), reads=[r_s], writes=[r_s])
                k.op("vector", lambda e, t=t: e.tensor_tensor(out=kk_o.rearrange("p (h d) -> p h d", h=8), in0=kk_o.rearrange("p (h d) -> p h d", h=8),
                                                             in1=st[:, t, :].unsqueeze(2).to_broadcast([128, 8, 64]), op=ALU.mult),
                     reads=[ro[0], r_s], writes=[ro[0]])
                k.op("gpsimd", lambda e: e.tensor_copy(out=o_[:, 1, 0:512], in_=kk_o), reads=[ro[0]], writes=[ro[1]])
                k.op("gpsimd", lambda e: e.tensor_copy(out=o_[:, :, 2048:2560], in_=rx_.unsqueeze(1).to_broadcast([128, 2, 512])), reads=[r_mx], writes=[ro[2]])
                k.op("gpsimd", lambda e: e.tensor_copy(out=o_[:, :, 2560:3072], in_=vx.unsqueeze(1).to_broadcast([128, 2, 512])), reads=[r_mx], writes=[ro[3]])
                k.op("scalar", lambda e: e.activation(out=lo_in[:, 0:64], in_=mx[:, 1024:1088], func=AF.Tanh), reads=[r_mx], writes=[r_loin])
                k.op("vector", lambda e: e.tensor_copy(out=lo_in[:, 64:128], in_=mx[:, 1088:1152]), reads=[r_mx, r_loin], writes=[r_loin])

                def trl(e):
                    last = None
                    for j in range(4):
                        last = e.transpose(out=ptr[0:32, j * 128:(j + 1) * 128], in_=lo_in[:, j * 32:(j + 1) * 32], identity=self.identf[:])
                    return last
                k.op("tensor", trl, reads=[r_loin, self.r_ident], writes=[r_ptr])
                k.op("vector", lambda e: e.tensor_copy(out=lT[:], in_=ptr[0:32, :].rearrange("p (a b) -> p a b", a=4)), reads=[r_ptr], writes=[r_lT])
                for d in range(2):
                    p, rp = pw[d], r_pw[d]
                    k.op("tensor", lambda e, p=p, d=d: e.matmul(p[:], lhsT=lT[:, d, :], rhs=w2[:, 0, d, :], start=True, stop=True), reads=[r_lT, r_c], writes=[rp])
                    k.op("vector", lambda e, p=p, d=d: e.tensor_tensor(out=tw[:], in0=p[:], in1=w0[:, d * 512:(d + 1) * 512], op=ALU.add), reads=[rp, r_c], writes=[r_tw])
                    k.op("scalar", lambda e: e.activation(out=tw[:], in_=tw[:], func=AF.Sigmoid), reads=[r_tw], writes=[r_tw])
                    k.op("vector", lambda e, d=d: e.tensor_scalar(out=o_[:, d, 512:1024], in0=tw[:], scalar1=-float(np.exp(-0.5)), scalar2=None, op0=ALU.mult), reads=[r_tw], writes=[ro[4 + d]])
                    p, rp = pw[2 + d], r_pw[2 + d]
                    k.op("tensor", lambda e, p=p, d=d: e.matmul(p[:], lhsT=lT[:, 2 + d, :], rhs=w2[:, 1, d, :], start=True, stop=True), reads=[r_lT, r_c], writes=[rp])
                    k.op("vector", lambda e, p=p, d=d: e.tensor_tensor(out=ta[:, d, :], in0=p[:], in1=a0[:, d * 512:(d + 1) * 512], op=ALU.add), reads=[rp, r_c], writes=[r_ta[d]])
                    k.op("scalar", lambda e, d=d: e.activation(out=ta[:, d, :], in_=ta[:, d, :], func=AF.Sigmoid), reads=[r_ta[d]], writes=[r_ta[d]])
                    k.op("gpsimd", lambda e, d=d: e.tensor_tensor(out=o_[:, d, 1024:1536], in0=kk_o, in1=ta[:, d, :], op=ALU.mult), reads=[ro[0], r_ta[d]], writes=[ro[6 + d]])
                    k.op("vector", lambda e, d=d: e.scalar_tensor_tensor(out=ta[:, d, :], in0=ta[:, d, :], scalar=-1.0, in1=kag[:], op0=ALU.add, op1=ALU.mult),
                         reads=[r_ta[d], ro[6 + d], r_c], writes=[r_ta[d]])
                    k.op("vector", lambda e, d=d: e.scalar_tensor_tensor(out=o_[:, d, 1536:2048], in0=ta[:, d, :], scalar=1.0, in1=kx, op0=ALU.add, op1=ALU.mult),
                         reads=[r_ta[d], r_mx], writes=[ro[6 + d]])
                k.op("vector", lambda e: e.tensor_tensor(out=oo[:, 0:512], in0=o_[:, 0, 1536:2048], in1=o_[:, 1, 1536:2048], op=ALU.add), reads=[ro[6], ro[7]], writes=[r_oo])
                k.op("gpsimd", lambda e: e.tensor_copy(out=oo[:, 512:608], in_=mx[:, 1664:1760]), reads=[r_mx, r_oo], writes=[r_oo])
                for d in range(2):
                    k.dma(rwd[d, rows, :], o_[:, d, :], reads=ro, writes=[self.r_rwd_t[t]])
                k.dma(rwo[rows, :], oo[:], reads=[r_oo], writes=[r_rwo])
            k.barrier()

    def stage_rwscan_seq(self, l):
        k = self.k
        rwd, _ = self.dr["rwd%d" % l]
        r_rwd_t = self.r_rwd_t
        yfb, r_yfb = self.dram("rwy%d" % l, [2, T, 512], F32)
        self.r_rwy_t = [R() for _ in range(NT)]
        e2d, r_e2d = self.dr["rw_e2"]
        pmd, r_pmd = self.dr["rw_pm"]
        j64d, r_j64d = self.dr["rw_j64"]
        with ExitStack() as es:
            E2 = k.sb("rs_E2", [128, 64, 128], F32, es)
            Pm = k.sb("rs_Pm", [128, 128], F32, es)
            J64 = k.sb("rs_J64", [64, 64], F32, es)
            r_cst = R()
            for c in range(4):
                k.dma(E2[:, c * 16:(c + 1) * 16, :], e2d[:, c * 16:(c + 1) * 16, :], reads=[r_e2d], writes=[r_cst])
            k.dma(Pm[:], pmd, reads=[r_pmd], writes=[r_cst])
            k.dma(J64[:], j64d, reads=[r_j64d], writes=[r_cst])
            S = k.sb("rs_S", [128, 512], F32, es)
            r_S = R()
            k.op("vector", lambda e: e.memset(S[:], 0.0), writes=[r_S])
            X = [k.sb("rs_X%d" % i, [128, self.RWC], F32, es) for i in range(2)]
            r_X = [R() for _ in range(2)]
            Vr = k.sb("rs_Vr", [128, 8, 2, 64], F32, es)
            r_Vr = R()
            vT = [k.sb("rs_vT%d" % i, [128, 8, 64], F32, es) for i in range(2)]
            r_vT = [R() for _ in range(2)]
            ys = [k.sb("rs_ys%d" % i, [128, 8, 64], F32, es) for i in range(2)]
            r_ys = [R() for _ in range(2)]
            yo = k.sb("rs_yo", [64, 8, 2, 64], F32, es)
            r_yo = R()
            yb2 = k.sb("rs_yb2", [64, 512], F32, es)
            r_yb2 = R()
            yb3 = k.sb("rs_yb3", [64, 512], F32, es)
            r_yb3 = R()
            t1 = k.sb("rs_t1", [128, 512], F32, es)
            t2 = k.sb("rs_t2", [128, 512], F32, es)
            t3 = k.sb("rs_t3", [128, 512], F32, es)
            r_t1, r_t2, r_t3 = R(), R(), R()
            sa = k.sb("rs_sa", [128, 8], F32, es)
            r_sa = R()
            bcp = [k.ps("rs_bc%d" % i, [128, 512], F32, es) for i in range(5)]
            r_bcp = [PR() for _ in range(5)]
            pmisc = k.ps("rs_pm", [128, 8, 128], F32, es)
            r_pmisc = PR()
            pj = k.ps("rs_pj", [128, 512], F32, es)
            r_pj = PR()
            S3 = S[:].rearrange("p (h d) -> p h d", h=8)
            blk = 0
            nblk_dbg = self.dbg.get("RW_NBLK", 10 ** 9)
            for seg0, seglen in ((0, NCTX), (NCTX, L)):
                nb = seglen // 64
                for j in range(nb):
                    if blk >= nblk_dbg:
                        break
                    X_, rX = X[blk % 2], r_X[blk % 2]
                    vT_, rvT = vT[blk % 2], r_vT[blk % 2]
                    ys_, rys = ys[blk % 2], r_ys[blk % 2]
                    blk += 1
                    f0 = seg0 + 64 * j
                    b0 = seg0 + seglen - 64 * (j + 1)
                    k.dma(X_[0:64, :], rwd[0, f0:f0 + 64, :], reads=[r_rwd_t[f0 // 128]], writes=[rX])
                    k.dma(X_[64:128, :], rwd[1, b0:b0 + 64, :], reads=[r_rwd_t[b0 // 128]], writes=[rX])
                    k.op("gpsimd", lambda e, X_=X_: e.tensor_copy(out=Vr[:], in_=X_[:, 2560:3072].rearrange("p (h d) -> p h d", h=8).unsqueeze(2).to_broadcast([128, 8, 2, 64])),
                         reads=[rX], writes=[r_Vr])

                    def vtr(e):
                        last = None
                        for h in range(8):
                            last = e.matmul(pmisc[:, h, :], lhsT=Vr[:, h, :, :].rearrange("p a b -> p (a b)"), rhs=Pm[:], start=True, stop=True)
                        return last
                    k.op("tensor", vtr, reads=[r_Vr, r_cst], writes=[r_pmisc])
                    k.op("vector", lambda e, vT_=vT_: e.tensor_copy(out=vT_[0:64, :, :], in_=pmisc[0:64, :, 0:64]), reads=[r_pmisc], writes=[rvT])
                    k.op("vector", lambda e, vT_=vT_: e.tensor_copy(out=vT_[64:128, :, :], in_=pmisc[64:128, :, 64:128]), reads=[r_pmisc, rvT], writes=[rvT])
                    for i in range(64):
                        for q in range(5):
                            k.op("tensor", lambda e, q=q, i=i, X_=X_: e.matmul(bcp[q][:], lhsT=E2[:, i, :], rhs=X_[:, q * 512:(q + 1) * 512], start=True, stop=True),
                                 reads=[rX, r_cst], writes=[r_bcp[q]])
                        bkk, bdec, bb, bkd, br = [bcp[q][:].rearrange("p (h d) -> p h d", h=8) for q in range(5)]
                        k.op("vector", lambda e, bkk=bkk: e.tensor_tensor(out=t1[:].rearrange("p (h d) -> p h d", h=8), in0=S3, in1=bkk, op=ALU.mult),
                             reads=[r_S, r_bcp[0]], writes=[r_t1])
                        k.op("vector", lambda e: e.tensor_reduce(out=sa[:], in_=t1[:].rearrange("p (h d) -> p h d", h=8), axis=AX.X, op=ALU.add),
                             reads=[r_t1], writes=[r_sa])
                        k.op("vector", lambda e, bdec=bdec: e.tensor_tensor(out=S3, in0=S3, in1=bdec, op=ALU.mult), reads=[r_S, r_bcp[1]], writes=[r_S])
                        k.op("vector", lambda e, bb=bb: e.tensor_tensor(out=t2[:].rearrange("p (h d) -> p h d", h=8), in0=bb,
                                                                       in1=sa[:].unsqueeze(2).to_broadcast([128, 8, 64]), op=ALU.mult),
                             reads=[r_bcp[2], r_sa], writes=[r_t2])
                        k.op("vector", lambda e: e.tensor_tensor(out=S[:], in0=S[:], in1=t2[:], op=ALU.subtract), reads=[r_S, r_t2], writes=[r_S])
                        k.op("vector", lambda e, bkd=bkd, vT_=vT_, i=i: e.tensor_tensor(out=t3[:].rearrange("p (h d) -> p h d", h=8), in0=bkd,
                                                                                       in1=vT_[:, :, i:i + 1].to_broadcast([128, 8, 64]), op=ALU.mult),
                             reads=[r_bcp[3], rvT], writes=[r_t3])
                        k.op("vector", lambda e: e.tensor_tensor(out=S[:], in0=S[:], in1=t3[:], op=ALU.add), reads=[r_S, r_t3], writes=[r_S])
                        k.op("vector", lambda e, br=br: e.tensor_tensor(out=t1[:].rearrange("p (h d) -> p h d", h=8), in0=S3, in1=br, op=ALU.mult),
                             reads=[r_S, r_bcp[4]], writes=[r_t1])
                        k.op("vector", lambda e, ys_=ys_, i=i: e.tensor_reduce(out=ys_[:, :, i:i + 1].rearrange("p h o -> p (h o)"), in_=t1[:].rearrange("p (h d) -> p h d", h=8), axis=AX.X, op=ALU.add),
                             reads=[r_t1], writes=[rys])
                    def ytr(e, ys_=ys_):
                        last = None
                        for h in range(8):
                            last = e.matmul(pmisc[0:64, h, :], lhsT=ys_[:, h, :], rhs=self.identf[:], start=True, stop=True)
                        return last
                    k.op("tensor", ytr, reads=[rys, self.r_ident], writes=[r_pmisc])
                    k.op("vector", lambda e: e.tensor_copy(out=yo[:].rearrange("p h a d -> p h (a d)"), in_=pmisc[0:64, :, :]), reads=[r_pmisc], writes=[r_yo])
                    k.op("gpsimd", lambda e: e.tensor_copy(out=yb2[:].rearrange("p (h d) -> p h d", h=8), in_=yo[:, :, 1, :]), reads=[r_yo], writes=[r_yb2])
                    k.op("tensor", lambda e: e.matmul(pj[0:64, :], lhsT=J64[:], rhs=yb2[:], start=True, stop=True), reads=[r_yb2, r_cst], writes=[r_pj])
                    k.op("scalar", lambda e: e.activation(out=yb3[:], in_=pj[0:64, :], func=AF.Copy), reads=[r_pj], writes=[r_yb3])
                    k.dma(yfb[0, f0:f0 + 64, :].rearrange("p (h d) -> p h d", h=8), yo[:, :, 0, :], reads=[r_yo], writes=[self.r_rwy_t[f0 // 128]])
                    k.dma(yfb[1, b0:b0 + 64, :], yb3[:], reads=[r_yb3], writes=[self.r_rwy_t[b0 // 128]])
            k.barrier()

    def stage_rwscan(self, l):
        k = self.k
        rwd, _ = self.dr["rwd%d" % l]
        r_rwd_t = self.r_rwd_t
        yfb, r_yfb = self.dram("rwy%d" % l, [2, T, 512], F32)
        self.r_rwy_t = [R() for _ in range(NT)]
        mkd, r_mkd = self.dr["rw_masks"]
        dmd, r_dmd = self.dr["rw_dm"]
        with ExitStack() as es:
            MK = k.sb("rc_MK", [128, 6, 128], F32, es)
            DM = k.sb("rc_DM", [128, 2, 1024], F32, es)
            ones = k.sb("rc_ones", [128, 64], F32, es)
            r_cst = R()
            k.dma(MK[:], mkd, reads=[r_mkd], writes=[r_cst])
            k.dma(DM[:], dmd, reads=[r_dmd], writes=[r_cst])
            k.op("vector", lambda e: e.memset(ones[:], 1.0), writes=[r_cst])
            MI, BO, MS, nMS, nMST, nMI = [MK[:, i, :] for i in range(6)]
            S = k.sb("rc_S", [128, 512], F32R, es)
            r_S = R()
            X = [k.sb("rc_X%d" % i, [128, self.RWC], F32, es) for i in range(2)]
            r_X = [R() for _ in range(2)]
            fac = [k.sb("rc_fac%d" % i, [128, 512], F32, es) for i in range(4)]
            r_fac = [R() for _ in range(4)]
            tl = k.sb("rc_tl", [128, 2, 512], F32, es)
            r_tl = [R(), R()]
            k.op("vector", lambda e: e.memset(tl[:, 0, :], 0.0), writes=[r_tl[0]])
            k.op("vector", lambda e: e.tensor_copy(out=S[:], in_=tl[:, 0, :]), reads=[r_tl[0]], writes=[r_S])
            pl = [k.sb("rc_pl%d" % i, [128, 512], F32, es) for i in range(6)]
            r_pl = [R() for _ in range(6)]
            bd03 = [k.sb("rc_bd%d" % i, [128, 8, 128], F32R, es) for i in range(4)]
            r_bd03 = [R() for _ in range(4)]
            PB = []
            for par in range(2):
                d = {}
                d["bd4"] = k.sb("rc_bd4_%d" % par, [128, 8, 128], F32R, es)
                d["bd5"] = k.sb("rc_bd5_%d" % par, [128, 8, 128], F32R, es)
                d["bd6"] = k.sb("rc_bd6_%d" % par, [128, 8, 128], F32, es)
                d["r_bd"] = [R(), R(), R()]
                d["xT"] = [k.sb("rc_xT%d_%d" % (i, par), [128, 8, 128], F32R, es) for i in range(4)]
                d["r_xT"] = [[R(), R()] for _ in range(4)]
                for nm in ("AT", "ApT", "nBpT", "NT0", "M0"):
                    d[nm] = k.sb("rc_%s_%d" % (nm, par), [128, 8, 128], F32R, es)
                    d["r_" + nm] = [R(), R()]
                d["Vc"] = k.sb("rc_Vc%d" % par, [128, 512], F32R, es)
                d["r_Vc"] = R()
                PB.append(d)
            NTb = [k.sb("rc_NT%d" % i, [128, 8, 128], F32R, es) for i in range(2)]
            Mb = [k.sb("rc_M%d" % i, [128, 8, 128], F32R, es) for i in range(2)]
            r_NT = [[R(), R()] for _ in range(2)]
            r_M = [[R(), R()] for _ in range(2)]
            G = k.sb("rc_G", [128, 512], F32R, es)
            r_G = R()
            Pend = k.sb("rc_Pend", [128, 512], F32, es)
            r_Pend = R()
            Yo = [k.sb("rc_Yo%d" % i, [128, 512], F32, es) for i in range(2)]
            r_Yo = [R() for _ in range(2)]
            pb = [k.ps("rc_pb%d" % i, [128, 512], F32, es) for i in range(8)]
            r_pb = [PR() for _ in range(8)]
            ipb = [0]

            def bank():
                i = ipb[0] % 8
                ipb[0] += 1
                return pb[i], r_pb[i]

            chunks = []
            for seg0, seglen in ((0, NCTX), (NCTX, L)):
                for j in range(seglen // 64):
                    chunks.append((seg0 + 64 * j, seg0 + seglen - 64 * (j + 1)))
            chunks = chunks[:self.dbg.get("RW_NBLK", 10 ** 9)]

            def prologue_steps(ci):
                f0, b0 = chunks[ci]
                par = ci % 2
                d = PB[par]
                X_, rX = X[par], r_X[par]
                kk_, lw_, b_, kd_, r_ = [X_[:, q * 512:(q + 1) * 512] for q in range(5)]
                steps = []

                def s_load():
                    k.dma(X_[0:64, :], rwd[0, f0:f0 + 64, :], reads=[r_rwd_t[f0 // 128]], writes=[rX])
                    k.dma(X_[64:128, :], rwd[1, b0:b0 + 64, :], reads=[r_rwd_t[b0 // 128]], writes=[rX])
                    k.op("gpsimd", lambda e: e.tensor_copy(out=d["Vc"][:], in_=X_[:, 2560:3072]), reads=[rX], writes=[d["r_Vc"]])
                steps.append(s_load)

                def s_cum():
                    pLi, rLi = bank()
                    pLt, rLt = bank()
                    k.op("tensor", lambda e: e.matmul(pLi[:], lhsT=MI, rhs=lw_, start=True, stop=True), reads=[rX, r_cst], writes=[rLi])
                    k.op("tensor", lambda e: e.matmul(pLt[:], lhsT=BO, rhs=lw_, start=True, stop=True), reads=[rX, r_cst], writes=[rLt])
                    k.op("scalar", lambda e: e.activation(out=fac[0][:], in_=pLi[:], func=AF.Exp), reads=[rLi], writes=[r_fac[0]])
                    k.op("scalar", lambda e: e.activation(out=fac[1][:], in_=pLi[:], func=AF.Exp, scale=-1.0), reads=[rLi], writes=[r_fac[1]])
                    k.op("vector", lambda e: e.tensor_tensor(out=tl[:, 0, :], in0=pLi[:], in1=lw_, op=ALU.subtract), reads=[rLi, rX], writes=[r_tl[0]])
                    k.op("scalar", lambda e: e.activation(out=fac[2][:], in_=tl[:, 0, :], func=AF.Exp), reads=[r_tl[0]], writes=[r_fac[2]])
                    k.op("vector", lambda e: e.tensor_copy(out=tl[:, 1, :], in_=pLt[:]), reads=[rLt], writes=[r_tl[1]])
                    k.op("vector", lambda e: e.tensor_tensor(out=tl[:, 1, :], in0=tl[:, 1, :], in1=pLi[:], op=ALU.subtract), reads=[rLi, r_tl[1]], writes=[r_tl[1]])
                    k.op("scalar", lambda e: e.activation(out=fac[3][:], in_=tl[:, 1, :], func=AF.Exp), reads=[r_tl[1]], writes=[r_fac[3]])
                steps.append(s_cum)

                def s_prod():
                    for i, (src, fi) in enumerate(((kk_, 2), (r_, 0), (kd_, 1), (b_, 1), (kd_, 3), (b_, 3))):
                        k.op("vector" if i < 4 else "gpsimd", lambda e, i=i, src=src, fi=fi: e.tensor_tensor(out=pl[i][:], in0=src, in1=fac[fi][:], op=ALU.mult),
                             reads=[rX, r_fac[fi]], writes=[r_pl[i]])
                    for i in range(7):
                        src = pl[i][:] if i < 6 else lw_
                        dm = DM[:, 1 if i == 5 else 0, :]
                        rs = [r_pl[i]] if i < 6 else [rX]
                        if i < 4:
                            dst, rd = bd03[i], r_bd03[i]
                        else:
                            dst, rd = d["bd%d" % i], d["r_bd"][i - 4]
                        k.op("vector" if i < 4 else "gpsimd", lambda e, src=src, dm=dm, dst=dst: e.tensor_tensor(
                            out=dst[:].rearrange("p h (a d) -> p h a d", a=2), in0=src.rearrange("p (h d) -> p h d", h=8).unsqueeze(2).to_broadcast([128, 8, 2, 64]),
                            in1=dm.rearrange("p (h a d) -> p h a d", h=8, a=2), op=ALU.mult), reads=rs + [r_cst], writes=[rd])
                steps.append(s_prod)
                for qi in range(4):
                    def s_tr(qi=qi):
                        for hf in range(2):
                            p_, rp_ = bank()

                            def tr(e, p_=p_, hf=hf):
                                last = None
                                for hh in range(4):
                                    last = e.transpose(out=p_[:, hh * 128:(hh + 1) * 128], in_=bd03[qi][:, hf * 4 + hh, :].bitcast(F32), identity=self.identf[:])
                                return last
                            k.op("tensor", tr, reads=[r_bd03[qi], self.r_ident], writes=[rp_])
                            k.op("scalar", lambda e, p_=p_, hf=hf: e.activation(out=d["xT"][qi][:, hf * 4:(hf + 1) * 4, :].rearrange("p a b -> p (a b)"), in_=p_[:], func=AF.Copy),
                                 reads=[rp_], writes=[d["r_xT"][qi][hf]])
                    steps.append(s_tr)
                qT, rT, kT, bT = d["xT"]
                specs = ((bT, 3, qT, 0, nMS, "NT0"), (qT, 0, bT, 3, nMST, "M0"), (kT, 2, qT, 0, MS, "AT"), (kT, 2, rT, 1, MI, "ApT"), (bT, 3, rT, 1, nMI, "nBpT"))
                for (lt, li, rt, ri, mk, nm) in specs:
                    def s_in(lt=lt, li=li, rt=rt, ri=ri, mk=mk, nm=nm):
                        for hf in range(2):
                            p_, rp_ = bank()

                            def mm(e, p_=p_, hf=hf):
                                last = None
                                for hh in range(4):
                                    h = hf * 4 + hh
                                    last = e.matmul(p_[:, hh * 128:(hh + 1) * 128], lhsT=lt[:, h, :], rhs=rt[:, h, :], start=True, stop=True)
                                return last
                            k.op("tensor", mm, reads=[d["r_xT"][li][hf], d["r_xT"][ri][hf]], writes=[rp_])
                            k.op("gpsimd" if False else "vector", lambda e, p_=p_, hf=hf: e.tensor_tensor(
                                out=d[nm][:, hf * 4:(hf + 1) * 4, :], in0=p_[:].rearrange("p (a b) -> p a b", a=4), in1=mk.unsqueeze(1).to_broadcast([128, 4, 128]), op=ALU.mult),
                                reads=[rp_, r_cst], writes=[d["r_" + nm][hf]])
                    steps.append(s_in)
                return steps

            def solve(ci, fillers):
                f0, b0 = chunks[ci]
                par = ci % 2
                d = PB[par]
                Yo_, rYo = Yo[par], r_Yo[par]
                qT, rT, kT, bT = d["xT"]
                Vc_, rVc = d["Vc"], d["r_Vc"]

                def Vh(h):
                    return Vc_[:, h * 64:(h + 1) * 64]

                def fill(n):
                    for _ in range(n):
                        if fillers:
                            fillers.pop(0)()
                pG, rG = bank()

                def mmG(e):
                    last = None
                    for h in range(8):
                        e.matmul(pG[:, h * 64:(h + 1) * 64], lhsT=qT[:, h, :], rhs=S[:, h * 64:(h + 1) * 64], start=True, stop=False)
                        last = e.matmul(pG[:, h * 64:(h + 1) * 64], lhsT=d["AT"][:, h, :], rhs=Vh(h), start=False, stop=True)
                    return last
                k.op("tensor", mmG, reads=d["r_xT"][0] + d["r_AT"] + [r_S, rVc], writes=[rG])
                k.op("vector", lambda e: e.tensor_copy(out=G[:], in_=pG[:]), reads=[rG], writes=[r_G])
                fill(2)
                curT, curM, rcT, rcM = d["NT0"], d["M0"], d["r_NT0"], d["r_M0"]
                for lev in range(6):
                    pU, rU = bank()

                    def mmU(e, pU=pU, NTc=curT):
                        last = None
                        for h in range(8):
                            last = e.matmul(pU[:, h * 64:(h + 1) * 64], lhsT=NTc[:, h, :], rhs=G[:, h * 64:(h + 1) * 64], start=True, stop=True)
                        return last
                    k.op("tensor", mmU, reads=rcT + [r_G], writes=[rU])
                    if lev < 5:
                        nx = lev % 2
                        for (lt, rt, dst, rdst, rl, rr_) in ((curM, curT, NTb[nx], r_NT[nx], rcM, rcT), (curT, curM, Mb[nx], r_M[nx], rcT, rcM)):
                            for hf in range(2):
                                p_, rp_ = bank()

                                def mm2(e, p_=p_, lt=lt, rt=rt, hf=hf):
                                    last = None
                                    for hh in range(4):
                                        h = hf * 4 + hh
                                        last = e.matmul(p_[:, hh * 128:(hh + 1) * 128], lhsT=lt[:, h, :], rhs=rt[:, h, :], start=True, stop=True)
                                    return last
                                k.op("tensor", mm2, reads=[rl[hf], rr_[hf]], writes=[rp_])
                                k.op("scalar", lambda e, p_=p_, dst=dst, hf=hf: e.activation(out=dst[:, hf * 4:(hf + 1) * 4, :].rearrange("p a b -> p (a b)"), in_=p_[:], func=AF.Copy),
                                     reads=[rp_], writes=[rdst[hf]])
                        curT, curM, rcT, rcM = NTb[nx], Mb[nx], r_NT[nx], r_M[nx]
                    k.op("vector", lambda e, pU=pU: e.tensor_tensor(out=G[:], in0=G[:].bitcast(F32), in1=pU[:], op=ALU.add), reads=[rU, r_G], writes=[r_G])
                    fill(2)
                pY, rY = bank()

                def mmY(e):
                    last = None
                    for h in range(8):
                        hs = slice(h * 64, (h + 1) * 64)
                        e.matmul(pY[:, hs], lhsT=rT[:, h, :], rhs=S[:, hs], start=True, stop=False)
                        e.matmul(pY[:, hs], lhsT=d["ApT"][:, h, :], rhs=Vh(h), start=False, stop=False)
                        last = e.matmul(pY[:, hs], lhsT=d["nBpT"][:, h, :], rhs=G[:, hs], start=False, stop=True)
                    return last
                k.op("tensor", mmY, reads=d["r_xT"][1] + d["r_ApT"] + d["r_nBpT"] + [r_S, rVc, r_G], writes=[rY])
                k.op("scalar", lambda e: e.activation(out=Yo_[:], in_=pY[:], func=AF.Copy), reads=[rY], writes=[rYo])
                k.dma(yfb[0, f0:f0 + 64, :], Yo_[0:64, :], reads=[rYo], writes=[self.r_rwy_t[f0 // 128]])
                k.dma(yfb[1, b0:b0 + 64, :], Yo_[64:128, :], reads=[rYo], writes=[self.r_rwy_t[b0 // 128]])
                pL, rL = bank()

                def mmL(e):
                    last = None
                    for h in range(8):
                        last = e.matmul(pL[:, h * 64:(h + 1) * 64], lhsT=d["bd6"][:, h, :], rhs=ones[:], start=True, stop=True)
                    return last
                k.op("tensor", mmL, reads=[d["r_bd"][2], r_cst], writes=[rL])
                k.op("scalar", lambda e: e.activation(out=Pend[:], in_=pL[:], func=AF.Exp), reads=[rL], writes=[r_Pend])
                pS, rS = bank()

                def mmS(e):
                    last = None
                    for h in range(8):
                        hs = slice(h * 64, (h + 1) * 64)
                        e.matmul(pS[:, hs], lhsT=d["bd4"][:, h, :], rhs=Vh(h), start=True, stop=False)
                        last = e.matmul(pS[:, hs], lhsT=d["bd5"][:, h, :], rhs=G[:, hs], start=False, stop=True)
                    return last
                k.op("tensor", mmS, reads=[d["r_bd"][0], d["r_bd"][1], rVc, r_G], writes=[rS])
                k.op("vector", lambda e: e.tensor_tensor(out=S[:], in0=S[:].bitcast(F32), in1=Pend[:], op=ALU.mult), reads=[r_S, r_Pend], writes=[r_S])
                k.op("vector", lambda e: e.tensor_tensor(out=S[:], in0=S[:].bitcast(F32), in1=pS[:], op=ALU.add), reads=[r_S, rS], writes=[r_S])
                while fillers:
                    fillers.pop(0)()

            if chunks:
                for st_ in prologue_steps(0):
                    st_()
                for ci in range(len(chunks)):
                    fillers = prologue_steps(ci + 1) if ci + 1 < len(chunks) else []
                    solve(ci, fillers)
            k.barrier()

    def stage_rwout(self, l):
        k = self.k
        rwd, _ = self.dr["rwd%d" % l]
        rwo, r_rwo = self.dr["rwo%d" % l]
        yfb, _ = self.dr["rwy%d" % l]
        yd, r_yd = self.dram("y_rw%d" % l, [T, 512], F32)
        with ExitStack() as es:
            r_c = R()

            def bc(nm, n):
                tl = k.sb("ro_" + nm, [128, n], F32, es)
                ap, rr = self.dr[nm]
                k.dma(tl[:], ap[l].partition_broadcast(128), reads=[rr], writes=[r_c])
                return tl
            lnw = bc("rw_lnw", 512)
            lnb = bc("rw_lnb", 512)
            rkg = bc("rw_rk", 512)
            g2 = k.sb("ro_g2", [96, 512], F32, es)
            ap, rr = self.dr["rw_g2"]
            k.dma(g2[:], ap[l], reads=[rr], writes=[r_c])
            gne = k.sb("ro_gne", [128, 1], F32, es)
            k.op("vector", lambda e: e.memset(gne[:], 64e-5), writes=[r_c])
            yf = [k.sb("ro_yf%d" % i, [128, 512], F32, es) for i in range(2)]
            yb = [k.sb("ro_yb%d" % i, [128, 512], F32, es) for i in range(2)]
            rv = [k.sb("ro_rv%d" % i, [128, 1024], F32, es) for i in range(2)]
            kg = [k.sb("ro_kg%d" % i, [128, 608], F32, es) for i in range(2)]
            r_in = [[R() for _ in range(4)] for _ in range(2)]
            y = k.sb("ro_y", [128, 512], F32, es)
            r_y = R()
            sq = k.sb("ro_sq", [128, 512], F32, es)
            r_sq = R()
            st = k.sb("ro_st", [128, NT, 3, 8], F32, es)
            sg = k.sb("ro_sg", [128, 128], F32, es)
            r_sg = R()
            gT = k.sb("ro_gT", [96, 128], F32, es)
            r_gT = R()
            out = [k.sb("ro_out%d" % i, [128, 512], F32, es) for i in range(2)]
            r_out = [R() for _ in range(2)]
            ptr = k.ps("ro_ptr", [128, 512], F32, es)
            r_ptr = PR()
            pg = k.ps("ro_pg", [128, 512], F32, es)
            r_pg = PR()
            k.op("vector", lambda e: e.memset(sg[:], 0.0), writes=[r_sg])
            for t in range(NT):
                rows = slice(t * 128, (t + 1) * 128)
                i2 = t % 2
                ri = r_in[i2]
                k.dma(yf[i2][:], yfb[0, rows, :], reads=[self.r_rwy_t[t]], writes=[ri[0]])
                k.dma(yb[i2][:], yfb[1, rows, :], reads=[self.r_rwy_t[t]], writes=[ri[1]])
                k.dma(rv[i2][:], rwd[0, rows, 2048:3072], reads=[self.r_rwd_t[t]], writes=[ri[2]])
                k.dma(kg[i2][:], rwo[rows, :], reads=[r_rwo], writes=[ri[3]])
                r_s = R()
                y3 = y[:].rearrange("p (h d) -> p h d", h=8)
                k.op("vector", lambda e, i2=i2: e.tensor_tensor(out=y[:], in0=yf[i2][:], in1=yb[i2][:], op=ALU.add), reads=[ri[0], ri[1]], writes=[r_y])
                k.op("vector", lambda e, t=t: e.tensor_reduce(out=st[:, t, 0, :], in_=y3, axis=AX.X, op=ALU.add), reads=[r_y], writes=[r_s])
                k.op("vector", lambda e, t=t: e.tensor_scalar(out=st[:, t, 0, :], in0=st[:, t, 0, :], scalar1=1.0 / 64, scalar2=None, op0=ALU.mult), reads=[r_s], writes=[r_s])
                k.op("vector", lambda e, t=t: e.tensor_tensor(out=y3, in0=y3, in1=st[:, t, 0, :].unsqueeze(2).to_broadcast([128, 8, 64]), op=ALU.subtract),
                     reads=[r_y, r_s], writes=[r_y])
                k.op("gpsimd", lambda e: e.tensor_tensor(out=sq[:], in0=y[:], in1=y[:], op=ALU.mult), reads=[r_y], writes=[r_sq])
                k.op("vector", lambda e, t=t: e.tensor_reduce(out=st[:, t, 1, :], in_=sq[:].rearrange("p (h d) -> p h d", h=8), axis=AX.X, op=ALU.add), reads=[r_sq], writes=[r_s])
                k.op("scalar", lambda e, t=t: e.activation(out=st[:, t, 1, :], in_=st[:, t, 1, :], func=AF.Sqrt, bias=gne[:, 0:1], scale=1.0 / 64), reads=[r_s, r_c], writes=[r_s])
                k.op("vector", lambda e, t=t: e.reciprocal(out=st[:, t, 1, :], in_=st[:, t, 1, :]), reads=[r_s], writes=[r_s])
                k.op("vector", lambda e, t=t: e.tensor_tensor(out=y3, in0=y3, in1=st[:, t, 1, :].unsqueeze(2).to_broadcast([128, 8, 64]), op=ALU.mult),
                     reads=[r_y, r_s], writes=[r_y])
                k.op("gpsimd", lambda e: e.tensor_tensor(out=y[:], in0=y[:], in1=lnw[:], op=ALU.mult), reads=[r_y, r_c], writes=[r_y])
                k.op("gpsimd", lambda e: e.tensor_tensor(out=y[:], in0=y[:], in1=lnb[:], op=ALU.add), reads=[r_y, r_c], writes=[r_y])
                k.op("vector", lambda e, i2=i2: e.tensor_tensor(out=sq[:], in0=rv[i2][:, 0:512], in1=kg[i2][:, 0:512], op=ALU.mult), reads=[ri[2], ri[3], r_sq], writes=[r_sq])
                k.op("vector", lambda e: e.tensor_tensor(out=sq[:], in0=sq[:], in1=rkg[:], op=ALU.mult), reads=[r_sq, r_c], writes=[r_sq])
                k.op("vector", lambda e, t=t: e.tensor_reduce(out=st[:, t, 2, :], in_=sq[:].rearrange("p (h d) -> p h d", h=8), axis=AX.X, op=ALU.add), reads=[r_sq], writes=[r_s])
                k.op("vector", lambda e, t=t, i2=i2: e.tensor_tensor(out=sq[:].rearrange("p (h d) -> p h d", h=8), in0=rv[i2][:, 512:1024].rearrange("p (h d) -> p h d", h=8),
                                                                    in1=st[:, t, 2, :].unsqueeze(2).to_broadcast([128, 8, 64]), op=ALU.mult),
                     reads=[ri[2], r_s, r_sq], writes=[r_sq])
                k.op("vector", lambda e: e.tensor_tensor(out=y[:], in0=y[:], in1=sq[:], op=ALU.add), reads=[r_y, r_sq], writes=[r_y])
                k.op("scalar", lambda e, i2=i2: e.activation(out=sg[:, 0:96], in_=kg[i2][:, 512:608], func=AF.Sigmoid), reads=[ri[3]], writes=[r_sg])
                k.op("tensor", lambda e: e.transpose(out=ptr[:, 0:128], in_=sg[:], identity=self.identf[:]), reads=[r_sg, self.r_ident], writes=[r_ptr])
                k.op("vector", lambda e: e.tensor_copy(out=gT[:], in_=ptr[0:96, 0:128]), reads=[r_ptr], writes=[r_gT])
                k.op("tensor", lambda e: e.matmul(pg[:], lhsT=gT[:], rhs=g2[:], start=True, stop=True), reads=[r_gT, r_c], writes=[r_pg])
                o_, ro = out[i2], r_out[i2]
                k.op("vector", lambda e, o_=o_: e.tensor_tensor(out=o_[:], in0=y[:], in1=pg[:], op=ALU.mult), reads=[r_y, r_pg], writes=[ro])
                k.dma(yd[rows, :], o_[:], reads=[ro], writes=[r_yd])
            k.barrier()

    def stage_merge(self, l, xname):
        k = self.k
        z, _ = self.dr["z%d" % l]
        r_zt = self.r_ztile
        xin, r_xin = self.dr[xname]
        modD, r_modD = self.dr["modD%d" % l]
        wbr, r_wbr = self.dr["w_br"]
        wout, r_wout = self.dr["w_out"]
        accD, r_accD = self.dram("accD%d" % l, [T, D], BF16)
        r_acc_t = [R() for _ in range(NT)]
        x1, r_x1 = self.dram("x1_%d" % l, [T, D], F32)
        self.r_x1_t = [R() for _ in range(NT)]
        ybr = [self.dr[n % l] for n in ("y_fn%d", "y_na%d", "y_mla%d", "y_rw%d")]
        with ExitStack() as es:
            W = k.sb("mg_W", [128, 16, 2048], BF16, es)
            r_W = [R() for _ in range(16)]
            wst = [k.sb("mg_wst%d" % i, [128, 2048], F32, es) for i in range(2)]
            r_wst = [R() for _ in range(2)]
            for i in range(16):
                b, kc = i // 4, i % 4
                k.dma(wst[i % 2][:], wbr[l, b, kc * 128:(kc + 1) * 128, :], reads=[r_wbr], writes=[r_wst[i % 2]])
                if i % 2 == 0:
                    k.op("vector", lambda e, i=i: e.tensor_copy(out=W[:, i, :], in_=wst[i % 2][:]), reads=[r_wst[i % 2]], writes=[r_W[i]])
                else:
                    k.op("scalar", lambda e, i=i: e.activation(out=W[:, i, :], in_=wst[i % 2][:], func=AF.Copy), reads=[r_wst[i % 2]], writes=[r_W[i]])
            with ExitStack() as es2:
                yt = k.sb("mg_yt", [128, 4, 512], F32, es2)
                r_yt = R()
                ytb = k.sb("mg_ytb", [128, 2048], BF16, es2)
                r_ytb = R()
                yT = k.sb("mg_yT", [128, 16, 128], BF16, es2)
                r_yT = R()
                gt = [k.sb("mg_gt%d" % i, [128, 2048], F32, es2) for i in range(2)]
                r_gt = [R() for _ in range(2)]
                acc = k.sb("mg_acc", [128, 2048], F32, es2)
                r_acc = R()
                accb = k.sb("mg_accb", [128, 2048], BF16, es2)
                r_accb = R()
                tmp = k.sb("mg_tmp", [128, 512], F32, es2)
                r_tmp = R()
                ptr = [k.ps("mg_ptr%d" % i, [128, 8, 128], BF16, es2) for i in range(2)]
                r_ptr = [PR() for _ in range(2)]
                pm = [k.ps("mg_pm%d" % i, [128, 512], F32, es2) for i in range(5)]
                r_pm = [PR() for _ in range(5)]
                ip = 0
                ig = 0
                for t in range(NT):
                    rows = slice(t * 128, (t + 1) * 128)
                    for b in range(4):
                        k.dma(yt[:, b, :], ybr[b][0][rows, :], reads=[ybr[b][1]], writes=[r_yt])
                    k.op("scalar", lambda e: e.activation(out=ytb[:], in_=yt[:].rearrange("p a b -> p (a b)"), func=AF.Copy), reads=[r_yt], writes=[r_ytb])
                    for hf in range(2):
                        def tr(e, hf=hf):
                            last = None
                            for j in range(8):
                                c = hf * 8 + j
                                last = e.transpose(out=ptr[hf][:, j, :], in_=ytb[:, c * 128:(c + 1) * 128], identity=self.ident[:])
                            return last
                        k.op("tensor", tr, reads=[r_ytb, self.r_ident], writes=[r_ptr[hf]])
                        if hf == 0:
                            k.op("vector", lambda e: e.tensor_copy(out=yT[:, 0:8, :], in_=ptr[0][:]), reads=[r_ptr[0]], writes=[r_yT])
                        else:
                            k.op("scalar", lambda e: e.activation(out=yT[:, 8:16, :], in_=ptr[1][:], func=AF.Copy), reads=[r_ptr[1], r_yT], writes=[r_yT])
                    for b in range(4):
                        g_, rg = gt[ig % 2], r_gt[ig % 2]
                        ig += 1
                        k.dma(g_[:], z[rows, C_GATE + b * 2048:C_GATE + (b + 1) * 2048], reads=[r_zt[t]], writes=[rg])
                        k.op("scalar", lambda e, g_=g_: e.activation(out=g_[:], in_=g_[:], func=AF.Sigmoid), reads=[rg], writes=[rg])
                        for cbk in range(4):
                            p, rp = pm[ip % 5], r_pm[ip % 5]
                            ip += 1
                            cs_ = slice(cbk * 512, (cbk + 1) * 512)

                            def mm(e, p=p, b=b, cs_=cs_):
                                last = None
                                for kc in range(4):
                                    last = e.matmul(p[:], lhsT=yT[:, b * 4 + kc, :], rhs=W[:, b * 4 + kc, cs_], start=(kc == 0), stop=(kc == 3))
                                return last
                            k.op("tensor", mm, reads=[r_yT] + r_W[b * 4:b * 4 + 4], writes=[rp])
                            if b == 0:
                                k.op("vector", lambda e, p=p, g_=g_, cs_=cs_: e.tensor_tensor(out=acc[:, cs_], in0=p[:], in1=g_[:, cs_], op=ALU.mult),
                                     reads=[rp, rg], writes=[r_acc])
                            else:
                                k.op("vector", lambda e, p=p, g_=g_, cs_=cs_: e.tensor_tensor(out=tmp[:], in0=p[:], in1=g_[:, cs_], op=ALU.mult),
                                     reads=[rp, rg], writes=[r_tmp])
                                k.op("vector", lambda e, cs_=cs_: e.tensor_tensor(out=acc[:, cs_], in0=acc[:, cs_], in1=tmp[:], op=ALU.add),
                                     reads=[r_tmp, r_acc], writes=[r_acc])
                    k.op("vector", lambda e: e.tensor_copy(out=accb[:], in_=acc[:]), reads=[r_acc], writes=[r_accb])
                    k.dma(accD[rows, :], accb[:], reads=[r_accb], writes=[r_acc_t[t]])
                k.barrier()
            for i in range(16):
                k.dma(wst[i % 2][:], wout[l, i * 128:(i + 1) * 128, :], reads=[r_wout], writes=[r_wst[i % 2]])
                if i % 2 == 0:
                    k.op("vector", lambda e, i=i: e.tensor_copy(out=W[:, i, :], in_=wst[i % 2][:]), reads=[r_wst[i % 2]], writes=[r_W[i]])
                else:
                    k.op("scalar", lambda e, i=i: e.activation(out=W[:, i, :], in_=wst[i % 2][:], func=AF.Copy), reads=[r_wst[i % 2]], writes=[r_W[i]])
            with ExitStack() as es3:
                g1 = k.sb("mg_g1", [128, 2, 2048], F32, es3)
                r_g1 = R()
                for r_ in range(2):
                    k.dma(g1[:, r_, :], modD[r_:r_ + 1, 2 * D:3 * D].partition_broadcast(128), reads=[r_modD], writes=[r_g1])
                ab = [k.sb("mg_ab%d" % i, [128, 2048], BF16, es3) for i in range(2)]
                r_ab = [R() for _ in range(2)]
                aT = k.sb("mg_aT", [128, 16, 128], BF16, es3)
                r_aT = R()
                xt = [k.sb("mg_xt%d" % i, [128, 2048], F32, es3) for i in range(2)]
                r_xt = [R() for _ in range(2)]
                xo = [k.sb("mg_xo%d" % i, [128, 2048], F32, es3) for i in range(2)]
                r_xo = [R() for _ in range(2)]
                tmp = k.sb("mg_tmp2", [128, 512], F32, es3)
                r_tmp = R()
                ptr = [k.ps("mg_ptrb%d" % i, [128, 8, 128], BF16, es3) for i in range(2)]
                r_ptr = [PR() for _ in range(2)]
                pm = [k.ps("mg_pmb%d" % i, [128, 512], F32, es3) for i in range(4)]
                r_pm = [PR() for _ in range(4)]
                ip = 0
                for t in range(NT):
                    rows = slice(t * 128, (t + 1) * 128)
                    row = 1 if t < 2 else 0
                    a_, ra = ab[t % 2], r_ab[t % 2]
                    x_, rx = xt[t % 2], r_xt[t % 2]
                    o_, ro = xo[t % 2], r_xo[t % 2]
                    k.dma(a_[:], accD[rows, :], reads=[r_acc_t[t]], writes=[ra])
                    k.dma(x_[:], xin[rows, :], reads=[r_xin], writes=[rx])
                    for hf in range(2):
                        def tr(e, hf=hf, a_=a_):
                            last = None
                            for j in range(8):
                                c = hf * 8 + j
                                last = e.transpose(out=ptr[hf][:, j, :], in_=a_[:, c * 128:(c + 1) * 128], identity=self.ident[:])
                            return last
                        k.op("tensor", tr, reads=[ra, self.r_ident], writes=[r_ptr[hf]])
                        if hf == 0:
                            k.op("vector", lambda e: e.tensor_copy(out=aT[:, 0:8, :], in_=ptr[0][:]), reads=[r_ptr[0]], writes=[r_aT])
                        else:
                            k.op("scalar", lambda e: e.activation(out=aT[:, 8:16, :], in_=ptr[1][:], func=AF.Copy), reads=[r_ptr[1], r_aT], writes=[r_aT])
                    for cbk in range(4):
                        p, rp = pm[ip % 4], r_pm[ip % 4]
                        ip += 1
                        cs_ = slice(cbk * 512, (cbk + 1) * 512)

                        def mm(e, p=p, cs_=cs_):
                            last = None
                            for kc in range(16):
                                last = e.matmul(p[:], lhsT=aT[:, kc, :], rhs=W[:, kc, cs_], start=(kc == 0), stop=(kc == 15))
                            return last
                        k.op("tensor", mm, reads=[r_aT] + r_W, writes=[rp])
                        k.op("vector", lambda e, p=p, cs_=cs_, row=row: e.tensor_tensor(out=tmp[:], in0=p[:], in1=g1[:, row, cs_], op=ALU.mult),
                             reads=[rp, r_g1], writes=[r_tmp])
                        k.op("vector", lambda e, o_=o_, x_=x_, cs_=cs_: e.tensor_tensor(out=o_[:, cs_], in0=tmp[:], in1=x_[:, cs_], op=ALU.add),
                             reads=[r_tmp, rx], writes=[ro])
                    k.dma(x1[rows, :], o_[:], reads=[ro], writes=[self.r_x1_t[t]])
                k.barrier()

    def stage_moe(self, l, final):
        k = self.k
        x1, _ = self.dr["x1_%d" % l]
        r_x1_t = self.r_x1_t
        modD, r_modD = self.dr["modD%d" % l]
        XE, r_XE = self.dram("moe_xe%d" % l, [16, 128, 16 * 288], BF16)
        WS, r_WS = self.dram("moe_ws%d" % l, [16, 128, 2 * 2048], BF16)
        WSC, r_WSC = self.dram("moe_wsc%d" % l, [16, 32, 256], BF16)
        YE, r_YE = self.dram("moe_ye%d" % l, [16, 288, 2048], BF16)
        r_XEe = [R() for _ in range(16)]
        r_WSe = [R() for _ in range(16)]
        r_YEe = [R() for _ in range(16)]
        if final:
            xo_d, r_xo_d = self.dram("out", [L, D], F32, "ExternalOutput")
        else:
            xo_d, r_xo_d = self.dram("x2_%d" % l, [T, D], F32)
        self.r_x2_t = [R() for _ in range(NT)]
        w1d, r_w1d = self.dr["moe_w1"]
        w3d, r_w3d = self.dr["moe_w3"]
        w2d, r_w2d = self.dr["moe_w2"]
        with ExitStack() as es:
            h2 = k.sb("mo_h2", [128, NT, 2048], BF16, es)
            r_h2 = [R() for _ in range(NT)]
            aff = k.sb("mo_aff", [128, NT, 32], F32, es)
            r_aff = [R() for _ in range(NT)]
            affT = k.sb("mo_affT", [32, T], F32, es)
            r_affT = R()
            k.op("gpsimd", lambda e: e.memset(aff[:], 0.0), writes=r_aff)
            r_c = R()
            with ExitStack() as es2:
                abc = k.sb("mo_abc", [128, 2, 2048], F32, es2)
                shbc = k.sb("mo_shbc", [128, 2, 2048], F32, es2)
                gbc = k.sb("mo_gbc", [128, 2048], F32, es2)
                ap, rr = self.dr["norm2_gb"]
                k.dma(gbc[:], ap[l].partition_broadcast(128), reads=[rr], writes=[r_c])
                for r_ in range(2):
                    k.dma(abc[:, r_, :], modD[r_:r_ + 1, 4 * D:5 * D].partition_broadcast(128), reads=[r_modD], writes=[r_c])
                    k.dma(shbc[:, r_, :], modD[r_:r_ + 1, 3 * D:4 * D].partition_broadcast(128), reads=[r_modD], writes=[r_c])
                for r_ in range(2):
                    k.op("vector", lambda e, r_=r_: e.scalar_tensor_tensor(out=abc[:, r_, :], in0=abc[:, r_, :], scalar=1.0, in1=gbc[:], op0=ALU.add, op1=ALU.mult),
                         reads=[r_c], writes=[r_c])
                rtr = k.sb("mo_rtr", [128, 16, 32], F32, es2)
                k.op("gpsimd", lambda e: e.memset(rtr[:], 0.0), writes=[r_c])
                ap, rr = self.dr["moe_router"]
                k.dma(rtr[:, :, 0:16], ap[l].rearrange("(kc p) n -> p kc n", p=128), reads=[rr], writes=[r_c])
                xt = [k.sb("mo_xt%d" % i, [128, 2048], F32, es2) for i in range(2)]
                r_xt = [R() for _ in range(2)]
                hf_ = k.sb("mo_hf", [128, 2048], F32, es2)
                r_hf = R()
                hT = k.sb("mo_hT", [128, 16, 128], F32, es2)
                r_hT = R()
                junk = k.sb("mo_junk", [128, 2048], BF16, es2)
                r_junk = R()
                st = k.sb("mo_st", [128, NT, 4], F32, es2)
                lg = k.sb("mo_lg", [128, 16], F32, es2)
                r_lg = R()
                ptr = [k.ps("mo_ptr%d" % i, [128, 4, 128], F32, es2) for i in range(4)]
                r_ptr = [PR() for _ in range(4)]
                pl = k.ps("mo_pl", [128, 512], F32, es2)
                r_pl = PR()
                pT = k.ps("mo_pT", [128, 512], F32, es2)
                r_pT = PR()
                for t in range(NT):
                    rows = slice(t * 128, (t + 1) * 128)
                    row = 1 if t < 2 else 0
                    x_, rx = xt[t % 2], r_xt[t % 2]
                    r_s = R()
                    k.dma(x_[:], x1[rows, :], reads=[r_x1_t[t]], writes=[rx])
                    k.op("scalar", lambda e, x_=x_, t=t: e.activation(out=junk[:], in_=x_[:], func=AF.Square, accum_out=st[:, t, 0:1]), reads=[rx], writes=[r_junk, r_s])
                    k.op("scalar", lambda e, t=t: e.activation(out=st[:, t, 1:2], in_=st[:, t, 0:1], func=AF.Sqrt, bias=self.eps_t[:, 0:1], scale=1.0 / D), reads=[r_s], writes=[r_s])
                    k.op("vector", lambda e, t=t: e.reciprocal(out=st[:, t, 1:2], in_=st[:, t, 1:2]), reads=[r_s], writes=[r_s])
                    k.op("vector", lambda e, x_=x_, t=t, row=row: e.scalar_tensor_tensor(out=hf_[:], in0=x_[:], scalar=st[:, t, 1:2], in1=abc[:, row, :], op0=ALU.mult, op1=ALU.mult),
                         reads=[rx, r_s, r_c], writes=[r_hf])
                    k.op("vector", lambda e, row=row: e.tensor_tensor(out=hf_[:], in0=hf_[:], in1=shbc[:, row, :], op=ALU.add), reads=[r_hf, r_c], writes=[r_hf])
                    k.op("scalar", lambda e, t=t: e.activation(out=h2[:, t, :], in_=hf_[:], func=AF.Copy), reads=[r_hf], writes=[r_h2[t]])
                    for g in range(4):
                        def tr(e, g=g):
                            last = None
                            for j in range(4):
                                c = g * 4 + j
                                last = e.transpose(out=ptr[g][:, j, :], in_=hf_[:, c * 128:(c + 1) * 128], identity=self.identf[:])
                            return last
                        k.op("tensor", tr, reads=[r_hf, self.r_ident], writes=[r_ptr[g]])
                        if g % 2 == 0:
                            k.op("vector", lambda e, g=g: e.tensor_copy(out=hT[:, g * 4:(g + 1) * 4, :], in_=ptr[g][:]), reads=[r_ptr[g]], writes=[r_hT])
                        else:
                            k.op("scalar", lambda e, g=g: e.activation(out=hT[:, g * 4:(g + 1) * 4, :], in_=ptr[g][:], func=AF.Copy), reads=[r_ptr[g], r_hT], writes=[r_hT])

                    def mmr(e):
                        last = None
                        for kc in range(16):
                            last = e.matmul(pl[:, 0:32], lhsT=hT[:, kc, :], rhs=rtr[:, kc, :], start=(kc == 0), stop=(kc == 15))
                        return last
                    k.op("tensor", mmr, reads=[r_hT, r_c], writes=[r_pl])
                    k.op("vector", lambda e, t=t: e.tensor_reduce(out=st[:, t, 2:3], in_=pl[:, 0:16], axis=AX.X, op=ALU.max), reads=[r_pl], writes=[r_s])
                    k.op("vector", lambda e, t=t: e.tensor_scalar(out=lg[:], in0=pl[:, 0:16], scalar1=st[:, t, 2:3], scalar2=None, op0=ALU.subtract), reads=[r_pl, r_s], writes=[r_lg])
                    k.op("scalar", lambda e, t=t: e.activation(out=lg[:], in_=lg[:], func=AF.Exp, accum_out=st[:, t, 3:4]), reads=[r_lg], writes=[r_lg, r_s])
                    k.op("vector", lambda e, t=t: e.reciprocal(out=st[:, t, 3:4], in_=st[:, t, 3:4]), reads=[r_s], writes=[r_s])
                    k.op("vector", lambda e, t=t: e.tensor_scalar(out=aff[:, t, 0:16], in0=lg[:], scalar1=st[:, t, 3:4], scalar2=None, op0=ALU.mult), reads=[r_lg, r_s], writes=[r_aff[t]])
                    k.op("tensor", lambda e, t=t: e.transpose(out=pT[0:32, 0:128], in_=aff[:, t, :], identity=self.identf[:]), reads=[r_aff[t], self.r_ident], writes=[r_pT])
                    k.op("vector", lambda e, rows=rows: e.tensor_copy(out=affT[:, rows], in_=pT[0:32, 0:128]), reads=[r_pT], writes=[r_affT])
                k.barrier()
            with ExitStack() as es3:
                sel = k.sb("mo_sel", [32, 16, 128], F32, es3)
                ap, rr = self.dr["moe_sel"]
                k.dma(sel[:], ap, reads=[rr], writes=[r_c])
                jidx = k.sb("mo_jidx", [128, 2], F32, es3)
                ap, rr = self.dr["moe_jidx"]
                k.dma(jidx[:], ap, reads=[rr], writes=[r_c])
                ones = k.sb("mo_ones", [128, 128], BF16, es3)
                k.op("vector", lambda e: e.memset(ones[:], 1.0), writes=[r_c])
                A = k.sb("mo_A", [128, T], F32, es3)
                r_A = R()
                cmp = [k.sb("mo_cmp%d" % i, [128, 2048], BF16, es3) for i in range(2)]
                r_cmp = [R() for _ in range(2)]
                Ws = k.sb("mo_Ws", [128, 2, 2048], BF16, es3)
                r_Ws = R()
                Ps = k.sb("mo_Ps", [128, 2, 2048], BF16, es3)
                r_Ps = R()
                Wc = k.sb("mo_Wc", [128, 256], BF16, es3)
                r_Wc = R()
                Pc = k.sb("mo_Pc", [128, 256], BF16, es3)
                r_Pc = R()
                PsT = k.sb("mo_PsT", [128, NT, 256], BF16, es3)
                r_PsT = R()
                xe = [k.sb("mo_xe%d" % i, [128, 16, 288], BF16, es3) for i in range(2)]
                r_xe = [R() for _ in range(2)]
                prk = k.ps("mo_prk", [128, 2048], F32, es3)
                r_prk = PR()
                pA = k.ps("mo_pA", [128, 512], F32, es3)
                r_pA = PR()
                ptb = k.ps("mo_ptb", [128, 8, 128], BF16, es3)
                r_ptb = PR()
                pg = [k.ps("mo_pg%d" % i, [128, 512], F32, es3) for i in range(2)]
                r_pg = [PR() for _ in range(2)]
                ic = 0
                ig = 0
                for e_ in range(16):
                    for cb in range(5):
                        c0 = cb * 512
                        cw = min(512, T - c0)
                        k.op("tensor", lambda e, c0=c0, cw=cw, e_=e_: e.matmul(pA[:, 0:cw], lhsT=sel[:, e_, :], rhs=affT[:, c0:c0 + cw], start=True, stop=True),
                             reads=[r_affT, r_c], writes=[r_pA])
                        k.op("scalar", lambda e, c0=c0, cw=cw: e.activation(out=A[:, c0:c0 + cw], in_=pA[:, 0:cw], func=AF.Copy), reads=[r_pA], writes=[r_A])
                    for mt in range(16):
                        c_, rc_ = cmp[ic % 2], r_cmp[ic % 2]
                        ic += 1
                        k.op("vector", lambda e, c_=c_, mt=mt, e_=e_: e.tensor_scalar(out=c_[:], in0=A[:, NCTX:T], scalar1=aff[:, 2 + mt, e_:e_ + 1], scalar2=None, op0=ALU.is_lt),
                             reads=[r_A, r_aff[2 + mt]], writes=[rc_])

                        def mmc(e, c_=c_, mt=mt):
                            last = None
                            for cb in range(4):
                                last = e.matmul(prk[:, cb * 512:(cb + 1) * 512], lhsT=ones[:], rhs=c_[:, cb * 512:(cb + 1) * 512], start=(mt == 0), stop=(mt == 15))
                            return last
                        k.op("tensor", mmc, reads=[rc_, r_c], writes=[r_prk])
                    for jt in range(2):
                        k.op("vector", lambda e, jt=jt: e.scalar_tensor_tensor(out=Ws[:, jt, :], in0=prk[:], scalar=jidx[:, jt:jt + 1], in1=A[:, NCTX:T], op0=ALU.is_equal, op1=ALU.mult),
                             reads=[r_prk, r_A, r_c], writes=[r_Ws])
                        k.op("vector", lambda e, jt=jt: e.tensor_scalar(out=Ps[:, jt, :], in0=prk[:], scalar1=jidx[:, jt:jt + 1], scalar2=None, op0=ALU.is_equal),
                             reads=[r_prk, r_c], writes=[r_Ps])
                    k.dma(WS[e_].rearrange("p (a b) -> p a b", a=2), Ws[:], reads=[r_Ws], writes=[r_WSe[e_]])
                    for mt in range(2):
                        c_, rc_ = cmp[ic % 2], r_cmp[ic % 2]
                        ic += 1
                        k.op("vector", lambda e, c_=c_, mt=mt, e_=e_: e.tensor_scalar(out=c_[:, 0:256], in0=A[:, 0:NCTX], scalar1=aff[:, mt, e_:e_ + 1], scalar2=None, op0=ALU.is_lt),
                             reads=[r_A, r_aff[mt]], writes=[rc_])
                        k.op("tensor", lambda e, c_=c_, mt=mt: e.matmul(prk[:, 0:256], lhsT=ones[:], rhs=c_[:, 0:256], start=(mt == 0), stop=(mt == 1)),
                             reads=[rc_, r_c], writes=[r_prk])
                    k.op("vector", lambda e: e.scalar_tensor_tensor(out=Wc[:], in0=prk[:, 0:256], scalar=jidx[:, 0:1], in1=A[:, 0:NCTX], op0=ALU.is_equal, op1=ALU.mult),
                         reads=[r_prk, r_A, r_c], writes=[r_Wc])
                    k.op("vector", lambda e: e.tensor_scalar(out=Pc[:], in0=prk[:, 0:256], scalar1=jidx[:, 0:1], scalar2=None, op0=ALU.is_equal),
                         reads=[r_prk, r_c], writes=[r_Pc])
                    k.dma(WSC[e_], Wc[0:32, :], reads=[r_Wc], writes=[r_WSe[e_]])
                    for nt in range(16):
                        if nt % 4 == 0:
                            def trp(e, nt=nt):
                                last = None
                                for q in range(4):
                                    for jt in range(2):
                                        last = e.transpose(out=ptb[:, q * 2 + jt, :], in_=Ps[:, jt, (nt + q) * 128:(nt + q + 1) * 128], identity=self.ident[:])
                                return last
                            k.op("tensor", trp, reads=[r_Ps, self.r_ident], writes=[r_ptb])
                            k.op("scalar", lambda e, nt=nt: e.activation(out=PsT[:, 2 + nt:2 + nt + 4, :], in_=ptb[:].rearrange("p (q j) n -> p q (j n)", q=4), func=AF.Copy),
                                 reads=[r_ptb], writes=[r_PsT])

                    def trc(e):
                        last = None
                        for nt in range(2):
                            last = e.transpose(out=ptb[:, nt, 0:32], in_=Pc[0:32, nt * 128:(nt + 1) * 128], identity=self.ident[0:32, 0:32])
                        return last
                    k.op("tensor", trc, reads=[r_Pc, self.r_ident], writes=[r_ptb])
                    k.op("scalar", lambda e: e.activation(out=PsT[:, 0:2, 0:32], in_=ptb[:, 0:2, 0:32], func=AF.Copy), reads=[r_ptb], writes=[r_PsT])
                    x_, rx = xe[e_ % 2], r_xe[e_ % 2]
                    for dc in range(16):
                        p, rp = pg[ig % 2], r_pg[ig % 2]
                        ig += 1

                        def mmg(e, p=p, dc=dc):
                            last = None
                            for nt in range(16):
                                last = e.matmul(p[:, 0:256], lhsT=h2[:, 2 + nt, dc * 128:(dc + 1) * 128], rhs=PsT[:, 2 + nt, :], start=(nt == 0), stop=(nt == 15))
                            for nt in range(2):
                                last = e.matmul(p[:, 256:288], lhsT=h2[:, nt, dc * 128:(dc + 1) * 128], rhs=PsT[:, nt, 0:32], start=(nt == 0), stop=(nt == 1))
                            return last
                        k.op("tensor", mmg, reads=[r_PsT] + r_h2, writes=[rp])
                        k.op("scalar", lambda e, p=p, x_=x_, dc=dc: e.activation(out=x_[:, dc, :], in_=p[:, 0:288], func=AF.Copy), reads=[rp, rx], writes=[rx])
                    k.dma(XE[e_], x_[:].rearrange("p a b -> p (a b)"), reads=[rx], writes=[r_XEe[e_]])
                k.barrier()
        if self.dbg.get("MOE_PH", 9) < 2:
            return
        with ExitStack() as es:
            w1b = k.sb("mo_w1b", [128, 16, 1024], BF16, es)
            w3b = k.sb("mo_w3b", [128, 16, 1024], BF16, es)
            w2b = k.sb("mo_w2b", [128, 8, 2048], BF16, es)
            r_w1 = [R() for _ in range(8)]
            r_w3 = [R() for _ in range(8)]
            r_w2 = [R() for _ in range(8)]
            wst = [k.sb("mo_wst%d" % i, [128, 4096], F32, es) for i in range(3)]
            r_wst = [R() for _ in range(3)]
            xe = [k.sb("mo_xeb%d" % i, [128, 16, 288], BF16, es) for i in range(2)]
            r_xe = [R() for _ in range(2)]
            gT = k.sb("mo_gT", [128, 8, 288], BF16, es)
            r_gT = [R() for _ in range(8)]
            sa8 = k.sb("mo_sa8", [128, 8, 288], F32, es)
            r_sa8 = [R() for _ in range(8)]
            yeb = [k.sb("mo_yeb%d" % i, [128, 3, 2048], BF16, es) for i in range(1)]
            r_yeb = [R() for _ in range(1)]
            pa = [k.ps("mo_pa%d" % i, [128, 512], F32, es) for i in range(2)]
            r_pa = [PR() for _ in range(2)]
            pu = [k.ps("mo_pu%d" % i, [128, 512], F32, es) for i in range(2)]
            r_pu = [PR() for _ in range(2)]
            py = [k.ps("mo_py%d" % i, [128, 512], F32, es) for i in range(3)]
            r_py = [PR() for _ in range(3)]
            iw = 0
            iy = 0
            engs = ("vector", "gpsimd")
            for e_ in range(16):
                x_, rx = xe[e_ % 2], r_xe[e_ % 2]
                k.dma(x_[:].rearrange("p a b -> p (a b)"), XE[e_], reads=[r_XEe[e_]], writes=[rx])
                for (wd, rwd_, wb_, rw_) in ((w1d, r_w1d, w1b, r_w1), (w3d, r_w3d, w3b, r_w3)):
                    for pr in range(4):
                        s_, rs = wst[iw % 3], r_wst[iw % 3]
                        k.dma(s_[:].rearrange("p (a n) -> p a n", a=4), wd[l, e_, pr * 512:(pr + 1) * 512, :].rearrange("(a p) n -> p a n", p=128), reads=[rwd_], writes=[rs])
                        for hh in range(2):
                            sv = s_[:, hh * 2048:(hh + 1) * 2048].rearrange("p (a n) -> p a n", a=2)
                            dv = wb_[:, 4 * pr + 2 * hh:4 * pr + 2 * hh + 2, :]
                            if hh == 0:
                                k.op("vector", lambda e, sv=sv, dv=dv: e.tensor_copy(out=dv, in_=sv), reads=[rs], writes=[rw_[2 * pr + hh]])
                            else:
                                k.op("scalar", lambda e, sv=sv, dv=dv: e.activation(out=dv, in_=sv, func=AF.Copy), reads=[rs], writes=[rw_[2 * pr + hh]])
                        iw += 1
                for fp in range(4):
                    s_, rs = wst[iw % 3], r_wst[iw % 3]
                    k.dma(s_[:].rearrange("p (a n) -> p a n", a=2), w2d[l, e_, fp * 256:(fp + 1) * 256, :].rearrange("(a p) n -> p a n", p=128), reads=[r_w2d], writes=[rs])
                    for hh in range(2):
                        fc = 2 * fp + hh
                        sv = s_[:, hh * 2048:(hh + 1) * 2048]
                        if hh == 0:
                            k.op("vector", lambda e, sv=sv, fc=fc: e.tensor_copy(out=w2b[:, fc, :], in_=sv), reads=[rs], writes=[r_w2[fc]])
                        else:
                            k.op("scalar", lambda e, sv=sv, fc=fc: e.activation(out=w2b[:, fc, :], in_=sv, func=AF.Copy), reads=[rs], writes=[r_w2[fc]])
                    iw += 1
                for fc in range(8):
                    pa_, rpa = pa[fc % 2], r_pa[fc % 2]

                    def mma(e, p_=pa_, fc=fc, x_=x_):
                        last = None
                        for kc in range(16):
                            last = e.matmul(p_[:, 0:288], lhsT=w1b[:, kc, fc * 128:(fc + 1) * 128], rhs=x_[:, kc, :], start=(kc == 0), stop=(kc == 15))
                        return last
                    k.op("tensor", mma, reads=r_w1 + [rx], writes=[rpa])
                    k.op("scalar", lambda e, pa_=pa_, fc=fc: e.activation(out=sa8[:, fc, :], in_=pa_[:, 0:288], func=AF.Silu), reads=[rpa], writes=[r_sa8[fc]])
                for fc in range(8):
                    pu_, rpu = pu[fc % 2], r_pu[fc % 2]

                    def mmu(e, p_=pu_, fc=fc, x_=x_):
                        last = None
                        for kc in range(16):
                            last = e.matmul(p_[:, 0:288], lhsT=w3b[:, kc, fc * 128:(fc + 1) * 128], rhs=x_[:, kc, :], start=(kc == 0), stop=(kc == 15))
                        return last
                    k.op("tensor", mmu, reads=r_w3 + [rx], writes=[rpu])
                    k.op("vector", lambda e, pu_=pu_, fc=fc: e.tensor_tensor(out=gT[:, fc, :], in0=sa8[:, fc, :], in1=pu_[:, 0:288], op=ALU.mult), reads=[r_sa8[fc], rpu], writes=[r_gT[fc]])
                y_, ry = yeb[0], r_yeb[0]
                for jt, (j0, M) in enumerate(((0, 128), (128, 128), (256, 32))):
                    for cbk in range(4):
                        p_, rp_ = py[iy % 3], r_py[iy % 3]
                        iy += 1
                        cs_ = slice(cbk * 512, (cbk + 1) * 512)

                        def mmy(e, p_=p_, j0=j0, M=M, cs_=cs_):
                            last = None
                            for fc in range(8):
                                last = e.matmul(p_[0:M, :], lhsT=gT[:, fc, j0:j0 + M], rhs=w2b[:, fc, cs_], start=(fc == 0), stop=(fc == 7))
                            return last
                        k.op("tensor", mmy, reads=r_gT + r_w2, writes=[rp_])
                        if iy % 2 == 0:
                            k.op("vector", lambda e, p_=p_, y_=y_, jt=jt, M=M, cs_=cs_: e.tensor_copy(out=y_[0:M, jt, cs_], in_=p_[0:M, :]), reads=[rp_], writes=[ry])
                        else:
                            k.op("scalar", lambda e, p_=p_, y_=y_, jt=jt, M=M, cs_=cs_: e.activation(out=y_[0:M, jt, cs_], in_=p_[0:M, :], func=AF.Copy), reads=[rp_], writes=[ry])
                k.dma(YE[e_, 0:256, :].rearrange("(a p) n -> p a n", p=128), y_[:, 0:2, :], reads=[ry], writes=[r_YEe[e_]])
                k.dma(YE[e_, 256:288, :], y_[0:32, 2, :], reads=[ry], writes=[r_YEe[e_]])
            k.barrier()
        if self.dbg.get("MOE_PH", 9) < 3:
            return
        if not final:
            with ExitStack() as es:
                macc = k.sb("mo_macc", [128, 2, 2048], F32, es)
                r_macc = [R() for _ in range(2)]
                g2 = k.sb("mo_g2c", [128, 2048], F32, es)
                r_g2 = R()
                k.dma(g2[:], modD[1:2, 5 * D:6 * D].partition_broadcast(128), reads=[r_modD], writes=[r_g2])
                ye = [k.sb("mo_yec%d" % i, [32, 2048], BF16, es) for i in range(2)]
                r_ye = [R() for _ in range(2)]
                ws = [k.sb("mo_wsc%d" % i, [32, 256], BF16, es) for i in range(2)]
                r_ws = [R() for _ in range(2)]
                xt = [k.sb("mo_x2c%d" % i, [128, 2048], F32, es) for i in range(2)]
                r_xt = [R() for _ in range(2)]
                ps = [k.ps("mo_psc%d" % i, [128, 512], F32, es) for i in range(4)]
                r_ps = [PR() for _ in range(4)]
                ip = 0
                for e_ in range(16):
                    y_, ry = ye[e_ % 2], r_ye[e_ % 2]
                    w_, rw = ws[e_ % 2], r_ws[e_ % 2]
                    k.dma(y_[:], YE[e_, 256:288, :], reads=[r_YEe[e_]], writes=[ry])
                    k.dma(w_[:], WSC[e_], reads=[r_WSe[e_]], writes=[rw])
                    for tl in range(2):
                        for cbk in range(4):
                            p_, rp_ = ps[ip % 4], r_ps[ip % 4]
                            ip += 1
                            cs_ = slice(cbk * 512, (cbk + 1) * 512)
                            k.op("tensor", lambda e, p_=p_, w_=w_, y_=y_, tl=tl, cs_=cs_: e.matmul(p_[:], lhsT=w_[:, tl * 128:(tl + 1) * 128], rhs=y_[:, cs_], start=True, stop=True),
                                 reads=[rw, ry], writes=[rp_])
                            if e_ == 0:
                                k.op("vector", lambda e, p_=p_, tl=tl, cs_=cs_: e.tensor_copy(out=macc[:, tl, cs_], in_=p_[:]), reads=[rp_], writes=[r_macc[tl]])
                            else:
                                k.op("vector", lambda e, p_=p_, tl=tl, cs_=cs_: e.tensor_tensor(out=macc[:, tl, cs_], in0=macc[:, tl, cs_], in1=p_[:], op=ALU.add),
                                     reads=[rp_, r_macc[tl]], writes=[r_macc[tl]])
                if not final:
                    for tl in range(2):
                        rows = slice(tl * 128, (tl + 1) * 128)
                        x_, rx = xt[tl], r_xt[tl]
                        k.dma(x_[:], x1[rows, :], reads=[r_x1_t[tl]], writes=[rx])
                        k.op("vector", lambda e, tl=tl: e.tensor_tensor(out=macc[:, tl, :], in0=macc[:, tl, :], in1=g2[:], op=ALU.mult), reads=[r_macc[tl], r_g2], writes=[r_macc[tl]])
                        k.op("vector", lambda e, tl=tl, x_=x_: e.tensor_tensor(out=x_[:], in0=x_[:], in1=macc[:, tl, :], op=ALU.add), reads=[rx, r_macc[tl]], writes=[rx])
                        k.dma(xo_d[rows, :], x_[:], reads=[rx], writes=[self.r_x2_t[tl]])
                k.barrier()
        with ExitStack() as es:
            YA = k.sb("mo_YA", [128, 16, 2, 1024], BF16, es)
            r_YA = [R() for _ in range(16)]
            WA = k.sb("mo_WA", [128, 16, 2, 1024], BF16, es)
            r_WA = [R() for _ in range(16)]
            g2 = k.sb("mo_g2l", [128, 2048], F32, es)
            r_g2 = R()
            k.dma(g2[:], modD[0:1, 5 * D:6 * D].partition_broadcast(128), reads=[r_modD], writes=[r_g2])
            xt = [k.sb("mo_x2l%d" % i, [128, 1024], F32, es) for i in range(2)]
            r_xt = [R() for _ in range(2)]
            tmp = [k.sb("mo_t2l%d" % i, [128, 512], F32, es) for i in range(2)]
            r_tmp = [R() for _ in range(2)]
            ps = [k.ps("mo_psl%d" % i, [128, 512], F32, es) for i in range(4)]
            r_ps = [PR() for _ in range(4)]
            ip = 0
            ix = 0
            for dh in range(2):
                d0 = dh * 1024
                for e_ in range(16):
                    k.dma(YA[:, e_, :, :], YE[e_, 0:256, d0:d0 + 1024].rearrange("(a p) n -> p a n", p=128), reads=[r_YEe[e_]], writes=[r_YA[e_]])
                for nh in range(2):
                    n0 = nh * 1024
                    for e_ in range(16):
                        k.dma(WA[:, e_, :, :], WS[e_].rearrange("p (a b) -> p a b", a=2)[:, :, n0:n0 + 1024], reads=[r_WSe[e_]], writes=[r_WA[e_]])
                    for tl in range(8):
                        t = 2 + nh * 8 + tl
                        rows = slice(t * 128, (t + 1) * 128)
                        x_, rx = xt[ix % 2], r_xt[ix % 2]
                        ix += 1
                        k.dma(x_[:], x1[rows, d0:d0 + 1024], reads=[r_x1_t[t]], writes=[rx])
                        for cb in range(2):
                            p_, rp_ = ps[ip % 4], r_ps[ip % 4]
                            t_, rt_ = tmp[ip % 2], r_tmp[ip % 2]
                            ip += 1
                            cs_ = slice(cb * 512, (cb + 1) * 512)

                            def mms(e, p_=p_, tl=tl, cs_=cs_):
                                last = None
                                n = 0
                                for e2 in range(16):
                                    for jt in range(2):
                                        last = e.matmul(p_[:], lhsT=WA[:, e2, jt, tl * 128:(tl + 1) * 128], rhs=YA[:, e2, jt, cs_], start=(n == 0), stop=(n == 31))
                                        n += 1
                                return last
                            k.op("tensor", mms, reads=r_WA + r_YA, writes=[rp_])
                            k.op("vector", lambda e, p_=p_, t_=t_, cs_=cs_, d0=d0: e.tensor_tensor(out=t_[:], in0=p_[:], in1=g2[:, d0 + cs_.start:d0 + cs_.stop], op=ALU.mult),
                                 reads=[rp_, r_g2], writes=[rt_])
                            k.op("vector", lambda e, t_=t_, x_=x_, cs_=cs_: e.tensor_tensor(out=x_[:, cs_], in0=x_[:, cs_], in1=t_[:], op=ALU.add), reads=[rt_, rx], writes=[rx])
                        if final:
                            k.dma(xo_d[(t - 2) * 128:(t - 1) * 128, d0:d0 + 1024], x_[:], reads=[rx], writes=[r_xo_d])
                        else:
                            k.dma(xo_d[rows, d0:d0 + 1024], x_[:], reads=[rx], writes=[self.r_x2_t[t]])
            k.barrier()

    def declare_inputs(self):
        dp = self.depth
        self.dram("xcat", [T, D], F32, "ExternalInput")
        self.dram("ada_w", [dp, D, 6 * D], F32, "ExternalInput")
        self.dram("ada_b2", [dp, 2, 6 * D], F32, "ExternalInput")
        self.dram("norm1_gT", [dp, 128, 16], F32, "ExternalInput")
        self.dram("w_in", [dp, D, ZC], F32, "ExternalInput")
        self.dram("na_qg", [dp, 1, 512], F32, "ExternalInput")
        self.dram("na_kg", [dp, 1, 512], F32, "ExternalInput")
        self.dram("na_bias", [dp, 8, 5, 5, 128, 128], F32, "ExternalInput")
        self.dram("mla_cqg", [dp, 1, 512], F32, "ExternalInput")
        self.dram("mla_ckvg", [dp, 1, 256], F32, "ExternalInput")
        self.dram("mla_qg", [dp, 1, 768], F32, "ExternalInput")
        self.dram("mla_kng", [dp, 1, 512], F32, "ExternalInput")
        self.dram("mla_kpg", [dp, 1, 64], F32, "ExternalInput")
        self.dram("mla_w_uq", [dp, 512, 768], F32, "ExternalInput")
        self.dram("mla_w_ukv", [dp, 256, 1024], F32, "ExternalInput")
        self.dram("rope_cos", [T, 64], F32, "ExternalInput")
        self.dram("dft_c", [128, 256], BF16, "ExternalInput")
        self.dram("norm2_gb", [dp, 1, D], F32, "ExternalInput")
        self.dram("moe_router", [dp, D, 16], F32, "ExternalInput")
        self.dram("moe_w1", [dp, 16, D, 1024], F32, "ExternalInput")
        self.dram("moe_w3", [dp, 16, D, 1024], F32, "ExternalInput")
        self.dram("moe_w2", [dp, 16, 1024, D], F32, "ExternalInput")
        self.dram("moe_sel", [32, 16, 128], F32, "ExternalInput")
        self.dram("moe_jidx", [128, 2], F32, "ExternalInput")
        self.dram("w_br", [dp, 4, 512, D], F32, "ExternalInput")
        self.dram("w_out", [dp, D, D], F32, "ExternalInput")
        self.dram("rw_mu", [dp, 1, 1760], F32, "ExternalInput")
        self.dram("rw_kk", [dp, 1, 512], F32, "ExternalInput")
        self.dram("rw_ka", [dp, 1, 512], F32, "ExternalInput")
        self.dram("rw_w0f", [dp, 1, 1024], F32, "ExternalInput")
        self.dram("rw_a0f", [dp, 1, 1024], F32, "ExternalInput")
        self.dram("rw_wa2", [dp, 32, 2, 2, 512], F32, "ExternalInput")
        self.dram("rw_lnw", [dp, 1, 512], F32, "ExternalInput")
        self.dram("rw_lnb", [dp, 1, 512], F32, "ExternalInput")
        self.dram("rw_rk", [dp, 1, 512], F32, "ExternalInput")
        self.dram("rw_g2", [dp, 96, 512], F32, "ExternalInput")
        self.dram("rw_e2", [128, 64, 128], F32, "ExternalInput")
        self.dram("rw_pm", [128, 128], F32, "ExternalInput")
        self.dram("rw_j64", [64, 64], F32, "ExternalInput")
        self.dram("rw_masks", [128, 6, 128], F32, "ExternalInput")
        self.dram("rw_dm", [128, 2, 1024], F32, "ExternalInput")
        self.dram("dft_L", [2, L, L], BF16, "ExternalInput")
        self.dram("dft_C", [2, NCTX, NCTX], BF16, "ExternalInput")
        self.dram("rope_sin", [T, 64], F32, "ExternalInput")

    def build(self):
        k = self.k
        self.declare_inputs()
        self.eps_t = k.sb("eps_t", [128, 1], F32)
        r = R()
        k.op("vector", lambda e: e.memset(self.eps_t[:], EPS), writes=[r])
        k.barrier()
        self.consts()
        xname = "xcat"
        for l in range(self.depth):
            if self.want("mod"):
                self.stage_mod(l)
            else:
                self.dram("modD%d" % l, [2, 6 * D], F32)
            if self.want("normz"):
                self.stage_normz(l, xname)
            else:
                self.dram("z%d" % l, [T, ZC], F32)
                self.r_ztile = [self.dr["z%d" % l][1]] * NT
            if self.want("na"):
                self.stage_na(l)
            if self.want("mla"):
                self.stage_mla(l)
            if self.want("fourier"):
                self.stage_fourier(l)
            if self.want("rwprep"):
                self.stage_rwprep(l)
            if self.want("rwscan"):
                self.stage_rwscan(l)
            if self.want("rwout"):
                self.stage_rwout(l)
            else:
                for n in ("y_fn%d", "y_na%d", "y_mla%d", "y_rw%d"):
                    if (n % l) not in self.dr:
                        self.dram(n % l, [T, 512], F32)
            if self.want("merge"):
                self.stage_merge(l, xname)
            else:
                self.dram("x1_%d" % l, [T, D], F32)
                self.r_x1_t = [self.dr["x1_%d" % l][1]] * NT
            if self.want("moe"):
                self.stage_moe(l, final=(l == self.depth - 1) and self.final_out)
                xname = "x2_%d" % l
        k.finish([])
        return self.nc


def _na_bias_index():
    rows, kh, W, KW = 32, 8, 64, 16
    dr = np.zeros((5, 5, 128, 128), np.int64)
    dc = np.zeros((5, 5, 128, 128), np.int64)
    va = np.zeros((5, 5, 128, 128), bool)
    kk = np.arange(128)
    qq = np.arange(128)
    for cls, j in enumerate((0, 1, 5, 14, 15)):
        lo = min(max(j - 2, 0), 11)
        for i in range(5):
            kt = lo + i
            krow = (2 * kt + kk // 64)[:, None]
            kc = (kk % 64)[:, None]
            r = (2 * j + qq // 64)[None, :]
            qc = (qq % 64)[None, :]
            rs = np.clip(r - kh // 2, 0, rows - kh)
            cs = np.clip(qc - KW // 2, 0, W - KW)
            v = (krow >= rs) & (krow < rs + kh) & (kc >= cs) & (kc < cs + KW)
            dr[cls, i] = np.clip(krow - r + 8 - 1, 0, 14)
            dc[cls, i] = np.clip(kc - qc + KW - 1, 0, 2 * KW - 2)
            va[cls, i] = v
    return dr, dc, va


def _rope_tables():
    pos = np.arange(L)
    prow, pcol = pos // 64, pos % 64
    freqs = 10000.0 ** (-np.arange(16, dtype=np.float32) / 16)
    ar = prow[:, None].astype(np.float32) * freqs[None, :]
    ac = pcol[:, None].astype(np.float32) * freqs[None, :]
    cos = np.concatenate([np.cos(ar), np.cos(ar), np.cos(ac), np.cos(ac)], 1)
    sin = np.concatenate([-np.sin(ar), np.sin(ar), -np.sin(ac), np.sin(ac)], 1)
    cosT = np.concatenate([np.ones((NCTX, 64), np.float32), cos.astype(np.float32)], 0)
    sinT = np.concatenate([np.zeros((NCTX, 64), np.float32), sin.astype(np.float32)], 0)
    return np.ascontiguousarray(cosT), np.ascontiguousarray(sinT)


_DFT = None


def _dft_consts():
    global _DFT
    if _DFT is None:
        c = np.arange(128)
        ang = 2 * np.pi * ((c[:, None] * c[None, :]) % 128) / 128.0
        dft_c = bf16_np(np.concatenate([np.cos(ang), np.sin(ang)], 1))
        out = []
        for n in (L, NCTX):
            i = np.arange(n, dtype=np.int64)
            ang = 2 * np.pi * ((i[:, None] * i[None, :]) % n) / float(n)
            sc = 1.0 / np.sqrt(n * 128.0)
            out.append(bf16_np(np.stack([np.cos(ang) * sc, -np.sin(ang) * sc], 0)))
        _DFT = (dft_c, out[0], out[1])
    return _DFT


def _rw_chunk_consts():
    MS = np.zeros((128, 128), np.float32)
    MI = np.zeros((128, 128), np.float32)
    BO = np.zeros((128, 128), np.float32)
    for d in range(2):
        for a in range(64):
            for c in range(64):
                before = (a < c) if d == 0 else (a > c)
                MS[d * 64 + a, d * 64 + c] = 1.0 if before else 0.0
                MI[d * 64 + a, d * 64 + c] = 1.0 if (before or a == c) else 0.0
                BO[d * 64 + a, d * 64 + c] = 1.0
    masks = np.stack([MI, BO, MS, -MS, -MS.T, -MI], 1).astype(np.float32)
    dm = np.zeros((128, 8, 2, 64), np.float32)
    dm[0:64, :, 0, :] = 1.0
    dm[64:128, :, 1, :] = 1.0
    dm = dm.reshape(128, 1024)
    return np.ascontiguousarray(masks), np.ascontiguousarray(np.stack([dm, -dm], 1))


def _rw_consts():
    e2 = np.zeros((128, 64, 128), np.float32)
    for i in range(64):
        e2[i, i, 0:64] = 1.0
        e2[64 + 63 - i, i, 64:128] = 1.0
    pm = np.zeros((128, 128), np.float32)
    for i in range(64):
        pm[i, i] = 1.0
        pm[64 + i, 64 + 63 - i] = 1.0
    j64 = np.ascontiguousarray(np.eye(64, dtype=np.float32)[::-1])
    return e2, pm, j64


def host_core(inp, b):
    f32 = np.float32
    m = {}
    m["xcat"] = np.concatenate([inp["ctx"][b], inp["x"][b]], 0).astype(f32)
    cv = np.stack([inp["c"][b], inp["c_ctx"]], 0)
    m["cvT"] = np.ascontiguousarray(cv.reshape(2, 16, 128).transpose(2, 1, 0))
    return m


def host_inputs(inp, b, depth=DEPTH):
    m = host_shared(inp, depth)
    m.update(host_core(inp, b))
    return m


def host_shared(inp, depth=DEPTH):
    f32 = np.float32
    m = {}
    m["ada_w"] = inp["ada_w"][:depth]
    m["ada_b2"] = np.ascontiguousarray(np.broadcast_to(inp["ada_b"][:depth, None, :], (depth, 2, 6 * D)))
    m["norm1_gT"] = np.ascontiguousarray(inp["norm1_g"][:depth].reshape(depth, 16, 128).transpose(0, 2, 1))
    m["w_in"] = inp["w_in"][:depth]
    m["na_qg"] = np.ascontiguousarray(np.tile(inp["na_q_norm"][:depth, None, :], (1, 1, 8)))
    m["na_kg"] = np.ascontiguousarray(np.tile(inp["na_k_norm"][:depth, None, :], (1, 1, 8)))
    dr, dc, va = _na_bias_index()
    rpb = inp["na_rpb"][:depth]
    gathered = rpb[:, :, dr, dc]
    m["na_bias"] = np.where(va[None, None], gathered, f32(-1e30)).astype(f32)
    m["mla_cqg"] = np.ascontiguousarray(inp["mla_cq_norm"][:depth, None, :])
    m["mla_ckvg"] = np.ascontiguousarray(inp["mla_ckv_norm"][:depth, None, :])
    m["mla_qg"] = np.ascontiguousarray(np.tile(inp["mla_q_norm"][:depth, None, :], (1, 1, 4)))
    m["mla_kng"] = np.ascontiguousarray(np.tile(inp["mla_k_norm"][:depth, None, :128], (1, 1, 4)))
    m["mla_kpg"] = np.ascontiguousarray(inp["mla_k_norm"][:depth, None, 128:192])
    m["mla_w_uq"] = inp["mla_w_uq"][:depth]
    m["mla_w_ukv"] = inp["mla_w_ukv"][:depth]
    m["dft_c"], m["dft_L"], m["dft_C"] = _dft_consts()
    m["rw_mu"] = np.ascontiguousarray(np.concatenate([inp["rw_mu_ks"][:depth], inp["rw_mu_qs"][:depth]], 1)[:, None, :])
    m["rw_kk"] = np.ascontiguousarray(inp["rw_k_k"][:depth, None, :])
    m["rw_ka"] = np.ascontiguousarray(inp["rw_k_a"][:depth, None, :])
    m["rw_w0f"] = np.ascontiguousarray(inp["rw_w0"][:depth].reshape(depth, 1, 1024))
    m["rw_a0f"] = np.ascontiguousarray(inp["rw_a0"][:depth].reshape(depth, 1, 1024))
    wa2 = np.stack([inp["rw_w2"][:depth], inp["rw_a2"][:depth]], 1)
    m["rw_wa2"] = np.ascontiguousarray(wa2.transpose(0, 3, 1, 2, 4))
    m["rw_lnw"] = np.ascontiguousarray(inp["rw_ln_w"][:depth, None, :])
    m["rw_lnb"] = np.ascontiguousarray(inp["rw_ln_b"][:depth, None, :])
    m["rw_rk"] = np.ascontiguousarray(inp["rw_r_k"][:depth].reshape(depth, 1, 512))
    m["rw_g2"] = inp["rw_g2"][:depth]
    m["rw_e2"], m["rw_pm"], m["rw_j64"] = _rw_consts()
    m["rw_masks"], m["rw_dm"] = _rw_chunk_consts()
    m["w_br"] = inp["w_br"][:depth]
    m["norm2_gb"] = np.ascontiguousarray(inp["norm2_g"][:depth, None, :])
    m["moe_router"] = inp["moe_router"][:depth]
    m["moe_w1"] = inp["moe_w1"][:depth]
    m["moe_w3"] = inp["moe_w3"][:depth]
    m["moe_w2"] = inp["moe_w2"][:depth]
    sel = np.zeros((32, 16, 128), np.float32)
    for e in range(16):
        sel[e, e, :] = 1.0
    m["moe_sel"] = sel
    m["moe_jidx"] = np.ascontiguousarray(np.stack([np.arange(128), np.arange(128) + 128], 1).astype(np.float32))
    m["w_out"] = inp["w_out"][:depth]
    cosT, sinT = _rope_tables()
    m["rope_cos"], m["rope_sin"] = cosT, sinT
    return m


_PROG = None


def kernel(**inputs):
    global _PROG
    inp = {k_: np.asarray(v) for k_, v in inputs.items()}
    if _PROG is None:
        P = Prog(depth=DEPTH)
        P.build()
        _PROG = P
    P = _PROG
    shared = host_shared(inp, DEPTH)
    in_maps = []
    for b in range(8):
        m = dict(shared)
        m.update(host_core(inp, b))
        in_maps.append({n: m[n] for n, kd in P.kinds.items() if kd == "ExternalInput"})
    res = run_bass_kernel_spmd(P.nc, in_maps, core_ids=list(range(8)))
    return np.stack([np.asarray(r["out"], dtype=np.float32) for r in res.results], 0)
```

```python
import numpy as np
import ml_dtypes
from contextlib import ExitStack
import concourse.bass as bass
import concourse.mybir as mybir
from concourse.bass_utils import run_bass_kernel_spmd

F32 = mybir.dt.float32
BF16 = mybir.dt.bfloat16
I32 = mybir.dt.int32
F32R = mybir.dt.float32r


def fr(ap):
    return ap.bitcast(F32R)
AF = mybir.ActivationFunctionType
ALU = mybir.AluOpType
AX = mybir.AxisListType


class R:
    __slots__ = ("w", "r", "name", "excl")

    def __init__(self, name="", excl=False):
        self.w = []
        self.r = []
        self.name = name
        self.excl = excl


def PR():
    return R("psum", True)


class K:
    ENG = ("tensor", "vector", "scalar", "gpsimd", "sync")
    NDMA = 24
    MAXSEM = 30000

    def __init__(self, nc, es):
        self.nc = nc
        self.es = es
        self.prog = {e: [] for e in self.ENG}
        self.sem = {}
        self.epoch = {e: 0 for e in self.ENG}
        for e in self.ENG:
            self.sem[(e, 0)] = es.enter_context(nc.semaphore("s_%s_0" % e))
        self.cnt = {e: 0 for e in self.ENG}
        self.waited = {e: {} for e in self.ENG}
        self.dsem = [es.enter_context(nc.semaphore("d%d" % i)) for i in range(self.NDMA)]
        self.dval = [0] * self.NDMA
        self.dnext = 0
        self.ninstr = 0

    def sb(self, name, shape, dt=F32, es=None):
        self.uid = getattr(self, "uid", 0) + 1
        name = "%s_u%d" % (name, self.uid)
        return (es or self.es).enter_context(self.nc.sbuf_tensor(name, list(shape), dt))

    def ps(self, name, shape, dt=F32, es=None):
        n = int(np.prod(shape[1:])) * (2 if dt == BF16 else 4)
        assert n % 2048 == 0, (name, shape, n)
        self.uid = getattr(self, "uid", 0) + 1
        name = "%s_u%d" % (name, self.uid)
        return (es or self.es).enter_context(self.nc.psum_tensor(name, list(shape), dt))

    def _key(self, dep):
        return dep[0] if dep[0] in self.ENG else ("d", dep[0])

    def _collect(self, reads, writes):
        deps = {}
        for r in reads:
            for d in r.w:
                k = d[0]
                if deps.get(k, 0) < d[1]:
                    deps[k] = d[1]
        for w in writes:
            for d in w.w + w.r:
                k = d[0]
                if deps.get(k, 0) < d[1]:
                    deps[k] = d[1]
        return deps

    def _emit_waits(self, eng, deps):
        wd = self.waited[eng]
        for k, v in deps.items():
            if wd.get(k, 0) >= v:
                continue
            wd[k] = v
            sem = self.sem[k] if isinstance(k, tuple) else self.dsem[k]
            getattr(self.nc, eng).wait_ge(sem, v)

    def _mark(self, dep, reads, writes):
        for r in reads:
            r.r.append(dep)
        for w in writes:
            w.w = [dep]
            w.r = []

    def op(self, eng, fn, reads=(), writes=()):
        if any(r.excl for r in reads):
            writes = list(writes) + [r for r in reads if r.excl]
            reads = [r for r in reads if not r.excl]
        deps = self._collect(reads, writes)
        self._emit_waits(eng, deps)
        if self.cnt[eng] >= self.MAXSEM:
            self.epoch[eng] += 1
            self.cnt[eng] = 0
            self.sem[(eng, self.epoch[eng])] = self.es.enter_context(
                self.nc.semaphore("s_%s_%d" % (eng, self.epoch[eng])))
        self.cnt[eng] += 1
        key = (eng, self.epoch[eng])
        dep = (key, self.cnt[eng])
        fn(getattr(self.nc, eng)).then_inc(self.sem[key], 1)
        self._mark(dep, reads, writes)
        self.ninstr += 1

    STORE_Q = "scalar"

    def dma(self, out, in_, reads=(), writes=(), q="sync", **kw):
        q = self.STORE_Q if str(out.space) == "DRAM" else "sync"
        deps = self._collect(reads, writes)
        i = self.dnext
        self.dnext = (self.dnext + 1) % self.NDMA
        if self.dval[i] > 0:
            deps[i] = max(deps.get(i, 0), self.dval[i])
        self._emit_waits(q, deps)
        self.dval[i] += 16
        dep = (i, self.dval[i])
        getattr(self.nc, q).dma_start(out=out, in_=in_, **kw).then_inc(self.dsem[i], 16)
        self._mark(dep, reads, writes)
        self.ninstr += 1

    def finish(self, final_regions):
        deps = {}
        for e in self.ENG:
            if self.cnt[e]:
                deps[(e, self.epoch[e])] = self.cnt[e]
        for i in range(self.NDMA):
            if self.dval[i]:
                deps[i] = self.dval[i]
        self._emit_waits("sync", deps)

    def barrier(self):
        deps = {}
        for e in self.ENG:
            if self.cnt[e]:
                deps[(e, self.epoch[e])] = self.cnt[e]
        for i in range(self.NDMA):
            if self.dval[i]:
                deps[i] = self.dval[i]
        for e in self.ENG:
            self._emit_waits(e, dict(deps))


D = 2048
L = 2048
NCTX = 256
T = L + NCTX
NT = T // 128
DEPTH = 2
ZC = 12832
C_NAK, C_NAV, C_CKV, C_KPE, C_RWKS = 0, 512, 1024, 1280, 1344
C_NAQ, C_CQ, C_RWQS, C_FN, C_GATE = 2496, 3008, 3520, 4128, 4640
EPS = 1e-6


def bf16_np(a):
    return np.asarray(a, dtype=np.float32).astype(ml_dtypes.bfloat16)


class Prog:
    def __init__(self, depth=DEPTH, ext=None, stages=None):
        self.depth = depth
        self.ext = ext or {}
        self.stages = stages
        self.nc = bass.Bass("TRN2", target_bir_lowering=False)
        self.es = ExitStack()
        self.k = K(self.nc, self.es)
        self.dr = {}
        self.kinds = {}
        self.modT = {}
        self.final_out = True
        import os
        self.dbg = {kk[4:]: int(v) for kk, v in os.environ.items() if kk.startswith("DBG_")}

    def dram(self, name, shape, dt=F32, kind="Internal"):
        kind = self.ext.get(name, kind)
        t = self.nc.dram_tensor(name, list(shape), dt, kind=kind)
        self.dr[name] = (t.ap(), R(name))
        self.kinds[name] = kind
        return self.dr[name]

    def want(self, s):
        return self.stages is None or s in self.stages

    def consts(self):
        k = self.k
        self.identf = k.sb("identf", [128, 128], F32)
        self.ident = k.sb("ident", [128, 128], BF16)
        self.r_ident = R("ident")
        k.op("gpsimd", lambda e: e.memset(self.identf[:], 0.0), writes=[self.r_ident])
        k.op("gpsimd", lambda e: e.affine_select(out=self.identf[:], in_=self.identf[:], pattern=[[-1, 128]],
                                                  compare_op=ALU.not_equal, fill=1.0, base=0, channel_multiplier=1),
             reads=[self.r_ident], writes=[self.r_ident])
        k.op("vector", lambda e: e.tensor_copy(out=self.ident[:], in_=self.identf[:]),
             reads=[self.r_ident], writes=[self.r_ident])
        ap, r = self.dram("cvT", [128, 16, 2], F32, "ExternalInput")
        self.sT = k.sb("sT", [128, 16, 2], F32)
        self.r_sT = R("sT")
        k.dma(self.sT[:], ap, reads=[r], writes=[self.r_sT])
        k.op("scalar", lambda e: e.activation(out=self.sT[:], in_=self.sT[:], func=AF.Silu),
             reads=[self.r_sT], writes=[self.r_sT])

    def stage_mod(self, l):
        k = self.k
        adaw, r_adaw = self.dr["ada_w"]
        adab, r_adab = self.dr["ada_b2"]
        modD, r_modD = self.dram("modD%d" % l, [2, 6 * D], F32)
        modT = k.sb("modT%d" % l, [128, 6, 16, 2], F32)
        r_modT = R()
        self.modT[l] = (modT, r_modT)
        with ExitStack() as es:
            wt = [k.sb("mod_w%d" % i, [128, 2048], F32, es) for i in range(3)]
            r_wt = [R() for _ in range(3)]
            ps = k.ps("mod_ps", [2, 2048], F32, es)
            r_ps = PR()
            pT = k.ps("mod_pT", [128, 4, 128], F32, es)
            r_pT = PR()
            row = [k.sb("mod_row%d" % i, [128, 2048], F32, es) for i in range(2)]
            r_row = [R() for _ in range(2)]
            for i in range(2):
                k.op("gpsimd", lambda e, i=i: e.memset(row[i][:], 0.0), writes=[r_row[i]])
            bias = k.sb("mod_bias", [2, 6 * D], F32, es)
            r_bias = R()
            k.dma(bias[:], adab[l], reads=[r_adab], writes=[r_bias])
            i2 = self.identf[0:2, 0:2]
            it = 0
            for nb in range(6):
                for kc in range(16):
                    w = wt[it % 3]
                    rw = r_wt[it % 3]
                    it += 1
                    k.dma(w[:], adaw[l, kc * 128:(kc + 1) * 128, nb * 2048:(nb + 1) * 2048], reads=[r_adaw], writes=[rw])

                    def mm(e, w=w, kc=kc):
                        last = None
                        for j in range(4):
                            last = e.matmul(ps[:, j * 512:(j + 1) * 512], lhsT=self.sT[:, kc, :], rhs=w[:, j * 512:(j + 1) * 512],
                                            start=(kc == 0), stop=(kc == 15))
                        return last
                    k.op("tensor", mm, reads=[rw, self.r_sT], writes=[r_ps])
                rt = row[nb % 2]
                rr = r_row[nb % 2]
                k.op("vector", lambda e, rt=rt, nb=nb: e.tensor_tensor(out=rt[0:2, :], in0=ps[:], in1=bias[:, nb * 2048:(nb + 1) * 2048], op=ALU.add),
                     reads=[r_ps, r_bias], writes=[rr])
                k.dma(modD[:, nb * 2048:(nb + 1) * 2048], rt[0:2, :], reads=[rr], writes=[r_modD], q="gpsimd")

                for g in range(4):
                    def tr(e, rt=rt, g=g):
                        last = None
                        for c in range(4):
                            cc = g * 4 + c
                            last = e.transpose(out=pT[:, c, :], in_=rt[:, cc * 128:(cc + 1) * 128], identity=self.identf[:])
                        return last
                    k.op("tensor", tr, reads=[rr, self.r_ident], writes=[r_pT])
                    k.op("vector", lambda e, nb=nb, g=g: e.tensor_copy(out=modT[:, nb, g * 4:(g + 1) * 4, :], in_=pT[:, :, 0:2]),
                         reads=[r_pT], writes=[r_modT])
            k.barrier()

    def stage_normz(self, l, xname):
        k = self.k
        xin, r_xin = self.dr[xname]
        win, r_win = self.dr["w_in"]
        g1T_d, r_g1T = self.dr["norm1_gT"]
        z, r_z = self.dram("z%d" % l, [T, ZC], F32)
        self.r_ztile = [R() for _ in range(NT)]
        modT, r_modT = self.modT[l]
        with ExitStack() as es:
            hT = k.sb("hT", [128, 16, T], BF16, es)
            r_hT = [[R() for _ in range(16)] for _ in range(NT)]
            aT = k.sb("nz_aT", [128, 16, 2], F32, es)
            r_aT = R()
            gT = k.sb("nz_gT", [128, 16], F32, es)
            k.dma(gT[:], g1T_d[l], reads=[r_g1T], writes=[r_aT])
            k.op("vector", lambda e: e.tensor_scalar(out=aT[:], in0=modT[:, 1, :, :], scalar1=1.0, scalar2=None, op0=ALU.add),
                 reads=[r_modT], writes=[r_aT])
            k.op("vector", lambda e: e.tensor_tensor(out=aT[:], in0=aT[:], in1=gT[:].unsqueeze(2).to_broadcast([128, 16, 2]), op=ALU.mult),
                 reads=[r_aT], writes=[r_aT])
            self.norm_to_hT(es, xin, r_xin, hT, r_hT, aT, r_aT, modT[:, 0, :, :], r_modT, "nz")
            wf = [k.sb("nz_wf%d" % i, [128, 4, 512], F32, es) for i in range(3)]
            r_wf = [R() for _ in range(3)]
            wb = [k.sb("nz_wb%d" % i, [128, 16, 512], BF16, es) for i in range(2)]
            r_wb = [[R() for _ in range(4)] for _ in range(2)]
            pz = [k.ps("nz_pz%d" % i, [128, 512], F32, es) for i in range(4)]
            r_pz = [PR() for _ in range(4)]
            zo = [k.sb("nz_zo%d" % i, [128, 512], F32, es) for i in range(4)]
            r_zo = [R() for _ in range(4)]
            ncb = (ZC + 511) // 512
            cnt = 0
            ld = 0
            for cb in range(self.dbg.get("NZ_NCB", ncb)):
                c0 = cb * 512
                cw = min(512, ZC - c0)
                b, rb = wb[cb % 2], r_wb[cb % 2]
                for g in range(4):
                    f, rf = wf[ld % 3], r_wf[ld % 3]
                    k.dma(f[:, :, 0:cw], win[l, g * 512:(g + 1) * 512, c0:c0 + cw].rearrange("(kc p) n -> p kc n", p=128),
                          reads=[r_win], writes=[rf])
                    eng = ("vector", "scalar")[ld % 2]
                    ld += 1
                    if eng == "scalar":
                        k.op("scalar", lambda e, f=f, b=b, cw=cw, g=g: e.activation(out=b[:, g * 4:(g + 1) * 4, 0:cw], in_=f[:, :, 0:cw], func=AF.Copy),
                             reads=[rf], writes=[rb[g]])
                    else:
                        k.op(eng, lambda e, f=f, b=b, cw=cw, g=g: e.tensor_copy(out=b[:, g * 4:(g + 1) * 4, 0:cw], in_=f[:, :, 0:cw]),
                             reads=[rf], writes=[rb[g]])
                for t in range(self.dbg.get("NZ_NT", NT)):
                    p, rp = pz[cnt % 4], r_pz[cnt % 4]
                    o, ro = zo[cnt % 4], r_zo[cnt % 4]
                    cnt += 1

                    def mm(e, p=p, t=t, b=b, cw=cw):
                        last = None
                        for kc in range(16):
                            last = e.matmul(p[:, 0:cw], lhsT=hT[:, kc, t * 128:(t + 1) * 128], rhs=b[:, kc, 0:cw],
                                            start=(kc == 0), stop=(kc == 15))
                        return last
                    k.op("tensor", mm, reads=rb + r_hT[t], writes=[rp])
                    if t % 2 == 0:
                        k.op("vector", lambda e, o=o, p=p, cw=cw: e.tensor_copy(out=o[:, 0:cw], in_=p[:, 0:cw]), reads=[rp], writes=[ro])
                    else:
                        k.op("scalar", lambda e, o=o, p=p, cw=cw: e.activation(out=o[:, 0:cw], in_=p[:, 0:cw], func=AF.Copy), reads=[rp], writes=[ro])
                    k.dma(z[t * 128:(t + 1) * 128, c0:c0 + cw], o[:, 0:cw], reads=[ro], writes=[self.r_ztile[t]], q="gpsimd")
            k.barrier()

    def norm_to_hT(self, es, xin, r_xin, hT, r_hT, aT, r_aT, shT, r_shT, pfx, h_tok=None):
        k = self.k
        xt = [k.sb(pfx + "_xt%d" % i, [128, D], F32, es) for i in range(2)]
        r_xt = [R() for _ in range(2)]
        xn = [k.sb(pfx + "_xn%d" % i, [128, D], BF16, es) for i in range(2)]
        r_xn = [R() for _ in range(2)]
        junk = k.sb(pfx + "_junk", [128, D], BF16, es)
        r_junk = R()
        st = k.sb(pfx + "_st", [128, NT, 2], F32, es)
        r_stl = [R() for _ in range(NT)]
        pt = [k.ps(pfx + "_pt%d" % i, [128, 8, 128], BF16, es) for i in range(2)]
        r_pt = [PR() for _ in range(2)]
        for t in range(self.dbg.get("NZ_NT", NT)):
            row = 1 if t < 2 else 0
            r_st = r_stl[t]
            x_, rx = xt[t % 2], r_xt[t % 2]
            n_, rn = xn[t % 2], r_xn[t % 2]
            STEP = self.dbg.get("STEP", 99)
            if STEP < 1:
                continue
            k.dma(x_[:], xin[t * 128:(t + 1) * 128, :], reads=[r_xin], writes=[rx])
            if STEP < 2:
                continue
            k.op("scalar", lambda e, x_=x_, t=t: e.activation(out=junk[:], in_=x_[:], func=AF.Square, accum_out=st[:, t, 0:1]),
                 reads=[rx], writes=[r_junk, r_st])
            if STEP < 3:
                continue
            k.op("scalar", lambda e, t=t: e.activation(out=st[:, t, 1:2], in_=st[:, t, 0:1], func=AF.Sqrt, bias=self.eps_t[:, 0:1], scale=1.0 / D),
                 reads=[r_st], writes=[r_st])
            if STEP < 4:
                continue
            k.op("vector", lambda e, t=t: e.reciprocal(out=st[:, t, 1:2], in_=st[:, t, 1:2]), reads=[r_st], writes=[r_st])
            if STEP < 5:
                continue
            k.op("vector", lambda e, x_=x_, n_=n_, t=t: e.tensor_scalar(out=n_[:], in0=x_[:], scalar1=st[:, t, 1:2], scalar2=None, op0=ALU.mult),
                 reads=[rx, r_st], writes=[rn])
            if STEP < 6:
                continue
            for half in range(2):
                p, rp = pt[half], r_pt[half]

                def tr(e, p=p, n_=n_, half=half):
                    last = None
                    for j in range(8):
                        dc = half * 8 + j
                        last = e.transpose(out=p[:, j, :], in_=n_[:, dc * 128:(dc + 1) * 128], identity=self.ident[:])
                    return last
                k.op("tensor", tr, reads=[rn, self.r_ident], writes=[rp])
                if STEP < 7:
                    continue
                for j in range(8):
                    dc = half * 8 + j
                    if STEP == 7 and j % 2 == 1:
                        continue
                    if STEP == 8 and j % 2 == 0:
                        continue
                    eng = "vector" if half == 0 else "scalar"
                    if eng == "vector":
                        k.op("vector", lambda e, p=p, j=j, dc=dc, t=t, row=row: e.tensor_scalar(
                            out=hT[:, dc, t * 128:(t + 1) * 128], in0=p[:, j, :], scalar1=aT[:, dc, row:row + 1],
                            scalar2=shT[:, dc, row:row + 1], op0=ALU.mult, op1=ALU.add),
                            reads=[rp, r_aT, r_shT], writes=[r_hT[t][dc]])
                    else:
                        k.op("scalar", lambda e, p=p, j=j, dc=dc, t=t, row=row: e.activation(
                            out=hT[:, dc, t * 128:(t + 1) * 128], in_=p[:, j, :], func=AF.Identity,
                            scale=aT[:, dc, row:row + 1], bias=shT[:, dc, row:row + 1]),
                            reads=[rp, r_aT, r_shT], writes=[r_hT[t][dc]])

    def head_rms(self, eng_sq, x_, w, nh, dh, st, junk, out, gain_bc, reads, r_st, r_junk, writes, tmp, r_tmp):
        k = self.k
        x3 = x_.rearrange("p (h d) -> p h d", h=nh)
        k.op("gpsimd", lambda e: e.tensor_tensor(out=tmp, in0=x_, in1=gain_bc, op=ALU.mult), reads=reads + [self.r_gain], writes=[r_tmp])
        k.op("vector", lambda e: e.tensor_tensor(out=junk, in0=x_, in1=x_, op=ALU.mult), reads=reads, writes=[r_junk])
        k.op("vector", lambda e: e.tensor_reduce(out=st, in_=junk.rearrange("p (h d) -> p h d", h=nh), axis=AX.X, op=ALU.add),
             reads=[r_junk], writes=[r_st])
        k.op("scalar", lambda e: e.activation(out=st, in_=st, func=AF.Sqrt, bias=self.eps_t[:, 0:1], scale=1.0 / dh),
             reads=[r_st], writes=[r_st])
        k.op("vector", lambda e: e.reciprocal(out=st, in_=st), reads=[r_st], writes=[r_st])
        k.op("vector", lambda e: e.tensor_tensor(out=out.rearrange("p (h d) -> p h d", h=nh), in0=tmp.rearrange("p (h d) -> p h d", h=nh),
                                                in1=st.unsqueeze(2).to_broadcast([128, nh, dh]), op=ALU.mult),
             reads=[r_tmp, r_st], writes=writes)

    def stage_na(self, l):
        k = self.k
        z, _ = self.dr["z%d" % l]
        r_zt = self.r_ztile
        yd, r_yd = self.dram("y_na%d" % l, [T, 512], F32)
        gq_d, r_gq = self.dr["na_qg"]
        gk_d, r_gk = self.dr["na_kg"]
        bias_d, r_bias_d = self.dr["na_bias"]
        with ExitStack() as es:
            QT = k.sb("na_QT", [128, 4, T], BF16, es)
            KT = k.sb("na_KT", [128, 4, T], BF16, es)
            r_QT = [R() for _ in range(NT)]
            r_KT = [R() for _ in range(NT)]
            Va = k.sb("na_V", [128, NT, 8, 65], BF16, es)
            r_Va = [R() for _ in range(NT)]
            yna = k.sb("na_y", [128, NT, 512], F32, es)
            r_yna = [R() for _ in range(NT)]
            gq = k.sb("na_gq", [128, 512], F32, es)
            gk = k.sb("na_gk", [128, 512], F32, es)
            self.r_gain = R()
            k.dma(gq[:], gq_d[l].partition_broadcast(128), reads=[r_gq], writes=[self.r_gain])
            k.dma(gk[:], gk_d[l].partition_broadcast(128), reads=[r_gk], writes=[self.r_gain])
            k.op("vector", lambda e: e.tensor_scalar(out=gq[:], in0=gq[:], scalar1=0.125, scalar2=None, op0=ALU.mult),
                 reads=[self.r_gain], writes=[self.r_gain])
            r_ones = R()
            k.op("gpsimd", lambda e: e.memset(Va[:, :, :, 64:65], 1.0), writes=r_Va)
            xin = [k.sb("na_x%d" % i, [128, 512], F32, es) for i in range(6)]
            r_xin = [R() for _ in range(6)]
            junk = k.sb("na_junk", [128, 512], F32, es)
            r_junk = R()
            tmp = k.sb("na_tmp", [128, 512], F32, es)
            r_tmp = R()
            st = k.sb("na_st", [128, NT, 2, 8], F32, es)
            xb = [k.sb("na_xb%d" % i, [128, 512], BF16, es) for i in range(2)]
            r_xb = [R() for _ in range(2)]
            ptr = [k.ps("na_ptr%d" % i, [128, 8, 128], BF16, es) for i in range(1)]
            r_ptr = [PR() for _ in range(1)]
            ld = 0
            nb = 0
            for t in range(NT):
                rows = slice(t * 128, (t + 1) * 128)
                for which, c0 in ((0, C_NAQ), (1, C_NAK)):
                    x_, rx = xin[ld % 6], r_xin[ld % 6]
                    ld += 1
                    k.dma(x_[:], z[rows, c0:c0 + 512], reads=[r_zt[t]], writes=[rx])
                    b_, rb = xb[nb % 2], r_xb[nb % 2]
                    nb += 1
                    r_st = R()
                    self.head_rms(None, x_[:], 512, 8, 64, st[:, t, which, :], junk[:], b_[:], (gq if which == 0 else gk)[:],
                                  [rx], r_st, r_junk, [rb], tmp[:], r_tmp)
                    p, rp = ptr[0], r_ptr[0]

                    def tr(e, p=p, b_=b_):
                        last = None
                        for j in range(4):
                            last = e.transpose(out=p[:, j, :], in_=b_[:, j * 128:(j + 1) * 128], identity=self.ident[:])
                        return last
                    k.op("tensor", tr, reads=[rb, self.r_ident], writes=[rp])
                    dst = QT if which == 0 else KT
                    rd = (r_QT if which == 0 else r_KT)[t]
                    k.op("scalar", lambda e, dst=dst, p=p, rows=rows: e.activation(out=dst[:, :, rows], in_=p[:, 0:4, :], func=AF.Copy),
                         reads=[rp], writes=[rd])
                x_, rx = xin[ld % 6], r_xin[ld % 6]
                ld += 1
                k.dma(x_[:], z[rows, C_NAV:C_NAV + 512], reads=[r_zt[t]], writes=[rx])
                k.op("vector", lambda e, x_=x_, t=t: e.tensor_copy(out=Va[:, t, :, 0:64], in_=x_[:].rearrange("p (h d) -> p h d", h=8)),
                     reads=[rx], writes=[r_Va[t]])
            bias = [k.sb("na_bias%d" % i, [128, 5, 5, 128], F32, es) for i in range(2)]
            r_bias = [R() for _ in range(2)]
            stp = [k.ps("na_st%d" % i, [128, 8, 128], F32, es) for i in range(2)]
            r_stp = [PR() for _ in range(2)]
            po = [k.ps("na_po%d" % i, [128, 512], F32, es) for i in range(2)]
            r_po = [PR() for _ in range(2)]
            sbt = [k.sb("na_sb%d" % i, [128, 5, 128], F32, es) for i in range(2)]
            r_sbt = [R() for _ in range(2)]
            PT = [k.sb("na_PT%d" % i, [128, 7, 128], BF16, es) for i in range(2)]
            r_PT = [R() for _ in range(2)]
            rc = k.sb("na_rc", [128, 2], F32, es)
            r_rc = [R(), R()]
            it = 0
            for h in range(8):
                bs, rbs = bias[h % 2], r_bias[h % 2]
                for c in range(5):
                    k.dma(bs[:, c, :, :], bias_d[l, h, c].rearrange("t k q -> k t q"), reads=[r_bias_d], writes=[rbs])
                hp, hs = h // 2, (h % 2) * 64
                for t in range(NT):
                    if t < 2:
                        kts = [0, 1]
                        nbias = 0
                        cls = 0
                    else:
                        j = t - 2
                        lo = min(max(j - 2, 0), 11)
                        kts = [lo + 2 + i for i in range(5)] + [0, 1]
                        nbias = 5
                        cls = {0: 0, 1: 1, 14: 3, 15: 4}.get(j, 2)
                    nk = len(kts)
                    sp, rsp = stp[it % 2], r_stp[it % 2]
                    pp, rpp = PT[it % 2], r_PT[it % 2]
                    sb_, rsb = sbt[it % 2], r_sbt[it % 2]
                    o_, ro = po[it % 2], r_po[it % 2]
                    rcc, rrc = rc[:, it % 2:it % 2 + 1], r_rc[it % 2]
                    it += 1

                    def qk(e, sp=sp, kts=kts, t=t, hp=hp, hs=hs):
                        last = None
                        for i, kt in enumerate(kts):
                            last = e.matmul(sp[:, i, :], lhsT=KT[hs:hs + 64, hp, kt * 128:(kt + 1) * 128],
                                            rhs=QT[hs:hs + 64, hp, t * 128:(t + 1) * 128], start=True, stop=True)
                        return last
                    k.op("tensor", qk, reads=[r_KT[kt] for kt in kts] + [r_QT[t]], writes=[rsp])
                    if nbias:
                        k.op("vector", lambda e, sb_=sb_, sp=sp, bs=bs, cls=cls: e.tensor_tensor(out=sb_[:], in0=sp[:, 0:5, :], in1=bs[:, cls, :, :], op=ALU.add),
                             reads=[rsp, rbs], writes=[rsb])
                        k.op("scalar", lambda e, pp=pp, sb_=sb_: e.activation(out=pp[:, 0:5, :], in_=sb_[:], func=AF.Exp), reads=[rsb], writes=[rpp])
                        k.op("scalar", lambda e, pp=pp, sp=sp: e.activation(out=pp[:, 5:7, :], in_=sp[:, 5:7, :], func=AF.Exp), reads=[rsp, rpp], writes=[rpp])
                    else:
                        k.op("scalar", lambda e, pp=pp, sp=sp: e.activation(out=pp[:, 0:2, :], in_=sp[:, 0:2, :], func=AF.Exp), reads=[rsp], writes=[rpp])

                    def pv(e, o_=o_, pp=pp, kts=kts, h=h, nk=nk):
                        last = None
                        for i, kt in enumerate(kts):
                            last = e.matmul(o_[:, 0:65], lhsT=pp[:, i, :], rhs=Va[:, kt, h, :], start=(i == 0), stop=(i == nk - 1))
                        return last
                    k.op("tensor", pv, reads=[rpp] + [r_Va[kt] for kt in kts], writes=[ro])
                    k.op("vector", lambda e, rcc=rcc, o_=o_: e.reciprocal(out=rcc, in_=o_[:, 64:65]), reads=[ro], writes=[rrc])
                    k.op("vector", lambda e, rcc=rcc, o_=o_, t=t, h=h: e.tensor_scalar(out=yna[:, t, h * 64:(h + 1) * 64], in0=o_[:, 0:64],
                                                                                   scalar1=rcc, scalar2=None, op0=ALU.mult),
                         reads=[ro, rrc], writes=[r_yna[t]])
            for t in range(NT):
                k.dma(yd[t * 128:(t + 1) * 128, :], yna[:, t, :], reads=[r_yna[t]], writes=[r_yd], q="gpsimd")
            k.barrier()

    def stage_mla(self, l):
        k = self.k
        z, _ = self.dr["z%d" % l]
        r_zt = self.r_ztile
        yd, r_yd = self.dram("y_mla%d" % l, [T, 512], F32)
        MSC = 192.0 ** -0.5
        with ExitStack() as es:
            QTn = k.sb("ml_QTn", [128, 4, T], BF16, es)
            KTn = k.sb("ml_KTn", [128, 4, T], BF16, es)
            QTp = k.sb("ml_QTp", [128, 2, T], BF16, es)
            KTp = k.sb("ml_KTp", [128, 2, T], BF16, es)
            r_Q = [R() for _ in range(NT)]
            r_K = [R() for _ in range(NT)]
            Va = k.sb("ml_V", [128, NT, 4, 129], BF16, es)
            r_Va = [R() for _ in range(NT)]
            ym = k.sb("ml_y", [128, NT, 512], F32, es)
            r_ym = [R() for _ in range(NT)]
            k.op("gpsimd", lambda e: e.memset(Va[:, :, :, 128:129], 1.0), writes=r_Va)
            with ExitStack() as es2:
                self.r_gain = R()
                g_cq = k.sb("ml_gcq", [128, 512], F32, es2)
                g_ckv = k.sb("ml_gckv", [128, 256], F32, es2)
                g_q = k.sb("ml_gq", [128, 768], F32, es2)
                g_kn = k.sb("ml_gkn", [128, 512], F32, es2)
                g_kp = k.sb("ml_gkp", [128, 64], F32, es2)
                for tl, nm in ((g_cq, "mla_cqg"), (g_ckv, "mla_ckvg"), (g_q, "mla_qg"), (g_kn, "mla_kng"), (g_kp, "mla_kpg")):
                    ap, rr = self.dr[nm]
                    k.dma(tl[:], ap[l].partition_broadcast(128), reads=[rr], writes=[self.r_gain])
                k.op("vector", lambda e: e.tensor_scalar(out=g_q[:], in0=g_q[:], scalar1=MSC, scalar2=None, op0=ALU.mult),
                     reads=[self.r_gain], writes=[self.r_gain])
                wuq = k.sb("ml_wuq", [128, 4, 768], BF16, es2)
                wukv = k.sb("ml_wukv", [128, 2, 1024], BF16, es2)
                r_w = R()
                wst = k.sb("ml_wst", [128, 4, 768], F32, es2)
                r_wst = R()
                ap, rr = self.dr["mla_w_uq"]
                k.dma(wst[:], ap[l].rearrange("(kc p) n -> p kc n", p=128), reads=[rr], writes=[r_wst])
                k.op("vector", lambda e: e.tensor_copy(out=wuq[:], in_=wst[:]), reads=[r_wst], writes=[r_w])
                ap, rr = self.dr["mla_w_ukv"]
                wst2 = wst[:].rearrange("p a b -> p (a b)")[:, 0:2048].rearrange("p (a b) -> p a b", a=2)
                k.dma(wst2, ap[l].rearrange("(kc p) n -> p kc n", p=128), reads=[rr], writes=[r_wst])
                k.op("vector", lambda e: e.tensor_copy(out=wukv[:], in_=wst2), reads=[r_wst], writes=[r_w])
                cos_d, r_cos = self.dr["rope_cos"]
                sin_d, r_sin = self.dr["rope_sin"]
                xin = [k.sb("ml_x%d" % i, [128, 832], F32, es2) for i in range(2)]
                r_xin = [R() for _ in range(2)]
                cs = [k.sb("ml_cs%d" % i, [128, 2, 64], F32, es2) for i in range(2)]
                r_cs = [R() for _ in range(2)]
                junk = k.sb("ml_junk", [128, 1024], F32, es2)
                r_junk = R()
                tmp = k.sb("ml_tmp", [128, 1024], F32, es2)
                r_tmp = R()
                st = k.sb("ml_st", [128, NT, 16], F32, es2)
                cb = k.sb("ml_cb", [128, 768], BF16, es2)
                r_cb = [R(), R()]
                cT = k.sb("ml_cT", [128, 6, 128], BF16, es2)
                r_cT = [R(), R()]
                qf = k.sb("ml_qf", [128, 768], F32, es2)
                r_qf = R()
                qn = k.sb("ml_qn", [128, 768], F32, es2)
                r_qn = R()
                kvf = k.sb("ml_kvf", [128, 1024], F32, es2)
                r_kvf = R()
                qb = k.sb("ml_qb", [128, 4, 128], BF16, es2)
                qpb = k.sb("ml_qpb", [128, 4, 64], BF16, es2)
                kb = k.sb("ml_kb", [128, 4, 128], BF16, es2)
                kpb = k.sb("ml_kpb", [128, 4, 64], BF16, es2)
                r_qb, r_qpb, r_kb, r_kpb = R(), R(), R(), R()
                rp1 = k.sb("ml_rp1", [128, 4, 64], F32, es2)
                rp2 = k.sb("ml_rp2", [128, 4, 64], F32, es2)
                r_rp1, r_rp2 = R(), R()
                kpg = k.sb("ml_kpgt", [128, 64], F32, es2)
                kpr = k.sb("ml_kpr", [128, 64], F32, es2)
                r_kpg, r_kpr = R(), R()
                ptr = k.ps("ml_ptr", [128, 8, 128], BF16, es2)
                r_ptr = PR()
                pq = k.ps("ml_pq", [128, 2, 512], F32, es2)
                r_pq = PR()
                pkv = k.ps("ml_pkv", [128, 2, 512], F32, es2)
                r_pkv = PR()

                def rope(x3, nh, cs_, out3, reads, writes):
                    cosb = cs_[:, 0, :].unsqueeze(1).to_broadcast([128, nh, 64])
                    k.op("vector", lambda e: e.tensor_tensor(out=rp1[:, 0:nh, :], in0=x3, in1=cosb, op=ALU.mult), reads=reads, writes=[r_rp1])
                    x5 = x3.rearrange("p h (g a d) -> p h g a d", g=2, a=2)
                    o5 = rp2[:, 0:nh, :].rearrange("p h (g a d) -> p h g a d", g=2, a=2)
                    s4 = cs_[:, 1, :].rearrange("p (g a d) -> p g a d", g=2, a=2)
                    for a in range(2):
                        k.op("gpsimd", lambda e, a=a: e.tensor_tensor(out=o5[:, :, :, a, :], in0=x5[:, :, :, 1 - a, :],
                                                                      in1=s4[:, :, a, :].unsqueeze(1).to_broadcast([128, nh, 2, 16]), op=ALU.mult),
                             reads=reads, writes=[r_rp2])
                    k.op("vector", lambda e: e.tensor_tensor(out=out3, in0=rp1[:, 0:nh, :], in1=rp2[:, 0:nh, :], op=ALU.add),
                         reads=[r_rp1, r_rp2], writes=writes)

                for t in range(NT):
                    rows = slice(t * 128, (t + 1) * 128)
                    x_, rx = xin[t % 2], r_xin[t % 2]
                    cs_, rcs = cs[t % 2], r_cs[t % 2]
                    k.dma(x_[:, 0:512], z[rows, C_CQ:C_CQ + 512], reads=[r_zt[t]], writes=[rx])
                    k.dma(x_[:, 512:832], z[rows, C_CKV:C_CKV + 320], reads=[r_zt[t]], writes=[rx])
                    k.dma(cs_[:, 0, :], cos_d[rows, :], reads=[r_cos], writes=[rcs])
                    k.dma(cs_[:, 1, :], sin_d[rows, :], reads=[r_sin], writes=[rcs])
                    self.head_rms(None, x_[:, 0:512], 512, 1, 512, st[:, t, 0:1], junk[:, 0:512], cb[:, 0:512], g_cq[:],
                                  [rx], R(), r_junk, [r_cb[0]], tmp[:, 0:512], r_tmp)
                    self.head_rms(None, x_[:, 512:768], 256, 1, 256, st[:, t, 1:2], junk[:, 0:256], cb[:, 512:768], g_ckv[:],
                                  [rx], R(), r_junk, [r_cb[1]], tmp[:, 0:256], r_tmp)

                    def tr1(e):
                        last = None
                        for j in range(6):
                            last = e.transpose(out=ptr[:, j, :], in_=cb[:, j * 128:(j + 1) * 128], identity=self.ident[:])
                        return last
                    k.op("tensor", tr1, reads=r_cb + [self.r_ident], writes=[r_ptr])
                    k.op("scalar", lambda e: e.activation(out=cT[:], in_=ptr[:, 0:6, :], func=AF.Copy), reads=[r_ptr], writes=r_cT)

                    def mmq(e):
                        last = None
                        for hf in range(2):
                            for kc in range(4):
                                last = e.matmul(pq[:, hf, 0:384], lhsT=cT[:, kc, :], rhs=wuq[:, kc, hf * 384:(hf + 1) * 384],
                                                start=(kc == 0), stop=(kc == 3))
                        return last
                    k.op("tensor", mmq, reads=r_cT + [r_w], writes=[r_pq])

                    def mmkv(e):
                        last = None
                        for hf in range(2):
                            for kc in range(2):
                                last = e.matmul(pkv[:, hf, :], lhsT=cT[:, 4 + kc, :], rhs=wukv[:, kc, hf * 512:(hf + 1) * 512],
                                                start=(kc == 0), stop=(kc == 1))
                        return last
                    k.op("tensor", mmkv, reads=r_cT + [r_w], writes=[r_pkv])
                    k.op("scalar", lambda e: e.activation(out=qf[:].rearrange("p (a b) -> p a b", a=2), in_=pq[:, :, 0:384], func=AF.Copy),
                         reads=[r_pq], writes=[r_qf])
                    k.op("vector", lambda e: e.tensor_copy(out=kvf[:].rearrange("p (a b) -> p a b", a=2), in_=pkv[:]), reads=[r_pkv], writes=[r_kvf])
                    self.head_rms(None, qf[:], 768, 4, 192, st[:, t, 2:6], junk[:, 0:768], qn[:], g_q[:], [r_qf], R(), r_junk, [r_qn],
                                  tmp[:, 0:768], r_tmp)
                    qn3 = qn[:].rearrange("p (h d) -> p h d", h=4)
                    k.op("vector", lambda e: e.tensor_copy(out=qb[:], in_=qn3[:, :, 0:128]), reads=[r_qn], writes=[r_qb])
                    rope(qn3[:, :, 128:192], 4, cs_, qpb[:], [r_qn, rcs], [r_qpb])
                    kv4 = kvf[:].rearrange("p (h s d) -> p h s d", h=4, s=2)
                    k.op("vector", lambda e, t=t: e.tensor_copy(out=Va[:, t, :, 0:128], in_=kv4[:, :, 1, :]), reads=[r_kvf], writes=[r_Va[t]])
                    kpe = x_[:, 768:832]
                    r_sk = R()
                    k.op("vector", lambda e: e.tensor_tensor(out=junk[:, 0:512].rearrange("p (h d) -> p h d", h=4), in0=kv4[:, :, 0, :], in1=kv4[:, :, 0, :], op=ALU.mult),
                         reads=[r_kvf], writes=[r_junk])
                    k.op("vector", lambda e, t=t: e.tensor_reduce(out=st[:, t, 6:10], in_=junk[:, 0:512].rearrange("p (h d) -> p h d", h=4), axis=AX.X, op=ALU.add),
                         reads=[r_junk], writes=[r_sk])
                    k.op("vector", lambda e: e.tensor_tensor(out=junk[:, 512:576], in0=kpe, in1=kpe, op=ALU.mult), reads=[rx], writes=[r_junk])
                    k.op("vector", lambda e, t=t: e.tensor_reduce(out=st[:, t, 10:11], in_=junk[:, 512:576], axis=AX.X, op=ALU.add), reads=[r_junk], writes=[r_sk])
                    k.op("vector", lambda e, t=t: e.tensor_scalar(out=st[:, t, 6:10], in0=st[:, t, 6:10], scalar1=st[:, t, 10:11], scalar2=None, op0=ALU.add),
                         reads=[r_sk], writes=[r_sk])
                    k.op("scalar", lambda e, t=t: e.activation(out=st[:, t, 6:10], in_=st[:, t, 6:10], func=AF.Sqrt, bias=self.eps_t[:, 0:1], scale=1.0 / 192),
                         reads=[r_sk], writes=[r_sk])
                    k.op("vector", lambda e, t=t: e.reciprocal(out=st[:, t, 6:10], in_=st[:, t, 6:10]), reads=[r_sk], writes=[r_sk])
                    k.op("vector", lambda e, t=t: e.tensor_tensor(out=tmp[:, 0:512].rearrange("p (h d) -> p h d", h=4), in0=kv4[:, :, 0, :],
                                                                 in1=st[:, t, 6:10].unsqueeze(2).to_broadcast([128, 4, 128]), op=ALU.mult),
                         reads=[r_kvf, r_sk], writes=[r_tmp])
                    k.op("gpsimd", lambda e: e.tensor_tensor(out=kb[:].rearrange("p h d -> p (h d)"), in0=tmp[:, 0:512], in1=g_kn[:], op=ALU.mult),
                         reads=[r_tmp, self.r_gain], writes=[r_kb])
                    k.op("vector", lambda e: e.tensor_tensor(out=kpg[:], in0=kpe, in1=g_kp[:], op=ALU.mult), reads=[rx, self.r_gain], writes=[r_kpg])
                    rope(kpg[:].unsqueeze(1), 1, cs_, kpr[:].unsqueeze(1), [r_kpg, rcs], [r_kpr])
                    k.op("vector", lambda e, t=t: e.tensor_tensor(out=kpb[:], in0=kpr[:].unsqueeze(1).to_broadcast([128, 4, 64]),
                                                                 in1=st[:, t, 6:10].unsqueeze(2).to_broadcast([128, 4, 64]), op=ALU.mult),
                         reads=[r_kpr, r_sk], writes=[r_kpb])
                    for (nb_, pb_, rnb, rpb_, dn, dp, rd) in ((qb, qpb, r_qb, r_qpb, QTn, QTp, r_Q[t]), (kb, kpb, r_kb, r_kpb, KTn, KTp, r_K[t])):
                        def tr2(e, nb_=nb_, pb_=pb_):
                            last = None
                            for h in range(4):
                                last = e.transpose(out=ptr[:, h, :], in_=nb_[:, h, :], identity=self.ident[:])
                            for i in range(2):
                                last = e.transpose(out=ptr[:, 4 + i, :], in_=pb_[:, 2 * i:2 * i + 2, :].rearrange("p a b -> p (a b)"), identity=self.ident[:])
                            return last
                        k.op("tensor", tr2, reads=[rnb, rpb_, self.r_ident], writes=[r_ptr])
                        k.op("scalar", lambda e, dn=dn, rows=rows: e.activation(out=dn[:, :, rows], in_=ptr[:, 0:4, :], func=AF.Copy), reads=[r_ptr], writes=[rd])
                        k.op("vector", lambda e, dp=dp, rows=rows: e.tensor_copy(out=dp[:, :, rows], in_=ptr[:, 4:6, :]), reads=[r_ptr], writes=[rd])
                k.barrier()
            with ExitStack() as es3:
                stp = [k.ps("ml_st%d" % i, [128, 512], F32, es3) for i in range(2)]
                r_stp = [PR() for _ in range(2)]
                acc = [k.ps("ml_acc%d" % i, [128, 512], F32, es3) for i in range(4)]
                r_acc = [PR() for _ in range(4)]
                PT = [k.sb("ml_PT%d" % i, [128, 512], BF16, es3) for i in range(3)]
                r_PT = [R() for _ in range(3)]
                rc = k.sb("ml_rc", [128, 4], F32, es3)
                r_rc = [R() for _ in range(4)]
                groups = [[0, 1]] + [[2 + 4 * g + i for i in range(4)] for g in range(4)]
                it = 0
                for grp in groups:
                    kts = [0, 1] if grp[0] == 0 else list(range(NT))
                    q0 = grp[0] * 128
                    nq = len(grp) * 128
                    for h in range(4):
                        hp, hs = h // 2, (h % 2) * 64
                        for ki, kt in enumerate(kts):
                            sp, rsp = stp[it % 2], r_stp[it % 2]
                            pp, rpp = PT[it % 3], r_PT[it % 3]
                            it += 1

                            def qk(e, sp=sp, kt=kt, h=h, hp=hp, hs=hs, q0=q0, nq=nq):
                                e.matmul(sp[:, 0:nq], lhsT=KTn[:, h, kt * 128:(kt + 1) * 128], rhs=QTn[:, h, q0:q0 + nq], start=True, stop=False)
                                return e.matmul(sp[:, 0:nq], lhsT=KTp[hs:hs + 64, hp, kt * 128:(kt + 1) * 128], rhs=QTp[hs:hs + 64, hp, q0:q0 + nq],
                                                start=False, stop=True)
                            k.op("tensor", qk, reads=[r_K[kt]] + [r_Q[t] for t in grp], writes=[rsp])
                            k.op("scalar", lambda e, pp=pp, sp=sp, nq=nq: e.activation(out=pp[:, 0:nq], in_=sp[:, 0:nq], func=AF.Exp), reads=[rsp], writes=[rpp])
                            for i, t in enumerate(grp):
                                k.op("tensor", lambda e, i=i, pp=pp, kt=kt, h=h, ki=ki, kts=kts: e.matmul(
                                    acc[i][:, 0:129], lhsT=pp[:, i * 128:(i + 1) * 128], rhs=Va[:, kt, h, :], start=(ki == 0), stop=(ki == len(kts) - 1)),
                                    reads=[rpp, r_Va[kt]], writes=[r_acc[i]])
                        for i, t in enumerate(grp):
                            k.op("vector", lambda e, i=i: e.reciprocal(out=rc[:, i:i + 1], in_=acc[i][:, 128:129]), reads=[r_acc[i]], writes=[r_rc[i]])
                            k.op("vector", lambda e, i=i, t=t, h=h: e.tensor_scalar(out=ym[:, t, h * 128:(h + 1) * 128], in0=acc[i][:, 0:128],
                                                                                  scalar1=rc[:, i:i + 1], scalar2=None, op0=ALU.mult),
                                 reads=[r_acc[i], r_rc[i]], writes=[r_ym[t]])
                for t in range(NT):
                    k.dma(yd[t * 128:(t + 1) * 128, :], ym[:, t, :], reads=[r_ym[t]], writes=[r_yd])
                k.barrier()

    def stage_fourier(self, l):
        k = self.k
        z, _ = self.dr["z%d" % l]
        r_zt = self.r_ztile
        yd, r_yd = self.dram("y_fn%d" % l, [T, 512], F32)
        csd, r_csd = self.dr["dft_c"]
        dL, r_dL = self.dr["dft_L"]
        dC, r_dC = self.dr["dft_C"]
        with ExitStack() as es:
            G = k.sb("fn_G", [128, NT, 2, 512], BF16, es)
            r_G = [R() for _ in range(NT)]
            CS = k.sb("fn_CS", [128, 256], BF16, es)
            r_CS = R()
            k.dma(CS[:], csd, reads=[r_csd], writes=[r_CS])
            xin = [k.sb("fn_x%d" % i, [128, 512], F32, es) for i in range(2)]
            r_xin = [R() for _ in range(2)]
            xb = [k.sb("fn_xb%d" % i, [128, 512], BF16, es) for i in range(2)]
            r_xb = [R() for _ in range(2)]
            xT = [k.sb("fn_xT%d" % i, [128, 4, 128], BF16, es) for i in range(2)]
            r_xT = [R() for _ in range(2)]
            ptr = k.ps("fn_ptr", [128, 8, 128], BF16, es)
            r_ptr = PR()
            pg = [k.ps("fn_pg%d" % i, [128, 2, 256], F32, es) for i in range(2)]
            r_pg = [PR() for _ in range(2)]
            py = [k.ps("fn_py%d" % i, [128, 512], F32, es) for i in range(2)]
            r_py = [PR() for _ in range(2)]
            for t in range(NT):
                x_, rx = xin[t % 2], r_xin[t % 2]
                b_, rb = xb[t % 2], r_xb[t % 2]
                T_, rT = xT[t % 2], r_xT[t % 2]
                k.dma(x_[:], z[t * 128:(t + 1) * 128, C_FN:C_FN + 512], reads=[r_zt[t]], writes=[rx])
                k.op("gpsimd", lambda e, x_=x_, b_=b_: e.tensor_copy(out=b_[:], in_=x_[:]), reads=[rx], writes=[rb])

                def tr(e, b_=b_):
                    last = None
                    for g in range(4):
                        last = e.transpose(out=ptr[:, g, :], in_=b_[:, g * 128:(g + 1) * 128], identity=self.ident[:])
                    return last
                k.op("tensor", tr, reads=[rb, self.r_ident], writes=[r_ptr])
                k.op("scalar", lambda e, T_=T_: e.activation(out=T_[:], in_=ptr[:, 0:4, :], func=AF.Copy), reads=[r_ptr], writes=[rT])
                for hf in range(2):
                    p, rp = pg[hf], r_pg[hf]

                    def mm(e, p=p, T_=T_, hf=hf):
                        last = None
                        for gg in range(2):
                            last = e.matmul(p[:, gg, :], lhsT=T_[:, hf * 2 + gg, :], rhs=CS[:], start=True, stop=True)
                        return last
                    k.op("tensor", mm, reads=[rT, r_CS], writes=[rp])
                    G4 = G[:, t, :, :].rearrange("p c (g d) -> p c g d", g=4)
                    if hf == 0:
                        k.op("vector", lambda e, p=p, G4=G4, hf=hf: e.tensor_copy(out=G4[:, :, 0:2, :].rearrange("p c g d -> p g c d"),
                                                                                in_=p[:].rearrange("p g (c d) -> p g c d", c=2)),
                             reads=[rp], writes=[r_G[t]])
                    else:
                        k.op("scalar", lambda e, p=p, G4=G4, hf=hf: e.activation(out=G4[:, :, 2:4, :].rearrange("p c g d -> p g c d"),
                                                                                in_=p[:].rearrange("p g (c d) -> p g c d", c=2), func=AF.Copy),
                             reads=[rp, r_G[t]], writes=[r_G[t]])
            Dm = [k.sb("fn_D%d" % i, [128, 2, 16, 128], BF16, es) for i in range(2)]
            r_Dm = [R() for _ in range(2)]
            yo = [k.sb("fn_yo%d" % i, [128, 512], F32, es) for i in range(2)]
            r_yo = [R() for _ in range(2)]
            for to in range(NT):
                ctx = to < 2
                nin = 2 if ctx else 16
                t0 = 0 if ctx else 2
                src = dC if ctx else dL
                rsrc = r_dC if ctx else r_dL
                col0 = (to - t0) * 128
                D_, rD = Dm[to % 2], r_Dm[to % 2]
                for c in range(2):
                    k.dma(D_[:, c, 0:nin, :], src[c, :, col0:col0 + 128].rearrange("(nt p) q -> p nt q", p=128), reads=[rsrc], writes=[rD])
                p, rp = py[to % 2], r_py[to % 2]

                def mm2(e, p=p, D_=D_, nin=nin, t0=t0):
                    last = None
                    n = 0
                    for nt in range(nin):
                        for c in range(2):
                            last = e.matmul(p[:], lhsT=D_[:, c, nt, :], rhs=G[:, t0 + nt, c, :], start=(n == 0), stop=(n == 2 * nin - 1))
                            n += 1
                    return last
                k.op("tensor", mm2, reads=[rD] + [r_G[t0 + nt] for nt in range(nin)], writes=[rp])
                o_, ro = yo[to % 2], r_yo[to % 2]
                k.op("vector", lambda e, o_=o_, p=p: e.tensor_copy(out=o_[:], in_=p[:]), reads=[rp], writes=[ro])
                k.dma(yd[to * 128:(to + 1) * 128, :], o_[:], reads=[ro], writes=[r_yd])
            k.barrier()

    RWC = 3072

    def stage_rwprep(self, l):
        k = self.k
        z, _ = self.dr["z%d" % l]
        r_zt = self.r_ztile
        rwd, r_rwd = self.dram("rwd%d" % l, [2, T, self.RWC], F32)
        rwo, r_rwo = self.dram("rwo%d" % l, [T, 608], F32)
        self.r_rwd_t = [R() for _ in range(NT)]
        with ExitStack() as es:
            def bc(nm, n):
                tl = k.sb("rp_" + nm, [128, n], F32, es)
                ap, rr = self.dr[nm]
                k.dma(tl[:], ap[l].partition_broadcast(128), reads=[rr], writes=[r_c])
                return tl
            r_c = R()
            mu = bc("rw_mu", 1760)
            kkg = bc("rw_kk", 512)
            kag = bc("rw_ka", 512)
            w0 = bc("rw_w0f", 1024)
            a0 = bc("rw_a0f", 1024)
            w2 = k.sb("rp_w2", [32, 2, 2, 512], F32, es)
            ap, rr = self.dr["rw_wa2"]
            k.dma(w2[:], ap[l], reads=[rr], writes=[r_c])
            cur = [k.sb("rp_cur%d" % i, [128, 1760], F32, es) for i in range(2)]
            prv = [k.sb("rp_prv%d" % i, [128, 1760], F32, es) for i in range(2)]
            nxt = [k.sb("rp_nxt%d" % i, [128, 1760], F32, es) for i in range(2)]
            r_cur = [R() for _ in range(2)]
            r_prv = [R() for _ in range(2)]
            r_nxt = [R() for _ in range(2)]
            mx = k.sb("rp_mx", [128, 1760], F32, es)
            r_mx = R()
            d_ = k.sb("rp_d", [128, 1760], F32, es)
            r_d = R()
            st = k.sb("rp_st", [128, NT, 8], F32, es)
            lo_in = k.sb("rp_loin", [128, 128], F32, es)
            r_loin = R()
            lT = k.sb("rp_lT", [32, 4, 128], F32, es)
            r_lT = R()
            ob = [k.sb("rp_ob%d" % i, [128, 2, self.RWC], F32, es) for i in range(1)]
            r_ob = [[R() for _ in range(8)] for _ in range(1)]
            oo = k.sb("rp_oo", [128, 608], F32, es)
            r_oo = R()
            tw = k.sb("rp_tw", [128, 512], F32, es)
            r_tw = R()
            ta = k.sb("rp_ta", [128, 2, 512], F32, es)
            r_ta = [R(), R()]
            ptr = k.ps("rp_ptr", [128, 512], F32, es)
            r_ptr = PR()
            pw = [k.ps("rp_pw%d" % i, [128, 512], F32, es) for i in range(4)]
            r_pw = [PR() for _ in range(4)]
            for t in range(NT):
                rows = slice(t * 128, (t + 1) * 128)
                c_, rc_ = cur[t % 2], r_cur[t % 2]
                p_, rp_ = prv[t % 2], r_prv[t % 2]
                n_, rn_ = nxt[t % 2], r_nxt[t % 2]
                first = t in (0, 2)
                last_ = t in (1, NT - 1)
                for dst, rdst, off, lo_, hi_ in ((c_, rc_, 0, 0, 128), (p_, rp_, -1, 1 if first else 0, 128), (n_, rn_, 1, 0, 127 if last_ else 128)):
                    if hi_ - lo_ < 128:
                        k.op("gpsimd", lambda e, dst=dst: e.memset(dst[:], 0.0), writes=[rdst])
                    r0 = t * 128 + off
                    zt = [r_zt[t]] + ([r_zt[t - 1]] if off < 0 and t > 0 else []) + ([r_zt[t + 1]] if off > 0 and t < NT - 1 else [])
                    k.dma(dst[lo_:hi_, 0:1152], z[r0 + lo_:r0 + hi_, C_RWKS:C_RWKS + 1152], reads=zt, writes=[rdst])
                    k.dma(dst[lo_:hi_, 1152:1760], z[r0 + lo_:r0 + hi_, C_RWQS:C_RWQS + 608], reads=zt, writes=[rdst])
                k.op("vector", lambda e, p_=p_, n_=n_: e.tensor_tensor(out=d_[:], in0=p_[:], in1=n_[:], op=ALU.add), reads=[rp_, rn_], writes=[r_d])
                k.op("vector", lambda e, c_=c_: e.scalar_tensor_tensor(out=d_[:], in0=d_[:], scalar=0.5, in1=c_[:], op0=ALU.mult, op1=ALU.subtract),
                     reads=[r_d, rc_], writes=[r_d])
                k.op("vector", lambda e: e.tensor_tensor(out=d_[:], in0=d_[:], in1=mu[:], op=ALU.mult), reads=[r_d, r_c], writes=[r_d])
                k.op("vector", lambda e, c_=c_: e.tensor_tensor(out=mx[:], in0=d_[:], in1=c_[:], op=ALU.add), reads=[r_d, rc_], writes=[r_mx])
                o_, ro = ob[0], r_ob[0]
                kx = mx[:, 0:512]
                vx = mx[:, 512:1024]
                rx_ = mx[:, 1152:1664]
                r_s = R()
                kk_o = o_[:, 0, 0:512]
                k.op("vector", lambda e: e.tensor_tensor(out=kk_o, in0=kx, in1=kkg[:], op=ALU.mult), reads=[r_mx, r_c], writes=[ro[0]])
                k.op("vector", lambda e: e.tensor_tensor(out=d_[:, 0:512], in0=kk_o, in1=kk_o, op=ALU.mult), reads=[ro[0]], writes=[r_d])
                k.op("vector", lambda e, t=t: e.tensor_reduce(out=st[:, t, :], in_=d_[:, 0:512].rearrange("p (h d) -> p h d", h=8), axis=AX.X, op=ALU.add),
                     reads=[r_d], writes=[r_s])
                k.op("scalar", lambda e, t=t: e.activation(out=st[:, t, :], in_=st[:, t, :], func=AF.Sqrt), reads=[r_s], writes=[r_s])
                k.op("vector", lambda e, t=t: e.tensor_scalar(out=st[:, t, :], in0=st[:, t, :], scalar1=1e-12, scalar2=None, op0=ALU.max), reads=[r_s], writes=[r_s])
                k.op("vector", lambda e, t=t: e.reciprocal(out=st[:, t, :], in_=st[:, t, :]), reads=[r_s], writes=[r_s])
                k.op("vector", lambda e, t=t: e.tensor_tensor(out=kk_o.rearrange("p (h d) -> p h d", h=8), in0=kk_o.rearrange("p (h d) -> p h d", h=8),
                                                             in1=st[:, t, :].unsqueeze(2).to_broadcast([128, 8, 64]), op=ALU.mult),
                     reads=[ro[0], r_s], writes=[ro[0]])
                k.op("gpsimd", lambda e: e.tensor_copy(out=o_[:, 1, 0:512], in_=kk_o), reads=[ro[0]], writes=[ro[1]])
                k.op("gpsimd", lambda e: e.tensor_copy(out=o_[:, :, 2048:2560], in_=rx_.unsqueeze(1).to_broadcast([128, 2, 512])), reads=[r_mx], writes=[ro[2]])
                k.op("gpsimd", lambda e: e.tensor_copy(out=o_[:, :, 2560:3072], in_=vx.unsqueeze(1).to_broadcast([128, 2, 512])), reads=[r_mx], writes=[ro[3]])
                k.op("scalar", lambda e: e.activation(out=lo_in[:, 0:64], in_=mx[:, 1024:1088], func=AF.Tanh), reads=[r_mx], writes=[r_loin])
                k.op("vector", lambda e: e.tensor_copy(out=lo_in[:, 64:128], in_=mx[:, 1088:1152]), reads=[r_mx, r_loin], writes=[r_loin])

                def trl(e):
                    last = None
                    for j in range(4):
                        last = e.transpose(out=ptr[0:32, j * 128:(j + 1) * 128], in_=lo_in[:, j * 32:(j + 1) * 32], identity=self.identf[:])
                    return last
                k.op("tensor", trl, reads=[r_loin, self.r_ident], writes=[r_ptr])
                k.op("vector", lambda e: e.tensor_copy(out=lT[:], in_=ptr[0:32, :].rearrange("p (a b) -> p a b", a=4)), reads=[r_ptr], writes=[r_lT])
                for d in range(2):
                    p, rp = pw[d], r_pw[d]
                    k.op("tensor", lambda e, p=p, d=d: e.matmul(p[:], lhsT=lT[:, d, :], rhs=w2[:, 0, d, :], start=True, stop=True), reads=[r_lT, r_c], writes=[rp])
                    k.op("vector", lambda e, p=p, d=d: e.tensor_tensor(out=tw[:], in0=p[:], in1=w0[:, d * 512:(d + 1) * 512], op=ALU.add), reads=[rp, r_c], writes=[r_tw])
                    k.op("scalar", lambda e: e.activation(out=tw[:], in_=tw[:], func=AF.Sigmoid), reads=[r_tw], writes=[r_tw])
                    k.op("vector", lambda e, d=d: e.tensor_scalar(out=o_[:, d, 512:1024], in0=tw[:], scalar1=-float(np.exp(-0.5)), scalar2=None, op0=ALU.mult), reads=[r_tw], writes=[ro[4 + d]])
                    p, rp = pw[2 + d], r_pw[2 + d]
                    k.op("tensor", lambda e, p=p, d=d: e.matmul(p[:], lhsT=lT[:, 2 + d, :], rhs=w2[:, 1, d, :], start=True, stop=True), reads=[r_lT, r_c], writes=[rp])
                    k.op("vector", lambda e, p=p, d=d: e.tensor_tensor(out=ta[:, d, :], in0=p[:], in1=a0[:, d * 512:(d + 1) * 512], op=ALU.add), reads=[rp, r_c], writes=[r_ta[d]])
                    k.op("scalar", lambda e, d=d: e.activation(out=ta[:, d, :], in_=ta[:, d, :], func=AF.Sigmoid), reads=[r_ta[d]], writes=[r_ta[d]])
                    k.op("gpsimd", lambda e, d=d: e.tensor_tensor(out=o_[:, d, 1024:1536], in0=kk_o, in1=ta[:, d, :], op=ALU.mult), reads=[ro[0], r_ta[d]], writes=[ro[6 + d]])
                    k.op("vector", lambda e, d=d: e.scalar_tensor_tensor(out=ta[:, d, :], in0=ta[:, d, :], scalar=-1.0, in1=kag[:], op0=ALU.add, op1=ALU.mult),
                         reads=[r_ta[d], ro[6 + d], r_c], writes=[r_ta[d]])
                    k.op("vector", lambda e, d=d: e.scalar_tensor_tensor(out=o_[:, d, 1536:2048], in0=ta[:, d, :], scalar=1.0, in1=kx, op0=ALU.add, op1=ALU.mult),
                         reads=[r_ta[d], r_mx], writes=[ro[6 + d]])
                k.op("vector", lambda e: e.tensor_tensor(out=oo[:, 0:512], in0=o_[:, 0, 1536:2048], in1=o_[:, 1, 1536:2048], op=ALU.add), reads=[ro[6], ro[7]], writes=[r_oo])
                k.op("gpsimd", lambda e: e.tensor_copy(out=oo[:, 512:608], in_=mx[:, 1664:1760]), reads=[r_mx, r_oo], writes=[r_oo])
                for d in range(2):
                    k.dma(rwd[d, rows, :], o_[:, d, :], reads=ro, writes=[self.r_rwd_t[t]])
                k.dma(rwo[rows, :], oo[:], reads=[r_oo], writes=[r_rwo])
            k.barrier()

    def stage_rwscan_seq(self, l):
        k = self.k
        rwd, _ = self.dr["rwd%d" % l]
        r_rwd_t = self.r_rwd_t
        yfb, r_yfb = self.dram("rwy%d" % l, [2, T, 512], F32)
        self.r_rwy_t = [R() for _ in range(NT)]
        e2d, r_e2d = self.dr["rw_e2"]
        pmd, r_pmd = self.dr["rw_pm"]
        j64d, r_j64d = self.dr["rw_j64"]
        with ExitStack() as es:
            E2 = k.sb("rs_E2", [128, 64, 128], F32, es)
            Pm = k.sb("rs_Pm", [128, 128], F32, es)
            J64 = k.sb("rs_J64", [64, 64], F32, es)
            r_cst = R()
            for c in range(4):
                k.dma(E2[:, c * 16:(c + 1) * 16, :], e2d[:, c * 16:(c + 1) * 16, :], reads=[r_e2d], writes=[r_cst])
            k.dma(Pm[:], pmd, reads=[r_pmd], writes=[r_cst])
            k.dma(J64[:], j64d, reads=[r_j64d], writes=[r_cst])
            S = k.sb("rs_S", [128, 512], F32, es)
            r_S = R()
            k.op("vector", lambda e: e.memset(S[:], 0.0), writes=[r_S])
            X = [k.sb("rs_X%d" % i, [128, self.RWC], F32, es) for i in range(2)]
            r_X = [R() for _ in range(2)]
            Vr = k.sb("rs_Vr", [128, 8, 2, 64], F32, es)
            r_Vr = R()
            vT = [k.sb("rs_vT%d" % i, [128, 8, 64], F32, es) for i in range(2)]
            r_vT = [R() for _ in range(2)]
            ys = [k.sb("rs_ys%d" % i, [128, 8, 64], F32, es) for i in range(2)]
            r_ys = [R() for _ in range(2)]
            yo = k.sb("rs_yo", [64, 8, 2, 64], F32, es)
            r_yo = R()
            yb2 = k.sb("rs_yb2", [64, 512], F32, es)
            r_yb2 = R()
            yb3 = k.sb("rs_yb3", [64, 512], F32, es)
            r_yb3 = R()
            t1 = k.sb("rs_t1", [128, 512], F32, es)
            t2 = k.sb("rs_t2", [128, 512], F32, es)
            t3 = k.sb("rs_t3", [128, 512], F32, es)
            r_t1, r_t2, r_t3 = R(), R(), R()
            sa = k.sb("rs_sa", [128, 8], F32, es)
            r_sa = R()
            bcp = [k.ps("rs_bc%d" % i, [128, 512], F32, es) for i in range(5)]
            r_bcp = [PR() for _ in range(5)]
            pmisc = k.ps("rs_pm", [128, 8, 128], F32, es)
            r_pmisc = PR()
            pj = k.ps("rs_pj", [128, 512], F32, es)
            r_pj = PR()
            S3 = S[:].rearrange("p (h d) -> p h d", h=8)
            blk = 0
            nblk_dbg = self.dbg.get("RW_NBLK", 10 ** 9)
            for seg0, seglen in ((0, NCTX), (NCTX, L)):
                nb = seglen // 64
                for j in range(nb):
                    if blk >= nblk_dbg:
                        break
                    X_, rX = X[blk % 2], r_X[blk % 2]
                    vT_, rvT = vT[blk % 2], r_vT[blk % 2]
                    ys_, rys = ys[blk % 2], r_ys[blk % 2]
                    blk += 1
                    f0 = seg0 + 64 * j
                    b0 = seg0 + seglen - 64 * (j + 1)
                    k.dma(X_[0:64, :], rwd[0, f0:f0 + 64, :], reads=[r_rwd_t[f0 // 128]], writes=[rX])
                    k.dma(X_[64:128, :], rwd[1, b0:b0 + 64, :], reads=[r_rwd_t[b0 // 128]], writes=[rX])
                    k.op("gpsimd", lambda e, X_=X_: e.tensor_copy(out=Vr[:], in_=X_[:, 2560:3072].rearrange("p (h d) -> p h d", h=8).unsqueeze(2).to_broadcast([128, 8, 2, 64])),
                         reads=[rX], writes=[r_Vr])

                    def vtr(e):
                        last = None
                        for h in range(8):
                            last = e.matmul(pmisc[:, h, :], lhsT=Vr[:, h, :, :].rearrange("p a b -> p (a b)"), rhs=Pm[:], start=True, stop=True)
                        return last
                    k.op("tensor", vtr, reads=[r_Vr, r_cst], writes=[r_pmisc])
                    k.op("vector", lambda e, vT_=vT_: e.tensor_copy(out=vT_[0:64, :, :], in_=pmisc[0:64, :, 0:64]), reads=[r_pmisc], writes=[rvT])
                    k.op("vector", lambda e, vT_=vT_: e.tensor_copy(out=vT_[64:128, :, :], in_=pmisc[64:128, :, 64:128]), reads=[r_pmisc, rvT], writes=[rvT])
                    for i in range(64):
                        for q in range(5):
                            k.op("tensor", lambda e, q=q, i=i, X_=X_: e.matmul(bcp[q][:], lhsT=E2[:, i, :], rhs=X_[:, q * 512:(q + 1) * 512], start=True, stop=True),
                                 reads=[rX, r_cst], writes=[r_bcp[q]])
                        bkk, bdec, bb, bkd, br = [bcp[q][:].rearrange("p (h d) -> p h d", h=8) for q in range(5)]
                        k.op("vector", lambda e, bkk=bkk: e.tensor_tensor(out=t1[:].rearrange("p (h d) -> p h d", h=8), in0=S3, in1=bkk, op=ALU.mult),
                             reads=[r_S, r_bcp[0]], writes=[r_t1])
                        k.op("vector", lambda e: e.tensor_reduce(out=sa[:], in_=t1[:].rearrange("p (h d) -> p h d", h=8), axis=AX.X, op=ALU.add),
                             reads=[r_t1], writes=[r_sa])
                        k.op("vector", lambda e, bdec=bdec: e.tensor_tensor(out=S3, in0=S3, in1=bdec, op=ALU.mult), reads=[r_S, r_bcp[1]], writes=[r_S])
                        k.op("vector", lambda e, bb=bb: e.tensor_tensor(out=t2[:].rearrange("p (h d) -> p h d", h=8), in0=bb,
                                                                       in1=sa[:].unsqueeze(2).to_broadcast([128, 8, 64]), op=ALU.mult),
                             reads=[r_bcp[2], r_sa], writes=[r_t2])
                        k.op("vector", lambda e: e.tensor_tensor(out=S[:], in0=S[:], in1=t2[:], op=ALU.subtract), reads=[r_S, r_t2], writes=[r_S])
                        k.op("vector", lambda e, bkd=bkd, vT_=vT_, i=i: e.tensor_tensor(out=t3[:].rearrange("p (h d) -> p h d", h=8), in0=bkd,
                                                                                       in1=vT_[:, :, i:i + 1].to_broadcast([128, 8, 64]), op=ALU.mult),
                             reads=[r_bcp[3], rvT], writes=[r_t3])
                        k.op("vector", lambda e: e.tensor_tensor(out=S[:], in0=S[:], in1=t3[:], op=ALU.add), reads=[r_S, r_t3], writes=[r_S])
                        k.op("vector", lambda e, br=br: e.tensor_tensor(out=t1[:].rearrange("p (h d) -> p h d", h=8), in0=S3, in1=br, op=ALU.mult),
                             reads=[r_S, r_bcp[4]], writes=[r_t1])
                        k.op("vector", lambda e, ys_=ys_, i=i: e.tensor_reduce(out=ys_[:, :, i:i + 1].rearrange("p h o -> p (h o)"), in_=t1[:].rearrange("p (h d) -> p h d", h=8), axis=AX.X, op=ALU.add),
                             reads=[r_t1], writes=[rys])
                    def ytr(e, ys_=ys_):
                        last = None
                        for h in range(8):
                            last = e.matmul(pmisc[0:64, h, :], lhsT=ys_[:, h, :], rhs=self.identf[:], start=True, stop=True)
                        return last
                    k.op("tensor", ytr, reads=[rys, self.r_ident], writes=[r_pmisc])
                    k.op("vector", lambda e: e.tensor_copy(out=yo[:].rearrange("p h a d -> p h (a d)"), in_=pmisc[0:64, :, :]), reads=[r_pmisc], writes=[r_yo])
                    k.op("gpsimd", lambda e: e.tensor_copy(out=yb2[:].rearrange("p (h d) -> p h d", h=8), in_=yo[:, :, 1, :]), reads=[r_yo], writes=[r_yb2])
                    k.op("tensor", lambda e: e.matmul(pj[0:64, :], lhsT=J64[:], rhs=yb2[:], start=True, stop=True), reads=[r_yb2, r_cst], writes=[r_pj])
                    k.op("scalar", lambda e: e.activation(out=yb3[:], in_=pj[0:64, :], func=AF.Copy), reads=[r_pj], writes=[r_yb3])
                    k.dma(yfb[0, f0:f0 + 64, :].rearrange("p (h d) -> p h d", h=8), yo[:, :, 0, :], reads=[r_yo], writes=[self.r_rwy_t[f0 // 128]])
                    k.dma(yfb[1, b0:b0 + 64, :], yb3[:], reads=[r_yb3], writes=[self.r_rwy_t[b0 // 128]])
            k.barrier()

    def stage_rwscan(self, l):
        k = self.k
        rwd, _ = self.dr["rwd%d" % l]
        r_rwd_t = self.r_rwd_t
        yfb, r_yfb = self.dram("rwy%d" % l, [2, T, 512], F32)
        self.r_rwy_t = [R() for _ in range(NT)]
        mkd, r_mkd = self.dr["rw_masks"]
        dmd, r_dmd = self.dr["rw_dm"]
        with ExitStack() as es:
            MK = k.sb("rc_MK", [128, 6, 128], F32, es)
            DM = k.sb("rc_DM", [128, 2, 1024], F32, es)
            ones = k.sb("rc_ones", [128, 64], F32, es)
            r_cst = R()
            k.dma(MK[:], mkd, reads=[r_mkd], writes=[r_cst])
            k.dma(DM[:], dmd, reads=[r_dmd], writes=[r_cst])
            k.op("vector", lambda e: e.memset(ones[:], 1.0), writes=[r_cst])
            MI, BO, MS, nMS, nMST, nMI = [MK[:, i, :] for i in range(6)]
            S = k.sb("rc_S", [128, 512], F32R, es)
            r_S = R()
            X = [k.sb("rc_X%d" % i, [128, self.RWC], F32, es) for i in range(2)]
            r_X = [R() for _ in range(2)]
            fac = [k.sb("rc_fac%d" % i, [128, 512], F32, es) for i in range(4)]
            r_fac = [R() for _ in range(4)]
            tl = k.sb("rc_tl", [128, 2, 512], F32, es)
            r_tl = [R(), R()]
            k.op("vector", lambda e: e.memset(tl[:, 0, :], 0.0), writes=[r_tl[0]])
            k.op("vector", lambda e: e.tensor_copy(out=S[:], in_=tl[:, 0, :]), reads=[r_tl[0]], writes=[r_S])
            pl = [k.sb("rc_pl%d" % i, [128, 512], F32, es) for i in range(6)]
            r_pl = [R() for _ in range(6)]
            bd03 = [k.sb("rc_bd%d" % i, [128, 8, 128], F32R, es) for i in range(4)]
            r_bd03 = [R() for _ in range(4)]
            PB = []
            for par in range(2):
                d = {}
                d["bd4"] = k.sb("rc_bd4_%d" % par, [128, 8, 128], F32R, es)
                d["bd5"] = k.sb("rc_bd5_%d" % par, [128, 8, 128], F32R, es)
                d["bd6"] = k.sb("rc_bd6_%d" % par, [128, 8, 128], F32, es)
                d["r_bd"] = [R(), R(), R()]
                d["xT"] = [k.sb("rc_xT%d_%d" % (i, par), [128, 8, 128], F32R, es) for i in range(4)]
                d["r_xT"] = [[R(), R()] for _ in range(4)]
                for nm in ("AT", "ApT", "nBpT", "NT0", "M0"):
                    d[nm] = k.sb("rc_%s_%d" % (nm, par), [128, 8, 128], F32R, es)
                    d["r_" + nm] = [R(), R()]
                d["Vc"] = k.sb("rc_Vc%d" % par, [128, 512], F32R, es)
                d["r_Vc"] = R()
                PB.append(d)
            NTb = [k.sb("rc_NT%d" % i, [128, 8, 128], F32R, es) for i in range(2)]
            Mb = [k.sb("rc_M%d" % i, [128, 8, 128], F32R, es) for i in range(2)]
            r_NT = [[R(), R()] for _ in range(2)]
            r_M = [[R(), R()] for _ in range(2)]
            G = k.sb("rc_G", [128, 512], F32R, es)
            r_G = R()
            Pend = k.sb("rc_Pend", [128, 512], F32, es)
            r_Pend = R()
            Yo = [k.sb("rc_Yo%d" % i, [128, 512], F32, es) for i in range(2)]
            r_Yo = [R() for _ in range(2)]
            pb = [k.ps("rc_pb%d" % i, [128, 512], F32, es) for i in range(8)]
            r_pb = [PR() for _ in range(8)]
            ipb = [0]

            def bank():
                i = ipb[0] % 8
                ipb[0] += 1
                return pb[i], r_pb[i]

            chunks = []
            for seg0, seglen in ((0, NCTX), (NCTX, L)):
                for j in range(seglen // 64):
                    chunks.append((seg0 + 64 * j, seg0 + seglen - 64 * (j + 1)))
            chunks = chunks[:self.dbg.get("RW_NBLK", 10 ** 9)]

            def prologue_steps(ci):
                f0, b0 = chunks[ci]
                par = ci % 2
                d = PB[par]
                X_, rX = X[par], r_X[par]
                kk_, lw_, b_, kd_, r_ = [X_[:, q * 512:(q + 1) * 512] for q in range(5)]
                steps = []

                def s_load():
                    k.dma(X_[0:64, :], rwd[0, f0:f0 + 64, :], reads=[r_rwd_t[f0 // 128]], writes=[rX])
                    k.dma(X_[64:128, :], rwd[1, b0:b0 + 64, :], reads=[r_rwd_t[b0 // 128]], writes=[rX])
                    k.op("gpsimd", lambda e: e.tensor_copy(out=d["Vc"][:], in_=X_[:, 2560:3072]), reads=[rX], writes=[d["r_Vc"]])
                steps.append(s_load)

                def s_cum():
                    pLi, rLi = bank()
                    pLt, rLt = bank()
                    k.op("tensor", lambda e: e.matmul(pLi[:], lhsT=MI, rhs=lw_, start=True, stop=True), reads=[rX, r_cst], writes=[rLi])
                    k.op("tensor", lambda e: e.matmul(pLt[:], lhsT=BO, rhs=lw_, start=True, stop=True), reads=[rX, r_cst], writes=[rLt])
                    k.op("scalar", lambda e: e.activation(out=fac[0][:], in_=pLi[:], func=AF.Exp), reads=[rLi], writes=[r_fac[0]])
                    k.op("scalar", lambda e: e.activation(out=fac[1][:], in_=pLi[:], func=AF.Exp, scale=-1.0), reads=[rLi], writes=[r_fac[1]])
                    k.op("vector", lambda e: e.tensor_tensor(out=tl[:, 0, :], in0=pLi[:], in1=lw_, op=ALU.subtract), reads=[rLi, rX], writes=[r_tl[0]])
                    k.op("scalar", lambda e: e.activation(out=fac[2][:], in_=tl[:, 0, :], func=AF.Exp), reads=[r_tl[0]], writes=[r_fac[2]])
                    k.op("vector", lambda e: e.tensor_copy(out=tl[:, 1, :], in_=pLt[:]), reads=[rLt], writes=[r_tl[1]])
                    k.op("vector", lambda e: e.tensor_tensor(out=tl[:, 1, :], in0=tl[:, 1, :], in1=pLi[:], op=ALU.subtract), reads=[rLi, r_tl[1]], writes=[r_tl[1]])
                    k.op("scalar", lambda e: e.activation(out=fac[3][:], in_=tl[:, 1, :], func=AF.Exp), reads=[r_tl[1]], writes=[r_fac[3]])
                steps.append(s_cum)

                def s_prod():
                    for i, (src, fi) in enumerate(((kk_, 2), (r_, 0), (kd_, 1), (b_, 1), (kd_, 3), (b_, 3))):
                        k.op("vector" if i < 4 else "gpsimd", lambda e, i=i, src=src, fi=fi: e.tensor_tensor(out=pl[i][:], in0=src, in1=fac[fi][:], op=ALU.mult),
                             reads=[rX, r_fac[fi]], writes=[r_pl[i]])
                    for i in range(7):
                        src = pl[i][:] if i < 6 else lw_
                        dm = DM[:, 1 if i == 5 else 0, :]
                        rs = [r_pl[i]] if i < 6 else [rX]
                        if i < 4:
                            dst, rd = bd03[i], r_bd03[i]
                        else:
                            dst, rd = d["bd%d" % i], d["r_bd"][i - 4]
                        k.op("vector" if i < 4 else "gpsimd", lambda e, src=src, dm=dm, dst=dst: e.tensor_tensor(
                            out=dst[:].rearrange("p h (a d) -> p h a d", a=2), in0=src.rearrange("p (h d) -> p h d", h=8).unsqueeze(2).to_broadcast([128, 8, 2, 64]),
                            in1=dm.rearrange("p (h a d) -> p h a d", h=8, a=2), op=ALU.mult), reads=rs + [r_cst], writes=[rd])
                steps.append(s_prod)
                for qi in range(4):
                    def s_tr(qi=qi):
                        for hf in range(2):
                            p_, rp_ = bank()

                            def tr(e, p_=p_, hf=hf):
                                last = None
                                for hh in range(4):
                                    last = e.transpose(out=p_[:, hh * 128:(hh + 1) * 128], in_=bd03[qi][:, hf * 4 + hh, :].bitcast(F32), identity=self.identf[:])
                                return last
                            k.op("tensor", tr, reads=[r_bd03[qi], self.r_ident], writes=[rp_])
                            k.op("scalar", lambda e, p_=p_, hf=hf: e.activation(out=d["xT"][qi][:, hf * 4:(hf + 1) * 4, :].rearrange("p a b -> p (a b)"), in_=p_[:], func=AF.Copy),
                                 reads=[rp_], writes=[d["r_xT"][qi][hf]])
                    steps.append(s_tr)
                qT, rT, kT, bT = d["xT"]
                specs = ((bT, 3, qT, 0, nMS, "NT0"), (qT, 0, bT, 3, nMST, "M0"), (kT, 2, qT, 0, MS, "AT"), (kT, 2, rT, 1, MI, "ApT"), (bT, 3, rT, 1, nMI, "nBpT"))
                for (lt, li, rt, ri, mk, nm) in specs:
                    def s_in(lt=lt, li=li, rt=rt, ri=ri, mk=mk, nm=nm):
                        for hf in range(2):
                            p_, rp_ = bank()

                            def mm(e, p_=p_, hf=hf):
                                last = None
                                for hh in range(4):
                                    h = hf * 4 + hh
                                    last = e.matmul(p_[:, hh * 128:(hh + 1) * 128], lhsT=lt[:, h, :], rhs=rt[:, h, :], start=True, stop=True)
                                return last
                            k.op("tensor", mm, reads=[d["r_xT"][li][hf], d["r_xT"][ri][hf]], writes=[rp_])
                            k.op("gpsimd" if False else "vector", lambda e, p_=p_, hf=hf: e.tensor_tensor(
                                out=d[nm][:, hf * 4:(hf + 1) * 4, :], in0=p_[:].rearrange("p (a b) -> p a b", a=4), in1=mk.unsqueeze(1).to_broadcast([128, 4, 128]), op=ALU.mult),
                                reads=[rp_, r_cst], writes=[d["r_" + nm][hf]])
                    steps.append(s_in)
                return steps

            def solve(ci, fillers):
                f0, b0 = chunks[ci]
                par = ci % 2
                d = PB[par]
                Yo_, rYo = Yo[par], r_Yo[par]
                qT, rT, kT, bT = d["xT"]
                Vc_, rVc = d["Vc"], d["r_Vc"]

                def Vh(h):
                    return Vc_[:, h * 64:(h + 1) * 64]

                def fill(n):
                    for _ in range(n):
                        if fillers:
                            fillers.pop(0)()
                pG, rG = bank()

                def mmG(e):
                    last = None
                    for h in range(8):
                        e.matmul(pG[:, h * 64:(h + 1) * 64], lhsT=qT[:, h, :], rhs=S[:, h * 64:(h + 1) * 64], start=True, stop=False)
                        last = e.matmul(pG[:, h * 64:(h + 1) * 64], lhsT=d["AT"][:, h, :], rhs=Vh(h), start=False, stop=True)
                    return last
                k.op("tensor", mmG, reads=d["r_xT"][0] + d["r_AT"] + [r_S, rVc], writes=[rG])
                k.op("vector", lambda e: e.tensor_copy(out=G[:], in_=pG[:]), reads=[rG], writes=[r_G])
                fill(2)
                curT, curM, rcT, rcM = d["NT0"], d["M0"], d["r_NT0"], d["r_M0"]
                for lev in range(6):
                    pU, rU = bank()

                    def mmU(e, pU=pU, NTc=curT):
                        last = None
                        for h in range(8):
                            last = e.matmul(pU[:, h * 64:(h + 1) * 64], lhsT=NTc[:, h, :], rhs=G[:, h * 64:(h + 1) * 64], start=True, stop=True)
                        return last
                    k.op("tensor", mmU, reads=rcT + [r_G], writes=[rU])
                    if lev < 5:
                        nx = lev % 2
                        for (lt, rt, dst, rdst, rl, rr_) in ((curM, curT, NTb[nx], r_NT[nx], rcM, rcT), (curT, curM, Mb[nx], r_M[nx], rcT, rcM)):
                            for hf in range(2):
                                p_, rp_ = bank()

                                def mm2(e, p_=p_, lt=lt, rt=rt, hf=hf):
                                    last = None
                                    for hh in range(4):
                                        h = hf * 4 + hh
                                        last = e.matmul(p_[:, hh * 128:(hh + 1) * 128], lhsT=lt[:, h, :], rhs=rt[:, h, :], start=True, stop=True)
                                    return last
                                k.op("tensor", mm2, reads=[rl[hf], rr_[hf]], writes=[rp_])
                                k.op("scalar", lambda e, p_=p_, dst=dst, hf=hf: e.activation(out=dst[:, hf * 4:(hf + 1) * 4, :].rearrange("p a b -> p (a b)"), in_=p_[:], func=AF.Copy),
                                     reads=[rp_], writes=[rdst[hf]])
                        curT, curM, rcT, rcM = NTb[nx], Mb[nx], r_NT[nx], r_M[nx]
                    k.op("vector", lambda e, pU=pU: e.tensor_tensor(out=G[:], in0=G[:].bitcast(F32), in1=pU[:], op=ALU.add), reads=[rU, r_G], writes=[r_G])
                    fill(2)
                pY, rY = bank()

                def mmY(e):
                    last = None
                    for h in range(8):
                        hs = slice(h * 64, (h + 1) * 64)
                        e.matmul(pY[:, hs], lhsT=rT[:, h, :], rhs=S[:, hs], start=True, stop=False)
                        e.matmul(pY[:, hs], lhsT=d["ApT"][:, h, :], rhs=Vh(h), start=False, stop=False)
                        last = e.matmul(pY[:, hs], lhsT=d["nBpT"][:, h, :], rhs=G[:, hs], start=False, stop=True)
                    return last
                k.op("tensor", mmY, reads=d["r_xT"][1] + d["r_ApT"] + d["r_nBpT"] + [r_S, rVc, r_G], writes=[rY])
                k.op("scalar", lambda e: e.activation(out=Yo_[:], in_=pY[:], func=AF.Copy), reads=[rY], writes=[rYo])
                k.dma(yfb[0, f0:f0 + 64, :], Yo_[0:64, :], reads=[rYo], writes=[self.r_rwy_t[f0 // 128]])
                k.dma(yfb[1, b0:b0 + 64, :], Yo_[64:128, :], reads=[rYo], writes=[self.r_rwy_t[b0 // 128]])
                pL, rL = bank()

                def mmL(e):
                    last = None
                    for h in range(8):
                        last = e.matmul(pL[:, h * 64:(h + 1) * 64], lhsT=d["bd6"][:, h, :], rhs=ones[:], start=True, stop=True)
                    return last
                k.op("tensor", mmL, reads=[d["r_bd"][2], r_cst], writes=[rL])
                k.op("scalar", lambda e: e.activation(out=Pend[:], in_=pL[:], func=AF.Exp), reads=[rL], writes=[r_Pend])
                pS, rS = bank()

                def mmS(e):
                    last = None
                    for h in range(8):
                        hs = slice(h * 64, (h + 1) * 64)
                        e.matmul(pS[:, hs], lhsT=d["bd4"][:, h, :], rhs=Vh(h), start=True, stop=False)
                        last = e.matmul(pS[:, hs], lhsT=d["bd5"][:, h, :], rhs=G[:, hs], start=False, stop=True)
                    return last
                k.op("tensor", mmS, reads=[d["r_bd"][0], d["r_bd"][1], rVc, r_G], writes=[rS])
                k.op("vector", lambda e: e.tensor_tensor(out=S[:], in0=S[:].bitcast(F32), in1=Pend[:], op=ALU.mult), reads=[r_S, r_Pend], writes=[r_S])
                k.op("vector", lambda e: e.tensor_tensor(out=S[:], in0=S[:].bitcast(F32), in1=pS[:], op=ALU.add), reads=[r_S, rS], writes=[r_S])
                while fillers:
                    fillers.pop(0)()

            if chunks:
                for st_ in prologue_steps(0):
                    st_()
                for ci in range(len(chunks)):
                    fillers = prologue_steps(ci + 1) if ci + 1 < len(chunks) else []
                    solve(ci, fillers)
            k.barrier()

    def stage_rwout(self, l):
        k = self.k
        rwd, _ = self.dr["rwd%d" % l]
        rwo, r_rwo = self.dr["rwo%d" % l]
        yfb, _ = self.dr["rwy%d" % l]
        yd, r_yd = self.dram("y_rw%d" % l, [T, 512], F32)
        with ExitStack() as es:
            r_c = R()

            def bc(nm, n):
                tl = k.sb("ro_" + nm, [128, n], F32, es)
                ap, rr = self.dr[nm]
                k.dma(tl[:], ap[l].partition_broadcast(128), reads=[rr], writes=[r_c])
                return tl
            lnw = bc("rw_lnw", 512)
            lnb = bc("rw_lnb", 512)
            rkg = bc("rw_rk", 512)
            g2 = k.sb("ro_g2", [96, 512], F32, es)
            ap, rr = self.dr["rw_g2"]
            k.dma(g2[:], ap[l], reads=[rr], writes=[r_c])
            gne = k.sb("ro_gne", [128, 1], F32, es)
            k.op("vector", lambda e: e.memset(gne[:], 64e-5), writes=[r_c])
            yf = [k.sb("ro_yf%d" % i, [128, 512], F32, es) for i in range(2)]
            yb = [k.sb("ro_yb%d" % i, [128, 512], F32, es) for i in range(2)]
            rv = [k.sb("ro_rv%d" % i, [128, 1024], F32, es) for i in range(2)]
            kg = [k.sb("ro_kg%d" % i, [128, 608], F32, es) for i in range(2)]
            r_in = [[R() for _ in range(4)] for _ in range(2)]
            y = k.sb("ro_y", [128, 512], F32, es)
            r_y = R()
            sq = k.sb("ro_sq", [128, 512], F32, es)
            r_sq = R()
            st = k.sb("ro_st", [128, NT, 3, 8], F32, es)
            sg = k.sb("ro_sg", [128, 128], F32, es)
            r_sg = R()
            gT = k.sb("ro_gT", [96, 128], F32, es)
            r_gT = R()
            out = [k.sb("ro_out%d" % i, [128, 512], F32, es) for i in range(2)]
            r_out = [R() for _ in range(2)]
            ptr = k.ps("ro_ptr", [128, 512], F32, es)
            r_ptr = PR()
            pg = k.ps("ro_pg", [128, 512], F32, es)
            r_pg = PR()
            k.op("vector", lambda e: e.memset(sg[:], 0.0), writes=[r_sg])
            for t in range(NT):
                rows = slice(t * 128, (t + 1) * 128)
                i2 = t % 2
                ri = r_in[i2]
                k.dma(yf[i2][:], yfb[0, rows, :], reads=[self.r_rwy_t[t]], writes=[ri[0]])
                k.dma(yb[i2][:], yfb[1, rows, :], reads=[self.r_rwy_t[t]], writes=[ri[1]])
                k.dma(rv[i2][:], rwd[0, rows, 2048:3072], reads=[self.r_rwd_t[t]], writes=[ri[2]])
                k.dma(kg[i2][:], rwo[rows, :], reads=[r_rwo], writes=[ri[3]])
                r_s = R()
                y3 = y[:].rearrange("p (h d) -> p h d", h=8)
                k.op("vector", lambda e, i2=i2: e.tensor_tensor(out=y[:], in0=yf[i2][:], in1=yb[i2][:], op=ALU.add), reads=[ri[0], ri[1]], writes=[r_y])
                k.op("vector", lambda e, t=t: e.tensor_reduce(out=st[:, t, 0, :], in_=y3, axis=AX.X, op=ALU.add), reads=[r_y], writes=[r_s])
                k.op("vector", lambda e, t=t: e.tensor_scalar(out=st[:, t, 0, :], in0=st[:, t, 0, :], scalar1=1.0 / 64, scalar2=None, op0=ALU.mult), reads=[r_s], writes=[r_s])
                k.op("vector", lambda e, t=t: e.tensor_tensor(out=y3, in0=y3, in1=st[:, t, 0, :].unsqueeze(2).to_broadcast([128, 8, 64]), op=ALU.subtract),
                     reads=[r_y, r_s], writes=[r_y])
                k.op("gpsimd", lambda e: e.tensor_tensor(out=sq[:], in0=y[:], in1=y[:], op=ALU.mult), reads=[r_y], writes=[r_sq])
                k.op("vector", lambda e, t=t: e.tensor_reduce(out=st[:, t, 1, :], in_=sq[:].rearrange("p (h d) -> p h d", h=8), axis=AX.X, op=ALU.add), reads=[r_sq], writes=[r_s])
                k.op("scalar", lambda e, t=t: e.activation(out=st[:, t, 1, :], in_=st[:, t, 1, :], func=AF.Sqrt, bias=gne[:, 0:1], scale=1.0 / 64), reads=[r_s, r_c], writes=[r_s])
                k.op("vector", lambda e, t=t: e.reciprocal(out=st[:, t, 1, :], in_=st[:, t, 1, :]), reads=[r_s], writes=[r_s])
                k.op("vector", lambda e, t=t: e.tensor_tensor(out=y3, in0=y3, in1=st[:, t, 1, :].unsqueeze(2).to_broadcast([128, 8, 64]), op=ALU.mult),
                     reads=[r_y, r_s], writes=[r_y])
                k.op("gpsimd", lambda e: e.tensor_tensor(out=y[:], in0=y[:], in1=lnw[:], op=ALU.mult), reads=[r_y, r_c], writes=[r_y])
                k.op("gpsimd", lambda e: e.tensor_tensor(out=y[:], in0=y[:], in1=lnb[:], op=ALU.add), reads=[r_y, r_c], writes=[r_y])
                k.op("vector", lambda e, i2=i2: e.tensor_tensor(out=sq[:], in0=rv[i2][:, 0:512], in1=kg[i2][:, 0:512], op=ALU.mult), reads=[ri[2], ri[3], r_sq], writes=[r_sq])
                k.op("vector", lambda e: e.tensor_tensor(out=sq[:], in0=sq[:], in1=rkg[:], op=ALU.mult), reads=[r_sq, r_c], writes=[r_sq])
                k.op("vector", lambda e, t=t: e.tensor_reduce(out=st[:, t, 2, :], in_=sq[:].rearrange("p (h d) -> p h d", h=8), axis=AX.X, op=ALU.add), reads=[r_sq], writes=[r_s])
                k.op("vector", lambda e, t=t, i2=i2: e.tensor_tensor(out=sq[:].rearrange("p (h d) -> p h d", h=8), in0=rv[i2][:, 512:1024].rearrange("p (h d) -> p h d", h=8),
                                                                    in1=st[:, t, 2, :].unsqueeze(2).to_broadcast([128, 8, 64]), op=ALU.mult),
                     reads=[ri[2], r_s, r_sq], writes=[r_sq])
                k.op("vector", lambda e: e.tensor_tensor(out=y[:], in0=y[:], in1=sq[:], op=ALU.add), reads=[r_y, r_sq], writes=[r_y])
                k.op("scalar", lambda e, i2=i2: e.activation(out=sg[:, 0:96], in_=kg[i2][:, 512:608], func=AF.Sigmoid), reads=[ri[3]], writes=[r_sg])
                k.op("tensor", lambda e: e.transpose(out=ptr[:, 0:128], in_=sg[:], identity=self.identf[:]), reads=[r_sg, self.r_ident], writes=[r_ptr])
                k.op("vector", lambda e: e.tensor_copy(out=gT[:], in_=ptr[0:96, 0:128]), reads=[r_ptr], writes=[r_gT])
                k.op("tensor", lambda e: e.matmul(pg[:], lhsT=gT[:], rhs=g2[:], start=True, stop=True), reads=[r_gT, r_c], writes=[r_pg])
                o_, ro = out[i2], r_out[i2]
                k.op("vector", lambda e, o_=o_: e.tensor_tensor(out=o_[:], in0=y[:], in1=pg[:], op=ALU.mult), reads=[r_y, r_pg], writes=[ro])
                k.dma(yd[rows, :], o_[:], reads=[ro], writes=[r_yd])
            k.barrier()

    def stage_merge(self, l, xname):
        k = self.k
        z, _ = self.dr["z%d" % l]
        r_zt = self.r_ztile
        xin, r_xin = self.dr[xname]
        modD, r_modD = self.dr["modD%d" % l]
        wbr, r_wbr = self.dr["w_br"]
        wout, r_wout = self.dr["w_out"]
        accD, r_accD = self.dram("accD%d" % l, [T, D], BF16)
        r_acc_t = [R() for _ in range(NT)]
        x1, r_x1 = self.dram("x1_%d" % l, [T, D], F32)
        self.r_x1_t = [R() for _ in range(NT)]
        ybr = [self.dr[n % l] for n in ("y_fn%d", "y_na%d", "y_mla%d", "y_rw%d")]
        with ExitStack() as es:
            W = k.sb("mg_W", [128, 16, 2048], BF16, es)
            r_W = [R() for _ in range(16)]
            wst = [k.sb("mg_wst%d" % i, [128, 2048], F32, es) for i in range(2)]
            r_wst = [R() for _ in range(2)]
            for i in range(16):
                b, kc = i // 4, i % 4
                k.dma(wst[i % 2][:], wbr[l, b, kc * 128:(kc + 1) * 128, :], reads=[r_wbr], writes=[r_wst[i % 2]])
                if i % 2 == 0:
                    k.op("vector", lambda e, i=i: e.tensor_copy(out=W[:, i, :], in_=wst[i % 2][:]), reads=[r_wst[i % 2]], writes=[r_W[i]])
                else:
                    k.op("scalar", lambda e, i=i: e.activation(out=W[:, i, :], in_=wst[i % 2][:], func=AF.Copy), reads=[r_wst[i % 2]], writes=[r_W[i]])
            with ExitStack() as es2:
                yt = k.sb("mg_yt", [128, 4, 512], F32, es2)
                r_yt = R()
                ytb = k.sb("mg_ytb", [128, 2048], BF16, es2)
                r_ytb = R()
                yT = k.sb("mg_yT", [128, 16, 128], BF16, es2)
                r_yT = R()
                gt = [k.sb("mg_gt%d" % i, [128, 2048], F32, es2) for i in range(2)]
                r_gt = [R() for _ in range(2)]
                acc = k.sb("mg_acc", [128, 2048], F32, es2)
                r_acc = R()
                accb = k.sb("mg_accb", [128, 2048], BF16, es2)
                r_accb = R()
                tmp = k.sb("mg_tmp", [128, 512], F32, es2)
                r_tmp = R()
                ptr = [k.ps("mg_ptr%d" % i, [128, 8, 128], BF16, es2) for i in range(2)]
                r_ptr = [PR() for _ in range(2)]
                pm = [k.ps("mg_pm%d" % i, [128, 512], F32, es2) for i in range(5)]
                r_pm = [PR() for _ in range(5)]
                ip = 0
                ig = 0
                for t in range(NT):
                    rows = slice(t * 128, (t + 1) * 128)
                    for b in range(4):
                        k.dma(yt[:, b, :], ybr[b][0][rows, :], reads=[ybr[b][1]], writes=[r_yt])
                    k.op("scalar", lambda e: e.activation(out=ytb[:], in_=yt[:].rearrange("p a b -> p (a b)"), func=AF.Copy), reads=[r_yt], writes=[r_ytb])
                    for hf in range(2):
                        def tr(e, hf=hf):
                            last = None
                            for j in range(8):
                                c = hf * 8 + j
                                last = e.transpose(out=ptr[hf][:, j, :], in_=ytb[:, c * 128:(c + 1) * 128], identity=self.ident[:])
                            return last
                        k.op("tensor", tr, reads=[r_ytb, self.r_ident], writes=[r_ptr[hf]])
                        if hf == 0:
                            k.op("vector", lambda e: e.tensor_copy(out=yT[:, 0:8, :], in_=ptr[0][:]), reads=[r_ptr[0]], writes=[r_yT])
                        else:
                            k.op("scalar", lambda e: e.activation(out=yT[:, 8:16, :], in_=ptr[1][:], func=AF.Copy), reads=[r_ptr[1], r_yT], writes=[r_yT])
                    for b in range(4):
                        g_, rg = gt[ig % 2], r_gt[ig % 2]
                        ig += 1
                        k.dma(g_[:], z[rows, C_GATE + b * 2048:C_GATE + (b + 1) * 2048], reads=[r_zt[t]], writes=[rg])
                        k.op("scalar", lambda e, g_=g_: e.activation(out=g_[:], in_=g_[:], func=AF.Sigmoid), reads=[rg], writes=[rg])
                        for cbk in range(4):
                            p, rp = pm[ip % 5], r_pm[ip % 5]
                            ip += 1
                            cs_ = slice(cbk * 512, (cbk + 1) * 512)

                            def mm(e, p=p, b=b, cs_=cs_):
                                last = None
                                for kc in range(4):
                                    last = e.matmul(p[:], lhsT=yT[:, b * 4 + kc, :], rhs=W[:, b * 4 + kc, cs_], start=(kc == 0), stop=(kc == 3))
                                return last
                            k.op("tensor", mm, reads=[r_yT] + r_W[b * 4:b * 4 + 4], writes=[rp])
                            if b == 0:
                                k.op("vector", lambda e, p=p, g_=g_, cs_=cs_: e.tensor_tensor(out=acc[:, cs_], in0=p[:], in1=g_[:, cs_], op=ALU.mult),
                                     reads=[rp, rg], writes=[r_acc])
                            else:
                                k.op("vector", lambda e, p=p, g_=g_, cs_=cs_: e.tensor_tensor(out=tmp[:], in0=p[:], in1=g_[:, cs_], op=ALU.mult),
                                     reads=[rp, rg], writes=[r_tmp])
                                k.op("vector", lambda e, cs_=cs_: e.tensor_tensor(out=acc[:, cs_], in0=acc[:, cs_], in1=tmp[:], op=ALU.add),
                                     reads=[r_tmp, r_acc], writes=[r_acc])
                    k.op("vector", lambda e: e.tensor_copy(out=accb[:], in_=acc[:]), reads=[r_acc], writes=[r_accb])
                    k.dma(accD[rows, :], accb[:], reads=[r_accb], writes=[r_acc_t[t]])
                k.barrier()
            for i in range(16):
                k.dma(wst[i % 2][:], wout[l, i * 128:(i + 1) * 128, :], reads=[r_wout], writes=[r_wst[i % 2]])
                if i % 2 == 0:
                    k.op("vector", lambda e, i=i: e.tensor_copy(out=W[:, i, :], in_=wst[i % 2][:]), reads=[r_wst[i % 2]], writes=[r_W[i]])
                else:
                    k.op("scalar", lambda e, i=i: e.activation(out=W[:, i, :], in_=wst[i % 2][:], func=AF.Copy), reads=[r_wst[i % 2]], writes=[r_W[i]])
            with ExitStack() as es3:
                g1 = k.sb("mg_g1", [128, 2, 2048], F32, es3)
                r_g1 = R()
                for r_ in range(2):
                    k.dma(g1[:, r_, :], modD[r_:r_ + 1, 2 * D:3 * D].partition_broadcast(128), reads=[r_modD], writes=[r_g1])
                ab = [k.sb("mg_ab%d" % i, [128, 2048], BF16, es3) for i in range(2)]
                r_ab = [R() for _ in range(2)]
                aT = k.sb("mg_aT", [128, 16, 128], BF16, es3)
                r_aT = R()
                xt = [k.sb("mg_xt%d" % i, [128, 2048], F32, es3) for i in range(2)]
                r_xt = [R() for _ in range(2)]
                xo = [k.sb("mg_xo%d" % i, [128, 2048], F32, es3) for i in range(2)]
                r_xo = [R() for _ in range(2)]
                tmp = k.sb("mg_tmp2", [128, 512], F32, es3)
                r_tmp = R()
                ptr = [k.ps("mg_ptrb%d" % i, [128, 8, 128], BF16, es3) for i in range(2)]
                r_ptr = [PR() for _ in range(2)]
                pm = [k.ps("mg_pmb%d" % i, [128, 512], F32, es3) for i in range(4)]
                r_pm = [PR() for _ in range(4)]
                ip = 0
                for t in range(NT):
                    rows = slice(t * 128, (t + 1) * 128)
                    row = 1 if t < 2 else 0
                    a_, ra = ab[t % 2], r_ab[t % 2]
                    x_, rx = xt[t % 2], r_xt[t % 2]
                    o_, ro = xo[t % 2], r_xo[t % 2]
                    k.dma(a_[:], accD[rows, :], reads=[r_acc_t[t]], writes=[ra])
                    k.dma(x_[:], xin[rows, :], reads=[r_xin], writes=[rx])
                    for hf in range(2):
                        def tr(e, hf=hf, a_=a_):
                            last = None
                            for j in range(8):
                                c = hf * 8 + j
                                last = e.transpose(out=ptr[hf][:, j, :], in_=a_[:, c * 128:(c + 1) * 128], identity=self.ident[:])
                            return last
                        k.op("tensor", tr, reads=[ra, self.r_ident], writes=[r_ptr[hf]])
                        if hf == 0:
                            k.op("vector", lambda e: e.tensor_copy(out=aT[:, 0:8, :], in_=ptr[0][:]), reads=[r_ptr[0]], writes=[r_aT])
                        else:
                            k.op("scalar", lambda e: e.activation(out=aT[:, 8:16, :], in_=ptr[1][:], func=AF.Copy), reads=[r_ptr[1], r_aT], writes=[r_aT])
                    for cbk in range(4):
                        p, rp = pm[ip % 4], r_pm[ip % 4]
                        ip += 1
                        cs_ = slice(cbk * 512, (cbk + 1) * 512)

                        def mm(e, p=p, cs_=cs_):
                            last = None
                            for kc in range(16):
                                last = e.matmul(p[:], lhsT=aT[:, kc, :], rhs=W[:, kc, cs_], start=(kc == 0), stop=(kc == 15))
                            return last
                        k.op("tensor", mm, reads=[r_aT] + r_W, writes=[rp])
                        k.op("vector", lambda e, p=p, cs_=cs_, row=row: e.tensor_tensor(out=tmp[:], in0=p[:], in1=g1[:, row, cs_], op=ALU.mult),
                             reads=[rp, r_g1], writes=[r_tmp])
                        k.op("vector", lambda e, o_=o_, x_=x_, cs_=cs_: e.tensor_tensor(out=o_[:, cs_], in0=tmp[:], in1=x_[:, cs_], op=ALU.add),
                             reads=[r_tmp, rx], writes=[ro])
                    k.dma(x1[rows, :], o_[:], reads=[ro], writes=[self.r_x1_t[t]])
                k.barrier()

    def stage_moe(self, l, final):
        k = self.k
        x1, _ = self.dr["x1_%d" % l]
        r_x1_t = self.r_x1_t
        modD, r_modD = self.dr["modD%d" % l]
        XE, r_XE = self.dram("moe_xe%d" % l, [16, 128, 16 * 288], BF16)
        WS, r_WS = self.dram("moe_ws%d" % l, [16, 128, 2 * 2048], BF16)
        WSC, r_WSC = self.dram("moe_wsc%d" % l, [16, 32, 256], BF16)
        YE, r_YE = self.dram("moe_ye%d" % l, [16, 288, 2048], BF16)
        r_XEe = [R() for _ in range(16)]
        r_WSe = [R() for _ in range(16)]
        r_YEe = [R() for _ in range(16)]
        if final:
            xo_d, r_xo_d = self.dram("out", [L, D], F32, "ExternalOutput")
        else:
            xo_d, r_xo_d = self.dram("x2_%d" % l, [T, D], F32)
        self.r_x2_t = [R() for _ in range(NT)]
        w1d, r_w1d = self.dr["moe_w1"]
        w3d, r_w3d = self.dr["moe_w3"]
        w2d, r_w2d = self.dr["moe_w2"]
        with ExitStack() as es:
            h2 = k.sb("mo_h2", [128, NT, 2048], BF16, es)
            r_h2 = [R() for _ in range(NT)]
            aff = k.sb("mo_aff", [128, NT, 32], F32, es)
            r_aff = [R() for _ in range(NT)]
            affT = k.sb("mo_affT", [32, T], F32, es)
            r_affT = R()
            k.op("gpsimd", lambda e: e.memset(aff[:], 0.0), writes=r_aff)
            r_c = R()
            with ExitStack() as es2:
                abc = k.sb("mo_abc", [128, 2, 2048], F32, es2)
                shbc = k.sb("mo_shbc", [128, 2, 2048], F32, es2)
                gbc = k.sb("mo_gbc", [128, 2048], F32, es2)
                ap, rr = self.dr["norm2_gb"]
                k.dma(gbc[:], ap[l].partition_broadcast(128), reads=[rr], writes=[r_c])
                for r_ in range(2):
                    k.dma(abc[:, r_, :], modD[r_:r_ + 1, 4 * D:5 * D].partition_broadcast(128), reads=[r_modD], writes=[r_c])
                    k.dma(shbc[:, r_, :], modD[r_:r_ + 1, 3 * D:4 * D].partition_broadcast(128), reads=[r_modD], writes=[r_c])
                for r_ in range(2):
                    k.op("vector", lambda e, r_=r_: e.scalar_tensor_tensor(out=abc[:, r_, :], in0=abc[:, r_, :], scalar=1.0, in1=gbc[:], op0=ALU.add, op1=ALU.mult),
                         reads=[r_c], writes=[r_c])
                rtr = k.sb("mo_rtr", [128, 16, 32], F32, es2)
                k.op("gpsimd", lambda e: e.memset(rtr[:], 0.0), writes=[r_c])
                ap, rr = self.dr["moe_router"]
                k.dma(rtr[:, :, 0:16], ap[l].rearrange("(kc p) n -> p kc n", p=128), reads=[rr], writes=[r_c])
                xt = [k.sb("mo_xt%d" % i, [128, 2048], F32, es2) for i in range(2)]
                r_xt = [R() for _ in range(2)]
                hf_ = k.sb("mo_hf", [128, 2048], F32, es2)
                r_hf = R()
                hT = k.sb("mo_hT", [128, 16, 128], F32, es2)
                r_hT = R()
                junk = k.sb("mo_junk", [128, 2048], BF16, es2)
                r_junk = R()
                st = k.sb("mo_st", [128, NT, 4], F32, es2)
                lg = k.sb("mo_lg", [128, 16], F32, es2)
                r_lg = R()
                ptr = [k.ps("mo_ptr%d" % i, [128, 4, 128], F32, es2) for i in range(4)]
                r_ptr = [PR() for _ in range(4)]
                pl = k.ps("mo_pl", [128, 512], F32, es2)
                r_pl = PR()
                pT = k.ps("mo_pT", [128, 512], F32, es2)
                r_pT = PR()
                for t in range(NT):
                    rows = slice(t * 128, (t + 1) * 128)
                    row = 1 if t < 2 else 0
                    x_, rx = xt[t % 2], r_xt[t % 2]
                    r_s = R()
                    k.dma(x_[:], x1[rows, :], reads=[r_x1_t[t]], writes=[rx])
                    k.op("scalar", lambda e, x_=x_, t=t: e.activation(out=junk[:], in_=x_[:], func=AF.Square, accum_out=st[:, t, 0:1]), reads=[rx], writes=[r_junk, r_s])
                    k.op("scalar", lambda e, t=t: e.activation(out=st[:, t, 1:2], in_=st[:, t, 0:1], func=AF.Sqrt, bias=self.eps_t[:, 0:1], scale=1.0 / D), reads=[r_s], writes=[r_s])
                    k.op("vector", lambda e, t=t: e.reciprocal(out=st[:, t, 1:2], in_=st[:, t, 1:2]), reads=[r_s], writes=[r_s])
                    k.op("vector", lambda e, x_=x_, t=t, row=row: e.scalar_tensor_tensor(out=hf_[:], in0=x_[:], scalar=st[:, t, 1:2], in1=abc[:, row, :], op0=ALU.mult, op1=ALU.mult),
                         reads=[rx, r_s, r_c], writes=[r_hf])
                    k.op("vector", lambda e, row=row: e.tensor_tensor(out=hf_[:], in0=hf_[:], in1=shbc[:, row, :], op=ALU.add), reads=[r_hf, r_c], writes=[r_hf])
                    k.op("scalar", lambda e, t=t: e.activation(out=h2[:, t, :], in_=hf_[:], func=AF.Copy), reads=[r_hf], writes=[r_h2[t]])
                    for g in range(4):
                        def tr(e, g=g):
                            last = None
                            for j in range(4):
                                c = g * 4 + j
                                last = e.transpose(out=ptr[g][:, j, :], in_=hf_[:, c * 128:(c + 1) * 128], identity=self.identf[:])
                            return last
                        k.op("tensor", tr, reads=[r_hf, self.r_ident], writes=[r_ptr[g]])
                        if g % 2 == 0:
                            k.op("vector", lambda e, g=g: e.tensor_copy(out=hT[:, g * 4:(g + 1) * 4, :], in_=ptr[g][:]), reads=[r_ptr[g]], writes=[r_hT])
                        else:
                            k.op("scalar", lambda e, g=g: e.activation(out=hT[:, g * 4:(g + 1) * 4, :], in_=ptr[g][:], func=AF.Copy), reads=[r_ptr[g], r_hT], writes=[r_hT])

                    def mmr(e):
                        last = None
                        for kc in range(16):
                            last = e.matmul(pl[:, 0:32], lhsT=hT[:, kc, :], rhs=rtr[:, kc, :], start=(kc == 0), stop=(kc == 15))
                        return last
                    k.op("tensor", mmr, reads=[r_hT, r_c], writes=[r_pl])
                    k.op("vector", lambda e, t=t: e.tensor_reduce(out=st[:, t, 2:3], in_=pl[:, 0:16], axis=AX.X, op=ALU.max), reads=[r_pl], writes=[r_s])
                    k.op("vector", lambda e, t=t: e.tensor_scalar(out=lg[:], in0=pl[:, 0:16], scalar1=st[:, t, 2:3], scalar2=None, op0=ALU.subtract), reads=[r_pl, r_s], writes=[r_lg])
                    k.op("scalar", lambda e, t=t: e.activation(out=lg[:], in_=lg[:], func=AF.Exp, accum_out=st[:, t, 3:4]), reads=[r_lg], writes=[r_lg, r_s])
                    k.op("vector", lambda e, t=t: e.reciprocal(out=st[:, t, 3:4], in_=st[:, t, 3:4]), reads=[r_s], writes=[r_s])
                    k.op("vector", lambda e, t=t: e.tensor_scalar(out=aff[:, t, 0:16], in0=lg[:], scalar1=st[:, t, 3:4], scalar2=None, op0=ALU.mult), reads=[r_lg, r_s], writes=[r_aff[t]])
                    k.op("tensor", lambda e, t=t: e.transpose(out=pT[0:32, 0:128], in_=aff[:, t, :], identity=self.identf[:]), reads=[r_aff[t], self.r_ident], writes=[r_pT])
                    k.op("vector", lambda e, rows=rows: e.tensor_copy(out=affT[:, rows], in_=pT[0:32, 0:128]), reads=[r_pT], writes=[r_affT])
                k.barrier()
            with ExitStack() as es3:
                sel = k.sb("mo_sel", [32, 16, 128], F32, es3)
                ap, rr = self.dr["moe_sel"]
                k.dma(sel[:], ap, reads=[rr], writes=[r_c])
                jidx = k.sb("mo_jidx", [128, 2], F32, es3)
                ap, rr = self.dr["moe_jidx"]
                k.dma(jidx[:], ap, reads=[rr], writes=[r_c])
                ones = k.sb("mo_ones", [128, 128], BF16, es3)
                k.op("vector", lambda e: e.memset(ones[:], 1.0), writes=[r_c])
                A2 = [k.sb("mo_A%d" % i, [128, T], F32, es3) for i in range(2)]
                r_A2 = [R() for _ in range(2)]
                cmp = [k.sb("mo_cmp%d" % i, [128, 2048], BF16, es3) for i in range(2)]
                r_cmp = [R() for _ in range(2)]
                Ws = k.sb("mo_Ws", [128, 2, 2048], BF16, es3)
                r_Ws = R()
                Ps = k.sb("mo_Ps", [128, 2, 2048], BF16, es3)
                r_Ps = R()
                Wc = k.sb("mo_Wc", [128, 256], BF16, es3)
                r_Wc = R()
                Pc = k.sb("mo_Pc", [128, 256], BF16, es3)
                r_Pc = R()
                PsT = k.sb("mo_PsT", [128, NT, 256], BF16, es3)
                r_PsT = R()
                xe = [k.sb("mo_xe%d" % i, [128, 16, 288], BF16, es3) for i in range(2)]
                r_xe = [R() for _ in range(2)]
                prk = k.ps("mo_prk", [128, 2048], F32, es3)
                r_prk = PR()
                pA = k.ps("mo_pA", [128, 512], F32, es3)
                r_pA = PR()
                ptb = k.ps("mo_ptb", [128, 8, 128], BF16, es3)
                r_ptb = PR()
                pg = [k.ps("mo_pg%d" % i, [128, 512], F32, es3) for i in range(2)]
                r_pg = [PR() for _ in range(2)]
                ic = 0
                ig = 0
                def emit_A(e_):
                    A, r_A = A2[e_ % 2], r_A2[e_ % 2]
                    for cb in range(5):
                        c0 = cb * 512
                        cw = min(512, T - c0)
                        k.op("tensor", lambda e, c0=c0, cw=cw, e_=e_: e.matmul(pA[:, 0:cw], lhsT=sel[:, e_, :], rhs=affT[:, c0:c0 + cw], start=True, stop=True),
                             reads=[r_affT, r_c], writes=[r_pA])
                        k.op("scalar", lambda e, c0=c0, cw=cw, A=A: e.activation(out=A[:, c0:c0 + cw], in_=pA[:, 0:cw], func=AF.Copy), reads=[r_pA], writes=[r_A])
                icn = [0]

                def rank_piece(e_, mt):
                    A, r_A = A2[e_ % 2], r_A2[e_ % 2]
                    c_, rc_ = cmp[icn[0] % 2], r_cmp[icn[0] % 2]
                    icn[0] += 1
                    k.op("vector", lambda e, c_=c_, mt=mt, e_=e_, A=A: e.tensor_scalar(out=c_[:], in0=A[:, NCTX:T], scalar1=aff[:, 2 + mt, e_:e_ + 1], scalar2=None, op0=ALU.is_lt),
                         reads=[r_A, r_aff[2 + mt]], writes=[rc_])

                    def mmc(e, c_=c_, mt=mt):
                        last = None
                        for cb in range(4):
                            last = e.matmul(prk[:, cb * 512:(cb + 1) * 512], lhsT=ones[:], rhs=c_[:, cb * 512:(cb + 1) * 512], start=(mt == 0), stop=(mt == 15))
                        return last
                    k.op("tensor", mmc, reads=[rc_, r_c], writes=[r_prk])

                def gather_piece(dc, x_, rx):
                    p, rp = pg[dc % 2], r_pg[dc % 2]

                    def mmg(e, p=p, dc=dc):
                        last = None
                        for nt in range(16):
                            last = e.matmul(p[:, 0:256], lhsT=h2[:, 2 + nt, dc * 128:(dc + 1) * 128], rhs=PsT[:, 2 + nt, :], start=(nt == 0), stop=(nt == 15))
                        for nt in range(2):
                            last = e.matmul(p[:, 256:288], lhsT=h2[:, nt, dc * 128:(dc + 1) * 128], rhs=PsT[:, nt, 0:32], start=(nt == 0), stop=(nt == 1))
                        return last
                    k.op("tensor", mmg, reads=[r_PsT] + r_h2, writes=[rp])
                    k.op("scalar", lambda e, p=p, x_=x_, dc=dc: e.activation(out=x_[:, dc, :], in_=p[:, 0:288], func=AF.Copy), reads=[rp, rx], writes=[rx])

                emit_A(0)
                for mt in range(16):
                    rank_piece(0, mt)
                for e_ in range(16):
                    A, r_A = A2[e_ % 2], r_A2[e_ % 2]
                    ic = icn[0]
                    for jt in range(2):
                        k.op("vector", lambda e, jt=jt: e.scalar_tensor_tensor(out=Ws[:, jt, :], in0=prk[:], scalar=jidx[:, jt:jt + 1], in1=A[:, NCTX:T], op0=ALU.is_equal, op1=ALU.mult),
                             reads=[r_prk, r_A, r_c], writes=[r_Ws])
                        k.op("vector", lambda e, jt=jt: e.tensor_scalar(out=Ps[:, jt, :], in0=prk[:], scalar1=jidx[:, jt:jt + 1], scalar2=None, op0=ALU.is_equal),
                             reads=[r_prk, r_c], writes=[r_Ps])
                    k.dma(WS[e_].rearrange("p (a b) -> p a b", a=2), Ws[:], reads=[r_Ws], writes=[r_WSe[e_]])
                    for mt in range(2):
                        c_, rc_ = cmp[ic % 2], r_cmp[ic % 2]
                        ic += 1
                        k.op("vector", lambda e, c_=c_, mt=mt, e_=e_: e.tensor_scalar(out=c_[:, 0:256], in0=A[:, 0:NCTX], scalar1=aff[:, mt, e_:e_ + 1], scalar2=None, op0=ALU.is_lt),
                             reads=[r_A, r_aff[mt]], writes=[rc_])
                        k.op("tensor", lambda e, c_=c_, mt=mt: e.matmul(prk[:, 0:256], lhsT=ones[:], rhs=c_[:, 0:256], start=(mt == 0), stop=(mt == 1)),
                             reads=[rc_, r_c], writes=[r_prk])
                    k.op("vector", lambda e: e.scalar_tensor_tensor(out=Wc[:], in0=prk[:, 0:256], scalar=jidx[:, 0:1], in1=A[:, 0:NCTX], op0=ALU.is_equal, op1=ALU.mult),
                         reads=[r_prk, r_A, r_c], writes=[r_Wc])
                    k.op("vector", lambda e: e.tensor_scalar(out=Pc[:], in0=prk[:, 0:256], scalar1=jidx[:, 0:1], scalar2=None, op0=ALU.is_equal),
                         reads=[r_prk, r_c], writes=[r_Pc])
                    k.dma(WSC[e_], Wc[0:32, :], reads=[r_Wc], writes=[r_WSe[e_]])
                    for nt in range(16):
                        if nt % 4 == 0:
                            def trp(e, nt=nt):
                                last = None
                                for q in range(4):
                                    for jt in range(2):
                                        last = e.transpose(out=ptb[:, q * 2 + jt, :], in_=Ps[:, jt, (nt + q) * 128:(nt + q + 1) * 128], identity=self.ident[:])
                                return last
                            k.op("tensor", trp, reads=[r_Ps, self.r_ident], writes=[r_ptb])
                            k.op("scalar", lambda e, nt=nt: e.activation(out=PsT[:, 2 + nt:2 + nt + 4, :], in_=ptb[:].rearrange("p (q j) n -> p q (j n)", q=4), func=AF.Copy),
                                 reads=[r_ptb], writes=[r_PsT])

                    def trc(e):
                        last = None
                        for nt in range(2):
                            last = e.transpose(out=ptb[:, nt, 0:32], in_=Pc[0:32, nt * 128:(nt + 1) * 128], identity=self.ident[0:32, 0:32])
                        return last
                    k.op("tensor", trc, reads=[r_Pc, self.r_ident], writes=[r_ptb])
                    k.op("scalar", lambda e: e.activation(out=PsT[:, 0:2, 0:32], in_=ptb[:, 0:2, 0:32], func=AF.Copy), reads=[r_ptb], writes=[r_PsT])
                    icn[0] = ic
                    if e_ + 1 < 16:
                        emit_A(e_ + 1)
                    x_, rx = xe[e_ % 2], r_xe[e_ % 2]
                    for dc in range(16):
                        gather_piece(dc, x_, rx)
                        if e_ + 1 < 16:
                            rank_piece(e_ + 1, dc)
                    k.dma(XE[e_], x_[:].rearrange("p a b -> p (a b)"), reads=[rx], writes=[r_XEe[e_]])
                k.barrier()
        if self.dbg.get("MOE_PH", 9) < 2:
            return
        with ExitStack() as es:
            w1b = k.sb("mo_w1b", [128, 16, 1024], BF16, es)
            w3b = k.sb("mo_w3b", [128, 16, 1024], BF16, es)
            w2b = k.sb("mo_w2b", [128, 8, 2048], BF16, es)
            r_w1 = [R() for _ in range(8)]
            r_w3 = [R() for _ in range(8)]
            r_w2 = [R() for _ in range(8)]
            wst = [k.sb("mo_wst%d" % i, [128, 4096], F32, es) for i in range(3)]
            r_wst = [R() for _ in range(3)]
            xe = [k.sb("mo_xeb%d" % i, [128, 16, 288], BF16, es) for i in range(2)]
            r_xe = [R() for _ in range(2)]
            gT = k.sb("mo_gT", [128, 8, 288], BF16, es)
            r_gT = [R() for _ in range(8)]
            sa8 = k.sb("mo_sa8", [128, 8, 288], F32, es)
            r_sa8 = [R() for _ in range(8)]
            yeb = [k.sb("mo_yeb%d" % i, [128, 3, 2048], BF16, es) for i in range(1)]
            r_yeb = [R() for _ in range(1)]
            pa = [k.ps("mo_pa%d" % i, [128, 512], F32, es) for i in range(2)]
            r_pa = [PR() for _ in range(2)]
            pu = [k.ps("mo_pu%d" % i, [128, 512], F32, es) for i in range(2)]
            r_pu = [PR() for _ in range(2)]
            py = [k.ps("mo_py%d" % i, [128, 512], F32, es) for i in range(3)]
            r_py = [PR() for _ in range(3)]
            iw = 0
            iy = 0
            engs = ("vector", "gpsimd")
            for e_ in range(16):
                x_, rx = xe[e_ % 2], r_xe[e_ % 2]
                k.dma(x_[:].rearrange("p a b -> p (a b)"), XE[e_], reads=[r_XEe[e_]], writes=[rx])
                for (wd, rwd_, wb_, rw_) in ((w1d, r_w1d, w1b, r_w1), (w3d, r_w3d, w3b, r_w3)):
                    for pr in range(4):
                        s_, rs = wst[iw % 3], r_wst[iw % 3]
                        k.dma(s_[:].rearrange("p (a n) -> p a n", a=4), wd[l, e_, pr * 512:(pr + 1) * 512, :].rearrange("(a p) n -> p a n", p=128), reads=[rwd_], writes=[rs])
                        for hh in range(2):
                            sv = s_[:, hh * 2048:(hh + 1) * 2048].rearrange("p (a n) -> p a n", a=2)
                            dv = wb_[:, 4 * pr + 2 * hh:4 * pr + 2 * hh + 2, :]
                            if hh == 0:
                                k.op("vector", lambda e, sv=sv, dv=dv: e.tensor_copy(out=dv, in_=sv), reads=[rs], writes=[rw_[2 * pr + hh]])
                            else:
                                k.op("scalar", lambda e, sv=sv, dv=dv: e.activation(out=dv, in_=sv, func=AF.Copy), reads=[rs], writes=[rw_[2 * pr + hh]])
                        iw += 1
                for fp in range(4):
                    s_, rs = wst[iw % 3], r_wst[iw % 3]
                    k.dma(s_[:].rearrange("p (a n) -> p a n", a=2), w2d[l, e_, fp * 256:(fp + 1) * 256, :].rearrange("(a p) n -> p a n", p=128), reads=[r_w2d], writes=[rs])
                    for hh in range(2):
                        fc = 2 * fp + hh
                        sv = s_[:, hh * 2048:(hh + 1) * 2048]
                        if hh == 0:
                            k.op("vector", lambda e, sv=sv, fc=fc: e.tensor_copy(out=w2b[:, fc, :], in_=sv), reads=[rs], writes=[r_w2[fc]])
                        else:
                            k.op("scalar", lambda e, sv=sv, fc=fc: e.activation(out=w2b[:, fc, :], in_=sv, func=AF.Copy), reads=[rs], writes=[r_w2[fc]])
                    iw += 1
                for fc in range(8):
                    pa_, rpa = pa[fc % 2], r_pa[fc % 2]

                    def mma(e, p_=pa_, fc=fc, x_=x_):
                        last = None
                        for kc in range(16):
                            last = e.matmul(p_[:, 0:288], lhsT=w1b[:, kc, fc * 128:(fc + 1) * 128], rhs=x_[:, kc, :], start=(kc == 0), stop=(kc == 15))
                        return last
                    k.op("tensor", mma, reads=r_w1 + [rx], writes=[rpa])
                    k.op("scalar", lambda e, pa_=pa_, fc=fc: e.activation(out=sa8[:, fc, :], in_=pa_[:, 0:288], func=AF.Silu), reads=[rpa], writes=[r_sa8[fc]])
                for fc in range(8):
                    pu_, rpu = pu[fc % 2], r_pu[fc % 2]

                    def mmu(e, p_=pu_, fc=fc, x_=x_):
                        last = None
                        for kc in range(16):
                            last = e.matmul(p_[:, 0:288], lhsT=w3b[:, kc, fc * 128:(fc + 1) * 128], rhs=x_[:, kc, :], start=(kc == 0), stop=(kc == 15))
                        return last
                    k.op("tensor", mmu, reads=r_w3 + [rx], writes=[rpu])
                    k.op("vector", lambda e, pu_=pu_, fc=fc: e.tensor_tensor(out=gT[:, fc, :], in0=sa8[:, fc, :], in1=pu_[:, 0:288], op=ALU.mult), reads=[r_sa8[fc], rpu], writes=[r_gT[fc]])
                y_, ry = yeb[0], r_yeb[0]
                for jt, (j0, M) in enumerate(((0, 128), (128, 128), (256, 32))):
                    for cbk in range(4):
                        p_, rp_ = py[iy % 3], r_py[iy % 3]
                        iy += 1
                        cs_ = slice(cbk * 512, (cbk + 1) * 512)

                        def mmy(e, p_=p_, j0=j0, M=M, cs_=cs_):
                            last = None
                            for fc in range(8):
                                last = e.matmul(p_[0:M, :], lhsT=gT[:, fc, j0:j0 + M], rhs=w2b[:, fc, cs_], start=(fc == 0), stop=(fc == 7))
                            return last
                        k.op("tensor", mmy, reads=r_gT + r_w2, writes=[rp_])
                        if iy % 2 == 0:
                            k.op("vector", lambda e, p_=p_, y_=y_, jt=jt, M=M, cs_=cs_: e.tensor_copy(out=y_[0:M, jt, cs_], in_=p_[0:M, :]), reads=[rp_], writes=[ry])
                        else:
                            k.op("scalar", lambda e, p_=p_, y_=y_, jt=jt, M=M, cs_=cs_: e.activation(out=y_[0:M, jt, cs_], in_=p_[0:M, :], func=AF.Copy), reads=[rp_], writes=[ry])
                k.dma(YE[e_, 0:256, :].rearrange("(a p) n -> p a n", p=128), y_[:, 0:2, :], reads=[ry], writes=[r_YEe[e_]])
                k.dma(YE[e_, 256:288, :], y_[0:32, 2, :], reads=[ry], writes=[r_YEe[e_]])
            k.barrier()
        if self.dbg.get("MOE_PH", 9) < 3:
            return
        if not final:
            with ExitStack() as es:
                macc = k.sb("mo_macc", [128, 2, 2048], F32, es)
                r_macc = [R() for _ in range(2)]
                g2 = k.sb("mo_g2c", [128, 2048], F32, es)
                r_g2 = R()
                k.dma(g2[:], modD[1:2, 5 * D:6 * D].partition_broadcast(128), reads=[r_modD], writes=[r_g2])
                ye = [k.sb("mo_yec%d" % i, [32, 2048], BF16, es) for i in range(2)]
                r_ye = [R() for _ in range(2)]
                ws = [k.sb("mo_wsc%d" % i, [32, 256], BF16, es) for i in range(2)]
                r_ws = [R() for _ in range(2)]
                xt = [k.sb("mo_x2c%d" % i, [128, 2048], F32, es) for i in range(2)]
                r_xt = [R() for _ in range(2)]
                ps = [k.ps("mo_psc%d" % i, [128, 512], F32, es) for i in range(4)]
                r_ps = [PR() for _ in range(4)]
                ip = 0
                for e_ in range(16):
                    y_, ry = ye[e_ % 2], r_ye[e_ % 2]
                    w_, rw = ws[e_ % 2], r_ws[e_ % 2]
                    k.dma(y_[:], YE[e_, 256:288, :], reads=[r_YEe[e_]], writes=[ry])
                    k.dma(w_[:], WSC[e_], reads=[r_WSe[e_]], writes=[rw])
                    for tl in range(2):
                        for cbk in range(4):
                            p_, rp_ = ps[ip % 4], r_ps[ip % 4]
                            ip += 1
                            cs_ = slice(cbk * 512, (cbk + 1) * 512)
                            k.op("tensor", lambda e, p_=p_, w_=w_, y_=y_, tl=tl, cs_=cs_: e.matmul(p_[:], lhsT=w_[:, tl * 128:(tl + 1) * 128], rhs=y_[:, cs_], start=True, stop=True),
                                 reads=[rw, ry], writes=[rp_])
                            if e_ == 0:
                                k.op("vector", lambda e, p_=p_, tl=tl, cs_=cs_: e.tensor_copy(out=macc[:, tl, cs_], in_=p_[:]), reads=[rp_], writes=[r_macc[tl]])
                            else:
                                k.op("vector", lambda e, p_=p_, tl=tl, cs_=cs_: e.tensor_tensor(out=macc[:, tl, cs_], in0=macc[:, tl, cs_], in1=p_[:], op=ALU.add),
                                     reads=[rp_, r_macc[tl]], writes=[r_macc[tl]])
                if not final:
                    for tl in range(2):
                        rows = slice(tl * 128, (tl + 1) * 128)
                        x_, rx = xt[tl], r_xt[tl]
                        k.dma(x_[:], x1[rows, :], reads=[r_x1_t[tl]], writes=[rx])
                        k.op("vector", lambda e, tl=tl: e.tensor_tensor(out=macc[:, tl, :], in0=macc[:, tl, :], in1=g2[:], op=ALU.mult), reads=[r_macc[tl], r_g2], writes=[r_macc[tl]])
                        k.op("vector", lambda e, tl=tl, x_=x_: e.tensor_tensor(out=x_[:], in0=x_[:], in1=macc[:, tl, :], op=ALU.add), reads=[rx, r_macc[tl]], writes=[rx])
                        k.dma(xo_d[rows, :], x_[:], reads=[rx], writes=[self.r_x2_t[tl]])
                k.barrier()
        with ExitStack() as es:
            YA = k.sb("mo_YA", [128, 16, 2, 1024], BF16, es)
            r_YA = [R() for _ in range(16)]
            WA = k.sb("mo_WA", [128, 16, 2, 1024], BF16, es)
            r_WA = [R() for _ in range(16)]
            g2 = k.sb("mo_g2l", [128, 2048], F32, es)
            r_g2 = R()
            k.dma(g2[:], modD[0:1, 5 * D:6 * D].partition_broadcast(128), reads=[r_modD], writes=[r_g2])
            xt = [k.sb("mo_x2l%d" % i, [128, 1024], F32, es) for i in range(2)]
            r_xt = [R() for _ in range(2)]
            tmp = [k.sb("mo_t2l%d" % i, [128, 512], F32, es) for i in range(2)]
            r_tmp = [R() for _ in range(2)]
            ps = [k.ps("mo_psl%d" % i, [128, 512], F32, es) for i in range(4)]
            r_ps = [PR() for _ in range(4)]
            ip = 0
            ix = 0
            for dh in range(2):
                d0 = dh * 1024
                for e_ in range(16):
                    k.dma(YA[:, e_, :, :], YE[e_, 0:256, d0:d0 + 1024].rearrange("(a p) n -> p a n", p=128), reads=[r_YEe[e_]], writes=[r_YA[e_]])
                for nh in range(2):
                    n0 = nh * 1024
                    for e_ in range(16):
                        k.dma(WA[:, e_, :, :], WS[e_].rearrange("p (a b) -> p a b", a=2)[:, :, n0:n0 + 1024], reads=[r_WSe[e_]], writes=[r_WA[e_]])
                    for tl in range(8):
                        t = 2 + nh * 8 + tl
                        rows = slice(t * 128, (t + 1) * 128)
                        x_, rx = xt[ix % 2], r_xt[ix % 2]
                        ix += 1
                        k.dma(x_[:], x1[rows, d0:d0 + 1024], reads=[r_x1_t[t]], writes=[rx])
                        for cb in range(2):
                            p_, rp_ = ps[ip % 4], r_ps[ip % 4]
                            t_, rt_ = tmp[ip % 2], r_tmp[ip % 2]
                            ip += 1
                            cs_ = slice(cb * 512, (cb + 1) * 512)

                            def mms(e, p_=p_, tl=tl, cs_=cs_):
                                last = None
                                n = 0
                                for e2 in range(16):
                                    for jt in range(2):
                                        last = e.matmul(p_[:], lhsT=WA[:, e2, jt, tl * 128:(tl + 1) * 128], rhs=YA[:, e2, jt, cs_], start=(n == 0), stop=(n == 31))
                                        n += 1
                                return last
                            k.op("tensor", mms, reads=r_WA + r_YA, writes=[rp_])
                            k.op("vector", lambda e, p_=p_, t_=t_, cs_=cs_, d0=d0: e.tensor_tensor(out=t_[:], in0=p_[:], in1=g2[:, d0 + cs_.start:d0 + cs_.stop], op=ALU.mult),
                                 reads=[rp_, r_g2], writes=[rt_])
                            k.op("vector", lambda e, t_=t_, x_=x_, cs_=cs_: e.tensor_tensor(out=x_[:, cs_], in0=x_[:, cs_], in1=t_[:], op=ALU.add), reads=[rt_, rx], writes=[rx])
                        if final:
                            k.dma(xo_d[(t - 2) * 128:(t - 1) * 128, d0:d0 + 1024], x_[:], reads=[rx], writes=[r_xo_d])
                        else:
                            k.dma(xo_d[rows, d0:d0 + 1024], x_[:], reads=[rx], writes=[self.r_x2_t[t]])
            k.barrier()

    def declare_inputs(self):
        dp = self.depth
        self.dram("xcat", [T, D], F32, "ExternalInput")
        self.dram("ada_w", [dp, D, 6 * D], F32, "ExternalInput")
        self.dram("ada_b2", [dp, 2, 6 * D], F32, "ExternalInput")
        self.dram("norm1_gT", [dp, 128, 16], F32, "ExternalInput")
        self.dram("w_in", [dp, D, ZC], F32, "ExternalInput")
        self.dram("na_qg", [dp, 1, 512], F32, "ExternalInput")
        self.dram("na_kg", [dp, 1, 512], F32, "ExternalInput")
        self.dram("na_bias", [dp, 8, 5, 5, 128, 128], F32, "ExternalInput")
        self.dram("mla_cqg", [dp, 1, 512], F32, "ExternalInput")
        self.dram("mla_ckvg", [dp, 1, 256], F32, "ExternalInput")
        self.dram("mla_qg", [dp, 1, 768], F32, "ExternalInput")
        self.dram("mla_kng", [dp, 1, 512], F32, "ExternalInput")
        self.dram("mla_kpg", [dp, 1, 64], F32, "ExternalInput")
        self.dram("mla_w_uq", [dp, 512, 768], F32, "ExternalInput")
        self.dram("mla_w_ukv", [dp, 256, 1024], F32, "ExternalInput")
        self.dram("rope_cos", [T, 64], F32, "ExternalInput")
        self.dram("dft_c", [128, 256], BF16, "ExternalInput")
        self.dram("norm2_gb", [dp, 1, D], F32, "ExternalInput")
        self.dram("moe_router", [dp, D, 16], F32, "ExternalInput")
        self.dram("moe_w1", [dp, 16, D, 1024], F32, "ExternalInput")
        self.dram("moe_w3", [dp, 16, D, 1024], F32, "ExternalInput")
        self.dram("moe_w2", [dp, 16, 1024, D], F32, "ExternalInput")
        self.dram("moe_sel", [32, 16, 128], F32, "ExternalInput")
        self.dram("moe_jidx", [128, 2], F32, "ExternalInput")
        self.dram("w_br", [dp, 4, 512, D], F32, "ExternalInput")
        self.dram("w_out", [dp, D, D], F32, "ExternalInput")
        self.dram("rw_mu", [dp, 1, 1760], F32, "ExternalInput")
        self.dram("rw_kk", [dp, 1, 512], F32, "ExternalInput")
        self.dram("rw_ka", [dp, 1, 512], F32, "ExternalInput")
        self.dram("rw_w0f", [dp, 1, 1024], F32, "ExternalInput")
        self.dram("rw_a0f", [dp, 1, 1024], F32, "ExternalInput")
        self.dram("rw_wa2", [dp, 32, 2, 2, 512], F32, "ExternalInput")
        self.dram("rw_lnw", [dp, 1, 512], F32, "ExternalInput")
        self.dram("rw_lnb", [dp, 1, 512], F32, "ExternalInput")
        self.dram("rw_rk", [dp, 1, 512], F32, "ExternalInput")
        self.dram("rw_g2", [dp, 96, 512], F32, "ExternalInput")
        self.dram("rw_e2", [128, 64, 128], F32, "ExternalInput")
        self.dram("rw_pm", [128, 128], F32, "ExternalInput")
        self.dram("rw_j64", [64, 64], F32, "ExternalInput")
        self.dram("rw_masks", [128, 6, 128], F32, "ExternalInput")
        self.dram("rw_dm", [128, 2, 1024], F32, "ExternalInput")
        self.dram("dft_L", [2, L, L], BF16, "ExternalInput")
        self.dram("dft_C", [2, NCTX, NCTX], BF16, "ExternalInput")
        self.dram("rope_sin", [T, 64], F32, "ExternalInput")

    def build(self):
        k = self.k
        self.declare_inputs()
        self.eps_t = k.sb("eps_t", [128, 1], F32)
        r = R()
        k.op("vector", lambda e: e.memset(self.eps_t[:], EPS), writes=[r])
        k.barrier()
        self.consts()
        xname = "xcat"
        for l in range(self.depth):
            if self.want("mod"):
                self.stage_mod(l)
            else:
                self.dram("modD%d" % l, [2, 6 * D], F32)
            if self.want("normz"):
                self.stage_normz(l, xname)
            else:
                self.dram("z%d" % l, [T, ZC], F32)
                self.r_ztile = [self.dr["z%d" % l][1]] * NT
            if self.want("na"):
                self.stage_na(l)
            if self.want("mla"):
                self.stage_mla(l)
            if self.want("fourier"):
                self.stage_fourier(l)
            if self.want("rwprep"):
                self.stage_rwprep(l)
            if self.want("rwscan"):
                self.stage_rwscan(l)
            if self.want("rwout"):
                self.stage_rwout(l)
            else:
                for n in ("y_fn%d", "y_na%d", "y_mla%d", "y_rw%d"):
                    if (n % l) not in self.dr:
                        self.dram(n % l, [T, 512], F32)
            if self.want("merge"):
                self.stage_merge(l, xname)
            else:
                self.dram("x1_%d" % l, [T, D], F32)
                self.r_x1_t = [self.dr["x1_%d" % l][1]] * NT
            if self.want("moe"):
                self.stage_moe(l, final=(l == self.depth - 1) and self.final_out)
                xname = "x2_%d" % l
        k.finish([])
        return self.nc


def _na_bias_index():
    rows, kh, W, KW = 32, 8, 64, 16
    dr = np.zeros((5, 5, 128, 128), np.int64)
    dc = np.zeros((5, 5, 128, 128), np.int64)
    va = np.zeros((5, 5, 128, 128), bool)
    kk = np.arange(128)
    qq = np.arange(128)
    for cls, j in enumerate((0, 1, 5, 14, 15)):
        lo = min(max(j - 2, 0), 11)
        for i in range(5):
            kt = lo + i
            krow = (2 * kt + kk // 64)[:, None]
            kc = (kk % 64)[:, None]
            r = (2 * j + qq // 64)[None, :]
            qc = (qq % 64)[None, :]
            rs = np.clip(r - kh // 2, 0, rows - kh)
            cs = np.clip(qc - KW // 2, 0, W - KW)
            v = (krow >= rs) & (krow < rs + kh) & (kc >= cs) & (kc < cs + KW)
            dr[cls, i] = np.clip(krow - r + 8 - 1, 0, 14)
            dc[cls, i] = np.clip(kc - qc + KW - 1, 0, 2 * KW - 2)
            va[cls, i] = v
    return dr, dc, va


def _rope_tables():
    pos = np.arange(L)
    prow, pcol = pos // 64, pos % 64
    freqs = 10000.0 ** (-np.arange(16, dtype=np.float32) / 16)
    ar = prow[:, None].astype(np.float32) * freqs[None, :]
    ac = pcol[:, None].astype(np.float32) * freqs[None, :]
    cos = np.concatenate([np.cos(ar), np.cos(ar), np.cos(ac), np.cos(ac)], 1)
    sin = np.concatenate([-np.sin(ar), np.sin(ar), -np.sin(ac), np.sin(ac)], 1)
    cosT = np.concatenate([np.ones((NCTX, 64), np.float32), cos.astype(np.float32)], 0)
    sinT = np.concatenate([np.zeros((NCTX, 64), np.float32), sin.astype(np.float32)], 0)
    return np.ascontiguousarray(cosT), np.ascontiguousarray(sinT)


_DFT = None


def _dft_consts():
    global _DFT
    if _DFT is None:
        c = np.arange(128)
        ang = 2 * np.pi * ((c[:, None] * c[None, :]) % 128) / 128.0
        dft_c = bf16_np(np.concatenate([np.cos(ang), np.sin(ang)], 1))
        out = []
        for n in (L, NCTX):
            i = np.arange(n, dtype=np.int64)
            ang = 2 * np.pi * ((i[:, None] * i[None, :]) % n) / float(n)
            sc = 1.0 / np.sqrt(n * 128.0)
            out.append(bf16_np(np.stack([np.cos(ang) * sc, -np.sin(ang) * sc], 0)))
        _DFT = (dft_c, out[0], out[1])
    return _DFT


def _rw_chunk_consts():
    MS = np.zeros((128, 128), np.float32)
    MI = np.zeros((128, 128), np.float32)
    BO = np.zeros((128, 128), np.float32)
    for d in range(2):
        for a in range(64):
            for c in range(64):
                before = (a < c) if d == 0 else (a > c)
                MS[d * 64 + a, d * 64 + c] = 1.0 if before else 0.0
                MI[d * 64 + a, d * 64 + c] = 1.0 if (before or a == c) else 0.0
                BO[d * 64 + a, d * 64 + c] = 1.0
    masks = np.stack([MI, BO, MS, -MS, -MS.T, -MI], 1).astype(np.float32)
    dm = np.zeros((128, 8, 2, 64), np.float32)
    dm[0:64, :, 0, :] = 1.0
    dm[64:128, :, 1, :] = 1.0
    dm = dm.reshape(128, 1024)
    return np.ascontiguousarray(masks), np.ascontiguousarray(np.stack([dm, -dm], 1))


def _rw_consts():
    e2 = np.zeros((128, 64, 128), np.float32)
    for i in range(64):
        e2[i, i, 0:64] = 1.0
        e2[64 + 63 - i, i, 64:128] = 1.0
    pm = np.zeros((128, 128), np.float32)
    for i in range(64):
        pm[i, i] = 1.0
        pm[64 + i, 64 + 63 - i] = 1.0
    j64 = np.ascontiguousarray(np.eye(64, dtype=np.float32)[::-1])
    return e2, pm, j64


def host_core(inp, b):
    f32 = np.float32
    m = {}
    m["xcat"] = np.concatenate([inp["ctx"][b], inp["x"][b]], 0).astype(f32)
    cv = np.stack([inp["c"][b], inp["c_ctx"]], 0)
    m["cvT"] = np.ascontiguousarray(cv.reshape(2, 16, 128).transpose(2, 1, 0))
    return m


def host_inputs(inp, b, depth=DEPTH):
    m = host_shared(inp, depth)
    m.update(host_core(inp, b))
    return m


def host_shared(inp, depth=DEPTH):
    f32 = np.float32
    m = {}
    m["ada_w"] = inp["ada_w"][:depth]
    m["ada_b2"] = np.ascontiguousarray(np.broadcast_to(inp["ada_b"][:depth, None, :], (depth, 2, 6 * D)))
    m["norm1_gT"] = np.ascontiguousarray(inp["norm1_g"][:depth].reshape(depth, 16, 128).transpose(0, 2, 1))
    m["w_in"] = inp["w_in"][:depth]
    m["na_qg"] = np.ascontiguousarray(np.tile(inp["na_q_norm"][:depth, None, :], (1, 1, 8)))
    m["na_kg"] = np.ascontiguousarray(np.tile(inp["na_k_norm"][:depth, None, :], (1, 1, 8)))
    dr, dc, va = _na_bias_index()
    rpb = inp["na_rpb"][:depth]
    gathered = rpb[:, :, dr, dc]
    m["na_bias"] = np.where(va[None, None], gathered, f32(-1e30)).astype(f32)
    m["mla_cqg"] = np.ascontiguousarray(inp["mla_cq_norm"][:depth, None, :])
    m["mla_ckvg"] = np.ascontiguousarray(inp["mla_ckv_norm"][:depth, None, :])
    m["mla_qg"] = np.ascontiguousarray(np.tile(inp["mla_q_norm"][:depth, None, :], (1, 1, 4)))
    m["mla_kng"] = np.ascontiguousarray(np.tile(inp["mla_k_norm"][:depth, None, :128], (1, 1, 4)))
    m["mla_kpg"] = np.ascontiguousarray(inp["mla_k_norm"][:depth, None, 128:192])
    m["mla_w_uq"] = inp["mla_w_uq"][:depth]
    m["mla_w_ukv"] = inp["mla_w_ukv"][:depth]
    m["dft_c"], m["dft_L"], m["dft_C"] = _dft_consts()
    m["rw_mu"] = np.ascontiguousarray(np.concatenate([inp["rw_mu_ks"][:depth], inp["rw_mu_qs"][:depth]], 1)[:, None, :])
    m["rw_kk"] = np.ascontiguousarray(inp["rw_k_k"][:depth, None, :])
    m["rw_ka"] = np.ascontiguousarray(inp["rw_k_a"][:depth, None, :])
    m["rw_w0f"] = np.ascontiguousarray(inp["rw_w0"][:depth].reshape(depth, 1, 1024))
    m["rw_a0f"] = np.ascontiguousarray(inp["rw_a0"][:depth].reshape(depth, 1, 1024))
    wa2 = np.stack([inp["rw_w2"][:depth], inp["rw_a2"][:depth]], 1)
    m["rw_wa2"] = np.ascontiguousarray(wa2.transpose(0, 3, 1, 2, 4))
    m["rw_lnw"] = np.ascontiguousarray(inp["rw_ln_w"][:depth, None, :])
    m["rw_lnb"] = np.ascontiguousarray(inp["rw_ln_b"][:depth, None, :])
    m["rw_rk"] = np.ascontiguousarray(inp["rw_r_k"][:depth].reshape(depth, 1, 512))
    m["rw_g2"] = inp["rw_g2"][:depth]
    m["rw_e2"], m["rw_pm"], m["rw_j64"] = _rw_consts()
    m["rw_masks"], m["rw_dm"] = _rw_chunk_consts()
    m["w_br"] = inp["w_br"][:depth]
    m["norm2_gb"] = np.ascontiguousarray(inp["norm2_g"][:depth, None, :])
    m["moe_router"] = inp["moe_router"][:depth]
    m["moe_w1"] = inp["moe_w1"][:depth]
    m["moe_w3"] = inp["moe_w3"][:depth]
    m["moe_w2"] = inp["moe_w2"][:depth]
    sel = np.zeros((32, 16, 128), np.float32)
    for e in range(16):
        sel[e, e, :] = 1.0
    m["moe_sel"] = sel
    m["moe_jidx"] = np.ascontiguousarray(np.stack([np.arange(128), np.arange(128) + 128], 1).astype(np.float32))
    m["w_out"] = inp["w_out"][:depth]
    cosT, sinT = _rope_tables()
    m["rope_cos"], m["rope_sin"] = cosT, sinT
    return m


_PROG = None


def kernel(**inputs):
    global _PROG
    inp = {k_: np.asarray(v) for k_, v in inputs.items()}
    if _PROG is None:
        P = Prog(depth=DEPTH)
        P.build()
        _PROG = P
    P = _PROG
    shared = host_shared(inp, DEPTH)
    in_maps = []
    for b in range(8):
        m = dict(shared)
        m.update(host_core(inp, b))
        in_maps.append({n: m[n] for n, kd in P.kinds.items() if kd == "ExternalInput"})
    res = run_bass_kernel_spmd(P.nc, in_maps, core_ids=list(range(8)))
    return np.stack([np.asarray(r["out"], dtype=np.float32) for r in res.results], 0)
```

```python
import numpy as np
import ml_dtypes
from contextlib import ExitStack
import concourse.bass as bass
import concourse.mybir as mybir
from concourse.bass_utils import run_bass_kernel_spmd

F32 = mybir.dt.float32
BF16 = mybir.dt.bfloat16
I32 = mybir.dt.int32
F32R = mybir.dt.float32r


def fr(ap):
    return ap.bitcast(F32R)
AF = mybir.ActivationFunctionType
ALU = mybir.AluOpType
AX = mybir.AxisListType


class R:
    __slots__ = ("w", "r", "name", "excl")

    def __init__(self, name="", excl=False):
        self.w = []
        self.r = []
        self.name = name
        self.excl = excl


def PR():
    return R("psum", True)


class K:
    ENG = ("tensor", "vector", "scalar", "gpsimd", "sync")
    NDMA = 24
    MAXSEM = 30000

    def __init__(self, nc, es):
        self.nc = nc
        self.es = es
        self.prog = {e: [] for e in self.ENG}
        self.sem = {}
        self.epoch = {e: 0 for e in self.ENG}
        for e in self.ENG:
            self.sem[(e, 0)] = es.enter_context(nc.semaphore("s_%s_0" % e))
        self.cnt = {e: 0 for e in self.ENG}
        self.waited = {e: {} for e in self.ENG}
        self.dsem = [es.enter_context(nc.semaphore("d%d" % i)) for i in range(self.NDMA)]
        self.dval = [0] * self.NDMA
        self.dnext = 0
        self.ninstr = 0

    def sb(self, name, shape, dt=F32, es=None):
        self.uid = getattr(self, "uid", 0) + 1
        name = "%s_u%d" % (name, self.uid)
        return (es or self.es).enter_context(self.nc.sbuf_tensor(name, list(shape), dt))

    def ps(self, name, shape, dt=F32, es=None):
        n = int(np.prod(shape[1:])) * (2 if dt == BF16 else 4)
        assert n % 2048 == 0, (name, shape, n)
        self.uid = getattr(self, "uid", 0) + 1
        name = "%s_u%d" % (name, self.uid)
        return (es or self.es).enter_context(self.nc.psum_tensor(name, list(shape), dt))

    def _key(self, dep):
        return dep[0] if dep[0] in self.ENG else ("d", dep[0])

    def _collect(self, reads, writes):
        deps = {}
        for r in reads:
            for d in r.w:
                k = d[0]
                if deps.get(k, 0) < d[1]:
                    deps[k] = d[1]
        for w in writes:
            for d in w.w + w.r:
                k = d[0]
                if deps.get(k, 0) < d[1]:
                    deps[k] = d[1]
        return deps

    def _emit_waits(self, eng, deps):
        wd = self.waited[eng]
        for k, v in deps.items():
            if wd.get(k, 0) >= v:
                continue
            wd[k] = v
            sem = self.sem[k] if isinstance(k, tuple) else self.dsem[k]
            getattr(self.nc, eng).wait_ge(sem, v)

    def _mark(self, dep, reads, writes):
        for r in reads:
            r.r.append(dep)
        for w in writes:
            w.w = [dep]
            w.r = []

    def op(self, eng, fn, reads=(), writes=()):
        if any(r.excl for r in reads):
            writes = list(writes) + [r for r in reads if r.excl]
            reads = [r for r in reads if not r.excl]
        deps = self._collect(reads, writes)
        self._emit_waits(eng, deps)
        if self.cnt[eng] >= self.MAXSEM:
            self.epoch[eng] += 1
            self.cnt[eng] = 0
            self.sem[(eng, self.epoch[eng])] = self.es.enter_context(
                self.nc.semaphore("s_%s_%d" % (eng, self.epoch[eng])))
        self.cnt[eng] += 1
        key = (eng, self.epoch[eng])
        dep = (key, self.cnt[eng])
        fn(getattr(self.nc, eng)).then_inc(self.sem[key], 1)
        self._mark(dep, reads, writes)
        self.ninstr += 1

    STORE_Q = "scalar"

    def dma(self, out, in_, reads=(), writes=(), q="sync", **kw):
        q = self.STORE_Q if str(out.space) == "DRAM" else "sync"
        deps = self._collect(reads, writes)
        i = self.dnext
        self.dnext = (self.dnext + 1) % self.NDMA
        if self.dval[i] > 0:
            deps[i] = max(deps.get(i, 0), self.dval[i])
        self._emit_waits(q, deps)
        self.dval[i] += 16
        dep = (i, self.dval[i])
        getattr(self.nc, q).dma_start(out=out, in_=in_, **kw).then_inc(self.dsem[i], 16)
        self._mark(dep, reads, writes)
        self.ninstr += 1

    def finish(self, final_regions):
        deps = {}
        for e in self.ENG:
            if self.cnt[e]:
                deps[(e, self.epoch[e])] = self.cnt[e]
        for i in range(self.NDMA):
            if self.dval[i]:
                deps[i] = self.dval[i]
        self._emit_waits("sync", deps)

    def barrier(self):
        deps = {}
        for e in self.ENG:
            if self.cnt[e]:
                deps[(e, self.epoch[e])] = self.cnt[e]
        for i in range(self.NDMA):
            if self.dval[i]:
                deps[i] = self.dval[i]
        for e in self.ENG:
            self._emit_waits(e, dict(deps))


D = 2048
L = 2048
NCTX = 256
T = L + NCTX
NT = T // 128
DEPTH = 2
ZC = 12832
C_NAK, C_NAV, C_CKV, C_KPE, C_RWKS = 0, 512, 1024, 1280, 1344
C_NAQ, C_CQ, C_RWQS, C_FN, C_GATE = 2496, 3008, 3520, 4128, 4640
EPS = 1e-6


def bf16_np(a):
    return np.asarray(a, dtype=np.float32).astype(ml_dtypes.bfloat16)


class Prog:
    def __init__(self, depth=DEPTH, ext=None, stages=None):
        self.depth = depth
        self.ext = ext or {}
        self.stages = stages
        self.nc = bass.Bass("TRN2", target_bir_lowering=False)
        self.es = ExitStack()
        self.k = K(self.nc, self.es)
        self.dr = {}
        self.kinds = {}
        self.modT = {}
        self.final_out = True
        import os
        self.dbg = {kk[4:]: int(v) for kk, v in os.environ.items() if kk.startswith("DBG_")}

    def dram(self, name, shape, dt=F32, kind="Internal"):
        kind = self.ext.get(name, kind)
        t = self.nc.dram_tensor(name, list(shape), dt, kind=kind)
        self.dr[name] = (t.ap(), R(name))
        self.kinds[name] = kind
        return self.dr[name]

    def want(self, s):
        return self.stages is None or s in self.stages

    def consts(self):
        k = self.k
        self.identf = k.sb("identf", [128, 128], F32)
        self.ident = k.sb("ident", [128, 128], BF16)
        self.r_ident = R("ident")
        k.op("gpsimd", lambda e: e.memset(self.identf[:], 0.0), writes=[self.r_ident])
        k.op("gpsimd", lambda e: e.affine_select(out=self.identf[:], in_=self.identf[:], pattern=[[-1, 128]],
                                                  compare_op=ALU.not_equal, fill=1.0, base=0, channel_multiplier=1),
             reads=[self.r_ident], writes=[self.r_ident])
        k.op("vector", lambda e: e.tensor_copy(out=self.ident[:], in_=self.identf[:]),
             reads=[self.r_ident], writes=[self.r_ident])
        ap, r = self.dram("cvT", [128, 16, 2], F32, "ExternalInput")
        self.sT = k.sb("sT", [128, 16, 2], F32)
        self.r_sT = R("sT")
        k.dma(self.sT[:], ap, reads=[r], writes=[self.r_sT])
        k.op("scalar", lambda e: e.activation(out=self.sT[:], in_=self.sT[:], func=AF.Silu),
             reads=[self.r_sT], writes=[self.r_sT])

    def stage_mod(self, l):
        k = self.k
        adaw, r_adaw = self.dr["ada_w"]
        adab, r_adab = self.dr["ada_b2"]
        modD, r_modD = self.dram("modD%d" % l, [2, 6 * D], F32)
        modT = k.sb("modT%d" % l, [128, 6, 16, 2], F32)
        r_modT = R()
        self.modT[l] = (modT, r_modT)
        with ExitStack() as es:
            wt = [k.sb("mod_w%d" % i, [128, 2048], F32, es) for i in range(3)]
            r_wt = [R() for _ in range(3)]
            ps = k.ps("mod_ps", [2, 2048], F32, es)
            r_ps = PR()
            pT = k.ps("mod_pT", [128, 4, 128], F32, es)
            r_pT = PR()
            row = [k.sb("mod_row%d" % i, [128, 2048], F32, es) for i in range(2)]
            r_row = [R() for _ in range(2)]
            for i in range(2):
                k.op("gpsimd", lambda e, i=i: e.memset(row[i][:], 0.0), writes=[r_row[i]])
            bias = k.sb("mod_bias", [2, 6 * D], F32, es)
            r_bias = R()
            k.dma(bias[:], adab[l], reads=[r_adab], writes=[r_bias])
            i2 = self.identf[0:2, 0:2]
            it = 0
            for nb in range(6):
                for kc in range(16):
                    w = wt[it % 3]
                    rw = r_wt[it % 3]
                    it += 1
                    k.dma(w[:], adaw[l, kc * 128:(kc + 1) * 128, nb * 2048:(nb + 1) * 2048], reads=[r_adaw], writes=[rw])

                    def mm(e, w=w, kc=kc):
                        last = None
                        for j in range(4):
                            last = e.matmul(ps[:, j * 512:(j + 1) * 512], lhsT=self.sT[:, kc, :], rhs=w[:, j * 512:(j + 1) * 512],
                                            start=(kc == 0), stop=(kc == 15))
                        return last
                    k.op("tensor", mm, reads=[rw, self.r_sT], writes=[r_ps])
                rt = row[nb % 2]
                rr = r_row[nb % 2]
                k.op("vector", lambda e, rt=rt, nb=nb: e.tensor_tensor(out=rt[0:2, :], in0=ps[:], in1=bias[:, nb * 2048:(nb + 1) * 2048], op=ALU.add),
                     reads=[r_ps, r_bias], writes=[rr])
                k.dma(modD[:, nb * 2048:(nb + 1) * 2048], rt[0:2, :], reads=[rr], writes=[r_modD], q="gpsimd")

                for g in range(4):
                    def tr(e, rt=rt, g=g):
                        last = None
                        for c in range(4):
                            cc = g * 4 + c
                            last = e.transpose(out=pT[:, c, :], in_=rt[:, cc * 128:(cc + 1) * 128], identity=self.identf[:])
                        return last
                    k.op("tensor", tr, reads=[rr, self.r_ident], writes=[r_pT])
                    k.op("vector", lambda e, nb=nb, g=g: e.tensor_copy(out=modT[:, nb, g * 4:(g + 1) * 4, :], in_=pT[:, :, 0:2]),
                         reads=[r_pT], writes=[r_modT])
            k.barrier()

    def stage_normz(self, l, xname):
        k = self.k
        xin, r_xin = self.dr[xname]
        win, r_win = self.dr["w_in"]
        g1T_d, r_g1T = self.dr["norm1_gT"]
        z, r_z = self.dram("z%d" % l, [T, ZC], F32)
        self.r_ztile = [R() for _ in range(NT)]
        modT, r_modT = self.modT[l]
        with ExitStack() as es:
            hT = k.sb("hT", [128, 16, T], BF16, es)
            r_hT = [[R() for _ in range(16)] for _ in range(NT)]
            aT = k.sb("nz_aT", [128, 16, 2], F32, es)
            r_aT = R()
            gT = k.sb("nz_gT", [128, 16], F32, es)
            k.dma(gT[:], g1T_d[l], reads=[r_g1T], writes=[r_aT])
            k.op("vector", lambda e: e.tensor_scalar(out=aT[:], in0=modT[:, 1, :, :], scalar1=1.0, scalar2=None, op0=ALU.add),
                 reads=[r_modT], writes=[r_aT])
            k.op("vector", lambda e: e.tensor_tensor(out=aT[:], in0=aT[:], in1=gT[:].unsqueeze(2).to_broadcast([128, 16, 2]), op=ALU.mult),
                 reads=[r_aT], writes=[r_aT])
            self.norm_to_hT(es, xin, r_xin, hT, r_hT, aT, r_aT, modT[:, 0, :, :], r_modT, "nz")
            wf = [k.sb("nz_wf%d" % i, [128, 4, 512], F32, es) for i in range(3)]
            r_wf = [R() for _ in range(3)]
            wb = [k.sb("nz_wb%d" % i, [128, 16, 512], BF16, es) for i in range(2)]
            r_wb = [[R() for _ in range(4)] for _ in range(2)]
            pz = [k.ps("nz_pz%d" % i, [128, 512], F32, es) for i in range(4)]
            r_pz = [PR() for _ in range(4)]
            zo = [k.sb("nz_zo%d" % i, [128, 512], F32, es) for i in range(4)]
            r_zo = [R() for _ in range(4)]
            ncb = (ZC + 511) // 512
            cnt = 0
            ld = 0
            for cb in range(self.dbg.get("NZ_NCB", ncb)):
                c0 = cb * 512
                cw = min(512, ZC - c0)
                b, rb = wb[cb % 2], r_wb[cb % 2]
                for g in range(4):
                    f, rf = wf[ld % 3], r_wf[ld % 3]
                    k.dma(f[:, :, 0:cw], win[l, g * 512:(g + 1) * 512, c0:c0 + cw].rearrange("(kc p) n -> p kc n", p=128),
                          reads=[r_win], writes=[rf])
                    eng = ("vector", "scalar")[ld % 2]
                    ld += 1
                    if eng == "scalar":
                        k.op("scalar", lambda e, f=f, b=b, cw=cw, g=g: e.activation(out=b[:, g * 4:(g + 1) * 4, 0:cw], in_=f[:, :, 0:cw], func=AF.Copy),
                             reads=[rf], writes=[rb[g]])
                    else:
                        k.op(eng, lambda e, f=f, b=b, cw=cw, g=g: e.tensor_copy(out=b[:, g * 4:(g + 1) * 4, 0:cw], in_=f[:, :, 0:cw]),
                             reads=[rf], writes=[rb[g]])
                for t in range(self.dbg.get("NZ_NT", NT)):
                    p, rp = pz[cnt % 4], r_pz[cnt % 4]
                    o, ro = zo[cnt % 4], r_zo[cnt % 4]
                    cnt += 1

                    def mm(e, p=p, t=t, b=b, cw=cw):
                        last = None
                        for kc in range(16):
                            last = e.matmul(p[:, 0:cw], lhsT=hT[:, kc, t * 128:(t + 1) * 128], rhs=b[:, kc, 0:cw],
                                            start=(kc == 0), stop=(kc == 15))
                        return last
                    k.op("tensor", mm, reads=rb + r_hT[t], writes=[rp])
                    if t % 2 == 0:
                        k.op("vector", lambda e, o=o, p=p, cw=cw: e.tensor_copy(out=o[:, 0:cw], in_=p[:, 0:cw]), reads=[rp], writes=[ro])
                    else:
                        k.op("scalar", lambda e, o=o, p=p, cw=cw: e.activation(out=o[:, 0:cw], in_=p[:, 0:cw], func=AF.Copy), reads=[rp], writes=[ro])
                    k.dma(z[t * 128:(t + 1) * 128, c0:c0 + cw], o[:, 0:cw], reads=[ro], writes=[self.r_ztile[t]], q="gpsimd")
            k.barrier()

    def norm_to_hT(self, es, xin, r_xin, hT, r_hT, aT, r_aT, shT, r_shT, pfx, h_tok=None):
        k = self.k
        xt = [k.sb(pfx + "_xt%d" % i, [128, D], F32, es) for i in range(2)]
        r_xt = [R() for _ in range(2)]
        xn = [k.sb(pfx + "_xn%d" % i, [128, D], BF16, es) for i in range(2)]
        r_xn = [R() for _ in range(2)]
        junk = k.sb(pfx + "_junk", [128, D], BF16, es)
        r_junk = R()
        st = k.sb(pfx + "_st", [128, NT, 2], F32, es)
        r_stl = [R() for _ in range(NT)]
        pt = [k.ps(pfx + "_pt%d" % i, [128, 8, 128], BF16, es) for i in range(2)]
        r_pt = [PR() for _ in range(2)]
        for t in range(self.dbg.get("NZ_NT", NT)):
            row = 1 if t < 2 else 0
            r_st = r_stl[t]
            x_, rx = xt[t % 2], r_xt[t % 2]
            n_, rn = xn[t % 2], r_xn[t % 2]
            STEP = self.dbg.get("STEP", 99)
            if STEP < 1:
                continue
            k.dma(x_[:], xin[t * 128:(t + 1) * 128, :], reads=[r_xin], writes=[rx])
            if STEP < 2:
                continue
            k.op("scalar", lambda e, x_=x_, t=t: e.activation(out=junk[:], in_=x_[:], func=AF.Square, accum_out=st[:, t, 0:1]),
                 reads=[rx], writes=[r_junk, r_st])
            if STEP < 3:
                continue
            k.op("scalar", lambda e, t=t: e.activation(out=st[:, t, 1:2], in_=st[:, t, 0:1], func=AF.Sqrt, bias=self.eps_t[:, 0:1], scale=1.0 / D),
                 reads=[r_st], writes=[r_st])
            if STEP < 4:
                continue
            k.op("vector", lambda e, t=t: e.reciprocal(out=st[:, t, 1:2], in_=st[:, t, 1:2]), reads=[r_st], writes=[r_st])
            if STEP < 5:
                continue
            k.op("vector", lambda e, x_=x_, n_=n_, t=t: e.tensor_scalar(out=n_[:], in0=x_[:], scalar1=st[:, t, 1:2], scalar2=None, op0=ALU.mult),
                 reads=[rx, r_st], writes=[rn])
            if STEP < 6:
                continue
            for half in range(2):
                p, rp = pt[half], r_pt[half]

                def tr(e, p=p, n_=n_, half=half):
                    last = None
                    for j in range(8):
                        dc = half * 8 + j
                        last = e.transpose(out=p[:, j, :], in_=n_[:, dc * 128:(dc + 1) * 128], identity=self.ident[:])
                    return last
                k.op("tensor", tr, reads=[rn, self.r_ident], writes=[rp])
                if STEP < 7:
                    continue
                for j in range(8):
                    dc = half * 8 + j
                    if STEP == 7 and j % 2 == 1:
                        continue
                    if STEP == 8 and j % 2 == 0:
                        continue
                    eng = "vector" if half == 0 else "scalar"
                    if eng == "vector":
                        k.op("vector", lambda e, p=p, j=j, dc=dc, t=t, row=row: e.tensor_scalar(
                            out=hT[:, dc, t * 128:(t + 1) * 128], in0=p[:, j, :], scalar1=aT[:, dc, row:row + 1],
                            scalar2=shT[:, dc, row:row + 1], op0=ALU.mult, op1=ALU.add),
                            reads=[rp, r_aT, r_shT], writes=[r_hT[t][dc]])
                    else:
                        k.op("scalar", lambda e, p=p, j=j, dc=dc, t=t, row=row: e.activation(
                            out=hT[:, dc, t * 128:(t + 1) * 128], in_=p[:, j, :], func=AF.Identity,
                            scale=aT[:, dc, row:row + 1], bias=shT[:, dc, row:row + 1]),
                            reads=[rp, r_aT, r_shT], writes=[r_hT[t][dc]])

    def head_rms(self, eng_sq, x_, w, nh, dh, st, junk, out, gain_bc, reads, r_st, r_junk, writes, tmp, r_tmp):
        k = self.k
        x3 = x_.rearrange("p (h d) -> p h d", h=nh)
        k.op("gpsimd", lambda e: e.tensor_tensor(out=tmp, in0=x_, in1=gain_bc, op=ALU.mult), reads=reads + [self.r_gain], writes=[r_tmp])
        k.op("vector", lambda e: e.tensor_tensor(out=junk, in0=x_, in1=x_, op=ALU.mult), reads=reads, writes=[r_junk])
        k.op("vector", lambda e: e.tensor_reduce(out=st, in_=junk.rearrange("p (h d) -> p h d", h=nh), axis=AX.X, op=ALU.add),
             reads=[r_junk], writes=[r_st])
        k.op("scalar", lambda e: e.activation(out=st, in_=st, func=AF.Sqrt, bias=self.eps_t[:, 0:1], scale=1.0 / dh),
             reads=[r_st], writes=[r_st])
        k.op("vector", lambda e: e.reciprocal(out=st, in_=st), reads=[r_st], writes=[r_st])
        k.op("vector", lambda e: e.tensor_tensor(out=out.rearrange("p (h d) -> p h d", h=nh), in0=tmp.rearrange("p (h d) -> p h d", h=nh),
                                                in1=st.unsqueeze(2).to_broadcast([128, nh, dh]), op=ALU.mult),
             reads=[r_tmp, r_st], writes=writes)

    def stage_na(self, l):
        k = self.k
        z, _ = self.dr["z%d" % l]
        r_zt = self.r_ztile
        yd, r_yd = self.dram("y_na%d" % l, [T, 512], F32)
        gq_d, r_gq = self.dr["na_qg"]
        gk_d, r_gk = self.dr["na_kg"]
        bias_d, r_bias_d = self.dr["na_bias"]
        with ExitStack() as es:
            QT = k.sb("na_QT", [128, 4, T], BF16, es)
            KT = k.sb("na_KT", [128, 4, T], BF16, es)
            r_QT = [R() for _ in range(NT)]
            r_KT = [R() for _ in range(NT)]
            Va = k.sb("na_V", [128, NT, 8, 65], BF16, es)
            r_Va = [R() for _ in range(NT)]
            yna = k.sb("na_y", [128, NT, 512], F32, es)
            r_yna = [R() for _ in range(NT)]
            gq = k.sb("na_gq", [128, 512], F32, es)
            gk = k.sb("na_gk", [128, 512], F32, es)
            self.r_gain = R()
            k.dma(gq[:], gq_d[l].partition_broadcast(128), reads=[r_gq], writes=[self.r_gain])
            k.dma(gk[:], gk_d[l].partition_broadcast(128), reads=[r_gk], writes=[self.r_gain])
            k.op("vector", lambda e: e.tensor_scalar(out=gq[:], in0=gq[:], scalar1=0.125, scalar2=None, op0=ALU.mult),
                 reads=[self.r_gain], writes=[self.r_gain])
            r_ones = R()
            k.op("gpsimd", lambda e: e.memset(Va[:, :, :, 64:65], 1.0), writes=r_Va)
            xin = [k.sb("na_x%d" % i, [128, 512], F32, es) for i in range(6)]
            r_xin = [R() for _ in range(6)]
            junk = k.sb("na_junk", [128, 512], F32, es)
            r_junk = R()
            tmp = k.sb("na_tmp", [128, 512], F32, es)
            r_tmp = R()
            st = k.sb("na_st", [128, NT, 2, 8], F32, es)
            xb = [k.sb("na_xb%d" % i, [128, 512], BF16, es) for i in range(2)]
            r_xb = [R() for _ in range(2)]
            ptr = [k.ps("na_ptr%d" % i, [128, 8, 128], BF16, es) for i in range(1)]
            r_ptr = [PR() for _ in range(1)]
            ld = 0
            nb = 0
            for t in range(NT):
                rows = slice(t * 128, (t + 1) * 128)
                for which, c0 in ((0, C_NAQ), (1, C_NAK)):
                    x_, rx = xin[ld % 6], r_xin[ld % 6]
                    ld += 1
                    k.dma(x_[:], z[rows, c0:c0 + 512], reads=[r_zt[t]], writes=[rx])
                    b_, rb = xb[nb % 2], r_xb[nb % 2]
                    nb += 1
                    r_st = R()
                    self.head_rms(None, x_[:], 512, 8, 64, st[:, t, which, :], junk[:], b_[:], (gq if which == 0 else gk)[:],
                                  [rx], r_st, r_junk, [rb], tmp[:], r_tmp)
                    p, rp = ptr[0], r_ptr[0]

                    def tr(e, p=p, b_=b_):
                        last = None
                        for j in range(4):
                            last = e.transpose(out=p[:, j, :], in_=b_[:, j * 128:(j + 1) * 128], identity=self.ident[:])
                        return last
                    k.op("tensor", tr, reads=[rb, self.r_ident], writes=[rp])
                    dst = QT if which == 0 else KT
                    rd = (r_QT if which == 0 else r_KT)[t]
                    k.op("scalar", lambda e, dst=dst, p=p, rows=rows: e.activation(out=dst[:, :, rows], in_=p[:, 0:4, :], func=AF.Copy),
                         reads=[rp], writes=[rd])
                x_, rx = xin[ld % 6], r_xin[ld % 6]
                ld += 1
                k.dma(x_[:], z[rows, C_NAV:C_NAV + 512], reads=[r_zt[t]], writes=[rx])
                k.op("vector", lambda e, x_=x_, t=t: e.tensor_copy(out=Va[:, t, :, 0:64], in_=x_[:].rearrange("p (h d) -> p h d", h=8)),
                     reads=[rx], writes=[r_Va[t]])
            bias = [k.sb("na_bias%d" % i, [128, 5, 5, 128], F32, es) for i in range(2)]
            r_bias = [R() for _ in range(2)]
            stp = [k.ps("na_st%d" % i, [128, 8, 128], F32, es) for i in range(2)]
            r_stp = [PR() for _ in range(2)]
            po = [k.ps("na_po%d" % i, [128, 512], F32, es) for i in range(2)]
            r_po = [PR() for _ in range(2)]
            sbt = [k.sb("na_sb%d" % i, [128, 5, 128], F32, es) for i in range(2)]
            r_sbt = [R() for _ in range(2)]
            PT = [k.sb("na_PT%d" % i, [128, 7, 128], BF16, es) for i in range(2)]
            r_PT = [R() for _ in range(2)]
            rc = k.sb("na_rc", [128, 2], F32, es)
            r_rc = [R(), R()]
            it = 0
            for h in range(8):
                bs, rbs = bias[h % 2], r_bias[h % 2]
                for c in range(5):
                    k.dma(bs[:, c, :, :], bias_d[l, h, c].rearrange("t k q -> k t q"), reads=[r_bias_d], writes=[rbs])
                hp, hs = h // 2, (h % 2) * 64
                for t in range(NT):
                    if t < 2:
                        kts = [0, 1]
                        nbias = 0
                        cls = 0
                    else:
                        j = t - 2
                        lo = min(max(j - 2, 0), 11)
                        kts = [lo + 2 + i for i in range(5)] + [0, 1]
                        nbias = 5
                        cls = {0: 0, 1: 1, 14: 3, 15: 4}.get(j, 2)
                    nk = len(kts)
                    sp, rsp = stp[it % 2], r_stp[it % 2]
                    pp, rpp = PT[it % 2], r_PT[it % 2]
                    sb_, rsb = sbt[it % 2], r_sbt[it % 2]
                    o_, ro = po[it % 2], r_po[it % 2]
                    rcc, rrc = rc[:, it % 2:it % 2 + 1], r_rc[it % 2]
                    it += 1

                    def qk(e, sp=sp, kts=kts, t=t, hp=hp, hs=hs):
                        last = None
                        for i, kt in enumerate(kts):
                            last = e.matmul(sp[:, i, :], lhsT=KT[hs:hs + 64, hp, kt * 128:(kt + 1) * 128],
                                            rhs=QT[hs:hs + 64, hp, t * 128:(t + 1) * 128], start=True, stop=True)
                        return last
                    k.op("tensor", qk, reads=[r_KT[kt] for kt in kts] + [r_QT[t]], writes=[rsp])
                    if nbias:
                        k.op("vector", lambda e, sb_=sb_, sp=sp, bs=bs, cls=cls: e.tensor_tensor(out=sb_[:], in0=sp[:, 0:5, :], in1=bs[:, cls, :, :], op=ALU.add),
                             reads=[rsp, rbs], writes=[rsb])
                        k.op("scalar", lambda e, pp=pp, sb_=sb_: e.activation(out=pp[:, 0:5, :], in_=sb_[:], func=AF.Exp), reads=[rsb], writes=[rpp])
                        k.op("scalar", lambda e, pp=pp, sp=sp: e.activation(out=pp[:, 5:7, :], in_=sp[:, 5:7, :], func=AF.Exp), reads=[rsp, rpp], writes=[rpp])
                    else:
                        k.op("scalar", lambda e, pp=pp, sp=sp: e.activation(out=pp[:, 0:2, :], in_=sp[:, 0:2, :], func=AF.Exp), reads=[rsp], writes=[rpp])

                    def pv(e, o_=o_, pp=pp, kts=kts, h=h, nk=nk):
                        last = None
                        for i, kt in enumerate(kts):
                            last = e.matmul(o_[:, 0:65], lhsT=pp[:, i, :], rhs=Va[:, kt, h, :], start=(i == 0), stop=(i == nk - 1))
                        return last
                    k.op("tensor", pv, reads=[rpp] + [r_Va[kt] for kt in kts], writes=[ro])
                    k.op("vector", lambda e, rcc=rcc, o_=o_: e.reciprocal(out=rcc, in_=o_[:, 64:65]), reads=[ro], writes=[rrc])
                    k.op("vector", lambda e, rcc=rcc, o_=o_, t=t, h=h: e.tensor_scalar(out=yna[:, t, h * 64:(h + 1) * 64], in0=o_[:, 0:64],
                                                                                   scalar1=rcc, scalar2=None, op0=ALU.mult),
                         reads=[ro, rrc], writes=[r_yna[t]])
            for t in range(NT):
                k.dma(yd[t * 128:(t + 1) * 128, :], yna[:, t, :], reads=[r_yna[t]], writes=[r_yd], q="gpsimd")
            k.barrier()

    def stage_mla(self, l):
        k = self.k
        z, _ = self.dr["z%d" % l]
        r_zt = self.r_ztile
        yd, r_yd = self.dram("y_mla%d" % l, [T, 512], F32)
        MSC = 192.0 ** -0.5
        with ExitStack() as es:
            QTn = k.sb("ml_QTn", [128, 4, T], BF16, es)
            KTn = k.sb("ml_KTn", [128, 4, T], BF16, es)
            QTp = k.sb("ml_QTp", [128, 2, T], BF16, es)
            KTp = k.sb("ml_KTp", [128, 2, T], BF16, es)
            r_Q = [R() for _ in range(NT)]
            r_K = [R() for _ in range(NT)]
            Va = k.sb("ml_V", [128, NT, 4, 129], BF16, es)
            r_Va = [R() for _ in range(NT)]
            ym = k.sb("ml_y", [128, NT, 512], F32, es)
            r_ym = [R() for _ in range(NT)]
            k.op("gpsimd", lambda e: e.memset(Va[:, :, :, 128:129], 1.0), writes=r_Va)
            with ExitStack() as es2:
                self.r_gain = R()
                g_cq = k.sb("ml_gcq", [128, 512], F32, es2)
                g_ckv = k.sb("ml_gckv", [128, 256], F32, es2)
                g_q = k.sb("ml_gq", [128, 768], F32, es2)
                g_kn = k.sb("ml_gkn", [128, 512], F32, es2)
                g_kp = k.sb("ml_gkp", [128, 64], F32, es2)
                for tl, nm in ((g_cq, "mla_cqg"), (g_ckv, "mla_ckvg"), (g_q, "mla_qg"), (g_kn, "mla_kng"), (g_kp, "mla_kpg")):
                    ap, rr = self.dr[nm]
                    k.dma(tl[:], ap[l].partition_broadcast(128), reads=[rr], writes=[self.r_gain])
                k.op("vector", lambda e: e.tensor_scalar(out=g_q[:], in0=g_q[:], scalar1=MSC, scalar2=None, op0=ALU.mult),
                     reads=[self.r_gain], writes=[self.r_gain])
                wuq = k.sb("ml_wuq", [128, 4, 768], BF16, es2)
                wukv = k.sb("ml_wukv", [128, 2, 1024], BF16, es2)
                r_w = R()
                wst = k.sb("ml_wst", [128, 4, 768], F32, es2)
                r_wst = R()
                ap, rr = self.dr["mla_w_uq"]
                k.dma(wst[:], ap[l].rearrange("(kc p) n -> p kc n", p=128), reads=[rr], writes=[r_wst])
                k.op("vector", lambda e: e.tensor_copy(out=wuq[:], in_=wst[:]), reads=[r_wst], writes=[r_w])
                ap, rr = self.dr["mla_w_ukv"]
                wst2 = wst[:].rearrange("p a b -> p (a b)")[:, 0:2048].rearrange("p (a b) -> p a b", a=2)
                k.dma(wst2, ap[l].rearrange("(kc p) n -> p kc n", p=128), reads=[rr], writes=[r_wst])
                k.op("vector", lambda e: e.tensor_copy(out=wukv[:], in_=wst2), reads=[r_wst], writes=[r_w])
                cos_d, r_cos = self.dr["rope_cos"]
                sin_d, r_sin = self.dr["rope_sin"]
                xin = [k.sb("ml_x%d" % i, [128, 832], F32, es2) for i in range(2)]
                r_xin = [R() for _ in range(2)]
                cs = [k.sb("ml_cs%d" % i, [128, 2, 64], F32, es2) for i in range(2)]
                r_cs = [R() for _ in range(2)]
                junk = k.sb("ml_junk", [128, 1024], F32, es2)
                r_junk = R()
                tmp = k.sb("ml_tmp", [128, 1024], F32, es2)
                r_tmp = R()
                st = k.sb("ml_st", [128, NT, 16], F32, es2)
                cb = k.sb("ml_cb", [128, 768], BF16, es2)
                r_cb = [R(), R()]
                cT = k.sb("ml_cT", [128, 6, 128], BF16, es2)
                r_cT = [R(), R()]
                qf = k.sb("ml_qf", [128, 768], F32, es2)
                r_qf = R()
                qn = k.sb("ml_qn", [128, 768], F32, es2)
                r_qn = R()
                kvf = k.sb("ml_kvf", [128, 1024], F32, es2)
                r_kvf = R()
                qb = k.sb("ml_qb", [128, 4, 128], BF16, es2)
                qpb = k.sb("ml_qpb", [128, 4, 64], BF16, es2)
                kb = k.sb("ml_kb", [128, 4, 128], BF16, es2)
                kpb = k.sb("ml_kpb", [128, 4, 64], BF16, es2)
                r_qb, r_qpb, r_kb, r_kpb = R(), R(), R(), R()
                rp1 = k.sb("ml_rp1", [128, 4, 64], F32, es2)
                rp2 = k.sb("ml_rp2", [128, 4, 64], F32, es2)
                r_rp1, r_rp2 = R(), R()
                kpg = k.sb("ml_kpgt", [128, 64], F32, es2)
                kpr = k.sb("ml_kpr", [128, 64], F32, es2)
                r_kpg, r_kpr = R(), R()
                ptr = k.ps("ml_ptr", [128, 8, 128], BF16, es2)
                r_ptr = PR()
                pq = k.ps("ml_pq", [128, 2, 512], F32, es2)
                r_pq = PR()
                pkv = k.ps("ml_pkv", [128, 2, 512], F32, es2)
                r_pkv = PR()

                def rope(x3, nh, cs_, out3, reads, writes):
                    cosb = cs_[:, 0, :].unsqueeze(1).to_broadcast([128, nh, 64])
                    k.op("vector", lambda e: e.tensor_tensor(out=rp1[:, 0:nh, :], in0=x3, in1=cosb, op=ALU.mult), reads=reads, writes=[r_rp1])
                    x5 = x3.rearrange("p h (g a d) -> p h g a d", g=2, a=2)
                    o5 = rp2[:, 0:nh, :].rearrange("p h (g a d) -> p h g a d", g=2, a=2)
                    s4 = cs_[:, 1, :].rearrange("p (g a d) -> p g a d", g=2, a=2)
                    for a in range(2):
                        k.op("gpsimd", lambda e, a=a: e.tensor_tensor(out=o5[:, :, :, a, :], in0=x5[:, :, :, 1 - a, :],
                                                                      in1=s4[:, :, a, :].unsqueeze(1).to_broadcast([128, nh, 2, 16]), op=ALU.mult),
                             reads=reads, writes=[r_rp2])
                    k.op("vector", lambda e: e.tensor_tensor(out=out3, in0=rp1[:, 0:nh, :], in1=rp2[:, 0:nh, :], op=ALU.add),
                         reads=[r_rp1, r_rp2], writes=writes)

                for t in range(NT):
                    rows = slice(t * 128, (t + 1) * 128)
                    x_, rx = xin[t % 2], r_xin[t % 2]
                    cs_, rcs = cs[t % 2], r_cs[t % 2]
                    k.dma(x_[:, 0:512], z[rows, C_CQ:C_CQ + 512], reads=[r_zt[t]], writes=[rx])
                    k.dma(x_[:, 512:832], z[rows, C_CKV:C_CKV + 320], reads=[r_zt[t]], writes=[rx])
                    k.dma(cs_[:, 0, :], cos_d[rows, :], reads=[r_cos], writes=[rcs])
                    k.dma(cs_[:, 1, :], sin_d[rows, :], reads=[r_sin], writes=[rcs])
                    self.head_rms(None, x_[:, 0:512], 512, 1, 512, st[:, t, 0:1], junk[:, 0:512], cb[:, 0:512], g_cq[:],
                                  [rx], R(), r_junk, [r_cb[0]], tmp[:, 0:512], r_tmp)
                    self.head_rms(None, x_[:, 512:768], 256, 1, 256, st[:, t, 1:2], junk[:, 0:256], cb[:, 512:768], g_ckv[:],
                                  [rx], R(), r_junk, [r_cb[1]], tmp[:, 0:256], r_tmp)

                    def tr1(e):
                        last = None
                        for j in range(6):
                            last = e.transpose(out=ptr[:, j, :], in_=cb[:, j * 128:(j + 1) * 128], identity=self.ident[:])
                        return last
                    k.op("tensor", tr1, reads=r_cb + [self.r_ident], writes=[r_ptr])
                    k.op("scalar", lambda e: e.activation(out=cT[:], in_=ptr[:, 0:6, :], func=AF.Copy), reads=[r_ptr], writes=r_cT)

                    def mmq(e):
                        last = None
                        for hf in range(2):
                            for kc in range(4):
                                last = e.matmul(pq[:, hf, 0:384], lhsT=cT[:, kc, :], rhs=wuq[:, kc, hf * 384:(hf + 1) * 384],
                                                start=(kc == 0), stop=(kc == 3))
                        return last
                    k.op("tensor", mmq, reads=r_cT + [r_w], writes=[r_pq])

                    def mmkv(e):
                        last = None
                        for hf in range(2):
                            for kc in range(2):
                                last = e.matmul(pkv[:, hf, :], lhsT=cT[:, 4 + kc, :], rhs=wukv[:, kc, hf * 512:(hf + 1) * 512],
                                                start=(kc == 0), stop=(kc == 1))
                        return last
                    k.op("tensor", mmkv, reads=r_cT + [r_w], writes=[r_pkv])
                    k.op("scalar", lambda e: e.activation(out=qf[:].rearrange("p (a b) -> p a b", a=2), in_=pq[:, :, 0:384], func=AF.Copy),
                         reads=[r_pq], writes=[r_qf])
                    k.op("vector", lambda e: e.tensor_copy(out=kvf[:].rearrange("p (a b) -> p a b", a=2), in_=pkv[:]), reads=[r_pkv], writes=[r_kvf])
                    self.head_rms(None, qf[:], 768, 4, 192, st[:, t, 2:6], junk[:, 0:768], qn[:], g_q[:], [r_qf], R(), r_junk, [r_qn],
                                  tmp[:, 0:768], r_tmp)
                    qn3 = qn[:].rearrange("p (h d) -> p h d", h=4)
                    k.op("vector", lambda e: e.tensor_copy(out=qb[:], in_=qn3[:, :, 0:128]), reads=[r_qn], writes=[r_qb])
                    rope(qn3[:, :, 128:192], 4, cs_, qpb[:], [r_qn, rcs], [r_qpb])
                    kv4 = kvf[:].rearrange("p (h s d) -> p h s d", h=4, s=2)
                    k.op("vector", lambda e, t=t: e.tensor_copy(out=Va[:, t, :, 0:128], in_=kv4[:, :, 1, :]), reads=[r_kvf], writes=[r_Va[t]])
                    kpe = x_[:, 768:832]
                    r_sk = R()
                    k.op("vector", lambda e: e.tensor_tensor(out=junk[:, 0:512].rearrange("p (h d) -> p h d", h=4), in0=kv4[:, :, 0, :], in1=kv4[:, :, 0, :], op=ALU.mult),
                         reads=[r_kvf], writes=[r_junk])
                    k.op("vector", lambda e, t=t: e.tensor_reduce(out=st[:, t, 6:10], in_=junk[:, 0:512].rearrange("p (h d) -> p h d", h=4), axis=AX.X, op=ALU.add),
                         reads=[r_junk], writes=[r_sk])
                    k.op("vector", lambda e: e.tensor_tensor(out=junk[:, 512:576], in0=kpe, in1=kpe, op=ALU.mult), reads=[rx], writes=[r_junk])
                    k.op("vector", lambda e, t=t: e.tensor_reduce(out=st[:, t, 10:11], in_=junk[:, 512:576], axis=AX.X, op=ALU.add), reads=[r_junk], writes=[r_sk])
                    k.op("vector", lambda e, t=t: e.tensor_scalar(out=st[:, t, 6:10], in0=st[:, t, 6:10], scalar1=st[:, t, 10:11], scalar2=None, op0=ALU.add),
                         reads=[r_sk], writes=[r_sk])
                    k.op("scalar", lambda e, t=t: e.activation(out=st[:, t, 6:10], in_=st[:, t, 6:10], func=AF.Sqrt, bias=self.eps_t[:, 0:1], scale=1.0 / 192),
                         reads=[r_sk], writes=[r_sk])
                    k.op("vector", lambda e, t=t: e.reciprocal(out=st[:, t, 6:10], in_=st[:, t, 6:10]), reads=[r_sk], writes=[r_sk])
                    k.op("vector", lambda e, t=t: e.tensor_tensor(out=tmp[:, 0:512].rearrange("p (h d) -> p h d", h=4), in0=kv4[:, :, 0, :],
                                                                 in1=st[:, t, 6:10].unsqueeze(2).to_broadcast([128, 4, 128]), op=ALU.mult),
                         reads=[r_kvf, r_sk], writes=[r_tmp])
                    k.op("gpsimd", lambda e: e.tensor_tensor(out=kb[:].rearrange("p h d -> p (h d)"), in0=tmp[:, 0:512], in1=g_kn[:], op=ALU.mult),
                         reads=[r_tmp, self.r_gain], writes=[r_kb])
                    k.op("vector", lambda e: e.tensor_tensor(out=kpg[:], in0=kpe, in1=g_kp[:], op=ALU.mult), reads=[rx, self.r_gain], writes=[r_kpg])
                    rope(kpg[:].unsqueeze(1), 1, cs_, kpr[:].unsqueeze(1), [r_kpg, rcs], [r_kpr])
                    k.op("vector", lambda e, t=t: e.tensor_tensor(out=kpb[:], in0=kpr[:].unsqueeze(1).to_broadcast([128, 4, 64]),
                                                                 in1=st[:, t, 6:10].unsqueeze(2).to_broadcast([128, 4, 64]), op=ALU.mult),
                         reads=[r_kpr, r_sk], writes=[r_kpb])
                    for (nb_, pb_, rnb, rpb_, dn, dp, rd) in ((qb, qpb, r_qb, r_qpb, QTn, QTp, r_Q[t]), (kb, kpb, r_kb, r_kpb, KTn, KTp, r_K[t])):
                        def tr2(e, nb_=nb_, pb_=pb_):
                            last = None
                            for h in range(4):
                                last = e.transpose(out=ptr[:, h, :], in_=nb_[:, h, :], identity=self.ident[:])
                            for i in range(2):
                                last = e.transpose(out=ptr[:, 4 + i, :], in_=pb_[:, 2 * i:2 * i + 2, :].rearrange("p a b -> p (a b)"), identity=self.ident[:])
                            return last
                        k.op("tensor", tr2, reads=[rnb, rpb_, self.r_ident], writes=[r_ptr])
                        k.op("scalar", lambda e, dn=dn, rows=rows: e.activation(out=dn[:, :, rows], in_=ptr[:, 0:4, :], func=AF.Copy), reads=[r_ptr], writes=[rd])
                        k.op("vector", lambda e, dp=dp, rows=rows: e.tensor_copy(out=dp[:, :, rows], in_=ptr[:, 4:6, :]), reads=[r_ptr], writes=[rd])
                k.barrier()
            with ExitStack() as es3:
                stp = [k.ps("ml_st%d" % i, [128, 512], F32, es3) for i in range(2)]
                r_stp = [PR() for _ in range(2)]
                acc = [k.ps("ml_acc%d" % i, [128, 512], F32, es3) for i in range(4)]
                r_acc = [PR() for _ in range(4)]
                PT = [k.sb("ml_PT%d" % i, [128, 512], BF16, es3) for i in range(3)]
                r_PT = [R() for _ in range(3)]
                rc = k.sb("ml_rc", [128, 4], F32, es3)
                r_rc = [R() for _ in range(4)]
                groups = [[0, 1]] + [[2 + 4 * g + i for i in range(4)] for g in range(4)]
                it = 0
                for grp in groups:
                    kts = [0, 1] if grp[0] == 0 else list(range(NT))
                    q0 = grp[0] * 128
                    nq = len(grp) * 128
                    for h in range(4):
                        hp, hs = h // 2, (h % 2) * 64
                        def emit_qk_exp(ki, kt, it_):
                            sp, rsp = stp[it_ % 2], r_stp[it_ % 2]
                            pp, rpp = PT[it_ % 3], r_PT[it_ % 3]

                            def qk(e, sp=sp, kt=kt):
                                e.matmul(sp[:, 0:nq], lhsT=KTn[:, h, kt * 128:(kt + 1) * 128], rhs=QTn[:, h, q0:q0 + nq], start=True, stop=False)
                                return e.matmul(sp[:, 0:nq], lhsT=KTp[hs:hs + 64, hp, kt * 128:(kt + 1) * 128], rhs=QTp[hs:hs + 64, hp, q0:q0 + nq],
                                                start=False, stop=True)
                            k.op("tensor", qk, reads=[r_K[kt]] + [r_Q[t] for t in grp], writes=[rsp])
                            k.op("scalar", lambda e, pp=pp, sp=sp: e.activation(out=pp[:, 0:nq], in_=sp[:, 0:nq], func=AF.Exp), reads=[rsp], writes=[rpp])

                        def emit_pv(ki, kt, it_):
                            pp, rpp = PT[it_ % 3], r_PT[it_ % 3]
                            for i, t in enumerate(grp):
                                k.op("tensor", lambda e, i=i, pp=pp, kt=kt, ki=ki: e.matmul(
                                    acc[i][:, 0:129], lhsT=pp[:, i * 128:(i + 1) * 128], rhs=Va[:, kt, h, :], start=(ki == 0), stop=(ki == len(kts) - 1)),
                                    reads=[rpp, r_Va[kt]], writes=[r_acc[i]])
                        emit_qk_exp(0, kts[0], it)
                        for ki, kt in enumerate(kts):
                            if ki + 1 < len(kts):
                                emit_qk_exp(ki + 1, kts[ki + 1], it + ki + 1)
                            emit_pv(ki, kt, it + ki)
                        it += len(kts)
                        for i, t in enumerate(grp):
                            k.op("vector", lambda e, i=i: e.reciprocal(out=rc[:, i:i + 1], in_=acc[i][:, 128:129]), reads=[r_acc[i]], writes=[r_rc[i]])
                            k.op("vector", lambda e, i=i, t=t, h=h: e.tensor_scalar(out=ym[:, t, h * 128:(h + 1) * 128], in0=acc[i][:, 0:128],
                                                                                  scalar1=rc[:, i:i + 1], scalar2=None, op0=ALU.mult),
                                 reads=[r_acc[i], r_rc[i]], writes=[r_ym[t]])
                for t in range(NT):
                    k.dma(yd[t * 128:(t + 1) * 128, :], ym[:, t, :], reads=[r_ym[t]], writes=[r_yd])
                k.barrier()

    def stage_fourier(self, l):
        k = self.k
        z, _ = self.dr["z%d" % l]
        r_zt = self.r_ztile
        yd, r_yd = self.dram("y_fn%d" % l, [T, 512], F32)
        csd, r_csd = self.dr["dft_c"]
        dL, r_dL = self.dr["dft_L"]
        dC, r_dC = self.dr["dft_C"]
        with ExitStack() as es:
            G = k.sb("fn_G", [128, NT, 2, 512], BF16, es)
            r_G = [R() for _ in range(NT)]
            CS = k.sb("fn_CS", [128, 256], BF16, es)
            r_CS = R()
            k.dma(CS[:], csd, reads=[r_csd], writes=[r_CS])
            xin = [k.sb("fn_x%d" % i, [128, 512], F32, es) for i in range(2)]
            r_xin = [R() for _ in range(2)]
            xb = [k.sb("fn_xb%d" % i, [128, 512], BF16, es) for i in range(2)]
            r_xb = [R() for _ in range(2)]
            xT = [k.sb("fn_xT%d" % i, [128, 4, 128], BF16, es) for i in range(2)]
            r_xT = [R() for _ in range(2)]
            ptr = k.ps("fn_ptr", [128, 8, 128], BF16, es)
            r_ptr = PR()
            pg = [k.ps("fn_pg%d" % i, [128, 2, 256], F32, es) for i in range(2)]
            r_pg = [PR() for _ in range(2)]
            py = [k.ps("fn_py%d" % i, [128, 512], F32, es) for i in range(2)]
            r_py = [PR() for _ in range(2)]
            for t in range(NT):
                x_, rx = xin[t % 2], r_xin[t % 2]
                b_, rb = xb[t % 2], r_xb[t % 2]
                T_, rT = xT[t % 2], r_xT[t % 2]
                k.dma(x_[:], z[t * 128:(t + 1) * 128, C_FN:C_FN + 512], reads=[r_zt[t]], writes=[rx])
                k.op("gpsimd", lambda e, x_=x_, b_=b_: e.tensor_copy(out=b_[:], in_=x_[:]), reads=[rx], writes=[rb])

                def tr(e, b_=b_):
                    last = None
                    for g in range(4):
                        last = e.transpose(out=ptr[:, g, :], in_=b_[:, g * 128:(g + 1) * 128], identity=self.ident[:])
                    return last
                k.op("tensor", tr, reads=[rb, self.r_ident], writes=[r_ptr])
                k.op("scalar", lambda e, T_=T_: e.activation(out=T_[:], in_=ptr[:, 0:4, :], func=AF.Copy), reads=[r_ptr], writes=[rT])
                for hf in range(2):
                    p, rp = pg[hf], r_pg[hf]

                    def mm(e, p=p, T_=T_, hf=hf):
                        last = None
                        for gg in range(2):
                            last = e.matmul(p[:, gg, :], lhsT=T_[:, hf * 2 + gg, :], rhs=CS[:], start=True, stop=True)
                        return last
                    k.op("tensor", mm, reads=[rT, r_CS], writes=[rp])
                    G4 = G[:, t, :, :].rearrange("p c (g d) -> p c g d", g=4)
                    if hf == 0:
                        k.op("vector", lambda e, p=p, G4=G4, hf=hf: e.tensor_copy(out=G4[:, :, 0:2, :].rearrange("p c g d -> p g c d"),
                                                                                in_=p[:].rearrange("p g (c d) -> p g c d", c=2)),
                             reads=[rp], writes=[r_G[t]])
                    else:
                        k.op("scalar", lambda e, p=p, G4=G4, hf=hf: e.activation(out=G4[:, :, 2:4, :].rearrange("p c g d -> p g c d"),
                                                                                in_=p[:].rearrange("p g (c d) -> p g c d", c=2), func=AF.Copy),
                             reads=[rp, r_G[t]], writes=[r_G[t]])
            Dm = [k.sb("fn_D%d" % i, [128, 2, 16, 128], BF16, es) for i in range(2)]
            r_Dm = [R() for _ in range(2)]
            yo = [k.sb("fn_yo%d" % i, [128, 512], F32, es) for i in range(2)]
            r_yo = [R() for _ in range(2)]
            for to in range(NT):
                ctx = to < 2
                nin = 2 if ctx else 16
                t0 = 0 if ctx else 2
                src = dC if ctx else dL
                rsrc = r_dC if ctx else r_dL
                col0 = (to - t0) * 128
                D_, rD = Dm[to % 2], r_Dm[to % 2]
                for c in range(2):
                    k.dma(D_[:, c, 0:nin, :], src[c, :, col0:col0 + 128].rearrange("(nt p) q -> p nt q", p=128), reads=[rsrc], writes=[rD])
                p, rp = py[to % 2], r_py[to % 2]

                def mm2(e, p=p, D_=D_, nin=nin, t0=t0):
                    last = None
                    n = 0
                    for nt in range(nin):
                        for c in range(2):
                            last = e.matmul(p[:], lhsT=D_[:, c, nt, :], rhs=G[:, t0 + nt, c, :], start=(n == 0), stop=(n == 2 * nin - 1))
                            n += 1
                    return last
                k.op("tensor", mm2, reads=[rD] + [r_G[t0 + nt] for nt in range(nin)], writes=[rp])
                o_, ro = yo[to % 2], r_yo[to % 2]
                k.op("vector", lambda e, o_=o_, p=p: e.tensor_copy(out=o_[:], in_=p[:]), reads=[rp], writes=[ro])
                k.dma(yd[to * 128:(to + 1) * 128, :], o_[:], reads=[ro], writes=[r_yd])
            k.barrier()

    RWC = 3072

    def stage_rwprep(self, l):
        k = self.k
        z, _ = self.dr["z%d" % l]
        r_zt = self.r_ztile
        rwd, r_rwd = self.dram("rwd%d" % l, [2, T, self.RWC], F32)
        rwo, r_rwo = self.dram("rwo%d" % l, [T, 608], F32)
        self.r_rwd_t = [R() for _ in range(NT)]
        with ExitStack() as es:
            def bc(nm, n):
                tl = k.sb("rp_" + nm, [128, n], F32, es)
                ap, rr = self.dr[nm]
                k.dma(tl[:], ap[l].partition_broadcast(128), reads=[rr], writes=[r_c])
                return tl
            r_c = R()
            mu = bc("rw_mu", 1760)
            kkg = bc("rw_kk", 512)
            kag = bc("rw_ka", 512)
            w0 = bc("rw_w0f", 1024)
            a0 = bc("rw_a0f", 1024)
            w2 = k.sb("rp_w2", [32, 2, 2, 512], F32, es)
            ap, rr = self.dr["rw_wa2"]
            k.dma(w2[:], ap[l], reads=[rr], writes=[r_c])
            cur = [k.sb("rp_cur%d" % i, [128, 1760], F32, es) for i in range(2)]
            prv = [k.sb("rp_prv%d" % i, [128, 1760], F32, es) for i in range(2)]
            nxt = [k.sb("rp_nxt%d" % i, [128, 1760], F32, es) for i in range(2)]
            r_cur = [R() for _ in range(2)]
            r_prv = [R() for _ in range(2)]
            r_nxt = [R() for _ in range(2)]
            mx = k.sb("rp_mx", [128, 1760], F32, es)
            r_mx = R()
            d_ = k.sb("rp_d", [128, 1760], F32, es)
            r_d = R()
            st = k.sb("rp_st", [128, NT, 8], F32, es)
            lo_in = k.sb("rp_loin", [128, 128], F32, es)
            r_loin = R()
            lT = k.sb("rp_lT", [32, 4, 128], F32, es)
            r_lT = R()
            ob = [k.sb("rp_ob%d" % i, [128, 2, self.RWC], F32, es) for i in range(1)]
            r_ob = [[R() for _ in range(8)] for _ in range(1)]
            oo = k.sb("rp_oo", [128, 608], F32, es)
            r_oo = R()
            tw = k.sb("rp_tw", [128, 512], F32, es)
            r_tw = R()
            ta = k.sb("rp_ta", [128, 2, 512], F32, es)
            r_ta = [R(), R()]
            ptr = k.ps("rp_ptr", [128, 512], F32, es)
            r_ptr = PR()
            pw = [k.ps("rp_pw%d" % i, [128, 512], F32, es) for i in range(4)]
            r_pw = [PR() for _ in range(4)]
            for t in range(NT):
                rows = slice(t * 128, (t + 1) * 128)
                c_, rc_ = cur[t % 2], r_cur[t % 2]
                p_, rp_ = prv[t % 2], r_prv[t % 2]
                n_, rn_ = nxt[t % 2], r_nxt[t % 2]
                first = t in (0, 2)
                last_ = t in (1, NT - 1)
                for dst, rdst, off, lo_, hi_ in ((c_, rc_, 0, 0, 128), (p_, rp_, -1, 1 if first else 0, 128), (n_, rn_, 1, 0, 127 if last_ else 128)):
                    if hi_ - lo_ < 128:
                        k.op("gpsimd", lambda e, dst=dst: e.memset(dst[:], 0.0), writes=[rdst])
                    r0 = t * 128 + off
                    zt = [r_zt[t]] + ([r_zt[t - 1]] if off < 0 and t > 0 else []) + ([r_zt[t + 1]] if off > 0 and t < NT - 1 else [])
                    k.dma(dst[lo_:hi_, 0:1152], z[r0 + lo_:r0 + hi_, C_RWKS:C_RWKS + 1152], reads=zt, writes=[rdst])
                    k.dma(dst[lo_:hi_, 1152:1760], z[r0 + lo_:r0 + hi_, C_RWQS:C_RWQS + 608], reads=zt, writes=[rdst])
                k.op("vector", lambda e, p_=p_, n_=n_: e.tensor_tensor(out=d_[:], in0=p_[:], in1=n_[:], op=ALU.add), reads=[rp_, rn_], writes=[r_d])
                k.op("vector", lambda e, c_=c_: e.scalar_tensor_tensor(out=d_[:], in0=d_[:], scalar=0.5, in1=c_[:], op0=ALU.mult, op1=ALU.subtract),
                     reads=[r_d, rc_], writes=[r_d])
                k.op("vector", lambda e: e.tensor_tensor(out=d_[:], in0=d_[:], in1=mu[:], op=ALU.mult), reads=[r_d, r_c], writes=[r_d])
                k.op("vector", lambda e, c_=c_: e.tensor_tensor(out=mx[:], in0=d_[:], in1=c_[:], op=ALU.add), reads=[r_d, rc_], writes=[r_mx])
                o_, ro = ob[0], r_ob[0]
                kx = mx[:, 0:512]
                vx = mx[:, 512:1024]
                rx_ = mx[:, 1152:1664]
                r_s = R()
                kk_o = o_[:, 0, 0:512]
                k.op("vector", lambda e: e.tensor_tensor(out=kk_o, in0=kx, in1=kkg[:], op=ALU.mult), reads=[r_mx, r_c], writes=[ro[0]])
                k.op("vector", lambda e: e.tensor_tensor(out=d_[:, 0:512], in0=kk_o, in1=kk_o, op=ALU.mult), reads=[ro[0]], writes=[r_d])
                k.op("vector", lambda e, t=t: e.tensor_reduce(out=st[:, t, :], in_=d_[:, 0:512].rearrange("p (h d) -> p h d", h=8), axis=AX.X, op=ALU.add),
                     reads=[r_d], writes=[r_s])
                k.op("scalar", lambda e, t=t: e.activation(out=st[:, t, :], in_=st[:, t, :], func=AF.Sqrt), reads=[r_s], writes=[r_s])
                k.op("vector", lambda e, t=t: e.tensor_scalar(out=st[:, t, :], in0=st[:, t, :], scalar1=1e-12, scalar2=None, op0=ALU.max), reads=[r_s], writes=[r_s])
                k.op("vector", lambda e, t=t: e.reciprocal(out=st[:, t, :], in_=st[:, t, :]), reads=[r_s], writes=[r_s])
                k.op("vector", lambda e, t=t: e.tensor_tensor(out=kk_o.rearrange("p (h d) -> p h d", h=8), in0=kk_o.rearrange("p (h d) -> p h d", h=8),
                                                             in1=st[:, t, :].unsqueeze(2).to_broadcast([128, 8, 64]), op=ALU.mult),
                     reads=[ro[0], r_s], writes=[ro[0]])
                k.op("gpsimd", lambda e: e.tensor_copy(out=o_[:, 1, 0:512], in_=kk_o), reads=[ro[0]], writes=[ro[1]])
                k.op("gpsimd", lambda e: e.tensor_copy(out=o_[:, :, 2048:2560], in_=rx_.unsqueeze(1).to_broadcast([128, 2, 512])), reads=[r_mx], writes=[ro[2]])
                k.op("gpsimd", lambda e: e.tensor_copy(out=o_[:, :, 2560:3072], in_=vx.unsqueeze(1).to_broadcast([128, 2, 512])), reads=[r_mx], writes=[ro[3]])
                k.op("scalar", lambda e: e.activation(out=lo_in[:, 0:64], in_=mx[:, 1024:1088], func=AF.Tanh), reads=[r_mx], writes=[r_loin])
                k.op("vector", lambda e: e.tensor_copy(out=lo_in[:, 64:128], in_=mx[:, 1088:1152]), reads=[r_mx, r_loin], writes=[r_loin])

                def trl(e):
                    last = None
                    for j in range(4):
                        last = e.transpose(out=ptr[0:32, j * 128:(j + 1) * 128], in_=lo_in[:, j * 32:(j + 1) * 32], identity=self.identf[:])
                    return last
                k.op("tensor", trl, reads=[r_loin, self.r_ident], writes=[r_ptr])
                k.op("vector", lambda e: e.tensor_copy(out=lT[:], in_=ptr[0:32, :].rearrange("p (a b) -> p a b", a=4)), reads=[r_ptr], writes=[r_lT])
                for d in range(2):
                    p, rp = pw[d], r_pw[d]
                    k.op("tensor", lambda e, p=p, d=d: e.matmul(p[:], lhsT=lT[:, d, :], rhs=w2[:, 0, d, :], start=True, stop=True), reads=[r_lT, r_c], writes=[rp])
                    k.op("vector", lambda e, p=p, d=d: e.tensor_tensor(out=tw[:], in0=p[:], in1=w0[:, d * 512:(d + 1) * 512], op=ALU.add), reads=[rp, r_c], writes=[r_tw])
                    k.op("scalar", lambda e: e.activation(out=tw[:], in_=tw[:], func=AF.Sigmoid), reads=[r_tw], writes=[r_tw])
                    k.op("vector", lambda e, d=d: e.tensor_scalar(out=o_[:, d, 512:1024], in0=tw[:], scalar1=-float(np.exp(-0.5)), scalar2=None, op0=ALU.mult), reads=[r_tw], writes=[ro[4 + d]])
                    p, rp = pw[2 + d], r_pw[2 + d]
                    k.op("tensor", lambda e, p=p, d=d: e.matmul(p[:], lhsT=lT[:, 2 + d, :], rhs=w2[:, 1, d, :], start=True, stop=True), reads=[r_lT, r_c], writes=[rp])
                    k.op("vector", lambda e, p=p, d=d: e.tensor_tensor(out=ta[:, d, :], in0=p[:], in1=a0[:, d * 512:(d + 1) * 512], op=ALU.add), reads=[rp, r_c], writes=[r_ta[d]])
                    k.op("scalar", lambda e, d=d: e.activation(out=ta[:, d, :], in_=ta[:, d, :], func=AF.Sigmoid), reads=[r_ta[d]], writes=[r_ta[d]])
                    k.op("gpsimd", lambda e, d=d: e.tensor_tensor(out=o_[:, d, 1024:1536], in0=kk_o, in1=ta[:, d, :], op=ALU.mult), reads=[ro[0], r_ta[d]], writes=[ro[6 + d]])
                    k.op("vector", lambda e, d=d: e.scalar_tensor_tensor(out=ta[:, d, :], in0=ta[:, d, :], scalar=-1.0, in1=kag[:], op0=ALU.add, op1=ALU.mult),
                         reads=[r_ta[d], ro[6 + d], r_c], writes=[r_ta[d]])
                    k.op("vector", lambda e, d=d: e.scalar_tensor_tensor(out=o_[:, d, 1536:2048], in0=ta[:, d, :], scalar=1.0, in1=kx, op0=ALU.add, op1=ALU.mult),
                         reads=[r_ta[d], r_mx], writes=[ro[6 + d]])
                k.op("vector", lambda e: e.tensor_tensor(out=oo[:, 0:512], in0=o_[:, 0, 1536:2048], in1=o_[:, 1, 1536:2048], op=ALU.add), reads=[ro[6], ro[7]], writes=[r_oo])
                k.op("gpsimd", lambda e: e.tensor_copy(out=oo[:, 512:608], in_=mx[:, 1664:1760]), reads=[r_mx, r_oo], writes=[r_oo])
                for d in range(2):
                    k.dma(rwd[d, rows, :], o_[:, d, :], reads=ro, writes=[self.r_rwd_t[t]])
                k.dma(rwo[rows, :], oo[:], reads=[r_oo], writes=[r_rwo])
            k.barrier()

    def stage_rwscan_seq(self, l):
        k = self.k
        rwd, _ = self.dr["rwd%d" % l]
        r_rwd_t = self.r_rwd_t
        yfb, r_yfb = self.dram("rwy%d" % l, [2, T, 512], F32)
        self.r_rwy_t = [R() for _ in range(NT)]
        e2d, r_e2d = self.dr["rw_e2"]
        pmd, r_pmd = self.dr["rw_pm"]
        j64d, r_j64d = self.dr["rw_j64"]
        with ExitStack() as es:
            E2 = k.sb("rs_E2", [128, 64, 128], F32, es)
            Pm = k.sb("rs_Pm", [128, 128], F32, es)
            J64 = k.sb("rs_J64", [64, 64], F32, es)
            r_cst = R()
            for c in range(4):
                k.dma(E2[:, c * 16:(c + 1) * 16, :], e2d[:, c * 16:(c + 1) * 16, :], reads=[r_e2d], writes=[r_cst])
            k.dma(Pm[:], pmd, reads=[r_pmd], writes=[r_cst])
            k.dma(J64[:], j64d, reads=[r_j64d], writes=[r_cst])
            S = k.sb("rs_S", [128, 512], F32, es)
            r_S = R()
            k.op("vector", lambda e: e.memset(S[:], 0.0), writes=[r_S])
            X = [k.sb("rs_X%d" % i, [128, self.RWC], F32, es) for i in range(2)]
            r_X = [R() for _ in range(2)]
            Vr = k.sb("rs_Vr", [128, 8, 2, 64], F32, es)
            r_Vr = R()
            vT = [k.sb("rs_vT%d" % i, [128, 8, 64], F32, es) for i in range(2)]
            r_vT = [R() for _ in range(2)]
            ys = [k.sb("rs_ys%d" % i, [128, 8, 64], F32, es) for i in range(2)]
            r_ys = [R() for _ in range(2)]
            yo = k.sb("rs_yo", [64, 8, 2, 64], F32, es)
            r_yo = R()
            yb2 = k.sb("rs_yb2", [64, 512], F32, es)
            r_yb2 = R()
            yb3 = k.sb("rs_yb3", [64, 512], F32, es)
            r_yb3 = R()
            t1 = k.sb("rs_t1", [128, 512], F32, es)
            t2 = k.sb("rs_t2", [128, 512], F32, es)
            t3 = k.sb("rs_t3", [128, 512], F32, es)
            r_t1, r_t2, r_t3 = R(), R(), R()
            sa = k.sb("rs_sa", [128, 8], F32, es)
            r_sa = R()
            bcp = [k.ps("rs_bc%d" % i, [128, 512], F32, es) for i in range(5)]
            r_bcp = [PR() for _ in range(5)]
            pmisc = k.ps("rs_pm", [128, 8, 128], F32, es)
            r_pmisc = PR()
            pj = k.ps("rs_pj", [128, 512], F32, es)
            r_pj = PR()
            S3 = S[:].rearrange("p (h d) -> p h d", h=8)
            blk = 0
            nblk_dbg = self.dbg.get("RW_NBLK", 10 ** 9)
            for seg0, seglen in ((0, NCTX), (NCTX, L)):
                nb = seglen // 64
                for j in range(nb):
                    if blk >= nblk_dbg:
                        break
                    X_, rX = X[blk % 2], r_X[blk % 2]
                    vT_, rvT = vT[blk % 2], r_vT[blk % 2]
                    ys_, rys = ys[blk % 2], r_ys[blk % 2]
                    blk += 1
                    f0 = seg0 + 64 * j
                    b0 = seg0 + seglen - 64 * (j + 1)
                    k.dma(X_[0:64, :], rwd[0, f0:f0 + 64, :], reads=[r_rwd_t[f0 // 128]], writes=[rX])
                    k.dma(X_[64:128, :], rwd[1, b0:b0 + 64, :], reads=[r_rwd_t[b0 // 128]], writes=[rX])
                    k.op("gpsimd", lambda e, X_=X_: e.tensor_copy(out=Vr[:], in_=X_[:, 2560:3072].rearrange("p (h d) -> p h d", h=8).unsqueeze(2).to_broadcast([128, 8, 2, 64])),
                         reads=[rX], writes=[r_Vr])

                    def vtr(e):
                        last = None
                        for h in range(8):
                            last = e.matmul(pmisc[:, h, :], lhsT=Vr[:, h, :, :].rearrange("p a b -> p (a b)"), rhs=Pm[:], start=True, stop=True)
                        return last
                    k.op("tensor", vtr, reads=[r_Vr, r_cst], writes=[r_pmisc])
                    k.op("vector", lambda e, vT_=vT_: e.tensor_copy(out=vT_[0:64, :, :], in_=pmisc[0:64, :, 0:64]), reads=[r_pmisc], writes=[rvT])
                    k.op("vector", lambda e, vT_=vT_: e.tensor_copy(out=vT_[64:128, :, :], in_=pmisc[64:128, :, 64:128]), reads=[r_pmisc, rvT], writes=[rvT])
                    for i in range(64):
                        for q in range(5):
                            k.op("tensor", lambda e, q=q, i=i, X_=X_: e.matmul(bcp[q][:], lhsT=E2[:, i, :], rhs=X_[:, q * 512:(q + 1) * 512], start=True, stop=True),
                                 reads=[rX, r_cst], writes=[r_bcp[q]])
                        bkk, bdec, bb, bkd, br = [bcp[q][:].rearrange("p (h d) -> p h d", h=8) for q in range(5)]
                        k.op("vector", lambda e, bkk=bkk: e.tensor_tensor(out=t1[:].rearrange("p (h d) -> p h d", h=8), in0=S3, in1=bkk, op=ALU.mult),
                             reads=[r_S, r_bcp[0]], writes=[r_t1])
                        k.op("vector", lambda e: e.tensor_reduce(out=sa[:], in_=t1[:].rearrange("p (h d) -> p h d", h=8), axis=AX.X, op=ALU.add),
                             reads=[r_t1], writes=[r_sa])
                        k.op("vector", lambda e, bdec=bdec: e.tensor_tensor(out=S3, in0=S3, in1=bdec, op=ALU.mult), reads=[r_S, r_bcp[1]], writes=[r_S])
                        k.op("vector", lambda e, bb=bb: e.tensor_tensor(out=t2[:].rearrange("p (h d) -> p h d", h=8), in0=bb,
                                                                       in1=sa[:].unsqueeze(2).to_broadcast([128, 8, 64]), op=ALU.mult),
                             reads=[r_bcp[2], r_sa], writes=[r_t2])
                        k.op("vector", lambda e: e.tensor_tensor(out=S[:], in0=S[:], in1=t2[:], op=ALU.subtract), reads=[r_S, r_t2], writes=[r_S])
                        k.op("vector", lambda e, bkd=bkd, vT_=vT_, i=i: e.tensor_tensor(out=t3[:].rearrange("p (h d) -> p h d", h=8), in0=bkd,
                                                                                       in1=vT_[:, :, i:i + 1].to_broadcast([128, 8, 64]), op=ALU.mult),
                             reads=[r_bcp[3], rvT], writes=[r_t3])
                        k.op("vector", lambda e: e.tensor_tensor(out=S[:], in0=S[:], in1=t3[:], op=ALU.add), reads=[r_S, r_t3], writes=[r_S])
                        k.op("vector", lambda e, br=br: e.tensor_tensor(out=t1[:].rearrange("p (h d) -> p h d", h=8), in0=S3, in1=br, op=ALU.mult),
                             reads=[r_S, r_bcp[4]], writes=[r_t1])
                        k.op("vector", lambda e, ys_=ys_, i=i: e.tensor_reduce(out=ys_[:, :, i:i + 1].rearrange("p h o -> p (h o)"), in_=t1[:].rearrange("p (h d) -> p h d", h=8), axis=AX.X, op=ALU.add),
                             reads=[r_t1], writes=[rys])
                    def ytr(e, ys_=ys_):
                        last = None
                        for h in range(8):
                            last = e.matmul(pmisc[0:64, h, :], lhsT=ys_[:, h, :], rhs=self.identf[:], start=True, stop=True)
                        return last
                    k.op("tensor", ytr, reads=[rys, self.r_ident], writes=[r_pmisc])
                    k.op("vector", lambda e: e.tensor_copy(out=yo[:].rearrange("p h a d -> p h (a d)"), in_=pmisc[0:64, :, :]), reads=[r_pmisc], writes=[r_yo])
                    k.op("gpsimd", lambda e: e.tensor_copy(out=yb2[:].rearrange("p (h d) -> p h d", h=8), in_=yo[:, :, 1, :]), reads=[r_yo], writes=[r_yb2])
                    k.op("tensor", lambda e: e.matmul(pj[0:64, :], lhsT=J64[:], rhs=yb2[:], start=True, stop=True), reads=[r_yb2, r_cst], writes=[r_pj])
                    k.op("scalar", lambda e: e.activation(out=yb3[:], in_=pj[0:64, :], func=AF.Copy), reads=[r_pj], writes=[r_yb3])
                    k.dma(yfb[0, f0:f0 + 64, :].rearrange("p (h d) -> p h d", h=8), yo[:, :, 0, :], reads=[r_yo], writes=[self.r_rwy_t[f0 // 128]])
                    k.dma(yfb[1, b0:b0 + 64, :], yb3[:], reads=[r_yb3], writes=[self.r_rwy_t[b0 // 128]])
            k.barrier()

    def stage_rwscan(self, l):
        k = self.k
        rwd, _ = self.dr["rwd%d" % l]
        r_rwd_t = self.r_rwd_t
        yfb, r_yfb = self.dram("rwy%d" % l, [2, T, 512], F32)
        self.r_rwy_t = [R() for _ in range(NT)]
        mkd, r_mkd = self.dr["rw_masks"]
        dmd, r_dmd = self.dr["rw_dm"]
        with ExitStack() as es:
            MK = k.sb("rc_MK", [128, 6, 128], F32, es)
            DM = k.sb("rc_DM", [128, 2, 1024], F32, es)
            ones = k.sb("rc_ones", [128, 64], F32, es)
            r_cst = R()
            k.dma(MK[:], mkd, reads=[r_mkd], writes=[r_cst])
            k.dma(DM[:], dmd, reads=[r_dmd], writes=[r_cst])
            k.op("vector", lambda e: e.memset(ones[:], 1.0), writes=[r_cst])
            MI, BO, MS, nMS, nMST, nMI = [MK[:, i, :] for i in range(6)]
            S = k.sb("rc_S", [128, 512], F32R, es)
            r_S = R()
            X = [k.sb("rc_X%d" % i, [128, self.RWC], F32, es) for i in range(2)]
            r_X = [R() for _ in range(2)]
            fac = [k.sb("rc_fac%d" % i, [128, 512], F32, es) for i in range(4)]
            r_fac = [R() for _ in range(4)]
            tl = k.sb("rc_tl", [128, 2, 512], F32, es)
            r_tl = [R(), R()]
            k.op("vector", lambda e: e.memset(tl[:, 0, :], 0.0), writes=[r_tl[0]])
            k.op("vector", lambda e: e.tensor_copy(out=S[:], in_=tl[:, 0, :]), reads=[r_tl[0]], writes=[r_S])
            pl = [k.sb("rc_pl%d" % i, [128, 512], F32, es) for i in range(6)]
            r_pl = [R() for _ in range(6)]
            bd03 = [k.sb("rc_bd%d" % i, [128, 8, 128], F32R, es) for i in range(4)]
            r_bd03 = [R() for _ in range(4)]
            PB = []
            for par in range(2):
                d = {}
                d["bd4"] = k.sb("rc_bd4_%d" % par, [128, 8, 128], F32R, es)
                d["bd5"] = k.sb("rc_bd5_%d" % par, [128, 8, 128], F32R, es)
                d["bd6"] = k.sb("rc_bd6_%d" % par, [128, 8, 128], F32, es)
                d["r_bd"] = [R(), R(), R()]
                d["xT"] = [k.sb("rc_xT%d_%d" % (i, par), [128, 8, 128], F32R, es) for i in range(4)]
                d["r_xT"] = [[R(), R()] for _ in range(4)]
                for nm in ("AT", "ApT", "nBpT", "NT0", "M0"):
                    d[nm] = k.sb("rc_%s_%d" % (nm, par), [128, 8, 128], F32R, es)
                    d["r_" + nm] = [R(), R()]
                d["Vc"] = k.sb("rc_Vc%d" % par, [128, 512], F32R, es)
                d["r_Vc"] = R()
                PB.append(d)
            NTb = [k.sb("rc_NT%d" % i, [128, 8, 128], F32R, es) for i in range(2)]
            Mb = [k.sb("rc_M%d" % i, [128, 8, 128], F32R, es) for i in range(2)]
            r_NT = [[R(), R()] for _ in range(2)]
            r_M = [[R(), R()] for _ in range(2)]
            G = k.sb("rc_G", [128, 512], F32R, es)
            r_G = R()
            Pend = k.sb("rc_Pend", [128, 512], F32, es)
            r_Pend = R()
            Yo = [k.sb("rc_Yo%d" % i, [128, 512], F32, es) for i in range(2)]
            r_Yo = [R() for _ in range(2)]
            pb = [k.ps("rc_pb%d" % i, [128, 512], F32, es) for i in range(8)]
            r_pb = [PR() for _ in range(8)]
            ipb = [0]

            def bank():
                i = ipb[0] % 8
                ipb[0] += 1
                return pb[i], r_pb[i]

            chunks = []
            for seg0, seglen in ((0, NCTX), (NCTX, L)):
                for j in range(seglen // 64):
                    chunks.append((seg0 + 64 * j, seg0 + seglen - 64 * (j + 1)))
            chunks = chunks[:self.dbg.get("RW_NBLK", 10 ** 9)]

            def prologue_steps(ci):
                f0, b0 = chunks[ci]
                par = ci % 2
                d = PB[par]
                X_, rX = X[par], r_X[par]
                kk_, lw_, b_, kd_, r_ = [X_[:, q * 512:(q + 1) * 512] for q in range(5)]
                steps = []

                def s_load():
                    k.dma(X_[0:64, :], rwd[0, f0:f0 + 64, :], reads=[r_rwd_t[f0 // 128]], writes=[rX])
                    k.dma(X_[64:128, :], rwd[1, b0:b0 + 64, :], reads=[r_rwd_t[b0 // 128]], writes=[rX])
                    k.op("gpsimd", lambda e: e.tensor_copy(out=d["Vc"][:], in_=X_[:, 2560:3072]), reads=[rX], writes=[d["r_Vc"]])
                steps.append(s_load)

                def s_cum():
                    pLi, rLi = bank()
                    pLt, rLt = bank()
                    k.op("tensor", lambda e: e.matmul(pLi[:], lhsT=MI, rhs=lw_, start=True, stop=True), reads=[rX, r_cst], writes=[rLi])
                    k.op("tensor", lambda e: e.matmul(pLt[:], lhsT=BO, rhs=lw_, start=True, stop=True), reads=[rX, r_cst], writes=[rLt])
                    k.op("scalar", lambda e: e.activation(out=fac[0][:], in_=pLi[:], func=AF.Exp), reads=[rLi], writes=[r_fac[0]])
                    k.op("scalar", lambda e: e.activation(out=fac[1][:], in_=pLi[:], func=AF.Exp, scale=-1.0), reads=[rLi], writes=[r_fac[1]])
                    k.op("vector", lambda e: e.tensor_tensor(out=tl[:, 0, :], in0=pLi[:], in1=lw_, op=ALU.subtract), reads=[rLi, rX], writes=[r_tl[0]])
                    k.op("scalar", lambda e: e.activation(out=fac[2][:], in_=tl[:, 0, :], func=AF.Exp), reads=[r_tl[0]], writes=[r_fac[2]])
                    k.op("vector", lambda e: e.tensor_copy(out=tl[:, 1, :], in_=pLt[:]), reads=[rLt], writes=[r_tl[1]])
                    k.op("vector", lambda e: e.tensor_tensor(out=tl[:, 1, :], in0=tl[:, 1, :], in1=pLi[:], op=ALU.subtract), reads=[rLi, r_tl[1]], writes=[r_tl[1]])
                    k.op("scalar", lambda e: e.activation(out=fac[3][:], in_=tl[:, 1, :], func=AF.Exp), reads=[r_tl[1]], writes=[r_fac[3]])
                steps.append(s_cum)

                def s_prod():
                    for i, (src, fi) in enumerate(((kk_, 2), (r_, 0), (kd_, 1), (b_, 1), (kd_, 3), (b_, 3))):
                        k.op("vector" if i < 4 else "gpsimd", lambda e, i=i, src=src, fi=fi: e.tensor_tensor(out=pl[i][:], in0=src, in1=fac[fi][:], op=ALU.mult),
                             reads=[rX, r_fac[fi]], writes=[r_pl[i]])
                    for i in range(7):
                        src = pl[i][:] if i < 6 else lw_
                        dm = DM[:, 1 if i == 5 else 0, :]
                        rs = [r_pl[i]] if i < 6 else [rX]
                        if i < 4:
                            dst, rd = bd03[i], r_bd03[i]
                        else:
                            dst, rd = d["bd%d" % i], d["r_bd"][i - 4]
                        k.op("vector" if i < 4 else "gpsimd", lambda e, src=src, dm=dm, dst=dst: e.tensor_tensor(
                            out=dst[:].rearrange("p h (a d) -> p h a d", a=2), in0=src.rearrange("p (h d) -> p h d", h=8).unsqueeze(2).to_broadcast([128, 8, 2, 64]),
                            in1=dm.rearrange("p (h a d) -> p h a d", h=8, a=2), op=ALU.mult), reads=rs + [r_cst], writes=[rd])
                steps.append(s_prod)
                for qi in range(4):
                    def s_tr(qi=qi):
                        for hf in range(2):
                            p_, rp_ = bank()

                            def tr(e, p_=p_, hf=hf):
                                last = None
                                for hh in range(4):
                                    last = e.transpose(out=p_[:, hh * 128:(hh + 1) * 128], in_=bd03[qi][:, hf * 4 + hh, :].bitcast(F32), identity=self.identf[:])
                                return last
                            k.op("tensor", tr, reads=[r_bd03[qi], self.r_ident], writes=[rp_])
                            k.op("scalar", lambda e, p_=p_, hf=hf: e.activation(out=d["xT"][qi][:, hf * 4:(hf + 1) * 4, :].rearrange("p a b -> p (a b)"), in_=p_[:], func=AF.Copy),
                                 reads=[rp_], writes=[d["r_xT"][qi][hf]])
                    steps.append(s_tr)
                qT, rT, kT, bT = d["xT"]
                specs = ((bT, 3, qT, 0, nMS, "NT0"), (qT, 0, bT, 3, nMST, "M0"), (kT, 2, qT, 0, MS, "AT"), (kT, 2, rT, 1, MI, "ApT"), (bT, 3, rT, 1, nMI, "nBpT"))
                for (lt, li, rt, ri, mk, nm) in specs:
                    def s_in(lt=lt, li=li, rt=rt, ri=ri, mk=mk, nm=nm):
                        for hf in range(2):
                            p_, rp_ = bank()

                            def mm(e, p_=p_, hf=hf):
                                last = None
                                for hh in range(4):
                                    h = hf * 4 + hh
                                    last = e.matmul(p_[:, hh * 128:(hh + 1) * 128], lhsT=lt[:, h, :], rhs=rt[:, h, :], start=True, stop=True)
                                return last
                            k.op("tensor", mm, reads=[d["r_xT"][li][hf], d["r_xT"][ri][hf]], writes=[rp_])
                            k.op("gpsimd" if False else "vector", lambda e, p_=p_, hf=hf: e.tensor_tensor(
                                out=d[nm][:, hf * 4:(hf + 1) * 4, :], in0=p_[:].rearrange("p (a b) -> p a b", a=4), in1=mk.unsqueeze(1).to_broadcast([128, 4, 128]), op=ALU.mult),
                                reads=[rp_, r_cst], writes=[d["r_" + nm][hf]])
                    steps.append(s_in)
                return steps

            def solve(ci, fillers):
                f0, b0 = chunks[ci]
                par = ci % 2
                d = PB[par]
                Yo_, rYo = Yo[par], r_Yo[par]
                qT, rT, kT, bT = d["xT"]
                Vc_, rVc = d["Vc"], d["r_Vc"]

                def Vh(h):
                    return Vc_[:, h * 64:(h + 1) * 64]

                def fill(n):
                    for _ in range(n):
                        if fillers:
                            fillers.pop(0)()
                pG, rG = bank()

                def mmG(e):
                    last = None
                    for h in range(8):
                        e.matmul(pG[:, h * 64:(h + 1) * 64], lhsT=qT[:, h, :], rhs=S[:, h * 64:(h + 1) * 64], start=True, stop=False)
                        last = e.matmul(pG[:, h * 64:(h + 1) * 64], lhsT=d["AT"][:, h, :], rhs=Vh(h), start=False, stop=True)
                    return last
                k.op("tensor", mmG, reads=d["r_xT"][0] + d["r_AT"] + [r_S, rVc], writes=[rG])
                k.op("vector", lambda e: e.tensor_copy(out=G[:], in_=pG[:]), reads=[rG], writes=[r_G])
                fill(2)
                curT, curM, rcT, rcM = d["NT0"], d["M0"], d["r_NT0"], d["r_M0"]
                for lev in range(6):
                    pU, rU = bank()

                    def mmU(e, pU=pU, NTc=curT):
                        last = None
                        for h in range(8):
                            last = e.matmul(pU[:, h * 64:(h + 1) * 64], lhsT=NTc[:, h, :], rhs=G[:, h * 64:(h + 1) * 64], start=True, stop=True)
                        return last
                    k.op("tensor", mmU, reads=rcT + [r_G], writes=[rU])
                    if lev < 5:
                        nx = lev % 2
                        for (lt, rt, dst, rdst, rl, rr_) in ((curM, curT, NTb[nx], r_NT[nx], rcM, rcT), (curT, curM, Mb[nx], r_M[nx], rcT, rcM)):
                            for hf in range(2):
                                p_, rp_ = bank()

                                def mm2(e, p_=p_, lt=lt, rt=rt, hf=hf):
                                    last = None
                                    for hh in range(4):
                                        h = hf * 4 + hh
                                        last = e.matmul(p_[:, hh * 128:(hh + 1) * 128], lhsT=lt[:, h, :], rhs=rt[:, h, :], start=True, stop=True)
                                    return last
                                k.op("tensor", mm2, reads=[rl[hf], rr_[hf]], writes=[rp_])
                                k.op("scalar", lambda e, p_=p_, dst=dst, hf=hf: e.activation(out=dst[:, hf * 4:(hf + 1) * 4, :].rearrange("p a b -> p (a b)"), in_=p_[:], func=AF.Copy),
                                     reads=[rp_], writes=[rdst[hf]])
                        curT, curM, rcT, rcM = NTb[nx], Mb[nx], r_NT[nx], r_M[nx]
                    k.op("vector", lambda e, pU=pU: e.tensor_tensor(out=G[:], in0=G[:].bitcast(F32), in1=pU[:], op=ALU.add), reads=[rU, r_G], writes=[r_G])
                    fill(2)
                pY, rY = bank()

                def mmY(e):
                    last = None
                    for h in range(8):
                        hs = slice(h * 64, (h + 1) * 64)
                        e.matmul(pY[:, hs], lhsT=rT[:, h, :], rhs=S[:, hs], start=True, stop=False)
                        e.matmul(pY[:, hs], lhsT=d["ApT"][:, h, :], rhs=Vh(h), start=False, stop=False)
                        last = e.matmul(pY[:, hs], lhsT=d["nBpT"][:, h, :], rhs=G[:, hs], start=False, stop=True)
                    return last
                k.op("tensor", mmY, reads=d["r_xT"][1] + d["r_ApT"] + d["r_nBpT"] + [r_S, rVc, r_G], writes=[rY])
                k.op("scalar", lambda e: e.activation(out=Yo_[:], in_=pY[:], func=AF.Copy), reads=[rY], writes=[rYo])
                k.dma(yfb[0, f0:f0 + 64, :], Yo_[0:64, :], reads=[rYo], writes=[self.r_rwy_t[f0 // 128]])
                k.dma(yfb[1, b0:b0 + 64, :], Yo_[64:128, :], reads=[rYo], writes=[self.r_rwy_t[b0 // 128]])
                pL, rL = bank()

                def mmL(e):
                    last = None
                    for h in range(8):
                        last = e.matmul(pL[:, h * 64:(h + 1) * 64], lhsT=d["bd6"][:, h, :], rhs=ones[:], start=True, stop=True)
                    return last
                k.op("tensor", mmL, reads=[d["r_bd"][2], r_cst], writes=[rL])
                k.op("scalar", lambda e: e.activation(out=Pend[:], in_=pL[:], func=AF.Exp), reads=[rL], writes=[r_Pend])
                pS, rS = bank()

                def mmS(e):
                    last = None
                    for h in range(8):
                        hs = slice(h * 64, (h + 1) * 64)
                        e.matmul(pS[:, hs], lhsT=d["bd4"][:, h, :], rhs=Vh(h), start=True, stop=False)
                        last = e.matmul(pS[:, hs], lhsT=d["bd5"][:, h, :], rhs=G[:, hs], start=False, stop=True)
                    return last
                k.op("tensor", mmS, reads=[d["r_bd"][0], d["r_bd"][1], rVc, r_G], writes=[rS])
                k.op("vector", lambda e: e.tensor_tensor(out=S[:], in0=S[:].bitcast(F32), in1=Pend[:], op=ALU.mult), reads=[r_S, r_Pend], writes=[r_S])
                k.op("vector", lambda e: e.tensor_tensor(out=S[:], in0=S[:].bitcast(F32), in1=pS[:], op=ALU.add), reads=[r_S, rS], writes=[r_S])
                while fillers:
                    fillers.pop(0)()

            if chunks:
                for st_ in prologue_steps(0):
                    st_()
                for ci in range(len(chunks)):
                    fillers = prologue_steps(ci + 1) if ci + 1 < len(chunks) else []
                    solve(ci, fillers)
            k.barrier()

    def stage_rwout(self, l):
        k = self.k
        rwd, _ = self.dr["rwd%d" % l]
        rwo, r_rwo = self.dr["rwo%d" % l]
        yfb, _ = self.dr["rwy%d" % l]
        yd, r_yd = self.dram("y_rw%d" % l, [T, 512], F32)
        with ExitStack() as es:
            r_c = R()

            def bc(nm, n):
                tl = k.sb("ro_" + nm, [128, n], F32, es)
                ap, rr = self.dr[nm]
                k.dma(tl[:], ap[l].partition_broadcast(128), reads=[rr], writes=[r_c])
                return tl
            lnw = bc("rw_lnw", 512)
            lnb = bc("rw_lnb", 512)
            rkg = bc("rw_rk", 512)
            g2 = k.sb("ro_g2", [96, 512], F32, es)
            ap, rr = self.dr["rw_g2"]
            k.dma(g2[:], ap[l], reads=[rr], writes=[r_c])
            gne = k.sb("ro_gne", [128, 1], F32, es)
            k.op("vector", lambda e: e.memset(gne[:], 64e-5), writes=[r_c])
            yf = [k.sb("ro_yf%d" % i, [128, 512], F32, es) for i in range(2)]
            yb = [k.sb("ro_yb%d" % i, [128, 512], F32, es) for i in range(2)]
            rv = [k.sb("ro_rv%d" % i, [128, 1024], F32, es) for i in range(2)]
            kg = [k.sb("ro_kg%d" % i, [128, 608], F32, es) for i in range(2)]
            r_in = [[R() for _ in range(4)] for _ in range(2)]
            y = k.sb("ro_y", [128, 512], F32, es)
            r_y = R()
            sq = k.sb("ro_sq", [128, 512], F32, es)
            r_sq = R()
            st = k.sb("ro_st", [128, NT, 3, 8], F32, es)
            sg = k.sb("ro_sg", [128, 128], F32, es)
            r_sg = R()
            gT = k.sb("ro_gT", [96, 128], F32, es)
            r_gT = R()
            out = [k.sb("ro_out%d" % i, [128, 512], F32, es) for i in range(2)]
            r_out = [R() for _ in range(2)]
            ptr = k.ps("ro_ptr", [128, 512], F32, es)
            r_ptr = PR()
            pg = k.ps("ro_pg", [128, 512], F32, es)
            r_pg = PR()
            k.op("vector", lambda e: e.memset(sg[:], 0.0), writes=[r_sg])
            for t in range(NT):
                rows = slice(t * 128, (t + 1) * 128)
                i2 = t % 2
                ri = r_in[i2]
                k.dma(yf[i2][:], yfb[0, rows, :], reads=[self.r_rwy_t[t]], writes=[ri[0]])
                k.dma(yb[i2][:], yfb[1, rows, :], reads=[self.r_rwy_t[t]], writes=[ri[1]])
                k.dma(rv[i2][:], rwd[0, rows, 2048:3072], reads=[self.r_rwd_t[t]], writes=[ri[2]])
                k.dma(kg[i2][:], rwo[rows, :], reads=[r_rwo], writes=[ri[3]])
                r_s = R()
                y3 = y[:].rearrange("p (h d) -> p h d", h=8)
                k.op("vector", lambda e, i2=i2: e.tensor_tensor(out=y[:], in0=yf[i2][:], in1=yb[i2][:], op=ALU.add), reads=[ri[0], ri[1]], writes=[r_y])
                k.op("vector", lambda e, t=t: e.tensor_reduce(out=st[:, t, 0, :], in_=y3, axis=AX.X, op=ALU.add), reads=[r_y], writes=[r_s])
                k.op("vector", lambda e, t=t: e.tensor_scalar(out=st[:, t, 0, :], in0=st[:, t, 0, :], scalar1=1.0 / 64, scalar2=None, op0=ALU.mult), reads=[r_s], writes=[r_s])
                k.op("vector", lambda e, t=t: e.tensor_tensor(out=y3, in0=y3, in1=st[:, t, 0, :].unsqueeze(2).to_broadcast([128, 8, 64]), op=ALU.subtract),
                     reads=[r_y, r_s], writes=[r_y])
                k.op("gpsimd", lambda e: e.tensor_tensor(out=sq[:], in0=y[:], in1=y[:], op=ALU.mult), reads=[r_y], writes=[r_sq])
                k.op("vector", lambda e, t=t: e.tensor_reduce(out=st[:, t, 1, :], in_=sq[:].rearrange("p (h d) -> p h d", h=8), axis=AX.X, op=ALU.add), reads=[r_sq], writes=[r_s])
                k.op("scalar", lambda e, t=t: e.activation(out=st[:, t, 1, :], in_=st[:, t, 1, :], func=AF.Sqrt, bias=gne[:, 0:1], scale=1.0 / 64), reads=[r_s, r_c], writes=[r_s])
                k.op("vector", lambda e, t=t: e.reciprocal(out=st[:, t, 1, :], in_=st[:, t, 1, :]), reads=[r_s], writes=[r_s])
                k.op("vector", lambda e, t=t: e.tensor_tensor(out=y3, in0=y3, in1=st[:, t, 1, :].unsqueeze(2).to_broadcast([128, 8, 64]), op=ALU.mult),
                     reads=[r_y, r_s], writes=[r_y])
                k.op("gpsimd", lambda e: e.tensor_tensor(out=y[:], in0=y[:], in1=lnw[:], op=ALU.mult), reads=[r_y, r_c], writes=[r_y])
                k.op("gpsimd", lambda e: e.tensor_tensor(out=y[:], in0=y[:], in1=lnb[:], op=ALU.add), reads=[r_y, r_c], writes=[r_y])
                k.op("vector", lambda e, i2=i2: e.tensor_tensor(out=sq[:], in0=rv[i2][:, 0:512], in1=kg[i2][:, 0:512], op=ALU.mult), reads=[ri[2], ri[3], r_sq], writes=[r_sq])
                k.op("vector", lambda e: e.tensor_tensor(out=sq[:], in0=sq[:], in1=rkg[:], op=ALU.mult), reads=[r_sq, r_c], writes=[r_sq])
                k.op("vector", lambda e, t=t: e.tensor_reduce(out=st[:, t, 2, :], in_=sq[:].rearrange("p (h d) -> p h d", h=8), axis=AX.X, op=ALU.add), reads=[r_sq], writes=[r_s])
                k.op("vector", lambda e, t=t, i2=i2: e.tensor_tensor(out=sq[:].rearrange("p (h d) -> p h d", h=8), in0=rv[i2][:, 512:1024].rearrange("p (h d) -> p h d", h=8),
                                                                    in1=st[:, t, 2, :].unsqueeze(2).to_broadcast([128, 8, 64]), op=ALU.mult),
                     reads=[ri[2], r_s, r_sq], writes=[r_sq])
                k.op("vector", lambda e: e.tensor_tensor(out=y[:], in0=y[:], in1=sq[:], op=ALU.add), reads=[r_y, r_sq], writes=[r_y])
                k.op("scalar", lambda e, i2=i2: e.activation(out=sg[:, 0:96], in_=kg[i2][:, 512:608], func=AF.Sigmoid), reads=[ri[3]], writes=[r_sg])
                k.op("tensor", lambda e: e.transpose(out=ptr[:, 0:128], in_=sg[:], identity=self.identf[:]), reads=[r_sg, self.r_ident], writes=[r_ptr])
                k.op("vector", lambda e: e.tensor_copy(out=gT[:], in_=ptr[0:96, 0:128]), reads=[r_ptr], writes=[r_gT])
                k.op("tensor", lambda e: e.matmul(pg[:], lhsT=gT[:], rhs=g2[:], start=True, stop=True), reads=[r_gT, r_c], writes=[r_pg])
                o_, ro = out[i2], r_out[i2]
                k.op("vector", lambda e, o_=o_: e.tensor_tensor(out=o_[:], in0=y[:], in1=pg[:], op=ALU.mult), reads=[r_y, r_pg], writes=[ro])
                k.dma(yd[rows, :], o_[:], reads=[ro], writes=[r_yd])
            k.barrier()

    def stage_merge(self, l, xname):
        k = self.k
        z, _ = self.dr["z%d" % l]
        r_zt = self.r_ztile
        xin, r_xin = self.dr[xname]
        modD, r_modD = self.dr["modD%d" % l]
        wbr, r_wbr = self.dr["w_br"]
        wout, r_wout = self.dr["w_out"]
        accD, r_accD = self.dram("accD%d" % l, [T, D], BF16)
        r_acc_t = [R() for _ in range(NT)]
        x1, r_x1 = self.dram("x1_%d" % l, [T, D], F32)
        self.r_x1_t = [R() for _ in range(NT)]
        ybr = [self.dr[n % l] for n in ("y_fn%d", "y_na%d", "y_mla%d", "y_rw%d")]
        with ExitStack() as es:
            W = k.sb("mg_W", [128, 16, 2048], BF16, es)
            r_W = [R() for _ in range(16)]
            wst = [k.sb("mg_wst%d" % i, [128, 2048], F32, es) for i in range(2)]
            r_wst = [R() for _ in range(2)]
            for i in range(16):
                b, kc = i // 4, i % 4
                k.dma(wst[i % 2][:], wbr[l, b, kc * 128:(kc + 1) * 128, :], reads=[r_wbr], writes=[r_wst[i % 2]])
                if i % 2 == 0:
                    k.op("vector", lambda e, i=i: e.tensor_copy(out=W[:, i, :], in_=wst[i % 2][:]), reads=[r_wst[i % 2]], writes=[r_W[i]])
                else:
                    k.op("scalar", lambda e, i=i: e.activation(out=W[:, i, :], in_=wst[i % 2][:], func=AF.Copy), reads=[r_wst[i % 2]], writes=[r_W[i]])
            with ExitStack() as es2:
                yt = k.sb("mg_yt", [128, 4, 512], F32, es2)
                r_yt = R()
                ytb = k.sb("mg_ytb", [128, 2048], BF16, es2)
                r_ytb = R()
                yT = k.sb("mg_yT", [128, 16, 128], BF16, es2)
                r_yT = R()
                gt = [k.sb("mg_gt%d" % i, [128, 2048], F32, es2) for i in range(2)]
                r_gt = [R() for _ in range(2)]
                acc = k.sb("mg_acc", [128, 2048], F32, es2)
                r_acc = R()
                accb = k.sb("mg_accb", [128, 2048], BF16, es2)
                r_accb = R()
                tmp = k.sb("mg_tmp", [128, 512], F32, es2)
                r_tmp = R()
                ptr = [k.ps("mg_ptr%d" % i, [128, 8, 128], BF16, es2) for i in range(2)]
                r_ptr = [PR() for _ in range(2)]
                pm = [k.ps("mg_pm%d" % i, [128, 512], F32, es2) for i in range(5)]
                r_pm = [PR() for _ in range(5)]
                ip = 0
                ig = 0
                for t in range(NT):
                    rows = slice(t * 128, (t + 1) * 128)
                    for b in range(4):
                        k.dma(yt[:, b, :], ybr[b][0][rows, :], reads=[ybr[b][1]], writes=[r_yt])
                    k.op("scalar", lambda e: e.activation(out=ytb[:], in_=yt[:].rearrange("p a b -> p (a b)"), func=AF.Copy), reads=[r_yt], writes=[r_ytb])
                    for hf in range(2):
                        def tr(e, hf=hf):
                            last = None
                            for j in range(8):
                                c = hf * 8 + j
                                last = e.transpose(out=ptr[hf][:, j, :], in_=ytb[:, c * 128:(c + 1) * 128], identity=self.ident[:])
                            return last
                        k.op("tensor", tr, reads=[r_ytb, self.r_ident], writes=[r_ptr[hf]])
                        if hf == 0:
                            k.op("vector", lambda e: e.tensor_copy(out=yT[:, 0:8, :], in_=ptr[0][:]), reads=[r_ptr[0]], writes=[r_yT])
                        else:
                            k.op("scalar", lambda e: e.activation(out=yT[:, 8:16, :], in_=ptr[1][:], func=AF.Copy), reads=[r_ptr[1], r_yT], writes=[r_yT])
                    for b in range(4):
                        g_, rg = gt[ig % 2], r_gt[ig % 2]
                        ig += 1
                        k.dma(g_[:], z[rows, C_GATE + b * 2048:C_GATE + (b + 1) * 2048], reads=[r_zt[t]], writes=[rg])
                        k.op("scalar", lambda e, g_=g_: e.activation(out=g_[:], in_=g_[:], func=AF.Sigmoid), reads=[rg], writes=[rg])
                        for cbk in range(4):
                            p, rp = pm[ip % 5], r_pm[ip % 5]
                            ip += 1
                            cs_ = slice(cbk * 512, (cbk + 1) * 512)

                            def mm(e, p=p, b=b, cs_=cs_):
                                last = None
                                for kc in range(4):
                                    last = e.matmul(p[:], lhsT=yT[:, b * 4 + kc, :], rhs=W[:, b * 4 + kc, cs_], start=(kc == 0), stop=(kc == 3))
                                return last
                            k.op("tensor", mm, reads=[r_yT] + r_W[b * 4:b * 4 + 4], writes=[rp])
                            if b == 0:
                                k.op("vector", lambda e, p=p, g_=g_, cs_=cs_: e.tensor_tensor(out=acc[:, cs_], in0=p[:], in1=g_[:, cs_], op=ALU.mult),
                                     reads=[rp, rg], writes=[r_acc])
                            else:
                                k.op("vector", lambda e, p=p, g_=g_, cs_=cs_: e.tensor_tensor(out=tmp[:], in0=p[:], in1=g_[:, cs_], op=ALU.mult),
                                     reads=[rp, rg], writes=[r_tmp])
                                k.op("vector", lambda e, cs_=cs_: e.tensor_tensor(out=acc[:, cs_], in0=acc[:, cs_], in1=tmp[:], op=ALU.add),
                                     reads=[r_tmp, r_acc], writes=[r_acc])
                    k.op("vector", lambda e: e.tensor_copy(out=accb[:], in_=acc[:]), reads=[r_acc], writes=[r_accb])
                    k.dma(accD[rows, :], accb[:], reads=[r_accb], writes=[r_acc_t[t]])
                k.barrier()
            for i in range(16):
                k.dma(wst[i % 2][:], wout[l, i * 128:(i + 1) * 128, :], reads=[r_wout], writes=[r_wst[i % 2]])
                if i % 2 == 0:
                    k.op("vector", lambda e, i=i: e.tensor_copy(out=W[:, i, :], in_=wst[i % 2][:]), reads=[r_wst[i % 2]], writes=[r_W[i]])
                else:
                    k.op("scalar", lambda e, i=i: e.activation(out=W[:, i, :], in_=wst[i % 2][:], func=AF.Copy), reads=[r_wst[i % 2]], writes=[r_W[i]])
            with ExitStack() as es3:
                g1 = k.sb("mg_g1", [128, 2, 2048], F32, es3)
                r_g1 = R()
                for r_ in range(2):
                    k.dma(g1[:, r_, :], modD[r_:r_ + 1, 2 * D:3 * D].partition_broadcast(128), reads=[r_modD], writes=[r_g1])
                ab = [k.sb("mg_ab%d" % i, [128, 2048], BF16, es3) for i in range(2)]
                r_ab = [R() for _ in range(2)]
                aT = k.sb("mg_aT", [128, 16, 128], BF16, es3)
                r_aT = R()
                xt = [k.sb("mg_xt%d" % i, [128, 2048], F32, es3) for i in range(2)]
                r_xt = [R() for _ in range(2)]
                xo = [k.sb("mg_xo%d" % i, [128, 2048], F32, es3) for i in range(2)]
                r_xo = [R() for _ in range(2)]
                tmp = k.sb("mg_tmp2", [128, 512], F32, es3)
                r_tmp = R()
                ptr = [k.ps("mg_ptrb%d" % i, [128, 8, 128], BF16, es3) for i in range(2)]
                r_ptr = [PR() for _ in range(2)]
                pm = [k.ps("mg_pmb%d" % i, [128, 512], F32, es3) for i in range(4)]
                r_pm = [PR() for _ in range(4)]
                ip = 0
                for t in range(NT):
                    rows = slice(t * 128, (t + 1) * 128)
                    row = 1 if t < 2 else 0
                    a_, ra = ab[t % 2], r_ab[t % 2]
                    x_, rx = xt[t % 2], r_xt[t % 2]
                    o_, ro = xo[t % 2], r_xo[t % 2]
                    k.dma(a_[:], accD[rows, :], reads=[r_acc_t[t]], writes=[ra])
                    k.dma(x_[:], xin[rows, :], reads=[r_xin], writes=[rx])
                    for hf in range(2):
                        def tr(e, hf=hf, a_=a_):
                            last = None
                            for j in range(8):
                                c = hf * 8 + j
                                last = e.transpose(out=ptr[hf][:, j, :], in_=a_[:, c * 128:(c + 1) * 128], identity=self.ident[:])
                            return last
                        k.op("tensor", tr, reads=[ra, self.r_ident], writes=[r_ptr[hf]])
                        if hf == 0:
                            k.op("vector", lambda e: e.tensor_copy(out=aT[:, 0:8, :], in_=ptr[0][:]), reads=[r_ptr[0]], writes=[r_aT])
                        else:
                            k.op("scalar", lambda e: e.activation(out=aT[:, 8:16, :], in_=ptr[1][:], func=AF.Copy), reads=[r_ptr[1], r_aT], writes=[r_aT])
                    for cbk in range(4):
                        p, rp = pm[ip % 4], r_pm[ip % 4]
                        ip += 1
                        cs_ = slice(cbk * 512, (cbk + 1) * 512)

                        def mm(e, p=p, cs_=cs_):
                            last = None
                            for kc in range(16):
                                last = e.matmul(p[:], lhsT=aT[:, kc, :], rhs=W[:, kc, cs_], start=(kc == 0), stop=(kc == 15))
                            return last
                        k.op("tensor", mm, reads=[r_aT] + r_W, writes=[rp])
                        k.op("vector", lambda e, p=p, cs_=cs_, row=row: e.tensor_tensor(out=tmp[:], in0=p[:], in1=g1[:, row, cs_], op=ALU.mult),
                             reads=[rp, r_g1], writes=[r_tmp])
                        k.op("vector", lambda e, o_=o_, x_=x_, cs_=cs_: e.tensor_tensor(out=o_[:, cs_], in0=tmp[:], in1=x_[:, cs_], op=ALU.add),
                             reads=[r_tmp, rx], writes=[ro])
                    k.dma(x1[rows, :], o_[:], reads=[ro], writes=[self.r_x1_t[t]])
                k.barrier()

    def stage_moe(self, l, final):
        k = self.k
        x1, _ = self.dr["x1_%d" % l]
        r_x1_t = self.r_x1_t
        modD, r_modD = self.dr["modD%d" % l]
        XE, r_XE = self.dram("moe_xe%d" % l, [16, 128, 16 * 288], BF16)
        WS, r_WS = self.dram("moe_ws%d" % l, [16, 128, 2 * 2048], BF16)
        WSC, r_WSC = self.dram("moe_wsc%d" % l, [16, 32, 256], BF16)
        YE, r_YE = self.dram("moe_ye%d" % l, [16, 288, 2048], BF16)
        r_XEe = [R() for _ in range(16)]
        r_WSe = [R() for _ in range(16)]
        r_YEe = [R() for _ in range(16)]
        if final:
            xo_d, r_xo_d = self.dram("out", [L, D], F32, "ExternalOutput")
        else:
            xo_d, r_xo_d = self.dram("x2_%d" % l, [T, D], F32)
        self.r_x2_t = [R() for _ in range(NT)]
        w1d, r_w1d = self.dr["moe_w1"]
        w3d, r_w3d = self.dr["moe_w3"]
        w2d, r_w2d = self.dr["moe_w2"]
        with ExitStack() as es:
            h2 = k.sb("mo_h2", [128, NT, 2048], BF16, es)
            r_h2 = [R() for _ in range(NT)]
            aff = k.sb("mo_aff", [128, NT, 32], F32, es)
            r_aff = [R() for _ in range(NT)]
            affT = k.sb("mo_affT", [32, T], F32, es)
            r_affT = R()
            k.op("gpsimd", lambda e: e.memset(aff[:], 0.0), writes=r_aff)
            r_c = R()
            with ExitStack() as es2:
                abc = k.sb("mo_abc", [128, 2, 2048], F32, es2)
                shbc = k.sb("mo_shbc", [128, 2, 2048], F32, es2)
                gbc = k.sb("mo_gbc", [128, 2048], F32, es2)
                ap, rr = self.dr["norm2_gb"]
                k.dma(gbc[:], ap[l].partition_broadcast(128), reads=[rr], writes=[r_c])
                for r_ in range(2):
                    k.dma(abc[:, r_, :], modD[r_:r_ + 1, 4 * D:5 * D].partition_broadcast(128), reads=[r_modD], writes=[r_c])
                    k.dma(shbc[:, r_, :], modD[r_:r_ + 1, 3 * D:4 * D].partition_broadcast(128), reads=[r_modD], writes=[r_c])
                for r_ in range(2):
                    k.op("vector", lambda e, r_=r_: e.scalar_tensor_tensor(out=abc[:, r_, :], in0=abc[:, r_, :], scalar=1.0, in1=gbc[:], op0=ALU.add, op1=ALU.mult),
                         reads=[r_c], writes=[r_c])
                rtr = k.sb("mo_rtr", [128, 16, 32], F32, es2)
                k.op("gpsimd", lambda e: e.memset(rtr[:], 0.0), writes=[r_c])
                ap, rr = self.dr["moe_router"]
                k.dma(rtr[:, :, 0:16], ap[l].rearrange("(kc p) n -> p kc n", p=128), reads=[rr], writes=[r_c])
                xt = [k.sb("mo_xt%d" % i, [128, 2048], F32, es2) for i in range(2)]
                r_xt = [R() for _ in range(2)]
                hf_ = k.sb("mo_hf", [128, 2048], F32, es2)
                r_hf = R()
                hT = k.sb("mo_hT", [128, 16, 128], F32, es2)
                r_hT = R()
                junk = k.sb("mo_junk", [128, 2048], BF16, es2)
                r_junk = R()
                st = k.sb("mo_st", [128, NT, 4], F32, es2)
                lg = k.sb("mo_lg", [128, 16], F32, es2)
                r_lg = R()
                ptr = [k.ps("mo_ptr%d" % i, [128, 4, 128], F32, es2) for i in range(4)]
                r_ptr = [PR() for _ in range(4)]
                pl = k.ps("mo_pl", [128, 512], F32, es2)
                r_pl = PR()
                pT = k.ps("mo_pT", [128, 512], F32, es2)
                r_pT = PR()
                for t in range(NT):
                    rows = slice(t * 128, (t + 1) * 128)
                    row = 1 if t < 2 else 0
                    x_, rx = xt[t % 2], r_xt[t % 2]
                    r_s = R()
                    k.dma(x_[:], x1[rows, :], reads=[r_x1_t[t]], writes=[rx])
                    k.op("scalar", lambda e, x_=x_, t=t: e.activation(out=junk[:], in_=x_[:], func=AF.Square, accum_out=st[:, t, 0:1]), reads=[rx], writes=[r_junk, r_s])
                    k.op("scalar", lambda e, t=t: e.activation(out=st[:, t, 1:2], in_=st[:, t, 0:1], func=AF.Sqrt, bias=self.eps_t[:, 0:1], scale=1.0 / D), reads=[r_s], writes=[r_s])
                    k.op("vector", lambda e, t=t: e.reciprocal(out=st[:, t, 1:2], in_=st[:, t, 1:2]), reads=[r_s], writes=[r_s])
                    k.op("vector", lambda e, x_=x_, t=t, row=row: e.scalar_tensor_tensor(out=hf_[:], in0=x_[:], scalar=st[:, t, 1:2], in1=abc[:, row, :], op0=ALU.mult, op1=ALU.mult),
                         reads=[rx, r_s, r_c], writes=[r_hf])
                    k.op("vector", lambda e, row=row: e.tensor_tensor(out=hf_[:], in0=hf_[:], in1=shbc[:, row, :], op=ALU.add), reads=[r_hf, r_c], writes=[r_hf])
                    k.op("scalar", lambda e, t=t: e.activation(out=h2[:, t, :], in_=hf_[:], func=AF.Copy), reads=[r_hf], writes=[r_h2[t]])
                    for g in range(4):
                        def tr(e, g=g):
                            last = None
                            for j in range(4):
                                c = g * 4 + j
                                last = e.transpose(out=ptr[g][:, j, :], in_=hf_[:, c * 128:(c + 1) * 128], identity=self.identf[:])
                            return last
                        k.op("tensor", tr, reads=[r_hf, self.r_ident], writes=[r_ptr[g]])
                        if g % 2 == 0:
                            k.op("vector", lambda e, g=g: e.tensor_copy(out=hT[:, g * 4:(g + 1) * 4, :], in_=ptr[g][:]), reads=[r_ptr[g]], writes=[r_hT])
                        else:
                            k.op("scalar", lambda e, g=g: e.activation(out=hT[:, g * 4:(g + 1) * 4, :], in_=ptr[g][:], func=AF.Copy), reads=[r_ptr[g], r_hT], writes=[r_hT])

                    def mmr(e):
                        last = None
                        for kc in range(16):
                            last = e.matmul(pl[:, 0:32], lhsT=hT[:, kc, :], rhs=rtr[:, kc, :], start=(kc == 0), stop=(kc == 15))
                        return last
                    k.op("tensor", mmr, reads=[r_hT, r_c], writes=[r_pl])
                    k.op("vector", lambda e, t=t: e.tensor_reduce(out=st[:, t, 2:3], in_=pl[:, 0:16], axis=AX.X, op=ALU.max), reads=[r_pl], writes=[r_s])
                    k.op("vector", lambda e, t=t: e.tensor_scalar(out=lg[:], in0=pl[:, 0:16], scalar1=st[:, t, 2:3], scalar2=None, op0=ALU.subtract), reads=[r_pl, r_s], writes=[r_lg])
                    k.op("scalar", lambda e, t=t: e.activation(out=lg[:], in_=lg[:], func=AF.Exp, accum_out=st[:, t, 3:4]), reads=[r_lg], writes=[r_lg, r_s])
                    k.op("vector", lambda e, t=t: e.reciprocal(out=st[:, t, 3:4], in_=st[:, t, 3:4]), reads=[r_s], writes=[r_s])
                    k.op("vector", lambda e, t=t: e.tensor_scalar(out=aff[:, t, 0:16], in0=lg[:], scalar1=st[:, t, 3:4], scalar2=None, op0=ALU.mult), reads=[r_lg, r_s], writes=[r_aff[t]])
                    k.op("tensor", lambda e, t=t: e.transpose(out=pT[0:32, 0:128], in_=aff[:, t, :], identity=self.identf[:]), reads=[r_aff[t], self.r_ident], writes=[r_pT])
                    k.op("vector", lambda e, rows=rows: e.tensor_copy(out=affT[:, rows], in_=pT[0:32, 0:128]), reads=[r_pT], writes=[r_affT])
                k.barrier()
            with ExitStack() as es3:
                sel = k.sb("mo_sel", [32, 16, 128], F32, es3)
                ap, rr = self.dr["moe_sel"]
                k.dma(sel[:], ap, reads=[rr], writes=[r_c])
                jidx = k.sb("mo_jidx", [128, 2], F32, es3)
                ap, rr = self.dr["moe_jidx"]
                k.dma(jidx[:], ap, reads=[rr], writes=[r_c])
                ones = k.sb("mo_ones", [128, 128], BF16, es3)
                k.op("vector", lambda e: e.memset(ones[:], 1.0), writes=[r_c])
                A2 = [k.sb("mo_A%d" % i, [128, T], F32, es3) for i in range(2)]
                r_A2 = [R() for _ in range(2)]
                cmp = [k.sb("mo_cmp%d" % i, [128, 2048], BF16, es3) for i in range(2)]
                r_cmp = [R() for _ in range(2)]
                Ws = k.sb("mo_Ws", [128, 2, 2048], BF16, es3)
                r_Ws = R()
                Ps = k.sb("mo_Ps", [128, 2, 2048], BF16, es3)
                r_Ps = R()
                Wc = k.sb("mo_Wc", [128, 256], BF16, es3)
                r_Wc = R()
                Pc = k.sb("mo_Pc", [128, 256], BF16, es3)
                r_Pc = R()
                PsT = k.sb("mo_PsT", [128, NT, 256], BF16, es3)
                r_PsT = R()
                xe = [k.sb("mo_xe%d" % i, [128, 16, 288], BF16, es3) for i in range(2)]
                r_xe = [R() for _ in range(2)]
                prk = k.ps("mo_prk", [128, 2048], F32, es3)
                r_prk = PR()
                pA = k.ps("mo_pA", [128, 512], F32, es3)
                r_pA = PR()
                ptb = k.ps("mo_ptb", [128, 8, 128], BF16, es3)
                r_ptb = PR()
                pg = [k.ps("mo_pg%d" % i, [128, 512], F32, es3) for i in range(2)]
                r_pg = [PR() for _ in range(2)]
                ic = 0
                ig = 0
                def emit_A(e_):
                    A, r_A = A2[e_ % 2], r_A2[e_ % 2]
                    for cb in range(5):
                        c0 = cb * 512
                        cw = min(512, T - c0)
                        k.op("tensor", lambda e, c0=c0, cw=cw, e_=e_: e.matmul(pA[:, 0:cw], lhsT=sel[:, e_, :], rhs=affT[:, c0:c0 + cw], start=True, stop=True),
                             reads=[r_affT, r_c], writes=[r_pA])
                        k.op("scalar", lambda e, c0=c0, cw=cw, A=A: e.activation(out=A[:, c0:c0 + cw], in_=pA[:, 0:cw], func=AF.Copy), reads=[r_pA], writes=[r_A])
                icn = [0]

                def rank_piece(e_, mt):
                    A, r_A = A2[e_ % 2], r_A2[e_ % 2]
                    c_, rc_ = cmp[icn[0] % 2], r_cmp[icn[0] % 2]
                    icn[0] += 1
                    k.op("vector", lambda e, c_=c_, mt=mt, e_=e_, A=A: e.tensor_scalar(out=c_[:], in0=A[:, NCTX:T], scalar1=aff[:, 2 + mt, e_:e_ + 1], scalar2=None, op0=ALU.is_lt),
                         reads=[r_A, r_aff[2 + mt]], writes=[rc_])

                    def mmc(e, c_=c_, mt=mt):
                        last = None
                        for cb in range(4):
                            last = e.matmul(prk[:, cb * 512:(cb + 1) * 512], lhsT=ones[:], rhs=c_[:, cb * 512:(cb + 1) * 512], start=(mt == 0), stop=(mt == 15))
                        return last
                    k.op("tensor", mmc, reads=[rc_, r_c], writes=[r_prk])

                def gather_piece(dc, x_, rx):
                    p, rp = pg[dc % 2], r_pg[dc % 2]

                    def mmg(e, p=p, dc=dc):
                        last = None
                        for nt in range(16):
                            last = e.matmul(p[:, 0:256], lhsT=h2[:, 2 + nt, dc * 128:(dc + 1) * 128], rhs=PsT[:, 2 + nt, :], start=(nt == 0), stop=(nt == 15))
                        for nt in range(2):
                            last = e.matmul(p[:, 256:288], lhsT=h2[:, nt, dc * 128:(dc + 1) * 128], rhs=PsT[:, nt, 0:32], start=(nt == 0), stop=(nt == 1))
                        return last
                    k.op("tensor", mmg, reads=[r_PsT] + r_h2, writes=[rp])
                    k.op("scalar", lambda e, p=p, x_=x_, dc=dc: e.activation(out=x_[:, dc, :], in_=p[:, 0:288], func=AF.Copy), reads=[rp, rx], writes=[rx])

                emit_A(0)
                for mt in range(16):
                    rank_piece(0, mt)
                for e_ in range(16):
                    A, r_A = A2[e_ % 2], r_A2[e_ % 2]
                    ic = icn[0]
                    for jt in range(2):
                        k.op("vector", lambda e, jt=jt: e.scalar_tensor_tensor(out=Ws[:, jt, :], in0=prk[:], scalar=jidx[:, jt:jt + 1], in1=A[:, NCTX:T], op0=ALU.is_equal, op1=ALU.mult),
                             reads=[r_prk, r_A, r_c], writes=[r_Ws])
                        k.op("vector", lambda e, jt=jt: e.tensor_scalar(out=Ps[:, jt, :], in0=prk[:], scalar1=jidx[:, jt:jt + 1], scalar2=None, op0=ALU.is_equal),
                             reads=[r_prk, r_c], writes=[r_Ps])
                    k.dma(WS[e_].rearrange("p (a b) -> p a b", a=2), Ws[:], reads=[r_Ws], writes=[r_WSe[e_]])
                    for mt in range(2):
                        c_, rc_ = cmp[ic % 2], r_cmp[ic % 2]
                        ic += 1
                        k.op("vector", lambda e, c_=c_, mt=mt, e_=e_: e.tensor_scalar(out=c_[:, 0:256], in0=A[:, 0:NCTX], scalar1=aff[:, mt, e_:e_ + 1], scalar2=None, op0=ALU.is_lt),
                             reads=[r_A, r_aff[mt]], writes=[rc_])
                        k.op("tensor", lambda e, c_=c_, mt=mt: e.matmul(prk[:, 0:256], lhsT=ones[:], rhs=c_[:, 0:256], start=(mt == 0), stop=(mt == 1)),
                             reads=[rc_, r_c], writes=[r_prk])
                    k.op("vector", lambda e: e.scalar_tensor_tensor(out=Wc[:], in0=prk[:, 0:256], scalar=jidx[:, 0:1], in1=A[:, 0:NCTX], op0=ALU.is_equal, op1=ALU.mult),
                         reads=[r_prk, r_A, r_c], writes=[r_Wc])
                    k.op("vector", lambda e: e.tensor_scalar(out=Pc[:], in0=prk[:, 0:256], scalar1=jidx[:, 0:1], scalar2=None, op0=ALU.is_equal),
                         reads=[r_prk, r_c], writes=[r_Pc])
                    k.dma(WSC[e_], Wc[0:32, :], reads=[r_Wc], writes=[r_WSe[e_]])
                    for nt in range(16):
                        if nt % 4 == 0:
                            def trp(e, nt=nt):
                                last = None
                                for q in range(4):
                                    for jt in range(2):
                                        last = e.transpose(out=ptb[:, q * 2 + jt, :], in_=Ps[:, jt, (nt + q) * 128:(nt + q + 1) * 128], identity=self.ident[:])
                                return last
                            k.op("tensor", trp, reads=[r_Ps, self.r_ident], writes=[r_ptb])
                            k.op("scalar", lambda e, nt=nt: e.activation(out=PsT[:, 2 + nt:2 + nt + 4, :], in_=ptb[:].rearrange("p (q j) n -> p q (j n)", q=4), func=AF.Copy),
                                 reads=[r_ptb], writes=[r_PsT])

                    def trc(e):
                        last = None
                        for nt in range(2):
                            last = e.transpose(out=ptb[:, nt, 0:32], in_=Pc[0:32, nt * 128:(nt + 1) * 128], identity=self.ident[0:32, 0:32])
                        return last
                    k.op("tensor", trc, reads=[r_Pc, self.r_ident], writes=[r_ptb])
                    k.op("scalar", lambda e: e.activation(out=PsT[:, 0:2, 0:32], in_=ptb[:, 0:2, 0:32], func=AF.Copy), reads=[r_ptb], writes=[r_PsT])
                    icn[0] = ic
                    if e_ + 1 < 16:
                        emit_A(e_ + 1)
                    x_, rx = xe[e_ % 2], r_xe[e_ % 2]
                    for dc in range(16):
                        gather_piece(dc, x_, rx)
                        if e_ + 1 < 16:
                            rank_piece(e_ + 1, dc)
                    k.dma(XE[e_], x_[:].rearrange("p a b -> p (a b)"), reads=[rx], writes=[r_XEe[e_]])
                k.barrier()
        if self.dbg.get("MOE_PH", 9) < 2:
            return
        with ExitStack() as es:
            w1b = k.sb("mo_w1b", [128, 16, 1024], BF16, es)
            w3b = k.sb("mo_w3b", [128, 16, 1024], BF16, es)
            w2b = k.sb("mo_w2b", [128, 8, 2048], BF16, es)
            r_w1 = [R() for _ in range(8)]
            r_w3 = [R() for _ in range(8)]
            r_w2 = [R() for _ in range(8)]
            wst = [k.sb("mo_wst%d" % i, [128, 4096], F32, es) for i in range(3)]
            r_wst = [R() for _ in range(3)]
            xe = [k.sb("mo_xeb%d" % i, [128, 16, 288], BF16, es) for i in range(2)]
            r_xe = [R() for _ in range(2)]
            gT = k.sb("mo_gT", [128, 8, 288], BF16, es)
            r_gT = [R() for _ in range(8)]
            sa8 = k.sb("mo_sa8", [128, 8, 288], F32, es)
            r_sa8 = [R() for _ in range(8)]
            yeb = [k.sb("mo_yeb%d" % i, [128, 3, 2048], BF16, es) for i in range(1)]
            r_yeb = [R() for _ in range(1)]
            pa = [k.ps("mo_pa%d" % i, [128, 512], F32, es) for i in range(2)]
            r_pa = [PR() for _ in range(2)]
            pu = [k.ps("mo_pu%d" % i, [128, 512], F32, es) for i in range(2)]
            r_pu = [PR() for _ in range(2)]
            py = [k.ps("mo_py%d" % i, [128, 512], F32, es) for i in range(3)]
            r_py = [PR() for _ in range(3)]
            iw = 0
            iy = 0
            engs = ("vector", "gpsimd")
            for e_ in range(16):
                x_, rx = xe[e_ % 2], r_xe[e_ % 2]
                k.dma(x_[:].rearrange("p a b -> p (a b)"), XE[e_], reads=[r_XEe[e_]], writes=[rx])
                for (wd, rwd_, wb_, rw_) in ((w1d, r_w1d, w1b, r_w1), (w3d, r_w3d, w3b, r_w3)):
                    for pr in range(4):
                        s_, rs = wst[iw % 3], r_wst[iw % 3]
                        k.dma(s_[:].rearrange("p (a n) -> p a n", a=4), wd[l, e_, pr * 512:(pr + 1) * 512, :].rearrange("(a p) n -> p a n", p=128), reads=[rwd_], writes=[rs])
                        for hh in range(2):
                            sv = s_[:, hh * 2048:(hh + 1) * 2048].rearrange("p (a n) -> p a n", a=2)
                            dv = wb_[:, 4 * pr + 2 * hh:4 * pr + 2 * hh + 2, :]
                            if hh == 0:
                                k.op("vector", lambda e, sv=sv, dv=dv: e.tensor_copy(out=dv, in_=sv), reads=[rs], writes=[rw_[2 * pr + hh]])
                            else:
                                k.op("scalar", lambda e, sv=sv, dv=dv: e.activation(out=dv, in_=sv, func=AF.Copy), reads=[rs], writes=[rw_[2 * pr + hh]])
                        iw += 1
                for fp in range(4):
                    s_, rs = wst[iw % 3], r_wst[iw % 3]
                    k.dma(s_[:].rearrange("p (a n) -> p a n", a=2), w2d[l, e_, fp * 256:(fp + 1) * 256, :].rearrange("(a p) n -> p a n", p=128), reads=[r_w2d], writes=[rs])
                    for hh in range(2):
                        fc = 2 * fp + hh
                        sv = s_[:, hh * 2048:(hh + 1) * 2048]
                        if hh == 0:
                            k.op("vector", lambda e, sv=sv, fc=fc: e.tensor_copy(out=w2b[:, fc, :], in_=sv), reads=[rs], writes=[r_w2[fc]])
                        else:
                            k.op("scalar", lambda e, sv=sv, fc=fc: e.activation(out=w2b[:, fc, :], in_=sv, func=AF.Copy), reads=[rs], writes=[r_w2[fc]])
                    iw += 1
                for fc in range(8):
                    pa_, rpa = pa[fc % 2], r_pa[fc % 2]

                    def mma(e, p_=pa_, fc=fc, x_=x_):
                        last = None
                        for kc in range(16):
                            last = e.matmul(p_[:, 0:288], lhsT=w1b[:, kc, fc * 128:(fc + 1) * 128], rhs=x_[:, kc, :], start=(kc == 0), stop=(kc == 15))
                        return last
                    k.op("tensor", mma, reads=r_w1 + [rx], writes=[rpa])
                    k.op("scalar", lambda e, pa_=pa_, fc=fc: e.activation(out=sa8[:, fc, :], in_=pa_[:, 0:288], func=AF.Silu), reads=[rpa], writes=[r_sa8[fc]])
                for fc in range(8):
                    pu_, rpu = pu[fc % 2], r_pu[fc % 2]

                    def mmu(e, p_=pu_, fc=fc, x_=x_):
                        last = None
                        for kc in range(16):
                            last = e.matmul(p_[:, 0:288], lhsT=w3b[:, kc, fc * 128:(fc + 1) * 128], rhs=x_[:, kc, :], start=(kc == 0), stop=(kc == 15))
                        return last
                    k.op("tensor", mmu, reads=r_w3 + [rx], writes=[rpu])
                    k.op("vector", lambda e, pu_=pu_, fc=fc: e.tensor_tensor(out=gT[:, fc, :], in0=sa8[:, fc, :], in1=pu_[:, 0:288], op=ALU.mult), reads=[r_sa8[fc], rpu], writes=[r_gT[fc]])
                y_, ry = yeb[0], r_yeb[0]
                for jt, (j0, M) in enumerate(((0, 128), (128, 128), (256, 32))):
                    for cbk in range(4):
                        p_, rp_ = py[iy % 3], r_py[iy % 3]
                        iy += 1
                        cs_ = slice(cbk * 512, (cbk + 1) * 512)

                        def mmy(e, p_=p_, j0=j0, M=M, cs_=cs_):
                            last = None
                            for fc in range(8):
                                last = e.matmul(p_[0:M, :], lhsT=gT[:, fc, j0:j0 + M], rhs=w2b[:, fc, cs_], start=(fc == 0), stop=(fc == 7))
                            return last
                        k.op("tensor", mmy, reads=r_gT + r_w2, writes=[rp_])
                        if iy % 2 == 0:
                            k.op("vector", lambda e, p_=p_, y_=y_, jt=jt, M=M, cs_=cs_: e.tensor_copy(out=y_[0:M, jt, cs_], in_=p_[0:M, :]), reads=[rp_], writes=[ry])
                        else:
                            k.op("scalar", lambda e, p_=p_, y_=y_, jt=jt, M=M, cs_=cs_: e.activation(out=y_[0:M, jt, cs_], in_=p_[0:M, :], func=AF.Copy), reads=[rp_], writes=[ry])
                k.dma(YE[e_, 0:256, :].rearrange("(a p) n -> p a n", p=128), y_[:, 0:2, :], reads=[ry], writes=[r_YEe[e_]])
                k.dma(YE[e_, 256:288, :], y_[0:32, 2, :], reads=[ry], writes=[r_YEe[e_]])
            k.barrier()
        if self.dbg.get("MOE_PH", 9) < 3:
            return
        if not final:
            with ExitStack() as es:
                macc = k.sb("mo_macc", [128, 2, 2048], F32, es)
                r_macc = [R() for _ in range(2)]
                g2 = k.sb("mo_g2c", [128, 2048], F32, es)
                r_g2 = R()
                k.dma(g2[:], modD[1:2, 5 * D:6 * D].partition_broadcast(128), reads=[r_modD], writes=[r_g2])
                ye = [k.sb("mo_yec%d" % i, [32, 2048], BF16, es) for i in range(2)]
                r_ye = [R() for _ in range(2)]
                ws = [k.sb("mo_wsc%d" % i, [32, 256], BF16, es) for i in range(2)]
                r_ws = [R() for _ in range(2)]
                xt = [k.sb("mo_x2c%d" % i, [128, 2048], F32, es) for i in range(2)]
                r_xt = [R() for _ in range(2)]
                ps = [k.ps("mo_psc%d" % i, [128, 512], F32, es) for i in range(4)]
                r_ps = [PR() for _ in range(4)]
                ip = 0
                for e_ in range(16):
                    y_, ry = ye[e_ % 2], r_ye[e_ % 2]
                    w_, rw = ws[e_ % 2], r_ws[e_ % 2]
                    k.dma(y_[:], YE[e_, 256:288, :], reads=[r_YEe[e_]], writes=[ry])
                    k.dma(w_[:], WSC[e_], reads=[r_WSe[e_]], writes=[rw])
                    for tl in range(2):
                        for cbk in range(4):
                            p_, rp_ = ps[ip % 4], r_ps[ip % 4]
                            ip += 1
                            cs_ = slice(cbk * 512, (cbk + 1) * 512)
                            k.op("tensor", lambda e, p_=p_, w_=w_, y_=y_, tl=tl, cs_=cs_: e.matmul(p_[:], lhsT=w_[:, tl * 128:(tl + 1) * 128], rhs=y_[:, cs_], start=True, stop=True),
                                 reads=[rw, ry], writes=[rp_])
                            if e_ == 0:
                                k.op("vector", lambda e, p_=p_, tl=tl, cs_=cs_: e.tensor_copy(out=macc[:, tl, cs_], in_=p_[:]), reads=[rp_], writes=[r_macc[tl]])
                            else:
                                k.op("vector", lambda e, p_=p_, tl=tl, cs_=cs_: e.tensor_tensor(out=macc[:, tl, cs_], in0=macc[:, tl, cs_], in1=p_[:], op=ALU.add),
                                     reads=[rp_, r_macc[tl]], writes=[r_macc[tl]])
                if not final:
                    for tl in range(2):
                        rows = slice(tl * 128, (tl + 1) * 128)
                        x_, rx = xt[tl], r_xt[tl]
                        k.dma(x_[:], x1[rows, :], reads=[r_x1_t[tl]], writes=[rx])
                        k.op("vector", lambda e, tl=tl: e.tensor_tensor(out=macc[:, tl, :], in0=macc[:, tl, :], in1=g2[:], op=ALU.mult), reads=[r_macc[tl], r_g2], writes=[r_macc[tl]])
                        k.op("vector", lambda e, tl=tl, x_=x_: e.tensor_tensor(out=x_[:], in0=x_[:], in1=macc[:, tl, :], op=ALU.add), reads=[rx, r_macc[tl]], writes=[rx])
                        k.dma(xo_d[rows, :], x_[:], reads=[rx], writes=[self.r_x2_t[tl]])
                k.barrier()
        with ExitStack() as es:
            YA = k.sb("mo_YA", [128, 16, 2, 1024], BF16, es)
            r_YA = [R() for _ in range(16)]
            WA = k.sb("mo_WA", [128, 16, 2, 1024], BF16, es)
            r_WA = [R() for _ in range(16)]
            g2 = k.sb("mo_g2l", [128, 2048], F32, es)
            r_g2 = R()
            k.dma(g2[:], modD[0:1, 5 * D:6 * D].partition_broadcast(128), reads=[r_modD], writes=[r_g2])
            xt = [k.sb("mo_x2l%d" % i, [128, 1024], F32, es) for i in range(2)]
            r_xt = [R() for _ in range(2)]
            tmp = [k.sb("mo_t2l%d" % i, [128, 512], F32, es) for i in range(2)]
            r_tmp = [R() for _ in range(2)]
            ps = [k.ps("mo_psl%d" % i, [128, 512], F32, es) for i in range(4)]
            r_ps = [PR() for _ in range(4)]
            ip = 0
            ix = 0
            for dh in range(2):
                d0 = dh * 1024
                for e_ in range(16):
                    k.dma(YA[:, e_, :, :], YE[e_, 0:256, d0:d0 + 1024].rearrange("(a p) n -> p a n", p=128), reads=[r_YEe[e_]], writes=[r_YA[e_]])
                for nh in range(2):
                    n0 = nh * 1024
                    for e_ in range(16):
                        k.dma(WA[:, e_, :, :], WS[e_].rearrange("p (a b) -> p a b", a=2)[:, :, n0:n0 + 1024], reads=[r_WSe[e_]], writes=[r_WA[e_]])
                    for tl in range(8):
                        t = 2 + nh * 8 + tl
                        rows = slice(t * 128, (t + 1) * 128)
                        x_, rx = xt[ix % 2], r_xt[ix % 2]
                        ix += 1
                        k.dma(x_[:], x1[rows, d0:d0 + 1024], reads=[r_x1_t[t]], writes=[rx])
                        for cb in range(2):
                            p_, rp_ = ps[ip % 4], r_ps[ip % 4]
                            t_, rt_ = tmp[ip % 2], r_tmp[ip % 2]
                            ip += 1
                            cs_ = slice(cb * 512, (cb + 1) * 512)

                            def mms(e, p_=p_, tl=tl, cs_=cs_):
                                last = None
                                n = 0
                                for e2 in range(16):
                                    for jt in range(2):
                                        last = e.matmul(p_[:], lhsT=WA[:, e2, jt, tl * 128:(tl + 1) * 128], rhs=YA[:, e2, jt, cs_], start=(n == 0), stop=(n == 31))
                                        n += 1
                                return last
                            k.op("tensor", mms, reads=r_WA + r_YA, writes=[rp_])
                            k.op("vector", lambda e, p_=p_, t_=t_, cs_=cs_, d0=d0: e.tensor_tensor(out=t_[:], in0=p_[:], in1=g2[:, d0 + cs_.start:d0 + cs_.stop], op=ALU.mult),
                                 reads=[rp_, r_g2], writes=[rt_])
                            k.op("vector", lambda e, t_=t_, x_=x_, cs_=cs_: e.tensor_tensor(out=x_[:, cs_], in0=x_[:, cs_], in1=t_[:], op=ALU.add), reads=[rt_, rx], writes=[rx])
                        if final:
                            k.dma(xo_d[(t - 2) * 128:(t - 1) * 128, d0:d0 + 1024], x_[:], reads=[rx], writes=[r_xo_d])
                        else:
                            k.dma(xo_d[rows, d0:d0 + 1024], x_[:], reads=[rx], writes=[self.r_x2_t[t]])
            k.barrier()

    def declare_inputs(self):
        dp = self.depth
        self.dram("xcat", [T, D], F32, "ExternalInput")
        self.dram("ada_w", [dp, D, 6 * D], F32, "ExternalInput")
        self.dram("ada_b2", [dp, 2, 6 * D], F32, "ExternalInput")
        self.dram("norm1_gT", [dp, 128, 16], F32, "ExternalInput")
        self.dram("w_in", [dp, D, ZC], F32, "ExternalInput")
        self.dram("na_qg", [dp, 1, 512], F32, "ExternalInput")
        self.dram("na_kg", [dp, 1, 512], F32, "ExternalInput")
        self.dram("na_bias", [dp, 8, 5, 5, 128, 128], F32, "ExternalInput")
        self.dram("mla_cqg", [dp, 1, 512], F32, "ExternalInput")
        self.dram("mla_ckvg", [dp, 1, 256], F32, "ExternalInput")
        self.dram("mla_qg", [dp, 1, 768], F32, "ExternalInput")
        self.dram("mla_kng", [dp, 1, 512], F32, "ExternalInput")
        self.dram("mla_kpg", [dp, 1, 64], F32, "ExternalInput")
        self.dram("mla_w_uq", [dp, 512, 768], F32, "ExternalInput")
        self.dram("mla_w_ukv", [dp, 256, 1024], F32, "ExternalInput")
        self.dram("rope_cos", [T, 64], F32, "ExternalInput")
        self.dram("dft_c", [128, 256], BF16, "ExternalInput")
        self.dram("norm2_gb", [dp, 1, D], F32, "ExternalInput")
        self.dram("moe_router", [dp, D, 16], F32, "ExternalInput")
        self.dram("moe_w1", [dp, 16, D, 1024], F32, "ExternalInput")
        self.dram("moe_w3", [dp, 16, D, 1024], F32, "ExternalInput")
        self.dram("moe_w2", [dp, 16, 1024, D], F32, "ExternalInput")
        self.dram("moe_sel", [32, 16, 128], F32, "ExternalInput")
        self.dram("moe_jidx", [128, 2], F32, "ExternalInput")
        self.dram("w_br", [dp, 4, 512, D], F32, "ExternalInput")
        self.dram("w_out", [dp, D, D], F32, "ExternalInput")
        self.dram("rw_mu", [dp, 1, 1760], F32, "ExternalInput")
        self.dram("rw_kk", [dp, 1, 512], F32, "ExternalInput")
        self.dram("rw_ka", [dp, 1, 512], F32, "ExternalInput")
        self.dram("rw_w0f", [dp, 1, 1024], F32, "ExternalInput")
        self.dram("rw_a0f", [dp, 1, 1024], F32, "ExternalInput")
        self.dram("rw_wa2", [dp, 32, 2, 2, 512], F32, "ExternalInput")
        self.dram("rw_lnw", [dp, 1, 512], F32, "ExternalInput")
        self.dram("rw_lnb", [dp, 1, 512], F32, "ExternalInput")
        self.dram("rw_rk", [dp, 1, 512], F32, "ExternalInput")
        self.dram("rw_g2", [dp, 96, 512], F32, "ExternalInput")
        self.dram("rw_e2", [128, 64, 128], F32, "ExternalInput")
        self.dram("rw_pm", [128, 128], F32, "ExternalInput")
        self.dram("rw_j64", [64, 64], F32, "ExternalInput")
        self.dram("rw_masks", [128, 6, 128], F32, "ExternalInput")
        self.dram("rw_dm", [128, 2, 1024], F32, "ExternalInput")
        self.dram("dft_L", [2, L, L], BF16, "ExternalInput")
        self.dram("dft_C", [2, NCTX, NCTX], BF16, "ExternalInput")
        self.dram("rope_sin", [T, 64], F32, "ExternalInput")

    def build(self):
        k = self.k
        self.declare_inputs()
        self.eps_t = k.sb("eps_t", [128, 1], F32)
        r = R()
        k.op("vector", lambda e: e.memset(self.eps_t[:], EPS), writes=[r])
        k.barrier()
        self.consts()
        xname = "xcat"
        for l in range(self.depth):
            if self.want("mod"):
                self.stage_mod(l)
            else:
                self.dram("modD%d" % l, [2, 6 * D], F32)
            if self.want("normz"):
                self.stage_normz(l, xname)
            else:
                self.dram("z%d" % l, [T, ZC], F32)
                self.r_ztile = [self.dr["z%d" % l][1]] * NT
            if self.want("na"):
                self.stage_na(l)
            if self.want("mla"):
                self.stage_mla(l)
            if self.want("fourier"):
                self.stage_fourier(l)
            if self.want("rwprep"):
                self.stage_rwprep(l)
            if self.want("rwscan"):
                self.stage_rwscan(l)
            if self.want("rwout"):
                self.stage_rwout(l)
            else:
                for n in ("y_fn%d", "y_na%d", "y_mla%d", "y_rw%d"):
                    if (n % l) not in self.dr:
                        self.dram(n % l, [T, 512], F32)
            if self.want("merge"):
                self.stage_merge(l, xname)
            else:
                self.dram("x1_%d" % l, [T, D], F32)
                self.r_x1_t = [self.dr["x1_%d" % l][1]] * NT
            if self.want("moe"):
                self.stage_moe(l, final=(l == self.depth - 1) and self.final_out)
                xname = "x2_%d" % l
        k.finish([])
        return self.nc


def _na_bias_index():
    rows, kh, W, KW = 32, 8, 64, 16
    dr = np.zeros((5, 5, 128, 128), np.int64)
    dc = np.zeros((5, 5, 128, 128), np.int64)
    va = np.zeros((5, 5, 128, 128), bool)
    kk = np.arange(128)
    qq = np.arange(128)
    for cls, j in enumerate((0, 1, 5, 14, 15)):
        lo = min(max(j - 2, 0), 11)
        for i in range(5):
            kt = lo + i
            krow = (2 * kt + kk // 64)[:, None]
            kc = (kk % 64)[:, None]
            r = (2 * j + qq // 64)[None, :]
            qc = (qq % 64)[None, :]
            rs = np.clip(r - kh // 2, 0, rows - kh)
            cs = np.clip(qc - KW // 2, 0, W - KW)
            v = (krow >= rs) & (krow < rs + kh) & (kc >= cs) & (kc < cs + KW)
            dr[cls, i] = np.clip(krow - r + 8 - 1, 0, 14)
            dc[cls, i] = np.clip(kc - qc + KW - 1, 0, 2 * KW - 2)
            va[cls, i] = v
    return dr, dc, va


def _rope_tables():
    pos = np.arange(L)
    prow, pcol = pos // 64, pos % 64
    freqs = 10000.0 ** (-np.arange(16, dtype=np.float32) / 16)
    ar = prow[:, None].astype(np.float32) * freqs[None, :]
    ac = pcol[:, None].astype(np.float32) * freqs[None, :]
    cos = np.concatenate([np.cos(ar), np.cos(ar), np.cos(ac), np.cos(ac)], 1)
    sin = np.concatenate([-np.sin(ar), np.sin(ar), -np.sin(ac), np.sin(ac)], 1)
    cosT = np.concatenate([np.ones((NCTX, 64), np.float32), cos.astype(np.float32)], 0)
    sinT = np.concatenate([np.zeros((NCTX, 64), np.float32), sin.astype(np.float32)], 0)
    return np.ascontiguousarray(cosT), np.ascontiguousarray(sinT)


_DFT = None


def _dft_consts():
    global _DFT
    if _DFT is None:
        c = np.arange(128)
        ang = 2 * np.pi * ((c[:, None] * c[None, :]) % 128) / 128.0
        dft_c = bf16_np(np.concatenate([np.cos(ang), np.sin(ang)], 1))
        out = []
        for n in (L, NCTX):
            i = np.arange(n, dtype=np.int64)
            ang = 2 * np.pi * ((i[:, None] * i[None, :]) % n) / float(n)
            sc = 1.0 / np.sqrt(n * 128.0)
            out.append(bf16_np(np.stack([np.cos(ang) * sc, -np.sin(ang) * sc], 0)))
        _DFT = (dft_c, out[0], out[1])
    return _DFT


def _rw_chunk_consts():
    MS = np.zeros((128, 128), np.float32)
    MI = np.zeros((128, 128), np.float32)
    BO = np.zeros((128, 128), np.float32)
    for d in range(2):
        for a in range(64):
            for c in range(64):
                before = (a < c) if d == 0 else (a > c)
                MS[d * 64 + a, d * 64 + c] = 1.0 if before else 0.0
                MI[d * 64 + a, d * 64 + c] = 1.0 if (before or a == c) else 0.0
                BO[d * 64 + a, d * 64 + c] = 1.0
    masks = np.stack([MI, BO, MS, -MS, -MS.T, -MI], 1).astype(np.float32)
    dm = np.zeros((128, 8, 2, 64), np.float32)
    dm[0:64, :, 0, :] = 1.0
    dm[64:128, :, 1, :] = 1.0
    dm = dm.reshape(128, 1024)
    return np.ascontiguousarray(masks), np.ascontiguousarray(np.stack([dm, -dm], 1))


def _rw_consts():
    e2 = np.zeros((128, 64, 128), np.float32)
    for i in range(64):
        e2[i, i, 0:64] = 1.0
        e2[64 + 63 - i, i, 64:128] = 1.0
    pm = np.zeros((128, 128), np.float32)
    for i in range(64):
        pm[i, i] = 1.0
        pm[64 + i, 64 + 63 - i] = 1.0
    j64 = np.ascontiguousarray(np.eye(64, dtype=np.float32)[::-1])
    return e2, pm, j64


def host_core(inp, b):
    f32 = np.float32
    m = {}
    m["xcat"] = np.concatenate([inp["ctx"][b], inp["x"][b]], 0).astype(f32)
    cv = np.stack([inp["c"][b], inp["c_ctx"]], 0)
    m["cvT"] = np.ascontiguousarray(cv.reshape(2, 16, 128).transpose(2, 1, 0))
    return m


def host_inputs(inp, b, depth=DEPTH):
    m = host_shared(inp, depth)
    m.update(host_core(inp, b))
    return m


def host_shared(inp, depth=DEPTH):
    f32 = np.float32
    m = {}
    m["ada_w"] = inp["ada_w"][:depth]
    m["ada_b2"] = np.ascontiguousarray(np.broadcast_to(inp["ada_b"][:depth, None, :], (depth, 2, 6 * D)))
    m["norm1_gT"] = np.ascontiguousarray(inp["norm1_g"][:depth].reshape(depth, 16, 128).transpose(0, 2, 1))
    m["w_in"] = inp["w_in"][:depth]
    m["na_qg"] = np.ascontiguousarray(np.tile(inp["na_q_norm"][:depth, None, :], (1, 1, 8)))
    m["na_kg"] = np.ascontiguousarray(np.tile(inp["na_k_norm"][:depth, None, :], (1, 1, 8)))
    dr, dc, va = _na_bias_index()
    rpb = inp["na_rpb"][:depth]
    gathered = rpb[:, :, dr, dc]
    m["na_bias"] = np.where(va[None, None], gathered, f32(-1e30)).astype(f32)
    m["mla_cqg"] = np.ascontiguousarray(inp["mla_cq_norm"][:depth, None, :])
    m["mla_ckvg"] = np.ascontiguousarray(inp["mla_ckv_norm"][:depth, None, :])
    m["mla_qg"] = np.ascontiguousarray(np.tile(inp["mla_q_norm"][:depth, None, :], (1, 1, 4)))
    m["mla_kng"] = np.ascontiguousarray(np.tile(inp["mla_k_norm"][:depth, None, :128], (1, 1, 4)))
    m["mla_kpg"] = np.ascontiguousarray(inp["mla_k_norm"][:depth, None, 128:192])
    m["mla_w_uq"] = inp["mla_w_uq"][:depth]
    m["mla_w_ukv"] = inp["mla_w_ukv"][:depth]
    m["dft_c"], m["dft_L"], m["dft_C"] = _dft_consts()
    m["rw_mu"] = np.ascontiguousarray(np.concatenate([inp["rw_mu_ks"][:depth], inp["rw_mu_qs"][:depth]], 1)[:, None, :])
    m["rw_kk"] = np.ascontiguousarray(inp["rw_k_k"][:depth, None, :])
    m["rw_ka"] = np.ascontiguousarray(inp["rw_k_a"][:depth, None, :])
    m["rw_w0f"] = np.ascontiguousarray(inp["rw_w0"][:depth].reshape(depth, 1, 1024))
    m["rw_a0f"] = np.ascontiguousarray(inp["rw_a0"][:depth].reshape(depth, 1, 1024))
    wa2 = np.stack([inp["rw_w2"][:depth], inp["rw_a2"][:depth]], 1)
    m["rw_wa2"] = np.ascontiguousarray(wa2.transpose(0, 3, 1, 2, 4))
    m["rw_lnw"] = np.ascontiguousarray(inp["rw_ln_w"][:depth, None, :])
    m["rw_lnb"] = np.ascontiguousarray(inp["rw_ln_b"][:depth, None, :])
    m["rw_rk"] = np.ascontiguousarray(inp["rw_r_k"][:depth].reshape(depth, 1, 512))
    m["rw_g2"] = inp["rw_g2"][:depth]
    m["rw_e2"], m["rw_pm"], m["rw_j64"] = _rw_consts()
    m["rw_masks"], m["rw_dm"] = _rw_chunk_consts()
    m["w_br"] = inp["w_br"][:depth]
    m["norm2_gb"] = np.ascontiguousarray(inp["norm2_g"][:depth, None, :])
    m["moe_router"] = inp["moe_router"][:depth]
    m["moe_w1"] = inp["moe_w1"][:depth]
    m["moe_w3"] = inp["moe_w3"][:depth]
    m["moe_w2"] = inp["moe_w2"][:depth]
    sel = np.zeros((32, 16, 128), np.float32)
    for e in range(16):
        sel[e, e, :] = 1.0
    m["moe_sel"] = sel
    m["moe_jidx"] = np.ascontiguousarray(np.stack([np.arange(128), np.arange(128) + 128], 1).astype(np.float32))
    m["w_out"] = inp["w_out"][:depth]
    cosT, sinT = _rope_tables()
    m["rope_cos"], m["rope_sin"] = cosT, sinT
    return m


_PROG = None


def kernel(**inputs):
    global _PROG
    inp = {k_: np.asarray(v) for k_, v in inputs.items()}
    if _PROG is None:
        P = Prog(depth=DEPTH)
        P.build()
        _PROG = P
    P = _PROG
    shared = host_shared(inp, DEPTH)
    in_maps = []
    for b in range(8):
        m = dict(shared)
        m.update(host_core(inp, b))
        in_maps.append({n: m[n] for n, kd in P.kinds.items() if kd == "ExternalInput"})
    res = run_bass_kernel_spmd(P.nc, in_maps, core_ids=list(range(8)))
    return np.stack([np.asarray(r["out"], dtype=np.float32) for r in res.results], 0)
```
